# Optimizing a Trainium2 kernel written in Bass

```python
import math
import jax
import jax.numpy as jnp
from jax import lax
import numpy as np

D_MODEL = 2048
BATCH = 2
SEQ = 8192
DEPTH = 4

N_MIXERS = 3
N_RET_LAYERS = (DEPTH + 2) // 3
N_SWA_LAYERS = (DEPTH + 1) // 3
N_RWKV_LAYERS = DEPTH // 3

RET_HEADS = 8
RET_QK_DIM = D_MODEL // RET_HEADS
RET_V_DIM = 2 * D_MODEL // RET_HEADS
RET_CHUNK = 128
RET_GN_EPS = 1e-5
ROPE_BASE = 10000.0

SWA_HEAD_DIM = 64
SWA_Q_HEADS = D_MODEL // SWA_HEAD_DIM
SWA_KV_HEADS = SWA_Q_HEADS // 8
SWA_WINDOW = 128
SWA_BLOCK = SWA_WINDOW
REL_BUCKETS = 32
REL_MAX_DIST = SWA_WINDOW
NEG_INF = -1e30

RWKV_HEAD_DIM = 64
RWKV_HEADS = D_MODEL // RWKV_HEAD_DIM
RWKV_DECAY_LORA = 96
RWKV_AAA_LORA = 96
RWKV_GATE_LORA = 256
RWKV_GN_EPS = 64e-5

D_FF = 5504
CONV_WIDTH = 3

PLE_DIM = 256

LN_EPS = 1e-5
DEEPNORM_ALPHA = (2.0 * DEPTH) ** 0.25
DEEPNORM_BETA = (8.0 * DEPTH) ** -0.25

kernel_name = 'hybrid_retnet_swa_rwkv7_deepnorm_trunk'


def _layer_norm(x, gain, bias):
    xf = x.astype(jnp.float32)
    mu = jnp.mean(xf, axis=-1, keepdims=True)
    var = jnp.mean(jnp.square(xf - mu), axis=-1, keepdims=True)
    return ((xf - mu) * lax.rsqrt(var + LN_EPS)).astype(x.dtype) * gain + bias


def _head_norm(x, gain, bias, eps):
    xf = x.astype(jnp.float32)
    mu = jnp.mean(xf, axis=-1, keepdims=True)
    var = jnp.mean(jnp.square(xf - mu), axis=-1, keepdims=True)
    y = (xf - mu) * lax.rsqrt(var + eps) * gain.astype(jnp.float32)
    if bias is not None:
        y = y + bias.astype(jnp.float32)
    return y


def _rotary(x, pos):
    d = x.shape[-1]
    inv = 1.0 / (ROPE_BASE ** (jnp.arange(0, d, 2, dtype=jnp.float32) / d))
    ang = pos.astype(jnp.float32)[:, None] * inv[None, :]
    cos = jnp.cos(ang)[None, :, None, :].astype(x.dtype)
    sin = jnp.sin(ang)[None, :, None, :].astype(x.dtype)
    x1, x2 = x[..., : d // 2], x[..., d // 2:]
    return jnp.concatenate([x1 * cos - x2 * sin, x1 * sin + x2 * cos], axis=-1)


def _retention(x, w_in, gn_gain, w_out):
    B, S, _ = x.shape
    H, dk, dv, C = RET_HEADS, RET_QK_DIM, RET_V_DIM, RET_CHUNK
    N = S // C
    proj = x @ w_in
    q, k, v, g = jnp.split(proj, [H * dk, 2 * H * dk, 2 * H * dk + H * dv], axis=-1)
    pos = jnp.arange(S)
    q = _rotary(q.reshape(B, S, H, dk), pos)
    k = _rotary(k.reshape(B, S, H, dk), pos) * (dk ** -0.5)
    v = v.reshape(B, S, H, dv)

    log_gamma = jnp.log(1.0 - 2.0 ** (-5.0 - jnp.arange(H, dtype=jnp.float32)))
    idx = jnp.arange(C, dtype=jnp.float32)
    diff = idx[:, None] - idx[None, :]
    decay_mask = jnp.where(diff >= 0, jnp.exp(log_gamma[:, None, None] * jnp.maximum(diff, 0.0)), 0.0)
    q_decay = jnp.exp(log_gamma[None, :] * (idx[:, None] + 1.0))
    k_decay = jnp.exp(log_gamma[None, :] * (C - 1.0 - idx[:, None]))
    chunk_decay = jnp.exp(log_gamma * C)

    qc = q.reshape(B, N, C, H, dk)
    kc = k.reshape(B, N, C, H, dk)
    vc = v.reshape(B, N, C, H, dv)
    scores = jnp.einsum('bnihd,bnjhd->bnhij', qc, kc) * decay_mask.astype(x.dtype)
    inner = jnp.einsum('bnhij,bnjhe->bnihe', scores, vc)

    def step(R, inp):
        q_n, k_n, v_n = inp
        cross = jnp.einsum('bihd,bhde->bihe', q_n, R) * q_decay[None, :, :, None]
        R = R * chunk_decay[None, :, None, None] + jnp.einsum(
            'bjhd,bjhe->bhde', k_n * k_decay[None, :, :, None], v_n)
        return R, cross

    R0 = jnp.zeros((B, H, dk, dv), jnp.float32)
    to_chunks = lambda t: jnp.transpose(t, (1, 0, 2, 3, 4))
    _, cross = lax.scan(step, R0, (to_chunks(qc), to_chunks(kc), to_chunks(vc)))
    cross = jnp.transpose(cross, (1, 0, 2, 3, 4))

    o = (inner + cross).reshape(B, S, H, dv)
    o = _head_norm(o, gn_gain.reshape(H, dv), None, RET_GN_EPS).astype(x.dtype)
    o = jax.nn.silu(g) * o.reshape(B, S, H * dv)
    return o @ w_out


def _t5_buckets(dist):
    n = np.maximum(dist, 0)
    max_exact = REL_BUCKETS // 2
    large = max_exact + (np.log(np.maximum(n, 1) / max_exact) / np.log(REL_MAX_DIST / max_exact)
                         * (REL_BUCKETS - max_exact)).astype(np.int32)
    large = np.minimum(large, REL_BUCKETS - 1)
    return np.where(n < max_exact, n, large).astype(np.int32)


def _swa_attention(x, w_qkv, sinks, w_out, rel_bias):
    B, S, _ = x.shape
    Hq, Hkv, hd, C = SWA_Q_HEADS, SWA_KV_HEADS, SWA_HEAD_DIM, SWA_BLOCK
    G = Hq // Hkv
    N = S // C
    qkv = x @ w_qkv
    q, k, v = jnp.split(qkv, [Hq * hd, (Hq + Hkv) * hd], axis=-1)
    q = q.reshape(B, N, C, Hkv, G, hd) * (hd ** -0.5)
    k = k.reshape(B, N, C, Hkv, hd)
    v = v.reshape(B, N, C, Hkv, hd)

    def with_prev(t):
        prev = jnp.pad(t, ((0, 0), (1, 0), (0, 0), (0, 0), (0, 0)))[:, :-1]
        return jnp.concatenate([prev, t], axis=2)

    kb, vb = with_prev(k), with_prev(v)

    qi = np.arange(C)[:, None]
    kj = np.arange(2 * C)[None, :]
    dist = qi + C - kj
    in_window = jnp.asarray((dist >= 0) & (dist < SWA_WINDOW))
    not_pad = (jnp.arange(N)[:, None, None] > 0) | jnp.asarray(kj >= C)[None]
    valid = in_window[None] & not_pad
    bias = rel_bias[_t5_buckets(dist)]
    bias = jnp.transpose(bias, (2, 0, 1)).reshape(Hkv, G, C, 2 * C).astype(jnp.float32)

    scores = jnp.einsum('bnihgd,bnjhd->bnhgij', q, kb).astype(jnp.float32) + bias[None, None]
    scores = jnp.where(valid[None, :, None, None], scores, NEG_INF)
    sink = sinks.reshape(Hkv, G).astype(jnp.float32)[None, None, :, :, None, None]
    m = jnp.maximum(jnp.max(scores, axis=-1, keepdims=True), sink)
    pexp = jnp.exp(scores - m)
    denom = jnp.sum(pexp, axis=-1, keepdims=True) + jnp.exp(sink - m)
    probs = (pexp / denom).astype(x.dtype)
    o = jnp.einsum('bnhgij,bnjhd->bnihgd', probs, vb).reshape(B, S, Hq * hd)
    return o @ w_out


def _rwkv7_time_mix(x, mix, w_rkv, w0, w1, w2, a0, a1, a2, g1, g2, k_k, k_a, r_k,
                    gn_gain, gn_bias, w_out):
    B, S, D = x.shape
    H, hd = RWKV_HEADS, RWKV_HEAD_DIM
    x_prev = jnp.pad(x, ((0, 0), (1, 0), (0, 0)))[:, :-1]
    xx = x_prev - x
    xr, xw, xk, xv, xa, xg = [x + xx * mix[i] for i in range(6)]
    r, k, v = jnp.einsum('tbsd,tde->tbse', jnp.stack([xr, xk, xv]), w_rkv)
    log_w = -jax.nn.softplus(-(w0 + jnp.tanh(xw @ w1) @ w2)) - 0.5
    decay = jnp.exp(-jnp.exp(log_w.astype(jnp.float32)))
    a = jax.nn.sigmoid(a0 + (xa @ a1) @ a2)
    g = jax.nn.sigmoid(xg @ g1) @ g2
    kk = (k * k_k).reshape(B, S, H, hd).astype(jnp.float32)
    kk = kk / jnp.maximum(jnp.sqrt(jnp.sum(kk * kk, axis=-1, keepdims=True)), 1e-12)
    k = k * (1.0 + (a - 1.0) * k_a)

    heads = lambda t: t.reshape(B, S, H, hd).astype(jnp.float32)
    r_h, k_h, v_h, a_h, w_h = heads(r), heads(k), heads(v), heads(a), heads(decay)
    a_vec = -kk
    b_vec = kk * a_h

    def step(state, inp):
        r_t, w_t, k_t, v_t, av_t, bv_t = inp
        sa = jnp.einsum('bhvk,bhk->bhv', state, av_t)
        state = (state * w_t[:, :, None, :] + sa[..., None] * bv_t[:, :, None, :]
                 + v_t[..., None] * k_t[:, :, None, :])
        y_t = jnp.einsum('bhvk,bhk->bhv', state, r_t)
        return state, y_t

    to_time = lambda t: jnp.transpose(t, (1, 0, 2, 3))
    S0 = jnp.zeros((B, H, hd, hd), jnp.float32)
    _, y = lax.scan(step, S0, (to_time(r_h), to_time(w_h), to_time(k_h), to_time(v_h),
                               to_time(a_vec), to_time(b_vec)))
    y = jnp.transpose(y, (1, 0, 2, 3))
    y = _head_norm(y, gn_gain.reshape(H, hd), gn_bias.reshape(H, hd), RWKV_GN_EPS)
    bonus = jnp.sum(r_h * k_h * r_k.astype(jnp.float32), axis=-1, keepdims=True) * v_h
    y = (y + bonus).reshape(B, S, D).astype(x.dtype) * g
    return y @ w_out


def _conv_ffn(x, w_up, conv_w, conv_b, w_down):
    S = x.shape[1]
    h = x @ w_up
    hp = jnp.pad(h, ((0, 0), (CONV_WIDTH - 1, 0), (0, 0)))
    hc = conv_b + hp[:, CONV_WIDTH - 1:] * conv_w[CONV_WIDTH - 1]
    for tap in range(CONV_WIDTH - 1):
        hc = hc + hp[:, tap:tap + S] * conv_w[tap]
    u, gate = jnp.split(hc, 2, axis=-1)
    return (jax.nn.silu(gate) * u) @ w_down


def setup_inputs(seed: int = 0) -> dict:
    key = jax.random.key(seed)
    ks = iter(jax.random.split(key, 48))
    nrm = lambda shape: jax.random.normal(next(ks), shape, jnp.float32)
    dense = lambda shape, fan_in, scale=1.0: nrm(shape) * (fan_in ** -0.5) * scale
    D, F = D_MODEL, D_FF
    ret_in_cols = 2 * RET_HEADS * RET_QK_DIM + 2 * RET_HEADS * RET_V_DIM
    swa_in_cols = (SWA_Q_HEADS + 2 * SWA_KV_HEADS) * SWA_HEAD_DIM
    nr, ns, nw = N_RET_LAYERS, N_SWA_LAYERS, N_RWKV_LAYERS
    return {
        'x': nrm((BATCH, SEQ, D)),
        'p': nrm((DEPTH, BATCH, SEQ, PLE_DIM)),
        'ln_gain': 1.0 + 0.02 * nrm((DEPTH, 2, D)),
        'ln_bias': 0.02 * nrm((DEPTH, 2, D)),
        'ret_w_in': dense((nr, D, ret_in_cols), D),
        'ret_gn_gain': 1.0 + 0.02 * nrm((nr, RET_HEADS * RET_V_DIM)),
        'ret_w_out': dense((nr, RET_HEADS * RET_V_DIM, D), RET_HEADS * RET_V_DIM, DEEPNORM_BETA),
        'swa_w_qkv': dense((ns, D, swa_in_cols), D),
        'swa_sinks': nrm((ns, SWA_Q_HEADS)),
        'swa_w_out': dense((ns, SWA_Q_HEADS * SWA_HEAD_DIM, D), SWA_Q_HEADS * SWA_HEAD_DIM, DEEPNORM_BETA),
        'rel_bias': 0.5 * nrm((REL_BUCKETS, SWA_Q_HEADS)),
        'rwkv_mix': jax.random.uniform(next(ks), (nw, 6, D), jnp.float32),
        'rwkv_w_rkv': dense((nw, 3, D, D), D),
        'rwkv_w0': 0.5 * nrm((nw, D)) - 0.5,
        'rwkv_w1': dense((nw, D, RWKV_DECAY_LORA), D),
        'rwkv_w2': dense((nw, RWKV_DECAY_LORA, D), RWKV_DECAY_LORA, 0.5),
        'rwkv_a0': 0.1 * nrm((nw, D)),
        'rwkv_a1': dense((nw, D, RWKV_AAA_LORA), D),
        'rwkv_a2': dense((nw, RWKV_AAA_LORA, D), RWKV_AAA_LORA, 0.5),
        'rwkv_g1': dense((nw, D, RWKV_GATE_LORA), D),
        'rwkv_g2': dense((nw, RWKV_GATE_LORA, D), RWKV_GATE_LORA),
        'rwkv_k_k': 0.85 + 0.05 * nrm((nw, D)),
        'rwkv_k_a': 1.0 + 0.05 * nrm((nw, D)),
        'rwkv_r_k': 0.1 * nrm((nw, RWKV_HEADS, RWKV_HEAD_DIM)),
        'rwkv_gn_gain': 1.0 + 0.02 * nrm((nw, D)),
        'rwkv_gn_bias': 0.02 * nrm((nw, D)),
        'rwkv_w_out': dense((nw, D, D), D, DEEPNORM_BETA),
        'ffn_w_up': dense((DEPTH, D, 2 * F), D),
        'ffn_conv_w': 0.5 * nrm((DEPTH, CONV_WIDTH, 2 * F)),
        'ffn_conv_b': 0.02 * nrm((DEPTH, 2 * F)),
        'ffn_w_down': dense((DEPTH, F, D), F, DEEPNORM_BETA),
        'ple_w_proj': dense((DEPTH, PLE_DIM, D), PLE_DIM, 0.5),
        'ple_w_gate': dense((DEPTH, D, D), D),
    }


def reference(x, p, ln_gain, ln_bias, ret_w_in, ret_gn_gain, ret_w_out,
              swa_w_qkv, swa_sinks, swa_w_out, rel_bias,
              rwkv_mix, rwkv_w_rkv, rwkv_w0, rwkv_w1, rwkv_w2, rwkv_a0, rwkv_a1, rwkv_a2,
              rwkv_g1, rwkv_g2, rwkv_k_k, rwkv_k_a, rwkv_r_k, rwkv_gn_gain, rwkv_gn_bias, rwkv_w_out,
              ffn_w_up, ffn_conv_w, ffn_conv_b, ffn_w_down, ple_w_proj, ple_w_gate):
    for i in range(DEPTH):
        kind, j = i % N_MIXERS, i // N_MIXERS
        if kind == 0:
            mixed = _retention(x, ret_w_in[j], ret_gn_gain[j], ret_w_out[j])
        elif kind == 1:
            mixed = _swa_attention(x, swa_w_qkv[j], swa_sinks[j], swa_w_out[j], rel_bias)
        else:
            mixed = _rwkv7_time_mix(x, rwkv_mix[j], rwkv_w_rkv[j], rwkv_w0[j], rwkv_w1[j], rwkv_w2[j],
                                    rwkv_a0[j], rwkv_a1[j], rwkv_a2[j], rwkv_g1[j], rwkv_g2[j],
                                    rwkv_k_k[j], rwkv_k_a[j], rwkv_r_k[j], rwkv_gn_gain[j],
                                    rwkv_gn_bias[j], rwkv_w_out[j])
        x = _layer_norm(DEEPNORM_ALPHA * x + mixed, ln_gain[i, 0], ln_bias[i, 0])
        ffn = _conv_ffn(x, ffn_w_up[i], ffn_conv_w[i], ffn_conv_b[i], ffn_w_down[i])
        x = _layer_norm(DEEPNORM_ALPHA * x + ffn, ln_gain[i, 1], ln_bias[i, 1])
        x = x + (p[i] @ ple_w_proj[i]) * jax.nn.sigmoid(x @ ple_w_gate[i])
    return x
```

```python
import contextlib
import math
import numpy as np
import ml_dtypes
import concourse.bass as bass
import concourse.mybir as mybir
from concourse.bass_utils import run_bass_kernel_spmd

F32 = mybir.dt.float32
BF16 = mybir.dt.bfloat16
AF = mybir.ActivationFunctionType
ALU = mybir.AluOpType
AX = mybir.AxisListType

P = 128
D = 2048
KC = D // P
DEPTH = 4
DFF = 5504
NFB = DFF // P
PLE = 256
ALPHA = (2.0 * DEPTH) ** 0.25
LN_EPS = 1e-5
RET_H, RET_DK, RET_DV = 8, 256, 512
RET_EPS = 1e-5
RET_GAMMA = [1.0 - 2.0 ** (-5.0 - h) for h in range(RET_H)]


class Ctx:
    NDMA = {"sp": 8, "act": 4, "pool": 8}

    def __init__(self, nc, stack):
        self.nc = nc
        self.stack = stack
        self.eng = {"pe": nc.tensor, "act": nc.scalar, "dve": nc.vector,
                    "pool": nc.gpsimd, "sp": nc.sync}
        self.sems = {}
        self.val = {}
        for e in ("pe", "act", "dve", "pool"):
            self.sems[e] = stack.enter_context(nc.semaphore("c_" + e))
            self.val[e] = 0
        self.dq = {}
        self.dq_next = {}
        for q, n in self.NDMA.items():
            keys = []
            for i in range(n):
                k = "d_%s%d" % (q, i)
                self.sems[k] = stack.enter_context(nc.semaphore(k))
                self.val[k] = 0
                keys.append(k)
            self.dq[q] = keys
            self.dq_next[q] = 0
        self.sems["cc"] = stack.enter_context(nc.semaphore("cc"))
        self.val["cc"] = 0
        self.known = {e: {} for e in self.eng}
        self.lastw = {}
        self.readers = {}
        self.uid = 0
        self.n_ins = 0

    def sb(self, stack, name, shape, dtype=F32):
        self.uid += 1
        return stack.enter_context(self.nc.sbuf_tensor("%s_%d" % (name, self.uid), list(shape), dtype))

    def ps(self, stack, name, shape, dtype=F32):
        self.uid += 1
        return stack.enter_context(self.nc.psum_tensor("%s_%d" % (name, self.uid), list(shape), dtype))

    def _key(self, b):
        return b if isinstance(b, (str, tuple)) else id(b)

    def _deps(self, reads, writes):
        deps = {}
        for b in list(reads) + list(writes):
            ev = self.lastw.get(self._key(b))
            if ev is not None and deps.get(ev[0], 0) < ev[1]:
                deps[ev[0]] = ev[1]
        for b in writes:
            for k, v in self.readers.get(self._key(b), {}).items():
                if deps.get(k, 0) < v:
                    deps[k] = v
        return deps

    def _wait(self, e, deps):
        kn = self.known[e]
        for k, v in deps.items():
            if e == "pe" and k == "pe":
                continue
            if kn.get(k, 0) >= v:
                continue
            self.eng[e].wait_ge(self.sems[k], v)
            kn[k] = v

    def _commit(self, ev, reads, writes):
        k, v = ev
        for b in reads:
            self.readers.setdefault(self._key(b), {})[k] = v
        for b in writes:
            self.lastw[self._key(b)] = ev
            self.readers[self._key(b)] = {}

    def op(self, e, fn, reads=(), writes=()):
        self._wait(e, self._deps(reads, writes))
        ins = fn()
        self.val[e] += 1
        ins.then_inc(self.sems[e], 1)
        self._commit((e, self.val[e]), reads, writes)
        self.n_ins += 1
        return ins

    def dma(self, q, out, in_, reads=(), writes=(), **kw):
        deps = self._deps(reads, writes)
        i = self.dq_next[q]
        self.dq_next[q] = (i + 1) % len(self.dq[q])
        k = self.dq[q][i]
        if self.val[k] > 0:
            deps[k] = max(deps.get(k, 0), self.val[k])
        self._wait(q, deps)
        ins = self.eng[q].dma_start(out=out, in_=in_, **kw)
        self.val[k] += 16
        ins.then_inc(self.sems[k], 16)
        self._commit((k, self.val[k]), reads, writes)
        self.n_ins += 1
        return ins

    def allreduce(self, groups, in_ap, out_ap, reads=(), writes=()):
        deps = self._deps(reads, writes)
        self._wait("pool", deps)
        ins = self.nc.gpsimd.collective_compute("AllReduce", ALU.add, replica_groups=groups,
                                                ins=[in_ap.opt()], outs=[out_ap.opt()])
        self.val["cc"] += 1
        ins.then_inc(self.sems["cc"], 1)
        self._commit(("cc", self.val["cc"]), reads, writes)

    def barrier(self, engines=("pe", "act", "dve", "pool", "sp")):
        deps = {k: v for k, v in self.val.items() if v > 0}
        for e in engines:
            self._wait(e, dict(deps))
        if len(engines) == 5:
            self.lastw = {}
            self.readers = {}

    def finish(self):
        self._wait("sp", {k: v for k, v in self.val.items() if v > 0})


class Cfg:
    def __init__(self, NB=2, NSEG=4, T=2048, layers=(0, 1, 2, 3), debug=(), stop=None):
        self.stop = stop
        self.NB, self.NSEG, self.T = NB, NSEG, T
        self.layers = tuple(layers)
        self.NT = T // P
        self.ncores = NB * NSEG
        self.groups = [[b * NSEG + s for s in range(NSEG)] for b in range(NB)]
        self.debug = tuple(debug)


class Prog:
    def __init__(self, cfg):
        self.cfg = cfg
        self.nc = bass.Bass("TRN2", target_bir_lowering=False)
        self.inputs = {}
        self.scratch = {}

    def inp(self, name, shape, dtype=F32):
        if name not in self.inputs:
            self.inputs[name] = self.nc.dram_tensor(name, list(shape), dtype, kind="ExternalInput")
        return self.inputs[name]

    def scr(self, name, shape, dtype=F32):
        if name not in self.scratch:
            kind = "ExternalOutput" if name in self.cfg.debug else "Internal"
            self.scratch[name] = self.nc.dram_tensor(name, list(shape), dtype, kind=kind)
        return self.scratch[name]

    def build(self):
        cfg = self.cfg
        nc = self.nc
        T = cfg.T
        with contextlib.ExitStack() as top:
            ctx = self.ctx = Ctx(nc, top)
            self.top = top
            self.ident = ctx.sb(top, "ident", [P, P], BF16)
            ctx.dma("sp", self.ident[:], self.inp("ident", [P, P], BF16).ap(), writes=[self.ident])
            self.identf = ctx.sb(top, "identf", [P, P], F32)
            ctx.dma("sp", self.identf[:], self.inp("identf", [P, P], F32).ap(), writes=[self.identf])
            self.own = ctx.sb(top, "own", [P, cfg.NSEG], F32)
            ctx.dma("sp", self.own[:], self.inp("own", [P, cfg.NSEG]).ap(), writes=[self.own])
            self.halo_sel = ctx.sb(top, "halo_sel", [P, cfg.NSEG], F32)
            ctx.dma("sp", self.halo_sel[:], self.inp("halo_sel", [P, cfg.NSEG]).ap(), writes=[self.halo_sel])

            x_in = self.inp("x", [T, D])
            out = self.nc.dram_tensor("out", [T, D], F32, kind="ExternalOutput")
            XT = self.scr("XT", [D, T], BF16)
            cur = x_in
            self.xt_stage(cur, XT)
            for li, layer in enumerate(cfg.layers):
                kind = layer % 3
                Z1 = self.scr("Z1", [T, D])
                if kind == 0:
                    self.retention_layer(layer, cur, XT, Z1)
                elif kind == 1:
                    self.swa_layer(layer, cur, XT, Z1)
                else:
                    self.rwkv_layer(layer, cur, XT, Z1)
                if cfg.stop is not None:
                    break
                X1 = self.scr("X1", [T, D])
                XT1 = self.scr("XT1", [D, T], BF16)
                self.ln_stage(Z1, layer, 0, X1, XT1)
                Z2 = self.scr("Z2", [T, D])
                self.ffn_layer(layer, X1, XT1, Z2)
                X2 = self.scr("X2", [T, D])
                self.ln_stage(Z2, layer, 1, X2, XT)
                last = li == len(cfg.layers) - 1
                X3 = out if last else self.scr("X3_%d" % (li % 2), [T, D])
                self.ple_layer(layer, X2, XT, X3)
                if not last:
                    self.xt_stage(X3, XT)
                cur = X3
            ctx.barrier()
            ctx.finish()
        return nc

    def load_w(self, dst, src_ap, key):
        self.ctx.dma("pool", dst, src_ap, writes=[key])

    def bcast_rows(self, stack, name, src_ap_1d, n):
        t = self.ctx.sb(stack, name, [P, n], F32)
        self.ctx.dma("sp", t[:], src_ap_1d.partition_broadcast(P), writes=[t])
        return t

    def load_cols(self, stack, name, src_ap_2d, R, ps):
        ctx, nc = self.ctx, self.nc
        out = ctx.sb(stack, name, [P, R], F32)
        with contextlib.ExitStack() as st:
            r0 = 0
            while r0 < R:
                r = min(P, R - r0)
                tmp = ctx.sb(st, name + "_r", [P, P], F32)
                ctx.dma("sp", tmp[0:r, :], src_ap_2d[r0:r0 + r, :], writes=[tmp])
                ctx.op("pe", lambda: nc.tensor.matmul(ps[:, 0:r], lhsT=tmp[0:r, :], rhs=self.identf[0:r, 0:r],
                                                      start=True, stop=True), reads=[tmp, self.identf], writes=[ps])
                ctx.op("dve", lambda: nc.vector.tensor_copy(out[:, r0:r0 + r], ps[:, 0:r]), reads=[ps], writes=[out])
                r0 += r
            ctx.barrier(("pe", "dve", "sp"))
        return out

    def transpose_to(self, src_bf16_ap, pst_ap, reads, writes):
        nc = self.nc
        self.ctx.op("pe", lambda: nc.tensor.transpose(pst_ap, src_bf16_ap, self.ident[:]),
                    reads=list(reads) + [self.ident], writes=writes)

    def xt_emit_tile(self, xb, xb_key, stg, stg_key, col0, pst):
        ctx, nc = self.ctx, self.nc
        for half in range(2):
            pt = pst[half]
            for j in range(8):
                kc = half * 8 + j
                self.transpose_to(xb[:, kc * P:(kc + 1) * P], pt[:, j * P:(j + 1) * P], [xb_key], [pt])
            eng = "dve" if half == 0 else "act"
            src = pt[:].rearrange("p (j c) -> p j c", j=8)
            dst = stg[:, half * 8:(half + 1) * 8, col0:col0 + P]
            if eng == "dve":
                ctx.op("dve", lambda: nc.vector.tensor_copy(dst, src), reads=[pt], writes=[stg_key])
            else:
                ctx.op("act", lambda: nc.scalar.copy(dst, src), reads=[pt], writes=[stg_key])

    def xt_stage(self, X, XT):
        ctx, nc, cfg = self.ctx, self.nc, self.cfg
        T = cfg.T
        G = 4 if cfg.NT % 4 == 0 else 2
        with contextlib.ExitStack() as st:
            xf = [ctx.sb(st, "xf", [P, D], F32) for _ in range(2)]
            xb = [ctx.sb(st, "xb", [P, D], BF16) for _ in range(2)]
            stg = [ctx.sb(st, "stg", [P, KC, G * P], BF16) for _ in range(2)]
            pst = [[ctx.ps(st, "pst", [P, 8 * P], BF16) for _ in range(2)] for _ in range(2)]
            XTv = XT.ap().rearrange("(kc p) t -> p kc t", p=P)
            for tt in range(cfg.NT):
                b = tt % 2
                g, gi = divmod(tt, G)
                ctx.dma("sp", xf[b][:], X.ap()[tt * P:(tt + 1) * P, :], writes=[xf[b]])
                ctx.op("pool", lambda: nc.gpsimd.tensor_copy(xb[b][:], xf[b][:]), reads=[xf[b]], writes=[xb[b]])
                self.xt_emit_tile(xb[b], xb[b], stg[g % 2], stg[g % 2], gi * P, pst[b])
                if gi == G - 1:
                    ctx.dma("sp", XTv[:, :, g * G * P:(g + 1) * G * P], stg[g % 2][:], reads=[stg[g % 2]],
                            writes=[("XT", g)])
            ctx.barrier()

    def ln_stage(self, Z, layer, which, X, XT):
        ctx, nc, cfg = self.ctx, self.nc, self.cfg
        G = 4 if cfg.NT % 4 == 0 else 2
        with contextlib.ExitStack() as st:
            gain = self.bcast_rows(st, "lng", self.inp("ln_gain", [DEPTH, 2, D]).ap()[layer, which, :], D)
            bias = self.bcast_rows(st, "lnb", self.inp("ln_bias", [DEPTH, 2, D]).ap()[layer, which, :], D)
            zf = [ctx.sb(st, "zf", [P, D], F32) for _ in range(2)]
            xn = [ctx.sb(st, "xn", [P, D], F32) for _ in range(2)]
            xo = [ctx.sb(st, "xo", [P, D], F32) for _ in range(2)]
            xb = [ctx.sb(st, "xb", [P, D], BF16) for _ in range(2)]
            stats = [ctx.sb(st, "stats", [P, 4, 6], F32) for _ in range(2)]
            mv = [ctx.sb(st, "mv", [P, 4], F32) for _ in range(2)]
            stg = [ctx.sb(st, "stg", [P, KC, G * P], BF16) for _ in range(2)]
            pst = [[ctx.ps(st, "pst", [P, 8 * P], BF16) for _ in range(2)] for _ in range(2)]
            XTv = XT.ap().rearrange("(kc p) t -> p kc t", p=P)
            for tt in range(cfg.NT):
                b = tt % 2
                g, gi = divmod(tt, G)
                ctx.dma("sp", zf[b][:], Z.ap()[tt * P:(tt + 1) * P, :], writes=[zf[b]])
                self.layernorm_tile(zf[b], xn[b], stats[b], mv[b], D, LN_EPS)
                ctx.op("dve", lambda: nc.vector.tensor_tensor(xn[b][:], xn[b][:], gain[:], ALU.mult),
                       reads=[xn[b], gain], writes=[xn[b]])
                ctx.op("pool", lambda: nc.gpsimd.tensor_tensor(xo[b][:], xn[b][:], bias[:], ALU.add),
                       reads=[xn[b], bias], writes=[xo[b]])
                ctx.dma("sp", X.ap()[tt * P:(tt + 1) * P, :], xo[b][:], reads=[xo[b]], writes=[("X", tt)])
                ctx.op("act", lambda: nc.scalar.copy(xb[b][:], xo[b][:]), reads=[xo[b]], writes=[xb[b]])
                self.xt_emit_tile(xb[b], xb[b], stg[g % 2], stg[g % 2], gi * P, pst[b])
                if gi == G - 1:
                    ctx.dma("sp", XTv[:, :, g * G * P:(g + 1) * G * P], stg[g % 2][:], reads=[stg[g % 2]],
                            writes=[("XT", g)])
            ctx.barrier()

    def layernorm_tile(self, src, dst, stats, mv, n, eps, src_key=None, dst_key=None):
        ctx, nc = self.ctx, self.nc
        src_key = src if src_key is None else src_key
        dst_key = dst if dst_key is None else dst_key
        nch = max(1, n // 512)
        w = n // nch
        for c in range(nch):
            ctx.op("dve", lambda: nc.vector.bn_stats(stats[:, c, :], src[:, c * w:(c + 1) * w]),
                   reads=[src_key], writes=[stats])
        ctx.op("dve", lambda: nc.vector.bn_aggr(mv[:, 0:2], stats[:, 0:nch, :]), reads=[stats], writes=[mv])
        ctx.op("dve", lambda: nc.vector.tensor_scalar_add(mv[:, 2:3], mv[:, 1:2], eps), reads=[mv], writes=[mv])
        ctx.op("act", lambda: nc.scalar.activation(mv[:, 2:3], mv[:, 2:3], AF.Sqrt), reads=[mv], writes=[mv])
        ctx.op("dve", lambda: nc.vector.reciprocal(mv[:, 2:3], mv[:, 2:3]), reads=[mv], writes=[mv])
        ctx.op("dve", lambda: nc.vector.tensor_scalar(mv[:, 3:4], mv[:, 0:1], mv[:, 2:3], -1.0, ALU.mult, ALU.mult),
               reads=[mv], writes=[mv])
        ctx.op("act", lambda: nc.scalar.activation(dst[:, 0:n], src[:, 0:n], AF.Identity, bias=mv[:, 3:4],
                                                   scale=mv[:, 2:3]), reads=[src_key, mv], writes=[dst_key])


class GemmRes:
    def __init__(self, prog, st, kcmax, nblk, npsum, nw=2):
        ctx = prog.ctx
        self.w = [ctx.sb(st, "wbuf", [P, kcmax, nblk], BF16) for _ in range(nw)]
        self.ps = [ctx.ps(st, "gps", [P, 512], F32) for _ in range(npsum)]
        self.wi = 0
        self.pi = 0

    def next_w(self):
        w = self.w[self.wi]
        self.wi = (self.wi + 1) % len(self.w)
        return w

    def next_ps(self):
        p = self.ps[self.pi]
        self.pi = (self.pi + 1) % len(self.ps)
        return p


def _gemm_tok(self, res, AT, at_key, kcn, W, n0, ncols, tts, epi, nblk=512, kp=P):
    ctx, nc = self.ctx, self.nc
    c0 = n0
    while c0 < n0 + ncols:
        nb = min(nblk, n0 + ncols - c0)
        wb = res.next_w()
        self.load_w(wb[0:kp, 0:kcn, 0:nb], W[:, c0:c0 + nb].rearrange("(kc p) n -> p kc n", p=kp), wb)
        for tt in tts:
            ps = res.next_ps()
            for kc in range(kcn):
                ctx.op("pe", lambda: nc.tensor.matmul(ps[:, 0:nb], lhsT=AT[0:kp, kc, tt * P:(tt + 1) * P],
                                                      rhs=wb[0:kp, kc, 0:nb], start=(kc == 0), stop=(kc == kcn - 1)),
                       reads=[at_key, wb], writes=[ps])
            epi(ps, tt, c0, nb)
        c0 += nb


def _gemm_feat(self, res, AT, at_key, kcn, W, n0, ncols, tgs, epi, nblk=512):
    ctx, nc = self.ctx, self.nc
    c0 = n0
    while c0 < n0 + ncols:
        nb = min(nblk, n0 + ncols - c0)
        wb = res.next_w()
        self.load_w(wb[:, 0:kcn, 0:nb], W[:, c0:c0 + nb].rearrange("(kc p) n -> p kc n", p=P), wb)
        for (t0, tn) in tgs:
            for fb in range((nb + P - 1) // P):
                fw = min(P, nb - fb * P)
                ps = res.next_ps()
                for kc in range(kcn):
                    ctx.op("pe", lambda: nc.tensor.matmul(ps[0:fw, 0:tn], lhsT=wb[:, kc, fb * P:fb * P + fw],
                                                          rhs=AT[:, kc, t0:t0 + tn], start=(kc == 0),
                                                          stop=(kc == kcn - 1)),
                           reads=[at_key, wb], writes=[ps])
                epi(ps, c0 + fb * P, t0, tn)
        c0 += nb


Prog.gemm_tok = _gemm_tok
Prog.gemm_feat = _gemm_feat


def _load_AT(self, st, name, XT, kcn, t0, tn, pad=0):
    ctx = self.ctx
    t = ctx.sb(st, name, [P, kcn, pad + tn], BF16)
    v = XT.ap().rearrange("(kc p) t -> p kc t", p=P)
    step = max(1, kcn // 4)
    for k0 in range(0, kcn, step):
        k1 = min(kcn, k0 + step)
        ctx.dma("sp", t[:, k0:k1, pad:pad + tn], v[:, k0:k1, t0:t0 + tn], writes=[t])
    return t


Prog.load_AT = _load_AT


def _epi_resid(self, st, Xold, Zout, tok_base=0):
    ctx, nc = self.ctx, self.nc
    xo = [ctx.sb(st, "rx", [P, 512], F32) for _ in range(3)]
    zt = [ctx.sb(st, "rz", [P, 512], F32) for _ in range(3)]
    cnt = [0]

    def epi(ps, tt, c0, nb):
        i = cnt[0] % 3
        cnt[0] += 1
        r0 = tok_base + tt * P
        ctx.dma("sp", xo[i][:, 0:nb], Xold.ap()[r0:r0 + P, c0:c0 + nb], writes=[xo[i]])
        ctx.op("dve", lambda: nc.vector.scalar_tensor_tensor(zt[i][:, 0:nb], xo[i][:, 0:nb], ALPHA, ps[:, 0:nb],
                                                             ALU.mult, ALU.add),
               reads=[xo[i], ps], writes=[zt[i]])
        ctx.dma("sp", Zout.ap()[r0:r0 + P, c0:c0 + nb], zt[i][:, 0:nb], reads=[zt[i]], writes=[("Z", r0, c0)])
    return epi


Prog.epi_resid = _epi_resid


def _retention_layer(self, layer, X, XT, Z1):
    ctx, nc, cfg = self.ctx, self.nc, self.cfg
    T, NT, NSEG = cfg.T, cfg.NT, cfg.NSEG
    j = layer // 3
    w_in = self.inp("ret_w_in_%d" % j, [D, 12288]).ap()
    w_out = self.inp("ret_w_out_%d" % j, [4096, D]).ap()
    gn_ap = self.inp("ret_gn_gain_%d" % j, [4096]).ap()
    KTs = self.scr("ret_KT", [D, T], BF16)
    Vs = self.scr("ret_V", [T, 4096], BF16)
    OGT = self.scr("ret_OGT", [4096, T], BF16)
    CCI = self.scr("ret_cci", [RET_H, NSEG, 2, P, 512])
    CCO = self.scr("ret_cco", [RET_H, NSEG, 2, P, 512])
    TH = min(1024, T)
    NTH = TH // P
    TGW = min(512, TH)
    tgs = [(t0, TGW) for t0 in range(0, TH, TGW)]
    QOFF, KOFF, VOFF, GOFF = 0, 2048, 4096, 8192
    rope_cos = self.inp("rope_cos", [P, T]).ap()
    rope_sin = self.inp("rope_sin", [P, T]).ap()

    def rotary_epi(cosT, sinT, dstT, tmp):
        state = {}

        def epi(ps, c, t0, tn):
            half = (c // P) % 2
            if half == 0:
                state["A"] = ps
                return
            psA, psB = state["A"], ps
            t1, t2, t3, t4 = tmp
            cs, sn = cosT[:, t0:t0 + tn], sinT[:, t0:t0 + tn]
            ctx.op("dve", lambda: nc.vector.tensor_tensor(t1[:, 0:tn], psA[:, 0:tn], cs, ALU.mult),
                   reads=[psA, cosT], writes=[t1])
            ctx.op("dve", lambda: nc.vector.tensor_tensor(t2[:, 0:tn], psB[:, 0:tn], sn, ALU.mult),
                   reads=[psB, sinT], writes=[t2])
            ctx.op("dve", lambda: nc.vector.tensor_tensor(t3[:, 0:tn], psA[:, 0:tn], sn, ALU.mult),
                   reads=[psA, sinT], writes=[t3])
            ctx.op("dve", lambda: nc.vector.tensor_tensor(t4[:, 0:tn], psB[:, 0:tn], cs, ALU.mult),
                   reads=[psB, cosT], writes=[t4])
            ctx.op("pool", lambda: nc.gpsimd.tensor_tensor(dstT[:, 0, t0:t0 + tn], t1[:, 0:tn], t2[:, 0:tn],
                                                           ALU.subtract), reads=[t1, t2], writes=[dstT])
            ctx.op("pool", lambda: nc.gpsimd.tensor_tensor(dstT[:, 1, t0:t0 + tn], t3[:, 0:tn], t4[:, 0:tn],
                                                           ALU.add), reads=[t3, t4], writes=[dstT])
        return epi

    KTv = KTs.ap().rearrange("(h two p) t -> h p two t", two=2, p=P)
    Vv = Vs.ap().rearrange("(tt p) e -> p tt e", p=P)
    OGTv = OGT.ap().rearrange("(h fc p) t -> h p fc t", fc=4, p=P)

    with contextlib.ExitStack() as st:
        kdec = ctx.sb(st, "kdec", [P, RET_H], F32)
        ctx.dma("sp", kdec[:], self.inp("ret_kdec", [P, RET_H]).ap(), writes=[kdec])
        res = GemmRes(self, st, KC, 512, 3)
        tmp = [ctx.sb(st, "rt", [P, 512], F32) for _ in range(4)]
        kT = [ctx.sb(st, "kT", [P, 2, TH], BF16) for _ in range(2)]
        vh = [ctx.sb(st, "vh", [P, NTH, 512], BF16) for _ in range(2)]
        kdA = [ctx.sb(st, "kdA", [P, 2 * P], BF16) for _ in range(2)]
        pst = [ctx.ps(st, "pst", [P, 2 * P], BF16) for _ in range(2)]
        Lps = [ctx.ps(st, "Lps", [P, 512], F32) for _ in range(2)]
        Lacc = ctx.sb(st, "Lacc", [P, RET_H, 2, 512], F32)
        Lm = [ctx.sb(st, "Lm", [P, 2, 512], F32) for _ in range(2)]
        cosk = ctx.sb(st, "cosk", [P, TH], F32)
        sink = ctx.sb(st, "sink", [P, TH], F32)
        xT = ctx.sb(st, "xT", [P, KC, TH], BF16)
        XTv = XT.ap().rearrange("(kc p) t -> p kc t", p=P)
        for th in range(T // TH):
            t0h = th * TH
            for k0 in range(0, KC, 4):
                ctx.dma("sp", xT[:, k0:k0 + 4, :], XTv[:, k0:k0 + 4, t0h:t0h + TH], writes=[xT])
            ctx.dma("sp", cosk[:], rope_cos[:, t0h:t0h + TH], writes=[cosk])
            ctx.dma("sp", sink[:], rope_sin[:, t0h:t0h + TH], writes=[sink])
            ctx.op("pool", lambda: nc.gpsimd.tensor_scalar_mul(cosk[:], cosk[:], RET_DK ** -0.5), reads=[cosk], writes=[cosk])
            ctx.op("pool", lambda: nc.gpsimd.tensor_scalar_mul(sink[:], sink[:], RET_DK ** -0.5), reads=[sink], writes=[sink])
            for h in range(RET_H):
                kTh, vhh = kT[h % 2], vh[h % 2]
                self.gemm_feat(res, xT, xT, KC, w_in, KOFF + h * 256, 256, tgs, rotary_epi(cosk, sink, kTh, tmp), nblk=256)
                ctx.dma("sp", KTv[h][:, :, t0h:t0h + TH], kTh[:], reads=[kTh], writes=[("KT", h, th)])

                def v_epi(ps, tt, c0, nb):
                    ctx.op("act", lambda: nc.scalar.copy(vhh[:, tt, :], ps[:, 0:nb]), reads=[ps], writes=[vhh])
                self.gemm_tok(res, xT, xT, KC, w_in, VOFF + h * 512, 512, range(NTH), v_epi)
                ctx.dma("sp", Vv[:, th * NTH:(th + 1) * NTH, h * 512:(h + 1) * 512], vhh[:],
                        reads=[vhh], writes=[("V", h, th)])
                if NSEG > 1:
                    g = RET_GAMMA[h]
                    for cl in range(NTH):
                        c = th * NTH + cl
                        pt = pst[cl % 2]
                        kd = kdA[cl % 2]
                        for half in range(2):
                            self.transpose_to(kTh[:, half, cl * P:(cl + 1) * P], pt[:, half * P:(half + 1) * P], [kTh], [pt])
                        ctx.op("dve", lambda: nc.vector.tensor_scalar(kd[:], pt[:], kdec[:, h:h + 1],
                                                                      float(g ** (P * (NT - 1 - c))), ALU.mult, ALU.mult),
                               reads=[pt, kdec], writes=[kd])
                        for half in range(2):
                            ctx.op("pe", lambda: nc.tensor.matmul(Lps[half][:], lhsT=kd[:, half * P:(half + 1) * P],
                                                                  rhs=vhh[:, cl, :], start=(cl == 0), stop=(cl == NTH - 1)),
                                   reads=[kd, vhh], writes=[Lps[half]])
                    for half in range(2):
                        if th == 0:
                            ctx.op("act", lambda: nc.scalar.copy(Lacc[:, h, half, :], Lps[half][:]),
                                   reads=[Lps[half]], writes=[(id(Lacc), h)])
                        else:
                            ctx.op("dve", lambda: nc.vector.tensor_tensor(Lacc[:, h, half, :], Lacc[:, h, half, :],
                                                                          Lps[half][:], ALU.add),
                                   reads=[Lps[half], (id(Lacc), h)], writes=[(id(Lacc), h)])
        if NSEG > 1:
            for h in range(RET_H):
                for s in range(NSEG):
                    lm = Lm[(h * NSEG + s) % 2]
                    ctx.op("dve", lambda: nc.vector.tensor_scalar_mul(lm[:], Lacc[:, h], self.own[:, s:s + 1]),
                           reads=[(id(Lacc), h), self.own], writes=[lm])
                    ctx.dma("sp", CCI.ap()[h, s].rearrange("two p e -> p two e"), lm[:], reads=[lm],
                            writes=[("CCI", s, h)])
        ctx.barrier()
    if NSEG > 1:
        for h in range(RET_H):
            ctx.allreduce(cfg.groups, CCI.ap()[h].rearrange("s two p e -> (s two p) e"),
                          CCO.ap()[h].rearrange("s two p e -> (s two p) e"), writes=[("CCO", h)])
        ctx.barrier()

    with contextlib.ExitStack() as st:
        kdec = ctx.sb(st, "kdec", [P, RET_H], F32)
        ctx.dma("sp", kdec[:], self.inp("ret_kdec", [P, RET_H]).ap(), writes=[kdec])
        maskT = ctx.sb(st, "maskT", [P, RET_H, P], F32)
        ctx.dma("sp", maskT[:], self.inp("ret_maskT", [P, RET_H, P]).ap(), writes=[maskT])
        qdec = ctx.sb(st, "qdec", [P, RET_H, P], F32)
        ctx.dma("sp", qdec[:], self.inp("ret_qdec", [P, RET_H, P]).ap(), writes=[qdec])
        coef = ctx.sb(st, "coef", [P, NSEG, RET_H], F32)
        ctx.dma("sp", coef[:], self.inp("ret_coef", [P, NSEG, RET_H]).ap(), writes=[coef])
        gain = self.bcast_rows(st, "gng", gn_ap, 4096)
        res = GemmRes(self, st, KC, 512, 2)
        tmp = [ctx.sb(st, "rt", [P, 512], F32) for _ in range(4)]
        cosT = ctx.sb(st, "cos", [P, TH], F32)
        sinT = ctx.sb(st, "sin", [P, TH], F32)
        xT = ctx.sb(st, "xT", [P, KC, TH], BF16)
        XTv = XT.ap().rearrange("(kc p) t -> p kc t", p=P)
        kTh = ctx.sb(st, "kT", [P, 2, TH], BF16)
        qTh = ctx.sb(st, "qT", [P, 2, TH], BF16)
        vhh = ctx.sb(st, "vh", [P, NTH, 512], BF16)
        gsh = ctx.sb(st, "gs", [P, NTH, 512], BF16)
        ogTh = ctx.sb(st, "ogT", [P, 4, TH], BF16)
        Rall = ctx.sb(st, "Rall", [P, RET_H, 2, 512], F32)
        Rb = ctx.sb(st, "Rb", [P, 2, 512], BF16)
        cin = [ctx.sb(st, "cin", [P, 2, 512], F32) for _ in range(2)]
        sT = [ctx.sb(st, "sT", [P, P], BF16) for _ in range(2)]
        qd = [ctx.sb(st, "qd", [P, 2, P], BF16) for _ in range(2)]
        kd = [ctx.sb(st, "kd", [P, 2 * P], BF16) for _ in range(2)]
        on = [ctx.sb(st, "on", [P, 512], F32) for _ in range(2)]
        og = [ctx.sb(st, "og", [P, 512], F32) for _ in range(2)]
        og2 = [ctx.sb(st, "og2", [P, 512], BF16) for _ in range(2)]
        stats = [ctx.sb(st, "stats", [P, 4, 6], F32) for _ in range(2)]
        mv = [ctx.sb(st, "mv", [P, 4], F32) for _ in range(2)]
        ps_s = ctx.ps(st, "ps_s", [P, 512], F32)
        ps_o = ctx.ps(st, "ps_o", [P, 512], F32)
        ps_t = ctx.ps(st, "ps_t", [P, 2 * P], BF16)
        ps_R = [ctx.ps(st, "ps_R", [P, 512], F32) for _ in range(2)]
        ps_g = ctx.ps(st, "ps_g", [P, 4 * P], BF16)
        for h in range(RET_H):
            Rk = (id(Rall), h)
            if NSEG > 1:
                for s in range(NSEG):
                    ci = cin[s % 2]
                    ctx.dma("sp", ci[:], CCO.ap()[h, s].rearrange("two p e -> p two e"), writes=[ci])
                    if s == 0:
                        ctx.op("dve", lambda: nc.vector.tensor_scalar_mul(Rall[:, h], ci[:], coef[:, s, h:h + 1]),
                               reads=[ci, coef], writes=[Rk])
                    else:
                        ctx.op("dve", lambda: nc.vector.scalar_tensor_tensor(Rall[:, h], ci[:], coef[:, s, h:h + 1],
                                                                             Rall[:, h], ALU.mult, ALU.add),
                               reads=[ci, coef, Rk], writes=[Rk])
            else:
                ctx.op("dve", lambda: nc.vector.memset(Rall[:, h], 0.0), writes=[Rk])
        for th in range(T // TH):
            t0h = th * TH
            for k0 in range(0, KC, 4):
                ctx.dma("sp", xT[:, k0:k0 + 4, :], XTv[:, k0:k0 + 4, t0h:t0h + TH], writes=[xT])
            ctx.dma("sp", cosT[:], rope_cos[:, t0h:t0h + TH], writes=[cosT])
            ctx.dma("sp", sinT[:], rope_sin[:, t0h:t0h + TH], writes=[sinT])
            for h in range(RET_H):
                Rk = (id(Rall), h)
                ctx.dma("sp", kTh[:], KTv[h][:, :, t0h:t0h + TH], writes=[kTh])
                ctx.dma("sp", vhh[:], Vv[:, th * NTH:(th + 1) * NTH, h * 512:(h + 1) * 512], writes=[vhh])
                ctx.op("act", lambda: nc.scalar.copy(Rb[:], Rall[:, h]), reads=[Rk], writes=[Rb])
                self.gemm_feat(res, xT, xT, KC, w_in, QOFF + h * 256, 256, tgs, rotary_epi(cosT, sinT, qTh, tmp), nblk=256)

                def g_epi(ps, tt, c0, nb):
                    ctx.op("act", lambda: nc.scalar.activation(gsh[:, tt, :], ps[:, 0:nb], AF.Silu), reads=[ps], writes=[gsh])
                self.gemm_tok(res, xT, xT, KC, w_in, GOFF + h * 512, 512, range(NTH), g_epi)
                gam = RET_GAMMA[h]
                for cl in range(NTH):
                    cb = cl % 2
                    cs = slice(cl * P, (cl + 1) * P)
                    for half in range(2):
                        ctx.op("pe", lambda: nc.tensor.matmul(ps_s[:, 0:P], lhsT=kTh[:, half, cs], rhs=qTh[:, half, cs],
                                                              start=(half == 0), stop=(half == 1)),
                               reads=[kTh, qTh], writes=[ps_s])
                    ctx.op("dve", lambda: nc.vector.tensor_tensor(sT[cb][:], ps_s[:, 0:P], maskT[:, h, :], ALU.mult),
                           reads=[ps_s, maskT], writes=[sT[cb]])
                    for half in range(2):
                        ctx.op("pool", lambda: nc.gpsimd.tensor_tensor(qd[cb][:, half, :], qTh[:, half, cs],
                                                                       qdec[:, h, :], ALU.mult),
                               reads=[qTh, qdec], writes=[qd[cb]])
                    ctx.op("pe", lambda: nc.tensor.matmul(ps_o[:], lhsT=sT[cb][:], rhs=vhh[:, cl, :], start=True, stop=False),
                           reads=[sT[cb], vhh], writes=[ps_o])
                    for half in range(2):
                        ctx.op("pe", lambda: nc.tensor.matmul(ps_o[:], lhsT=qd[cb][:, half, :], rhs=Rb[:, half, :],
                                                              start=False, stop=(half == 1)),
                               reads=[qd[cb], Rb], writes=[ps_o])
                    for half in range(2):
                        self.transpose_to(kTh[:, half, cs], ps_t[:, half * P:(half + 1) * P], [kTh], [ps_t])
                    ctx.op("dve", lambda: nc.vector.tensor_scalar_mul(kd[cb][:], ps_t[:], kdec[:, h:h + 1]),
                           reads=[ps_t, kdec], writes=[kd[cb]])
                    for half in range(2):
                        ctx.op("pe", lambda: nc.tensor.matmul(ps_R[half][:], lhsT=kd[cb][:, half * P:(half + 1) * P],
                                                              rhs=vhh[:, cl, :], start=True, stop=True),
                               reads=[kd[cb], vhh], writes=[ps_R[half]])
                        ctx.op("dve", lambda: nc.vector.scalar_tensor_tensor(Rall[:, h, half, :], Rall[:, h, half, :],
                                                                             float(gam ** P), ps_R[half][:],
                                                                             ALU.mult, ALU.add),
                               reads=[Rk, ps_R[half]], writes=[Rk])
                    ctx.op("act", lambda: nc.scalar.copy(Rb[:], Rall[:, h]), reads=[Rk], writes=[Rb])
                    self.layernorm_tile(ps_o, on[cb], stats[cb], mv[cb], 512, RET_EPS)
                    ctx.op("dve", lambda: nc.vector.tensor_tensor(og[cb][:], on[cb][:], gain[:, h * 512:(h + 1) * 512], ALU.mult),
                           reads=[on[cb], gain], writes=[og[cb]])
                    ctx.op("pool", lambda: nc.gpsimd.tensor_tensor(og2[cb][:], og[cb][:], gsh[:, cl, :], ALU.mult),
                           reads=[og[cb], gsh], writes=[og2[cb]])
                    for fc in range(4):
                        self.transpose_to(og2[cb][:, fc * P:(fc + 1) * P], ps_g[:, fc * P:(fc + 1) * P], [og2[cb]], [ps_g])
                    ctx.op("act", lambda: nc.scalar.copy(ogTh[:, :, cs], ps_g[:].rearrange("p (f c) -> p f c", f=4)),
                           reads=[ps_g], writes=[ogTh])
                ctx.dma("sp", OGTv[h][:, :, t0h:t0h + TH], ogTh[:], reads=[ogTh], writes=[("OGT", h, th)])
        ctx.barrier()

    for t0 in range(0, T, TH):
        with contextlib.ExitStack() as st:
            aT = self.load_AT(st, "ogTa", OGT, 32, t0, TH)
            res = GemmRes(self, st, 32, 512, 3)
            epi = self.epi_resid(st, X, Z1, tok_base=t0)
            self.gemm_tok(res, aT, aT, 32, w_out, 0, D, range(TH // P), epi)
            ctx.barrier()


Prog.retention_layer = _retention_layer
def _halo_rows(self, st, Xsrc, nrows, name):
    ctx, nc, cfg = self.ctx, self.nc, self.cfg
    NSEG, T = cfg.NSEG, cfg.T
    halo = ctx.sb(st, name, [nrows, D], F32)
    if NSEG == 1:
        ctx.op("dve", lambda: nc.vector.memset(halo[:], 0.0), writes=[halo])
        return halo
    HCI = self.scr("halo_ci_%d" % nrows, [NSEG, nrows, D])
    HCO = self.scr("halo_co_%d" % nrows, [NSEG, nrows, D])
    hx = ctx.sb(st, name + "_x", [nrows, D], F32)
    hm = [ctx.sb(st, name + "_m", [nrows, D], F32) for _ in range(2)]
    ctx.dma("sp", hx[:], Xsrc.ap()[T - nrows:T, :], writes=[hx])
    for s in range(NSEG):
        ctx.op("dve", lambda: nc.vector.tensor_scalar_mul(hm[s % 2][:], hx[:], self.own[0:nrows, s:s + 1]),
               reads=[hx, self.own], writes=[hm[s % 2]])
        ctx.dma("sp", HCI.ap()[s], hm[s % 2][:], reads=[hm[s % 2]], writes=[("HCI", s)])
    ctx.barrier()
    ctx.allreduce(cfg.groups, HCI.ap().rearrange("s r d -> (s r) d"), HCO.ap().rearrange("s r d -> (s r) d"),
                  writes=[("HCO",)])
    ctx.barrier()
    for s in range(NSEG):
        ctx.dma("sp", hm[s % 2][:], HCO.ap()[s], writes=[hm[s % 2]])
        if s == 0:
            ctx.op("dve", lambda: nc.vector.tensor_scalar_mul(halo[:], hm[s % 2][:], self.halo_sel[0:nrows, s:s + 1]),
                   reads=[hm[s % 2], self.halo_sel], writes=[halo])
        else:
            ctx.op("dve", lambda: nc.vector.scalar_tensor_tensor(halo[:], hm[s % 2][:], self.halo_sel[0:nrows, s:s + 1],
                                                                 halo[:], ALU.mult, ALU.add),
                   reads=[hm[s % 2], self.halo_sel, halo], writes=[halo])
    return halo


Prog.halo_rows = _halo_rows


def _ffn_layer(self, layer, X1, XT1, Z2):
    ctx, nc, cfg = self.ctx, self.nc, self.cfg
    T = cfg.T
    w_up = self.inp("ffn_w_up_%d" % layer, [D, 2 * DFF]).ap()
    w_dn = self.inp("ffn_w_down_%d" % layer, [DFF, D]).ap()
    cw_ap = self.inp("ffn_conv_w_%d" % layer, [3, 2 * DFF]).ap().rearrange("t (b p) -> (t b) p", p=P)
    cb_ap = self.inp("ffn_conv_b_%d" % layer, [2 * DFF]).ap().rearrange("(b p) -> b p", p=P)
    NB2 = 2 * NFB
    TG = min(1024, T)
    W = TG + 2
    nsub = (W + 511) // 512
    bounds = [(W * i) // nsub for i in range(nsub + 1)]
    with contextlib.ExitStack() as st0:
        psc = ctx.ps(st0, "psc", [P, 512], F32)
        cw = self.load_cols(st0, "cw", cw_ap, 3 * NB2, psc)
        cb = self.load_cols(st0, "cb", cb_ap, NB2, psc)
        haloT = ctx.sb(st0, "haloT", [P, KC, 2], BF16)
        with contextlib.ExitStack() as sth:
            halo = self.halo_rows(sth, X1, 2, "halo2")
            for kc in range(KC):
                ctx.op("pe", lambda: nc.tensor.matmul(psc[:, kc * 2:kc * 2 + 2], lhsT=halo[0:2, kc * P:(kc + 1) * P],
                                                      rhs=self.identf[0:2, 0:2], start=True, stop=True),
                       reads=[halo, self.identf], writes=[psc])
            ctx.op("dve", lambda: nc.vector.tensor_copy(haloT[:], psc[:, 0:2 * KC].rearrange("p (k c) -> p k c", c=2)),
                   reads=[psc], writes=[haloT])
            ctx.barrier()
        gT = ctx.sb(st0, "gT", [P, NFB, TG], BF16)
        XTv = XT1.ap().rearrange("(kc p) t -> p kc t", p=P)
        for g in range(T // TG):
            t0 = g * TG
            with contextlib.ExitStack() as st:
                xT = ctx.sb(st, "x1T", [P, KC, W], BF16)
                for k0 in range(0, KC, 4):
                    ctx.dma("sp", xT[:, k0:k0 + 4, 2:W], XTv[:, k0:k0 + 4, t0:t0 + TG], writes=[xT])
                if g == 0:
                    ctx.op("pool", lambda: nc.gpsimd.tensor_copy(xT[:, :, 0:2], haloT[:]), reads=[haloT], writes=[xT])
                else:
                    ctx.dma("sp", xT[:, :, 0:2], XTv[:, :, t0 - 2:t0], writes=[xT])
                wu = [ctx.sb(st, "wu", [P, KC, 256], BF16) for _ in range(2)]
                wg = [ctx.sb(st, "wg", [P, KC, 256], BF16) for _ in range(2)]
                pss = [ctx.ps(st, "fps", [P, 512], F32) for _ in range(6)]
                hs = [ctx.sb(st, "hs", [P, W], F32) for _ in range(2)]
                acc = [ctx.sb(st, "acc", [P, TG], F32) for _ in range(2)]
                sg = ctx.sb(st, "sg", [P, TG], F32)
                pi = 0
                for fb in range(NFB):
                    if fb % 2 == 0:
                        wi = (fb // 2) % 2
                        nbk = min(256, DFF - fb * P)
                        self.load_w(wu[wi][:, :, 0:nbk], w_up[:, fb * P:fb * P + nbk].rearrange("(kc p) n -> p kc n", p=P), wu[wi])
                        self.load_w(wg[wi][:, :, 0:nbk], w_up[:, DFF + fb * P:DFF + fb * P + nbk].rearrange("(kc p) n -> p kc n", p=P), wg[wi])
                    wi = (fb // 2) % 2
                    fo = (fb % 2) * P
                    for ui, wt in enumerate((wu[wi], wg[wi])):
                        for si in range(nsub):
                            a, b_ = bounds[si], bounds[si + 1]
                            ps = pss[pi % 6]
                            pi += 1
                            for kc in range(KC):
                                ctx.op("pe", lambda: nc.tensor.matmul(ps[:, 0:b_ - a], lhsT=wt[:, kc, fo:fo + P],
                                                                      rhs=xT[:, kc, a:b_], start=(kc == 0), stop=(kc == KC - 1)),
                                       reads=[wt, xT], writes=[ps])
                            ctx.op("act", lambda: nc.scalar.copy(hs[ui][:, a:b_], ps[:, 0:b_ - a]), reads=[ps], writes=[hs[ui]])
                        blk = fb if ui == 0 else NFB + fb
                        w0 = cw[:, 0 * NB2 + blk:0 * NB2 + blk + 1]
                        w1 = cw[:, 1 * NB2 + blk:1 * NB2 + blk + 1]
                        w2 = cw[:, 2 * NB2 + blk:2 * NB2 + blk + 1]
                        ctx.op("act", lambda: nc.scalar.activation(acc[ui][:], hs[ui][:, 2:W], AF.Identity,
                                                                   bias=cb[:, blk:blk + 1], scale=w2),
                               reads=[hs[ui], cw, cb], writes=[acc[ui]])
                        ctx.op("dve", lambda: nc.vector.scalar_tensor_tensor(acc[ui][:], hs[ui][:, 1:W - 1], w1, acc[ui][:],
                                                                             ALU.mult, ALU.add),
                               reads=[hs[ui], cw, acc[ui]], writes=[acc[ui]])
                        ctx.op("dve", lambda: nc.vector.scalar_tensor_tensor(acc[ui][:], hs[ui][:, 0:W - 2], w0, acc[ui][:],
                                                                             ALU.mult, ALU.add),
                               reads=[hs[ui], cw, acc[ui]], writes=[acc[ui]])
                    ctx.op("act", lambda: nc.scalar.activation(sg[:], acc[1][:], AF.Silu), reads=[acc[1]], writes=[sg])
                    ctx.op("pool", lambda: nc.gpsimd.tensor_tensor(gT[:, fb, :], sg[:], acc[0][:], ALU.mult),
                           reads=[sg, acc[0]], writes=[gT])
                ctx.barrier()
            with contextlib.ExitStack() as st:
                res = GemmRes(self, st, NFB, 256, 4)
                epi = self.epi_resid(st, X1, Z2, tok_base=t0)
                self.gemm_tok(res, gT, gT, NFB, w_dn, 0, D, range(TG // P), epi, nblk=256)
                ctx.barrier()


Prog.ffn_layer = _ffn_layer


def _ple_layer(self, layer, X2, XT2, X3):
    ctx, nc, cfg = self.ctx, self.nc, self.cfg
    T, NT = cfg.T, cfg.NT
    w_gate = self.inp("ple_w_gate_%d" % layer, [D, D]).ap()
    w_proj = self.inp("ple_w_proj_%d" % layer, [PLE, D]).ap()
    pT_in = self.inp("pT_%d" % layer, [PLE, T]).ap()
    with contextlib.ExitStack() as st:
        xT = self.load_AT(st, "x2T", XT2, KC, 0, T)
        pT = ctx.sb(st, "pT", [P, 2, T], BF16)
        self.load_w(pT[:], pT_in.rearrange("(kc p) t -> p kc t", p=P), pT)
        wg = [ctx.sb(st, "wg", [P, KC, 512], BF16) for _ in range(2)]
        wp = [ctx.sb(st, "wp", [P, 2, 512], BF16) for _ in range(2)]
        psg = [ctx.ps(st, "psg", [P, 512], F32) for _ in range(3)]
        psp = [ctx.ps(st, "psp", [P, 512], F32) for _ in range(3)]
        sg = [ctx.sb(st, "sg", [P, 512], F32) for _ in range(3)]
        x2 = [ctx.sb(st, "x2", [P, 512], F32) for _ in range(3)]
        x3 = [ctx.sb(st, "x3", [P, 512], F32) for _ in range(3)]
        it = 0
        for ci, c0 in enumerate(range(0, D, 512)):
            wgi, wpi = wg[ci % 2], wp[ci % 2]
            self.load_w(wgi[:], w_gate[:, c0:c0 + 512].rearrange("(kc p) n -> p kc n", p=P), wgi)
            self.load_w(wpi[:], w_proj[:, c0:c0 + 512].rearrange("(kc p) n -> p kc n", p=P), wpi)
            for tt in range(NT):
                i = it % 3
                it += 1
                ts = slice(tt * P, (tt + 1) * P)
                for kc in range(KC):
                    ctx.op("pe", lambda: nc.tensor.matmul(psg[i][:], lhsT=xT[:, kc, ts], rhs=wgi[:, kc, :],
                                                          start=(kc == 0), stop=(kc == KC - 1)),
                           reads=[xT, wgi], writes=[psg[i]])
                for kc in range(2):
                    ctx.op("pe", lambda: nc.tensor.matmul(psp[i][:], lhsT=pT[:, kc, ts], rhs=wpi[:, kc, :],
                                                          start=(kc == 0), stop=(kc == 1)),
                           reads=[pT, wpi], writes=[psp[i]])
                ctx.dma("sp", x2[i][:], X2.ap()[tt * P:(tt + 1) * P, c0:c0 + 512], writes=[x2[i]])
                ctx.op("act", lambda: nc.scalar.activation(sg[i][:], psg[i][:], AF.Sigmoid), reads=[psg[i]], writes=[sg[i]])
                ctx.op("dve", lambda: nc.vector.tensor_tensor(sg[i][:], sg[i][:], psp[i][:], ALU.mult),
                       reads=[sg[i], psp[i]], writes=[sg[i]])
                ctx.op("pool", lambda: nc.gpsimd.tensor_tensor(x3[i][:], sg[i][:], x2[i][:], ALU.add),
                       reads=[sg[i], x2[i]], writes=[x3[i]])
                ctx.dma("sp", X3.ap()[tt * P:(tt + 1) * P, c0:c0 + 512], x3[i][:], reads=[x3[i]], writes=[("X3", tt, c0)])
        ctx.barrier()


Prog.ple_layer = _ple_layer
SWA_HQ, SWA_HKV, SWA_HD, SWA_W = 32, 4, 64, 128
NEG = -1e30


def _swa_head_order():
    order = []
    for pair in range(2):
        for g in range(8):
            order.append((2 * pair) * 8 + g)
            order.append((2 * pair + 1) * 8 + g)
    return order


def _t5_bucket(n):
    max_exact = 16
    if n < max_exact:
        return n
    large = max_exact + int(np.log(max(n, 1) / max_exact) / np.log(SWA_W / max_exact) * (32 - max_exact))
    return min(large, 31)


def _swa_consts(cfg, core, c):
    seg = core % cfg.NSEG
    E = np.zeros((32, 383), np.float32)
    for u in range(383):
        d = u - 127
        if 0 <= d < SWA_W:
            nn = np.maximum(np.array([d]), 0)
            large = 16 + (np.log(np.maximum(nn, 1) / 16) / np.log(SWA_W / 16) * 16).astype(np.int32)
            large = np.minimum(large, 31)
            b = int(np.where(nn < 16, nn, large)[0])
            E[b, u] = 1.0
    c["swa_E"] = E
    i = np.arange(P)[:, None]
    j = np.arange(2 * P)[None, :]
    d = i + P - j
    c["swa_maskc"] = np.where((d >= 0) & (d < SWA_W), 0.0, NEG).astype(np.float32)
    mf = np.zeros((P, 2 * P), np.float32)
    if seg == 0:
        mf[:, :P] = NEG
    c["swa_mask_first"] = mf


def _swa_layer(self, layer, X, XT, Z1):
    ctx, nc, cfg = self.ctx, self.nc, self.cfg
    T, NT, NSEG = cfg.T, cfg.NT, cfg.NSEG
    j = layer // 3
    w_qkv = self.inp("swa_w_qkv_p%d" % j, [D, 2560]).ap()
    w_out = self.inp("swa_w_out_p%d" % j, [D, D]).ap()
    sinks_ap = self.inp("swa_sinks_%d" % j, [SWA_HQ]).ap()
    relb_ap = self.inp("rel_bias", [32, SWA_HQ]).ap()
    OT = self.scr("swa_OT", [D, T], BF16)
    QTs = self.scr("swa_QT", [D, T], BF16)
    order = _swa_head_order()
    TGW = min(512, T)
    tgs = [(t0, TGW) for t0 in range(0, T, TGW)]
    with contextlib.ExitStack() as st0:
        kT = ctx.sb(st0, "kT", [P, 2, P + T], BF16)
        vS = ctx.sb(st0, "vS", [P, 1 + NT, 256], BF16)
        biasS = ctx.sb(st0, "biasS", [P, SWA_HQ, 2 * P], F32)
        sinkb = self.bcast_rows(st0, "sinkb", sinks_ap, SWA_HQ)
        mfirst = ctx.sb(st0, "mfirst", [P, 2 * P], F32)
        ctx.dma("sp", mfirst[:], self.inp("swa_mask_first", [P, 2 * P]).ap(), writes=[mfirst])
        with contextlib.ExitStack() as st:
            E = ctx.sb(st, "E", [32, 383], F32)
            RB = ctx.sb(st, "RB", [32, SWA_HQ], F32)
            maskc = ctx.sb(st, "maskc", [P, 2 * P], F32)
            ctx.dma("sp", E[:], self.inp("swa_E", [32, 383]).ap(), writes=[E])
            ctx.dma("sp", RB[:], relb_ap, writes=[RB])
            ctx.dma("sp", maskc[:], self.inp("swa_maskc", [P, 2 * P]).ap(), writes=[maskc])
            psb = [ctx.ps(st, "psb", [P, 512], F32) for _ in range(2)]
            for r in range(16):
                ps = psb[r % 2]
                for jj in range(16):
                    jk = r * 16 + jj
                    ctx.op("pe", lambda: nc.tensor.matmul(ps[:, jj * 32:(jj + 1) * 32], lhsT=E[:, 255 - jk:383 - jk], rhs=RB[:],
                                                          start=True, stop=True), reads=[E, RB], writes=[ps])
                ctx.op("dve", lambda: nc.vector.tensor_tensor(
                    biasS[:, :, r * 16:(r + 1) * 16].rearrange("p h j -> p j h"),
                    ps[:].rearrange("p (j h) -> p j h", h=32),
                    maskc[:, r * 16:(r + 1) * 16].unsqueeze(2).broadcast_to([P, 16, 32]), ALU.add),
                    reads=[ps, maskc], writes=[biasS])
            ctx.barrier()
        with contextlib.ExitStack() as st:
            xT = self.load_AT(st, "xT", XT, KC, 0, T)
            res = GemmRes(self, st, KC, 512, 3)
            qst = [ctx.sb(st, "qst", [P, 4, TGW], BF16) for _ in range(2)]
            QTv = QTs.ap().rearrange("(kc p) t -> p kc t", p=P)
            qcnt = [0]

            def q_epi(ps, c, t0, tn):
                kc = c // P
                qs = qst[(qcnt[0] // 4) % 2]
                ctx.op("act", lambda: nc.scalar.activation(qs[:, kc % 4, 0:tn], ps[:, 0:tn], AF.Copy, scale=SWA_HD ** -0.5),
                       reads=[ps], writes=[qs])
                qcnt[0] += 1
                if kc % 4 == 3:
                    ctx.dma("sp", QTv[:, kc - 3:kc + 1, t0:t0 + tn], qs[:, :, 0:tn], reads=[qs], writes=[("QT", kc, t0)])
            self.gemm_feat(res, xT, xT, KC, w_qkv, 0, 2048, tgs, q_epi)

            def k_epi(ps, c, t0, tn):
                fb = (c - 2048) // P
                ctx.op("act", lambda: nc.scalar.copy(kT[:, fb, P + t0:P + t0 + tn], ps[:, 0:tn]), reads=[ps], writes=[kT])
            self.gemm_feat(res, xT, xT, KC, w_qkv, 2048, 256, tgs, k_epi, nblk=256)

            def v_epi(ps, tt, c0, nb):
                ctx.op("act", lambda: nc.scalar.copy(vS[:, 1 + tt, :], ps[:, 0:nb]), reads=[ps], writes=[vS])
            self.gemm_tok(res, xT, xT, KC, w_qkv, 2304, 256, range(NT), v_epi, nblk=256)
            ctx.barrier()
        with contextlib.ExitStack() as st:
            if NSEG == 1:
                ctx.op("dve", lambda: nc.vector.memset(kT[:, :, 0:P], 0.0), writes=[kT])
                ctx.op("dve", lambda: nc.vector.memset(vS[:, 0, :], 0.0), writes=[vS])
            else:
                HCI = self.scr("swa_ci", [NSEG, P, 512])
                HCO = self.scr("swa_co", [NSEG, P, 512])
                hb = ctx.sb(st, "hb", [P, 512], F32)
                hm = [ctx.sb(st, "hm", [P, 512], F32) for _ in range(2)]
                ctx.op("dve", lambda: nc.vector.tensor_copy(hb[:, 0:256].rearrange("p (a b) -> p a b", a=2), kT[:, :, T:T + P]),
                       reads=[kT], writes=[hb])
                ctx.op("dve", lambda: nc.vector.tensor_copy(hb[:, 256:512], vS[:, NT, :]), reads=[vS], writes=[hb])
                for s in range(NSEG):
                    ctx.op("dve", lambda: nc.vector.tensor_scalar_mul(hm[s % 2][:], hb[:], self.own[:, s:s + 1]),
                           reads=[hb, self.own], writes=[hm[s % 2]])
                    ctx.dma("sp", HCI.ap()[s], hm[s % 2][:], reads=[hm[s % 2]], writes=[("HCI", s)])
                ctx.barrier()
                ctx.allreduce(cfg.groups, HCI.ap().rearrange("s p e -> (s p) e"), HCO.ap().rearrange("s p e -> (s p) e"),
                              writes=[("HCO",)])
                ctx.barrier()
                for s in range(NSEG):
                    ctx.dma("sp", hm[s % 2][:], HCO.ap()[s], writes=[hm[s % 2]])
                    if s == 0:
                        ctx.op("dve", lambda: nc.vector.tensor_scalar_mul(hb[:], hm[s % 2][:], self.halo_sel[:, s:s + 1]),
                               reads=[hm[s % 2], self.halo_sel], writes=[hb])
                    else:
                        ctx.op("dve", lambda: nc.vector.scalar_tensor_tensor(hb[:], hm[s % 2][:], self.halo_sel[:, s:s + 1],
                                                                             hb[:], ALU.mult, ALU.add),
                               reads=[hm[s % 2], self.halo_sel, hb], writes=[hb])
                ctx.op("dve", lambda: nc.vector.tensor_copy(kT[:, :, 0:P], hb[:, 0:256].rearrange("p (a b) -> p a b", a=2)),
                       reads=[hb], writes=[kT])
                ctx.op("dve", lambda: nc.vector.tensor_copy(vS[:, 0, :], hb[:, 256:512]), reads=[hb], writes=[vS])
            ctx.barrier()
        with contextlib.ExitStack() as st:
            qT = self.load_AT(st, "qT", QTs, KC, 0, T)
            ps_s = ctx.ps(st, "ps_s", [P, 8, 2 * P], F32)
            ps_t = ctx.ps(st, "ps_t", [P, 16, P], BF16)
            ps_o = ctx.ps(st, "ps_o", [P, 8, P], F32)
            s_sb = ctx.sb(st, "s_sb", [P, 8, 2 * P], F32)
            e_sb = ctx.sb(st, "e_sb", [P, 8, 2 * P], F32)
            p_sb = ctx.sb(st, "p_sb", [P, 8, 2 * P], BF16)
            pT = ctx.sb(st, "pT", [P, 16, P], BF16)
            mx = ctx.sb(st, "mx", [P, 8], F32)
            nmx = ctx.sb(st, "nmx", [P, 8], F32)
            rs = ctx.sb(st, "rs", [P, 8], F32)
            es = ctx.sb(st, "es", [P, 8], F32)
            G = 4 if NT % 4 == 0 else 2
            ost = [ctx.sb(st, "ost", [P, KC, G * P], BF16) for _ in range(2)]
            OTv = OT.ap().rearrange("(kc p) t -> p kc t", p=P)
            for n in range(NT):
                og = ost[(n // G) % 2]
                for pair in range(2):
                    for par in range(2):
                        hk = 2 * pair + par
                        po = par * 64
                        kc_k = hk // 2
                        for g in range(8):
                            ch = pair * 8 + g
                            ctx.op("pe", lambda: nc.tensor.matmul(ps_s[:, g, :], lhsT=qT[po:po + 64, ch, n * P:(n + 1) * P],
                                                                  rhs=kT[po:po + 64, kc_k, n * P:n * P + 2 * P],
                                                                  start=True, stop=True),
                                   reads=[qT, kT], writes=[ps_s])
                        ctx.op("dve", lambda: nc.vector.tensor_tensor(s_sb[:], ps_s[:], biasS[:, hk * 8:(hk + 1) * 8, :], ALU.add),
                               reads=[ps_s, biasS], writes=[s_sb])
                        if n == 0:
                            ctx.op("pool", lambda: nc.gpsimd.tensor_tensor(s_sb[:], s_sb[:],
                                                                           mfirst[:].unsqueeze(1).broadcast_to([P, 8, 2 * P]), ALU.add),
                                   reads=[s_sb, mfirst], writes=[s_sb])
                        ctx.op("dve", lambda: nc.vector.tensor_reduce(mx[:], s_sb[:], AX.X, ALU.max), reads=[s_sb], writes=[mx])
                        ctx.op("dve", lambda: nc.vector.tensor_tensor(mx[:], mx[:], sinkb[:, hk * 8:(hk + 1) * 8], ALU.max),
                               reads=[mx, sinkb], writes=[mx])
                        ctx.op("dve", lambda: nc.vector.tensor_scalar_mul(nmx[:], mx[:], -1.0), reads=[mx], writes=[nmx])
                        ctx.op("dve", lambda: nc.vector.memset(rs[:], 0.0), writes=[rs])
                        for g in range(8):
                            ctx.op("act", lambda: nc.scalar.activation(e_sb[:, g, :], s_sb[:, g, :], AF.Exp, bias=nmx[:, g:g + 1],
                                                                       scale=1.0, accum_out=rs[:, g:g + 1]),
                                   reads=[s_sb, nmx], writes=[e_sb, rs])
                        ctx.op("dve", lambda: nc.vector.tensor_tensor(es[:], sinkb[:, hk * 8:(hk + 1) * 8], mx[:], ALU.subtract),
                               reads=[sinkb, mx], writes=[es])
                        ctx.op("act", lambda: nc.scalar.activation(es[:], es[:], AF.Exp), reads=[es], writes=[es])
                        ctx.op("dve", lambda: nc.vector.tensor_tensor(rs[:], rs[:], es[:], ALU.add), reads=[rs, es], writes=[rs])
                        ctx.op("dve", lambda: nc.vector.reciprocal(rs[:], rs[:]), reads=[rs], writes=[rs])
                        ctx.op("pool", lambda: nc.gpsimd.tensor_tensor(p_sb[:], e_sb[:], rs[:].unsqueeze(2).broadcast_to([P, 8, 2 * P]),
                                                                       ALU.mult), reads=[e_sb, rs], writes=[p_sb])
                        for g in range(8):
                            for hf in range(2):
                                self.transpose_to(p_sb[:, g, hf * P:(hf + 1) * P], ps_t[:, g * 2 + hf, :], [p_sb], [ps_t])
                        ctx.op("act", lambda: nc.scalar.copy(pT[:], ps_t[:]), reads=[ps_t], writes=[pT])
                        for g in range(8):
                            for hf in range(2):
                                ctx.op("pe", lambda: nc.tensor.matmul(ps_o[po:po + 64, g, :], lhsT=vS[:, n + hf, hk * 64:(hk + 1) * 64],
                                                                      rhs=pT[:, g * 2 + hf, :], start=(hf == 0), stop=(hf == 1)),
                                       reads=[vS, pT], writes=[ps_o])
                    ctx.op("dve", lambda: nc.vector.tensor_copy(og[:, pair * 8:(pair + 1) * 8, (n % G) * P:(n % G + 1) * P], ps_o[:]),
                           reads=[ps_o], writes=[og])
                if n % G == G - 1:
                    g0 = (n // G) * G * P
                    ctx.dma("sp", OTv[:, :, g0:g0 + G * P], og[:], reads=[og], writes=[("OT", n)])
            ctx.barrier()
    with contextlib.ExitStack() as st:
        aT = self.load_AT(st, "oTa", OT, KC, 0, T)
        res = GemmRes(self, st, KC, 512, 3)
        epi = self.epi_resid(st, X, Z1)
        self.gemm_tok(res, aT, aT, KC, w_out, 0, D, range(NT), epi)
        ctx.barrier()


Prog.swa_layer = _swa_layer
RW_H, RW_HD = 32, 64
RW_EPS = 64e-5
RW_NHB = RW_H // 2


def _rwkv_consts(cfg, core, c):
    seg = core % cfg.NSEG
    f32 = np.float32
    s = np.arange(P)[:, None]
    t = np.arange(P)[None, :]
    c["rw_mus"] = (s < t).astype(f32)
    c["rw_mui"] = (s <= t).astype(f32)
    c["rw_mls"] = (s > t).astype(f32)
    c["rw_tri"] = (s <= t).astype(f32)
    c["rw_suf"] = (s > t).astype(f32)
    i2 = np.zeros((P, 64), f32)
    i2[np.arange(P), np.arange(P) % 64] = 1.0
    c["rw_i2"] = i2
    selm = np.zeros((P, cfg.NSEG), f32)
    selm[:, :seg] = 1.0
    c["rw_selm"] = selm
    c["rw_nselm"] = 1.0 - selm


def _rwkv_layer(self, layer, X, XT, Z1):
    ctx, nc, cfg = self.ctx, self.nc, self.cfg
    T, NT, NSEG = cfg.T, cfg.NT, cfg.NSEG
    j = layer // 3
    gi = lambda name, shape: self.inp("%s_%d" % (name, j), shape).ap()
    w_rkv = gi("rwkv_w_rkv", [3, D, D])
    w1, w2 = gi("rwkv_w1", [D, 96]), gi("rwkv_w2", [96, D])
    a1, a2 = gi("rwkv_a1", [D, 96]), gi("rwkv_a2", [96, D])
    g1, g2 = gi("rwkv_g1", [D, 256]), gi("rwkv_g2", [256, D])
    w_out = gi("rwkv_w_out", [D, D])
    mix_ap = gi("rwkv_mix", [6, D]).rearrange("i (kc p) -> (i kc) p", p=P)
    Rs, Ks, Vs = self.scr("rw_R", [T, D]), self.scr("rw_K", [T, D]), self.scr("rw_V", [T, D])
    WLs, ALs, Gs = self.scr("rw_WL", [T, D]), self.scr("rw_AL", [T, D]), self.scr("rw_G", [T, D])
    Y0 = self.scr("rw_Y0", [T, D])
    BON = self.scr("rw_BON", [T, RW_H])
    YTR = self.scr("rw_YTR", [NT, P, RW_NHB, P], BF16)
    OGT = self.scr("rw_OGT", [D, T], BF16)
    TGW = min(512, T)
    tgs = [(t0, TGW) for t0 in range(0, T, TGW)]

    with contextlib.ExitStack() as st:
        psc = ctx.ps(st, "psc", [P, 512], F32)
        mixc = self.load_cols(st, "mixc", mix_ap, 6 * KC, psc)
        xT = self.load_AT(st, "xT", XT, KC, 0, T, pad=1)
        with contextlib.ExitStack() as sth:
            halo = self.halo_rows(sth, X, 1, "halo1")
            for kc in range(KC):
                ctx.op("pe", lambda: nc.tensor.matmul(psc[:, kc:kc + 1], lhsT=halo[0:1, kc * P:(kc + 1) * P],
                                                      rhs=self.identf[0:1, 0:1], start=True, stop=True),
                       reads=[halo, self.identf], writes=[psc])
            ctx.op("dve", lambda: nc.vector.tensor_copy(xT[:, :, 0:1], psc[:, 0:KC].unsqueeze(2)), reads=[psc], writes=[xT])
            ctx.barrier()
        xm = ctx.sb(st, "xm", [P, KC, T], BF16)
        dtmp = [ctx.sb(st, "dtmp", [P, T], F32) for _ in range(2)]
        hT = ctx.sb(st, "hT", [P, 2, T], BF16)
        res = GemmRes(self, st, KC, 256, 4)
        obuf = [ctx.sb(st, "obuf", [P, 512], F32) for _ in range(3)]
        ocnt = [0]

        def store_epi(dst):
            def epi(ps, tt, c0, nb):
                o = obuf[ocnt[0] % 3]
                ocnt[0] += 1
                ctx.op("act", lambda: nc.scalar.copy(o[:, 0:nb], ps[:, 0:nb]), reads=[ps], writes=[o])
                ctx.dma("sp", dst.ap()[tt * P:(tt + 1) * P, c0:c0 + nb], o[:, 0:nb], reads=[o], writes=[("o", id(dst), tt, c0)])
            return epi

        def build_mix(i):
            for kc in range(KC):
                d = dtmp[kc % 2]
                ctx.op("pool", lambda: nc.gpsimd.tensor_tensor(d[:], xT[:, kc, 0:T], xT[:, kc, 1:T + 1], ALU.subtract),
                       reads=[xT], writes=[d])
                ctx.op("dve", lambda: nc.vector.scalar_tensor_tensor(xm[:, kc, :], d[:], mixc[:, i * KC + kc:i * KC + kc + 1],
                                                                     xT[:, kc, 1:T + 1], ALU.mult, ALU.add),
                       reads=[d, mixc, xT], writes=[xm])

        def lora(i, wa, na, func, wb_, dst):
            build_mix(i)
            kcn2 = (na + P - 1) // P
            kp = min(P, na)

            def h_epi(ps, c, t0, tn):
                fw = min(P, na - c)
                ctx.op("act", lambda: nc.scalar.activation(hT[0:fw, c // P, t0:t0 + tn], ps[0:fw, 0:tn], func),
                       reads=[ps], writes=[hT])
            self.gemm_feat(res, xm, xm, KC, wa, 0, na, tgs, h_epi, nblk=256)
            self.gemm_tok(res, hT, hT, kcn2, wb_, 0, D, range(NT), store_epi(dst), kp=kp, nblk=256)

        build_mix(0)
        self.gemm_tok(res, xm, xm, KC, w_rkv[0], 0, D, range(NT), store_epi(Rs), nblk=256)
        build_mix(2)
        self.gemm_tok(res, xm, xm, KC, w_rkv[1], 0, D, range(NT), store_epi(Ks), nblk=256)
        build_mix(3)
        self.gemm_tok(res, xm, xm, KC, w_rkv[2], 0, D, range(NT), store_epi(Vs), nblk=256)
        lora(1, w1, 96, AF.Tanh, w2, WLs)
        lora(4, a1, 96, AF.Identity, a2, ALs)
        lora(5, g1, 256, AF.Sigmoid, g2, Gs)
        ctx.barrier()
    if cfg.stop == "rw1":
        return

    SXs = self.scr("rw_SX", [P, RW_NHB, P])
    with contextlib.ExitStack() as st:
        def cload(name, shape, dtype=F32):
            t_ = ctx.sb(st, name, shape, dtype)
            ctx.dma("sp", t_[:], self.inp(name, shape, dtype).ap(), writes=[t_])
            return t_
        mus, mui, mls = cload("rw_mus", [P, P]), cload("rw_mui", [P, P]), cload("rw_mls", [P, P])
        tri, suft = cload("rw_tri", [P, P]), cload("rw_suf", [P, P])
        i2 = cload("rw_i2", [P, 64])
        ones = ctx.sb(st, "ones", [P, 1], F32)
        ctx.op("dve", lambda: nc.vector.memset(ones[:], 1.0), writes=[ones])
        w0b = self.bcast_rows(st, "w0b", gi("rwkv_w0", [D]), D)
        a0b = self.bcast_rows(st, "a0b", gi("rwkv_a0", [D]), D)
        kkb = self.bcast_rows(st, "kkb", gi("rwkv_k_k", [D]), D)
        kab = self.bcast_rows(st, "kab", gi("rwkv_k_a", [D]), D)
        rkb = self.bcast_rows(st, "rkb", gi("rwkv_r_k", [RW_H, RW_HD]).rearrange("h d -> (h d)"), D)
        A = ctx.sb(st, "A", [P, D], F32)
        B = ctx.sb(st, "B", [P, D], F32)
        Dw = ctx.sb(st, "Dw", [P, D], F32)
        Ea = ctx.sb(st, "Ea", [P, D], F32)
        Fk = ctx.sb(st, "Fk", [P, D], F32)
        T1 = ctx.sb(st, "T1", [P, D], F32)
        ET = [ctx.sb(st, "ET", [P, 512], F32) for _ in range(4)]
        tok = [ctx.sb(st, "tokb", [P, D], BF16) for _ in range(4)]
        bbk = ctx.sb(st, "bbk", [P, 2, D], BF16)
        vbx = ctx.sb(st, "vbx", [P, RW_H, P], BF16)
        ctx.op("pool", lambda: nc.gpsimd.memset(vbx[:], 0.0), writes=[vbx])
        CM = ctx.sb(st, "CM", [P, RW_NHB, 4, P], BF16)
        ytile = ctx.sb(st, "ytile", [P, D], F32)
        T2 = ytile
        ytr = ctx.sb(st, "ytr", [P, RW_NHB, P], BF16)
        ss = ctx.sb(st, "ss", [P, RW_H], F32)
        bon = ctx.sb(st, "bon", [P, RW_H], F32)
        dectot = ctx.sb(st, "dectot", [P, RW_NHB], F32)
        SX = ctx.sb(st, "SX", [P, RW_NHB, P], F32)
        SXb = ctx.sb(st, "SXb", [P, RW_NHB, P], BF16)
        GA = [ctx.sb(st, "GA", [P, 2, 2, P], BF16) for _ in range(2)]
        Brb = ctx.sb(st, "Brb", [P, 2, P], BF16)
        Aak = ctx.sb(st, "Aak", [P, 2, P], BF16)
        Brk = ctx.sb(st, "Brk", [P, 2, P], BF16)
        Tt = [ctx.sb(st, "Tt", [P, 2, P], BF16) for _ in range(2)]
        Wb = ctx.sb(st, "Wb", [P, 2, P], BF16)
        Ub = ctx.sb(st, "Ub", [P, 2, P], BF16)
        pb = [ctx.ps(st, "pb", [P, 512], F32) for _ in range(7)]
        ptr = ctx.ps(st, "ptr", [P, 8, P], BF16)
        ctx.op("dve", lambda: nc.vector.memset(SX[:], 0.0), writes=[SX])
        ctx.op("dve", lambda: nc.vector.tensor_copy(SX[:, :, 64:128], i2[:].unsqueeze(1).broadcast_to([P, RW_NHB, 64])),
               reads=[i2, SX], writes=[SX])
        ctx.op("pool", lambda: nc.gpsimd.tensor_copy(SXb[:], SX[:]), reads=[SX], writes=[SXb])
        v3 = lambda t_: t_[:].rearrange("p (h d) -> p h d", d=RW_HD)
        bc3 = lambda small: small[:].unsqueeze(2).broadcast_to([P, RW_H, RW_HD])
        for n in range(NT):
            rows = slice(n * P, (n + 1) * P)
            ctx.dma("sp", A[:], Rs.ap()[rows, :], writes=[A])
            ctx.dma("sp", B[:], Ks.ap()[rows, :], writes=[B])
            ctx.dma("sp", T1[:], Vs.ap()[rows, :], writes=[T1])
            ctx.op("act", lambda: nc.scalar.copy(vbx[:, :, 0:64], v3(T1)), reads=[T1], writes=[vbx])
            ctx.dma("sp", Dw[:], WLs.ap()[rows, :], writes=[Dw])
            ctx.dma("sp", Ea[:], ALs.ap()[rows, :], writes=[Ea])
            ctx.op("dve", lambda: nc.vector.tensor_tensor(Dw[:], Dw[:], w0b[:], ALU.add), reads=[Dw, w0b], writes=[Dw])
            ctx.op("act", lambda: nc.scalar.activation(Dw[:], Dw[:], AF.Sigmoid), reads=[Dw], writes=[Dw])
            ctx.op("dve", lambda: nc.vector.tensor_scalar_mul(Dw[:], Dw[:], -math.exp(-0.5)), reads=[Dw], writes=[Dw])
            ctx.op("dve", lambda: nc.vector.tensor_tensor(Ea[:], Ea[:], a0b[:], ALU.add), reads=[Ea, a0b], writes=[Ea])
            ctx.op("act", lambda: nc.scalar.activation(Ea[:], Ea[:], AF.Sigmoid), reads=[Ea], writes=[Ea])
            ctx.op("pool", lambda: nc.gpsimd.tensor_tensor(Fk[:], B[:], kkb[:], ALU.mult), reads=[B, kkb], writes=[Fk])
            ctx.op("pool", lambda: nc.gpsimd.tensor_tensor(T2[:], Fk[:], Fk[:], ALU.mult), reads=[Fk], writes=[T2])
            ctx.op("dve", lambda: nc.vector.tensor_reduce(ss[:], v3(T2), AX.X, ALU.add), reads=[T2], writes=[ss])
            ctx.op("act", lambda: nc.scalar.activation(ss[:], ss[:], AF.Sqrt), reads=[ss], writes=[ss])
            ctx.op("dve", lambda: nc.vector.tensor_scalar_max(ss[:], ss[:], 1e-12), reads=[ss], writes=[ss])
            ctx.op("dve", lambda: nc.vector.reciprocal(ss[:], ss[:]), reads=[ss], writes=[ss])
            ctx.op("dve", lambda: nc.vector.tensor_tensor(v3(Fk), v3(Fk), bc3(ss), ALU.mult), reads=[Fk, ss], writes=[Fk])
            ctx.op("dve", lambda: nc.vector.scalar_tensor_tensor(T1[:], Ea[:], -1.0, kab[:], ALU.add, ALU.mult),
                   reads=[Ea, kab], writes=[T1])
            ctx.op("pool", lambda: nc.gpsimd.tensor_tensor(T1[:], T1[:], B[:], ALU.mult), reads=[T1, B], writes=[T1])
            ctx.op("pool", lambda: nc.gpsimd.tensor_tensor(B[:], B[:], T1[:], ALU.add), reads=[T1, B], writes=[B])
            ctx.op("pool", lambda: nc.gpsimd.tensor_tensor(T2[:], A[:], B[:], ALU.mult), reads=[A, B, T2], writes=[T2])
            ctx.op("dve", lambda: nc.vector.tensor_tensor(T2[:], T2[:], rkb[:], ALU.mult), reads=[T2, rkb], writes=[T2])
            ctx.op("dve", lambda: nc.vector.tensor_reduce(bon[:], v3(T2), AX.X, ALU.add), reads=[T2], writes=[bon])
            ctx.dma("sp", BON.ap()[rows, :], bon[:], reads=[bon], writes=[("BON", n)])
            ctx.op("dve", lambda: nc.vector.tensor_tensor(T1[:], Fk[:], Ea[:], ALU.mult), reads=[Fk, Ea, T1], writes=[T1])
            for hb in range(RW_NHB):
                ctx.op("pe", lambda: nc.tensor.matmul(pb[0][:, hb:hb + 1], lhsT=Dw[:, hb * P:(hb + 1) * P], rhs=ones[:, 0:1],
                                                      start=True, stop=True), reads=[Dw, ones], writes=[pb[0]])
            ctx.op("act", lambda: nc.scalar.activation(dectot[:], pb[0][:, 0:RW_NHB], AF.Exp), reads=[pb[0]], writes=[dectot])
            for cb in range(4):
                cs = slice(cb * 512, (cb + 1) * 512)
                pcum, psuf = pb[1 + (cb % 2) * 2], pb[2 + (cb % 2) * 2]
                ctx.op("pe", lambda: nc.tensor.matmul(pcum[:], lhsT=tri[:], rhs=Dw[:, cs], start=True, stop=True),
                       reads=[tri, Dw], writes=[pcum])
                ctx.op("pe", lambda: nc.tensor.matmul(psuf[:], lhsT=suft[:], rhs=Dw[:, cs], start=True, stop=True),
                       reads=[suft, Dw], writes=[psuf])
                ctx.op("act", lambda: nc.scalar.activation(ET[0][:], pcum[:], AF.Exp), reads=[pcum], writes=[ET[0]])
                ctx.op("pool", lambda: nc.gpsimd.tensor_tensor(tok[3][:, cs], A[:, cs], ET[0][:], ALU.mult),
                       reads=[A, ET[0]], writes=[tok[3]])
                ctx.op("act", lambda: nc.scalar.activation(ET[1][:], pcum[:], AF.Exp, scale=-1.0), reads=[pcum], writes=[ET[1]])
                ctx.op("dve", lambda: nc.vector.tensor_tensor(tok[0][:, cs], T1[:, cs], ET[1][:], ALU.mult),
                       reads=[T1, ET[1]], writes=[tok[0]])
                ctx.op("pool", lambda: nc.gpsimd.tensor_tensor(tok[1][:, cs], B[:, cs], ET[1][:], ALU.mult),
                       reads=[B, ET[1]], writes=[tok[1]])
                ctx.op("dve", lambda: nc.vector.tensor_tensor(ET[2][:], pcum[:], Dw[:, cs], ALU.subtract),
                       reads=[pcum, Dw], writes=[ET[2]])
                ctx.op("act", lambda: nc.scalar.activation(ET[2][:], ET[2][:], AF.Exp), reads=[ET[2]], writes=[ET[2]])
                ctx.op("dve", lambda: nc.vector.scalar_tensor_tensor(tok[2][:, cs], Fk[:, cs], -1.0, ET[2][:], ALU.mult, ALU.mult),
                       reads=[Fk, ET[2]], writes=[tok[2]])
                ctx.op("act", lambda: nc.scalar.activation(ET[3][:], psuf[:], AF.Exp), reads=[psuf], writes=[ET[3]])
                ctx.op("dve", lambda: nc.vector.tensor_tensor(bbk[:, 0, cs], T1[:, cs], ET[3][:], ALU.mult),
                       reads=[T1, ET[3]], writes=[bbk])
                ctx.op("pool", lambda: nc.gpsimd.tensor_tensor(bbk[:, 1, cs], B[:, cs], ET[3][:], ALU.mult),
                       reads=[B, ET[3]], writes=[bbk])
            for kind in range(4):
                for half in range(2):
                    for jj in range(8):
                        hb = half * 8 + jj
                        self.transpose_to(tok[kind][:, hb * P:(hb + 1) * P], ptr[:, jj, :], [tok[kind]], [ptr])
                    if (kind + half) % 2 == 0:
                        ctx.op("act", lambda: nc.scalar.copy(CM[:, half * 8:(half + 1) * 8, kind, :], ptr[:]), reads=[ptr], writes=[CM])
                    else:
                        ctx.op("dve", lambda: nc.vector.tensor_copy(CM[:, half * 8:(half + 1) * 8, kind, :], ptr[:]), reads=[ptr], writes=[CM])
            for hb in range(RW_NHB if cfg.stop != "rw2p" else 0):
                v2 = lambda ap_, k: ap_.rearrange("p (h k) -> p h k", k=k)
                for hi, po in enumerate((0, 64)):
                    rhs_ar = CM[po:po + 64, hb, 2:4, :].rearrange("p k t -> p (k t)")
                    qb = pb[hi]
                    ctx.op("pe", lambda: nc.tensor.matmul(qb[:, 0:256], lhsT=CM[po:po + 64, hb, 0, :], rhs=rhs_ar,
                                                          start=True, stop=True), reads=[CM], writes=[qb])
                    ctx.op("pe", lambda: nc.tensor.matmul(qb[:, 256:512], lhsT=CM[po:po + 64, hb, 1, :], rhs=rhs_ar,
                                                          start=True, stop=True), reads=[CM], writes=[qb])
                    ctx.op("pe", lambda: nc.tensor.matmul(pb[2 + hi][:, 0:P], lhsT=CM[po:po + 64, hb, 2, :],
                                                          rhs=CM[po:po + 64, hb, 0, :], start=True, stop=True),
                           reads=[CM], writes=[pb[2 + hi]])
                g0 = GA[0]
                for hi in range(2):
                    qb = pb[hi]
                    ctx.op("dve", lambda: nc.vector.tensor_tensor(g0[:, hi, 0, :], qb[:, 0:P], mus[:], ALU.mult),
                           reads=[qb, mus], writes=[g0])
                    ctx.op("dve", lambda: nc.vector.tensor_tensor(Brb[:, hi, :], qb[:, P:2 * P], mui[:], ALU.mult),
                           reads=[qb, mui], writes=[Brb])
                    ctx.op("dve", lambda: nc.vector.tensor_tensor(Aak[:, hi, :], qb[:, 2 * P:3 * P], mus[:], ALU.mult),
                           reads=[qb, mus], writes=[Aak])
                    ctx.op("dve", lambda: nc.vector.tensor_tensor(Brk[:, hi, :], qb[:, 3 * P:4 * P], mui[:], ALU.mult),
                           reads=[qb, mui], writes=[Brk])
                    ctx.op("dve", lambda: nc.vector.tensor_tensor(g0[:, hi, 1, :], pb[2 + hi][:, 0:P], mls[:], ALU.mult),
                           reads=[pb[2 + hi], mls], writes=[g0])
                    ctx.op("pool", lambda: nc.gpsimd.tensor_tensor(Tt[0][:, hi, :], g0[:, hi, 0, :], self.ident[:], ALU.add),
                           reads=[g0, self.ident], writes=[Tt[0]])
                tcur = 0
                if cfg.stop == "rw2q":
                    continue
                for lvl in range(1, 7):
                    gc, gn = GA[(lvl - 1) % 2], GA[lvl % 2]
                    for hi in range(2):
                        if lvl < 6:
                            ctx.op("pe", lambda: nc.tensor.matmul(pb[3][:, (2 * hi) * P:(2 * hi + 1) * P], lhsT=gc[:, hi, 1, :],
                                                                  rhs=gc[:, hi, 0, :], start=True, stop=True),
                                   reads=[gc], writes=[pb[3]])
                        ctx.op("pe", lambda: nc.tensor.matmul(pb[3][:, (2 * hi + 1) * P:(2 * hi + 2) * P], lhsT=gc[:, hi, 0, :],
                                                              rhs=gc[:, hi, 1, :], start=True, stop=True),
                               reads=[gc], writes=[pb[3]])
                    if lvl < 6:
                        ctx.op("act", lambda: nc.scalar.copy(gn[:].rearrange("p h k t -> p (h k t)"), pb[3][:]),
                               reads=[pb[3]], writes=[gn])
                    else:
                        ctx.op("act", lambda: nc.scalar.copy(gn[:, :, 1, :], v2(pb[3][:], 2 * P)[:, :, P:2 * P]),
                               reads=[pb[3]], writes=[gn])
                    for hi in range(2):
                        ctx.op("pe", lambda: nc.tensor.matmul(pb[4][:, hi * P:(hi + 1) * P], lhsT=gn[:, hi, 1, :],
                                                              rhs=Tt[tcur][:, hi, :], start=True, stop=True),
                               reads=[gn, Tt[tcur]], writes=[pb[4]])
                    ctx.op("dve", lambda: nc.vector.tensor_tensor(Tt[1 - tcur][:], v2(pb[4][:, 0:2 * P], P), Tt[tcur][:], ALU.add),
                           reads=[pb[4], Tt[tcur]], writes=[Tt[1 - tcur]])
                    tcur = 1 - tcur
                TT = Tt[tcur]
                if cfg.stop == "rw2i":
                    continue
                for hi, po in enumerate((0, 64)):
                    h = 2 * hb + hi
                    wps = pb[5 + hi]
                    ctx.op("pe", lambda: nc.tensor.matmul(wps[:, 0:P], lhsT=CM[po:po + 64, hb, 2, :],
                                                          rhs=SXb[po:po + 64, hb, :], start=True, stop=False),
                           reads=[CM, SXb], writes=[wps])
                    ctx.op("pe", lambda: nc.tensor.matmul(wps[:, 0:P], lhsT=Aak[:, hi, :], rhs=vbx[:, h, :],
                                                          start=False, stop=True), reads=[Aak, vbx], writes=[wps])
                    ctx.op("act", lambda: nc.scalar.copy(Wb[:, hi, :], wps[:, 0:P]), reads=[wps], writes=[Wb])
                for hi in range(2):
                    ctx.op("pe", lambda: nc.tensor.matmul(pb[5][:, 2 * P + hi * P:2 * P + (hi + 1) * P], lhsT=TT[:, hi, :], rhs=Wb[:, hi, :],
                                                          start=True, stop=True), reads=[TT, Wb], writes=[pb[5]])
                ctx.op("dve", lambda: nc.vector.tensor_copy(Ub[:].rearrange("p h t -> p (h t)"), pb[5][:, 2 * P:4 * P]),
                       reads=[pb[5]], writes=[Ub])
                for hi, po in enumerate((0, 64)):
                    h = 2 * hb + hi
                    yb = pb[hi]
                    yo = yb[:, 0:64]
                    ctx.op("pe", lambda: nc.tensor.matmul(yo, lhsT=CM[po:po + 64, hb, 3, :], rhs=SXb[po:po + 64, hb, 0:64],
                                                          start=True, stop=False), reads=[CM, SXb], writes=[yb])
                    ctx.op("pe", lambda: nc.tensor.matmul(yo, lhsT=Brb[:, hi, :], rhs=Ub[:, hi, 0:64], start=False, stop=False),
                           reads=[Brb, Ub], writes=[yb])
                    ctx.op("pe", lambda: nc.tensor.matmul(yo, lhsT=Brk[:, hi, :], rhs=vbx[:, h, 0:64], start=False, stop=True),
                           reads=[Brk, vbx], writes=[yb])
                    if NSEG > 1:
                        to = yb[po:po + 64, P:2 * P]
                        ctx.op("pe", lambda: nc.tensor.matmul(to, lhsT=SXb[po:po + 64, hb, 64:128], rhs=CM[po:po + 64, hb, 3, :],
                                                              start=True, stop=False), reads=[CM, SXb], writes=[yb])
                        ctx.op("pe", lambda: nc.tensor.matmul(to, lhsT=Ub[:, hi, 64:128], rhs=Brb[:, hi, :], start=False, stop=True),
                               reads=[Brb, Ub], writes=[yb])
                    so = pb[6][po:po + 64, 2 * P:3 * P]
                    ctx.op("pe", lambda: nc.tensor.matmul(so, lhsT=bbk[:, 0, h * 64:(h + 1) * 64], rhs=Ub[:, hi, :], start=True, stop=False),
                           reads=[bbk, Ub], writes=[pb[6]])
                    ctx.op("pe", lambda: nc.tensor.matmul(so, lhsT=bbk[:, 1, h * 64:(h + 1) * 64], rhs=vbx[:, h, :], start=False, stop=True),
                           reads=[bbk, vbx], writes=[pb[6]])
                    ctx.op("act", lambda: nc.scalar.copy(ytile[:, hb * P + hi * 64:hb * P + (hi + 1) * 64], yb[:, 0:64]),
                           reads=[yb], writes=[ytile])
                    if NSEG > 1:
                        ctx.op("act", lambda: nc.scalar.copy(ytr[po:po + 64, hb, :], yb[po:po + 64, P:2 * P]), reads=[yb], writes=[ytr])
                ctx.op("dve", lambda: nc.vector.scalar_tensor_tensor(SX[:, hb, :], SX[:, hb, :], dectot[:, hb:hb + 1],
                                                                     pb[6][:, 2 * P:3 * P], ALU.mult, ALU.add),
                       reads=[SX, dectot, pb[6]], writes=[SX])
                ctx.op("pool", lambda: nc.gpsimd.tensor_copy(SXb[:, hb, :], SX[:, hb, :]), reads=[SX], writes=[SXb])
            ctx.dma("sp", Y0.ap()[rows, :], ytile[:], reads=[ytile], writes=[("Y0", n)])
            if NSEG > 1:
                ctx.dma("sp", YTR.ap()[n], ytr[:], reads=[ytr], writes=[("YTR", n)])
        if NSEG > 1:
            ctx.dma("sp", SXs.ap(), SX[:], reads=[SX], writes=[("SXs",)])
        ctx.barrier()

    if cfg.stop in ("rw2", "rw2p", "rw2q", "rw2i"):
        return
    S0b = ctx.sb(self.top, "rw_S0b_%d" % layer, [P, RW_NHB, 64], BF16)
    if NSEG > 1:
        NSL = NSEG - 1
        CI = self.scr("rw_ci", [NSL, P, RW_NHB * P])
        CO = self.scr("rw_co", [NSL, P, RW_NHB * P])
        with contextlib.ExitStack() as st:
            sx = ctx.sb(st, "sx", [P, RW_NHB * P], F32)
            sm = [ctx.sb(st, "sm", [P, RW_NHB * P], F32) for _ in range(2)]
            ctx.dma("sp", sx[:], SXs.ap().rearrange("p h k -> p (h k)"), writes=[sx])
            for s in range(NSL):
                ctx.op("dve", lambda: nc.vector.tensor_scalar_mul(sm[s % 2][:], sx[:], self.own[:, s:s + 1]),
                       reads=[sx, self.own], writes=[sm[s % 2]])
                ctx.dma("sp", CI.ap()[s], sm[s % 2][:], reads=[sm[s % 2]], writes=[("CI", s)])
            ctx.barrier()
            ctx.allreduce(cfg.groups, CI.ap().rearrange("s p e -> (s p) e"), CO.ap().rearrange("s p e -> (s p) e"),
                          writes=[("CO",)])
            ctx.barrier()
            selm = ctx.sb(st, "selm", [P, NSEG], F32)
            nselm = ctx.sb(st, "nselm", [P, NSEG], F32)
            ctx.dma("sp", selm[:], self.inp("rw_selm", [P, NSEG]).ap(), writes=[selm])
            ctx.dma("sp", nselm[:], self.inp("rw_nselm", [P, NSEG]).ap(), writes=[nselm])
            i2 = ctx.sb(st, "i2", [P, 64], F32)
            ctx.dma("sp", i2[:], self.inp("rw_i2", [P, 64]).ap(), writes=[i2])
            S0 = ctx.sb(st, "S0", [P, RW_NHB, 64], F32)
            ctx.op("dve", lambda: nc.vector.memset(S0[:], 0.0), writes=[S0])
            Mp = ctx.sb(st, "Mp", [P, RW_NHB, 64], F32)
            Lp = ctx.sb(st, "Lp", [P, RW_NHB, 64], F32)
            MT = ctx.sb(st, "MT", [P, RW_NHB, 64], F32)
            pm = [ctx.ps(st, "pm", [P, 8, 64], F32) for _ in range(2)]
            for s in range(NSL):
                slot = sm[s % 2]
                ctx.dma("sp", slot[:], CO.ap()[s], writes=[slot])
                sv = slot[:].rearrange("p (h k) -> p h k", k=P)
                ctx.op("dve", lambda: nc.vector.tensor_scalar_mul(Lp[:], sv[:, :, 0:64], selm[:, s:s + 1]),
                       reads=[slot, selm], writes=[Lp])
                ctx.op("dve", lambda: nc.vector.tensor_scalar_mul(Mp[:], sv[:, :, 64:128], selm[:, s:s + 1]),
                       reads=[slot, selm], writes=[Mp])
                ctx.op("dve", lambda: nc.vector.scalar_tensor_tensor(Mp[:], i2[:].unsqueeze(1).broadcast_to([P, RW_NHB, 64]),
                                                                     nselm[:, s:s + 1], Mp[:], ALU.mult, ALU.add),
                       reads=[i2, nselm, Mp], writes=[Mp])
                for half in range(2):
                    for jj in range(8):
                        hb = half * 8 + jj
                        for hi, po in enumerate((0, 64)):
                            ctx.op("pe", lambda: nc.tensor.matmul(pm[hi][po:po + 64, jj, :], lhsT=Mp[po:po + 64, hb, :],
                                                                  rhs=self.identf[po:po + 64, po:po + 64], start=True, stop=True),
                                   reads=[Mp, self.identf], writes=[pm[hi]])
                    for hi, po in enumerate((0, 64)):
                        ctx.op("act", lambda: nc.scalar.copy(MT[po:po + 64, half * 8:(half + 1) * 8, :], pm[hi][po:po + 64]),
                               reads=[pm[hi]], writes=[MT])
                for half in range(2):
                    for jj in range(8):
                        hb = half * 8 + jj
                        for hi, po in enumerate((0, 64)):
                            ctx.op("pe", lambda: nc.tensor.matmul(pm[hi][po:po + 64, jj, :], lhsT=MT[po:po + 64, hb, :],
                                                                  rhs=S0[po:po + 64, hb, :], start=True, stop=True),
                                   reads=[MT, S0], writes=[pm[hi]])
                    for hi, po in enumerate((0, 64)):
                        ctx.op("dve", lambda: nc.vector.tensor_tensor(S0[po:po + 64, half * 8:(half + 1) * 8, :], pm[hi][po:po + 64],
                                                                      Lp[po:po + 64, half * 8:(half + 1) * 8, :], ALU.add),
                               reads=[pm[hi], Lp, S0], writes=[S0])
            ctx.op("act", lambda: nc.scalar.copy(S0b[:], S0[:]), reads=[S0], writes=[S0b])
            ctx.barrier()

    if cfg.stop == "rw3":
        return
    with contextlib.ExitStack() as st:
        gnb = self.bcast_rows(st, "gnb", gi("rwkv_gn_gain", [D]), D)
        gbb = self.bcast_rows(st, "gbb", gi("rwkv_gn_bias", [D]), D)
        y = ctx.sb(st, "y", [P, D], F32)
        vv = ctx.sb(st, "vv", [P, D], F32)
        gg = ctx.sb(st, "gg", [P, D], F32)
        sq = ctx.sb(st, "sq", [P, D], F32)
        bon = ctx.sb(st, "bon", [P, RW_H], F32)
        s1 = ctx.sb(st, "s1", [P, RW_H], F32)
        s2 = ctx.sb(st, "s2", [P, RW_H], F32)
        ytr = ctx.sb(st, "ytr", [P, RW_NHB, P], BF16)
        ogb = [ctx.sb(st, "ogb", [P, D], BF16) for _ in range(2)]
        G = 4 if NT % 4 == 0 else 2
        stg = [ctx.sb(st, "stg", [P, KC, G * P], BF16) for _ in range(2)]
        pst = [[ctx.ps(st, "pst", [P, 8 * P], BF16) for _ in range(2)] for _ in range(2)]
        pc = [ctx.ps(st, "pc", [P, 512], F32) for _ in range(4)]
        OGTv = OGT.ap().rearrange("(kc p) t -> p kc t", p=P)
        v3 = lambda t_: t_[:].rearrange("p (h d) -> p h d", d=RW_HD)
        bc3 = lambda small: small[:].unsqueeze(2).broadcast_to([P, RW_H, RW_HD])
        for n in range(NT):
            rows = slice(n * P, (n + 1) * P)
            ctx.dma("sp", y[:], Y0.ap()[rows, :], writes=[y])
            ctx.dma("sp", vv[:], Vs.ap()[rows, :], writes=[vv])
            ctx.dma("sp", gg[:], Gs.ap()[rows, :], writes=[gg])
            ctx.dma("sp", bon[:], BON.ap()[rows, :], writes=[bon])
            if NSEG > 1:
                ctx.dma("sp", ytr[:], YTR.ap()[n], writes=[ytr])
                for q4 in range(4):
                    for jj in range(4):
                        hb = q4 * 4 + jj
                        for hi, po in enumerate((0, 64)):
                            pcc = pc[2 * (q4 % 2) + hi]
                            ctx.op("pe", lambda: nc.tensor.matmul(pcc[:, jj * 64:(jj + 1) * 64],
                                                                  lhsT=ytr[po:po + 64, hb, :], rhs=S0b[po:po + 64, hb, :],
                                                                  start=True, stop=True), reads=[ytr, S0b], writes=[pcc])
                    for hi in range(2):
                        pcc = pc[2 * (q4 % 2) + hi]
                        yv = y[:, q4 * 512:(q4 + 1) * 512].rearrange("p (j k) -> p j k", k=P)[:, :, hi * 64:(hi + 1) * 64]
                        ctx.op("dve", lambda: nc.vector.tensor_tensor(yv, yv, pcc[:, 0:256].rearrange("p (j k) -> p j k", k=64), ALU.add),
                               reads=[pcc, y], writes=[y])
            ctx.op("dve", lambda: nc.vector.tensor_reduce(s1[:], v3(y), AX.X, ALU.add), reads=[y], writes=[s1])
            ctx.op("dve", lambda: nc.vector.tensor_scalar_mul(s1[:], s1[:], 1.0 / RW_HD), reads=[s1], writes=[s1])
            ctx.op("dve", lambda: nc.vector.tensor_tensor(v3(y), v3(y), bc3(s1), ALU.subtract), reads=[y, s1], writes=[y])
            ctx.op("pool", lambda: nc.gpsimd.tensor_tensor(sq[:], y[:], y[:], ALU.mult), reads=[y], writes=[sq])
            ctx.op("dve", lambda: nc.vector.tensor_reduce(s2[:], v3(sq), AX.X, ALU.add), reads=[sq], writes=[s2])
            ctx.op("dve", lambda: nc.vector.tensor_scalar(s2[:], s2[:], 1.0 / RW_HD, RW_EPS, ALU.mult, ALU.add), reads=[s2], writes=[s2])
            ctx.op("act", lambda: nc.scalar.activation(s2[:], s2[:], AF.Sqrt), reads=[s2], writes=[s2])
            ctx.op("dve", lambda: nc.vector.reciprocal(s2[:], s2[:]), reads=[s2], writes=[s2])
            ctx.op("dve", lambda: nc.vector.tensor_tensor(v3(y), v3(y), bc3(s2), ALU.mult), reads=[y, s2], writes=[y])
            ctx.op("pool", lambda: nc.gpsimd.tensor_tensor(y[:], y[:], gnb[:], ALU.mult), reads=[y, gnb], writes=[y])
            ctx.op("pool", lambda: nc.gpsimd.tensor_tensor(y[:], y[:], gbb[:], ALU.add), reads=[y, gbb], writes=[y])
            ctx.op("dve", lambda: nc.vector.tensor_tensor(v3(vv), v3(vv), bc3(bon), ALU.mult), reads=[vv, bon], writes=[vv])
            ctx.op("pool", lambda: nc.gpsimd.tensor_tensor(y[:], y[:], vv[:], ALU.add), reads=[y, vv], writes=[y])
            ob = ogb[n % 2]
            ctx.op("dve", lambda: nc.vector.tensor_tensor(ob[:], y[:], gg[:], ALU.mult), reads=[y, gg], writes=[ob])
            g_, gi_ = divmod(n, G)
            self.xt_emit_tile(ob, ob, stg[g_ % 2], stg[g_ % 2], gi_ * P, pst[n % 2])
            if gi_ == G - 1:
                ctx.dma("sp", OGTv[:, :, g_ * G * P:(g_ + 1) * G * P], stg[g_ % 2][:], reads=[stg[g_ % 2]], writes=[("OGT", g_)])
        ctx.barrier()

    if cfg.stop == "rw4":
        return
    with contextlib.ExitStack() as st:
        aT = self.load_AT(st, "oTa", OGT, KC, 0, T)
        res = GemmRes(self, st, KC, 512, 3)
        epi = self.epi_resid(st, X, Z1)
        self.gemm_tok(res, aT, aT, KC, w_out, 0, D, range(NT), epi)
        ctx.barrier()


Prog.rwkv_layer = _rwkv_layer
def _const_inputs(cfg, core):
    T, NSEG = cfg.T, cfg.NSEG
    seg = core % NSEG
    f32 = np.float32
    own = np.zeros((P, NSEG), f32)
    own[:, seg] = 1
    hs = np.zeros((P, NSEG), f32)
    if seg > 0:
        hs[:, seg - 1] = 1
    c = {"ident": np.eye(P, dtype=f32).astype(ml_dtypes.bfloat16), "identf": np.eye(P, dtype=f32),
         "own": own, "halo_sel": hs}
    inv = (1.0 / (10000.0 ** (np.arange(0, RET_DK, 2, dtype=f32) / f32(RET_DK)))).astype(f32)
    pos = (seg * T + np.arange(T)).astype(f32)
    ang = (pos[None, :] * inv[:, None]).astype(f32)
    c["rope_cos"] = np.cos(ang).astype(f32)
    c["rope_sin"] = np.sin(ang).astype(f32)
    gam = np.array(RET_GAMMA, np.float64)
    idx = np.arange(P, dtype=np.float64)
    c["ret_kdec"] = (gam[None, :] ** (P - 1 - idx[:, None])).astype(f32)
    diff = idx[None, :] - idx[:, None]
    m = np.where(diff[:, None, :] >= 0, gam[None, :, None] ** np.maximum(diff[:, None, :], 0), 0.0)
    c["ret_maskT"] = m.astype(f32)
    c["ret_qdec"] = np.broadcast_to((gam[:, None] ** (idx[None, :] + 1.0))[None], (P, RET_H, P)).astype(f32).copy()
    coef = np.zeros((P, NSEG, RET_H), f32)
    for s in range(seg):
        coef[:, s, :] = (gam ** (T * (seg - s - 1)))[None, :]
    c["ret_coef"] = coef
    _swa_consts(cfg, core, c)
    _rwkv_consts(cfg, core, c)
    return c


def make_in_maps(cfg, prog, inputs):
    T, NSEG = cfg.T, cfg.NSEG
    maps = []
    shared = {}
    per_layer = ["ret_w_in", "ret_w_out", "ret_gn_gain", "swa_w_qkv", "swa_sinks", "swa_w_out",
                 "rwkv_mix", "rwkv_w_rkv", "rwkv_w0", "rwkv_w1", "rwkv_w2", "rwkv_a0", "rwkv_a1", "rwkv_a2",
                 "rwkv_g1", "rwkv_g2", "rwkv_k_k", "rwkv_k_a", "rwkv_r_k", "rwkv_gn_gain", "rwkv_gn_bias",
                 "rwkv_w_out", "ffn_w_up", "ffn_conv_w", "ffn_conv_b", "ffn_w_down", "ple_w_proj", "ple_w_gate"]
    order = _swa_head_order()
    for name in prog.inputs:
        if name.startswith("swa_w_qkv_p"):
            w = inputs["swa_w_qkv"][int(name[len("swa_w_qkv_p"):])]
            qcols = np.concatenate([np.arange(h * 64, (h + 1) * 64) for h in order])
            shared[name] = np.ascontiguousarray(np.concatenate([w[:, qcols], w[:, 2048:]], axis=1))
            continue
        if name.startswith("swa_w_out_p"):
            w = inputs["swa_w_out"][int(name[len("swa_w_out_p"):])]
            rows = np.concatenate([np.arange(h * 64, (h + 1) * 64) for h in order])
            shared[name] = np.ascontiguousarray(w[rows, :])
            continue
        if name in inputs and name not in ("x",):
            shared[name] = np.ascontiguousarray(inputs[name])
            continue
        for base in per_layer:
            if name.startswith(base + "_") and name[len(base) + 1:].isdigit():
                shared[name] = np.ascontiguousarray(inputs[base][int(name[len(base) + 1:])])
    for core in range(cfg.ncores):
        b, seg = divmod(core, NSEG)
        consts = _const_inputs(cfg, core)
        m = {}
        for name in prog.inputs:
            if name in shared:
                m[name] = shared[name]
            elif name == "x":
                m[name] = np.ascontiguousarray(inputs["x"][b, seg * T:(seg + 1) * T, :])
            elif name.startswith("pT_"):
                l = int(name[3:])
                m[name] = np.ascontiguousarray(inputs["p"][l, b, seg * T:(seg + 1) * T, :].T)
            elif name in consts:
                m[name] = consts[name]
            else:
                raise KeyError(name)
        maps.append(m)
    return maps


def run_cfg(cfg, inputs):
    prog = Prog(cfg)
    prog.build()
    maps = make_in_maps(cfg, prog, inputs)
    res = run_bass_kernel_spmd(prog.nc, maps, core_ids=list(range(cfg.ncores)))
    return prog, res.results


def kernel(**inputs):
    cfg = Cfg()
    prog, results = run_cfg(cfg, inputs)
    out = np.empty((cfg.NB, cfg.NSEG * cfg.T, D), np.float32)
    for core in range(cfg.ncores):
        b, seg = divmod(core, cfg.NSEG)
        out[b, seg * cfg.T:(seg + 1) * cfg.T, :] = results[core]["out"]
    return out
```

```python
import contextlib
import math
import numpy as np
import ml_dtypes
import concourse.bass as bass
import concourse.mybir as mybir
from concourse.bass_utils import run_bass_kernel_spmd

F32 = mybir.dt.float32
BF16 = mybir.dt.bfloat16
AF = mybir.ActivationFunctionType
ALU = mybir.AluOpType
AX = mybir.AxisListType

P = 128
D = 2048
KC = D // P
DEPTH = 4
DFF = 5504
NFB = DFF // P
PLE = 256
ALPHA = (2.0 * DEPTH) ** 0.25
LN_EPS = 1e-5
RET_H, RET_DK, RET_DV = 8, 256, 512
RET_EPS = 1e-5
RET_GAMMA = [1.0 - 2.0 ** (-5.0 - h) for h in range(RET_H)]


class Ctx:
    NDMA = {"sp": 8, "act": 4, "pool": 8}

    def __init__(self, nc, stack):
        self.nc = nc
        self.stack = stack
        self.eng = {"pe": nc.tensor, "act": nc.scalar, "dve": nc.vector,
                    "pool": nc.gpsimd, "sp": nc.sync}
        self.sems = {}
        self.val = {}
        for e in ("pe", "act", "dve", "pool"):
            self.sems[e] = stack.enter_context(nc.semaphore("c_" + e))
            self.val[e] = 0
        self.dq = {}
        self.dq_next = {}
        for q, n in self.NDMA.items():
            keys = []
            for i in range(n):
                k = "d_%s%d" % (q, i)
                self.sems[k] = stack.enter_context(nc.semaphore(k))
                self.val[k] = 0
                keys.append(k)
            self.dq[q] = keys
            self.dq_next[q] = 0
        self.sems["cc"] = stack.enter_context(nc.semaphore("cc"))
        self.val["cc"] = 0
        self.known = {e: {} for e in self.eng}
        self.lastw = {}
        self.readers = {}
        self.uid = 0
        self.n_ins = 0

    def sb(self, stack, name, shape, dtype=F32):
        self.uid += 1
        return stack.enter_context(self.nc.sbuf_tensor("%s_%d" % (name, self.uid), list(shape), dtype))

    def ps(self, stack, name, shape, dtype=F32):
        self.uid += 1
        return stack.enter_context(self.nc.psum_tensor("%s_%d" % (name, self.uid), list(shape), dtype))

    def _key(self, b):
        return b if isinstance(b, (str, tuple)) else id(b)

    def _deps(self, reads, writes, merge=False):
        deps = {}
        for b in list(reads) + ([] if merge else list(writes)):
            for k, v in self.lastw.get(self._key(b), {}).items():
                if deps.get(k, 0) < v:
                    deps[k] = v
        for b in writes:
            for k, v in self.readers.get(self._key(b), {}).items():
                if deps.get(k, 0) < v:
                    deps[k] = v
        return deps

    def _wait(self, e, deps):
        kn = self.known[e]
        for k, v in deps.items():
            if e == "pe" and k == "pe":
                continue
            if kn.get(k, 0) >= v:
                continue
            self.eng[e].wait_ge(self.sems[k], v)
            kn[k] = v

    def _commit(self, ev, reads, writes, merge=False):
        k, v = ev
        for b in reads:
            self.readers.setdefault(self._key(b), {})[k] = v
        for b in writes:
            if merge:
                self.lastw.setdefault(self._key(b), {})[k] = v
            else:
                self.lastw[self._key(b)] = {k: v}
                self.readers[self._key(b)] = {}

    def op(self, e, fn, reads=(), writes=()):
        self._wait(e, self._deps(reads, writes))
        ins = fn()
        self.val[e] += 1
        ins.then_inc(self.sems[e], 1)
        self._commit((e, self.val[e]), reads, writes)
        self.n_ins += 1
        return ins

    def dma(self, q, out, in_, reads=(), writes=(), merge=False, **kw):
        deps = self._deps(reads, writes, merge)
        i = self.dq_next[q]
        self.dq_next[q] = (i + 1) % len(self.dq[q])
        k = self.dq[q][i]
        if self.val[k] > 0:
            deps[k] = max(deps.get(k, 0), self.val[k])
        self._wait(q, deps)
        ins = self.eng[q].dma_start(out=out, in_=in_, **kw)
        self.val[k] += 16
        ins.then_inc(self.sems[k], 16)
        self._commit((k, self.val[k]), reads, writes, merge)
        self.n_ins += 1
        return ins

    def allreduce(self, groups, in_ap, out_ap, reads=(), writes=()):
        deps = self._deps(reads, writes)
        self._wait("pool", deps)
        ins = self.nc.gpsimd.collective_compute("AllReduce", ALU.add, replica_groups=groups,
                                                ins=[in_ap.opt()], outs=[out_ap.opt()])
        self.val["cc"] += 1
        ins.then_inc(self.sems["cc"], 1)
        self._commit(("cc", self.val["cc"]), reads, writes)

    def barrier(self, engines=("pe", "act", "dve", "pool", "sp")):
        deps = {k: v for k, v in self.val.items() if v > 0}
        for e in engines:
            self._wait(e, dict(deps))
        if len(engines) == 5:
            self.lastw = {}
            self.readers = {}

    def finish(self):
        self._wait("sp", {k: v for k, v in self.val.items() if v > 0})


class Cfg:
    def __init__(self, NB=2, NSEG=4, T=2048, layers=(0, 1, 2, 3), debug=(), stop=None):
        self.stop = stop
        self.NB, self.NSEG, self.T = NB, NSEG, T
        self.layers = tuple(layers)
        self.NT = T // P
        self.ncores = NB * NSEG
        self.groups = [[b * NSEG + s for s in range(NSEG)] for b in range(NB)]
        self.debug = tuple(debug)


class Prog:
    def __init__(self, cfg):
        self.cfg = cfg
        self.nc = bass.Bass("TRN2", target_bir_lowering=False)
        self.inputs = {}
        self.scratch = {}
        self.tiled = {}

    def inp(self, name, shape, dtype=F32):
        if name not in self.inputs:
            self.inputs[name] = self.nc.dram_tensor(name, list(shape), dtype, kind="ExternalInput")
        return self.inputs[name]

    def scr(self, name, shape, dtype=F32):
        if name not in self.scratch:
            kind = "ExternalOutput" if name in self.cfg.debug else "Internal"
            self.scratch[name] = self.nc.dram_tensor(name, list(shape), dtype, kind=kind)
        return self.scratch[name]

    def build(self):
        cfg = self.cfg
        nc = self.nc
        T = cfg.T
        with contextlib.ExitStack() as top:
            ctx = self.ctx = Ctx(nc, top)
            self.top = top
            self.ident = ctx.sb(top, "ident", [P, P], BF16)
            ctx.dma("sp", self.ident[:], self.inp("ident", [P, P], BF16).ap(), writes=[self.ident])
            self.identf = ctx.sb(top, "identf", [P, P], F32)
            ctx.dma("sp", self.identf[:], self.inp("identf", [P, P], F32).ap(), writes=[self.identf])
            self.own = ctx.sb(top, "own", [P, cfg.NSEG], F32)
            ctx.dma("sp", self.own[:], self.inp("own", [P, cfg.NSEG]).ap(), writes=[self.own])
            self.halo_sel = ctx.sb(top, "halo_sel", [P, cfg.NSEG], F32)
            ctx.dma("sp", self.halo_sel[:], self.inp("halo_sel", [P, cfg.NSEG]).ap(), writes=[self.halo_sel])

            x_in = self.inp("x", [T, D])
            out = self.nc.dram_tensor("out", [T, D], F32, kind="ExternalOutput")
            XT = self.scr("XT", [D, T], BF16)
            cur = x_in
            self.xt_stage(cur, XT)
            for li, layer in enumerate(cfg.layers):
                kind = layer % 3
                Z1 = self.scr("Z1", [T, D])
                if kind == 0:
                    self.retention_layer(layer, cur, XT, Z1)
                elif kind == 1:
                    self.swa_layer(layer, cur, XT, Z1)
                else:
                    self.rwkv_layer(layer, cur, XT, Z1)
                if cfg.stop is not None:
                    break
                X1 = self.scr("X1", [T, D])
                XT1 = self.scr("XT1", [D, T], BF16)
                self.ln_stage(Z1, layer, 0, X1, XT1)
                Z2 = self.scr("Z2", [T, D])
                self.ffn_layer(layer, X1, XT1, Z2)
                X2 = self.scr("X2", [T, D])
                self.ln_stage(Z2, layer, 1, X2, XT)
                last = li == len(cfg.layers) - 1
                X3 = out if last else self.scr("X3_%d" % (li % 2), [T, D])
                self.ple_layer(layer, X2, XT, X3)
                if not last:
                    self.xt_stage(X3, XT)
                cur = X3
            ctx.barrier()
            ctx.finish()
        return nc

    def load_w(self, dst, src_ap, key):
        self.ctx.dma("pool", dst, src_ap, writes=[key])

    def bcast_rows(self, stack, name, src_ap_1d, n):
        t = self.ctx.sb(stack, name, [P, n], F32)
        self.ctx.dma("sp", t[:], src_ap_1d.partition_broadcast(P), writes=[t])
        return t

    def load_cols(self, stack, name, src_ap_2d, R, ps):
        ctx, nc = self.ctx, self.nc
        out = ctx.sb(stack, name, [P, R], F32)
        with contextlib.ExitStack() as st:
            r0 = 0
            while r0 < R:
                r = min(P, R - r0)
                tmp = ctx.sb(st, name + "_r", [P, P], F32)
                ctx.dma("sp", tmp[0:r, :], src_ap_2d[r0:r0 + r, :], writes=[tmp])
                ctx.op("pe", lambda: nc.tensor.matmul(ps[:, 0:r], lhsT=tmp[0:r, :], rhs=self.identf[0:r, 0:r],
                                                      start=True, stop=True), reads=[tmp, self.identf], writes=[ps])
                ctx.op("dve", lambda: nc.vector.tensor_copy(out[:, r0:r0 + r], ps[:, 0:r]), reads=[ps], writes=[out])
                r0 += r
            ctx.barrier(("pe", "dve", "sp"))
        return out

    def transpose_to(self, src_bf16_ap, pst_ap, reads, writes):
        nc = self.nc
        self.ctx.op("pe", lambda: nc.tensor.transpose(pst_ap, src_bf16_ap, self.ident[:]),
                    reads=list(reads) + [self.ident], writes=writes)

    def xt_emit_tile(self, xb, xb_key, stg, stg_key, col0, pst):
        ctx, nc = self.ctx, self.nc
        for half in range(2):
            pt = pst[half]
            for j in range(8):
                kc = half * 8 + j
                self.transpose_to(xb[:, kc * P:(kc + 1) * P], pt[:, j * P:(j + 1) * P], [xb_key], [pt])
            eng = "dve" if half == 0 else "act"
            src = pt[:].rearrange("p (j c) -> p j c", j=8)
            dst = stg[:, half * 8:(half + 1) * 8, col0:col0 + P]
            if eng == "dve":
                ctx.op("dve", lambda: nc.vector.tensor_copy(dst, src), reads=[pt], writes=[stg_key])
            else:
                ctx.op("act", lambda: nc.scalar.copy(dst, src), reads=[pt], writes=[stg_key])

    def xt_stage(self, X, XT):
        ctx, nc, cfg = self.ctx, self.nc, self.cfg
        T = cfg.T
        G = 4 if cfg.NT % 4 == 0 else 2
        with contextlib.ExitStack() as st:
            xf = [ctx.sb(st, "xf", [P, D], F32) for _ in range(2)]
            xb = [ctx.sb(st, "xb", [P, D], BF16) for _ in range(2)]
            stg = [ctx.sb(st, "stg", [P, KC, G * P], BF16) for _ in range(2)]
            pst = [[ctx.ps(st, "pst", [P, 8 * P], BF16) for _ in range(2)] for _ in range(2)]
            XTv = XT.ap().rearrange("(kc p) t -> p kc t", p=P)
            for tt in range(cfg.NT):
                b = tt % 2
                g, gi = divmod(tt, G)
                ctx.dma("sp", xf[b][:], X.ap()[tt * P:(tt + 1) * P, :], writes=[xf[b]])
                ctx.op("pool", lambda: nc.gpsimd.tensor_copy(xb[b][:], xf[b][:]), reads=[xf[b]], writes=[xb[b]])
                self.xt_emit_tile(xb[b], xb[b], stg[g % 2], stg[g % 2], gi * P, pst[b])
                if gi == G - 1:
                    ctx.dma("sp", XTv[:, :, g * G * P:(g + 1) * G * P], stg[g % 2][:], reads=[stg[g % 2]],
                            writes=[("XT", g)])
            ctx.barrier()

    def ln_stage(self, Z, layer, which, X, XT):
        ctx, nc, cfg = self.ctx, self.nc, self.cfg
        G = 4 if cfg.NT % 4 == 0 else 2
        with contextlib.ExitStack() as st:
            gain = self.bcast_rows(st, "lng", self.inp("ln_gain", [DEPTH, 2, D]).ap()[layer, which, :], D)
            bias = self.bcast_rows(st, "lnb", self.inp("ln_bias", [DEPTH, 2, D]).ap()[layer, which, :], D)
            zf = [ctx.sb(st, "zf", [P, D], F32) for _ in range(2)]
            xn = [ctx.sb(st, "xn", [P, D], F32) for _ in range(2)]
            xo = [ctx.sb(st, "xo", [P, D], F32) for _ in range(2)]
            xb = [ctx.sb(st, "xb", [P, D], BF16) for _ in range(2)]
            stats = [ctx.sb(st, "stats", [P, 4, 6], F32) for _ in range(2)]
            mv = [ctx.sb(st, "mv", [P, 4], F32) for _ in range(2)]
            stg = [ctx.sb(st, "stg", [P, KC, G * P], BF16) for _ in range(2)]
            pst = [[ctx.ps(st, "pst", [P, 8 * P], BF16) for _ in range(2)] for _ in range(2)]
            XTv = XT.ap().rearrange("(kc p) t -> p kc t", p=P)
            for tt in range(cfg.NT):
                b = tt % 2
                g, gi = divmod(tt, G)
                ctx.dma("sp", zf[b][:], Z.ap()[tt * P:(tt + 1) * P, :], writes=[zf[b]])
                self.layernorm_tile(zf[b], xn[b], stats[b], mv[b], D, LN_EPS)
                ctx.op("dve", lambda: nc.vector.tensor_tensor(xn[b][:], xn[b][:], gain[:], ALU.mult),
                       reads=[xn[b], gain], writes=[xn[b]])
                ctx.op("pool", lambda: nc.gpsimd.tensor_tensor(xo[b][:], xn[b][:], bias[:], ALU.add),
                       reads=[xn[b], bias], writes=[xo[b]])
                ctx.dma("sp", X.ap()[tt * P:(tt + 1) * P, :], xo[b][:], reads=[xo[b]], writes=[("X", tt)])
                ctx.op("act", lambda: nc.scalar.copy(xb[b][:], xo[b][:]), reads=[xo[b]], writes=[xb[b]])
                self.xt_emit_tile(xb[b], xb[b], stg[g % 2], stg[g % 2], gi * P, pst[b])
                if gi == G - 1:
                    ctx.dma("sp", XTv[:, :, g * G * P:(g + 1) * G * P], stg[g % 2][:], reads=[stg[g % 2]],
                            writes=[("XT", g)])
            ctx.barrier()

    def layernorm_tile(self, src, dst, stats, mv, n, eps, src_key=None, dst_key=None):
        ctx, nc = self.ctx, self.nc
        src_key = src if src_key is None else src_key
        dst_key = dst if dst_key is None else dst_key
        nch = max(1, n // 512)
        w = n // nch
        for c in range(nch):
            ctx.op("dve", lambda: nc.vector.bn_stats(stats[:, c, :], src[:, c * w:(c + 1) * w]),
                   reads=[src_key], writes=[stats])
        ctx.op("dve", lambda: nc.vector.bn_aggr(mv[:, 0:2], stats[:, 0:nch, :]), reads=[stats], writes=[mv])
        ctx.op("dve", lambda: nc.vector.tensor_scalar_add(mv[:, 2:3], mv[:, 1:2], eps), reads=[mv], writes=[mv])
        ctx.op("act", lambda: nc.scalar.activation(mv[:, 2:3], mv[:, 2:3], AF.Sqrt), reads=[mv], writes=[mv])
        ctx.op("dve", lambda: nc.vector.reciprocal(mv[:, 2:3], mv[:, 2:3]), reads=[mv], writes=[mv])
        ctx.op("dve", lambda: nc.vector.tensor_scalar(mv[:, 3:4], mv[:, 0:1], mv[:, 2:3], -1.0, ALU.mult, ALU.mult),
               reads=[mv], writes=[mv])
        ctx.op("act", lambda: nc.scalar.activation(dst[:, 0:n], src[:, 0:n], AF.Identity, bias=mv[:, 3:4],
                                                   scale=mv[:, 2:3]), reads=[src_key, mv], writes=[dst_key])


class TiledW:
    def __init__(self, h, K, N, tw, kp):
        self.h, self.K, self.N, self.tw, self.kp = h, K, N, tw, kp
        self.kcn = K // kp


def _tw(self, name, src_fn, K, N, tw, kp=P):
    if name not in self.inputs:
        self.inp(name, [N // tw, kp, (K // kp) * tw])
        self.tiled[name] = (src_fn, K, N, tw, kp)
    return TiledW(self.inputs[name], K, N, tw, kp)


def _load_wt(self, wb, W, c0, nb, key):
    assert c0 % W.tw == 0 and nb % W.tw == 0, (c0, nb, W.tw)
    i0, nt = c0 // W.tw, nb // W.tw
    for i in range(nt):
        dst = wb[0:W.kp, 0:W.kcn, i * W.tw:(i + 1) * W.tw]
        src = W.h.ap()[i0 + i].rearrange("p (kc j) -> p kc j", j=W.tw)
        self.ctx.dma("pool", dst, src, writes=[key], merge=(i > 0))


Prog.tw = _tw
Prog.load_wt = _load_wt


class GemmRes:
    def __init__(self, prog, st, kcmax, nblk, npsum, nw=2):
        ctx = prog.ctx
        self.w = [ctx.sb(st, "wbuf", [P, kcmax, nblk], BF16) for _ in range(nw)]
        self.ps = [ctx.ps(st, "gps", [P, 512], F32) for _ in range(npsum)]
        self.wi = 0
        self.pi = 0

    def next_w(self):
        w = self.w[self.wi]
        self.wi = (self.wi + 1) % len(self.w)
        return w

    def next_ps(self):
        p = self.ps[self.pi]
        self.pi = (self.pi + 1) % len(self.ps)
        return p


def _gemm_tok(self, res, AT, at_key, kcn, W, n0, ncols, tts, epi, nblk=512, kp=P):
    ctx, nc = self.ctx, self.nc
    c0 = n0
    while c0 < n0 + ncols:
        nb = min(nblk, n0 + ncols - c0)
        wb = res.next_w()
        self.load_wt(wb, W, c0, nb, wb)
        for tt in tts:
            ps = res.next_ps()
            for kc in range(kcn):
                ctx.op("pe", lambda: nc.tensor.matmul(ps[:, 0:nb], lhsT=AT[0:kp, kc, tt * P:(tt + 1) * P],
                                                      rhs=wb[0:kp, kc, 0:nb], start=(kc == 0), stop=(kc == kcn - 1)),
                       reads=[at_key, wb], writes=[ps])
            epi(ps, tt, c0, nb)
        c0 += nb


def _gemm_feat(self, res, AT, at_key, kcn, W, n0, ncols, tgs, epi, nblk=512):
    ctx, nc = self.ctx, self.nc
    c0 = n0
    while c0 < n0 + ncols:
        nb = min(nblk, n0 + ncols - c0)
        wb = res.next_w()
        self.load_wt(wb, W, c0, nb, wb)
        for (t0, tn) in tgs:
            for fb in range((nb + P - 1) // P):
                fw = min(P, nb - fb * P)
                ps = res.next_ps()
                for kc in range(kcn):
                    ctx.op("pe", lambda: nc.tensor.matmul(ps[0:fw, 0:tn], lhsT=wb[:, kc, fb * P:fb * P + fw],
                                                          rhs=AT[:, kc, t0:t0 + tn], start=(kc == 0),
                                                          stop=(kc == kcn - 1)),
                           reads=[at_key, wb], writes=[ps])
                epi(ps, c0 + fb * P, t0, tn)
        c0 += nb


Prog.gemm_tok = _gemm_tok
Prog.gemm_feat = _gemm_feat


def _load_AT(self, st, name, XT, kcn, t0, tn, pad=0):
    ctx = self.ctx
    t = ctx.sb(st, name, [P, kcn, pad + tn], BF16)
    v = XT.ap().rearrange("(kc p) t -> p kc t", p=P)
    step = max(1, kcn // 4)
    for k0 in range(0, kcn, step):
        k1 = min(kcn, k0 + step)
        ctx.dma("sp", t[:, k0:k1, pad:pad + tn], v[:, k0:k1, t0:t0 + tn], writes=[t], merge=(k0 > 0))
    return t


Prog.load_AT = _load_AT


def _epi_resid(self, st, Xold, Zout, tok_base=0):
    ctx, nc = self.ctx, self.nc
    xo = [ctx.sb(st, "rx", [P, 512], F32) for _ in range(3)]
    zt = [ctx.sb(st, "rz", [P, 512], F32) for _ in range(3)]
    cnt = [0]

    def epi(ps, tt, c0, nb):
        i = cnt[0] % 3
        cnt[0] += 1
        r0 = tok_base + tt * P
        ctx.dma("sp", xo[i][:, 0:nb], Xold.ap()[r0:r0 + P, c0:c0 + nb], writes=[xo[i]])
        ctx.op("dve", lambda: nc.vector.scalar_tensor_tensor(zt[i][:, 0:nb], xo[i][:, 0:nb], ALPHA, ps[:, 0:nb],
                                                             ALU.mult, ALU.add),
               reads=[xo[i], ps], writes=[zt[i]])
        ctx.dma("sp", Zout.ap()[r0:r0 + P, c0:c0 + nb], zt[i][:, 0:nb], reads=[zt[i]], writes=[("Z", r0, c0)])
    return epi


Prog.epi_resid = _epi_resid


def _retention_layer(self, layer, X, XT, Z1):
    ctx, nc, cfg = self.ctx, self.nc, self.cfg
    T, NT, NSEG = cfg.T, cfg.NT, cfg.NSEG
    j = layer // 3
    w_in = self.tw("ret_w_in_t%d" % j, lambda inp, j=j: inp["ret_w_in"][j], D, 12288, 256)
    w_out = self.tw("ret_w_out_t%d" % j, lambda inp, j=j: inp["ret_w_out"][j], 4096, D, 256)
    gn_ap = self.inp("ret_gn_gain_%d" % j, [4096]).ap()
    KTs = self.scr("ret_KT", [D, T], BF16)
    Vs = self.scr("ret_V", [T, 4096], BF16)
    OGT = self.scr("ret_OGT", [4096, T], BF16)
    CCI = self.scr("ret_cci", [RET_H, NSEG, 2, P, 512])
    CCO = self.scr("ret_cco", [RET_H, NSEG, 2, P, 512])
    TH = min(1024, T)
    NTH = TH // P
    TGW = min(512, TH)
    tgs = [(t0, TGW) for t0 in range(0, TH, TGW)]
    QOFF, KOFF, VOFF, GOFF = 0, 2048, 4096, 8192
    rope_cos = self.inp("rope_cos", [P, T]).ap()
    rope_sin = self.inp("rope_sin", [P, T]).ap()

    def rotary_epi(cosT, sinT, dstT, tmp):
        state = {}

        def epi(ps, c, t0, tn):
            half = (c // P) % 2
            if half == 0:
                state["A"] = ps
                return
            psA, psB = state["A"], ps
            t1, t2, t3, t4 = tmp
            cs, sn = cosT[:, t0:t0 + tn], sinT[:, t0:t0 + tn]
            ctx.op("dve", lambda: nc.vector.tensor_tensor(t1[:, 0:tn], psA[:, 0:tn], cs, ALU.mult),
                   reads=[psA, cosT], writes=[t1])
            ctx.op("dve", lambda: nc.vector.tensor_tensor(t2[:, 0:tn], psB[:, 0:tn], sn, ALU.mult),
                   reads=[psB, sinT], writes=[t2])
            ctx.op("dve", lambda: nc.vector.tensor_tensor(t3[:, 0:tn], psA[:, 0:tn], sn, ALU.mult),
                   reads=[psA, sinT], writes=[t3])
            ctx.op("dve", lambda: nc.vector.tensor_tensor(t4[:, 0:tn], psB[:, 0:tn], cs, ALU.mult),
                   reads=[psB, cosT], writes=[t4])
            ctx.op("pool", lambda: nc.gpsimd.tensor_tensor(dstT[:, 0, t0:t0 + tn], t1[:, 0:tn], t2[:, 0:tn],
                                                           ALU.subtract), reads=[t1, t2], writes=[dstT])
            ctx.op("pool", lambda: nc.gpsimd.tensor_tensor(dstT[:, 1, t0:t0 + tn], t3[:, 0:tn], t4[:, 0:tn],
                                                           ALU.add), reads=[t3, t4], writes=[dstT])
        return epi

    KTv = KTs.ap().rearrange("(h two p) t -> h p two t", two=2, p=P)
    Vv = Vs.ap().rearrange("(tt p) e -> p tt e", p=P)
    OGTv = OGT.ap().rearrange("(h fc p) t -> h p fc t", fc=4, p=P)

    with contextlib.ExitStack() as st:
        kdec = ctx.sb(st, "kdec", [P, RET_H], F32)
        ctx.dma("sp", kdec[:], self.inp("ret_kdec", [P, RET_H]).ap(), writes=[kdec])
        res = GemmRes(self, st, KC, 512, 3)
        tmp = [ctx.sb(st, "rt", [P, 512], F32) for _ in range(4)]
        kT = [ctx.sb(st, "kT", [P, 2, TH], BF16) for _ in range(2)]
        vh = [ctx.sb(st, "vh", [P, NTH, 512], BF16) for _ in range(2)]
        kdA = [ctx.sb(st, "kdA", [P, 2 * P], BF16) for _ in range(2)]
        pst = [ctx.ps(st, "pst", [P, 2 * P], BF16) for _ in range(2)]
        Lps = [ctx.ps(st, "Lps", [P, 512], F32) for _ in range(2)]
        Lacc = ctx.sb(st, "Lacc", [P, RET_H, 2, 512], F32)
        Lm = [ctx.sb(st, "Lm", [P, 2, 512], F32) for _ in range(2)]
        cosk = ctx.sb(st, "cosk", [P, TH], F32)
        sink = ctx.sb(st, "sink", [P, TH], F32)
        xT = ctx.sb(st, "xT", [P, KC, TH], BF16)
        XTv = XT.ap().rearrange("(kc p) t -> p kc t", p=P)
        for th in range(T // TH):
            t0h = th * TH
            for k0 in range(0, KC, 4):
                ctx.dma("sp", xT[:, k0:k0 + 4, :], XTv[:, k0:k0 + 4, t0h:t0h + TH], writes=[xT], merge=(k0 > 0))
            ctx.dma("sp", cosk[:], rope_cos[:, t0h:t0h + TH], writes=[cosk])
            ctx.dma("sp", sink[:], rope_sin[:, t0h:t0h + TH], writes=[sink])
            ctx.op("pool", lambda: nc.gpsimd.tensor_scalar_mul(cosk[:], cosk[:], RET_DK ** -0.5), reads=[cosk], writes=[cosk])
            ctx.op("pool", lambda: nc.gpsimd.tensor_scalar_mul(sink[:], sink[:], RET_DK ** -0.5), reads=[sink], writes=[sink])
            for h in range(RET_H):
                kTh, vhh = kT[h % 2], vh[h % 2]
                self.gemm_feat(res, xT, xT, KC, w_in, KOFF + h * 256, 256, tgs, rotary_epi(cosk, sink, kTh, tmp), nblk=256)
                ctx.dma("sp", KTv[h][:, :, t0h:t0h + TH], kTh[:], reads=[kTh], writes=[("KT", h, th)])

                def v_epi(ps, tt, c0, nb):
                    ctx.op("act", lambda: nc.scalar.copy(vhh[:, tt, :], ps[:, 0:nb]), reads=[ps], writes=[vhh])
                self.gemm_tok(res, xT, xT, KC, w_in, VOFF + h * 512, 512, range(NTH), v_epi)
                ctx.dma("sp", Vv[:, th * NTH:(th + 1) * NTH, h * 512:(h + 1) * 512], vhh[:],
                        reads=[vhh], writes=[("V", h, th)])
                if NSEG > 1:
                    g = RET_GAMMA[h]
                    for cl in range(NTH):
                        c = th * NTH + cl
                        pt = pst[cl % 2]
                        kd = kdA[cl % 2]
                        for half in range(2):
                            self.transpose_to(kTh[:, half, cl * P:(cl + 1) * P], pt[:, half * P:(half + 1) * P], [kTh], [pt])
                        ctx.op("dve", lambda: nc.vector.tensor_scalar(kd[:], pt[:], kdec[:, h:h + 1],
                                                                      float(g ** (P * (NT - 1 - c))), ALU.mult, ALU.mult),
                               reads=[pt, kdec], writes=[kd])
                        for half in range(2):
                            ctx.op("pe", lambda: nc.tensor.matmul(Lps[half][:], lhsT=kd[:, half * P:(half + 1) * P],
                                                                  rhs=vhh[:, cl, :], start=(cl == 0), stop=(cl == NTH - 1)),
                                   reads=[kd, vhh], writes=[Lps[half]])
                    for half in range(2):
                        if th == 0:
                            ctx.op("act", lambda: nc.scalar.copy(Lacc[:, h, half, :], Lps[half][:]),
                                   reads=[Lps[half]], writes=[(id(Lacc), h)])
                        else:
                            ctx.op("dve", lambda: nc.vector.tensor_tensor(Lacc[:, h, half, :], Lacc[:, h, half, :],
                                                                          Lps[half][:], ALU.add),
                                   reads=[Lps[half], (id(Lacc), h)], writes=[(id(Lacc), h)])
        if NSEG > 1:
            for h in range(RET_H):
                for s in range(NSEG):
                    lm = Lm[(h * NSEG + s) % 2]
                    ctx.op("dve", lambda: nc.vector.tensor_scalar_mul(lm[:], Lacc[:, h], self.own[:, s:s + 1]),
                           reads=[(id(Lacc), h), self.own], writes=[lm])
                    ctx.dma("sp", CCI.ap()[h, s].rearrange("two p e -> p two e"), lm[:], reads=[lm],
                            writes=[("CCI", s, h)])
        ctx.barrier()
    if NSEG > 1:
        for h in range(RET_H):
            ctx.allreduce(cfg.groups, CCI.ap()[h].rearrange("s two p e -> (s two p) e"),
                          CCO.ap()[h].rearrange("s two p e -> (s two p) e"), writes=[("CCO", h)])
        ctx.barrier()

    with contextlib.ExitStack() as st:
        kdec = ctx.sb(st, "kdec", [P, RET_H], F32)
        ctx.dma("sp", kdec[:], self.inp("ret_kdec", [P, RET_H]).ap(), writes=[kdec])
        maskT = ctx.sb(st, "maskT", [P, RET_H, P], F32)
        ctx.dma("sp", maskT[:], self.inp("ret_maskT", [P, RET_H, P]).ap(), writes=[maskT])
        qdec = ctx.sb(st, "qdec", [P, RET_H, P], F32)
        ctx.dma("sp", qdec[:], self.inp("ret_qdec", [P, RET_H, P]).ap(), writes=[qdec])
        coef = ctx.sb(st, "coef", [P, NSEG, RET_H], F32)
        ctx.dma("sp", coef[:], self.inp("ret_coef", [P, NSEG, RET_H]).ap(), writes=[coef])
        gain = self.bcast_rows(st, "gng", gn_ap, 4096)
        res = GemmRes(self, st, KC, 512, 2)
        tmp = [ctx.sb(st, "rt", [P, 512], F32) for _ in range(4)]
        cosT = ctx.sb(st, "cos", [P, TH], F32)
        sinT = ctx.sb(st, "sin", [P, TH], F32)
        xT = ctx.sb(st, "xT", [P, KC, TH], BF16)
        XTv = XT.ap().rearrange("(kc p) t -> p kc t", p=P)
        kTh = ctx.sb(st, "kT", [P, 2, TH], BF16)
        qTh = ctx.sb(st, "qT", [P, 2, TH], BF16)
        vhh = ctx.sb(st, "vh", [P, NTH, 512], BF16)
        gsh = ctx.sb(st, "gs", [P, NTH, 512], BF16)
        ogTh = ctx.sb(st, "ogT", [P, 4, TH], BF16)
        Rall = ctx.sb(st, "Rall", [P, RET_H, 2, 512], F32)
        Rb = ctx.sb(st, "Rb", [P, 2, 512], BF16)
        cin = [ctx.sb(st, "cin", [P, 2, 512], F32) for _ in range(2)]
        sT = [ctx.sb(st, "sT", [P, P], BF16) for _ in range(2)]
        qd = [ctx.sb(st, "qd", [P, 2, P], BF16) for _ in range(2)]
        kd = [ctx.sb(st, "kd", [P, 2 * P], BF16) for _ in range(2)]
        on = [ctx.sb(st, "on", [P, 512], F32) for _ in range(2)]
        og = [ctx.sb(st, "og", [P, 512], F32) for _ in range(2)]
        og2 = [ctx.sb(st, "og2", [P, 512], BF16) for _ in range(2)]
        stats = [ctx.sb(st, "stats", [P, 4, 6], F32) for _ in range(2)]
        mv = [ctx.sb(st, "mv", [P, 4], F32) for _ in range(2)]
        ps_s = ctx.ps(st, "ps_s", [P, 512], F32)
        ps_o = ctx.ps(st, "ps_o", [P, 512], F32)
        ps_t = ctx.ps(st, "ps_t", [P, 2 * P], BF16)
        ps_R = [ctx.ps(st, "ps_R", [P, 512], F32) for _ in range(2)]
        ps_g = ctx.ps(st, "ps_g", [P, 4 * P], BF16)
        for h in range(RET_H):
            Rk = (id(Rall), h)
            if NSEG > 1:
                for s in range(NSEG):
                    ci = cin[s % 2]
                    ctx.dma("sp", ci[:], CCO.ap()[h, s].rearrange("two p e -> p two e"), writes=[ci])
                    if s == 0:
                        ctx.op("dve", lambda: nc.vector.tensor_scalar_mul(Rall[:, h], ci[:], coef[:, s, h:h + 1]),
                               reads=[ci, coef], writes=[Rk])
                    else:
                        ctx.op("dve", lambda: nc.vector.scalar_tensor_tensor(Rall[:, h], ci[:], coef[:, s, h:h + 1],
                                                                             Rall[:, h], ALU.mult, ALU.add),
                               reads=[ci, coef, Rk], writes=[Rk])
            else:
                ctx.op("dve", lambda: nc.vector.memset(Rall[:, h], 0.0), writes=[Rk])
        for th in range(T // TH):
            t0h = th * TH
            for k0 in range(0, KC, 4):
                ctx.dma("sp", xT[:, k0:k0 + 4, :], XTv[:, k0:k0 + 4, t0h:t0h + TH], writes=[xT], merge=(k0 > 0))
            ctx.dma("sp", cosT[:], rope_cos[:, t0h:t0h + TH], writes=[cosT])
            ctx.dma("sp", sinT[:], rope_sin[:, t0h:t0h + TH], writes=[sinT])
            for h in range(RET_H):
                Rk = (id(Rall), h)
                ctx.dma("sp", kTh[:], KTv[h][:, :, t0h:t0h + TH], writes=[kTh])
                ctx.dma("sp", vhh[:], Vv[:, th * NTH:(th + 1) * NTH, h * 512:(h + 1) * 512], writes=[vhh])
                ctx.op("act", lambda: nc.scalar.copy(Rb[:], Rall[:, h]), reads=[Rk], writes=[Rb])
                self.gemm_feat(res, xT, xT, KC, w_in, QOFF + h * 256, 256, tgs, rotary_epi(cosT, sinT, qTh, tmp), nblk=256)

                def g_epi(ps, tt, c0, nb):
                    ctx.op("act", lambda: nc.scalar.activation(gsh[:, tt, :], ps[:, 0:nb], AF.Silu), reads=[ps], writes=[gsh])
                self.gemm_tok(res, xT, xT, KC, w_in, GOFF + h * 512, 512, range(NTH), g_epi)
                gam = RET_GAMMA[h]
                for cl in range(NTH):
                    cb = cl % 2
                    cs = slice(cl * P, (cl + 1) * P)
                    for half in range(2):
                        ctx.op("pe", lambda: nc.tensor.matmul(ps_s[:, 0:P], lhsT=kTh[:, half, cs], rhs=qTh[:, half, cs],
                                                              start=(half == 0), stop=(half == 1)),
                               reads=[kTh, qTh], writes=[ps_s])
                    ctx.op("dve", lambda: nc.vector.tensor_tensor(sT[cb][:], ps_s[:, 0:P], maskT[:, h, :], ALU.mult),
                           reads=[ps_s, maskT], writes=[sT[cb]])
                    for half in range(2):
                        ctx.op("pool", lambda: nc.gpsimd.tensor_tensor(qd[cb][:, half, :], qTh[:, half, cs],
                                                                       qdec[:, h, :], ALU.mult),
                               reads=[qTh, qdec], writes=[qd[cb]])
                    ctx.op("pe", lambda: nc.tensor.matmul(ps_o[:], lhsT=sT[cb][:], rhs=vhh[:, cl, :], start=True, stop=False),
                           reads=[sT[cb], vhh], writes=[ps_o])
                    for half in range(2):
                        ctx.op("pe", lambda: nc.tensor.matmul(ps_o[:], lhsT=qd[cb][:, half, :], rhs=Rb[:, half, :],
                                                              start=False, stop=(half == 1)),
                               reads=[qd[cb], Rb], writes=[ps_o])
                    for half in range(2):
                        self.transpose_to(kTh[:, half, cs], ps_t[:, half * P:(half + 1) * P], [kTh], [ps_t])
                    ctx.op("dve", lambda: nc.vector.tensor_scalar_mul(kd[cb][:], ps_t[:], kdec[:, h:h + 1]),
                           reads=[ps_t, kdec], writes=[kd[cb]])
                    for half in range(2):
                        ctx.op("pe", lambda: nc.tensor.matmul(ps_R[half][:], lhsT=kd[cb][:, half * P:(half + 1) * P],
                                                              rhs=vhh[:, cl, :], start=True, stop=True),
                               reads=[kd[cb], vhh], writes=[ps_R[half]])
                        ctx.op("dve", lambda: nc.vector.scalar_tensor_tensor(Rall[:, h, half, :], Rall[:, h, half, :],
                                                                             float(gam ** P), ps_R[half][:],
                                                                             ALU.mult, ALU.add),
                               reads=[Rk, ps_R[half]], writes=[Rk])
                    ctx.op("act", lambda: nc.scalar.copy(Rb[:], Rall[:, h]), reads=[Rk], writes=[Rb])
                    self.layernorm_tile(ps_o, on[cb], stats[cb], mv[cb], 512, RET_EPS)
                    ctx.op("dve", lambda: nc.vector.tensor_tensor(og[cb][:], on[cb][:], gain[:, h * 512:(h + 1) * 512], ALU.mult),
                           reads=[on[cb], gain], writes=[og[cb]])
                    ctx.op("pool", lambda: nc.gpsimd.tensor_tensor(og2[cb][:], og[cb][:], gsh[:, cl, :], ALU.mult),
                           reads=[og[cb], gsh], writes=[og2[cb]])
                    for fc in range(4):
                        self.transpose_to(og2[cb][:, fc * P:(fc + 1) * P], ps_g[:, fc * P:(fc + 1) * P], [og2[cb]], [ps_g])
                    ctx.op("act", lambda: nc.scalar.copy(ogTh[:, :, cs], ps_g[:].rearrange("p (f c) -> p f c", f=4)),
                           reads=[ps_g], writes=[ogTh])
                ctx.dma("sp", OGTv[h][:, :, t0h:t0h + TH], ogTh[:], reads=[ogTh], writes=[("OGT", h, th)])
        ctx.barrier()

    for t0 in range(0, T, TH):
        with contextlib.ExitStack() as st:
            aT = self.load_AT(st, "ogTa", OGT, 32, t0, TH)
            res = GemmRes(self, st, 32, 512, 3)
            epi = self.epi_resid(st, X, Z1, tok_base=t0)
            self.gemm_tok(res, aT, aT, 32, w_out, 0, D, range(TH // P), epi)
            ctx.barrier()


Prog.retention_layer = _retention_layer
def _halo_rows(self, st, Xsrc, nrows, name):
    ctx, nc, cfg = self.ctx, self.nc, self.cfg
    NSEG, T = cfg.NSEG, cfg.T
    halo = ctx.sb(st, name, [nrows, D], F32)
    if NSEG == 1:
        ctx.op("dve", lambda: nc.vector.memset(halo[:], 0.0), writes=[halo])
        return halo
    HCI = self.scr("halo_ci_%d" % nrows, [NSEG, nrows, D])
    HCO = self.scr("halo_co_%d" % nrows, [NSEG, nrows, D])
    hx = ctx.sb(st, name + "_x", [nrows, D], F32)
    hm = [ctx.sb(st, name + "_m", [nrows, D], F32) for _ in range(2)]
    ctx.dma("sp", hx[:], Xsrc.ap()[T - nrows:T, :], writes=[hx])
    for s in range(NSEG):
        ctx.op("dve", lambda: nc.vector.tensor_scalar_mul(hm[s % 2][:], hx[:], self.own[0:nrows, s:s + 1]),
               reads=[hx, self.own], writes=[hm[s % 2]])
        ctx.dma("sp", HCI.ap()[s], hm[s % 2][:], reads=[hm[s % 2]], writes=[("HCI", s)])
    ctx.barrier()
    ctx.allreduce(cfg.groups, HCI.ap().rearrange("s r d -> (s r) d"), HCO.ap().rearrange("s r d -> (s r) d"),
                  writes=[("HCO",)])
    ctx.barrier()
    for s in range(NSEG):
        ctx.dma("sp", hm[s % 2][:], HCO.ap()[s], writes=[hm[s % 2]])
        if s == 0:
            ctx.op("dve", lambda: nc.vector.tensor_scalar_mul(halo[:], hm[s % 2][:], self.halo_sel[0:nrows, s:s + 1]),
                   reads=[hm[s % 2], self.halo_sel], writes=[halo])
        else:
            ctx.op("dve", lambda: nc.vector.scalar_tensor_tensor(halo[:], hm[s % 2][:], self.halo_sel[0:nrows, s:s + 1],
                                                                 halo[:], ALU.mult, ALU.add),
                   reads=[hm[s % 2], self.halo_sel, halo], writes=[halo])
    return halo


Prog.halo_rows = _halo_rows


def _ffn_layer(self, layer, X1, XT1, Z2):
    ctx, nc, cfg = self.ctx, self.nc, self.cfg
    T = cfg.T
    w_up = self.tw("ffn_w_up_t%d" % layer, lambda inp, l=layer: inp["ffn_w_up"][l], D, 2 * DFF, 128)
    w_dn = self.tw("ffn_w_down_t%d" % layer, lambda inp, l=layer: inp["ffn_w_down"][l], DFF, D, 256)
    cw_ap = self.inp("ffn_conv_w_%d" % layer, [3, 2 * DFF]).ap().rearrange("t (b p) -> (t b) p", p=P)
    cb_ap = self.inp("ffn_conv_b_%d" % layer, [2 * DFF]).ap().rearrange("(b p) -> b p", p=P)
    NB2 = 2 * NFB
    TG = min(1024, T)
    W = TG + 2
    nsub = (W + 511) // 512
    bounds = [(W * i) // nsub for i in range(nsub + 1)]
    with contextlib.ExitStack() as st0:
        psc = ctx.ps(st0, "psc", [P, 512], F32)
        cw = self.load_cols(st0, "cw", cw_ap, 3 * NB2, psc)
        cb = self.load_cols(st0, "cb", cb_ap, NB2, psc)
        haloT = ctx.sb(st0, "haloT", [P, KC, 2], BF16)
        with contextlib.ExitStack() as sth:
            halo = self.halo_rows(sth, X1, 2, "halo2")
            for kc in range(KC):
                ctx.op("pe", lambda: nc.tensor.matmul(psc[:, kc * 2:kc * 2 + 2], lhsT=halo[0:2, kc * P:(kc + 1) * P],
                                                      rhs=self.identf[0:2, 0:2], start=True, stop=True),
                       reads=[halo, self.identf], writes=[psc])
            ctx.op("dve", lambda: nc.vector.tensor_copy(haloT[:], psc[:, 0:2 * KC].rearrange("p (k c) -> p k c", c=2)),
                   reads=[psc], writes=[haloT])
            ctx.barrier()
        gT = ctx.sb(st0, "gT", [P, NFB, TG], BF16)
        XTv = XT1.ap().rearrange("(kc p) t -> p kc t", p=P)
        for g in range(T // TG):
            t0 = g * TG
            with contextlib.ExitStack() as st:
                xT = ctx.sb(st, "x1T", [P, KC, W], BF16)
                for k0 in range(0, KC, 4):
                    ctx.dma("sp", xT[:, k0:k0 + 4, 2:W], XTv[:, k0:k0 + 4, t0:t0 + TG], writes=[xT], merge=(k0 > 0))
                if g == 0:
                    ctx.op("pool", lambda: nc.gpsimd.tensor_copy(xT[:, :, 0:2], haloT[:]), reads=[haloT], writes=[xT])
                else:
                    ctx.dma("sp", xT[:, :, 0:2], XTv[:, :, t0 - 2:t0], writes=[xT], merge=True)
                wu = [ctx.sb(st, "wu", [P, KC, 256], BF16) for _ in range(2)]
                wg = [ctx.sb(st, "wg", [P, KC, 256], BF16) for _ in range(2)]
                pss = [ctx.ps(st, "fps", [P, 512], F32) for _ in range(6)]
                hs = [ctx.sb(st, "hs", [P, W], F32) for _ in range(2)]
                acc = [ctx.sb(st, "acc", [P, TG], F32) for _ in range(2)]
                sg = ctx.sb(st, "sg", [P, TG], F32)
                pi = 0
                for fb in range(NFB):
                    if fb % 2 == 0:
                        wi = (fb // 2) % 2
                        nbk = min(256, DFF - fb * P)
                        self.load_wt(wu[wi], w_up, fb * P, nbk, wu[wi])
                        self.load_wt(wg[wi], w_up, DFF + fb * P, nbk, wg[wi])
                    wi = (fb // 2) % 2
                    fo = (fb % 2) * P
                    for ui, wt in enumerate((wu[wi], wg[wi])):
                        for si in range(nsub):
                            a, b_ = bounds[si], bounds[si + 1]
                            ps = pss[pi % 6]
                            pi += 1
                            for kc in range(KC):
                                ctx.op("pe", lambda: nc.tensor.matmul(ps[:, 0:b_ - a], lhsT=wt[:, kc, fo:fo + P],
                                                                      rhs=xT[:, kc, a:b_], start=(kc == 0), stop=(kc == KC - 1)),
                                       reads=[wt, xT], writes=[ps])
                            ctx.op("act", lambda: nc.scalar.copy(hs[ui][:, a:b_], ps[:, 0:b_ - a]), reads=[ps], writes=[hs[ui]])
                        blk = fb if ui == 0 else NFB + fb
                        w0 = cw[:, 0 * NB2 + blk:0 * NB2 + blk + 1]
                        w1 = cw[:, 1 * NB2 + blk:1 * NB2 + blk + 1]
                        w2 = cw[:, 2 * NB2 + blk:2 * NB2 + blk + 1]
                        ctx.op("act", lambda: nc.scalar.activation(acc[ui][:], hs[ui][:, 2:W], AF.Identity,
                                                                   bias=cb[:, blk:blk + 1], scale=w2),
                               reads=[hs[ui], cw, cb], writes=[acc[ui]])
                        ctx.op("dve", lambda: nc.vector.scalar_tensor_tensor(acc[ui][:], hs[ui][:, 1:W - 1], w1, acc[ui][:],
                                                                             ALU.mult, ALU.add),
                               reads=[hs[ui], cw, acc[ui]], writes=[acc[ui]])
                        ctx.op("dve", lambda: nc.vector.scalar_tensor_tensor(acc[ui][:], hs[ui][:, 0:W - 2], w0, acc[ui][:],
                                                                             ALU.mult, ALU.add),
                               reads=[hs[ui], cw, acc[ui]], writes=[acc[ui]])
                    ctx.op("act", lambda: nc.scalar.activation(sg[:], acc[1][:], AF.Silu), reads=[acc[1]], writes=[sg])
                    ctx.op("pool", lambda: nc.gpsimd.tensor_tensor(gT[:, fb, :], sg[:], acc[0][:], ALU.mult),
                           reads=[sg, acc[0]], writes=[gT])
                ctx.barrier()
            with contextlib.ExitStack() as st:
                res = GemmRes(self, st, NFB, 256, 4)
                epi = self.epi_resid(st, X1, Z2, tok_base=t0)
                self.gemm_tok(res, gT, gT, NFB, w_dn, 0, D, range(TG // P), epi, nblk=256)
                ctx.barrier()


Prog.ffn_layer = _ffn_layer


def _ple_layer(self, layer, X2, XT2, X3):
    ctx, nc, cfg = self.ctx, self.nc, self.cfg
    T, NT = cfg.T, cfg.NT
    w_gate = self.tw("ple_w_gate_t%d" % layer, lambda inp, l=layer: inp["ple_w_gate"][l], D, D, 256)
    w_proj = self.tw("ple_w_proj_t%d" % layer, lambda inp, l=layer: inp["ple_w_proj"][l], PLE, D, 256)
    pT_in = self.inp("pT_%d" % layer, [PLE, T]).ap()
    with contextlib.ExitStack() as st:
        xT = self.load_AT(st, "x2T", XT2, KC, 0, T)
        pT = ctx.sb(st, "pT", [P, 2, T], BF16)
        self.load_w(pT[:], pT_in.rearrange("(kc p) t -> p kc t", p=P), pT)
        wg = [ctx.sb(st, "wg", [P, KC, 512], BF16) for _ in range(2)]
        wp = [ctx.sb(st, "wp", [P, 2, 512], BF16) for _ in range(2)]
        psg = [ctx.ps(st, "psg", [P, 512], F32) for _ in range(3)]
        psp = [ctx.ps(st, "psp", [P, 512], F32) for _ in range(3)]
        sg = [ctx.sb(st, "sg", [P, 512], F32) for _ in range(3)]
        x2 = [ctx.sb(st, "x2", [P, 512], F32) for _ in range(3)]
        x3 = [ctx.sb(st, "x3", [P, 512], F32) for _ in range(3)]
        it = 0
        for ci, c0 in enumerate(range(0, D, 512)):
            wgi, wpi = wg[ci % 2], wp[ci % 2]
            self.load_wt(wgi, w_gate, c0, 512, wgi)
            self.load_wt(wpi, w_proj, c0, 512, wpi)
            for tt in range(NT):
                i = it % 3
                it += 1
                ts = slice(tt * P, (tt + 1) * P)
                for kc in range(KC):
                    ctx.op("pe", lambda: nc.tensor.matmul(psg[i][:], lhsT=xT[:, kc, ts], rhs=wgi[:, kc, :],
                                                          start=(kc == 0), stop=(kc == KC - 1)),
                           reads=[xT, wgi], writes=[psg[i]])
                for kc in range(2):
                    ctx.op("pe", lambda: nc.tensor.matmul(psp[i][:], lhsT=pT[:, kc, ts], rhs=wpi[:, kc, :],
                                                          start=(kc == 0), stop=(kc == 1)),
                           reads=[pT, wpi], writes=[psp[i]])
                ctx.dma("sp", x2[i][:], X2.ap()[tt * P:(tt + 1) * P, c0:c0 + 512], writes=[x2[i]])
                ctx.op("act", lambda: nc.scalar.activation(sg[i][:], psg[i][:], AF.Sigmoid), reads=[psg[i]], writes=[sg[i]])
                ctx.op("dve", lambda: nc.vector.tensor_tensor(sg[i][:], sg[i][:], psp[i][:], ALU.mult),
                       reads=[sg[i], psp[i]], writes=[sg[i]])
                ctx.op("pool", lambda: nc.gpsimd.tensor_tensor(x3[i][:], sg[i][:], x2[i][:], ALU.add),
                       reads=[sg[i], x2[i]], writes=[x3[i]])
                ctx.dma("sp", X3.ap()[tt * P:(tt + 1) * P, c0:c0 + 512], x3[i][:], reads=[x3[i]], writes=[("X3", tt, c0)])
        ctx.barrier()


Prog.ple_layer = _ple_layer
SWA_HQ, SWA_HKV, SWA_HD, SWA_W = 32, 4, 64, 128
NEG = -1e30


def _swa_head_order():
    order = []
    for pair in range(2):
        for g in range(8):
            order.append((2 * pair) * 8 + g)
            order.append((2 * pair + 1) * 8 + g)
    return order


def _t5_bucket(n):
    max_exact = 16
    if n < max_exact:
        return n
    large = max_exact + int(np.log(max(n, 1) / max_exact) / np.log(SWA_W / max_exact) * (32 - max_exact))
    return min(large, 31)


def _swa_consts(cfg, core, c):
    seg = core % cfg.NSEG
    E = np.zeros((32, 383), np.float32)
    for u in range(383):
        d = u - 127
        if 0 <= d < SWA_W:
            nn = np.maximum(np.array([d]), 0)
            large = 16 + (np.log(np.maximum(nn, 1) / 16) / np.log(SWA_W / 16) * 16).astype(np.int32)
            large = np.minimum(large, 31)
            b = int(np.where(nn < 16, nn, large)[0])
            E[b, u] = 1.0
    c["swa_E"] = E
    i = np.arange(P)[:, None]
    j = np.arange(2 * P)[None, :]
    d = i + P - j
    c["swa_maskc"] = np.where((d >= 0) & (d < SWA_W), 0.0, NEG).astype(np.float32)
    mf = np.zeros((P, 2 * P), np.float32)
    if seg == 0:
        mf[:, :P] = NEG
    c["swa_mask_first"] = mf


def _swa_layer(self, layer, X, XT, Z1):
    ctx, nc, cfg = self.ctx, self.nc, self.cfg
    T, NT, NSEG = cfg.T, cfg.NT, cfg.NSEG
    j = layer // 3
    def _qkv_src(inp, j=j):
        w = inp["swa_w_qkv"][j]
        qcols = np.concatenate([np.arange(h * 64, (h + 1) * 64) for h in _swa_head_order()])
        return np.concatenate([w[:, qcols], w[:, 2048:]], axis=1)

    def _out_src(inp, j=j):
        rows = np.concatenate([np.arange(h * 64, (h + 1) * 64) for h in _swa_head_order()])
        return inp["swa_w_out"][j][rows, :]
    w_qkv = self.tw("swa_w_qkv_t%d" % j, _qkv_src, D, 2560, 256)
    w_out = self.tw("swa_w_out_t%d" % j, _out_src, D, D, 256)
    sinks_ap = self.inp("swa_sinks_%d" % j, [SWA_HQ]).ap()
    relb_ap = self.inp("rel_bias", [32, SWA_HQ]).ap()
    OT = self.scr("swa_OT", [D, T], BF16)
    QTs = self.scr("swa_QT", [D, T], BF16)
    order = _swa_head_order()
    TGW = min(512, T)
    tgs = [(t0, TGW) for t0 in range(0, T, TGW)]
    with contextlib.ExitStack() as st0:
        kT = ctx.sb(st0, "kT", [P, 2, P + T], BF16)
        vS = ctx.sb(st0, "vS", [P, 1 + NT, 256], BF16)
        biasS = ctx.sb(st0, "biasS", [P, SWA_HQ, 2 * P], F32)
        sinkb = self.bcast_rows(st0, "sinkb", sinks_ap, SWA_HQ)
        mfirst = ctx.sb(st0, "mfirst", [P, 2 * P], F32)
        ctx.dma("sp", mfirst[:], self.inp("swa_mask_first", [P, 2 * P]).ap(), writes=[mfirst])
        with contextlib.ExitStack() as st:
            E = ctx.sb(st, "E", [32, 383], F32)
            RB = ctx.sb(st, "RB", [32, SWA_HQ], F32)
            maskc = ctx.sb(st, "maskc", [P, 2 * P], F32)
            ctx.dma("sp", E[:], self.inp("swa_E", [32, 383]).ap(), writes=[E])
            ctx.dma("sp", RB[:], relb_ap, writes=[RB])
            ctx.dma("sp", maskc[:], self.inp("swa_maskc", [P, 2 * P]).ap(), writes=[maskc])
            psb = [ctx.ps(st, "psb", [P, 512], F32) for _ in range(2)]
            for r in range(16):
                ps = psb[r % 2]
                for jj in range(16):
                    jk = r * 16 + jj
                    ctx.op("pe", lambda: nc.tensor.matmul(ps[:, jj * 32:(jj + 1) * 32], lhsT=E[:, 255 - jk:383 - jk], rhs=RB[:],
                                                          start=True, stop=True), reads=[E, RB], writes=[ps])
                ctx.op("dve", lambda: nc.vector.tensor_tensor(
                    biasS[:, :, r * 16:(r + 1) * 16].rearrange("p h j -> p j h"),
                    ps[:].rearrange("p (j h) -> p j h", h=32),
                    maskc[:, r * 16:(r + 1) * 16].unsqueeze(2).broadcast_to([P, 16, 32]), ALU.add),
                    reads=[ps, maskc], writes=[biasS])
            ctx.barrier()
        with contextlib.ExitStack() as st:
            xT = self.load_AT(st, "xT", XT, KC, 0, T)
            res = GemmRes(self, st, KC, 512, 3)
            qst = [ctx.sb(st, "qst", [P, 4, TGW], BF16) for _ in range(2)]
            QTv = QTs.ap().rearrange("(kc p) t -> p kc t", p=P)
            qcnt = [0]

            def q_epi(ps, c, t0, tn):
                kc = c // P
                qs = qst[(qcnt[0] // 4) % 2]
                ctx.op("act", lambda: nc.scalar.activation(qs[:, kc % 4, 0:tn], ps[:, 0:tn], AF.Copy, scale=SWA_HD ** -0.5),
                       reads=[ps], writes=[qs])
                qcnt[0] += 1
                if kc % 4 == 3:
                    ctx.dma("sp", QTv[:, kc - 3:kc + 1, t0:t0 + tn], qs[:, :, 0:tn], reads=[qs], writes=[("QT", kc, t0)])
            self.gemm_feat(res, xT, xT, KC, w_qkv, 0, 2048, tgs, q_epi)

            def k_epi(ps, c, t0, tn):
                fb = (c - 2048) // P
                ctx.op("act", lambda: nc.scalar.copy(kT[:, fb, P + t0:P + t0 + tn], ps[:, 0:tn]), reads=[ps], writes=[kT])
            self.gemm_feat(res, xT, xT, KC, w_qkv, 2048, 256, tgs, k_epi, nblk=256)

            def v_epi(ps, tt, c0, nb):
                ctx.op("act", lambda: nc.scalar.copy(vS[:, 1 + tt, :], ps[:, 0:nb]), reads=[ps], writes=[vS])
            self.gemm_tok(res, xT, xT, KC, w_qkv, 2304, 256, range(NT), v_epi, nblk=256)
            ctx.barrier()
        with contextlib.ExitStack() as st:
            if NSEG == 1:
                ctx.op("dve", lambda: nc.vector.memset(kT[:, :, 0:P], 0.0), writes=[kT])
                ctx.op("dve", lambda: nc.vector.memset(vS[:, 0, :], 0.0), writes=[vS])
            else:
                HCI = self.scr("swa_ci", [NSEG, P, 512])
                HCO = self.scr("swa_co", [NSEG, P, 512])
                hb = ctx.sb(st, "hb", [P, 512], F32)
                hm = [ctx.sb(st, "hm", [P, 512], F32) for _ in range(2)]
                ctx.op("dve", lambda: nc.vector.tensor_copy(hb[:, 0:256].rearrange("p (a b) -> p a b", a=2), kT[:, :, T:T + P]),
                       reads=[kT], writes=[hb])
                ctx.op("dve", lambda: nc.vector.tensor_copy(hb[:, 256:512], vS[:, NT, :]), reads=[vS], writes=[hb])
                for s in range(NSEG):
                    ctx.op("dve", lambda: nc.vector.tensor_scalar_mul(hm[s % 2][:], hb[:], self.own[:, s:s + 1]),
                           reads=[hb, self.own], writes=[hm[s % 2]])
                    ctx.dma("sp", HCI.ap()[s], hm[s % 2][:], reads=[hm[s % 2]], writes=[("HCI", s)])
                ctx.barrier()
                ctx.allreduce(cfg.groups, HCI.ap().rearrange("s p e -> (s p) e"), HCO.ap().rearrange("s p e -> (s p) e"),
                              writes=[("HCO",)])
                ctx.barrier()
                for s in range(NSEG):
                    ctx.dma("sp", hm[s % 2][:], HCO.ap()[s], writes=[hm[s % 2]])
                    if s == 0:
                        ctx.op("dve", lambda: nc.vector.tensor_scalar_mul(hb[:], hm[s % 2][:], self.halo_sel[:, s:s + 1]),
                               reads=[hm[s % 2], self.halo_sel], writes=[hb])
                    else:
                        ctx.op("dve", lambda: nc.vector.scalar_tensor_tensor(hb[:], hm[s % 2][:], self.halo_sel[:, s:s + 1],
                                                                             hb[:], ALU.mult, ALU.add),
                               reads=[hm[s % 2], self.halo_sel, hb], writes=[hb])
                ctx.op("dve", lambda: nc.vector.tensor_copy(kT[:, :, 0:P], hb[:, 0:256].rearrange("p (a b) -> p a b", a=2)),
                       reads=[hb], writes=[kT])
                ctx.op("dve", lambda: nc.vector.tensor_copy(vS[:, 0, :], hb[:, 256:512]), reads=[hb], writes=[vS])
            ctx.barrier()
        with contextlib.ExitStack() as st:
            qT = self.load_AT(st, "qT", QTs, KC, 0, T)
            ps_s = ctx.ps(st, "ps_s", [P, 8, 2 * P], F32)
            ps_t = ctx.ps(st, "ps_t", [P, 16, P], BF16)
            ps_o = ctx.ps(st, "ps_o", [P, 8, P], F32)
            s_sb = ctx.sb(st, "s_sb", [P, 8, 2 * P], F32)
            e_sb = ctx.sb(st, "e_sb", [P, 8, 2 * P], F32)
            p_sb = ctx.sb(st, "p_sb", [P, 8, 2 * P], BF16)
            pT = ctx.sb(st, "pT", [P, 16, P], BF16)
            mx = ctx.sb(st, "mx", [P, 8], F32)
            nmx = ctx.sb(st, "nmx", [P, 8], F32)
            rs = ctx.sb(st, "rs", [P, 8], F32)
            es = ctx.sb(st, "es", [P, 8], F32)
            G = 4 if NT % 4 == 0 else 2
            ost = [ctx.sb(st, "ost", [P, KC, G * P], BF16) for _ in range(2)]
            OTv = OT.ap().rearrange("(kc p) t -> p kc t", p=P)
            for n in range(NT):
                og = ost[(n // G) % 2]
                for pair in range(2):
                    for par in range(2):
                        hk = 2 * pair + par
                        po = par * 64
                        kc_k = hk // 2
                        for g in range(8):
                            ch = pair * 8 + g
                            ctx.op("pe", lambda: nc.tensor.matmul(ps_s[:, g, :], lhsT=qT[po:po + 64, ch, n * P:(n + 1) * P],
                                                                  rhs=kT[po:po + 64, kc_k, n * P:n * P + 2 * P],
                                                                  start=True, stop=True),
                                   reads=[qT, kT], writes=[ps_s])
                        ctx.op("dve", lambda: nc.vector.tensor_tensor(s_sb[:], ps_s[:], biasS[:, hk * 8:(hk + 1) * 8, :], ALU.add),
                               reads=[ps_s, biasS], writes=[s_sb])
                        if n == 0:
                            ctx.op("pool", lambda: nc.gpsimd.tensor_tensor(s_sb[:], s_sb[:],
                                                                           mfirst[:].unsqueeze(1).broadcast_to([P, 8, 2 * P]), ALU.add),
                                   reads=[s_sb, mfirst], writes=[s_sb])
                        ctx.op("dve", lambda: nc.vector.tensor_reduce(mx[:], s_sb[:], AX.X, ALU.max), reads=[s_sb], writes=[mx])
                        ctx.op("dve", lambda: nc.vector.tensor_tensor(mx[:], mx[:], sinkb[:, hk * 8:(hk + 1) * 8], ALU.max),
                               reads=[mx, sinkb], writes=[mx])
                        ctx.op("dve", lambda: nc.vector.tensor_scalar_mul(nmx[:], mx[:], -1.0), reads=[mx], writes=[nmx])
                        ctx.op("dve", lambda: nc.vector.memset(rs[:], 0.0), writes=[rs])
                        for g in range(8):
                            ctx.op("act", lambda: nc.scalar.activation(e_sb[:, g, :], s_sb[:, g, :], AF.Exp, bias=nmx[:, g:g + 1],
                                                                       scale=1.0, accum_out=rs[:, g:g + 1]),
                                   reads=[s_sb, nmx], writes=[e_sb, rs])
                        ctx.op("dve", lambda: nc.vector.tensor_tensor(es[:], sinkb[:, hk * 8:(hk + 1) * 8], mx[:], ALU.subtract),
                               reads=[sinkb, mx], writes=[es])
                        ctx.op("act", lambda: nc.scalar.activation(es[:], es[:], AF.Exp), reads=[es], writes=[es])
                        ctx.op("dve", lambda: nc.vector.tensor_tensor(rs[:], rs[:], es[:], ALU.add), reads=[rs, es], writes=[rs])
                        ctx.op("dve", lambda: nc.vector.reciprocal(rs[:], rs[:]), reads=[rs], writes=[rs])
                        ctx.op("pool", lambda: nc.gpsimd.tensor_tensor(p_sb[:], e_sb[:], rs[:].unsqueeze(2).broadcast_to([P, 8, 2 * P]),
                                                                       ALU.mult), reads=[e_sb, rs], writes=[p_sb])
                        for g in range(8):
                            for hf in range(2):
                                self.transpose_to(p_sb[:, g, hf * P:(hf + 1) * P], ps_t[:, g * 2 + hf, :], [p_sb], [ps_t])
                        ctx.op("act", lambda: nc.scalar.copy(pT[:], ps_t[:]), reads=[ps_t], writes=[pT])
                        for g in range(8):
                            for hf in range(2):
                                ctx.op("pe", lambda: nc.tensor.matmul(ps_o[po:po + 64, g, :], lhsT=vS[:, n + hf, hk * 64:(hk + 1) * 64],
                                                                      rhs=pT[:, g * 2 + hf, :], start=(hf == 0), stop=(hf == 1)),
                                       reads=[vS, pT], writes=[ps_o])
                    ctx.op("dve", lambda: nc.vector.tensor_copy(og[:, pair * 8:(pair + 1) * 8, (n % G) * P:(n % G + 1) * P], ps_o[:]),
                           reads=[ps_o], writes=[og])
                if n % G == G - 1:
                    g0 = (n // G) * G * P
                    ctx.dma("sp", OTv[:, :, g0:g0 + G * P], og[:], reads=[og], writes=[("OT", n)])
            ctx.barrier()
    with contextlib.ExitStack() as st:
        aT = self.load_AT(st, "oTa", OT, KC, 0, T)
        res = GemmRes(self, st, KC, 512, 3)
        epi = self.epi_resid(st, X, Z1)
        self.gemm_tok(res, aT, aT, KC, w_out, 0, D, range(NT), epi)
        ctx.barrier()


Prog.swa_layer = _swa_layer
RW_H, RW_HD = 32, 64
RW_EPS = 64e-5
RW_NHB = RW_H // 2


def _rwkv_consts(cfg, core, c):
    seg = core % cfg.NSEG
    f32 = np.float32
    s = np.arange(P)[:, None]
    t = np.arange(P)[None, :]
    c["rw_mus"] = (s < t).astype(f32)
    c["rw_mui"] = (s <= t).astype(f32)
    c["rw_mls"] = (s > t).astype(f32)
    c["rw_tri"] = (s <= t).astype(f32)
    c["rw_suf"] = (s > t).astype(f32)
    i2 = np.zeros((P, 64), f32)
    i2[np.arange(P), np.arange(P) % 64] = 1.0
    c["rw_i2"] = i2
    selm = np.zeros((P, cfg.NSEG), f32)
    selm[:, :seg] = 1.0
    c["rw_selm"] = selm
    c["rw_nselm"] = 1.0 - selm


def _rwkv_layer(self, layer, X, XT, Z1):
    ctx, nc, cfg = self.ctx, self.nc, self.cfg
    T, NT, NSEG = cfg.T, cfg.NT, cfg.NSEG
    j = layer // 3
    gi = lambda name, shape: self.inp("%s_%d" % (name, j), shape).ap()
    tw_ = lambda nm, fn, K_, N_, t_, kp_=P: self.tw("%s_t%d" % (nm, j), fn, K_, N_, t_, kp_)
    w_rkv = [tw_("rwkv_w_rkv%d" % i, (lambda inp, i=i: inp["rwkv_w_rkv"][j][i]), D, D, 256) for i in range(3)]
    w1 = tw_("rwkv_w1", lambda inp: inp["rwkv_w1"][j], D, 96, 96)
    w2 = tw_("rwkv_w2", lambda inp: inp["rwkv_w2"][j], 96, D, 256, 96)
    a1 = tw_("rwkv_a1", lambda inp: inp["rwkv_a1"][j], D, 96, 96)
    a2 = tw_("rwkv_a2", lambda inp: inp["rwkv_a2"][j], 96, D, 256, 96)
    g1 = tw_("rwkv_g1", lambda inp: inp["rwkv_g1"][j], D, 256, 256)
    g2 = tw_("rwkv_g2", lambda inp: inp["rwkv_g2"][j], 256, D, 256)
    w_out = tw_("rwkv_w_out", lambda inp: inp["rwkv_w_out"][j], D, D, 256)
    mix_ap = gi("rwkv_mix", [6, D]).rearrange("i (kc p) -> (i kc) p", p=P)
    Rs, Ks, Vs = self.scr("rw_R", [T, D]), self.scr("rw_K", [T, D]), self.scr("rw_V", [T, D])
    WLs, ALs, Gs = self.scr("rw_WL", [T, D]), self.scr("rw_AL", [T, D]), self.scr("rw_G", [T, D])
    Y0 = self.scr("rw_Y0", [T, D])
    BON = self.scr("rw_BON", [T, RW_H])
    YTR = self.scr("rw_YTR", [NT, P, RW_NHB, P], BF16)
    OGT = self.scr("rw_OGT", [D, T], BF16)
    TGW = min(512, T)
    tgs = [(t0, TGW) for t0 in range(0, T, TGW)]

    with contextlib.ExitStack() as st:
        psc = ctx.ps(st, "psc", [P, 512], F32)
        mixc = self.load_cols(st, "mixc", mix_ap, 6 * KC, psc)
        xT = self.load_AT(st, "xT", XT, KC, 0, T, pad=1)
        with contextlib.ExitStack() as sth:
            halo = self.halo_rows(sth, X, 1, "halo1")
            for kc in range(KC):
                ctx.op("pe", lambda: nc.tensor.matmul(psc[:, kc:kc + 1], lhsT=halo[0:1, kc * P:(kc + 1) * P],
                                                      rhs=self.identf[0:1, 0:1], start=True, stop=True),
                       reads=[halo, self.identf], writes=[psc])
            ctx.op("dve", lambda: nc.vector.tensor_copy(xT[:, :, 0:1], psc[:, 0:KC].unsqueeze(2)), reads=[psc], writes=[xT])
            ctx.barrier()
        xm = ctx.sb(st, "xm", [P, KC, T], BF16)
        dtmp = [ctx.sb(st, "dtmp", [P, T], F32) for _ in range(2)]
        hT = ctx.sb(st, "hT", [P, 2, T], BF16)
        res = GemmRes(self, st, KC, 256, 4)
        obuf = [ctx.sb(st, "obuf", [P, 512], F32) for _ in range(3)]
        ocnt = [0]

        def store_epi(dst):
            def epi(ps, tt, c0, nb):
                o = obuf[ocnt[0] % 3]
                ocnt[0] += 1
                ctx.op("act", lambda: nc.scalar.copy(o[:, 0:nb], ps[:, 0:nb]), reads=[ps], writes=[o])
                ctx.dma("sp", dst.ap()[tt * P:(tt + 1) * P, c0:c0 + nb], o[:, 0:nb], reads=[o], writes=[("o", id(dst), tt, c0)])
            return epi

        def build_mix(i):
            for kc in range(KC):
                d = dtmp[kc % 2]
                ctx.op("pool", lambda: nc.gpsimd.tensor_tensor(d[:], xT[:, kc, 0:T], xT[:, kc, 1:T + 1], ALU.subtract),
                       reads=[xT], writes=[d])
                ctx.op("dve", lambda: nc.vector.scalar_tensor_tensor(xm[:, kc, :], d[:], mixc[:, i * KC + kc:i * KC + kc + 1],
                                                                     xT[:, kc, 1:T + 1], ALU.mult, ALU.add),
                       reads=[d, mixc, xT], writes=[xm])

        def lora(i, wa, na, func, wb_, dst):
            build_mix(i)
            kcn2 = (na + P - 1) // P
            kp = min(P, na)

            def h_epi(ps, c, t0, tn):
                fw = min(P, na - c)
                ctx.op("act", lambda: nc.scalar.activation(hT[0:fw, c // P, t0:t0 + tn], ps[0:fw, 0:tn], func),
                       reads=[ps], writes=[hT])
            self.gemm_feat(res, xm, xm, KC, wa, 0, na, tgs, h_epi, nblk=256)
            self.gemm_tok(res, hT, hT, kcn2, wb_, 0, D, range(NT), store_epi(dst), kp=kp, nblk=256)

        build_mix(0)
        self.gemm_tok(res, xm, xm, KC, w_rkv[0], 0, D, range(NT), store_epi(Rs), nblk=256)
        build_mix(2)
        self.gemm_tok(res, xm, xm, KC, w_rkv[1], 0, D, range(NT), store_epi(Ks), nblk=256)
        build_mix(3)
        self.gemm_tok(res, xm, xm, KC, w_rkv[2], 0, D, range(NT), store_epi(Vs), nblk=256)
        lora(1, w1, 96, AF.Tanh, w2, WLs)
        lora(4, a1, 96, AF.Identity, a2, ALs)
        lora(5, g1, 256, AF.Sigmoid, g2, Gs)
        ctx.barrier()
    if cfg.stop == "rw1":
        return

    SXs = self.scr("rw_SX", [P, RW_NHB, P])
    with contextlib.ExitStack() as st:
        def cload(name, shape, dtype=F32):
            t_ = ctx.sb(st, name, shape, dtype)
            ctx.dma("sp", t_[:], self.inp(name, shape, dtype).ap(), writes=[t_])
            return t_
        mus, mui, mls = cload("rw_mus", [P, P]), cload("rw_mui", [P, P]), cload("rw_mls", [P, P])
        tri, suft = cload("rw_tri", [P, P]), cload("rw_suf", [P, P])
        i2 = cload("rw_i2", [P, 64])
        ones = ctx.sb(st, "ones", [P, 1], F32)
        ctx.op("dve", lambda: nc.vector.memset(ones[:], 1.0), writes=[ones])
        w0b = self.bcast_rows(st, "w0b", gi("rwkv_w0", [D]), D)
        a0b = self.bcast_rows(st, "a0b", gi("rwkv_a0", [D]), D)
        kkb = self.bcast_rows(st, "kkb", gi("rwkv_k_k", [D]), D)
        kab = self.bcast_rows(st, "kab", gi("rwkv_k_a", [D]), D)
        rkb = self.bcast_rows(st, "rkb", gi("rwkv_r_k", [RW_H, RW_HD]).rearrange("h d -> (h d)"), D)
        A = ctx.sb(st, "A", [P, D], F32)
        B = ctx.sb(st, "B", [P, D], F32)
        Dw = ctx.sb(st, "Dw", [P, D], F32)
        Ea = ctx.sb(st, "Ea", [P, D], F32)
        Fk = ctx.sb(st, "Fk", [P, D], F32)
        T1 = ctx.sb(st, "T1", [P, D], F32)
        ET = [ctx.sb(st, "ET", [P, 512], F32) for _ in range(4)]
        tok = [ctx.sb(st, "tokb", [P, D], BF16) for _ in range(4)]
        bbk = ctx.sb(st, "bbk", [P, 2, D], BF16)
        vbx = ctx.sb(st, "vbx", [P, RW_H, P], BF16)
        ctx.op("pool", lambda: nc.gpsimd.memset(vbx[:], 0.0), writes=[vbx])
        CM = ctx.sb(st, "CM", [P, RW_NHB, 4, P], BF16)
        ytile = ctx.sb(st, "ytile", [P, D], F32)
        T2 = ytile
        ytr = ctx.sb(st, "ytr", [P, RW_NHB, P], BF16)
        ss = ctx.sb(st, "ss", [P, RW_H], F32)
        bon = ctx.sb(st, "bon", [P, RW_H], F32)
        dectot = ctx.sb(st, "dectot", [P, RW_NHB], F32)
        SX = ctx.sb(st, "SX", [P, RW_NHB, P], F32)
        SXb = ctx.sb(st, "SXb", [P, RW_NHB, P], BF16)
        GA = [ctx.sb(st, "GA", [P, 2, 2, P], BF16) for _ in range(2)]
        Brb = ctx.sb(st, "Brb", [P, 2, P], BF16)
        Aak = ctx.sb(st, "Aak", [P, 2, P], BF16)
        Brk = ctx.sb(st, "Brk", [P, 2, P], BF16)
        Tt = [ctx.sb(st, "Tt", [P, 2, P], BF16) for _ in range(2)]
        Wb = ctx.sb(st, "Wb", [P, 2, P], BF16)
        Ub = ctx.sb(st, "Ub", [P, 2, P], BF16)
        pb = [ctx.ps(st, "pb", [P, 512], F32) for _ in range(7)]
        ptr = ctx.ps(st, "ptr", [P, 8, P], BF16)
        ctx.op("dve", lambda: nc.vector.memset(SX[:], 0.0), writes=[SX])
        ctx.op("dve", lambda: nc.vector.tensor_copy(SX[:, :, 64:128], i2[:].unsqueeze(1).broadcast_to([P, RW_NHB, 64])),
               reads=[i2, SX], writes=[SX])
        ctx.op("pool", lambda: nc.gpsimd.tensor_copy(SXb[:], SX[:]), reads=[SX], writes=[SXb])
        v3 = lambda t_: t_[:].rearrange("p (h d) -> p h d", d=RW_HD)
        bc3 = lambda small: small[:].unsqueeze(2).broadcast_to([P, RW_H, RW_HD])
        for n in range(NT):
            rows = slice(n * P, (n + 1) * P)
            ctx.dma("sp", A[:], Rs.ap()[rows, :], writes=[A])
            ctx.dma("sp", B[:], Ks.ap()[rows, :], writes=[B])
            ctx.dma("sp", T1[:], Vs.ap()[rows, :], writes=[T1])
            ctx.op("act", lambda: nc.scalar.copy(vbx[:, :, 0:64], v3(T1)), reads=[T1], writes=[vbx])
            ctx.dma("sp", Dw[:], WLs.ap()[rows, :], writes=[Dw])
            ctx.dma("sp", Ea[:], ALs.ap()[rows, :], writes=[Ea])
            ctx.op("dve", lambda: nc.vector.tensor_tensor(Dw[:], Dw[:], w0b[:], ALU.add), reads=[Dw, w0b], writes=[Dw])
            ctx.op("act", lambda: nc.scalar.activation(Dw[:], Dw[:], AF.Sigmoid), reads=[Dw], writes=[Dw])
            ctx.op("dve", lambda: nc.vector.tensor_scalar_mul(Dw[:], Dw[:], -math.exp(-0.5)), reads=[Dw], writes=[Dw])
            ctx.op("dve", lambda: nc.vector.tensor_tensor(Ea[:], Ea[:], a0b[:], ALU.add), reads=[Ea, a0b], writes=[Ea])
            ctx.op("act", lambda: nc.scalar.activation(Ea[:], Ea[:], AF.Sigmoid), reads=[Ea], writes=[Ea])
            ctx.op("pool", lambda: nc.gpsimd.tensor_tensor(Fk[:], B[:], kkb[:], ALU.mult), reads=[B, kkb], writes=[Fk])
            ctx.op("pool", lambda: nc.gpsimd.tensor_tensor(T2[:], Fk[:], Fk[:], ALU.mult), reads=[Fk], writes=[T2])
            ctx.op("dve", lambda: nc.vector.tensor_reduce(ss[:], v3(T2), AX.X, ALU.add), reads=[T2], writes=[ss])
            ctx.op("act", lambda: nc.scalar.activation(ss[:], ss[:], AF.Sqrt), reads=[ss], writes=[ss])
            ctx.op("dve", lambda: nc.vector.tensor_scalar_max(ss[:], ss[:], 1e-12), reads=[ss], writes=[ss])
            ctx.op("dve", lambda: nc.vector.reciprocal(ss[:], ss[:]), reads=[ss], writes=[ss])
            ctx.op("dve", lambda: nc.vector.tensor_tensor(v3(Fk), v3(Fk), bc3(ss), ALU.mult), reads=[Fk, ss], writes=[Fk])
            ctx.op("dve", lambda: nc.vector.scalar_tensor_tensor(T1[:], Ea[:], -1.0, kab[:], ALU.add, ALU.mult),
                   reads=[Ea, kab], writes=[T1])
            ctx.op("pool", lambda: nc.gpsimd.tensor_tensor(T1[:], T1[:], B[:], ALU.mult), reads=[T1, B], writes=[T1])
            ctx.op("pool", lambda: nc.gpsimd.tensor_tensor(B[:], B[:], T1[:], ALU.add), reads=[T1, B], writes=[B])
            ctx.op("pool", lambda: nc.gpsimd.tensor_tensor(T2[:], A[:], B[:], ALU.mult), reads=[A, B, T2], writes=[T2])
            ctx.op("dve", lambda: nc.vector.tensor_tensor(T2[:], T2[:], rkb[:], ALU.mult), reads=[T2, rkb], writes=[T2])
            ctx.op("dve", lambda: nc.vector.tensor_reduce(bon[:], v3(T2), AX.X, ALU.add), reads=[T2], writes=[bon])
            ctx.dma("sp", BON.ap()[rows, :], bon[:], reads=[bon], writes=[("BON", n)])
            ctx.op("dve", lambda: nc.vector.tensor_tensor(T1[:], Fk[:], Ea[:], ALU.mult), reads=[Fk, Ea, T1], writes=[T1])
            for hb in range(RW_NHB):
                ctx.op("pe", lambda: nc.tensor.matmul(pb[0][:, hb:hb + 1], lhsT=Dw[:, hb * P:(hb + 1) * P], rhs=ones[:, 0:1],
                                                      start=True, stop=True), reads=[Dw, ones], writes=[pb[0]])
            ctx.op("act", lambda: nc.scalar.activation(dectot[:], pb[0][:, 0:RW_NHB], AF.Exp), reads=[pb[0]], writes=[dectot])
            for cb in range(4):
                cs = slice(cb * 512, (cb + 1) * 512)
                pcum, psuf = pb[1 + (cb % 2) * 2], pb[2 + (cb % 2) * 2]
                ctx.op("pe", lambda: nc.tensor.matmul(pcum[:], lhsT=tri[:], rhs=Dw[:, cs], start=True, stop=True),
                       reads=[tri, Dw], writes=[pcum])
                ctx.op("pe", lambda: nc.tensor.matmul(psuf[:], lhsT=suft[:], rhs=Dw[:, cs], start=True, stop=True),
                       reads=[suft, Dw], writes=[psuf])
                ctx.op("act", lambda: nc.scalar.activation(ET[0][:], pcum[:], AF.Exp), reads=[pcum], writes=[ET[0]])
                ctx.op("pool", lambda: nc.gpsimd.tensor_tensor(tok[3][:, cs], A[:, cs], ET[0][:], ALU.mult),
                       reads=[A, ET[0]], writes=[tok[3]])
                ctx.op("act", lambda: nc.scalar.activation(ET[1][:], pcum[:], AF.Exp, scale=-1.0), reads=[pcum], writes=[ET[1]])
                ctx.op("dve", lambda: nc.vector.tensor_tensor(tok[0][:, cs], T1[:, cs], ET[1][:], ALU.mult),
                       reads=[T1, ET[1]], writes=[tok[0]])
                ctx.op("pool", lambda: nc.gpsimd.tensor_tensor(tok[1][:, cs], B[:, cs], ET[1][:], ALU.mult),
                       reads=[B, ET[1]], writes=[tok[1]])
                ctx.op("dve", lambda: nc.vector.tensor_tensor(ET[2][:], pcum[:], Dw[:, cs], ALU.subtract),
                       reads=[pcum, Dw], writes=[ET[2]])
                ctx.op("act", lambda: nc.scalar.activation(ET[2][:], ET[2][:], AF.Exp), reads=[ET[2]], writes=[ET[2]])
                ctx.op("dve", lambda: nc.vector.scalar_tensor_tensor(tok[2][:, cs], Fk[:, cs], -1.0, ET[2][:], ALU.mult, ALU.mult),
                       reads=[Fk, ET[2]], writes=[tok[2]])
                ctx.op("act", lambda: nc.scalar.activation(ET[3][:], psuf[:], AF.Exp), reads=[psuf], writes=[ET[3]])
                ctx.op("dve", lambda: nc.vector.tensor_tensor(bbk[:, 0, cs], T1[:, cs], ET[3][:], ALU.mult),
                       reads=[T1, ET[3]], writes=[bbk])
                ctx.op("pool", lambda: nc.gpsimd.tensor_tensor(bbk[:, 1, cs], B[:, cs], ET[3][:], ALU.mult),
                       reads=[B, ET[3]], writes=[bbk])
            for kind in range(4):
                for half in range(2):
                    for jj in range(8):
                        hb = half * 8 + jj
                        self.transpose_to(tok[kind][:, hb * P:(hb + 1) * P], ptr[:, jj, :], [tok[kind]], [ptr])
                    if (kind + half) % 2 == 0:
                        ctx.op("act", lambda: nc.scalar.copy(CM[:, half * 8:(half + 1) * 8, kind, :], ptr[:]), reads=[ptr], writes=[CM])
                    else:
                        ctx.op("dve", lambda: nc.vector.tensor_copy(CM[:, half * 8:(half + 1) * 8, kind, :], ptr[:]), reads=[ptr], writes=[CM])
            for hb in range(RW_NHB if cfg.stop != "rw2p" else 0):
                v2 = lambda ap_, k: ap_.rearrange("p (h k) -> p h k", k=k)
                for hi, po in enumerate((0, 64)):
                    rhs_ar = CM[po:po + 64, hb, 2:4, :].rearrange("p k t -> p (k t)")
                    qb = pb[hi]
                    ctx.op("pe", lambda: nc.tensor.matmul(qb[:, 0:256], lhsT=CM[po:po + 64, hb, 0, :], rhs=rhs_ar,
                                                          start=True, stop=True), reads=[CM], writes=[qb])
                    ctx.op("pe", lambda: nc.tensor.matmul(qb[:, 256:512], lhsT=CM[po:po + 64, hb, 1, :], rhs=rhs_ar,
                                                          start=True, stop=True), reads=[CM], writes=[qb])
                    ctx.op("pe", lambda: nc.tensor.matmul(pb[2 + hi][:, 0:P], lhsT=CM[po:po + 64, hb, 2, :],
                                                          rhs=CM[po:po + 64, hb, 0, :], start=True, stop=True),
                           reads=[CM], writes=[pb[2 + hi]])
                g0 = GA[0]
                for hi in range(2):
                    qb = pb[hi]
                    ctx.op("dve", lambda: nc.vector.tensor_tensor(g0[:, hi, 0, :], qb[:, 0:P], mus[:], ALU.mult),
                           reads=[qb, mus], writes=[g0])
                    ctx.op("dve", lambda: nc.vector.tensor_tensor(Brb[:, hi, :], qb[:, P:2 * P], mui[:], ALU.mult),
                           reads=[qb, mui], writes=[Brb])
                    ctx.op("dve", lambda: nc.vector.tensor_tensor(Aak[:, hi, :], qb[:, 2 * P:3 * P], mus[:], ALU.mult),
                           reads=[qb, mus], writes=[Aak])
                    ctx.op("dve", lambda: nc.vector.tensor_tensor(Brk[:, hi, :], qb[:, 3 * P:4 * P], mui[:], ALU.mult),
                           reads=[qb, mui], writes=[Brk])
                    ctx.op("dve", lambda: nc.vector.tensor_tensor(g0[:, hi, 1, :], pb[2 + hi][:, 0:P], mls[:], ALU.mult),
                           reads=[pb[2 + hi], mls], writes=[g0])
                    ctx.op("pool", lambda: nc.gpsimd.tensor_tensor(Tt[0][:, hi, :], g0[:, hi, 0, :], self.ident[:], ALU.add),
                           reads=[g0, self.ident], writes=[Tt[0]])
                tcur = 0
                if cfg.stop == "rw2q":
                    continue
                for lvl in range(1, 7):
                    gc, gn = GA[(lvl - 1) % 2], GA[lvl % 2]
                    for hi in range(2):
                        if lvl < 6:
                            ctx.op("pe", lambda: nc.tensor.matmul(pb[3][:, (2 * hi) * P:(2 * hi + 1) * P], lhsT=gc[:, hi, 1, :],
                                                                  rhs=gc[:, hi, 0, :], start=True, stop=True),
                                   reads=[gc], writes=[pb[3]])
                        ctx.op("pe", lambda: nc.tensor.matmul(pb[3][:, (2 * hi + 1) * P:(2 * hi + 2) * P], lhsT=gc[:, hi, 0, :],
                                                              rhs=gc[:, hi, 1, :], start=True, stop=True),
                               reads=[gc], writes=[pb[3]])
                    if lvl < 6:
                        ctx.op("act", lambda: nc.scalar.copy(gn[:].rearrange("p h k t -> p (h k t)"), pb[3][:]),
                               reads=[pb[3]], writes=[gn])
                    else:
                        ctx.op("act", lambda: nc.scalar.copy(gn[:, :, 1, :], v2(pb[3][:], 2 * P)[:, :, P:2 * P]),
                               reads=[pb[3]], writes=[gn])
                    for hi in range(2):
                        ctx.op("pe", lambda: nc.tensor.matmul(pb[4][:, hi * P:(hi + 1) * P], lhsT=gn[:, hi, 1, :],
                                                              rhs=Tt[tcur][:, hi, :], start=True, stop=True),
                               reads=[gn, Tt[tcur]], writes=[pb[4]])
                    ctx.op("dve", lambda: nc.vector.tensor_tensor(Tt[1 - tcur][:], v2(pb[4][:, 0:2 * P], P), Tt[tcur][:], ALU.add),
                           reads=[pb[4], Tt[tcur]], writes=[Tt[1 - tcur]])
                    tcur = 1 - tcur
                TT = Tt[tcur]
                if cfg.stop == "rw2i":
                    continue
                for hi, po in enumerate((0, 64)):
                    h = 2 * hb + hi
                    wps = pb[5 + hi]
                    ctx.op("pe", lambda: nc.tensor.matmul(wps[:, 0:P], lhsT=CM[po:po + 64, hb, 2, :],
                                                          rhs=SXb[po:po + 64, hb, :], start=True, stop=False),
                           reads=[CM, SXb], writes=[wps])
                    ctx.op("pe", lambda: nc.tensor.matmul(wps[:, 0:P], lhsT=Aak[:, hi, :], rhs=vbx[:, h, :],
                                                          start=False, stop=True), reads=[Aak, vbx], writes=[wps])
                    ctx.op("act", lambda: nc.scalar.copy(Wb[:, hi, :], wps[:, 0:P]), reads=[wps], writes=[Wb])
                for hi in range(2):
                    ctx.op("pe", lambda: nc.tensor.matmul(pb[5][:, 2 * P + hi * P:2 * P + (hi + 1) * P], lhsT=TT[:, hi, :], rhs=Wb[:, hi, :],
                                                          start=True, stop=True), reads=[TT, Wb], writes=[pb[5]])
                ctx.op("dve", lambda: nc.vector.tensor_copy(Ub[:].rearrange("p h t -> p (h t)"), pb[5][:, 2 * P:4 * P]),
                       reads=[pb[5]], writes=[Ub])
                for hi, po in enumerate((0, 64)):
                    h = 2 * hb + hi
                    yb = pb[hi]
                    yo = yb[:, 0:64]
                    ctx.op("pe", lambda: nc.tensor.matmul(yo, lhsT=CM[po:po + 64, hb, 3, :], rhs=SXb[po:po + 64, hb, 0:64],
                                                          start=True, stop=False), reads=[CM, SXb], writes=[yb])
                    ctx.op("pe", lambda: nc.tensor.matmul(yo, lhsT=Brb[:, hi, :], rhs=Ub[:, hi, 0:64], start=False, stop=False),
                           reads=[Brb, Ub], writes=[yb])
                    ctx.op("pe", lambda: nc.tensor.matmul(yo, lhsT=Brk[:, hi, :], rhs=vbx[:, h, 0:64], start=False, stop=True),
                           reads=[Brk, vbx], writes=[yb])
                    if NSEG > 1:
                        to = yb[po:po + 64, P:2 * P]
                        ctx.op("pe", lambda: nc.tensor.matmul(to, lhsT=SXb[po:po + 64, hb, 64:128], rhs=CM[po:po + 64, hb, 3, :],
                                                              start=True, stop=False), reads=[CM, SXb], writes=[yb])
                        ctx.op("pe", lambda: nc.tensor.matmul(to, lhsT=Ub[:, hi, 64:128], rhs=Brb[:, hi, :], start=False, stop=True),
                               reads=[Brb, Ub], writes=[yb])
                    so = pb[6][po:po + 64, 2 * P:3 * P]
                    ctx.op("pe", lambda: nc.tensor.matmul(so, lhsT=bbk[:, 0, h * 64:(h + 1) * 64], rhs=Ub[:, hi, :], start=True, stop=False),
                           reads=[bbk, Ub], writes=[pb[6]])
                    ctx.op("pe", lambda: nc.tensor.matmul(so, lhsT=bbk[:, 1, h * 64:(h + 1) * 64], rhs=vbx[:, h, :], start=False, stop=True),
                           reads=[bbk, vbx], writes=[pb[6]])
                    ctx.op("act", lambda: nc.scalar.copy(ytile[:, hb * P + hi * 64:hb * P + (hi + 1) * 64], yb[:, 0:64]),
                           reads=[yb], writes=[ytile])
                    if NSEG > 1:
                        ctx.op("act", lambda: nc.scalar.copy(ytr[po:po + 64, hb, :], yb[po:po + 64, P:2 * P]), reads=[yb], writes=[ytr])
                ctx.op("dve", lambda: nc.vector.scalar_tensor_tensor(SX[:, hb, :], SX[:, hb, :], dectot[:, hb:hb + 1],
                                                                     pb[6][:, 2 * P:3 * P], ALU.mult, ALU.add),
                       reads=[SX, dectot, pb[6]], writes=[SX])
                ctx.op("pool", lambda: nc.gpsimd.tensor_copy(SXb[:, hb, :], SX[:, hb, :]), reads=[SX], writes=[SXb])
            ctx.dma("sp", Y0.ap()[rows, :], ytile[:], reads=[ytile], writes=[("Y0", n)])
            if NSEG > 1:
                ctx.dma("sp", YTR.ap()[n], ytr[:], reads=[ytr], writes=[("YTR", n)])
        if NSEG > 1:
            ctx.dma("sp", SXs.ap(), SX[:], reads=[SX], writes=[("SXs",)])
        ctx.barrier()

    if cfg.stop in ("rw2", "rw2p", "rw2q", "rw2i"):
        return
    S0b = ctx.sb(self.top, "rw_S0b_%d" % layer, [P, RW_NHB, 64], BF16)
    if NSEG > 1:
        NSL = NSEG - 1
        CI = self.scr("rw_ci", [NSL, P, RW_NHB * P])
        CO = self.scr("rw_co", [NSL, P, RW_NHB * P])
        with contextlib.ExitStack() as st:
            sx = ctx.sb(st, "sx", [P, RW_NHB * P], F32)
            sm = [ctx.sb(st, "sm", [P, RW_NHB * P], F32) for _ in range(2)]
            ctx.dma("sp", sx[:], SXs.ap().rearrange("p h k -> p (h k)"), writes=[sx])
            for s in range(NSL):
                ctx.op("dve", lambda: nc.vector.tensor_scalar_mul(sm[s % 2][:], sx[:], self.own[:, s:s + 1]),
                       reads=[sx, self.own], writes=[sm[s % 2]])
                ctx.dma("sp", CI.ap()[s], sm[s % 2][:], reads=[sm[s % 2]], writes=[("CI", s)])
            ctx.barrier()
            ctx.allreduce(cfg.groups, CI.ap().rearrange("s p e -> (s p) e"), CO.ap().rearrange("s p e -> (s p) e"),
                          writes=[("CO",)])
            ctx.barrier()
            selm = ctx.sb(st, "selm", [P, NSEG], F32)
            nselm = ctx.sb(st, "nselm", [P, NSEG], F32)
            ctx.dma("sp", selm[:], self.inp("rw_selm", [P, NSEG]).ap(), writes=[selm])
            ctx.dma("sp", nselm[:], self.inp("rw_nselm", [P, NSEG]).ap(), writes=[nselm])
            i2 = ctx.sb(st, "i2", [P, 64], F32)
            ctx.dma("sp", i2[:], self.inp("rw_i2", [P, 64]).ap(), writes=[i2])
            S0 = ctx.sb(st, "S0", [P, RW_NHB, 64], F32)
            ctx.op("dve", lambda: nc.vector.memset(S0[:], 0.0), writes=[S0])
            Mp = ctx.sb(st, "Mp", [P, RW_NHB, 64], F32)
            Lp = ctx.sb(st, "Lp", [P, RW_NHB, 64], F32)
            MT = ctx.sb(st, "MT", [P, RW_NHB, 64], F32)
            pm = [ctx.ps(st, "pm", [P, 8, 64], F32) for _ in range(2)]
            for s in range(NSL):
                slot = sm[s % 2]
                ctx.dma("sp", slot[:], CO.ap()[s], writes=[slot])
                sv = slot[:].rearrange("p (h k) -> p h k", k=P)
                ctx.op("dve", lambda: nc.vector.tensor_scalar_mul(Lp[:], sv[:, :, 0:64], selm[:, s:s + 1]),
                       reads=[slot, selm], writes=[Lp])
                ctx.op("dve", lambda: nc.vector.tensor_scalar_mul(Mp[:], sv[:, :, 64:128], selm[:, s:s + 1]),
                       reads=[slot, selm], writes=[Mp])
                ctx.op("dve", lambda: nc.vector.scalar_tensor_tensor(Mp[:], i2[:].unsqueeze(1).broadcast_to([P, RW_NHB, 64]),
                                                                     nselm[:, s:s + 1], Mp[:], ALU.mult, ALU.add),
                       reads=[i2, nselm, Mp], writes=[Mp])
                for half in range(2):
                    for jj in range(8):
                        hb = half * 8 + jj
                        for hi, po in enumerate((0, 64)):
                            ctx.op("pe", lambda: nc.tensor.matmul(pm[hi][po:po + 64, jj, :], lhsT=Mp[po:po + 64, hb, :],
                                                                  rhs=self.identf[po:po + 64, po:po + 64], start=True, stop=True),
                                   reads=[Mp, self.identf], writes=[pm[hi]])
                    for hi, po in enumerate((0, 64)):
                        ctx.op("act", lambda: nc.scalar.copy(MT[po:po + 64, half * 8:(half + 1) * 8, :], pm[hi][po:po + 64]),
                               reads=[pm[hi]], writes=[MT])
                for half in range(2):
                    for jj in range(8):
                        hb = half * 8 + jj
                        for hi, po in enumerate((0, 64)):
                            ctx.op("pe", lambda: nc.tensor.matmul(pm[hi][po:po + 64, jj, :], lhsT=MT[po:po + 64, hb, :],
                                                                  rhs=S0[po:po + 64, hb, :], start=True, stop=True),
                                   reads=[MT, S0], writes=[pm[hi]])
                    for hi, po in enumerate((0, 64)):
                        ctx.op("dve", lambda: nc.vector.tensor_tensor(S0[po:po + 64, half * 8:(half + 1) * 8, :], pm[hi][po:po + 64],
                                                                      Lp[po:po + 64, half * 8:(half + 1) * 8, :], ALU.add),
                               reads=[pm[hi], Lp, S0], writes=[S0])
            ctx.op("act", lambda: nc.scalar.copy(S0b[:], S0[:]), reads=[S0], writes=[S0b])
            ctx.barrier()

    if cfg.stop == "rw3":
        return
    with contextlib.ExitStack() as st:
        gnb = self.bcast_rows(st, "gnb", gi("rwkv_gn_gain", [D]), D)
        gbb = self.bcast_rows(st, "gbb", gi("rwkv_gn_bias", [D]), D)
        y = ctx.sb(st, "y", [P, D], F32)
        vv = ctx.sb(st, "vv", [P, D], F32)
        gg = ctx.sb(st, "gg", [P, D], F32)
        sq = ctx.sb(st, "sq", [P, D], F32)
        bon = ctx.sb(st, "bon", [P, RW_H], F32)
        s1 = ctx.sb(st, "s1", [P, RW_H], F32)
        s2 = ctx.sb(st, "s2", [P, RW_H], F32)
        ytr = ctx.sb(st, "ytr", [P, RW_NHB, P], BF16)
        ogb = [ctx.sb(st, "ogb", [P, D], BF16) for _ in range(2)]
        G = 4 if NT % 4 == 0 else 2
        stg = [ctx.sb(st, "stg", [P, KC, G * P], BF16) for _ in range(2)]
        pst = [[ctx.ps(st, "pst", [P, 8 * P], BF16) for _ in range(2)] for _ in range(2)]
        pc = [ctx.ps(st, "pc", [P, 512], F32) for _ in range(4)]
        OGTv = OGT.ap().rearrange("(kc p) t -> p kc t", p=P)
        v3 = lambda t_: t_[:].rearrange("p (h d) -> p h d", d=RW_HD)
        bc3 = lambda small: small[:].unsqueeze(2).broadcast_to([P, RW_H, RW_HD])
        for n in range(NT):
            rows = slice(n * P, (n + 1) * P)
            ctx.dma("sp", y[:], Y0.ap()[rows, :], writes=[y])
            ctx.dma("sp", vv[:], Vs.ap()[rows, :], writes=[vv])
            ctx.dma("sp", gg[:], Gs.ap()[rows, :], writes=[gg])
            ctx.dma("sp", bon[:], BON.ap()[rows, :], writes=[bon])
            if NSEG > 1:
                ctx.dma("sp", ytr[:], YTR.ap()[n], writes=[ytr])
                for q4 in range(4):
                    for jj in range(4):
                        hb = q4 * 4 + jj
                        for hi, po in enumerate((0, 64)):
                            pcc = pc[2 * (q4 % 2) + hi]
                            ctx.op("pe", lambda: nc.tensor.matmul(pcc[:, jj * 64:(jj + 1) * 64],
                                                                  lhsT=ytr[po:po + 64, hb, :], rhs=S0b[po:po + 64, hb, :],
                                                                  start=True, stop=True), reads=[ytr, S0b], writes=[pcc])
                    for hi in range(2):
                        pcc = pc[2 * (q4 % 2) + hi]
                        yv = y[:, q4 * 512:(q4 + 1) * 512].rearrange("p (j k) -> p j k", k=P)[:, :, hi * 64:(hi + 1) * 64]
                        ctx.op("dve", lambda: nc.vector.tensor_tensor(yv, yv, pcc[:, 0:256].rearrange("p (j k) -> p j k", k=64), ALU.add),
                               reads=[pcc, y], writes=[y])
            ctx.op("dve", lambda: nc.vector.tensor_reduce(s1[:], v3(y), AX.X, ALU.add), reads=[y], writes=[s1])
            ctx.op("dve", lambda: nc.vector.tensor_scalar_mul(s1[:], s1[:], 1.0 / RW_HD), reads=[s1], writes=[s1])
            ctx.op("dve", lambda: nc.vector.tensor_tensor(v3(y), v3(y), bc3(s1), ALU.subtract), reads=[y, s1], writes=[y])
            ctx.op("pool", lambda: nc.gpsimd.tensor_tensor(sq[:], y[:], y[:], ALU.mult), reads=[y], writes=[sq])
            ctx.op("dve", lambda: nc.vector.tensor_reduce(s2[:], v3(sq), AX.X, ALU.add), reads=[sq], writes=[s2])
            ctx.op("dve", lambda: nc.vector.tensor_scalar(s2[:], s2[:], 1.0 / RW_HD, RW_EPS, ALU.mult, ALU.add), reads=[s2], writes=[s2])
            ctx.op("act", lambda: nc.scalar.activation(s2[:], s2[:], AF.Sqrt), reads=[s2], writes=[s2])
            ctx.op("dve", lambda: nc.vector.reciprocal(s2[:], s2[:]), reads=[s2], writes=[s2])
            ctx.op("dve", lambda: nc.vector.tensor_tensor(v3(y), v3(y), bc3(s2), ALU.mult), reads=[y, s2], writes=[y])
            ctx.op("pool", lambda: nc.gpsimd.tensor_tensor(y[:], y[:], gnb[:], ALU.mult), reads=[y, gnb], writes=[y])
            ctx.op("pool", lambda: nc.gpsimd.tensor_tensor(y[:], y[:], gbb[:], ALU.add), reads=[y, gbb], writes=[y])
            ctx.op("dve", lambda: nc.vector.tensor_tensor(v3(vv), v3(vv), bc3(bon), ALU.mult), reads=[vv, bon], writes=[vv])
            ctx.op("pool", lambda: nc.gpsimd.tensor_tensor(y[:], y[:], vv[:], ALU.add), reads=[y, vv], writes=[y])
            ob = ogb[n % 2]
            ctx.op("dve", lambda: nc.vector.tensor_tensor(ob[:], y[:], gg[:], ALU.mult), reads=[y, gg], writes=[ob])
            g_, gi_ = divmod(n, G)
            self.xt_emit_tile(ob, ob, stg[g_ % 2], stg[g_ % 2], gi_ * P, pst[n % 2])
            if gi_ == G - 1:
                ctx.dma("sp", OGTv[:, :, g_ * G * P:(g_ + 1) * G * P], stg[g_ % 2][:], reads=[stg[g_ % 2]], writes=[("OGT", g_)])
        ctx.barrier()

    if cfg.stop == "rw4":
        return
    with contextlib.ExitStack() as st:
        aT = self.load_AT(st, "oTa", OGT, KC, 0, T)
        res = GemmRes(self, st, KC, 512, 3)
        epi = self.epi_resid(st, X, Z1)
        self.gemm_tok(res, aT, aT, KC, w_out, 0, D, range(NT), epi)
        ctx.barrier()


Prog.rwkv_layer = _rwkv_layer
def _const_inputs(cfg, core):
    T, NSEG = cfg.T, cfg.NSEG
    seg = core % NSEG
    f32 = np.float32
    own = np.zeros((P, NSEG), f32)
    own[:, seg] = 1
    hs = np.zeros((P, NSEG), f32)
    if seg > 0:
        hs[:, seg - 1] = 1
    c = {"ident": np.eye(P, dtype=f32).astype(ml_dtypes.bfloat16), "identf": np.eye(P, dtype=f32),
         "own": own, "halo_sel": hs}
    inv = (1.0 / (10000.0 ** (np.arange(0, RET_DK, 2, dtype=f32) / f32(RET_DK)))).astype(f32)
    pos = (seg * T + np.arange(T)).astype(f32)
    ang = (pos[None, :] * inv[:, None]).astype(f32)
    c["rope_cos"] = np.cos(ang).astype(f32)
    c["rope_sin"] = np.sin(ang).astype(f32)
    gam = np.array(RET_GAMMA, np.float64)
    idx = np.arange(P, dtype=np.float64)
    c["ret_kdec"] = (gam[None, :] ** (P - 1 - idx[:, None])).astype(f32)
    diff = idx[None, :] - idx[:, None]
    m = np.where(diff[:, None, :] >= 0, gam[None, :, None] ** np.maximum(diff[:, None, :], 0), 0.0)
    c["ret_maskT"] = m.astype(f32)
    c["ret_qdec"] = np.broadcast_to((gam[:, None] ** (idx[None, :] + 1.0))[None], (P, RET_H, P)).astype(f32).copy()
    coef = np.zeros((P, NSEG, RET_H), f32)
    for s in range(seg):
        coef[:, s, :] = (gam ** (T * (seg - s - 1)))[None, :]
    c["ret_coef"] = coef
    _swa_consts(cfg, core, c)
    _rwkv_consts(cfg, core, c)
    return c


def make_in_maps(cfg, prog, inputs):
    T, NSEG = cfg.T, cfg.NSEG
    maps = []
    shared = {}
    per_layer = ["ret_w_in", "ret_w_out", "ret_gn_gain", "swa_w_qkv", "swa_sinks", "swa_w_out",
                 "rwkv_mix", "rwkv_w_rkv", "rwkv_w0", "rwkv_w1", "rwkv_w2", "rwkv_a0", "rwkv_a1", "rwkv_a2",
                 "rwkv_g1", "rwkv_g2", "rwkv_k_k", "rwkv_k_a", "rwkv_r_k", "rwkv_gn_gain", "rwkv_gn_bias",
                 "rwkv_w_out", "ffn_w_up", "ffn_conv_w", "ffn_conv_b", "ffn_w_down", "ple_w_proj", "ple_w_gate"]
    for name in prog.inputs:
        if name in prog.tiled:
            fn, K_, N_, tw_, kp_ = prog.tiled[name]
            w = np.asarray(fn(inputs))
            assert w.shape == (K_, N_), (name, w.shape)
            shared[name] = np.ascontiguousarray(
                w.reshape(K_ // kp_, kp_, N_ // tw_, tw_).transpose(2, 1, 0, 3).reshape(N_ // tw_, kp_, (K_ // kp_) * tw_))
            continue
        if name in inputs and name not in ("x",):
            shared[name] = np.ascontiguousarray(inputs[name])
            continue
        for base in per_layer:
            if name.startswith(base + "_") and name[len(base) + 1:].isdigit():
                shared[name] = np.ascontiguousarray(inputs[base][int(name[len(base) + 1:])])
    for core in range(cfg.ncores):
        b, seg = divmod(core, NSEG)
        consts = _const_inputs(cfg, core)
        m = {}
        for name in prog.inputs:
            if name in shared:
                m[name] = shared[name]
            elif name == "x":
                m[name] = np.ascontiguousarray(inputs["x"][b, seg * T:(seg + 1) * T, :])
            elif name.startswith("pT_"):
                l = int(name[3:])
                m[name] = np.ascontiguousarray(inputs["p"][l, b, seg * T:(seg + 1) * T, :].T)
            elif name in consts:
                m[name] = consts[name]
            else:
                raise KeyError(name)
        maps.append(m)
    return maps


def run_cfg(cfg, inputs):
    prog = Prog(cfg)
    prog.build()
    maps = make_in_maps(cfg, prog, inputs)
    res = run_bass_kernel_spmd(prog.nc, maps, core_ids=list(range(cfg.ncores)))
    return prog, res.results


def kernel(**inputs):
    cfg = Cfg()
    prog, results = run_cfg(cfg, inputs)
    out = np.empty((cfg.NB, cfg.NSEG * cfg.T, D), np.float32)
    for core in range(cfg.ncores):
        b, seg = divmod(core, cfg.NSEG)
        out[b, seg * cfg.T:(seg + 1) * cfg.T, :] = results[core]["out"]
    return out
```

```python
import contextlib
import math
import numpy as np
import ml_dtypes
import concourse.bass as bass
import concourse.mybir as mybir
from concourse.bass_utils import run_bass_kernel_spmd

F32 = mybir.dt.float32
BF16 = mybir.dt.bfloat16
AF = mybir.ActivationFunctionType
ALU = mybir.AluOpType
AX = mybir.AxisListType

P = 128
D = 2048
KC = D // P
DEPTH = 4
DFF = 5504
NFB = DFF // P
PLE = 256
ALPHA = (2.0 * DEPTH) ** 0.25
LN_EPS = 1e-5
RET_H, RET_DK, RET_DV = 8, 256, 512
RET_EPS = 1e-5
RET_GAMMA = [1.0 - 2.0 ** (-5.0 - h) for h in range(RET_H)]


class Ctx:
    NDMA = {"sp": 8, "act": 4, "pool": 8}

    def __init__(self, nc, stack):
        self.nc = nc
        self.stack = stack
        self.eng = {"pe": nc.tensor, "act": nc.scalar, "dve": nc.vector,
                    "pool": nc.gpsimd, "sp": nc.sync}
        self.sems = {}
        self.val = {}
        for e in ("pe", "act", "dve", "pool"):
            self.sems[e] = stack.enter_context(nc.semaphore("c_" + e))
            self.val[e] = 0
        self.dq = {}
        self.dq_next = {}
        for q, n in self.NDMA.items():
            keys = []
            for i in range(n):
                k = "d_%s%d" % (q, i)
                self.sems[k] = stack.enter_context(nc.semaphore(k))
                self.val[k] = 0
                keys.append(k)
            self.dq[q] = keys
            self.dq_next[q] = 0
        self.sems["cc"] = stack.enter_context(nc.semaphore("cc"))
        self.val["cc"] = 0
        self.known = {e: {} for e in self.eng}
        self.lastw = {}
        self.readers = {}
        self.uid = 0
        self.n_ins = 0

    def sb(self, stack, name, shape, dtype=F32):
        self.uid += 1
        return stack.enter_context(self.nc.sbuf_tensor("%s_%d" % (name, self.uid), list(shape), dtype))

    def ps(self, stack, name, shape, dtype=F32):
        self.uid += 1
        return stack.enter_context(self.nc.psum_tensor("%s_%d" % (name, self.uid), list(shape), dtype))

    def _key(self, b):
        return b if isinstance(b, (str, tuple)) else id(b)

    def _deps(self, reads, writes, merge=False):
        deps = {}
        for b in list(reads) + ([] if merge else list(writes)):
            for k, v in self.lastw.get(self._key(b), {}).items():
                if deps.get(k, 0) < v:
                    deps[k] = v
        for b in writes:
            for k, v in self.readers.get(self._key(b), {}).items():
                if deps.get(k, 0) < v:
                    deps[k] = v
        return deps

    def _wait(self, e, deps):
        kn = self.known[e]
        for k, v in deps.items():
            if e == "pe" and k == "pe":
                continue
            if kn.get(k, 0) >= v:
                continue
            self.eng[e].wait_ge(self.sems[k], v)
            kn[k] = v

    def _commit(self, ev, reads, writes, merge=False):
        k, v = ev
        for b in reads:
            self.readers.setdefault(self._key(b), {})[k] = v
        for b in writes:
            if merge:
                self.lastw.setdefault(self._key(b), {})[k] = v
            else:
                self.lastw[self._key(b)] = {k: v}
                self.readers[self._key(b)] = {}

    def op(self, e, fn, reads=(), writes=()):
        self._wait(e, self._deps(reads, writes))
        ins = fn()
        self.val[e] += 1
        ins.then_inc(self.sems[e], 1)
        self._commit((e, self.val[e]), reads, writes)
        self.n_ins += 1
        return ins

    def dma(self, q, out, in_, reads=(), writes=(), merge=False, **kw):
        deps = self._deps(reads, writes, merge)
        i = self.dq_next[q]
        self.dq_next[q] = (i + 1) % len(self.dq[q])
        k = self.dq[q][i]
        if self.val[k] > 0:
            deps[k] = max(deps.get(k, 0), self.val[k])
        self._wait(q, deps)
        ins = self.eng[q].dma_start(out=out, in_=in_, **kw)
        self.val[k] += 16
        ins.then_inc(self.sems[k], 16)
        self._commit((k, self.val[k]), reads, writes, merge)
        self.n_ins += 1
        return ins

    def allreduce(self, groups, in_ap, out_ap, reads=(), writes=()):
        deps = self._deps(reads, writes)
        self._wait("pool", deps)
        ins = self.nc.gpsimd.collective_compute("AllReduce", ALU.add, replica_groups=groups,
                                                ins=[in_ap.opt()], outs=[out_ap.opt()])
        self.val["cc"] += 1
        ins.then_inc(self.sems["cc"], 1)
        self._commit(("cc", self.val["cc"]), reads, writes)

    def barrier(self, engines=("pe", "act", "dve", "pool", "sp")):
        deps = {k: v for k, v in self.val.items() if v > 0}
        for e in engines:
            self._wait(e, dict(deps))
        if len(engines) == 5:
            self.lastw = {}
            self.readers = {}

    def finish(self):
        self._wait("sp", {k: v for k, v in self.val.items() if v > 0})


class Cfg:
    def __init__(self, NB=2, NSEG=4, T=2048, layers=(0, 1, 2, 3), debug=(), stop=None):
        self.stop = stop
        self.NB, self.NSEG, self.T = NB, NSEG, T
        self.layers = tuple(layers)
        self.NT = T // P
        self.ncores = NB * NSEG
        self.groups = [[b * NSEG + s for s in range(NSEG)] for b in range(NB)]
        self.debug = tuple(debug)


class Prog:
    def __init__(self, cfg):
        self.cfg = cfg
        self.nc = bass.Bass("TRN2", target_bir_lowering=False)
        self.inputs = {}
        self.scratch = {}
        self.tiled = {}

    def inp(self, name, shape, dtype=F32):
        if name not in self.inputs:
            self.inputs[name] = self.nc.dram_tensor(name, list(shape), dtype, kind="ExternalInput")
        return self.inputs[name]

    def scr(self, name, shape, dtype=F32):
        if name not in self.scratch:
            kind = "ExternalOutput" if name in self.cfg.debug else "Internal"
            self.scratch[name] = self.nc.dram_tensor(name, list(shape), dtype, kind=kind)
        return self.scratch[name]

    def build(self):
        cfg = self.cfg
        nc = self.nc
        T = cfg.T
        with contextlib.ExitStack() as top:
            ctx = self.ctx = Ctx(nc, top)
            self.top = top
            self.ident = ctx.sb(top, "ident", [P, P], BF16)
            ctx.dma("sp", self.ident[:], self.inp("ident", [P, P], BF16).ap(), writes=[self.ident])
            self.identf = ctx.sb(top, "identf", [P, P], F32)
            ctx.dma("sp", self.identf[:], self.inp("identf", [P, P], F32).ap(), writes=[self.identf])
            self.own = ctx.sb(top, "own", [P, cfg.NSEG], F32)
            ctx.dma("sp", self.own[:], self.inp("own", [P, cfg.NSEG]).ap(), writes=[self.own])
            self.halo_sel = ctx.sb(top, "halo_sel", [P, cfg.NSEG], F32)
            ctx.dma("sp", self.halo_sel[:], self.inp("halo_sel", [P, cfg.NSEG]).ap(), writes=[self.halo_sel])

            x_in = self.inp("x", [T, D])
            out = self.nc.dram_tensor("out", [T, D], F32, kind="ExternalOutput")
            XT = self.scr("XT", [D, T], BF16)
            cur = x_in
            self.xt_stage(cur, XT)
            for li, layer in enumerate(cfg.layers):
                kind = layer % 3
                Z1 = self.scr("Z1", [T, D])
                if kind == 0:
                    self.retention_layer(layer, cur, XT, Z1)
                elif kind == 1:
                    self.swa_layer(layer, cur, XT, Z1)
                else:
                    self.rwkv_layer(layer, cur, XT, Z1)
                if cfg.stop is not None:
                    break
                X1 = self.scr("X1", [T, D])
                XT1 = self.scr("XT1", [D, T], BF16)
                self.ln_stage(Z1, layer, 0, X1, XT1)
                Z2 = self.scr("Z2", [T, D])
                self.ffn_layer(layer, X1, XT1, Z2)
                X2 = self.scr("X2", [T, D])
                self.ln_stage(Z2, layer, 1, X2, XT)
                last = li == len(cfg.layers) - 1
                X3 = out if last else self.scr("X3_%d" % (li % 2), [T, D])
                self.ple_layer(layer, X2, XT, X3)
                if not last:
                    self.xt_stage(X3, XT)
                cur = X3
            ctx.barrier()
            ctx.finish()
        return nc

    def load_w(self, dst, src_ap, key):
        self.ctx.dma("pool", dst, src_ap, writes=[key])

    def bcast_rows(self, stack, name, src_ap_1d, n):
        t = self.ctx.sb(stack, name, [P, n], F32)
        self.ctx.dma("sp", t[:], src_ap_1d.partition_broadcast(P), writes=[t])
        return t

    def load_cols(self, stack, name, src_ap_2d, R, ps):
        ctx, nc = self.ctx, self.nc
        out = ctx.sb(stack, name, [P, R], F32)
        with contextlib.ExitStack() as st:
            r0 = 0
            while r0 < R:
                r = min(P, R - r0)
                tmp = ctx.sb(st, name + "_r", [P, P], F32)
                ctx.dma("sp", tmp[0:r, :], src_ap_2d[r0:r0 + r, :], writes=[tmp])
                ctx.op("pe", lambda: nc.tensor.matmul(ps[:, 0:r], lhsT=tmp[0:r, :], rhs=self.identf[0:r, 0:r],
                                                      start=True, stop=True), reads=[tmp, self.identf], writes=[ps])
                ctx.op("dve", lambda: nc.vector.tensor_copy(out[:, r0:r0 + r], ps[:, 0:r]), reads=[ps], writes=[out])
                r0 += r
            ctx.barrier(("pe", "dve", "sp"))
        return out

    def transpose_to(self, src_bf16_ap, pst_ap, reads, writes):
        nc = self.nc
        self.ctx.op("pe", lambda: nc.tensor.transpose(pst_ap, src_bf16_ap, self.ident[:]),
                    reads=list(reads) + [self.ident], writes=writes)

    def xt_emit_tile(self, xb, xb_key, stg, stg_key, col0, pst):
        ctx, nc = self.ctx, self.nc
        for half in range(2):
            pt = pst[half]
            for j in range(8):
                kc = half * 8 + j
                self.transpose_to(xb[:, kc * P:(kc + 1) * P], pt[:, j * P:(j + 1) * P], [xb_key], [pt])
            eng = "dve" if half == 0 else "act"
            src = pt[:].rearrange("p (j c) -> p j c", j=8)
            dst = stg[:, half * 8:(half + 1) * 8, col0:col0 + P]
            if eng == "dve":
                ctx.op("dve", lambda: nc.vector.tensor_copy(dst, src), reads=[pt], writes=[stg_key])
            else:
                ctx.op("act", lambda: nc.scalar.copy(dst, src), reads=[pt], writes=[stg_key])

    def xt_stage(self, X, XT):
        ctx, nc, cfg = self.ctx, self.nc, self.cfg
        T = cfg.T
        G = 4 if cfg.NT % 4 == 0 else 2
        with contextlib.ExitStack() as st:
            xf = [ctx.sb(st, "xf", [P, D], F32) for _ in range(2)]
            xb = [ctx.sb(st, "xb", [P, D], BF16) for _ in range(2)]
            stg = [ctx.sb(st, "stg", [P, KC, G * P], BF16) for _ in range(2)]
            pst = [[ctx.ps(st, "pst", [P, 8 * P], BF16) for _ in range(2)] for _ in range(2)]
            XTv = XT.ap().rearrange("(kc p) t -> p kc t", p=P)
            for tt in range(cfg.NT):
                b = tt % 2
                g, gi = divmod(tt, G)
                ctx.dma("sp", xf[b][:], X.ap()[tt * P:(tt + 1) * P, :], writes=[xf[b]])
                ctx.op("pool", lambda: nc.gpsimd.tensor_copy(xb[b][:], xf[b][:]), reads=[xf[b]], writes=[xb[b]])
                self.xt_emit_tile(xb[b], xb[b], stg[g % 2], stg[g % 2], gi * P, pst[b])
                if gi == G - 1:
                    ctx.dma("sp", XTv[:, :, g * G * P:(g + 1) * G * P], stg[g % 2][:], reads=[stg[g % 2]],
                            writes=[("XT", g)])
            ctx.barrier()

    def ln_stage(self, Z, layer, which, X, XT):
        ctx, nc, cfg = self.ctx, self.nc, self.cfg
        G = 4 if cfg.NT % 4 == 0 else 2
        with contextlib.ExitStack() as st:
            gain = self.bcast_rows(st, "lng", self.inp("ln_gain", [DEPTH, 2, D]).ap()[layer, which, :], D)
            bias = self.bcast_rows(st, "lnb", self.inp("ln_bias", [DEPTH, 2, D]).ap()[layer, which, :], D)
            zf = [ctx.sb(st, "zf", [P, D], F32) for _ in range(2)]
            xn = [ctx.sb(st, "xn", [P, D], F32) for _ in range(2)]
            xo = [ctx.sb(st, "xo", [P, D], F32) for _ in range(2)]
            xb = [ctx.sb(st, "xb", [P, D], BF16) for _ in range(2)]
            stats = [ctx.sb(st, "stats", [P, 4, 6], F32) for _ in range(2)]
            mv = [ctx.sb(st, "mv", [P, 4], F32) for _ in range(2)]
            stg = [ctx.sb(st, "stg", [P, KC, G * P], BF16) for _ in range(2)]
            pst = [[ctx.ps(st, "pst", [P, 8 * P], BF16) for _ in range(2)] for _ in range(2)]
            XTv = XT.ap().rearrange("(kc p) t -> p kc t", p=P)
            for tt in range(cfg.NT):
                b = tt % 2
                g, gi = divmod(tt, G)
                ctx.dma("sp", zf[b][:], Z.ap()[tt * P:(tt + 1) * P, :], writes=[zf[b]])
                self.layernorm_tile(zf[b], xn[b], stats[b], mv[b], D, LN_EPS)
                ctx.op("dve", lambda: nc.vector.tensor_tensor(xn[b][:], xn[b][:], gain[:], ALU.mult),
                       reads=[xn[b], gain], writes=[xn[b]])
                ctx.op("pool", lambda: nc.gpsimd.tensor_tensor(xo[b][:], xn[b][:], bias[:], ALU.add),
                       reads=[xn[b], bias], writes=[xo[b]])
                ctx.dma("sp", X.ap()[tt * P:(tt + 1) * P, :], xo[b][:], reads=[xo[b]], writes=[("X", tt)])
                ctx.op("act", lambda: nc.scalar.copy(xb[b][:], xo[b][:]), reads=[xo[b]], writes=[xb[b]])
                self.xt_emit_tile(xb[b], xb[b], stg[g % 2], stg[g % 2], gi * P, pst[b])
                if gi == G - 1:
                    ctx.dma("sp", XTv[:, :, g * G * P:(g + 1) * G * P], stg[g % 2][:], reads=[stg[g % 2]],
                            writes=[("XT", g)])
            ctx.barrier()

    def layernorm_tile(self, src, dst, stats, mv, n, eps, src_key=None, dst_key=None):
        ctx, nc = self.ctx, self.nc
        src_key = src if src_key is None else src_key
        dst_key = dst if dst_key is None else dst_key
        nch = max(1, n // 512)
        w = n // nch
        for c in range(nch):
            ctx.op("dve", lambda: nc.vector.bn_stats(stats[:, c, :], src[:, c * w:(c + 1) * w]),
                   reads=[src_key], writes=[stats])
        ctx.op("dve", lambda: nc.vector.bn_aggr(mv[:, 0:2], stats[:, 0:nch, :]), reads=[stats], writes=[mv])
        ctx.op("dve", lambda: nc.vector.tensor_scalar_add(mv[:, 2:3], mv[:, 1:2], eps), reads=[mv], writes=[mv])
        ctx.op("act", lambda: nc.scalar.activation(mv[:, 2:3], mv[:, 2:3], AF.Sqrt), reads=[mv], writes=[mv])
        ctx.op("dve", lambda: nc.vector.reciprocal(mv[:, 2:3], mv[:, 2:3]), reads=[mv], writes=[mv])
        ctx.op("dve", lambda: nc.vector.tensor_scalar(mv[:, 3:4], mv[:, 0:1], mv[:, 2:3], -1.0, ALU.mult, ALU.mult),
               reads=[mv], writes=[mv])
        ctx.op("act", lambda: nc.scalar.activation(dst[:, 0:n], src[:, 0:n], AF.Identity, bias=mv[:, 3:4],
                                                   scale=mv[:, 2:3]), reads=[src_key, mv], writes=[dst_key])


class TiledW:
    def __init__(self, h, K, N, tw, kp):
        self.h, self.K, self.N, self.tw, self.kp = h, K, N, tw, kp
        self.kcn = K // kp


def _tw(self, name, src_fn, K, N, tw, kp=P):
    if name not in self.inputs:
        self.inp(name, [N // tw, kp, (K // kp) * tw])
        self.tiled[name] = (src_fn, K, N, tw, kp)
    return TiledW(self.inputs[name], K, N, tw, kp)


def _load_wt(self, wb, W, c0, nb, key):
    assert c0 % W.tw == 0 and nb % W.tw == 0, (c0, nb, W.tw)
    i0, nt = c0 // W.tw, nb // W.tw
    for i in range(nt):
        dst = wb[0:W.kp, 0:W.kcn, i * W.tw:(i + 1) * W.tw]
        src = W.h.ap()[i0 + i].rearrange("p (kc j) -> p kc j", j=W.tw)
        self.ctx.dma("pool", dst, src, writes=[key], merge=(i > 0))


Prog.tw = _tw
Prog.load_wt = _load_wt


class GemmRes:
    def __init__(self, prog, st, kcmax, nblk, npsum, nw=2):
        ctx = prog.ctx
        self.w = [ctx.sb(st, "wbuf", [P, kcmax, nblk], BF16) for _ in range(nw)]
        self.ps = [ctx.ps(st, "gps", [P, 512], F32) for _ in range(npsum)]
        self.wi = 0
        self.pi = 0
        self.pending = {}

    def next_w(self):
        w = self.w[self.wi]
        self.wi = (self.wi + 1) % len(self.w)
        for k in [k for k, v in self.pending.items() if v is w]:
            del self.pending[k]
        return w

    def prefetch(self, prog, W, c0, nb):
        key = (id(W.h), c0, nb)
        if key in self.pending:
            return
        wb = self.next_w()
        prog.load_wt(wb, W, c0, nb, wb)
        self.pending[key] = wb

    def get_w(self, prog, W, c0, nb):
        wb = self.pending.pop((id(W.h), c0, nb), None)
        if wb is None:
            wb = self.next_w()
            prog.load_wt(wb, W, c0, nb, wb)
        return wb

    def next_ps(self):
        p = self.ps[self.pi]
        self.pi = (self.pi + 1) % len(self.ps)
        return p


def _gemm_tok(self, res, AT, at_key, kcn, W, n0, ncols, tts, epi, nblk=512, kp=P, nxt=None):
    ctx, nc = self.ctx, self.nc
    blocks = [(c0, min(nblk, n0 + ncols - c0)) for c0 in range(n0, n0 + ncols, nblk)]
    for bi, (c0, nb) in enumerate(blocks):
        wb = res.get_w(self, W, c0, nb)
        if bi + 1 < len(blocks):
            res.prefetch(self, W, *blocks[bi + 1])
        elif nxt is not None:
            res.prefetch(self, *nxt)
        for tt in tts:
            ps = res.next_ps()
            for kc in range(kcn):
                ctx.op("pe", lambda: nc.tensor.matmul(ps[:, 0:nb], lhsT=AT[0:kp, kc, tt * P:(tt + 1) * P],
                                                      rhs=wb[0:kp, kc, 0:nb], start=(kc == 0), stop=(kc == kcn - 1)),
                       reads=[at_key, wb], writes=[ps])
            epi(ps, tt, c0, nb)


def _gemm_feat(self, res, AT, at_key, kcn, W, n0, ncols, tgs, epi, nblk=512, nxt=None):
    ctx, nc = self.ctx, self.nc
    blocks = [(c0, min(nblk, n0 + ncols - c0)) for c0 in range(n0, n0 + ncols, nblk)]
    for bi, (c0, nb) in enumerate(blocks):
        wb = res.get_w(self, W, c0, nb)
        if bi + 1 < len(blocks):
            res.prefetch(self, W, *blocks[bi + 1])
        elif nxt is not None:
            res.prefetch(self, *nxt)
        for (t0, tn) in tgs:
            for fb in range((nb + P - 1) // P):
                fw = min(P, nb - fb * P)
                ps = res.next_ps()
                for kc in range(kcn):
                    ctx.op("pe", lambda: nc.tensor.matmul(ps[0:fw, 0:tn], lhsT=wb[:, kc, fb * P:fb * P + fw],
                                                          rhs=AT[:, kc, t0:t0 + tn], start=(kc == 0),
                                                          stop=(kc == kcn - 1)),
                           reads=[at_key, wb], writes=[ps])
                epi(ps, c0 + fb * P, t0, tn)


Prog.gemm_tok = _gemm_tok
Prog.gemm_feat = _gemm_feat


def _load_AT(self, st, name, XT, kcn, t0, tn, pad=0):
    ctx = self.ctx
    t = ctx.sb(st, name, [P, kcn, pad + tn], BF16)
    v = XT.ap().rearrange("(kc p) t -> p kc t", p=P)
    step = max(1, kcn // 4)
    for k0 in range(0, kcn, step):
        k1 = min(kcn, k0 + step)
        ctx.dma("sp", t[:, k0:k1, pad:pad + tn], v[:, k0:k1, t0:t0 + tn], writes=[t], merge=(k0 > 0))
    return t


Prog.load_AT = _load_AT


def _epi_resid(self, st, Xold, Zout, tok_base=0):
    ctx, nc = self.ctx, self.nc
    xo = [ctx.sb(st, "rx", [P, 512], F32) for _ in range(3)]
    zt = [ctx.sb(st, "rz", [P, 512], F32) for _ in range(3)]
    cnt = [0]

    def epi(ps, tt, c0, nb):
        i = cnt[0] % 3
        cnt[0] += 1
        r0 = tok_base + tt * P
        ctx.dma("sp", xo[i][:, 0:nb], Xold.ap()[r0:r0 + P, c0:c0 + nb], writes=[xo[i]])
        ctx.op("dve", lambda: nc.vector.scalar_tensor_tensor(zt[i][:, 0:nb], xo[i][:, 0:nb], ALPHA, ps[:, 0:nb],
                                                             ALU.mult, ALU.add),
               reads=[xo[i], ps], writes=[zt[i]])
        ctx.dma("sp", Zout.ap()[r0:r0 + P, c0:c0 + nb], zt[i][:, 0:nb], reads=[zt[i]], writes=[("Z", r0, c0)])
    return epi


Prog.epi_resid = _epi_resid


def _retention_layer(self, layer, X, XT, Z1):
    ctx, nc, cfg = self.ctx, self.nc, self.cfg
    T, NT, NSEG = cfg.T, cfg.NT, cfg.NSEG
    j = layer // 3
    w_qk = self.tw("ret_w_qk_t%d" % j, lambda inp, j=j: inp["ret_w_in"][j][:, 0:4096], D, 4096, 256)
    w_vg = self.tw("ret_w_vg_t%d" % j, lambda inp, j=j: inp["ret_w_in"][j][:, 4096:12288], D, 8192, 512)
    w_out = self.tw("ret_w_out_t%d" % j, lambda inp, j=j: inp["ret_w_out"][j], 4096, D, 256)
    gn_ap = self.inp("ret_gn_gain_%d" % j, [4096]).ap()
    KTs = self.scr("ret_KT", [D, T], BF16)
    Vs = self.scr("ret_V", [T, 4096], BF16)
    OGT = self.scr("ret_OGT", [4096, T], BF16)
    CCI = self.scr("ret_cci", [RET_H, NSEG, 2, P, 512])
    CCO = self.scr("ret_cco", [RET_H, NSEG, 2, P, 512])
    TH = min(1024, T)
    NTH = TH // P
    TGW = min(512, TH)
    tgs = [(t0, TGW) for t0 in range(0, TH, TGW)]
    QOFF, KOFF, VOFF, GOFF = 0, 2048, 0, 4096
    rope_cos = self.inp("rope_cos", [P, T]).ap()
    rope_sin = self.inp("rope_sin", [P, T]).ap()

    def rotary_epi(cosT, sinT, dstT, tmp):
        state = {}

        def epi(ps, c, t0, tn):
            half = (c // P) % 2
            if half == 0:
                state["A"] = ps
                return
            psA, psB = state["A"], ps
            t1, t2, t3, t4 = tmp
            cs, sn = cosT[:, t0:t0 + tn], sinT[:, t0:t0 + tn]
            ctx.op("dve", lambda: nc.vector.tensor_tensor(t1[:, 0:tn], psA[:, 0:tn], cs, ALU.mult),
                   reads=[psA, cosT], writes=[t1])
            ctx.op("dve", lambda: nc.vector.tensor_tensor(t2[:, 0:tn], psB[:, 0:tn], sn, ALU.mult),
                   reads=[psB, sinT], writes=[t2])
            ctx.op("dve", lambda: nc.vector.tensor_tensor(t3[:, 0:tn], psA[:, 0:tn], sn, ALU.mult),
                   reads=[psA, sinT], writes=[t3])
            ctx.op("dve", lambda: nc.vector.tensor_tensor(t4[:, 0:tn], psB[:, 0:tn], cs, ALU.mult),
                   reads=[psB, cosT], writes=[t4])
            ctx.op("pool", lambda: nc.gpsimd.tensor_tensor(dstT[:, 0, t0:t0 + tn], t1[:, 0:tn], t2[:, 0:tn],
                                                           ALU.subtract), reads=[t1, t2], writes=[dstT])
            ctx.op("pool", lambda: nc.gpsimd.tensor_tensor(dstT[:, 1, t0:t0 + tn], t3[:, 0:tn], t4[:, 0:tn],
                                                           ALU.add), reads=[t3, t4], writes=[dstT])
        return epi

    KTv = KTs.ap().rearrange("(h two p) t -> h p two t", two=2, p=P)
    Vv = Vs.ap().rearrange("(tt p) e -> p tt e", p=P)
    OGTv = OGT.ap().rearrange("(h fc p) t -> h p fc t", fc=4, p=P)

    with contextlib.ExitStack() as st:
        kdec = ctx.sb(st, "kdec", [P, RET_H], F32)
        ctx.dma("sp", kdec[:], self.inp("ret_kdec", [P, RET_H]).ap(), writes=[kdec])
        res = GemmRes(self, st, KC, 512, 3)
        tmp = [ctx.sb(st, "rt", [P, 512], F32) for _ in range(4)]
        kT = [ctx.sb(st, "kT", [P, 2, TH], BF16) for _ in range(2)]
        vh = [ctx.sb(st, "vh", [P, NTH, 512], BF16) for _ in range(2)]
        kdA = [ctx.sb(st, "kdA", [P, 2 * P], BF16) for _ in range(2)]
        pst = [ctx.ps(st, "pst", [P, 2 * P], BF16) for _ in range(2)]
        Lps = [ctx.ps(st, "Lps", [P, 512], F32) for _ in range(2)]
        Lacc = ctx.sb(st, "Lacc", [P, RET_H, 2, 512], F32)
        Lm = [ctx.sb(st, "Lm", [P, 2, 512], F32) for _ in range(2)]
        cosk = ctx.sb(st, "cosk", [P, TH], F32)
        sink = ctx.sb(st, "sink", [P, TH], F32)
        xT = ctx.sb(st, "xT", [P, KC, TH], BF16)
        XTv = XT.ap().rearrange("(kc p) t -> p kc t", p=P)
        for th in range(T // TH):
            t0h = th * TH
            for k0 in range(0, KC, 4):
                ctx.dma("sp", xT[:, k0:k0 + 4, :], XTv[:, k0:k0 + 4, t0h:t0h + TH], writes=[xT], merge=(k0 > 0))
            ctx.dma("sp", cosk[:], rope_cos[:, t0h:t0h + TH], writes=[cosk])
            ctx.dma("sp", sink[:], rope_sin[:, t0h:t0h + TH], writes=[sink])
            ctx.op("pool", lambda: nc.gpsimd.tensor_scalar_mul(cosk[:], cosk[:], RET_DK ** -0.5), reads=[cosk], writes=[cosk])
            ctx.op("pool", lambda: nc.gpsimd.tensor_scalar_mul(sink[:], sink[:], RET_DK ** -0.5), reads=[sink], writes=[sink])
            for h in range(RET_H):
                kTh, vhh = kT[h % 2], vh[h % 2]
                self.gemm_feat(res, xT, xT, KC, w_qk, KOFF + h * 256, 256, tgs, rotary_epi(cosk, sink, kTh, tmp), nblk=256,
                               nxt=(w_vg, VOFF + h * 512, 512))
                ctx.dma("sp", KTv[h][:, :, t0h:t0h + TH], kTh[:], reads=[kTh], writes=[("KT", h, th)])

                def v_epi(ps, tt, c0, nb):
                    ctx.op("act", lambda: nc.scalar.copy(vhh[:, tt, :], ps[:, 0:nb]), reads=[ps], writes=[vhh])
                self.gemm_tok(res, xT, xT, KC, w_vg, VOFF + h * 512, 512, range(NTH), v_epi,
                              nxt=(w_qk, KOFF + ((h + 1) % RET_H) * 256, 256))
                ctx.dma("sp", Vv[:, th * NTH:(th + 1) * NTH, h * 512:(h + 1) * 512], vhh[:],
                        reads=[vhh], writes=[("V", h, th)])
                if NSEG > 1:
                    g = RET_GAMMA[h]
                    for cl in range(NTH):
                        c = th * NTH + cl
                        pt = pst[cl % 2]
                        kd = kdA[cl % 2]
                        for half in range(2):
                            self.transpose_to(kTh[:, half, cl * P:(cl + 1) * P], pt[:, half * P:(half + 1) * P], [kTh], [pt])
                        ctx.op("dve", lambda: nc.vector.tensor_scalar(kd[:], pt[:], kdec[:, h:h + 1],
                                                                      float(g ** (P * (NT - 1 - c))), ALU.mult, ALU.mult),
                               reads=[pt, kdec], writes=[kd])
                        for half in range(2):
                            ctx.op("pe", lambda: nc.tensor.matmul(Lps[half][:], lhsT=kd[:, half * P:(half + 1) * P],
                                                                  rhs=vhh[:, cl, :], start=(cl == 0), stop=(cl == NTH - 1)),
                                   reads=[kd, vhh], writes=[Lps[half]])
                    for half in range(2):
                        if th == 0:
                            ctx.op("act", lambda: nc.scalar.copy(Lacc[:, h, half, :], Lps[half][:]),
                                   reads=[Lps[half]], writes=[(id(Lacc), h)])
                        else:
                            ctx.op("dve", lambda: nc.vector.tensor_tensor(Lacc[:, h, half, :], Lacc[:, h, half, :],
                                                                          Lps[half][:], ALU.add),
                                   reads=[Lps[half], (id(Lacc), h)], writes=[(id(Lacc), h)])
        if NSEG > 1:
            for h in range(RET_H):
                for s in range(NSEG):
                    lm = Lm[(h * NSEG + s) % 2]
                    ctx.op("dve", lambda: nc.vector.tensor_scalar_mul(lm[:], Lacc[:, h], self.own[:, s:s + 1]),
                           reads=[(id(Lacc), h), self.own], writes=[lm])
                    ctx.dma("sp", CCI.ap()[h, s].rearrange("two p e -> p two e"), lm[:], reads=[lm],
                            writes=[("CCI", s, h)])
        ctx.barrier()
    if NSEG > 1:
        for h in range(RET_H):
            ctx.allreduce(cfg.groups, CCI.ap()[h].rearrange("s two p e -> (s two p) e"),
                          CCO.ap()[h].rearrange("s two p e -> (s two p) e"), writes=[("CCO", h)])
        ctx.barrier()

    with contextlib.ExitStack() as st:
        kdec = ctx.sb(st, "kdec", [P, RET_H], F32)
        ctx.dma("sp", kdec[:], self.inp("ret_kdec", [P, RET_H]).ap(), writes=[kdec])
        maskT = ctx.sb(st, "maskT", [P, RET_H, P], F32)
        ctx.dma("sp", maskT[:], self.inp("ret_maskT", [P, RET_H, P]).ap(), writes=[maskT])
        qdec = ctx.sb(st, "qdec", [P, RET_H, P], F32)
        ctx.dma("sp", qdec[:], self.inp("ret_qdec", [P, RET_H, P]).ap(), writes=[qdec])
        coef = ctx.sb(st, "coef", [P, NSEG, RET_H], F32)
        ctx.dma("sp", coef[:], self.inp("ret_coef", [P, NSEG, RET_H]).ap(), writes=[coef])
        gain = self.bcast_rows(st, "gng", gn_ap, 4096)
        res = GemmRes(self, st, KC, 512, 2)
        tmp = [ctx.sb(st, "rt", [P, 512], F32) for _ in range(4)]
        cosT = ctx.sb(st, "cos", [P, TH], F32)
        sinT = ctx.sb(st, "sin", [P, TH], F32)
        xT = ctx.sb(st, "xT", [P, KC, TH], BF16)
        XTv = XT.ap().rearrange("(kc p) t -> p kc t", p=P)
        kTh = ctx.sb(st, "kT", [P, 2, TH], BF16)
        qTh = ctx.sb(st, "qT", [P, 2, TH], BF16)
        vhh = ctx.sb(st, "vh", [P, NTH, 512], BF16)
        gsh = ctx.sb(st, "gs", [P, NTH, 512], BF16)
        ogTh = ctx.sb(st, "ogT", [P, 4, TH], BF16)
        Rall = ctx.sb(st, "Rall", [P, RET_H, 2, 512], F32)
        Rb = ctx.sb(st, "Rb", [P, 2, 512], BF16)
        cin = [ctx.sb(st, "cin", [P, 2, 512], F32) for _ in range(2)]
        sT = [ctx.sb(st, "sT", [P, P], BF16) for _ in range(2)]
        qd = [ctx.sb(st, "qd", [P, 2, P], BF16) for _ in range(2)]
        kd = [ctx.sb(st, "kd", [P, 2 * P], BF16) for _ in range(2)]
        on = [ctx.sb(st, "on", [P, 512], F32) for _ in range(2)]
        og = [ctx.sb(st, "og", [P, 512], F32) for _ in range(2)]
        og2 = [ctx.sb(st, "og2", [P, 512], BF16) for _ in range(2)]
        stats = [ctx.sb(st, "stats", [P, 4, 6], F32) for _ in range(2)]
        mv = [ctx.sb(st, "mv", [P, 4], F32) for _ in range(2)]
        ps_s = ctx.ps(st, "ps_s", [P, 512], F32)
        ps_o = ctx.ps(st, "ps_o", [P, 512], F32)
        ps_t = ctx.ps(st, "ps_t", [P, 2 * P], BF16)
        ps_R = [ctx.ps(st, "ps_R", [P, 512], F32) for _ in range(2)]
        ps_g = ctx.ps(st, "ps_g", [P, 4 * P], BF16)
        for h in range(RET_H):
            Rk = (id(Rall), h)
            if NSEG > 1:
                for s in range(NSEG):
                    ci = cin[s % 2]
                    ctx.dma("sp", ci[:], CCO.ap()[h, s].rearrange("two p e -> p two e"), writes=[ci])
                    if s == 0:
                        ctx.op("dve", lambda: nc.vector.tensor_scalar_mul(Rall[:, h], ci[:], coef[:, s, h:h + 1]),
                               reads=[ci, coef], writes=[Rk])
                    else:
                        ctx.op("dve", lambda: nc.vector.scalar_tensor_tensor(Rall[:, h], ci[:], coef[:, s, h:h + 1],
                                                                             Rall[:, h], ALU.mult, ALU.add),
                               reads=[ci, coef, Rk], writes=[Rk])
            else:
                ctx.op("dve", lambda: nc.vector.memset(Rall[:, h], 0.0), writes=[Rk])
        for th in range(T // TH):
            t0h = th * TH
            for k0 in range(0, KC, 4):
                ctx.dma("sp", xT[:, k0:k0 + 4, :], XTv[:, k0:k0 + 4, t0h:t0h + TH], writes=[xT], merge=(k0 > 0))
            ctx.dma("sp", cosT[:], rope_cos[:, t0h:t0h + TH], writes=[cosT])
            ctx.dma("sp", sinT[:], rope_sin[:, t0h:t0h + TH], writes=[sinT])
            for h in range(RET_H):
                Rk = (id(Rall), h)
                ctx.dma("sp", kTh[:], KTv[h][:, :, t0h:t0h + TH], writes=[kTh])
                ctx.dma("sp", vhh[:], Vv[:, th * NTH:(th + 1) * NTH, h * 512:(h + 1) * 512], writes=[vhh])
                ctx.op("act", lambda: nc.scalar.copy(Rb[:], Rall[:, h]), reads=[Rk], writes=[Rb])
                self.gemm_feat(res, xT, xT, KC, w_qk, QOFF + h * 256, 256, tgs, rotary_epi(cosT, sinT, qTh, tmp), nblk=256,
                               nxt=(w_vg, GOFF + h * 512, 512))

                def g_epi(ps, tt, c0, nb):
                    ctx.op("act", lambda: nc.scalar.activation(gsh[:, tt, :], ps[:, 0:nb], AF.Silu), reads=[ps], writes=[gsh])
                self.gemm_tok(res, xT, xT, KC, w_vg, GOFF + h * 512, 512, range(NTH), g_epi,
                              nxt=(w_qk, QOFF + ((h + 1) % RET_H) * 256, 256))
                gam = RET_GAMMA[h]
                for cl in range(NTH):
                    cb = cl % 2
                    cs = slice(cl * P, (cl + 1) * P)
                    for half in range(2):
                        ctx.op("pe", lambda: nc.tensor.matmul(ps_s[:, 0:P], lhsT=kTh[:, half, cs], rhs=qTh[:, half, cs],
                                                              start=(half == 0), stop=(half == 1)),
                               reads=[kTh, qTh], writes=[ps_s])
                    ctx.op("dve", lambda: nc.vector.tensor_tensor(sT[cb][:], ps_s[:, 0:P], maskT[:, h, :], ALU.mult),
                           reads=[ps_s, maskT], writes=[sT[cb]])
                    for half in range(2):
                        ctx.op("pool", lambda: nc.gpsimd.tensor_tensor(qd[cb][:, half, :], qTh[:, half, cs],
                                                                       qdec[:, h, :], ALU.mult),
                               reads=[qTh, qdec], writes=[qd[cb]])
                    ctx.op("pe", lambda: nc.tensor.matmul(ps_o[:], lhsT=sT[cb][:], rhs=vhh[:, cl, :], start=True, stop=False),
                           reads=[sT[cb], vhh], writes=[ps_o])
                    for half in range(2):
                        ctx.op("pe", lambda: nc.tensor.matmul(ps_o[:], lhsT=qd[cb][:, half, :], rhs=Rb[:, half, :],
                                                              start=False, stop=(half == 1)),
                               reads=[qd[cb], Rb], writes=[ps_o])
                    for half in range(2):
                        self.transpose_to(kTh[:, half, cs], ps_t[:, half * P:(half + 1) * P], [kTh], [ps_t])
                    ctx.op("dve", lambda: nc.vector.tensor_scalar_mul(kd[cb][:], ps_t[:], kdec[:, h:h + 1]),
                           reads=[ps_t, kdec], writes=[kd[cb]])
                    for half in range(2):
                        ctx.op("pe", lambda: nc.tensor.matmul(ps_R[half][:], lhsT=kd[cb][:, half * P:(half + 1) * P],
                                                              rhs=vhh[:, cl, :], start=True, stop=True),
                               reads=[kd[cb], vhh], writes=[ps_R[half]])
                        ctx.op("dve", lambda: nc.vector.scalar_tensor_tensor(Rall[:, h, half, :], Rall[:, h, half, :],
                                                                             float(gam ** P), ps_R[half][:],
                                                                             ALU.mult, ALU.add),
                               reads=[Rk, ps_R[half]], writes=[Rk])
                    ctx.op("act", lambda: nc.scalar.copy(Rb[:], Rall[:, h]), reads=[Rk], writes=[Rb])
                    self.layernorm_tile(ps_o, on[cb], stats[cb], mv[cb], 512, RET_EPS)
                    ctx.op("dve", lambda: nc.vector.tensor_tensor(og[cb][:], on[cb][:], gain[:, h * 512:(h + 1) * 512], ALU.mult),
                           reads=[on[cb], gain], writes=[og[cb]])
                    ctx.op("pool", lambda: nc.gpsimd.tensor_tensor(og2[cb][:], og[cb][:], gsh[:, cl, :], ALU.mult),
                           reads=[og[cb], gsh], writes=[og2[cb]])
                    for fc in range(4):
                        self.transpose_to(og2[cb][:, fc * P:(fc + 1) * P], ps_g[:, fc * P:(fc + 1) * P], [og2[cb]], [ps_g])
                    ctx.op("act", lambda: nc.scalar.copy(ogTh[:, :, cs], ps_g[:].rearrange("p (f c) -> p f c", f=4)),
                           reads=[ps_g], writes=[ogTh])
                ctx.dma("sp", OGTv[h][:, :, t0h:t0h + TH], ogTh[:], reads=[ogTh], writes=[("OGT", h, th)])
        ctx.barrier()

    for t0 in range(0, T, TH):
        with contextlib.ExitStack() as st:
            aT = self.load_AT(st, "ogTa", OGT, 32, t0, TH)
            res = GemmRes(self, st, 32, 256, 3)
            epi = self.epi_resid(st, X, Z1, tok_base=t0)
            self.gemm_tok(res, aT, aT, 32, w_out, 0, D, range(TH // P), epi, nblk=256)
            ctx.barrier()


Prog.retention_layer = _retention_layer
def _halo_rows(self, st, Xsrc, nrows, name):
    ctx, nc, cfg = self.ctx, self.nc, self.cfg
    NSEG, T = cfg.NSEG, cfg.T
    halo = ctx.sb(st, name, [nrows, D], F32)
    if NSEG == 1:
        ctx.op("dve", lambda: nc.vector.memset(halo[:], 0.0), writes=[halo])
        return halo
    HCI = self.scr("halo_ci_%d" % nrows, [NSEG, nrows, D])
    HCO = self.scr("halo_co_%d" % nrows, [NSEG, nrows, D])
    hx = ctx.sb(st, name + "_x", [nrows, D], F32)
    hm = [ctx.sb(st, name + "_m", [nrows, D], F32) for _ in range(2)]
    ctx.dma("sp", hx[:], Xsrc.ap()[T - nrows:T, :], writes=[hx])
    for s in range(NSEG):
        ctx.op("dve", lambda: nc.vector.tensor_scalar_mul(hm[s % 2][:], hx[:], self.own[0:nrows, s:s + 1]),
               reads=[hx, self.own], writes=[hm[s % 2]])
        ctx.dma("sp", HCI.ap()[s], hm[s % 2][:], reads=[hm[s % 2]], writes=[("HCI", s)])
    ctx.barrier()
    ctx.allreduce(cfg.groups, HCI.ap().rearrange("s r d -> (s r) d"), HCO.ap().rearrange("s r d -> (s r) d"),
                  writes=[("HCO",)])
    ctx.barrier()
    for s in range(NSEG):
        ctx.dma("sp", hm[s % 2][:], HCO.ap()[s], writes=[hm[s % 2]])
        if s == 0:
            ctx.op("dve", lambda: nc.vector.tensor_scalar_mul(halo[:], hm[s % 2][:], self.halo_sel[0:nrows, s:s + 1]),
                   reads=[hm[s % 2], self.halo_sel], writes=[halo])
        else:
            ctx.op("dve", lambda: nc.vector.scalar_tensor_tensor(halo[:], hm[s % 2][:], self.halo_sel[0:nrows, s:s + 1],
                                                                 halo[:], ALU.mult, ALU.add),
                   reads=[hm[s % 2], self.halo_sel, halo], writes=[halo])
    return halo


Prog.halo_rows = _halo_rows


def _ffn_layer(self, layer, X1, XT1, Z2):
    ctx, nc, cfg = self.ctx, self.nc, self.cfg
    T = cfg.T
    w_up = self.tw("ffn_w_up_t%d" % layer, lambda inp, l=layer: inp["ffn_w_up"][l], D, 2 * DFF, 128)
    w_dn = self.tw("ffn_w_down_t%d" % layer, lambda inp, l=layer: inp["ffn_w_down"][l], DFF, D, 256)
    cw_ap = self.inp("ffn_conv_w_%d" % layer, [3, 2 * DFF]).ap().rearrange("t (b p) -> (t b) p", p=P)
    cb_ap = self.inp("ffn_conv_b_%d" % layer, [2 * DFF]).ap().rearrange("(b p) -> b p", p=P)
    NB2 = 2 * NFB
    TG = min(1024, T)
    W = TG + 2
    nsub = (W + 511) // 512
    bounds = [(W * i) // nsub for i in range(nsub + 1)]
    with contextlib.ExitStack() as st0:
        psc = ctx.ps(st0, "psc", [P, 512], F32)
        cw = self.load_cols(st0, "cw", cw_ap, 3 * NB2, psc)
        cb = self.load_cols(st0, "cb", cb_ap, NB2, psc)
        haloT = ctx.sb(st0, "haloT", [P, KC, 2], BF16)
        with contextlib.ExitStack() as sth:
            halo = self.halo_rows(sth, X1, 2, "halo2")
            for kc in range(KC):
                ctx.op("pe", lambda: nc.tensor.matmul(psc[:, kc * 2:kc * 2 + 2], lhsT=halo[0:2, kc * P:(kc + 1) * P],
                                                      rhs=self.identf[0:2, 0:2], start=True, stop=True),
                       reads=[halo, self.identf], writes=[psc])
            ctx.op("dve", lambda: nc.vector.tensor_copy(haloT[:], psc[:, 0:2 * KC].rearrange("p (k c) -> p k c", c=2)),
                   reads=[psc], writes=[haloT])
            ctx.barrier()
        gT = ctx.sb(st0, "gT", [P, NFB, TG], BF16)
        XTv = XT1.ap().rearrange("(kc p) t -> p kc t", p=P)
        for g in range(T // TG):
            t0 = g * TG
            with contextlib.ExitStack() as st:
                xT = ctx.sb(st, "x1T", [P, KC, W], BF16)
                for k0 in range(0, KC, 4):
                    ctx.dma("sp", xT[:, k0:k0 + 4, 2:W], XTv[:, k0:k0 + 4, t0:t0 + TG], writes=[xT], merge=(k0 > 0))
                if g == 0:
                    ctx.op("pool", lambda: nc.gpsimd.tensor_copy(xT[:, :, 0:2], haloT[:]), reads=[haloT], writes=[xT])
                else:
                    ctx.dma("sp", xT[:, :, 0:2], XTv[:, :, t0 - 2:t0], writes=[xT], merge=True)
                wu = [ctx.sb(st, "wu", [P, 2, KC, P], BF16) for _ in range(2)]
                wg = [ctx.sb(st, "wg", [P, 2, KC, P], BF16) for _ in range(2)]
                pss = [ctx.ps(st, "fps", [P, 512], F32) for _ in range(6)]
                hs = [ctx.sb(st, "hs", [P, W], F32) for _ in range(2)]
                acc = [ctx.sb(st, "acc", [P, TG], F32) for _ in range(2)]
                sg = ctx.sb(st, "sg", [P, TG], F32)
                pi = 0

                def load_pair(pr):
                    fb0 = 2 * pr
                    wi_ = pr % 2
                    for ti in range(min(2, NFB - fb0)):
                        ctx.dma("pool", wu[wi_][:, ti], w_up.h.ap()[fb0 + ti].rearrange("p (kc j) -> p kc j", j=P),
                                writes=[wu[wi_]], merge=(ti > 0))
                        ctx.dma("pool", wg[wi_][:, ti], w_up.h.ap()[NFB + fb0 + ti].rearrange("p (kc j) -> p kc j", j=P),
                                writes=[wg[wi_]], merge=(ti > 0))
                load_pair(0)
                for fb in range(NFB):
                    if fb % 2 == 0 and fb + 2 < NFB:
                        load_pair(fb // 2 + 1)
                    wi = (fb // 2) % 2
                    fo = (fb % 2) * P
                    for ui, wt in enumerate((wu[wi], wg[wi])):
                        for si in range(nsub):
                            a, b_ = bounds[si], bounds[si + 1]
                            ps = pss[pi % 6]
                            pi += 1
                            for kc in range(KC):
                                ctx.op("pe", lambda: nc.tensor.matmul(ps[:, 0:b_ - a], lhsT=wt[:, fb % 2, kc, :],
                                                                      rhs=xT[:, kc, a:b_], start=(kc == 0), stop=(kc == KC - 1)),
                                       reads=[wt, xT], writes=[ps])
                            ctx.op("act", lambda: nc.scalar.copy(hs[ui][:, a:b_], ps[:, 0:b_ - a]), reads=[ps], writes=[hs[ui]])
                        blk = fb if ui == 0 else NFB + fb
                        w0 = cw[:, 0 * NB2 + blk:0 * NB2 + blk + 1]
                        w1 = cw[:, 1 * NB2 + blk:1 * NB2 + blk + 1]
                        w2 = cw[:, 2 * NB2 + blk:2 * NB2 + blk + 1]
                        ctx.op("act", lambda: nc.scalar.activation(acc[ui][:], hs[ui][:, 2:W], AF.Identity,
                                                                   bias=cb[:, blk:blk + 1], scale=w2),
                               reads=[hs[ui], cw, cb], writes=[acc[ui]])
                        ctx.op("dve", lambda: nc.vector.scalar_tensor_tensor(acc[ui][:], hs[ui][:, 1:W - 1], w1, acc[ui][:],
                                                                             ALU.mult, ALU.add),
                               reads=[hs[ui], cw, acc[ui]], writes=[acc[ui]])
                        ctx.op("dve", lambda: nc.vector.scalar_tensor_tensor(acc[ui][:], hs[ui][:, 0:W - 2], w0, acc[ui][:],
                                                                             ALU.mult, ALU.add),
                               reads=[hs[ui], cw, acc[ui]], writes=[acc[ui]])
                    ctx.op("act", lambda: nc.scalar.activation(sg[:], acc[1][:], AF.Silu), reads=[acc[1]], writes=[sg])
                    ctx.op("pool", lambda: nc.gpsimd.tensor_tensor(gT[:, fb, :], sg[:], acc[0][:], ALU.mult),
                           reads=[sg, acc[0]], writes=[gT])
                ctx.barrier()
            with contextlib.ExitStack() as st:
                res = GemmRes(self, st, NFB, 256, 4)
                epi = self.epi_resid(st, X1, Z2, tok_base=t0)
                self.gemm_tok(res, gT, gT, NFB, w_dn, 0, D, range(TG // P), epi, nblk=256)
                ctx.barrier()


Prog.ffn_layer = _ffn_layer


def _ple_layer(self, layer, X2, XT2, X3):
    ctx, nc, cfg = self.ctx, self.nc, self.cfg
    T, NT = cfg.T, cfg.NT
    w_gate = self.tw("ple_w_gate_t%d" % layer, lambda inp, l=layer: inp["ple_w_gate"][l], D, D, 512)
    w_proj = self.tw("ple_w_proj_t%d" % layer, lambda inp, l=layer: inp["ple_w_proj"][l], PLE, D, 512)
    pT_in = self.inp("pT_%d" % layer, [PLE, T]).ap()
    with contextlib.ExitStack() as st:
        xT = self.load_AT(st, "x2T", XT2, KC, 0, T)
        pT = ctx.sb(st, "pT", [P, 2, T], BF16)
        self.load_w(pT[:], pT_in.rearrange("(kc p) t -> p kc t", p=P), pT)
        wg = [ctx.sb(st, "wg", [P, KC, 512], BF16) for _ in range(2)]
        wp = [ctx.sb(st, "wp", [P, 2, 512], BF16) for _ in range(2)]
        psg = [ctx.ps(st, "psg", [P, 512], F32) for _ in range(3)]
        psp = [ctx.ps(st, "psp", [P, 512], F32) for _ in range(3)]
        sg = [ctx.sb(st, "sg", [P, 512], F32) for _ in range(3)]
        x2 = [ctx.sb(st, "x2", [P, 512], F32) for _ in range(3)]
        x3 = [ctx.sb(st, "x3", [P, 512], F32) for _ in range(3)]
        it = 0

        def load_blk(ci_):
            self.load_wt(wg[ci_ % 2], w_gate, ci_ * 512, 512, wg[ci_ % 2])
            self.load_wt(wp[ci_ % 2], w_proj, ci_ * 512, 512, wp[ci_ % 2])
        load_blk(0)
        for ci, c0 in enumerate(range(0, D, 512)):
            wgi, wpi = wg[ci % 2], wp[ci % 2]
            if ci + 1 < D // 512:
                load_blk(ci + 1)
            for tt in range(NT):
                i = it % 3
                it += 1
                ts = slice(tt * P, (tt + 1) * P)
                for kc in range(KC):
                    ctx.op("pe", lambda: nc.tensor.matmul(psg[i][:], lhsT=xT[:, kc, ts], rhs=wgi[:, kc, :],
                                                          start=(kc == 0), stop=(kc == KC - 1)),
                           reads=[xT, wgi], writes=[psg[i]])
                for kc in range(2):
                    ctx.op("pe", lambda: nc.tensor.matmul(psp[i][:], lhsT=pT[:, kc, ts], rhs=wpi[:, kc, :],
                                                          start=(kc == 0), stop=(kc == 1)),
                           reads=[pT, wpi], writes=[psp[i]])
                ctx.dma("sp", x2[i][:], X2.ap()[tt * P:(tt + 1) * P, c0:c0 + 512], writes=[x2[i]])
                ctx.op("act", lambda: nc.scalar.activation(sg[i][:], psg[i][:], AF.Sigmoid), reads=[psg[i]], writes=[sg[i]])
                ctx.op("dve", lambda: nc.vector.tensor_tensor(sg[i][:], sg[i][:], psp[i][:], ALU.mult),
                       reads=[sg[i], psp[i]], writes=[sg[i]])
                ctx.op("pool", lambda: nc.gpsimd.tensor_tensor(x3[i][:], sg[i][:], x2[i][:], ALU.add),
                       reads=[sg[i], x2[i]], writes=[x3[i]])
                ctx.dma("sp", X3.ap()[tt * P:(tt + 1) * P, c0:c0 + 512], x3[i][:], reads=[x3[i]], writes=[("X3", tt, c0)])
        ctx.barrier()


Prog.ple_layer = _ple_layer
SWA_HQ, SWA_HKV, SWA_HD, SWA_W = 32, 4, 64, 128
NEG = -1e30


def _swa_head_order():
    order = []
    for pair in range(2):
        for g in range(8):
            order.append((2 * pair) * 8 + g)
            order.append((2 * pair + 1) * 8 + g)
    return order


def _t5_bucket(n):
    max_exact = 16
    if n < max_exact:
        return n
    large = max_exact + int(np.log(max(n, 1) / max_exact) / np.log(SWA_W / max_exact) * (32 - max_exact))
    return min(large, 31)


def _swa_consts(cfg, core, c):
    seg = core % cfg.NSEG
    E = np.zeros((32, 383), np.float32)
    for u in range(383):
        d = u - 127
        if 0 <= d < SWA_W:
            nn = np.maximum(np.array([d]), 0)
            large = 16 + (np.log(np.maximum(nn, 1) / 16) / np.log(SWA_W / 16) * 16).astype(np.int32)
            large = np.minimum(large, 31)
            b = int(np.where(nn < 16, nn, large)[0])
            E[b, u] = 1.0
    c["swa_E"] = E
    i = np.arange(P)[:, None]
    j = np.arange(2 * P)[None, :]
    d = i + P - j
    c["swa_maskc"] = np.where((d >= 0) & (d < SWA_W), 0.0, NEG).astype(np.float32)
    mf = np.zeros((P, 2 * P), np.float32)
    if seg == 0:
        mf[:, :P] = NEG
    c["swa_mask_first"] = mf


def _swa_layer(self, layer, X, XT, Z1):
    ctx, nc, cfg = self.ctx, self.nc, self.cfg
    T, NT, NSEG = cfg.T, cfg.NT, cfg.NSEG
    j = layer // 3
    def _qkv_src(inp, j=j):
        w = inp["swa_w_qkv"][j]
        qcols = np.concatenate([np.arange(h * 64, (h + 1) * 64) for h in _swa_head_order()])
        return np.concatenate([w[:, qcols], w[:, 2048:]], axis=1)

    def _out_src(inp, j=j):
        rows = np.concatenate([np.arange(h * 64, (h + 1) * 64) for h in _swa_head_order()])
        return inp["swa_w_out"][j][rows, :]
    w_q = self.tw("swa_w_q_t%d" % j, lambda inp: _qkv_src(inp)[:, 0:2048], D, 2048, 512)
    w_kv = self.tw("swa_w_kv_t%d" % j, lambda inp: _qkv_src(inp)[:, 2048:2560], D, 512, 256)
    w_out = self.tw("swa_w_out_t%d" % j, _out_src, D, D, 512)
    sinks_ap = self.inp("swa_sinks_%d" % j, [SWA_HQ]).ap()
    relb_ap = self.inp("rel_bias", [32, SWA_HQ]).ap()
    OT = self.scr("swa_OT", [D, T], BF16)
    QTs = self.scr("swa_QT", [D, T], BF16)
    order = _swa_head_order()
    TGW = min(512, T)
    tgs = [(t0, TGW) for t0 in range(0, T, TGW)]
    with contextlib.ExitStack() as st0:
        kT = ctx.sb(st0, "kT", [P, 2, P + T], BF16)
        vS = ctx.sb(st0, "vS", [P, 1 + NT, 256], BF16)
        biasS = ctx.sb(st0, "biasS", [P, SWA_HQ, 2 * P], F32)
        sinkb = self.bcast_rows(st0, "sinkb", sinks_ap, SWA_HQ)
        mfirst = ctx.sb(st0, "mfirst", [P, 2 * P], F32)
        ctx.dma("sp", mfirst[:], self.inp("swa_mask_first", [P, 2 * P]).ap(), writes=[mfirst])
        with contextlib.ExitStack() as st:
            E = ctx.sb(st, "E", [32, 383], F32)
            RB = ctx.sb(st, "RB", [32, SWA_HQ], F32)
            maskc = ctx.sb(st, "maskc", [P, 2 * P], F32)
            ctx.dma("sp", E[:], self.inp("swa_E", [32, 383]).ap(), writes=[E])
            ctx.dma("sp", RB[:], relb_ap, writes=[RB])
            ctx.dma("sp", maskc[:], self.inp("swa_maskc", [P, 2 * P]).ap(), writes=[maskc])
            psb = [ctx.ps(st, "psb", [P, 512], F32) for _ in range(2)]
            for r in range(16):
                ps = psb[r % 2]
                for jj in range(16):
                    jk = r * 16 + jj
                    ctx.op("pe", lambda: nc.tensor.matmul(ps[:, jj * 32:(jj + 1) * 32], lhsT=E[:, 255 - jk:383 - jk], rhs=RB[:],
                                                          start=True, stop=True), reads=[E, RB], writes=[ps])
                ctx.op("dve", lambda: nc.vector.tensor_tensor(
                    biasS[:, :, r * 16:(r + 1) * 16].rearrange("p h j -> p j h"),
                    ps[:].rearrange("p (j h) -> p j h", h=32),
                    maskc[:, r * 16:(r + 1) * 16].unsqueeze(2).broadcast_to([P, 16, 32]), ALU.add),
                    reads=[ps, maskc], writes=[biasS])
            ctx.barrier()
        with contextlib.ExitStack() as st:
            xT = self.load_AT(st, "xT", XT, KC, 0, T)
            res = GemmRes(self, st, KC, 512, 3)
            qst = [ctx.sb(st, "qst", [P, 4, TGW], BF16) for _ in range(2)]
            QTv = QTs.ap().rearrange("(kc p) t -> p kc t", p=P)
            qcnt = [0]

            def q_epi(ps, c, t0, tn):
                kc = c // P
                qs = qst[(qcnt[0] // 4) % 2]
                ctx.op("act", lambda: nc.scalar.activation(qs[:, kc % 4, 0:tn], ps[:, 0:tn], AF.Copy, scale=SWA_HD ** -0.5),
                       reads=[ps], writes=[qs])
                qcnt[0] += 1
                if kc % 4 == 3:
                    ctx.dma("sp", QTv[:, kc - 3:kc + 1, t0:t0 + tn], qs[:, :, 0:tn], reads=[qs], writes=[("QT", kc, t0)])
            self.gemm_feat(res, xT, xT, KC, w_q, 0, 2048, tgs, q_epi)

            def k_epi(ps, c, t0, tn):
                fb = c // P
                ctx.op("act", lambda: nc.scalar.copy(kT[:, fb, P + t0:P + t0 + tn], ps[:, 0:tn]), reads=[ps], writes=[kT])
            self.gemm_feat(res, xT, xT, KC, w_kv, 0, 256, tgs, k_epi, nblk=256)

            def v_epi(ps, tt, c0, nb):
                ctx.op("act", lambda: nc.scalar.copy(vS[:, 1 + tt, :], ps[:, 0:nb]), reads=[ps], writes=[vS])
            self.gemm_tok(res, xT, xT, KC, w_kv, 256, 256, range(NT), v_epi, nblk=256)
            ctx.barrier()
        with contextlib.ExitStack() as st:
            if NSEG == 1:
                ctx.op("dve", lambda: nc.vector.memset(kT[:, :, 0:P], 0.0), writes=[kT])
                ctx.op("dve", lambda: nc.vector.memset(vS[:, 0, :], 0.0), writes=[vS])
            else:
                HCI = self.scr("swa_ci", [NSEG, P, 512])
                HCO = self.scr("swa_co", [NSEG, P, 512])
                hb = ctx.sb(st, "hb", [P, 512], F32)
                hm = [ctx.sb(st, "hm", [P, 512], F32) for _ in range(2)]
                ctx.op("dve", lambda: nc.vector.tensor_copy(hb[:, 0:256].rearrange("p (a b) -> p a b", a=2), kT[:, :, T:T + P]),
                       reads=[kT], writes=[hb])
                ctx.op("dve", lambda: nc.vector.tensor_copy(hb[:, 256:512], vS[:, NT, :]), reads=[vS], writes=[hb])
                for s in range(NSEG):
                    ctx.op("dve", lambda: nc.vector.tensor_scalar_mul(hm[s % 2][:], hb[:], self.own[:, s:s + 1]),
                           reads=[hb, self.own], writes=[hm[s % 2]])
                    ctx.dma("sp", HCI.ap()[s], hm[s % 2][:], reads=[hm[s % 2]], writes=[("HCI", s)])
                ctx.barrier()
                ctx.allreduce(cfg.groups, HCI.ap().rearrange("s p e -> (s p) e"), HCO.ap().rearrange("s p e -> (s p) e"),
                              writes=[("HCO",)])
                ctx.barrier()
                for s in range(NSEG):
                    ctx.dma("sp", hm[s % 2][:], HCO.ap()[s], writes=[hm[s % 2]])
                    if s == 0:
                        ctx.op("dve", lambda: nc.vector.tensor_scalar_mul(hb[:], hm[s % 2][:], self.halo_sel[:, s:s + 1]),
                               reads=[hm[s % 2], self.halo_sel], writes=[hb])
                    else:
                        ctx.op("dve", lambda: nc.vector.scalar_tensor_tensor(hb[:], hm[s % 2][:], self.halo_sel[:, s:s + 1],
                                                                             hb[:], ALU.mult, ALU.add),
                               reads=[hm[s % 2], self.halo_sel, hb], writes=[hb])
                ctx.op("dve", lambda: nc.vector.tensor_copy(kT[:, :, 0:P], hb[:, 0:256].rearrange("p (a b) -> p a b", a=2)),
                       reads=[hb], writes=[kT])
                ctx.op("dve", lambda: nc.vector.tensor_copy(vS[:, 0, :], hb[:, 256:512]), reads=[hb], writes=[vS])
            ctx.barrier()
        with contextlib.ExitStack() as st:
            qT = self.load_AT(st, "qT", QTs, KC, 0, T)
            ps_s = ctx.ps(st, "ps_s", [P, 8, 2 * P], F32)
            ps_t = ctx.ps(st, "ps_t", [P, 16, P], BF16)
            ps_o = ctx.ps(st, "ps_o", [P, 8, P], F32)
            s_sb = ctx.sb(st, "s_sb", [P, 8, 2 * P], F32)
            e_sb = ctx.sb(st, "e_sb", [P, 8, 2 * P], F32)
            p_sb = ctx.sb(st, "p_sb", [P, 8, 2 * P], BF16)
            pT = ctx.sb(st, "pT", [P, 16, P], BF16)
            mx = ctx.sb(st, "mx", [P, 8], F32)
            nmx = ctx.sb(st, "nmx", [P, 8], F32)
            rs = ctx.sb(st, "rs", [P, 8], F32)
            es = ctx.sb(st, "es", [P, 8], F32)
            G = 4 if NT % 4 == 0 else 2
            ost = [ctx.sb(st, "ost", [P, KC, G * P], BF16) for _ in range(2)]
            OTv = OT.ap().rearrange("(kc p) t -> p kc t", p=P)
            for n in range(NT):
                og = ost[(n // G) % 2]
                for pair in range(2):
                    for par in range(2):
                        hk = 2 * pair + par
                        po = par * 64
                        kc_k = hk // 2
                        for g in range(8):
                            ch = pair * 8 + g
                            ctx.op("pe", lambda: nc.tensor.matmul(ps_s[:, g, :], lhsT=qT[po:po + 64, ch, n * P:(n + 1) * P],
                                                                  rhs=kT[po:po + 64, kc_k, n * P:n * P + 2 * P],
                                                                  start=True, stop=True),
                                   reads=[qT, kT], writes=[ps_s])
                        ctx.op("dve", lambda: nc.vector.tensor_tensor(s_sb[:], ps_s[:], biasS[:, hk * 8:(hk + 1) * 8, :], ALU.add),
                               reads=[ps_s, biasS], writes=[s_sb])
                        if n == 0:
                            ctx.op("pool", lambda: nc.gpsimd.tensor_tensor(s_sb[:], s_sb[:],
                                                                           mfirst[:].unsqueeze(1).broadcast_to([P, 8, 2 * P]), ALU.add),
                                   reads=[s_sb, mfirst], writes=[s_sb])
                        ctx.op("dve", lambda: nc.vector.tensor_reduce(mx[:], s_sb[:], AX.X, ALU.max), reads=[s_sb], writes=[mx])
                        ctx.op("dve", lambda: nc.vector.tensor_tensor(mx[:], mx[:], sinkb[:, hk * 8:(hk + 1) * 8], ALU.max),
                               reads=[mx, sinkb], writes=[mx])
                        ctx.op("dve", lambda: nc.vector.tensor_scalar_mul(nmx[:], mx[:], -1.0), reads=[mx], writes=[nmx])
                        ctx.op("dve", lambda: nc.vector.memset(rs[:], 0.0), writes=[rs])
                        for g in range(8):
                            ctx.op("act", lambda: nc.scalar.activation(e_sb[:, g, :], s_sb[:, g, :], AF.Exp, bias=nmx[:, g:g + 1],
                                                                       scale=1.0, accum_out=rs[:, g:g + 1]),
                                   reads=[s_sb, nmx], writes=[e_sb, rs])
                        ctx.op("dve", lambda: nc.vector.tensor_tensor(es[:], sinkb[:, hk * 8:(hk + 1) * 8], mx[:], ALU.subtract),
                               reads=[sinkb, mx], writes=[es])
                        ctx.op("act", lambda: nc.scalar.activation(es[:], es[:], AF.Exp), reads=[es], writes=[es])
                        ctx.op("dve", lambda: nc.vector.tensor_tensor(rs[:], rs[:], es[:], ALU.add), reads=[rs, es], writes=[rs])
                        ctx.op("dve", lambda: nc.vector.reciprocal(rs[:], rs[:]), reads=[rs], writes=[rs])
                        ctx.op("pool", lambda: nc.gpsimd.tensor_tensor(p_sb[:], e_sb[:], rs[:].unsqueeze(2).broadcast_to([P, 8, 2 * P]),
                                                                       ALU.mult), reads=[e_sb, rs], writes=[p_sb])
                        for g in range(8):
                            for hf in range(2):
                                self.transpose_to(p_sb[:, g, hf * P:(hf + 1) * P], ps_t[:, g * 2 + hf, :], [p_sb], [ps_t])
                        ctx.op("act", lambda: nc.scalar.copy(pT[:], ps_t[:]), reads=[ps_t], writes=[pT])
                        for g in range(8):
                            for hf in range(2):
                                ctx.op("pe", lambda: nc.tensor.matmul(ps_o[po:po + 64, g, :], lhsT=vS[:, n + hf, hk * 64:(hk + 1) * 64],
                                                                      rhs=pT[:, g * 2 + hf, :], start=(hf == 0), stop=(hf == 1)),
                                       reads=[vS, pT], writes=[ps_o])
                    ctx.op("dve", lambda: nc.vector.tensor_copy(og[:, pair * 8:(pair + 1) * 8, (n % G) * P:(n % G + 1) * P], ps_o[:]),
                           reads=[ps_o], writes=[og])
                if n % G == G - 1:
                    g0 = (n // G) * G * P
                    ctx.dma("sp", OTv[:, :, g0:g0 + G * P], og[:], reads=[og], writes=[("OT", n)])
            ctx.barrier()
    with contextlib.ExitStack() as st:
        aT = self.load_AT(st, "oTa", OT, KC, 0, T)
        res = GemmRes(self, st, KC, 512, 3)
        epi = self.epi_resid(st, X, Z1)
        self.gemm_tok(res, aT, aT, KC, w_out, 0, D, range(NT), epi)
        ctx.barrier()


Prog.swa_layer = _swa_layer
RW_H, RW_HD = 32, 64
RW_EPS = 64e-5
RW_NHB = RW_H // 2


def _rwkv_consts(cfg, core, c):
    seg = core % cfg.NSEG
    f32 = np.float32
    s = np.arange(P)[:, None]
    t = np.arange(P)[None, :]
    c["rw_mus"] = (s < t).astype(f32)
    c["rw_mui"] = (s <= t).astype(f32)
    c["rw_mls"] = (s > t).astype(f32)
    c["rw_tri"] = (s <= t).astype(f32)
    c["rw_suf"] = (s > t).astype(f32)
    i2 = np.zeros((P, 64), f32)
    i2[np.arange(P), np.arange(P) % 64] = 1.0
    c["rw_i2"] = i2
    selm = np.zeros((P, cfg.NSEG), f32)
    selm[:, :seg] = 1.0
    c["rw_selm"] = selm
    c["rw_nselm"] = 1.0 - selm


def _rwkv_layer(self, layer, X, XT, Z1):
    ctx, nc, cfg = self.ctx, self.nc, self.cfg
    T, NT, NSEG = cfg.T, cfg.NT, cfg.NSEG
    j = layer // 3
    gi = lambda name, shape: self.inp("%s_%d" % (name, j), shape).ap()
    tw_ = lambda nm, fn, K_, N_, t_, kp_=P: self.tw("%s_t%d" % (nm, j), fn, K_, N_, t_, kp_)
    w_rkv = [tw_("rwkv_w_rkv%d" % i, (lambda inp, i=i: inp["rwkv_w_rkv"][j][i]), D, D, 256) for i in range(3)]
    w1 = tw_("rwkv_w1", lambda inp: inp["rwkv_w1"][j], D, 96, 96)
    w2 = tw_("rwkv_w2", lambda inp: inp["rwkv_w2"][j], 96, D, 256, 96)
    a1 = tw_("rwkv_a1", lambda inp: inp["rwkv_a1"][j], D, 96, 96)
    a2 = tw_("rwkv_a2", lambda inp: inp["rwkv_a2"][j], 96, D, 256, 96)
    g1 = tw_("rwkv_g1", lambda inp: inp["rwkv_g1"][j], D, 256, 256)
    g2 = tw_("rwkv_g2", lambda inp: inp["rwkv_g2"][j], 256, D, 256)
    w_out = tw_("rwkv_w_out", lambda inp: inp["rwkv_w_out"][j], D, D, 512)
    mix_ap = gi("rwkv_mix", [6, D]).rearrange("i (kc p) -> (i kc) p", p=P)
    Rs, Ks, Vs = self.scr("rw_R", [T, D]), self.scr("rw_K", [T, D]), self.scr("rw_V", [T, D])
    WLs, ALs, Gs = self.scr("rw_WL", [T, D]), self.scr("rw_AL", [T, D]), self.scr("rw_G", [T, D])
    Y0 = self.scr("rw_Y0", [T, D])
    BON = self.scr("rw_BON", [T, RW_H])
    YTR = self.scr("rw_YTR", [NT, P, RW_NHB, P], BF16)
    OGT = self.scr("rw_OGT", [D, T], BF16)
    TGW = min(512, T)
    tgs = [(t0, TGW) for t0 in range(0, T, TGW)]

    with contextlib.ExitStack() as st:
        psc = ctx.ps(st, "psc", [P, 512], F32)
        mixc = self.load_cols(st, "mixc", mix_ap, 6 * KC, psc)
        xT = self.load_AT(st, "xT", XT, KC, 0, T, pad=1)
        with contextlib.ExitStack() as sth:
            halo = self.halo_rows(sth, X, 1, "halo1")
            for kc in range(KC):
                ctx.op("pe", lambda: nc.tensor.matmul(psc[:, kc:kc + 1], lhsT=halo[0:1, kc * P:(kc + 1) * P],
                                                      rhs=self.identf[0:1, 0:1], start=True, stop=True),
                       reads=[halo, self.identf], writes=[psc])
            ctx.op("dve", lambda: nc.vector.tensor_copy(xT[:, :, 0:1], psc[:, 0:KC].unsqueeze(2)), reads=[psc], writes=[xT])
            ctx.barrier()
        xm = ctx.sb(st, "xm", [P, KC, T], BF16)
        dtmp = [ctx.sb(st, "dtmp", [P, T], F32) for _ in range(2)]
        hT = ctx.sb(st, "hT", [P, 2, T], BF16)
        res = GemmRes(self, st, KC, 256, 4)
        obuf = [ctx.sb(st, "obuf", [P, 512], F32) for _ in range(3)]
        ocnt = [0]

        def store_epi(dst):
            def epi(ps, tt, c0, nb):
                o = obuf[ocnt[0] % 3]
                ocnt[0] += 1
                ctx.op("act", lambda: nc.scalar.copy(o[:, 0:nb], ps[:, 0:nb]), reads=[ps], writes=[o])
                ctx.dma("sp", dst.ap()[tt * P:(tt + 1) * P, c0:c0 + nb], o[:, 0:nb], reads=[o], writes=[("o", id(dst), tt, c0)])
            return epi

        def build_mix(i):
            for kc in range(KC):
                d = dtmp[kc % 2]
                ctx.op("pool", lambda: nc.gpsimd.tensor_tensor(d[:], xT[:, kc, 0:T], xT[:, kc, 1:T + 1], ALU.subtract),
                       reads=[xT], writes=[d])
                ctx.op("dve", lambda: nc.vector.scalar_tensor_tensor(xm[:, kc, :], d[:], mixc[:, i * KC + kc:i * KC + kc + 1],
                                                                     xT[:, kc, 1:T + 1], ALU.mult, ALU.add),
                       reads=[d, mixc, xT], writes=[xm])

        def lora(i, wa, na, func, wb_, dst):
            build_mix(i)
            kcn2 = (na + P - 1) // P
            kp = min(P, na)

            def h_epi(ps, c, t0, tn):
                fw = min(P, na - c)
                ctx.op("act", lambda: nc.scalar.activation(hT[0:fw, c // P, t0:t0 + tn], ps[0:fw, 0:tn], func),
                       reads=[ps], writes=[hT])
            self.gemm_feat(res, xm, xm, KC, wa, 0, na, tgs, h_epi, nblk=256)
            self.gemm_tok(res, hT, hT, kcn2, wb_, 0, D, range(NT), store_epi(dst), kp=kp, nblk=256)

        build_mix(0)
        self.gemm_tok(res, xm, xm, KC, w_rkv[0], 0, D, range(NT), store_epi(Rs), nblk=256)
        build_mix(2)
        self.gemm_tok(res, xm, xm, KC, w_rkv[1], 0, D, range(NT), store_epi(Ks), nblk=256)
        build_mix(3)
        self.gemm_tok(res, xm, xm, KC, w_rkv[2], 0, D, range(NT), store_epi(Vs), nblk=256)
        lora(1, w1, 96, AF.Tanh, w2, WLs)
        lora(4, a1, 96, AF.Identity, a2, ALs)
        lora(5, g1, 256, AF.Sigmoid, g2, Gs)
        ctx.barrier()
    if cfg.stop == "rw1":
        return

    SXs = self.scr("rw_SX", [P, RW_NHB, P])
    with contextlib.ExitStack() as st:
        def cload(name, shape, dtype=F32):
            t_ = ctx.sb(st, name, shape, dtype)
            ctx.dma("sp", t_[:], self.inp(name, shape, dtype).ap(), writes=[t_])
            return t_
        mus, mui, mls = cload("rw_mus", [P, P]), cload("rw_mui", [P, P]), cload("rw_mls", [P, P])
        tri, suft = cload("rw_tri", [P, P]), cload("rw_suf", [P, P])
        i2 = cload("rw_i2", [P, 64])
        ones = ctx.sb(st, "ones", [P, 1], F32)
        ctx.op("dve", lambda: nc.vector.memset(ones[:], 1.0), writes=[ones])
        w0b = self.bcast_rows(st, "w0b", gi("rwkv_w0", [D]), D)
        a0b = self.bcast_rows(st, "a0b", gi("rwkv_a0", [D]), D)
        kkb = self.bcast_rows(st, "kkb", gi("rwkv_k_k", [D]), D)
        kab = self.bcast_rows(st, "kab", gi("rwkv_k_a", [D]), D)
        rkb = self.bcast_rows(st, "rkb", gi("rwkv_r_k", [RW_H, RW_HD]).rearrange("h d -> (h d)"), D)
        A = ctx.sb(st, "A", [P, D], F32)
        B = ctx.sb(st, "B", [P, D], F32)
        Dw = ctx.sb(st, "Dw", [P, D], F32)
        Ea = ctx.sb(st, "Ea", [P, D], F32)
        Fk = ctx.sb(st, "Fk", [P, D], F32)
        T1 = ctx.sb(st, "T1", [P, D], F32)
        ET = [ctx.sb(st, "ET", [P, 512], F32) for _ in range(4)]
        tok = [ctx.sb(st, "tokb", [P, D], BF16) for _ in range(4)]
        bbk = ctx.sb(st, "bbk", [P, 2, D], BF16)
        vbx = ctx.sb(st, "vbx", [P, RW_H, P], BF16)
        ctx.op("pool", lambda: nc.gpsimd.memset(vbx[:], 0.0), writes=[vbx])
        CM = ctx.sb(st, "CM", [P, RW_NHB, 4, P], BF16)
        ytile = ctx.sb(st, "ytile", [P, D], F32)
        T2 = ytile
        ytr = ctx.sb(st, "ytr", [P, RW_NHB, P], BF16)
        ss = ctx.sb(st, "ss", [P, RW_H], F32)
        bon = ctx.sb(st, "bon", [P, RW_H], F32)
        dectot = ctx.sb(st, "dectot", [P, RW_NHB], F32)
        SX = ctx.sb(st, "SX", [P, RW_NHB, P], F32)
        SXb = ctx.sb(st, "SXb", [P, RW_NHB, P], BF16)
        GA = [ctx.sb(st, "GA", [P, 2, 2, P], BF16) for _ in range(2)]
        Brb = ctx.sb(st, "Brb", [P, 2, P], BF16)
        Aak = ctx.sb(st, "Aak", [P, 2, P], BF16)
        Brk = ctx.sb(st, "Brk", [P, 2, P], BF16)
        Tt = [ctx.sb(st, "Tt", [P, 2, P], BF16) for _ in range(2)]
        Wb = ctx.sb(st, "Wb", [P, 2, P], BF16)
        Ub = ctx.sb(st, "Ub", [P, 2, P], BF16)
        pb = [ctx.ps(st, "pb", [P, 512], F32) for _ in range(7)]
        ptr = ctx.ps(st, "ptr", [P, 8, P], BF16)
        ctx.op("dve", lambda: nc.vector.memset(SX[:], 0.0), writes=[SX])
        ctx.op("dve", lambda: nc.vector.tensor_copy(SX[:, :, 64:128], i2[:].unsqueeze(1).broadcast_to([P, RW_NHB, 64])),
               reads=[i2, SX], writes=[SX])
        ctx.op("pool", lambda: nc.gpsimd.tensor_copy(SXb[:], SX[:]), reads=[SX], writes=[SXb])
        v3 = lambda t_: t_[:].rearrange("p (h d) -> p h d", d=RW_HD)
        bc3 = lambda small: small[:].unsqueeze(2).broadcast_to([P, RW_H, RW_HD])
        for n in range(NT):
            rows = slice(n * P, (n + 1) * P)
            ctx.dma("sp", A[:], Rs.ap()[rows, :], writes=[A])
            ctx.dma("sp", B[:], Ks.ap()[rows, :], writes=[B])
            ctx.dma("sp", T1[:], Vs.ap()[rows, :], writes=[T1])
            ctx.op("act", lambda: nc.scalar.copy(vbx[:, :, 0:64], v3(T1)), reads=[T1], writes=[vbx])
            ctx.dma("sp", Dw[:], WLs.ap()[rows, :], writes=[Dw])
            ctx.dma("sp", Ea[:], ALs.ap()[rows, :], writes=[Ea])
            ctx.op("dve", lambda: nc.vector.tensor_tensor(Dw[:], Dw[:], w0b[:], ALU.add), reads=[Dw, w0b], writes=[Dw])
            ctx.op("act", lambda: nc.scalar.activation(Dw[:], Dw[:], AF.Sigmoid), reads=[Dw], writes=[Dw])
            ctx.op("dve", lambda: nc.vector.tensor_scalar_mul(Dw[:], Dw[:], -math.exp(-0.5)), reads=[Dw], writes=[Dw])
            ctx.op("dve", lambda: nc.vector.tensor_tensor(Ea[:], Ea[:], a0b[:], ALU.add), reads=[Ea, a0b], writes=[Ea])
            ctx.op("act", lambda: nc.scalar.activation(Ea[:], Ea[:], AF.Sigmoid), reads=[Ea], writes=[Ea])
            ctx.op("pool", lambda: nc.gpsimd.tensor_tensor(Fk[:], B[:], kkb[:], ALU.mult), reads=[B, kkb], writes=[Fk])
            ctx.op("pool", lambda: nc.gpsimd.tensor_tensor(T2[:], Fk[:], Fk[:], ALU.mult), reads=[Fk], writes=[T2])
            ctx.op("dve", lambda: nc.vector.tensor_reduce(ss[:], v3(T2), AX.X, ALU.add), reads=[T2], writes=[ss])
            ctx.op("act", lambda: nc.scalar.activation(ss[:], ss[:], AF.Sqrt), reads=[ss], writes=[ss])
            ctx.op("dve", lambda: nc.vector.tensor_scalar_max(ss[:], ss[:], 1e-12), reads=[ss], writes=[ss])
            ctx.op("dve", lambda: nc.vector.reciprocal(ss[:], ss[:]), reads=[ss], writes=[ss])
            ctx.op("dve", lambda: nc.vector.tensor_tensor(v3(Fk), v3(Fk), bc3(ss), ALU.mult), reads=[Fk, ss], writes=[Fk])
            ctx.op("dve", lambda: nc.vector.scalar_tensor_tensor(T1[:], Ea[:], -1.0, kab[:], ALU.add, ALU.mult),
                   reads=[Ea, kab], writes=[T1])
            ctx.op("pool", lambda: nc.gpsimd.tensor_tensor(T1[:], T1[:], B[:], ALU.mult), reads=[T1, B], writes=[T1])
            ctx.op("pool", lambda: nc.gpsimd.tensor_tensor(B[:], B[:], T1[:], ALU.add), reads=[T1, B], writes=[B])
            ctx.op("pool", lambda: nc.gpsimd.tensor_tensor(T2[:], A[:], B[:], ALU.mult), reads=[A, B, T2], writes=[T2])
            ctx.op("dve", lambda: nc.vector.tensor_tensor(T2[:], T2[:], rkb[:], ALU.mult), reads=[T2, rkb], writes=[T2])
            ctx.op("dve", lambda: nc.vector.tensor_reduce(bon[:], v3(T2), AX.X, ALU.add), reads=[T2], writes=[bon])
            ctx.dma("sp", BON.ap()[rows, :], bon[:], reads=[bon], writes=[("BON", n)])
            ctx.op("dve", lambda: nc.vector.tensor_tensor(T1[:], Fk[:], Ea[:], ALU.mult), reads=[Fk, Ea, T1], writes=[T1])
            for hb in range(RW_NHB):
                ctx.op("pe", lambda: nc.tensor.matmul(pb[0][:, hb:hb + 1], lhsT=Dw[:, hb * P:(hb + 1) * P], rhs=ones[:, 0:1],
                                                      start=True, stop=True), reads=[Dw, ones], writes=[pb[0]])
            ctx.op("act", lambda: nc.scalar.activation(dectot[:], pb[0][:, 0:RW_NHB], AF.Exp), reads=[pb[0]], writes=[dectot])
            for cb in range(4):
                cs = slice(cb * 512, (cb + 1) * 512)
                pcum, psuf = pb[1 + (cb % 2) * 2], pb[2 + (cb % 2) * 2]
                ctx.op("pe", lambda: nc.tensor.matmul(pcum[:], lhsT=tri[:], rhs=Dw[:, cs], start=True, stop=True),
                       reads=[tri, Dw], writes=[pcum])
                ctx.op("pe", lambda: nc.tensor.matmul(psuf[:], lhsT=suft[:], rhs=Dw[:, cs], start=True, stop=True),
                       reads=[suft, Dw], writes=[psuf])
                ctx.op("act", lambda: nc.scalar.activation(ET[0][:], pcum[:], AF.Exp), reads=[pcum], writes=[ET[0]])
                ctx.op("pool", lambda: nc.gpsimd.tensor_tensor(tok[3][:, cs], A[:, cs], ET[0][:], ALU.mult),
                       reads=[A, ET[0]], writes=[tok[3]])
                ctx.op("act", lambda: nc.scalar.activation(ET[1][:], pcum[:], AF.Exp, scale=-1.0), reads=[pcum], writes=[ET[1]])
                ctx.op("dve", lambda: nc.vector.tensor_tensor(tok[0][:, cs], T1[:, cs], ET[1][:], ALU.mult),
                       reads=[T1, ET[1]], writes=[tok[0]])
                ctx.op("pool", lambda: nc.gpsimd.tensor_tensor(tok[1][:, cs], B[:, cs], ET[1][:], ALU.mult),
                       reads=[B, ET[1]], writes=[tok[1]])
                ctx.op("dve", lambda: nc.vector.tensor_tensor(ET[2][:], pcum[:], Dw[:, cs], ALU.subtract),
                       reads=[pcum, Dw], writes=[ET[2]])
                ctx.op("act", lambda: nc.scalar.activation(ET[2][:], ET[2][:], AF.Exp), reads=[ET[2]], writes=[ET[2]])
                ctx.op("dve", lambda: nc.vector.scalar_tensor_tensor(tok[2][:, cs], Fk[:, cs], -1.0, ET[2][:], ALU.mult, ALU.mult),
                       reads=[Fk, ET[2]], writes=[tok[2]])
                ctx.op("act", lambda: nc.scalar.activation(ET[3][:], psuf[:], AF.Exp), reads=[psuf], writes=[ET[3]])
                ctx.op("dve", lambda: nc.vector.tensor_tensor(bbk[:, 0, cs], T1[:, cs], ET[3][:], ALU.mult),
                       reads=[T1, ET[3]], writes=[bbk])
                ctx.op("pool", lambda: nc.gpsimd.tensor_tensor(bbk[:, 1, cs], B[:, cs], ET[3][:], ALU.mult),
                       reads=[B, ET[3]], writes=[bbk])
            for kind in range(4):
                for half in range(2):
                    for jj in range(8):
                        hb = half * 8 + jj
                        self.transpose_to(tok[kind][:, hb * P:(hb + 1) * P], ptr[:, jj, :], [tok[kind]], [ptr])
                    if (kind + half) % 2 == 0:
                        ctx.op("act", lambda: nc.scalar.copy(CM[:, half * 8:(half + 1) * 8, kind, :], ptr[:]), reads=[ptr], writes=[CM])
                    else:
                        ctx.op("dve", lambda: nc.vector.tensor_copy(CM[:, half * 8:(half + 1) * 8, kind, :], ptr[:]), reads=[ptr], writes=[CM])
            for hb in range(RW_NHB if cfg.stop != "rw2p" else 0):
                v2 = lambda ap_, k: ap_.rearrange("p (h k) -> p h k", k=k)
                for hi, po in enumerate((0, 64)):
                    rhs_ar = CM[po:po + 64, hb, 2:4, :].rearrange("p k t -> p (k t)")
                    qb = pb[hi]
                    ctx.op("pe", lambda: nc.tensor.matmul(qb[:, 0:256], lhsT=CM[po:po + 64, hb, 0, :], rhs=rhs_ar,
                                                          start=True, stop=True), reads=[CM], writes=[qb])
                    ctx.op("pe", lambda: nc.tensor.matmul(qb[:, 256:512], lhsT=CM[po:po + 64, hb, 1, :], rhs=rhs_ar,
                                                          start=True, stop=True), reads=[CM], writes=[qb])
                    ctx.op("pe", lambda: nc.tensor.matmul(pb[2 + hi][:, 0:P], lhsT=CM[po:po + 64, hb, 2, :],
                                                          rhs=CM[po:po + 64, hb, 0, :], start=True, stop=True),
                           reads=[CM], writes=[pb[2 + hi]])
                g0 = GA[0]
                for hi in range(2):
                    qb = pb[hi]
                    ctx.op("dve", lambda: nc.vector.tensor_tensor(g0[:, hi, 0, :], qb[:, 0:P], mus[:], ALU.mult),
                           reads=[qb, mus], writes=[g0])
                    ctx.op("dve", lambda: nc.vector.tensor_tensor(Brb[:, hi, :], qb[:, P:2 * P], mui[:], ALU.mult),
                           reads=[qb, mui], writes=[Brb])
                    ctx.op("dve", lambda: nc.vector.tensor_tensor(Aak[:, hi, :], qb[:, 2 * P:3 * P], mus[:], ALU.mult),
                           reads=[qb, mus], writes=[Aak])
                    ctx.op("dve", lambda: nc.vector.tensor_tensor(Brk[:, hi, :], qb[:, 3 * P:4 * P], mui[:], ALU.mult),
                           reads=[qb, mui], writes=[Brk])
                    ctx.op("dve", lambda: nc.vector.tensor_tensor(g0[:, hi, 1, :], pb[2 + hi][:, 0:P], mls[:], ALU.mult),
                           reads=[pb[2 + hi], mls], writes=[g0])
                    ctx.op("pool", lambda: nc.gpsimd.tensor_tensor(Tt[0][:, hi, :], g0[:, hi, 0, :], self.ident[:], ALU.add),
                           reads=[g0, self.ident], writes=[Tt[0]])
                tcur = 0
                if cfg.stop == "rw2q":
                    continue
                for lvl in range(1, 7):
                    gc, gn = GA[(lvl - 1) % 2], GA[lvl % 2]
                    for hi in range(2):
                        if lvl < 6:
                            ctx.op("pe", lambda: nc.tensor.matmul(pb[3][:, (2 * hi) * P:(2 * hi + 1) * P], lhsT=gc[:, hi, 1, :],
                                                                  rhs=gc[:, hi, 0, :], start=True, stop=True),
                                   reads=[gc], writes=[pb[3]])
                        ctx.op("pe", lambda: nc.tensor.matmul(pb[3][:, (2 * hi + 1) * P:(2 * hi + 2) * P], lhsT=gc[:, hi, 0, :],
                                                              rhs=gc[:, hi, 1, :], start=True, stop=True),
                               reads=[gc], writes=[pb[3]])
                    if lvl < 6:
                        ctx.op("act", lambda: nc.scalar.copy(gn[:].rearrange("p h k t -> p (h k t)"), pb[3][:]),
                               reads=[pb[3]], writes=[gn])
                    else:
                        ctx.op("act", lambda: nc.scalar.copy(gn[:, :, 1, :], v2(pb[3][:], 2 * P)[:, :, P:2 * P]),
                               reads=[pb[3]], writes=[gn])
                    for hi in range(2):
                        ctx.op("pe", lambda: nc.tensor.matmul(pb[4][:, hi * P:(hi + 1) * P], lhsT=gn[:, hi, 1, :],
                                                              rhs=Tt[tcur][:, hi, :], start=True, stop=True),
                               reads=[gn, Tt[tcur]], writes=[pb[4]])
                    ctx.op("dve", lambda: nc.vector.tensor_tensor(Tt[1 - tcur][:], v2(pb[4][:, 0:2 * P], P), Tt[tcur][:], ALU.add),
                           reads=[pb[4], Tt[tcur]], writes=[Tt[1 - tcur]])
                    tcur = 1 - tcur
                TT = Tt[tcur]
                if cfg.stop == "rw2i":
                    continue
                for hi, po in enumerate((0, 64)):
                    h = 2 * hb + hi
                    wps = pb[5 + hi]
                    ctx.op("pe", lambda: nc.tensor.matmul(wps[:, 0:P], lhsT=CM[po:po + 64, hb, 2, :],
                                                          rhs=SXb[po:po + 64, hb, :], start=True, stop=False),
                           reads=[CM, SXb], writes=[wps])
                    ctx.op("pe", lambda: nc.tensor.matmul(wps[:, 0:P], lhsT=Aak[:, hi, :], rhs=vbx[:, h, :],
                                                          start=False, stop=True), reads=[Aak, vbx], writes=[wps])
                    ctx.op("act", lambda: nc.scalar.copy(Wb[:, hi, :], wps[:, 0:P]), reads=[wps], writes=[Wb])
                for hi in range(2):
                    ctx.op("pe", lambda: nc.tensor.matmul(pb[5][:, 2 * P + hi * P:2 * P + (hi + 1) * P], lhsT=TT[:, hi, :], rhs=Wb[:, hi, :],
                                                          start=True, stop=True), reads=[TT, Wb], writes=[pb[5]])
                ctx.op("dve", lambda: nc.vector.tensor_copy(Ub[:].rearrange("p h t -> p (h t)"), pb[5][:, 2 * P:4 * P]),
                       reads=[pb[5]], writes=[Ub])
                for hi, po in enumerate((0, 64)):
                    h = 2 * hb + hi
                    yb = pb[hi]
                    yo = yb[:, 0:64]
                    ctx.op("pe", lambda: nc.tensor.matmul(yo, lhsT=CM[po:po + 64, hb, 3, :], rhs=SXb[po:po + 64, hb, 0:64],
                                                          start=True, stop=False), reads=[CM, SXb], writes=[yb])
                    ctx.op("pe", lambda: nc.tensor.matmul(yo, lhsT=Brb[:, hi, :], rhs=Ub[:, hi, 0:64], start=False, stop=False),
                           reads=[Brb, Ub], writes=[yb])
                    ctx.op("pe", lambda: nc.tensor.matmul(yo, lhsT=Brk[:, hi, :], rhs=vbx[:, h, 0:64], start=False, stop=True),
                           reads=[Brk, vbx], writes=[yb])
                    if NSEG > 1:
                        to = yb[po:po + 64, P:2 * P]
                        ctx.op("pe", lambda: nc.tensor.matmul(to, lhsT=SXb[po:po + 64, hb, 64:128], rhs=CM[po:po + 64, hb, 3, :],
                                                              start=True, stop=False), reads=[CM, SXb], writes=[yb])
                        ctx.op("pe", lambda: nc.tensor.matmul(to, lhsT=Ub[:, hi, 64:128], rhs=Brb[:, hi, :], start=False, stop=True),
                               reads=[Brb, Ub], writes=[yb])
                    so = pb[6][po:po + 64, 2 * P:3 * P]
                    ctx.op("pe", lambda: nc.tensor.matmul(so, lhsT=bbk[:, 0, h * 64:(h + 1) * 64], rhs=Ub[:, hi, :], start=True, stop=False),
                           reads=[bbk, Ub], writes=[pb[6]])
                    ctx.op("pe", lambda: nc.tensor.matmul(so, lhsT=bbk[:, 1, h * 64:(h + 1) * 64], rhs=vbx[:, h, :], start=False, stop=True),
                           reads=[bbk, vbx], writes=[pb[6]])
                    ctx.op("act", lambda: nc.scalar.copy(ytile[:, hb * P + hi * 64:hb * P + (hi + 1) * 64], yb[:, 0:64]),
                           reads=[yb], writes=[ytile])
                    if NSEG > 1:
                        ctx.op("act", lambda: nc.scalar.copy(ytr[po:po + 64, hb, :], yb[po:po + 64, P:2 * P]), reads=[yb], writes=[ytr])
                ctx.op("dve", lambda: nc.vector.scalar_tensor_tensor(SX[:, hb, :], SX[:, hb, :], dectot[:, hb:hb + 1],
                                                                     pb[6][:, 2 * P:3 * P], ALU.mult, ALU.add),
                       reads=[SX, dectot, pb[6]], writes=[SX])
                ctx.op("pool", lambda: nc.gpsimd.tensor_copy(SXb[:, hb, :], SX[:, hb, :]), reads=[SX], writes=[SXb])
            ctx.dma("sp", Y0.ap()[rows, :], ytile[:], reads=[ytile], writes=[("Y0", n)])
            if NSEG > 1:
                ctx.dma("sp", YTR.ap()[n], ytr[:], reads=[ytr], writes=[("YTR", n)])
        if NSEG > 1:
            ctx.dma("sp", SXs.ap(), SX[:], reads=[SX], writes=[("SXs",)])
        ctx.barrier()

    if cfg.stop in ("rw2", "rw2p", "rw2q", "rw2i"):
        return
    S0b = ctx.sb(self.top, "rw_S0b_%d" % layer, [P, RW_NHB, 64], BF16)
    if NSEG > 1:
        NSL = NSEG - 1
        CI = self.scr("rw_ci", [NSL, P, RW_NHB * P])
        CO = self.scr("rw_co", [NSL, P, RW_NHB * P])
        with contextlib.ExitStack() as st:
            sx = ctx.sb(st, "sx", [P, RW_NHB * P], F32)
            sm = [ctx.sb(st, "sm", [P, RW_NHB * P], F32) for _ in range(2)]
            ctx.dma("sp", sx[:], SXs.ap().rearrange("p h k -> p (h k)"), writes=[sx])
            for s in range(NSL):
                ctx.op("dve", lambda: nc.vector.tensor_scalar_mul(sm[s % 2][:], sx[:], self.own[:, s:s + 1]),
                       reads=[sx, self.own], writes=[sm[s % 2]])
                ctx.dma("sp", CI.ap()[s], sm[s % 2][:], reads=[sm[s % 2]], writes=[("CI", s)])
            ctx.barrier()
            ctx.allreduce(cfg.groups, CI.ap().rearrange("s p e -> (s p) e"), CO.ap().rearrange("s p e -> (s p) e"),
                          writes=[("CO",)])
            ctx.barrier()
            selm = ctx.sb(st, "selm", [P, NSEG], F32)
            nselm = ctx.sb(st, "nselm", [P, NSEG], F32)
            ctx.dma("sp", selm[:], self.inp("rw_selm", [P, NSEG]).ap(), writes=[selm])
            ctx.dma("sp", nselm[:], self.inp("rw_nselm", [P, NSEG]).ap(), writes=[nselm])
            i2 = ctx.sb(st, "i2", [P, 64], F32)
            ctx.dma("sp", i2[:], self.inp("rw_i2", [P, 64]).ap(), writes=[i2])
            S0 = ctx.sb(st, "S0", [P, RW_NHB, 64], F32)
            ctx.op("dve", lambda: nc.vector.memset(S0[:], 0.0), writes=[S0])
            Mp = ctx.sb(st, "Mp", [P, RW_NHB, 64], F32)
            Lp = ctx.sb(st, "Lp", [P, RW_NHB, 64], F32)
            MT = ctx.sb(st, "MT", [P, RW_NHB, 64], F32)
            pm = [ctx.ps(st, "pm", [P, 8, 64], F32) for _ in range(2)]
            for s in range(NSL):
                slot = sm[s % 2]
                ctx.dma("sp", slot[:], CO.ap()[s], writes=[slot])
                sv = slot[:].rearrange("p (h k) -> p h k", k=P)
                ctx.op("dve", lambda: nc.vector.tensor_scalar_mul(Lp[:], sv[:, :, 0:64], selm[:, s:s + 1]),
                       reads=[slot, selm], writes=[Lp])
                ctx.op("dve", lambda: nc.vector.tensor_scalar_mul(Mp[:], sv[:, :, 64:128], selm[:, s:s + 1]),
                       reads=[slot, selm], writes=[Mp])
                ctx.op("dve", lambda: nc.vector.scalar_tensor_tensor(Mp[:], i2[:].unsqueeze(1).broadcast_to([P, RW_NHB, 64]),
                                                                     nselm[:, s:s + 1], Mp[:], ALU.mult, ALU.add),
                       reads=[i2, nselm, Mp], writes=[Mp])
                for half in range(2):
                    for jj in range(8):
                        hb = half * 8 + jj
                        for hi, po in enumerate((0, 64)):
                            ctx.op("pe", lambda: nc.tensor.matmul(pm[hi][po:po + 64, jj, :], lhsT=Mp[po:po + 64, hb, :],
                                                                  rhs=self.identf[po:po + 64, po:po + 64], start=True, stop=True),
                                   reads=[Mp, self.identf], writes=[pm[hi]])
                    for hi, po in enumerate((0, 64)):
                        ctx.op("act", lambda: nc.scalar.copy(MT[po:po + 64, half * 8:(half + 1) * 8, :], pm[hi][po:po + 64]),
                               reads=[pm[hi]], writes=[MT])
                for half in range(2):
                    for jj in range(8):
                        hb = half * 8 + jj
                        for hi, po in enumerate((0, 64)):
                            ctx.op("pe", lambda: nc.tensor.matmul(pm[hi][po:po + 64, jj, :], lhsT=MT[po:po + 64, hb, :],
                                                                  rhs=S0[po:po + 64, hb, :], start=True, stop=True),
                                   reads=[MT, S0], writes=[pm[hi]])
                    for hi, po in enumerate((0, 64)):
                        ctx.op("dve", lambda: nc.vector.tensor_tensor(S0[po:po + 64, half * 8:(half + 1) * 8, :], pm[hi][po:po + 64],
                                                                      Lp[po:po + 64, half * 8:(half + 1) * 8, :], ALU.add),
                               reads=[pm[hi], Lp, S0], writes=[S0])
            ctx.op("act", lambda: nc.scalar.copy(S0b[:], S0[:]), reads=[S0], writes=[S0b])
            ctx.barrier()

    if cfg.stop == "rw3":
        return
    with contextlib.ExitStack() as st:
        gnb = self.bcast_rows(st, "gnb", gi("rwkv_gn_gain", [D]), D)
        gbb = self.bcast_rows(st, "gbb", gi("rwkv_gn_bias", [D]), D)
        y = ctx.sb(st, "y", [P, D], F32)
        vv = ctx.sb(st, "vv", [P, D], F32)
        gg = ctx.sb(st, "gg", [P, D], F32)
        sq = ctx.sb(st, "sq", [P, D], F32)
        bon = ctx.sb(st, "bon", [P, RW_H], F32)
        s1 = ctx.sb(st, "s1", [P, RW_H], F32)
        s2 = ctx.sb(st, "s2", [P, RW_H], F32)
        ytr = ctx.sb(st, "ytr", [P, RW_NHB, P], BF16)
        ogb = [ctx.sb(st, "ogb", [P, D], BF16) for _ in range(2)]
        G = 4 if NT % 4 == 0 else 2
        stg = [ctx.sb(st, "stg", [P, KC, G * P], BF16) for _ in range(2)]
        pst = [[ctx.ps(st, "pst", [P, 8 * P], BF16) for _ in range(2)] for _ in range(2)]
        pc = [ctx.ps(st, "pc", [P, 512], F32) for _ in range(4)]
        OGTv = OGT.ap().rearrange("(kc p) t -> p kc t", p=P)
        v3 = lambda t_: t_[:].rearrange("p (h d) -> p h d", d=RW_HD)
        bc3 = lambda small: small[:].unsqueeze(2).broadcast_to([P, RW_H, RW_HD])
        for n in range(NT):
            rows = slice(n * P, (n + 1) * P)
            ctx.dma("sp", y[:], Y0.ap()[rows, :], writes=[y])
            ctx.dma("sp", vv[:], Vs.ap()[rows, :], writes=[vv])
            ctx.dma("sp", gg[:], Gs.ap()[rows, :], writes=[gg])
            ctx.dma("sp", bon[:], BON.ap()[rows, :], writes=[bon])
            if NSEG > 1:
                ctx.dma("sp", ytr[:], YTR.ap()[n], writes=[ytr])
                for q4 in range(4):
                    for jj in range(4):
                        hb = q4 * 4 + jj
                        for hi, po in enumerate((0, 64)):
                            pcc = pc[2 * (q4 % 2) + hi]
                            ctx.op("pe", lambda: nc.tensor.matmul(pcc[:, jj * 64:(jj + 1) * 64],
                                                                  lhsT=ytr[po:po + 64, hb, :], rhs=S0b[po:po + 64, hb, :],
                                                                  start=True, stop=True), reads=[ytr, S0b], writes=[pcc])
                    for hi in range(2):
                        pcc = pc[2 * (q4 % 2) + hi]
                        yv = y[:, q4 * 512:(q4 + 1) * 512].rearrange("p (j k) -> p j k", k=P)[:, :, hi * 64:(hi + 1) * 64]
                        ctx.op("dve", lambda: nc.vector.tensor_tensor(yv, yv, pcc[:, 0:256].rearrange("p (j k) -> p j k", k=64), ALU.add),
                               reads=[pcc, y], writes=[y])
            ctx.op("dve", lambda: nc.vector.tensor_reduce(s1[:], v3(y), AX.X, ALU.add), reads=[y], writes=[s1])
            ctx.op("dve", lambda: nc.vector.tensor_scalar_mul(s1[:], s1[:], 1.0 / RW_HD), reads=[s1], writes=[s1])
            ctx.op("dve", lambda: nc.vector.tensor_tensor(v3(y), v3(y), bc3(s1), ALU.subtract), reads=[y, s1], writes=[y])
            ctx.op("pool", lambda: nc.gpsimd.tensor_tensor(sq[:], y[:], y[:], ALU.mult), reads=[y], writes=[sq])
            ctx.op("dve", lambda: nc.vector.tensor_reduce(s2[:], v3(sq), AX.X, ALU.add), reads=[sq], writes=[s2])
            ctx.op("dve", lambda: nc.vector.tensor_scalar(s2[:], s2[:], 1.0 / RW_HD, RW_EPS, ALU.mult, ALU.add), reads=[s2], writes=[s2])
            ctx.op("act", lambda: nc.scalar.activation(s2[:], s2[:], AF.Sqrt), reads=[s2], writes=[s2])
            ctx.op("dve", lambda: nc.vector.reciprocal(s2[:], s2[:]), reads=[s2], writes=[s2])
            ctx.op("dve", lambda: nc.vector.tensor_tensor(v3(y), v3(y), bc3(s2), ALU.mult), reads=[y, s2], writes=[y])
            ctx.op("pool", lambda: nc.gpsimd.tensor_tensor(y[:], y[:], gnb[:], ALU.mult), reads=[y, gnb], writes=[y])
            ctx.op("pool", lambda: nc.gpsimd.tensor_tensor(y[:], y[:], gbb[:], ALU.add), reads=[y, gbb], writes=[y])
            ctx.op("dve", lambda: nc.vector.tensor_tensor(v3(vv), v3(vv), bc3(bon), ALU.mult), reads=[vv, bon], writes=[vv])
            ctx.op("pool", lambda: nc.gpsimd.tensor_tensor(y[:], y[:], vv[:], ALU.add), reads=[y, vv], writes=[y])
            ob = ogb[n % 2]
            ctx.op("dve", lambda: nc.vector.tensor_tensor(ob[:], y[:], gg[:], ALU.mult), reads=[y, gg], writes=[ob])
            g_, gi_ = divmod(n, G)
            self.xt_emit_tile(ob, ob, stg[g_ % 2], stg[g_ % 2], gi_ * P, pst[n % 2])
            if gi_ == G - 1:
                ctx.dma("sp", OGTv[:, :, g_ * G * P:(g_ + 1) * G * P], stg[g_ % 2][:], reads=[stg[g_ % 2]], writes=[("OGT", g_)])
        ctx.barrier()

    if cfg.stop == "rw4":
        return
    with contextlib.ExitStack() as st:
        aT = self.load_AT(st, "oTa", OGT, KC, 0, T)
        res = GemmRes(self, st, KC, 512, 3)
        epi = self.epi_resid(st, X, Z1)
        self.gemm_tok(res, aT, aT, KC, w_out, 0, D, range(NT), epi)
        ctx.barrier()


Prog.rwkv_layer = _rwkv_layer
def _const_inputs(cfg, core):
    T, NSEG = cfg.T, cfg.NSEG
    seg = core % NSEG
    f32 = np.float32
    own = np.zeros((P, NSEG), f32)
    own[:, seg] = 1
    hs = np.zeros((P, NSEG), f32)
    if seg > 0:
        hs[:, seg - 1] = 1
    c = {"ident": np.eye(P, dtype=f32).astype(ml_dtypes.bfloat16), "identf": np.eye(P, dtype=f32),
         "own": own, "halo_sel": hs}
    inv = (1.0 / (10000.0 ** (np.arange(0, RET_DK, 2, dtype=f32) / f32(RET_DK)))).astype(f32)
    pos = (seg * T + np.arange(T)).astype(f32)
    ang = (pos[None, :] * inv[:, None]).astype(f32)
    c["rope_cos"] = np.cos(ang).astype(f32)
    c["rope_sin"] = np.sin(ang).astype(f32)
    gam = np.array(RET_GAMMA, np.float64)
    idx = np.arange(P, dtype=np.float64)
    c["ret_kdec"] = (gam[None, :] ** (P - 1 - idx[:, None])).astype(f32)
    diff = idx[None, :] - idx[:, None]
    m = np.where(diff[:, None, :] >= 0, gam[None, :, None] ** np.maximum(diff[:, None, :], 0), 0.0)
    c["ret_maskT"] = m.astype(f32)
    c["ret_qdec"] = np.broadcast_to((gam[:, None] ** (idx[None, :] + 1.0))[None], (P, RET_H, P)).astype(f32).copy()
    coef = np.zeros((P, NSEG, RET_H), f32)
    for s in range(seg):
        coef[:, s, :] = (gam ** (T * (seg - s - 1)))[None, :]
    c["ret_coef"] = coef
    _swa_consts(cfg, core, c)
    _rwkv_consts(cfg, core, c)
    return c


def make_in_maps(cfg, prog, inputs):
    T, NSEG = cfg.T, cfg.NSEG
    maps = []
    shared = {}
    per_layer = ["ret_w_in", "ret_w_out", "ret_gn_gain", "swa_w_qkv", "swa_sinks", "swa_w_out",
                 "rwkv_mix", "rwkv_w_rkv", "rwkv_w0", "rwkv_w1", "rwkv_w2", "rwkv_a0", "rwkv_a1", "rwkv_a2",
                 "rwkv_g1", "rwkv_g2", "rwkv_k_k", "rwkv_k_a", "rwkv_r_k", "rwkv_gn_gain", "rwkv_gn_bias",
                 "rwkv_w_out", "ffn_w_up", "ffn_conv_w", "ffn_conv_b", "ffn_w_down", "ple_w_proj", "ple_w_gate"]
    for name in prog.inputs:
        if name in prog.tiled:
            fn, K_, N_, tw_, kp_ = prog.tiled[name]
            w = np.asarray(fn(inputs))
            assert w.shape == (K_, N_), (name, w.shape)
            shared[name] = np.ascontiguousarray(
                w.reshape(K_ // kp_, kp_, N_ // tw_, tw_).transpose(2, 1, 0, 3).reshape(N_ // tw_, kp_, (K_ // kp_) * tw_))
            continue
        if name in inputs and name not in ("x",):
            shared[name] = np.ascontiguousarray(inputs[name])
            continue
        for base in per_layer:
            if name.startswith(base + "_") and name[len(base) + 1:].isdigit():
                shared[name] = np.ascontiguousarray(inputs[base][int(name[len(base) + 1:])])
    for core in range(cfg.ncores):
        b, seg = divmod(core, NSEG)
        consts = _const_inputs(cfg, core)
        m = {}
        for name in prog.inputs:
            if name in shared:
                m[name] = shared[name]
            elif name == "x":
                m[name] = np.ascontiguousarray(inputs["x"][b, seg * T:(seg + 1) * T, :])
            elif name.startswith("pT_"):
                l = int(name[3:])
                m[name] = np.ascontiguousarray(inputs["p"][l, b, seg * T:(seg + 1) * T, :].T)
            elif name in consts:
                m[name] = consts[name]
            else:
                raise KeyError(name)
        maps.append(m)
    return maps


def run_cfg(cfg, inputs):
    prog = Prog(cfg)
    prog.build()
    maps = make_in_maps(cfg, prog, inputs)
    res = run_bass_kernel_spmd(prog.nc, maps, core_ids=list(range(cfg.ncores)))
    return prog, res.results


def kernel(**inputs):
    cfg = Cfg()
    prog, results = run_cfg(cfg, inputs)
    out = np.empty((cfg.NB, cfg.NSEG * cfg.T, D), np.float32)
    for core in range(cfg.ncores):
        b, seg = divmod(core, cfg.NSEG)
        out[b, seg * cfg.T:(seg + 1) * cfg.T, :] = results[core]["out"]
    return out
```

```python
import contextlib
import math
import numpy as np
import ml_dtypes
import concourse.bass as bass
import concourse.mybir as mybir
from concourse.bass_utils import run_bass_kernel_spmd

F32 = mybir.dt.float32
BF16 = mybir.dt.bfloat16
AF = mybir.ActivationFunctionType
ALU = mybir.AluOpType
AX = mybir.AxisListType

P = 128
D = 2048
KC = D // P
DEPTH = 4
DFF = 5504
NFB = DFF // P
PLE = 256
ALPHA = (2.0 * DEPTH) ** 0.25
LN_EPS = 1e-5
RET_H, RET_DK, RET_DV = 8, 256, 512
RET_EPS = 1e-5
RET_GAMMA = [1.0 - 2.0 ** (-5.0 - h) for h in range(RET_H)]


class Ctx:
    NDMA = {"sp": 8, "act": 4, "pool": 8}

    def __init__(self, nc, stack):
        self.nc = nc
        self.stack = stack
        self.eng = {"pe": nc.tensor, "act": nc.scalar, "dve": nc.vector,
                    "pool": nc.gpsimd, "sp": nc.sync}
        self.sems = {}
        self.val = {}
        for e in ("pe", "act", "dve", "pool"):
            self.sems[e] = stack.enter_context(nc.semaphore("c_" + e))
            self.val[e] = 0
        self.dq = {}
        self.dq_next = {}
        for q, n in self.NDMA.items():
            keys = []
            for i in range(n):
                k = "d_%s%d" % (q, i)
                self.sems[k] = stack.enter_context(nc.semaphore(k))
                self.val[k] = 0
                keys.append(k)
            self.dq[q] = keys
            self.dq_next[q] = 0
        self.sems["cc"] = stack.enter_context(nc.semaphore("cc"))
        self.val["cc"] = 0
        self.known = {e: {} for e in self.eng}
        self.lastw = {}
        self.readers = {}
        self.uid = 0
        self.n_ins = 0

    def sb(self, stack, name, shape, dtype=F32):
        self.uid += 1
        return stack.enter_context(self.nc.sbuf_tensor("%s_%d" % (name, self.uid), list(shape), dtype))

    def ps(self, stack, name, shape, dtype=F32):
        self.uid += 1
        return stack.enter_context(self.nc.psum_tensor("%s_%d" % (name, self.uid), list(shape), dtype))

    def _key(self, b):
        return b if isinstance(b, (str, tuple)) else id(b)

    def _deps(self, reads, writes, merge=False):
        deps = {}
        for b in list(reads) + ([] if merge else list(writes)):
            for k, v in self.lastw.get(self._key(b), {}).items():
                if deps.get(k, 0) < v:
                    deps[k] = v
        for b in writes:
            for k, v in self.readers.get(self._key(b), {}).items():
                if deps.get(k, 0) < v:
                    deps[k] = v
        return deps

    def _wait(self, e, deps):
        kn = self.known[e]
        for k, v in deps.items():
            if e == "pe" and k == "pe":
                continue
            if kn.get(k, 0) >= v:
                continue
            self.eng[e].wait_ge(self.sems[k], v)
            kn[k] = v

    def _commit(self, ev, reads, writes, merge=False):
        k, v = ev
        for b in reads:
            self.readers.setdefault(self._key(b), {})[k] = v
        for b in writes:
            if merge:
                self.lastw.setdefault(self._key(b), {})[k] = v
            else:
                self.lastw[self._key(b)] = {k: v}
                self.readers[self._key(b)] = {}

    def op(self, e, fn, reads=(), writes=()):
        self._wait(e, self._deps(reads, writes))
        ins = fn()
        self.val[e] += 1
        ins.then_inc(self.sems[e], 1)
        self._commit((e, self.val[e]), reads, writes)
        self.n_ins += 1
        return ins

    def dma(self, q, out, in_, reads=(), writes=(), merge=False, **kw):
        deps = self._deps(reads, writes, merge)
        i = self.dq_next[q]
        self.dq_next[q] = (i + 1) % len(self.dq[q])
        k = self.dq[q][i]
        if self.val[k] > 0:
            deps[k] = max(deps.get(k, 0), self.val[k])
        self._wait(q, deps)
        ins = self.eng[q].dma_start(out=out, in_=in_, **kw)
        self.val[k] += 16
        ins.then_inc(self.sems[k], 16)
        self._commit((k, self.val[k]), reads, writes, merge)
        self.n_ins += 1
        return ins

    def allreduce(self, groups, in_ap, out_ap, reads=(), writes=()):
        deps = self._deps(reads, writes)
        self._wait("pool", deps)
        ins = self.nc.gpsimd.collective_compute("AllReduce", ALU.add, replica_groups=groups,
                                                ins=[in_ap.opt()], outs=[out_ap.opt()])
        self.val["cc"] += 1
        ins.then_inc(self.sems["cc"], 1)
        self._commit(("cc", self.val["cc"]), reads, writes)

    def barrier(self, engines=("pe", "act", "dve", "pool", "sp")):
        deps = {k: v for k, v in self.val.items() if v > 0}
        for e in engines:
            self._wait(e, dict(deps))
        if len(engines) == 5:
            self.lastw = {}
            self.readers = {}

    def finish(self):
        self._wait("sp", {k: v for k, v in self.val.items() if v > 0})


class Cfg:
    def __init__(self, NB=2, NSEG=4, T=2048, layers=(0, 1, 2, 3), debug=(), stop=None):
        self.stop = stop
        self.NB, self.NSEG, self.T = NB, NSEG, T
        self.layers = tuple(layers)
        self.NT = T // P
        self.ncores = NB * NSEG
        self.groups = [[b * NSEG + s for s in range(NSEG)] for b in range(NB)]
        self.debug = tuple(debug)


class Prog:
    def __init__(self, cfg):
        self.cfg = cfg
        self.nc = bass.Bass("TRN2", target_bir_lowering=False)
        self.inputs = {}
        self.scratch = {}
        self.tiled = {}

    def inp(self, name, shape, dtype=F32):
        if name not in self.inputs:
            self.inputs[name] = self.nc.dram_tensor(name, list(shape), dtype, kind="ExternalInput")
        return self.inputs[name]

    def scr(self, name, shape, dtype=F32):
        if name not in self.scratch:
            kind = "ExternalOutput" if name in self.cfg.debug else "Internal"
            self.scratch[name] = self.nc.dram_tensor(name, list(shape), dtype, kind=kind)
        return self.scratch[name]

    def build(self):
        cfg = self.cfg
        nc = self.nc
        T = cfg.T
        with contextlib.ExitStack() as top:
            ctx = self.ctx = Ctx(nc, top)
            self.top = top
            self.ident = ctx.sb(top, "ident", [P, P], BF16)
            ctx.dma("sp", self.ident[:], self.inp("ident", [P, P], BF16).ap(), writes=[self.ident])
            self.identf = ctx.sb(top, "identf", [P, P], F32)
            ctx.dma("sp", self.identf[:], self.inp("identf", [P, P], F32).ap(), writes=[self.identf])
            self.own = ctx.sb(top, "own", [P, cfg.NSEG], F32)
            ctx.dma("sp", self.own[:], self.inp("own", [P, cfg.NSEG]).ap(), writes=[self.own])
            self.halo_sel = ctx.sb(top, "halo_sel", [P, cfg.NSEG], F32)
            ctx.dma("sp", self.halo_sel[:], self.inp("halo_sel", [P, cfg.NSEG]).ap(), writes=[self.halo_sel])

            x_in = self.inp("x", [T, D])
            out = self.nc.dram_tensor("out", [T, D], F32, kind="ExternalOutput")
            XT = self.scr("XT", [D, T], BF16)
            cur = x_in
            self.xt_stage(cur, XT)
            for li, layer in enumerate(cfg.layers):
                kind = layer % 3
                Z1 = self.scr("Z1", [T, D])
                if kind == 0:
                    self.retention_layer(layer, cur, XT, Z1)
                elif kind == 1:
                    self.swa_layer(layer, cur, XT, Z1)
                else:
                    self.rwkv_layer(layer, cur, XT, Z1)
                if cfg.stop is not None:
                    break
                X1 = self.scr("X1", [T, D])
                XT1 = self.scr("XT1", [D, T], BF16)
                self.ln_stage(Z1, layer, 0, X1, XT1)
                Z2 = self.scr("Z2", [T, D])
                self.ffn_layer(layer, X1, XT1, Z2)
                X2 = self.scr("X2", [T, D])
                self.ln_stage(Z2, layer, 1, X2, XT)
                last = li == len(cfg.layers) - 1
                X3 = out if last else self.scr("X3_%d" % (li % 2), [T, D])
                self.ple_layer(layer, X2, XT, X3)
                if not last:
                    self.xt_stage(X3, XT)
                cur = X3
            ctx.barrier()
            ctx.finish()
        return nc

    def load_w(self, dst, src_ap, key):
        self.ctx.dma("pool", dst, src_ap, writes=[key])

    def bcast_rows(self, stack, name, src_ap_1d, n):
        t = self.ctx.sb(stack, name, [P, n], F32)
        self.ctx.dma("sp", t[:], src_ap_1d.partition_broadcast(P), writes=[t])
        return t

    def load_cols(self, stack, name, src_ap_2d, R, ps):
        ctx, nc = self.ctx, self.nc
        out = ctx.sb(stack, name, [P, R], F32)
        with contextlib.ExitStack() as st:
            r0 = 0
            while r0 < R:
                r = min(P, R - r0)
                tmp = ctx.sb(st, name + "_r", [P, P], F32)
                ctx.dma("sp", tmp[0:r, :], src_ap_2d[r0:r0 + r, :], writes=[tmp])
                ctx.op("pe", lambda: nc.tensor.matmul(ps[:, 0:r], lhsT=tmp[0:r, :], rhs=self.identf[0:r, 0:r],
                                                      start=True, stop=True), reads=[tmp, self.identf], writes=[ps])
                ctx.op("dve", lambda: nc.vector.tensor_copy(out[:, r0:r0 + r], ps[:, 0:r]), reads=[ps], writes=[out])
                r0 += r
            ctx.barrier(("pe", "dve", "sp"))
        return out

    def transpose_to(self, src_bf16_ap, pst_ap, reads, writes):
        nc = self.nc
        self.ctx.op("pe", lambda: nc.tensor.transpose(pst_ap, src_bf16_ap, self.ident[:]),
                    reads=list(reads) + [self.ident], writes=writes)

    def xt_emit_tile(self, xb, xb_key, stg, stg_key, col0, pst):
        ctx, nc = self.ctx, self.nc
        for half in range(2):
            pt = pst[half]
            for j in range(8):
                kc = half * 8 + j
                self.transpose_to(xb[:, kc * P:(kc + 1) * P], pt[:, j * P:(j + 1) * P], [xb_key], [pt])
            eng = "dve" if half == 0 else "act"
            src = pt[:].rearrange("p (j c) -> p j c", j=8)
            dst = stg[:, half * 8:(half + 1) * 8, col0:col0 + P]
            if eng == "dve":
                ctx.op("dve", lambda: nc.vector.tensor_copy(dst, src), reads=[pt], writes=[stg_key])
            else:
                ctx.op("act", lambda: nc.scalar.copy(dst, src), reads=[pt], writes=[stg_key])

    def xt_stage(self, X, XT):
        ctx, nc, cfg = self.ctx, self.nc, self.cfg
        T = cfg.T
        G = 4 if cfg.NT % 4 == 0 else 2
        with contextlib.ExitStack() as st:
            xf = [ctx.sb(st, "xf", [P, D], F32) for _ in range(2)]
            xb = [ctx.sb(st, "xb", [P, D], BF16) for _ in range(2)]
            stg = [ctx.sb(st, "stg", [P, KC, G * P], BF16) for _ in range(2)]
            pst = [[ctx.ps(st, "pst", [P, 8 * P], BF16) for _ in range(2)] for _ in range(2)]
            XTv = XT.ap().rearrange("(kc p) t -> p kc t", p=P)
            for tt in range(cfg.NT):
                b = tt % 2
                g, gi = divmod(tt, G)
                ctx.dma("sp", xf[b][:], X.ap()[tt * P:(tt + 1) * P, :], writes=[xf[b]])
                ctx.op("pool", lambda: nc.gpsimd.tensor_copy(xb[b][:], xf[b][:]), reads=[xf[b]], writes=[xb[b]])
                self.xt_emit_tile(xb[b], xb[b], stg[g % 2], stg[g % 2], gi * P, pst[b])
                if gi == G - 1:
                    ctx.dma("sp", XTv[:, :, g * G * P:(g + 1) * G * P], stg[g % 2][:], reads=[stg[g % 2]],
                            writes=[("XT", g)])
            ctx.barrier()

    def ln_stage(self, Z, layer, which, X, XT):
        ctx, nc, cfg = self.ctx, self.nc, self.cfg
        G = 4 if cfg.NT % 4 == 0 else 2
        with contextlib.ExitStack() as st:
            gain = self.bcast_rows(st, "lng", self.inp("ln_gain", [DEPTH, 2, D]).ap()[layer, which, :], D)
            bias = self.bcast_rows(st, "lnb", self.inp("ln_bias", [DEPTH, 2, D]).ap()[layer, which, :], D)
            zf = [ctx.sb(st, "zf", [P, D], F32) for _ in range(2)]
            xn = [ctx.sb(st, "xn", [P, D], F32) for _ in range(2)]
            xo = [ctx.sb(st, "xo", [P, D], F32) for _ in range(2)]
            xb = [ctx.sb(st, "xb", [P, D], BF16) for _ in range(2)]
            stats = [ctx.sb(st, "stats", [P, 4, 6], F32) for _ in range(2)]
            mv = [ctx.sb(st, "mv", [P, 4], F32) for _ in range(2)]
            stg = [ctx.sb(st, "stg", [P, KC, G * P], BF16) for _ in range(2)]
            pst = [[ctx.ps(st, "pst", [P, 8 * P], BF16) for _ in range(2)] for _ in range(2)]
            XTv = XT.ap().rearrange("(kc p) t -> p kc t", p=P)
            for tt in range(cfg.NT):
                b = tt % 2
                g, gi = divmod(tt, G)
                ctx.dma("sp", zf[b][:], Z.ap()[tt * P:(tt + 1) * P, :], writes=[zf[b]])
                self.layernorm_tile(zf[b], xn[b], stats[b], mv[b], D, LN_EPS)
                ctx.op("dve", lambda: nc.vector.tensor_tensor(xn[b][:], xn[b][:], gain[:], ALU.mult),
                       reads=[xn[b], gain], writes=[xn[b]])
                ctx.op("pool", lambda: nc.gpsimd.tensor_tensor(xo[b][:], xn[b][:], bias[:], ALU.add),
                       reads=[xn[b], bias], writes=[xo[b]])
                ctx.dma("sp", X.ap()[tt * P:(tt + 1) * P, :], xo[b][:], reads=[xo[b]], writes=[("X", tt)])
                ctx.op("act", lambda: nc.scalar.copy(xb[b][:], xo[b][:]), reads=[xo[b]], writes=[xb[b]])
                self.xt_emit_tile(xb[b], xb[b], stg[g % 2], stg[g % 2], gi * P, pst[b])
                if gi == G - 1:
                    ctx.dma("sp", XTv[:, :, g * G * P:(g + 1) * G * P], stg[g % 2][:], reads=[stg[g % 2]],
                            writes=[("XT", g)])
            ctx.barrier()

    def layernorm_tile(self, src, dst, stats, mv, n, eps, src_key=None, dst_key=None):
        ctx, nc = self.ctx, self.nc
        src_key = src if src_key is None else src_key
        dst_key = dst if dst_key is None else dst_key
        nch = max(1, n // 512)
        w = n // nch
        for c in range(nch):
            ctx.op("dve", lambda: nc.vector.bn_stats(stats[:, c, :], src[:, c * w:(c + 1) * w]),
                   reads=[src_key], writes=[stats])
        ctx.op("dve", lambda: nc.vector.bn_aggr(mv[:, 0:2], stats[:, 0:nch, :]), reads=[stats], writes=[mv])
        ctx.op("dve", lambda: nc.vector.tensor_scalar_add(mv[:, 2:3], mv[:, 1:2], eps), reads=[mv], writes=[mv])
        ctx.op("act", lambda: nc.scalar.activation(mv[:, 2:3], mv[:, 2:3], AF.Sqrt), reads=[mv], writes=[mv])
        ctx.op("dve", lambda: nc.vector.reciprocal(mv[:, 2:3], mv[:, 2:3]), reads=[mv], writes=[mv])
        ctx.op("dve", lambda: nc.vector.tensor_scalar(mv[:, 3:4], mv[:, 0:1], mv[:, 2:3], -1.0, ALU.mult, ALU.mult),
               reads=[mv], writes=[mv])
        ctx.op("act", lambda: nc.scalar.activation(dst[:, 0:n], src[:, 0:n], AF.Identity, bias=mv[:, 3:4],
                                                   scale=mv[:, 2:3]), reads=[src_key, mv], writes=[dst_key])


class TiledW:
    def __init__(self, h, K, N, tw, kp):
        self.h, self.K, self.N, self.tw, self.kp = h, K, N, tw, kp
        self.kcn = K // kp


def _tw(self, name, src_fn, K, N, tw, kp=P):
    if name not in self.inputs:
        self.inp(name, [N // tw, kp, (K // kp) * tw])
        self.tiled[name] = (src_fn, K, N, tw, kp)
    return TiledW(self.inputs[name], K, N, tw, kp)


def _load_wt(self, wb, W, c0, nb, key):
    assert c0 % W.tw == 0 and nb % W.tw == 0, (c0, nb, W.tw)
    i0, nt = c0 // W.tw, nb // W.tw
    for i in range(nt):
        dst = wb[0:W.kp, 0:W.kcn, i * W.tw:(i + 1) * W.tw]
        src = W.h.ap()[i0 + i].rearrange("p (kc j) -> p kc j", j=W.tw)
        self.ctx.dma("pool", dst, src, writes=[key], merge=(i > 0))


Prog.tw = _tw
Prog.load_wt = _load_wt


class GemmRes:
    def __init__(self, prog, st, kcmax, nblk, npsum, nw=2):
        ctx = prog.ctx
        self.w = [ctx.sb(st, "wbuf", [P, kcmax, nblk], BF16) for _ in range(nw)]
        self.ps = [ctx.ps(st, "gps", [P, 512], F32) for _ in range(npsum)]
        self.wi = 0
        self.pi = 0
        self.pending = {}

    def next_w(self):
        w = self.w[self.wi]
        self.wi = (self.wi + 1) % len(self.w)
        for k in [k for k, v in self.pending.items() if v is w]:
            del self.pending[k]
        return w

    def prefetch(self, prog, W, c0, nb):
        key = (id(W.h), c0, nb)
        if key in self.pending:
            return
        wb = self.next_w()
        prog.load_wt(wb, W, c0, nb, wb)
        self.pending[key] = wb

    def get_w(self, prog, W, c0, nb):
        wb = self.pending.pop((id(W.h), c0, nb), None)
        if wb is None:
            wb = self.next_w()
            prog.load_wt(wb, W, c0, nb, wb)
        return wb

    def next_ps(self):
        p = self.ps[self.pi]
        self.pi = (self.pi + 1) % len(self.ps)
        return p


def _gemm_tok(self, res, AT, at_key, kcn, W, n0, ncols, tts, epi, nblk=512, kp=P, nxt=None):
    ctx, nc = self.ctx, self.nc
    blocks = [(c0, min(nblk, n0 + ncols - c0)) for c0 in range(n0, n0 + ncols, nblk)]
    for bi, (c0, nb) in enumerate(blocks):
        wb = res.get_w(self, W, c0, nb)
        if bi + 1 < len(blocks):
            res.prefetch(self, W, *blocks[bi + 1])
        elif nxt is not None:
            res.prefetch(self, *nxt)
        for tt in tts:
            ps = res.next_ps()
            for kc in range(kcn):
                ctx.op("pe", lambda: nc.tensor.matmul(ps[:, 0:nb], lhsT=AT[0:kp, kc, tt * P:(tt + 1) * P],
                                                      rhs=wb[0:kp, kc, 0:nb], start=(kc == 0), stop=(kc == kcn - 1)),
                       reads=[at_key, wb], writes=[ps])
            epi(ps, tt, c0, nb)


def _gemm_feat(self, res, AT, at_key, kcn, W, n0, ncols, tgs, epi, nblk=512, nxt=None):
    ctx, nc = self.ctx, self.nc
    blocks = [(c0, min(nblk, n0 + ncols - c0)) for c0 in range(n0, n0 + ncols, nblk)]
    for bi, (c0, nb) in enumerate(blocks):
        wb = res.get_w(self, W, c0, nb)
        if bi + 1 < len(blocks):
            res.prefetch(self, W, *blocks[bi + 1])
        elif nxt is not None:
            res.prefetch(self, *nxt)
        for (t0, tn) in tgs:
            for fb in range((nb + P - 1) // P):
                fw = min(P, nb - fb * P)
                ps = res.next_ps()
                for kc in range(kcn):
                    ctx.op("pe", lambda: nc.tensor.matmul(ps[0:fw, 0:tn], lhsT=wb[:, kc, fb * P:fb * P + fw],
                                                          rhs=AT[:, kc, t0:t0 + tn], start=(kc == 0),
                                                          stop=(kc == kcn - 1)),
                           reads=[at_key, wb], writes=[ps])
                epi(ps, c0 + fb * P, t0, tn)


Prog.gemm_tok = _gemm_tok
Prog.gemm_feat = _gemm_feat


def _load_AT(self, st, name, XT, kcn, t0, tn, pad=0):
    ctx = self.ctx
    t = ctx.sb(st, name, [P, kcn, pad + tn], BF16)
    v = XT.ap().rearrange("(kc p) t -> p kc t", p=P)
    step = max(1, kcn // 4)
    for k0 in range(0, kcn, step):
        k1 = min(kcn, k0 + step)
        ctx.dma("sp", t[:, k0:k1, pad:pad + tn], v[:, k0:k1, t0:t0 + tn], writes=[t], merge=(k0 > 0))
    return t


Prog.load_AT = _load_AT


def _epi_resid(self, st, Xold, Zout, tok_base=0):
    ctx, nc = self.ctx, self.nc
    xo = [ctx.sb(st, "rx", [P, 512], F32) for _ in range(3)]
    zt = [ctx.sb(st, "rz", [P, 512], F32) for _ in range(3)]
    cnt = [0]

    def epi(ps, tt, c0, nb):
        i = cnt[0] % 3
        cnt[0] += 1
        r0 = tok_base + tt * P
        ctx.dma("sp", xo[i][:, 0:nb], Xold.ap()[r0:r0 + P, c0:c0 + nb], writes=[xo[i]])
        ctx.op("dve", lambda: nc.vector.scalar_tensor_tensor(zt[i][:, 0:nb], xo[i][:, 0:nb], ALPHA, ps[:, 0:nb],
                                                             ALU.mult, ALU.add),
               reads=[xo[i], ps], writes=[zt[i]])
        ctx.dma("sp", Zout.ap()[r0:r0 + P, c0:c0 + nb], zt[i][:, 0:nb], reads=[zt[i]], writes=[("Z", r0, c0)])
    return epi


Prog.epi_resid = _epi_resid


def _retention_layer(self, layer, X, XT, Z1):
    ctx, nc, cfg = self.ctx, self.nc, self.cfg
    T, NT, NSEG = cfg.T, cfg.NT, cfg.NSEG
    j = layer // 3
    w_qk = self.tw("ret_w_qk_t%d" % j, lambda inp, j=j: inp["ret_w_in"][j][:, 0:4096], D, 4096, 256)
    w_vg = self.tw("ret_w_vg_t%d" % j, lambda inp, j=j: inp["ret_w_in"][j][:, 4096:12288], D, 8192, 512)
    w_out = self.tw("ret_w_out_t%d" % j, lambda inp, j=j: inp["ret_w_out"][j], 4096, D, 256)
    gn_ap = self.inp("ret_gn_gain_%d" % j, [4096]).ap()
    KTs = self.scr("ret_KT", [D, T], BF16)
    Vs = self.scr("ret_V", [T, 4096], BF16)
    OGT = self.scr("ret_OGT", [4096, T], BF16)
    CCI = self.scr("ret_cci", [RET_H, NSEG, 2, P, 512])
    CCO = self.scr("ret_cco", [RET_H, NSEG, 2, P, 512])
    TH = min(1024, T)
    NTH = TH // P
    TGW = min(512, TH)
    tgs = [(t0, TGW) for t0 in range(0, TH, TGW)]
    QOFF, KOFF, VOFF, GOFF = 0, 2048, 0, 4096
    rope_cos = self.inp("rope_cos", [P, T]).ap()
    rope_sin = self.inp("rope_sin", [P, T]).ap()

    def rotary_epi(cosT, sinT, dstT, tmp):
        state = {}

        def epi(ps, c, t0, tn):
            half = (c // P) % 2
            if half == 0:
                state["A"] = ps
                return
            psA, psB = state["A"], ps
            t1, t2, t3, t4 = tmp
            cs, sn = cosT[:, t0:t0 + tn], sinT[:, t0:t0 + tn]
            ctx.op("dve", lambda: nc.vector.tensor_tensor(t1[:, 0:tn], psA[:, 0:tn], cs, ALU.mult),
                   reads=[psA, cosT], writes=[t1])
            ctx.op("dve", lambda: nc.vector.tensor_tensor(t2[:, 0:tn], psB[:, 0:tn], sn, ALU.mult),
                   reads=[psB, sinT], writes=[t2])
            ctx.op("dve", lambda: nc.vector.tensor_tensor(t3[:, 0:tn], psA[:, 0:tn], sn, ALU.mult),
                   reads=[psA, sinT], writes=[t3])
            ctx.op("dve", lambda: nc.vector.tensor_tensor(t4[:, 0:tn], psB[:, 0:tn], cs, ALU.mult),
                   reads=[psB, cosT], writes=[t4])
            ctx.op("pool", lambda: nc.gpsimd.tensor_tensor(dstT[:, 0, t0:t0 + tn], t1[:, 0:tn], t2[:, 0:tn],
                                                           ALU.subtract), reads=[t1, t2], writes=[dstT])
            ctx.op("pool", lambda: nc.gpsimd.tensor_tensor(dstT[:, 1, t0:t0 + tn], t3[:, 0:tn], t4[:, 0:tn],
                                                           ALU.add), reads=[t3, t4], writes=[dstT])
        return epi

    KTv = KTs.ap().rearrange("(h two p) t -> h p two t", two=2, p=P)
    Vv = Vs.ap().rearrange("(tt p) e -> p tt e", p=P)
    OGTv = OGT.ap().rearrange("(h fc p) t -> h p fc t", fc=4, p=P)

    with contextlib.ExitStack() as st:
        kdec = ctx.sb(st, "kdec", [P, RET_H], F32)
        ctx.dma("sp", kdec[:], self.inp("ret_kdec", [P, RET_H]).ap(), writes=[kdec])
        res = GemmRes(self, st, KC, 512, 3)
        tmp = [ctx.sb(st, "rt", [P, 512], F32) for _ in range(4)]
        kT = [ctx.sb(st, "kT", [P, 2, TH], BF16) for _ in range(2)]
        vh = [ctx.sb(st, "vh", [P, NTH, 512], BF16) for _ in range(2)]
        kdA = [ctx.sb(st, "kdA", [P, 2 * P], BF16) for _ in range(2)]
        pst = [ctx.ps(st, "pst", [P, 2 * P], BF16) for _ in range(2)]
        Lps = [ctx.ps(st, "Lps", [P, 512], F32) for _ in range(2)]
        Lacc = ctx.sb(st, "Lacc", [P, RET_H, 2, 512], F32)
        Lm = [ctx.sb(st, "Lm", [P, 2, 512], F32) for _ in range(2)]
        cosk = ctx.sb(st, "cosk", [P, TH], F32)
        sink = ctx.sb(st, "sink", [P, TH], F32)
        xT = ctx.sb(st, "xT", [P, KC, TH], BF16)
        XTv = XT.ap().rearrange("(kc p) t -> p kc t", p=P)
        for th in range(T // TH):
            t0h = th * TH
            for k0 in range(0, KC, 4):
                ctx.dma("sp", xT[:, k0:k0 + 4, :], XTv[:, k0:k0 + 4, t0h:t0h + TH], writes=[xT], merge=(k0 > 0))
            ctx.dma("sp", cosk[:], rope_cos[:, t0h:t0h + TH], writes=[cosk])
            ctx.dma("sp", sink[:], rope_sin[:, t0h:t0h + TH], writes=[sink])
            ctx.op("pool", lambda: nc.gpsimd.tensor_scalar_mul(cosk[:], cosk[:], RET_DK ** -0.5), reads=[cosk], writes=[cosk])
            ctx.op("pool", lambda: nc.gpsimd.tensor_scalar_mul(sink[:], sink[:], RET_DK ** -0.5), reads=[sink], writes=[sink])
            for h in range(RET_H):
                kTh, vhh = kT[h % 2], vh[h % 2]
                self.gemm_feat(res, xT, xT, KC, w_qk, KOFF + h * 256, 256, tgs, rotary_epi(cosk, sink, kTh, tmp), nblk=256,
                               nxt=(w_vg, VOFF + h * 512, 512))
                ctx.dma("sp", KTv[h][:, :, t0h:t0h + TH], kTh[:], reads=[kTh], writes=[("KT", h, th)])

                def v_epi(ps, tt, c0, nb):
                    ctx.op("act", lambda: nc.scalar.copy(vhh[:, tt, :], ps[:, 0:nb]), reads=[ps], writes=[vhh])
                self.gemm_tok(res, xT, xT, KC, w_vg, VOFF + h * 512, 512, range(NTH), v_epi,
                              nxt=(w_qk, KOFF + ((h + 1) % RET_H) * 256, 256))
                ctx.dma("sp", Vv[:, th * NTH:(th + 1) * NTH, h * 512:(h + 1) * 512], vhh[:],
                        reads=[vhh], writes=[("V", h, th)])
                if NSEG > 1:
                    g = RET_GAMMA[h]
                    for cl in range(NTH):
                        c = th * NTH + cl
                        pt = pst[cl % 2]
                        kd = kdA[cl % 2]
                        for half in range(2):
                            self.transpose_to(kTh[:, half, cl * P:(cl + 1) * P], pt[:, half * P:(half + 1) * P], [kTh], [pt])
                        ctx.op("dve", lambda: nc.vector.tensor_scalar(kd[:], pt[:], kdec[:, h:h + 1],
                                                                      float(g ** (P * (NT - 1 - c))), ALU.mult, ALU.mult),
                               reads=[pt, kdec], writes=[kd])
                        for half in range(2):
                            ctx.op("pe", lambda: nc.tensor.matmul(Lps[half][:], lhsT=kd[:, half * P:(half + 1) * P],
                                                                  rhs=vhh[:, cl, :], start=(cl == 0), stop=(cl == NTH - 1)),
                                   reads=[kd, vhh], writes=[Lps[half]])
                    for half in range(2):
                        if th == 0:
                            ctx.op("act", lambda: nc.scalar.copy(Lacc[:, h, half, :], Lps[half][:]),
                                   reads=[Lps[half]], writes=[(id(Lacc), h)])
                        else:
                            ctx.op("dve", lambda: nc.vector.tensor_tensor(Lacc[:, h, half, :], Lacc[:, h, half, :],
                                                                          Lps[half][:], ALU.add),
                                   reads=[Lps[half], (id(Lacc), h)], writes=[(id(Lacc), h)])
        if NSEG > 1:
            for h in range(RET_H):
                for s in range(NSEG):
                    lm = Lm[(h * NSEG + s) % 2]
                    ctx.op("dve", lambda: nc.vector.tensor_scalar_mul(lm[:], Lacc[:, h], self.own[:, s:s + 1]),
                           reads=[(id(Lacc), h), self.own], writes=[lm])
                    ctx.dma("sp", CCI.ap()[h, s].rearrange("two p e -> p two e"), lm[:], reads=[lm],
                            writes=[("CCI", s, h)])
        ctx.barrier()
    if NSEG > 1:
        for h in range(RET_H):
            ctx.allreduce(cfg.groups, CCI.ap()[h].rearrange("s two p e -> (s two p) e"),
                          CCO.ap()[h].rearrange("s two p e -> (s two p) e"), writes=[("CCO", h)])
        ctx.barrier()

    with contextlib.ExitStack() as st:
        kdec = ctx.sb(st, "kdec", [P, RET_H], F32)
        ctx.dma("sp", kdec[:], self.inp("ret_kdec", [P, RET_H]).ap(), writes=[kdec])
        maskT = ctx.sb(st, "maskT", [P, RET_H, P], F32)
        ctx.dma("sp", maskT[:], self.inp("ret_maskT", [P, RET_H, P]).ap(), writes=[maskT])
        qdec = ctx.sb(st, "qdec", [P, RET_H, P], F32)
        ctx.dma("sp", qdec[:], self.inp("ret_qdec", [P, RET_H, P]).ap(), writes=[qdec])
        coef = ctx.sb(st, "coef", [P, NSEG, RET_H], F32)
        ctx.dma("sp", coef[:], self.inp("ret_coef", [P, NSEG, RET_H]).ap(), writes=[coef])
        gain = self.bcast_rows(st, "gng", gn_ap, 4096)
        res = GemmRes(self, st, KC, 512, 2)
        tmp = [ctx.sb(st, "rt", [P, 512], F32) for _ in range(4)]
        cosT = ctx.sb(st, "cos", [P, TH], F32)
        sinT = ctx.sb(st, "sin", [P, TH], F32)
        xT = ctx.sb(st, "xT", [P, KC, TH], BF16)
        XTv = XT.ap().rearrange("(kc p) t -> p kc t", p=P)
        kTh = ctx.sb(st, "kT", [P, 2, TH], BF16)
        qTh = ctx.sb(st, "qT", [P, 2, TH], BF16)
        vhh = ctx.sb(st, "vh", [P, NTH, 512], BF16)
        gsh = ctx.sb(st, "gs", [P, NTH, 512], BF16)
        ogTh = ctx.sb(st, "ogT", [P, 4, TH], BF16)
        Rall = ctx.sb(st, "Rall", [P, RET_H, 2, 512], F32)
        Rb = ctx.sb(st, "Rb", [P, 2, 512], BF16)
        cin = [ctx.sb(st, "cin", [P, 2, 512], F32) for _ in range(2)]
        sT = [ctx.sb(st, "sT", [P, P], BF16) for _ in range(2)]
        qd = [ctx.sb(st, "qd", [P, 2, P], BF16) for _ in range(2)]
        kd = [ctx.sb(st, "kd", [P, 2 * P], BF16) for _ in range(2)]
        on = [ctx.sb(st, "on", [P, 512], F32) for _ in range(2)]
        og = [ctx.sb(st, "og", [P, 512], F32) for _ in range(2)]
        og2 = [ctx.sb(st, "og2", [P, 512], BF16) for _ in range(2)]
        stats = [ctx.sb(st, "stats", [P, 4, 6], F32) for _ in range(2)]
        mv = [ctx.sb(st, "mv", [P, 4], F32) for _ in range(2)]
        ps_s = ctx.ps(st, "ps_s", [P, 512], F32)
        ps_o = ctx.ps(st, "ps_o", [P, 512], F32)
        ps_t = ctx.ps(st, "ps_t", [P, 2 * P], BF16)
        ps_R = [ctx.ps(st, "ps_R", [P, 512], F32) for _ in range(2)]
        ps_g = ctx.ps(st, "ps_g", [P, 4 * P], BF16)
        for h in range(RET_H):
            Rk = (id(Rall), h)
            if NSEG > 1:
                for s in range(NSEG):
                    ci = cin[s % 2]
                    ctx.dma("sp", ci[:], CCO.ap()[h, s].rearrange("two p e -> p two e"), writes=[ci])
                    if s == 0:
                        ctx.op("dve", lambda: nc.vector.tensor_scalar_mul(Rall[:, h], ci[:], coef[:, s, h:h + 1]),
                               reads=[ci, coef], writes=[Rk])
                    else:
                        ctx.op("dve", lambda: nc.vector.scalar_tensor_tensor(Rall[:, h], ci[:], coef[:, s, h:h + 1],
                                                                             Rall[:, h], ALU.mult, ALU.add),
                               reads=[ci, coef, Rk], writes=[Rk])
            else:
                ctx.op("dve", lambda: nc.vector.memset(Rall[:, h], 0.0), writes=[Rk])
        for th in range(T // TH):
            t0h = th * TH
            for k0 in range(0, KC, 4):
                ctx.dma("sp", xT[:, k0:k0 + 4, :], XTv[:, k0:k0 + 4, t0h:t0h + TH], writes=[xT], merge=(k0 > 0))
            ctx.dma("sp", cosT[:], rope_cos[:, t0h:t0h + TH], writes=[cosT])
            ctx.dma("sp", sinT[:], rope_sin[:, t0h:t0h + TH], writes=[sinT])
            for h in range(RET_H):
                Rk = (id(Rall), h)
                ctx.dma("sp", kTh[:], KTv[h][:, :, t0h:t0h + TH], writes=[kTh])
                ctx.dma("sp", vhh[:], Vv[:, th * NTH:(th + 1) * NTH, h * 512:(h + 1) * 512], writes=[vhh])
                ctx.op("act", lambda: nc.scalar.copy(Rb[:], Rall[:, h]), reads=[Rk], writes=[Rb])
                self.gemm_feat(res, xT, xT, KC, w_qk, QOFF + h * 256, 256, tgs, rotary_epi(cosT, sinT, qTh, tmp), nblk=256,
                               nxt=(w_vg, GOFF + h * 512, 512))

                def g_epi(ps, tt, c0, nb):
                    ctx.op("act", lambda: nc.scalar.activation(gsh[:, tt, :], ps[:, 0:nb], AF.Silu), reads=[ps], writes=[gsh])
                self.gemm_tok(res, xT, xT, KC, w_vg, GOFF + h * 512, 512, range(NTH), g_epi,
                              nxt=(w_qk, QOFF + ((h + 1) % RET_H) * 256, 256))
                gam = RET_GAMMA[h]
                for cl in range(NTH):
                    cb = cl % 2
                    cs = slice(cl * P, (cl + 1) * P)
                    for half in range(2):
                        ctx.op("pe", lambda: nc.tensor.matmul(ps_s[:, 0:P], lhsT=kTh[:, half, cs], rhs=qTh[:, half, cs],
                                                              start=(half == 0), stop=(half == 1)),
                               reads=[kTh, qTh], writes=[ps_s])
                    ctx.op("dve", lambda: nc.vector.tensor_tensor(sT[cb][:], ps_s[:, 0:P], maskT[:, h, :], ALU.mult),
                           reads=[ps_s, maskT], writes=[sT[cb]])
                    for half in range(2):
                        ctx.op("pool", lambda: nc.gpsimd.tensor_tensor(qd[cb][:, half, :], qTh[:, half, cs],
                                                                       qdec[:, h, :], ALU.mult),
                               reads=[qTh, qdec], writes=[qd[cb]])
                    ctx.op("pe", lambda: nc.tensor.matmul(ps_o[:], lhsT=sT[cb][:], rhs=vhh[:, cl, :], start=True, stop=False),
                           reads=[sT[cb], vhh], writes=[ps_o])
                    for half in range(2):
                        ctx.op("pe", lambda: nc.tensor.matmul(ps_o[:], lhsT=qd[cb][:, half, :], rhs=Rb[:, half, :],
                                                              start=False, stop=(half == 1)),
                               reads=[qd[cb], Rb], writes=[ps_o])
                    for half in range(2):
                        self.transpose_to(kTh[:, half, cs], ps_t[:, half * P:(half + 1) * P], [kTh], [ps_t])
                    ctx.op("dve", lambda: nc.vector.tensor_scalar_mul(kd[cb][:], ps_t[:], kdec[:, h:h + 1]),
                           reads=[ps_t, kdec], writes=[kd[cb]])
                    for half in range(2):
                        ctx.op("pe", lambda: nc.tensor.matmul(ps_R[half][:], lhsT=kd[cb][:, half * P:(half + 1) * P],
                                                              rhs=vhh[:, cl, :], start=True, stop=True),
                               reads=[kd[cb], vhh], writes=[ps_R[half]])
                        ctx.op("dve", lambda: nc.vector.scalar_tensor_tensor(Rall[:, h, half, :], Rall[:, h, half, :],
                                                                             float(gam ** P), ps_R[half][:],
                                                                             ALU.mult, ALU.add),
                               reads=[Rk, ps_R[half]], writes=[Rk])
                    ctx.op("act", lambda: nc.scalar.copy(Rb[:], Rall[:, h]), reads=[Rk], writes=[Rb])
                    self.layernorm_tile(ps_o, on[cb], stats[cb], mv[cb], 512, RET_EPS)
                    ctx.op("dve", lambda: nc.vector.tensor_tensor(og[cb][:], on[cb][:], gain[:, h * 512:(h + 1) * 512], ALU.mult),
                           reads=[on[cb], gain], writes=[og[cb]])
                    ctx.op("pool", lambda: nc.gpsimd.tensor_tensor(og2[cb][:], og[cb][:], gsh[:, cl, :], ALU.mult),
                           reads=[og[cb], gsh], writes=[og2[cb]])
                    for fc in range(4):
                        self.transpose_to(og2[cb][:, fc * P:(fc + 1) * P], ps_g[:, fc * P:(fc + 1) * P], [og2[cb]], [ps_g])
                    ctx.op("act", lambda: nc.scalar.copy(ogTh[:, :, cs], ps_g[:].rearrange("p (f c) -> p f c", f=4)),
                           reads=[ps_g], writes=[ogTh])
                ctx.dma("sp", OGTv[h][:, :, t0h:t0h + TH], ogTh[:], reads=[ogTh], writes=[("OGT", h, th)])
        ctx.barrier()

    for t0 in range(0, T, TH):
        with contextlib.ExitStack() as st:
            aT = self.load_AT(st, "ogTa", OGT, 32, t0, TH)
            res = GemmRes(self, st, 32, 256, 3)
            epi = self.epi_resid(st, X, Z1, tok_base=t0)
            self.gemm_tok(res, aT, aT, 32, w_out, 0, D, range(TH // P), epi, nblk=256)
            ctx.barrier()


Prog.retention_layer = _retention_layer
def _halo_rows(self, st, Xsrc, nrows, name):
    ctx, nc, cfg = self.ctx, self.nc, self.cfg
    NSEG, T = cfg.NSEG, cfg.T
    halo = ctx.sb(st, name, [nrows, D], F32)
    if NSEG == 1:
        ctx.op("dve", lambda: nc.vector.memset(halo[:], 0.0), writes=[halo])
        return halo
    HCI = self.scr("halo_ci_%d" % nrows, [NSEG, nrows, D])
    HCO = self.scr("halo_co_%d" % nrows, [NSEG, nrows, D])
    hx = ctx.sb(st, name + "_x", [nrows, D], F32)
    hm = [ctx.sb(st, name + "_m", [nrows, D], F32) for _ in range(2)]
    ctx.dma("sp", hx[:], Xsrc.ap()[T - nrows:T, :], writes=[hx])
    for s in range(NSEG):
        ctx.op("dve", lambda: nc.vector.tensor_scalar_mul(hm[s % 2][:], hx[:], self.own[0:nrows, s:s + 1]),
               reads=[hx, self.own], writes=[hm[s % 2]])
        ctx.dma("sp", HCI.ap()[s], hm[s % 2][:], reads=[hm[s % 2]], writes=[("HCI", s)])
    ctx.barrier()
    ctx.allreduce(cfg.groups, HCI.ap().rearrange("s r d -> (s r) d"), HCO.ap().rearrange("s r d -> (s r) d"),
                  writes=[("HCO",)])
    ctx.barrier()
    for s in range(NSEG):
        ctx.dma("sp", hm[s % 2][:], HCO.ap()[s], writes=[hm[s % 2]])
        if s == 0:
            ctx.op("dve", lambda: nc.vector.tensor_scalar_mul(halo[:], hm[s % 2][:], self.halo_sel[0:nrows, s:s + 1]),
                   reads=[hm[s % 2], self.halo_sel], writes=[halo])
        else:
            ctx.op("dve", lambda: nc.vector.scalar_tensor_tensor(halo[:], hm[s % 2][:], self.halo_sel[0:nrows, s:s + 1],
                                                                 halo[:], ALU.mult, ALU.add),
                   reads=[hm[s % 2], self.halo_sel, halo], writes=[halo])
    return halo


Prog.halo_rows = _halo_rows


def _ffn_layer(self, layer, X1, XT1, Z2):
    ctx, nc, cfg = self.ctx, self.nc, self.cfg
    T = cfg.T
    w_up = self.tw("ffn_w_up_t%d" % layer, lambda inp, l=layer: inp["ffn_w_up"][l], D, 2 * DFF, 128)
    w_dn = self.tw("ffn_w_down_t%d" % layer, lambda inp, l=layer: inp["ffn_w_down"][l], DFF, D, 256)
    cw_ap = self.inp("ffn_conv_w_%d" % layer, [3, 2 * DFF]).ap().rearrange("t (b p) -> (t b) p", p=P)
    cb_ap = self.inp("ffn_conv_b_%d" % layer, [2 * DFF]).ap().rearrange("(b p) -> b p", p=P)
    NB2 = 2 * NFB
    TG = min(1024, T)
    W = TG + 2
    nsub = (W + 511) // 512
    bounds = [(W * i) // nsub for i in range(nsub + 1)]
    with contextlib.ExitStack() as st0:
        psc = ctx.ps(st0, "psc", [P, 512], F32)
        cw = self.load_cols(st0, "cw", cw_ap, 3 * NB2, psc)
        cb = self.load_cols(st0, "cb", cb_ap, NB2, psc)
        haloT = ctx.sb(st0, "haloT", [P, KC, 2], BF16)
        with contextlib.ExitStack() as sth:
            halo = self.halo_rows(sth, X1, 2, "halo2")
            for kc in range(KC):
                ctx.op("pe", lambda: nc.tensor.matmul(psc[:, kc * 2:kc * 2 + 2], lhsT=halo[0:2, kc * P:(kc + 1) * P],
                                                      rhs=self.identf[0:2, 0:2], start=True, stop=True),
                       reads=[halo, self.identf], writes=[psc])
            ctx.op("dve", lambda: nc.vector.tensor_copy(haloT[:], psc[:, 0:2 * KC].rearrange("p (k c) -> p k c", c=2)),
                   reads=[psc], writes=[haloT])
            ctx.barrier()
        gT = ctx.sb(st0, "gT", [P, NFB, TG], BF16)
        XTv = XT1.ap().rearrange("(kc p) t -> p kc t", p=P)
        for g in range(T // TG):
            t0 = g * TG
            with contextlib.ExitStack() as st:
                xT = ctx.sb(st, "x1T", [P, KC, W], BF16)
                for k0 in range(0, KC, 4):
                    ctx.dma("sp", xT[:, k0:k0 + 4, 2:W], XTv[:, k0:k0 + 4, t0:t0 + TG], writes=[xT], merge=(k0 > 0))
                if g == 0:
                    ctx.op("pool", lambda: nc.gpsimd.tensor_copy(xT[:, :, 0:2], haloT[:]), reads=[haloT], writes=[xT])
                else:
                    ctx.dma("sp", xT[:, :, 0:2], XTv[:, :, t0 - 2:t0], writes=[xT], merge=True)
                wu = [ctx.sb(st, "wu", [P, 2, KC, P], BF16) for _ in range(2)]
                wg = [ctx.sb(st, "wg", [P, 2, KC, P], BF16) for _ in range(2)]
                pss = [ctx.ps(st, "fps", [P, 512], F32) for _ in range(6)]
                hs = [ctx.sb(st, "hs", [P, W], F32) for _ in range(2)]
                acc = [ctx.sb(st, "acc", [P, TG], F32) for _ in range(2)]
                sg = ctx.sb(st, "sg", [P, TG], F32)
                pi = 0

                def load_pair(pr):
                    fb0 = 2 * pr
                    wi_ = pr % 2
                    for ti in range(min(2, NFB - fb0)):
                        ctx.dma("pool", wu[wi_][:, ti], w_up.h.ap()[fb0 + ti].rearrange("p (kc j) -> p kc j", j=P),
                                writes=[wu[wi_]], merge=(ti > 0))
                        ctx.dma("pool", wg[wi_][:, ti], w_up.h.ap()[NFB + fb0 + ti].rearrange("p (kc j) -> p kc j", j=P),
                                writes=[wg[wi_]], merge=(ti > 0))
                load_pair(0)
                for fb in range(NFB):
                    if fb % 2 == 0 and fb + 2 < NFB:
                        load_pair(fb // 2 + 1)
                    wi = (fb // 2) % 2
                    fo = (fb % 2) * P
                    for ui, wt in enumerate((wu[wi], wg[wi])):
                        for si in range(nsub):
                            a, b_ = bounds[si], bounds[si + 1]
                            ps = pss[pi % 6]
                            pi += 1
                            for kc in range(KC):
                                ctx.op("pe", lambda: nc.tensor.matmul(ps[:, 0:b_ - a], lhsT=wt[:, fb % 2, kc, :],
                                                                      rhs=xT[:, kc, a:b_], start=(kc == 0), stop=(kc == KC - 1)),
                                       reads=[wt, xT], writes=[ps])
                            ctx.op("act", lambda: nc.scalar.copy(hs[ui][:, a:b_], ps[:, 0:b_ - a]), reads=[ps], writes=[hs[ui]])
                        blk = fb if ui == 0 else NFB + fb
                        w0 = cw[:, 0 * NB2 + blk:0 * NB2 + blk + 1]
                        w1 = cw[:, 1 * NB2 + blk:1 * NB2 + blk + 1]
                        w2 = cw[:, 2 * NB2 + blk:2 * NB2 + blk + 1]
                        ctx.op("act", lambda: nc.scalar.activation(acc[ui][:], hs[ui][:, 2:W], AF.Identity,
                                                                   bias=cb[:, blk:blk + 1], scale=w2),
                               reads=[hs[ui], cw, cb], writes=[acc[ui]])
                        ctx.op("dve", lambda: nc.vector.scalar_tensor_tensor(acc[ui][:], hs[ui][:, 1:W - 1], w1, acc[ui][:],
                                                                             ALU.mult, ALU.add),
                               reads=[hs[ui], cw, acc[ui]], writes=[acc[ui]])
                        ctx.op("dve", lambda: nc.vector.scalar_tensor_tensor(acc[ui][:], hs[ui][:, 0:W - 2], w0, acc[ui][:],
                                                                             ALU.mult, ALU.add),
                               reads=[hs[ui], cw, acc[ui]], writes=[acc[ui]])
                    ctx.op("act", lambda: nc.scalar.activation(sg[:], acc[1][:], AF.Silu), reads=[acc[1]], writes=[sg])
                    ctx.op("pool", lambda: nc.gpsimd.tensor_tensor(gT[:, fb, :], sg[:], acc[0][:], ALU.mult),
                           reads=[sg, acc[0]], writes=[gT])
                ctx.barrier()
            with contextlib.ExitStack() as st:
                res = GemmRes(self, st, NFB, 256, 4)
                epi = self.epi_resid(st, X1, Z2, tok_base=t0)
                self.gemm_tok(res, gT, gT, NFB, w_dn, 0, D, range(TG // P), epi, nblk=256)
                ctx.barrier()


Prog.ffn_layer = _ffn_layer


def _ple_layer(self, layer, X2, XT2, X3):
    ctx, nc, cfg = self.ctx, self.nc, self.cfg
    T, NT = cfg.T, cfg.NT
    w_gate = self.tw("ple_w_gate_t%d" % layer, lambda inp, l=layer: inp["ple_w_gate"][l], D, D, 512)
    w_proj = self.tw("ple_w_proj_t%d" % layer, lambda inp, l=layer: inp["ple_w_proj"][l], PLE, D, 512)
    pT_in = self.inp("pT_%d" % layer, [PLE, T]).ap()
    with contextlib.ExitStack() as st:
        xT = self.load_AT(st, "x2T", XT2, KC, 0, T)
        pT = ctx.sb(st, "pT", [P, 2, T], BF16)
        self.load_w(pT[:], pT_in.rearrange("(kc p) t -> p kc t", p=P), pT)
        wg = [ctx.sb(st, "wg", [P, KC, 512], BF16) for _ in range(2)]
        wp = [ctx.sb(st, "wp", [P, 2, 512], BF16) for _ in range(2)]
        psg = [ctx.ps(st, "psg", [P, 512], F32) for _ in range(3)]
        psp = [ctx.ps(st, "psp", [P, 512], F32) for _ in range(3)]
        sg = [ctx.sb(st, "sg", [P, 512], F32) for _ in range(3)]
        x2 = [ctx.sb(st, "x2", [P, 512], F32) for _ in range(3)]
        x3 = [ctx.sb(st, "x3", [P, 512], F32) for _ in range(3)]
        it = 0

        def load_blk(ci_):
            self.load_wt(wg[ci_ % 2], w_gate, ci_ * 512, 512, wg[ci_ % 2])
            self.load_wt(wp[ci_ % 2], w_proj, ci_ * 512, 512, wp[ci_ % 2])
        load_blk(0)
        for ci, c0 in enumerate(range(0, D, 512)):
            wgi, wpi = wg[ci % 2], wp[ci % 2]
            if ci + 1 < D // 512:
                load_blk(ci + 1)
            for tt in range(NT):
                i = it % 3
                it += 1
                ts = slice(tt * P, (tt + 1) * P)
                for kc in range(KC):
                    ctx.op("pe", lambda: nc.tensor.matmul(psg[i][:], lhsT=xT[:, kc, ts], rhs=wgi[:, kc, :],
                                                          start=(kc == 0), stop=(kc == KC - 1)),
                           reads=[xT, wgi], writes=[psg[i]])
                for kc in range(2):
                    ctx.op("pe", lambda: nc.tensor.matmul(psp[i][:], lhsT=pT[:, kc, ts], rhs=wpi[:, kc, :],
                                                          start=(kc == 0), stop=(kc == 1)),
                           reads=[pT, wpi], writes=[psp[i]])
                ctx.dma("sp", x2[i][:], X2.ap()[tt * P:(tt + 1) * P, c0:c0 + 512], writes=[x2[i]])
                ctx.op("act", lambda: nc.scalar.activation(sg[i][:], psg[i][:], AF.Sigmoid), reads=[psg[i]], writes=[sg[i]])
                ctx.op("dve", lambda: nc.vector.tensor_tensor(sg[i][:], sg[i][:], psp[i][:], ALU.mult),
                       reads=[sg[i], psp[i]], writes=[sg[i]])
                ctx.op("pool", lambda: nc.gpsimd.tensor_tensor(x3[i][:], sg[i][:], x2[i][:], ALU.add),
                       reads=[sg[i], x2[i]], writes=[x3[i]])
                ctx.dma("sp", X3.ap()[tt * P:(tt + 1) * P, c0:c0 + 512], x3[i][:], reads=[x3[i]], writes=[("X3", tt, c0)])
        ctx.barrier()


Prog.ple_layer = _ple_layer
SWA_HQ, SWA_HKV, SWA_HD, SWA_W = 32, 4, 64, 128
NEG = -1e30


def _swa_head_order():
    order = []
    for pair in range(2):
        for g in range(8):
            order.append((2 * pair) * 8 + g)
            order.append((2 * pair + 1) * 8 + g)
    return order


def _t5_bucket(n):
    max_exact = 16
    if n < max_exact:
        return n
    large = max_exact + int(np.log(max(n, 1) / max_exact) / np.log(SWA_W / max_exact) * (32 - max_exact))
    return min(large, 31)


def _swa_consts(cfg, core, c):
    seg = core % cfg.NSEG
    E = np.zeros((32, 383), np.float32)
    for u in range(383):
        d = u - 127
        if 0 <= d < SWA_W:
            nn = np.maximum(np.array([d]), 0)
            large = 16 + (np.log(np.maximum(nn, 1) / 16) / np.log(SWA_W / 16) * 16).astype(np.int32)
            large = np.minimum(large, 31)
            b = int(np.where(nn < 16, nn, large)[0])
            E[b, u] = 1.0
    c["swa_E"] = E
    i = np.arange(P)[:, None]
    j = np.arange(2 * P)[None, :]
    d = i + P - j
    c["swa_maskc"] = np.where((d >= 0) & (d < SWA_W), 0.0, NEG).astype(np.float32)
    mf = np.zeros((P, 2 * P), np.float32)
    if seg == 0:
        mf[:, :P] = NEG
    c["swa_mask_first"] = mf


def _swa_layer(self, layer, X, XT, Z1):
    ctx, nc, cfg = self.ctx, self.nc, self.cfg
    T, NT, NSEG = cfg.T, cfg.NT, cfg.NSEG
    j = layer // 3
    def _qkv_src(inp, j=j):
        w = inp["swa_w_qkv"][j]
        qcols = np.concatenate([np.arange(h * 64, (h + 1) * 64) for h in _swa_head_order()])
        return np.concatenate([w[:, qcols], w[:, 2048:]], axis=1)

    def _out_src(inp, j=j):
        rows = np.concatenate([np.arange(h * 64, (h + 1) * 64) for h in _swa_head_order()])
        return inp["swa_w_out"][j][rows, :]
    w_q = self.tw("swa_w_q_t%d" % j, lambda inp: _qkv_src(inp)[:, 0:2048], D, 2048, 512)
    w_kv = self.tw("swa_w_kv_t%d" % j, lambda inp: _qkv_src(inp)[:, 2048:2560], D, 512, 256)
    w_out = self.tw("swa_w_out_t%d" % j, _out_src, D, D, 512)
    sinks_ap = self.inp("swa_sinks_%d" % j, [SWA_HQ]).ap()
    relb_ap = self.inp("rel_bias", [32, SWA_HQ]).ap()
    OT = self.scr("swa_OT", [D, T], BF16)
    QTs = self.scr("swa_QT", [D, T], BF16)
    order = _swa_head_order()
    TGW = min(512, T)
    tgs = [(t0, TGW) for t0 in range(0, T, TGW)]
    with contextlib.ExitStack() as st0:
        kT = ctx.sb(st0, "kT", [P, 2, P + T], BF16)
        vS = ctx.sb(st0, "vS", [P, 1 + NT, 256], BF16)
        biasS = ctx.sb(st0, "biasS", [P, SWA_HQ, 2 * P], F32)
        sinkb = self.bcast_rows(st0, "sinkb", sinks_ap, SWA_HQ)
        mfirst = ctx.sb(st0, "mfirst", [P, 2 * P], F32)
        ctx.dma("sp", mfirst[:], self.inp("swa_mask_first", [P, 2 * P]).ap(), writes=[mfirst])
        with contextlib.ExitStack() as st:
            E = ctx.sb(st, "E", [32, 383], F32)
            RB = ctx.sb(st, "RB", [32, SWA_HQ], F32)
            maskc = ctx.sb(st, "maskc", [P, 2 * P], F32)
            ctx.dma("sp", E[:], self.inp("swa_E", [32, 383]).ap(), writes=[E])
            ctx.dma("sp", RB[:], relb_ap, writes=[RB])
            ctx.dma("sp", maskc[:], self.inp("swa_maskc", [P, 2 * P]).ap(), writes=[maskc])
            psb = [ctx.ps(st, "psb", [P, 512], F32) for _ in range(2)]
            for r in range(16):
                ps = psb[r % 2]
                for jj in range(16):
                    jk = r * 16 + jj
                    ctx.op("pe", lambda: nc.tensor.matmul(ps[:, jj * 32:(jj + 1) * 32], lhsT=E[:, 255 - jk:383 - jk], rhs=RB[:],
                                                          start=True, stop=True), reads=[E, RB], writes=[ps])
                ctx.op("dve", lambda: nc.vector.tensor_tensor(
                    biasS[:, :, r * 16:(r + 1) * 16].rearrange("p h j -> p j h"),
                    ps[:].rearrange("p (j h) -> p j h", h=32),
                    maskc[:, r * 16:(r + 1) * 16].unsqueeze(2).broadcast_to([P, 16, 32]), ALU.add),
                    reads=[ps, maskc], writes=[biasS])
            ctx.barrier()
        with contextlib.ExitStack() as st:
            xT = self.load_AT(st, "xT", XT, KC, 0, T)
            res = GemmRes(self, st, KC, 512, 3)
            qst = [ctx.sb(st, "qst", [P, 4, TGW], BF16) for _ in range(2)]
            QTv = QTs.ap().rearrange("(kc p) t -> p kc t", p=P)
            qcnt = [0]

            def q_epi(ps, c, t0, tn):
                kc = c // P
                qs = qst[(qcnt[0] // 4) % 2]
                ctx.op("act", lambda: nc.scalar.activation(qs[:, kc % 4, 0:tn], ps[:, 0:tn], AF.Copy, scale=SWA_HD ** -0.5),
                       reads=[ps], writes=[qs])
                qcnt[0] += 1
                if kc % 4 == 3:
                    ctx.dma("sp", QTv[:, kc - 3:kc + 1, t0:t0 + tn], qs[:, :, 0:tn], reads=[qs], writes=[("QT", kc, t0)])
            self.gemm_feat(res, xT, xT, KC, w_q, 0, 2048, tgs, q_epi)

            def k_epi(ps, c, t0, tn):
                fb = c // P
                ctx.op("act", lambda: nc.scalar.copy(kT[:, fb, P + t0:P + t0 + tn], ps[:, 0:tn]), reads=[ps], writes=[kT])
            self.gemm_feat(res, xT, xT, KC, w_kv, 0, 256, tgs, k_epi, nblk=256)

            def v_epi(ps, tt, c0, nb):
                ctx.op("act", lambda: nc.scalar.copy(vS[:, 1 + tt, :], ps[:, 0:nb]), reads=[ps], writes=[vS])
            self.gemm_tok(res, xT, xT, KC, w_kv, 256, 256, range(NT), v_epi, nblk=256)
            ctx.barrier()
        with contextlib.ExitStack() as st:
            if NSEG == 1:
                ctx.op("dve", lambda: nc.vector.memset(kT[:, :, 0:P], 0.0), writes=[kT])
                ctx.op("dve", lambda: nc.vector.memset(vS[:, 0, :], 0.0), writes=[vS])
            else:
                HCI = self.scr("swa_ci", [NSEG, P, 512])
                HCO = self.scr("swa_co", [NSEG, P, 512])
                hb = ctx.sb(st, "hb", [P, 512], F32)
                hm = [ctx.sb(st, "hm", [P, 512], F32) for _ in range(2)]
                ctx.op("dve", lambda: nc.vector.tensor_copy(hb[:, 0:256].rearrange("p (a b) -> p a b", a=2), kT[:, :, T:T + P]),
                       reads=[kT], writes=[hb])
                ctx.op("dve", lambda: nc.vector.tensor_copy(hb[:, 256:512], vS[:, NT, :]), reads=[vS], writes=[hb])
                for s in range(NSEG):
                    ctx.op("dve", lambda: nc.vector.tensor_scalar_mul(hm[s % 2][:], hb[:], self.own[:, s:s + 1]),
                           reads=[hb, self.own], writes=[hm[s % 2]])
                    ctx.dma("sp", HCI.ap()[s], hm[s % 2][:], reads=[hm[s % 2]], writes=[("HCI", s)])
                ctx.barrier()
                ctx.allreduce(cfg.groups, HCI.ap().rearrange("s p e -> (s p) e"), HCO.ap().rearrange("s p e -> (s p) e"),
                              writes=[("HCO",)])
                ctx.barrier()
                for s in range(NSEG):
                    ctx.dma("sp", hm[s % 2][:], HCO.ap()[s], writes=[hm[s % 2]])
                    if s == 0:
                        ctx.op("dve", lambda: nc.vector.tensor_scalar_mul(hb[:], hm[s % 2][:], self.halo_sel[:, s:s + 1]),
                               reads=[hm[s % 2], self.halo_sel], writes=[hb])
                    else:
                        ctx.op("dve", lambda: nc.vector.scalar_tensor_tensor(hb[:], hm[s % 2][:], self.halo_sel[:, s:s + 1],
                                                                             hb[:], ALU.mult, ALU.add),
                               reads=[hm[s % 2], self.halo_sel, hb], writes=[hb])
                ctx.op("dve", lambda: nc.vector.tensor_copy(kT[:, :, 0:P], hb[:, 0:256].rearrange("p (a b) -> p a b", a=2)),
                       reads=[hb], writes=[kT])
                ctx.op("dve", lambda: nc.vector.tensor_copy(vS[:, 0, :], hb[:, 256:512]), reads=[hb], writes=[vS])
            ctx.barrier()
        with contextlib.ExitStack() as st:
            qT = self.load_AT(st, "qT", QTs, KC, 0, T)
            ps_s = ctx.ps(st, "ps_s", [P, 8, 2 * P], F32)
            ps_t = ctx.ps(st, "ps_t", [P, 16, P], BF16)
            ps_o = ctx.ps(st, "ps_o", [P, 8, P], F32)
            s_sb = ctx.sb(st, "s_sb", [P, 8, 2 * P], F32)
            e_sb = ctx.sb(st, "e_sb", [P, 8, 2 * P], F32)
            p_sb = ctx.sb(st, "p_sb", [P, 8, 2 * P], BF16)
            pT = ctx.sb(st, "pT", [P, 16, P], BF16)
            mx = ctx.sb(st, "mx", [P, 8], F32)
            nmx = ctx.sb(st, "nmx", [P, 8], F32)
            rs = ctx.sb(st, "rs", [P, 8], F32)
            es = ctx.sb(st, "es", [P, 8], F32)
            G = 4 if NT % 4 == 0 else 2
            ost = [ctx.sb(st, "ost", [P, KC, G * P], BF16) for _ in range(2)]
            OTv = OT.ap().rearrange("(kc p) t -> p kc t", p=P)
            for n in range(NT):
                og = ost[(n // G) % 2]
                for pair in range(2):
                    for par in range(2):
                        hk = 2 * pair + par
                        po = par * 64
                        kc_k = hk // 2
                        for g in range(8):
                            ch = pair * 8 + g
                            ctx.op("pe", lambda: nc.tensor.matmul(ps_s[:, g, :], lhsT=qT[po:po + 64, ch, n * P:(n + 1) * P],
                                                                  rhs=kT[po:po + 64, kc_k, n * P:n * P + 2 * P],
                                                                  start=True, stop=True),
                                   reads=[qT, kT], writes=[ps_s])
                        ctx.op("dve", lambda: nc.vector.tensor_tensor(s_sb[:], ps_s[:], biasS[:, hk * 8:(hk + 1) * 8, :], ALU.add),
                               reads=[ps_s, biasS], writes=[s_sb])
                        if n == 0:
                            ctx.op("pool", lambda: nc.gpsimd.tensor_tensor(s_sb[:], s_sb[:],
                                                                           mfirst[:].unsqueeze(1).broadcast_to([P, 8, 2 * P]), ALU.add),
                                   reads=[s_sb, mfirst], writes=[s_sb])
                        ctx.op("dve", lambda: nc.vector.tensor_reduce(mx[:], s_sb[:], AX.X, ALU.max), reads=[s_sb], writes=[mx])
                        ctx.op("dve", lambda: nc.vector.tensor_tensor(mx[:], mx[:], sinkb[:, hk * 8:(hk + 1) * 8], ALU.max),
                               reads=[mx, sinkb], writes=[mx])
                        ctx.op("dve", lambda: nc.vector.tensor_scalar_mul(nmx[:], mx[:], -1.0), reads=[mx], writes=[nmx])
                        ctx.op("dve", lambda: nc.vector.memset(rs[:], 0.0), writes=[rs])
                        for g in range(8):
                            ctx.op("act", lambda: nc.scalar.activation(e_sb[:, g, :], s_sb[:, g, :], AF.Exp, bias=nmx[:, g:g + 1],
                                                                       scale=1.0, accum_out=rs[:, g:g + 1]),
                                   reads=[s_sb, nmx], writes=[e_sb, rs])
                        ctx.op("dve", lambda: nc.vector.tensor_tensor(es[:], sinkb[:, hk * 8:(hk + 1) * 8], mx[:], ALU.subtract),
                               reads=[sinkb, mx], writes=[es])
                        ctx.op("act", lambda: nc.scalar.activation(es[:], es[:], AF.Exp), reads=[es], writes=[es])
                        ctx.op("dve", lambda: nc.vector.tensor_tensor(rs[:], rs[:], es[:], ALU.add), reads=[rs, es], writes=[rs])
                        ctx.op("dve", lambda: nc.vector.reciprocal(rs[:], rs[:]), reads=[rs], writes=[rs])
                        ctx.op("pool", lambda: nc.gpsimd.tensor_tensor(p_sb[:], e_sb[:], rs[:].unsqueeze(2).broadcast_to([P, 8, 2 * P]),
                                                                       ALU.mult), reads=[e_sb, rs], writes=[p_sb])
                        for g in range(8):
                            for hf in range(2):
                                self.transpose_to(p_sb[:, g, hf * P:(hf + 1) * P], ps_t[:, g * 2 + hf, :], [p_sb], [ps_t])
                        ctx.op("act", lambda: nc.scalar.copy(pT[:], ps_t[:]), reads=[ps_t], writes=[pT])
                        for g in range(8):
                            for hf in range(2):
                                ctx.op("pe", lambda: nc.tensor.matmul(ps_o[po:po + 64, g, :], lhsT=vS[:, n + hf, hk * 64:(hk + 1) * 64],
                                                                      rhs=pT[:, g * 2 + hf, :], start=(hf == 0), stop=(hf == 1)),
                                       reads=[vS, pT], writes=[ps_o])
                    ctx.op("dve", lambda: nc.vector.tensor_copy(og[:, pair * 8:(pair + 1) * 8, (n % G) * P:(n % G + 1) * P], ps_o[:]),
                           reads=[ps_o], writes=[og])
                if n % G == G - 1:
                    g0 = (n // G) * G * P
                    ctx.dma("sp", OTv[:, :, g0:g0 + G * P], og[:], reads=[og], writes=[("OT", n)])
            ctx.barrier()
    with contextlib.ExitStack() as st:
        aT = self.load_AT(st, "oTa", OT, KC, 0, T)
        res = GemmRes(self, st, KC, 512, 3)
        epi = self.epi_resid(st, X, Z1)
        self.gemm_tok(res, aT, aT, KC, w_out, 0, D, range(NT), epi)
        ctx.barrier()


Prog.swa_layer = _swa_layer
RW_H, RW_HD = 32, 64
RW_EPS = 64e-5
RW_NHB = RW_H // 2


def _rwkv_consts(cfg, core, c):
    seg = core % cfg.NSEG
    f32 = np.float32
    s = np.arange(P)[:, None]
    t = np.arange(P)[None, :]
    c["rw_mus"] = (s < t).astype(f32)
    c["rw_mui"] = (s <= t).astype(f32)
    c["rw_mls"] = (s > t).astype(f32)
    c["rw_tri"] = (s <= t).astype(f32)
    c["rw_suf"] = (s > t).astype(f32)
    i2 = np.zeros((P, 64), f32)
    i2[np.arange(P), np.arange(P) % 64] = 1.0
    c["rw_i2"] = i2
    selm = np.zeros((P, cfg.NSEG), f32)
    selm[:, :seg] = 1.0
    c["rw_selm"] = selm
    c["rw_nselm"] = 1.0 - selm


def _rwkv_layer(self, layer, X, XT, Z1):
    ctx, nc, cfg = self.ctx, self.nc, self.cfg
    T, NT, NSEG = cfg.T, cfg.NT, cfg.NSEG
    j = layer // 3
    gi = lambda name, shape: self.inp("%s_%d" % (name, j), shape).ap()
    tw_ = lambda nm, fn, K_, N_, t_, kp_=P: self.tw("%s_t%d" % (nm, j), fn, K_, N_, t_, kp_)
    w_rkv = [tw_("rwkv_w_rkv%d" % i, (lambda inp, i=i: inp["rwkv_w_rkv"][j][i]), D, D, 256) for i in range(3)]
    w1 = tw_("rwkv_w1", lambda inp: inp["rwkv_w1"][j], D, 96, 96)
    w2 = tw_("rwkv_w2", lambda inp: inp["rwkv_w2"][j], 96, D, 256, 96)
    a1 = tw_("rwkv_a1", lambda inp: inp["rwkv_a1"][j], D, 96, 96)
    a2 = tw_("rwkv_a2", lambda inp: inp["rwkv_a2"][j], 96, D, 256, 96)
    g1 = tw_("rwkv_g1", lambda inp: inp["rwkv_g1"][j], D, 256, 256)
    g2 = tw_("rwkv_g2", lambda inp: inp["rwkv_g2"][j], 256, D, 256)
    w_out = tw_("rwkv_w_out", lambda inp: inp["rwkv_w_out"][j], D, D, 512)
    mix_ap = gi("rwkv_mix", [6, D]).rearrange("i (kc p) -> (i kc) p", p=P)
    Rs, Ks, Vs = self.scr("rw_R", [T, D]), self.scr("rw_K", [T, D]), self.scr("rw_V", [T, D])
    WLs, ALs, Gs = self.scr("rw_WL", [T, D]), self.scr("rw_AL", [T, D]), self.scr("rw_G", [T, D])
    Y0 = self.scr("rw_Y0", [T, D])
    BON = self.scr("rw_BON", [T, RW_H])
    YTR = self.scr("rw_YTR", [NT, P, RW_NHB, P], BF16)
    OGT = self.scr("rw_OGT", [D, T], BF16)
    TGW = min(512, T)
    tgs = [(t0, TGW) for t0 in range(0, T, TGW)]

    with contextlib.ExitStack() as st:
        psc = ctx.ps(st, "psc", [P, 512], F32)
        mixc = self.load_cols(st, "mixc", mix_ap, 6 * KC, psc)
        xT = self.load_AT(st, "xT", XT, KC, 0, T, pad=1)
        with contextlib.ExitStack() as sth:
            halo = self.halo_rows(sth, X, 1, "halo1")
            for kc in range(KC):
                ctx.op("pe", lambda: nc.tensor.matmul(psc[:, kc:kc + 1], lhsT=halo[0:1, kc * P:(kc + 1) * P],
                                                      rhs=self.identf[0:1, 0:1], start=True, stop=True),
                       reads=[halo, self.identf], writes=[psc])
            ctx.op("dve", lambda: nc.vector.tensor_copy(xT[:, :, 0:1], psc[:, 0:KC].unsqueeze(2)), reads=[psc], writes=[xT])
            ctx.barrier()
        xm = ctx.sb(st, "xm", [P, KC, T], BF16)
        dtmp = [ctx.sb(st, "dtmp", [P, T], F32) for _ in range(2)]
        hT = ctx.sb(st, "hT", [P, 2, T], BF16)
        res = GemmRes(self, st, KC, 256, 4)
        obuf = [ctx.sb(st, "obuf", [P, 512], F32) for _ in range(3)]
        ocnt = [0]

        def store_epi(dst):
            def epi(ps, tt, c0, nb):
                o = obuf[ocnt[0] % 3]
                ocnt[0] += 1
                ctx.op("act", lambda: nc.scalar.copy(o[:, 0:nb], ps[:, 0:nb]), reads=[ps], writes=[o])
                ctx.dma("sp", dst.ap()[tt * P:(tt + 1) * P, c0:c0 + nb], o[:, 0:nb], reads=[o], writes=[("o", id(dst), tt, c0)])
            return epi

        def build_mix(i):
            for kc in range(KC):
                d = dtmp[kc % 2]
                ctx.op("pool", lambda: nc.gpsimd.tensor_tensor(d[:], xT[:, kc, 0:T], xT[:, kc, 1:T + 1], ALU.subtract),
                       reads=[xT], writes=[d])
                ctx.op("dve", lambda: nc.vector.scalar_tensor_tensor(xm[:, kc, :], d[:], mixc[:, i * KC + kc:i * KC + kc + 1],
                                                                     xT[:, kc, 1:T + 1], ALU.mult, ALU.add),
                       reads=[d, mixc, xT], writes=[xm])

        def lora(i, wa, na, func, wb_, dst):
            build_mix(i)
            kcn2 = (na + P - 1) // P
            kp = min(P, na)

            def h_epi(ps, c, t0, tn):
                fw = min(P, na - c)
                ctx.op("act", lambda: nc.scalar.activation(hT[0:fw, c // P, t0:t0 + tn], ps[0:fw, 0:tn], func),
                       reads=[ps], writes=[hT])
            self.gemm_feat(res, xm, xm, KC, wa, 0, na, tgs, h_epi, nblk=256)
            self.gemm_tok(res, hT, hT, kcn2, wb_, 0, D, range(NT), store_epi(dst), kp=kp, nblk=256)

        build_mix(0)
        self.gemm_tok(res, xm, xm, KC, w_rkv[0], 0, D, range(NT), store_epi(Rs), nblk=256)
        build_mix(2)
        self.gemm_tok(res, xm, xm, KC, w_rkv[1], 0, D, range(NT), store_epi(Ks), nblk=256)
        build_mix(3)
        self.gemm_tok(res, xm, xm, KC, w_rkv[2], 0, D, range(NT), store_epi(Vs), nblk=256)
        lora(1, w1, 96, AF.Tanh, w2, WLs)
        lora(4, a1, 96, AF.Identity, a2, ALs)
        lora(5, g1, 256, AF.Sigmoid, g2, Gs)
        ctx.barrier()
    if cfg.stop == "rw1":
        return

    SXs = self.scr("rw_SX", [P, RW_NHB, P])
    with contextlib.ExitStack() as st:
        def cload(name, shape, dtype=F32):
            t_ = ctx.sb(st, name, shape, dtype)
            ctx.dma("sp", t_[:], self.inp(name, shape, dtype).ap(), writes=[t_])
            return t_
        mus, mui, mls = cload("rw_mus", [P, P]), cload("rw_mui", [P, P]), cload("rw_mls", [P, P])
        tri, suft = cload("rw_tri", [P, P]), cload("rw_suf", [P, P])
        i2 = cload("rw_i2", [P, 64])
        ones = ctx.sb(st, "ones", [P, 1], F32)
        ctx.op("dve", lambda: nc.vector.memset(ones[:], 1.0), writes=[ones])
        w0b = self.bcast_rows(st, "w0b", gi("rwkv_w0", [D]), D)
        a0b = self.bcast_rows(st, "a0b", gi("rwkv_a0", [D]), D)
        kkb = self.bcast_rows(st, "kkb", gi("rwkv_k_k", [D]), D)
        kab = self.bcast_rows(st, "kab", gi("rwkv_k_a", [D]), D)
        rkb = self.bcast_rows(st, "rkb", gi("rwkv_r_k", [RW_H, RW_HD]).rearrange("h d -> (h d)"), D)
        A = ctx.sb(st, "A", [P, D], F32)
        B = ctx.sb(st, "B", [P, D], F32)
        Dw = ctx.sb(st, "Dw", [P, D], F32)
        Ea = ctx.sb(st, "Ea", [P, D], F32)
        Fk = ctx.sb(st, "Fk", [P, D], F32)
        T1 = ctx.sb(st, "T1", [P, D], F32)
        ET = [ctx.sb(st, "ET", [P, 512], F32) for _ in range(4)]
        tok = [ctx.sb(st, "tokb", [P, D], BF16) for _ in range(4)]
        bbk = ctx.sb(st, "bbk", [P, 2, D], BF16)
        vbx = ctx.sb(st, "vbx", [P, RW_H, P], BF16)
        ctx.op("pool", lambda: nc.gpsimd.memset(vbx[:], 0.0), writes=[vbx])
        CM = ctx.sb(st, "CM", [P, RW_NHB, 4, P], BF16)
        ytile = ctx.sb(st, "ytile", [P, D], F32)
        T2 = ytile
        ytr = ctx.sb(st, "ytr", [P, RW_NHB, P], BF16)
        ss = ctx.sb(st, "ss", [P, RW_H], F32)
        bon = ctx.sb(st, "bon", [P, RW_H], F32)
        dectot = ctx.sb(st, "dectot", [P, RW_NHB], F32)
        SX = ctx.sb(st, "SX", [P, RW_NHB, P], F32)
        SXb = ctx.sb(st, "SXb", [P, RW_NHB, P], BF16)
        NPAIR = 2

        class PairRes:
            pass
        PR = []
        for _ in range(NPAIR):
            r_ = PairRes()
            r_.GA = [ctx.sb(st, "GA", [P, 2, 2, P], BF16) for _ in range(2)]
            r_.Brb = ctx.sb(st, "Brb", [P, 2, P], BF16)
            r_.Aak = ctx.sb(st, "Aak", [P, 2, P], BF16)
            r_.Brk = ctx.sb(st, "Brk", [P, 2, P], BF16)
            r_.Tt = [ctx.sb(st, "Tt", [P, 2, P], BF16) for _ in range(2)]
            r_.Wb = ctx.sb(st, "Wb", [P, 2, P], BF16)
            r_.Ub = ctx.sb(st, "Ub", [P, 2, P], BF16)
            r_.H = [ctx.ps(st, "pbH", [P, 512], F32) for _ in range(2)]
            r_.S = ctx.ps(st, "pbS", [P, 512], F32)
            PR.append(r_)
        pb = [PR[0].H[0], PR[0].H[1], PR[0].S, PR[1].H[0], PR[1].H[1]]
        pbx = ctx.ps(st, "pbx", [P, 512], F32)
        ptr = ctx.ps(st, "ptr", [P, 8, P], BF16)
        sxk = [(id(SX), hb) for hb in range(RW_NHB)]
        sxbk = [(id(SXb), hb) for hb in range(RW_NHB)]
        ctx.op("dve", lambda: nc.vector.memset(SX[:], 0.0), writes=sxk)
        ctx.op("dve", lambda: nc.vector.tensor_copy(SX[:, :, 64:128], i2[:].unsqueeze(1).broadcast_to([P, RW_NHB, 64])),
               reads=[i2] + sxk, writes=sxk)
        ctx.op("pool", lambda: nc.gpsimd.tensor_copy(SXb[:], SX[:]), reads=sxk, writes=sxbk)
        v3 = lambda t_: t_[:].rearrange("p (h d) -> p h d", d=RW_HD)
        bc3 = lambda small: small[:].unsqueeze(2).broadcast_to([P, RW_H, RW_HD])
        for n in range(NT):
            rows = slice(n * P, (n + 1) * P)
            ctx.dma("sp", A[:], Rs.ap()[rows, :], writes=[A])
            ctx.dma("sp", B[:], Ks.ap()[rows, :], writes=[B])
            ctx.dma("sp", T1[:], Vs.ap()[rows, :], writes=[T1])
            ctx.op("act", lambda: nc.scalar.copy(vbx[:, :, 0:64], v3(T1)), reads=[T1], writes=[vbx])
            ctx.dma("sp", Dw[:], WLs.ap()[rows, :], writes=[Dw])
            ctx.dma("sp", Ea[:], ALs.ap()[rows, :], writes=[Ea])
            ctx.op("dve", lambda: nc.vector.tensor_tensor(Dw[:], Dw[:], w0b[:], ALU.add), reads=[Dw, w0b], writes=[Dw])
            ctx.op("act", lambda: nc.scalar.activation(Dw[:], Dw[:], AF.Sigmoid), reads=[Dw], writes=[Dw])
            ctx.op("dve", lambda: nc.vector.tensor_scalar_mul(Dw[:], Dw[:], -math.exp(-0.5)), reads=[Dw], writes=[Dw])
            ctx.op("dve", lambda: nc.vector.tensor_tensor(Ea[:], Ea[:], a0b[:], ALU.add), reads=[Ea, a0b], writes=[Ea])
            ctx.op("act", lambda: nc.scalar.activation(Ea[:], Ea[:], AF.Sigmoid), reads=[Ea], writes=[Ea])
            ctx.op("pool", lambda: nc.gpsimd.tensor_tensor(Fk[:], B[:], kkb[:], ALU.mult), reads=[B, kkb], writes=[Fk])
            ctx.op("pool", lambda: nc.gpsimd.tensor_tensor(T2[:], Fk[:], Fk[:], ALU.mult), reads=[Fk], writes=[T2])
            ctx.op("dve", lambda: nc.vector.tensor_reduce(ss[:], v3(T2), AX.X, ALU.add), reads=[T2], writes=[ss])
            ctx.op("act", lambda: nc.scalar.activation(ss[:], ss[:], AF.Sqrt), reads=[ss], writes=[ss])
            ctx.op("dve", lambda: nc.vector.tensor_scalar_max(ss[:], ss[:], 1e-12), reads=[ss], writes=[ss])
            ctx.op("dve", lambda: nc.vector.reciprocal(ss[:], ss[:]), reads=[ss], writes=[ss])
            ctx.op("dve", lambda: nc.vector.tensor_tensor(v3(Fk), v3(Fk), bc3(ss), ALU.mult), reads=[Fk, ss], writes=[Fk])
            ctx.op("dve", lambda: nc.vector.scalar_tensor_tensor(T1[:], Ea[:], -1.0, kab[:], ALU.add, ALU.mult),
                   reads=[Ea, kab], writes=[T1])
            ctx.op("pool", lambda: nc.gpsimd.tensor_tensor(T1[:], T1[:], B[:], ALU.mult), reads=[T1, B], writes=[T1])
            ctx.op("pool", lambda: nc.gpsimd.tensor_tensor(B[:], B[:], T1[:], ALU.add), reads=[T1, B], writes=[B])
            ctx.op("pool", lambda: nc.gpsimd.tensor_tensor(T2[:], A[:], B[:], ALU.mult), reads=[A, B, T2], writes=[T2])
            ctx.op("dve", lambda: nc.vector.tensor_tensor(T2[:], T2[:], rkb[:], ALU.mult), reads=[T2, rkb], writes=[T2])
            ctx.op("dve", lambda: nc.vector.tensor_reduce(bon[:], v3(T2), AX.X, ALU.add), reads=[T2], writes=[bon])
            ctx.dma("sp", BON.ap()[rows, :], bon[:], reads=[bon], writes=[("BON", n)])
            ctx.op("dve", lambda: nc.vector.tensor_tensor(T1[:], Fk[:], Ea[:], ALU.mult), reads=[Fk, Ea, T1], writes=[T1])
            for hb in range(RW_NHB):
                ctx.op("pe", lambda: nc.tensor.matmul(pbx[:, hb:hb + 1], lhsT=Dw[:, hb * P:(hb + 1) * P], rhs=ones[:, 0:1],
                                                      start=True, stop=True), reads=[Dw, ones], writes=[pbx])
            ctx.op("act", lambda: nc.scalar.activation(dectot[:], pbx[:, 0:RW_NHB], AF.Exp), reads=[pbx], writes=[dectot])
            for cb in range(4):
                cs = slice(cb * 512, (cb + 1) * 512)
                pcum, psuf = pb[1 + (cb % 2) * 2], pb[2 + (cb % 2) * 2]
                ctx.op("pe", lambda: nc.tensor.matmul(pcum[:], lhsT=tri[:], rhs=Dw[:, cs], start=True, stop=True),
                       reads=[tri, Dw], writes=[pcum])
                ctx.op("pe", lambda: nc.tensor.matmul(psuf[:], lhsT=suft[:], rhs=Dw[:, cs], start=True, stop=True),
                       reads=[suft, Dw], writes=[psuf])
                ctx.op("act", lambda: nc.scalar.activation(ET[0][:], pcum[:], AF.Exp), reads=[pcum], writes=[ET[0]])
                ctx.op("pool", lambda: nc.gpsimd.tensor_tensor(tok[3][:, cs], A[:, cs], ET[0][:], ALU.mult),
                       reads=[A, ET[0]], writes=[tok[3]])
                ctx.op("act", lambda: nc.scalar.activation(ET[1][:], pcum[:], AF.Exp, scale=-1.0), reads=[pcum], writes=[ET[1]])
                ctx.op("dve", lambda: nc.vector.tensor_tensor(tok[0][:, cs], T1[:, cs], ET[1][:], ALU.mult),
                       reads=[T1, ET[1]], writes=[tok[0]])
                ctx.op("pool", lambda: nc.gpsimd.tensor_tensor(tok[1][:, cs], B[:, cs], ET[1][:], ALU.mult),
                       reads=[B, ET[1]], writes=[tok[1]])
                ctx.op("dve", lambda: nc.vector.tensor_tensor(ET[2][:], pcum[:], Dw[:, cs], ALU.subtract),
                       reads=[pcum, Dw], writes=[ET[2]])
                ctx.op("act", lambda: nc.scalar.activation(ET[2][:], ET[2][:], AF.Exp), reads=[ET[2]], writes=[ET[2]])
                ctx.op("dve", lambda: nc.vector.scalar_tensor_tensor(tok[2][:, cs], Fk[:, cs], -1.0, ET[2][:], ALU.mult, ALU.mult),
                       reads=[Fk, ET[2]], writes=[tok[2]])
                ctx.op("act", lambda: nc.scalar.activation(ET[3][:], psuf[:], AF.Exp), reads=[psuf], writes=[ET[3]])
                ctx.op("dve", lambda: nc.vector.tensor_tensor(bbk[:, 0, cs], T1[:, cs], ET[3][:], ALU.mult),
                       reads=[T1, ET[3]], writes=[bbk])
                ctx.op("pool", lambda: nc.gpsimd.tensor_tensor(bbk[:, 1, cs], B[:, cs], ET[3][:], ALU.mult),
                       reads=[B, ET[3]], writes=[bbk])
            for kind in range(4):
                for half in range(2):
                    for jj in range(8):
                        hb = half * 8 + jj
                        self.transpose_to(tok[kind][:, hb * P:(hb + 1) * P], ptr[:, jj, :], [tok[kind]], [ptr])
                    if (kind + half) % 2 == 0:
                        ctx.op("act", lambda: nc.scalar.copy(CM[:, half * 8:(half + 1) * 8, kind, :], ptr[:]), reads=[ptr], writes=[CM])
                    else:
                        ctx.op("dve", lambda: nc.vector.tensor_copy(CM[:, half * 8:(half + 1) * 8, kind, :], ptr[:]), reads=[ptr], writes=[CM])
            def pair_gen(hb, R):
                H = R.H
                g0 = R.GA[0]
                hp = ((0, 0), (1, 64))
                for hi, po in hp:
                    rhs_ar = CM[po:po + 64, hb, 2:4, :].rearrange("p k t -> p (k t)")
                    ctx.op("pe", lambda: nc.tensor.matmul(H[hi][:, 0:256], lhsT=CM[po:po + 64, hb, 0, :], rhs=rhs_ar,
                                                          start=True, stop=True), reads=[CM], writes=[H[hi]])
                    ctx.op("pe", lambda: nc.tensor.matmul(H[hi][:, 256:384], lhsT=CM[po:po + 64, hb, 2, :],
                                                          rhs=CM[po:po + 64, hb, 0, :], start=True, stop=True),
                           reads=[CM], writes=[H[hi]])
                yield
                for hi, po in hp:
                    ctx.op("dve", lambda: nc.vector.tensor_tensor(g0[:, hi, 0, :], H[hi][:, 0:P], mus[:], ALU.mult),
                           reads=[H[hi], mus], writes=[g0])
                    ctx.op("dve", lambda: nc.vector.tensor_tensor(R.Brb[:, hi, :], H[hi][:, P:2 * P], mui[:], ALU.mult),
                           reads=[H[hi], mui], writes=[R.Brb])
                    ctx.op("dve", lambda: nc.vector.tensor_tensor(g0[:, hi, 1, :], H[hi][:, 2 * P:3 * P], mls[:], ALU.mult),
                           reads=[H[hi], mls], writes=[g0])
                    ctx.op("pool", lambda: nc.gpsimd.tensor_tensor(R.Tt[0][:, hi, :], g0[:, hi, 0, :], self.ident[:], ALU.add),
                           reads=[g0, self.ident], writes=[R.Tt[0]])
                yield
                if cfg.stop == "g1":
                    return
                tcur = 0
                for lvl in range(1, 7):
                    gc, gn = R.GA[(lvl - 1) % 2], R.GA[lvl % 2]
                    for hi, po in hp:
                        if lvl < 6:
                            ctx.op("pe", lambda: nc.tensor.matmul(H[hi][:, 0:P], lhsT=gc[:, hi, 1, :], rhs=gc[:, hi, 0, :],
                                                                  start=True, stop=True), reads=[gc], writes=[H[hi]])
                        ctx.op("pe", lambda: nc.tensor.matmul(H[hi][:, P:2 * P], lhsT=gc[:, hi, 0, :], rhs=gc[:, hi, 1, :],
                                                              start=True, stop=True), reads=[gc], writes=[H[hi]])
                    yield
                    for hi, po in hp:
                        if lvl < 6:
                            ctx.op("act", lambda: nc.scalar.copy(gn[:, hi].rearrange("p k t -> p (k t)"), H[hi][:, 0:2 * P]),
                                   reads=[H[hi]], writes=[gn])
                        else:
                            ctx.op("act", lambda: nc.scalar.copy(gn[:, hi, 1, :], H[hi][:, P:2 * P]), reads=[H[hi]], writes=[gn])
                    yield
                    for hi, po in hp:
                        ctx.op("pe", lambda: nc.tensor.matmul(H[hi][:, 2 * P:3 * P], lhsT=gn[:, hi, 1, :], rhs=R.Tt[tcur][:, hi, :],
                                                              start=True, stop=True), reads=[gn, R.Tt[tcur]], writes=[H[hi]])
                    yield
                    for hi, po in hp:
                        ctx.op("dve", lambda: nc.vector.tensor_tensor(R.Tt[1 - tcur][:, hi, :], H[hi][:, 2 * P:3 * P], R.Tt[tcur][:, hi, :],
                                                                      ALU.add), reads=[H[hi], R.Tt[tcur]], writes=[R.Tt[1 - tcur]])
                    tcur = 1 - tcur
                    yield
                TT = R.Tt[tcur]
                if cfg.stop == "g2":
                    return
                for hi, po in hp:
                    rhs_ar = CM[po:po + 64, hb, 2:4, :].rearrange("p k t -> p (k t)")
                    ctx.op("pe", lambda: nc.tensor.matmul(H[hi][:, 0:256], lhsT=CM[po:po + 64, hb, 1, :], rhs=rhs_ar,
                                                          start=True, stop=True), reads=[CM], writes=[H[hi]])
                    ctx.op("pe", lambda: nc.tensor.matmul(H[hi][:, 2 * P:3 * P], lhsT=CM[po:po + 64, hb, 2, :],
                                                          rhs=SXb[po:po + 64, hb, :], start=True, stop=False),
                           reads=[CM, (id(SXb), hb)], writes=[H[hi]])
                yield
                for hi, po in hp:
                    ctx.op("dve", lambda: nc.vector.tensor_tensor(R.Aak[:, hi, :], H[hi][:, 0:P], mus[:], ALU.mult),
                           reads=[H[hi], mus], writes=[R.Aak])
                    ctx.op("dve", lambda: nc.vector.tensor_tensor(R.Brk[:, hi, :], H[hi][:, P:2 * P], mui[:], ALU.mult),
                           reads=[H[hi], mui], writes=[R.Brk])
                yield
                for hi, po in hp:
                    h = 2 * hb + hi
                    ctx.op("pe", lambda: nc.tensor.matmul(H[hi][:, 2 * P:3 * P], lhsT=R.Aak[:, hi, :], rhs=vbx[:, h, :],
                                                          start=False, stop=True), reads=[R.Aak, vbx], writes=[H[hi]])
                yield
                for hi, po in hp:
                    ctx.op("act", lambda: nc.scalar.copy(R.Wb[:, hi, :], H[hi][:, 2 * P:3 * P]), reads=[H[hi]], writes=[R.Wb])
                yield
                for hi, po in hp:
                    ctx.op("pe", lambda: nc.tensor.matmul(H[hi][:, 3 * P:4 * P], lhsT=TT[:, hi, :], rhs=R.Wb[:, hi, :],
                                                          start=True, stop=True), reads=[TT, R.Wb], writes=[H[hi]])
                yield
                for hi, po in hp:
                    ctx.op("dve", lambda: nc.vector.tensor_copy(R.Ub[:, hi, :], H[hi][:, 3 * P:4 * P]), reads=[H[hi]], writes=[R.Ub])
                yield
                if cfg.stop == "g3":
                    return
                for hi, po in hp:
                    h = 2 * hb + hi
                    yo = H[hi][:, 0:64]
                    ctx.op("pe", lambda: nc.tensor.matmul(yo, lhsT=CM[po:po + 64, hb, 3, :], rhs=SXb[po:po + 64, hb, 0:64],
                                                          start=True, stop=False), reads=[CM, (id(SXb), hb)], writes=[H[hi]])
                    ctx.op("pe", lambda: nc.tensor.matmul(yo, lhsT=R.Brb[:, hi, :], rhs=R.Ub[:, hi, 0:64], start=False, stop=False),
                           reads=[R.Brb, R.Ub], writes=[H[hi]])
                    ctx.op("pe", lambda: nc.tensor.matmul(yo, lhsT=R.Brk[:, hi, :], rhs=vbx[:, h, 0:64], start=False, stop=True),
                           reads=[R.Brk, vbx], writes=[H[hi]])
                    if NSEG > 1:
                        to = H[hi][po:po + 64, P:2 * P]
                        ctx.op("pe", lambda: nc.tensor.matmul(to, lhsT=SXb[po:po + 64, hb, 64:128], rhs=CM[po:po + 64, hb, 3, :],
                                                              start=True, stop=False), reads=[CM, (id(SXb), hb)], writes=[H[hi]])
                        ctx.op("pe", lambda: nc.tensor.matmul(to, lhsT=R.Ub[:, hi, 64:128], rhs=R.Brb[:, hi, :], start=False, stop=True),
                               reads=[R.Brb, R.Ub], writes=[H[hi]])
                    so = R.S[po:po + 64, 0:P]
                    ctx.op("pe", lambda: nc.tensor.matmul(so, lhsT=bbk[:, 0, h * 64:(h + 1) * 64], rhs=R.Ub[:, hi, :], start=True, stop=False),
                           reads=[bbk, R.Ub], writes=[R.S])
                    ctx.op("pe", lambda: nc.tensor.matmul(so, lhsT=bbk[:, 1, h * 64:(h + 1) * 64], rhs=vbx[:, h, :], start=False, stop=True),
                           reads=[bbk, vbx], writes=[R.S])
                yield
                for hi, po in hp:
                    ctx.op("act", lambda: nc.scalar.copy(ytile[:, hb * P + hi * 64:hb * P + (hi + 1) * 64], H[hi][:, 0:64]),
                           reads=[H[hi]], writes=[(id(ytile), hb)])
                    if NSEG > 1:
                        ctx.op("act", lambda: nc.scalar.copy(ytr[po:po + 64, hb, :], H[hi][po:po + 64, P:2 * P]),
                               reads=[H[hi]], writes=[(id(ytr), hb)])
                ctx.op("dve", lambda: nc.vector.scalar_tensor_tensor(SX[:, hb, :], SX[:, hb, :], dectot[:, hb:hb + 1],
                                                                     R.S[:, 0:P], ALU.mult, ALU.add),
                       reads=[(id(SX), hb), dectot, R.S], writes=[(id(SX), hb)])
                ctx.op("pool", lambda: nc.gpsimd.tensor_copy(SXb[:, hb, :], SX[:, hb, :]), reads=[(id(SX), hb)], writes=[(id(SXb), hb)])
                yield

            if cfg.stop != "rw2p":
                for g0_ in range(0, RW_NHB, NPAIR):
                    gens = [pair_gen(hb, PR[i]) for i, hb in enumerate(range(g0_, min(RW_NHB, g0_ + NPAIR)))]
                    while gens:
                        for g_ in list(gens):
                            try:
                                next(g_)
                            except StopIteration:
                                gens.remove(g_)
            all_hb = list(range(RW_NHB))
            ctx.dma("sp", Y0.ap()[rows, :], ytile[:], reads=[ytile] + [(id(ytile), hb) for hb in all_hb], writes=[("Y0", n)])
            if NSEG > 1:
                ctx.dma("sp", YTR.ap()[n], ytr[:], reads=[ytr] + [(id(ytr), hb) for hb in all_hb], writes=[("YTR", n)])
        if NSEG > 1:
            ctx.dma("sp", SXs.ap(), SX[:], reads=[SX] + [(id(SX), hb) for hb in range(RW_NHB)], writes=[("SXs",)])
        ctx.barrier()

    if cfg.stop in ("rw2", "rw2p", "g1", "g2", "g3"):
        return
    S0b = ctx.sb(self.top, "rw_S0b_%d" % layer, [P, RW_NHB, 64], BF16)
    if NSEG > 1:
        NSL = NSEG - 1
        CI = self.scr("rw_ci", [NSL, P, RW_NHB * P])
        CO = self.scr("rw_co", [NSL, P, RW_NHB * P])
        with contextlib.ExitStack() as st:
            sx = ctx.sb(st, "sx", [P, RW_NHB * P], F32)
            sm = [ctx.sb(st, "sm", [P, RW_NHB * P], F32) for _ in range(2)]
            ctx.dma("sp", sx[:], SXs.ap().rearrange("p h k -> p (h k)"), writes=[sx])
            for s in range(NSL):
                ctx.op("dve", lambda: nc.vector.tensor_scalar_mul(sm[s % 2][:], sx[:], self.own[:, s:s + 1]),
                       reads=[sx, self.own], writes=[sm[s % 2]])
                ctx.dma("sp", CI.ap()[s], sm[s % 2][:], reads=[sm[s % 2]], writes=[("CI", s)])
            ctx.barrier()
            ctx.allreduce(cfg.groups, CI.ap().rearrange("s p e -> (s p) e"), CO.ap().rearrange("s p e -> (s p) e"),
                          writes=[("CO",)])
            ctx.barrier()
            selm = ctx.sb(st, "selm", [P, NSEG], F32)
            nselm = ctx.sb(st, "nselm", [P, NSEG], F32)
            ctx.dma("sp", selm[:], self.inp("rw_selm", [P, NSEG]).ap(), writes=[selm])
            ctx.dma("sp", nselm[:], self.inp("rw_nselm", [P, NSEG]).ap(), writes=[nselm])
            i2 = ctx.sb(st, "i2", [P, 64], F32)
            ctx.dma("sp", i2[:], self.inp("rw_i2", [P, 64]).ap(), writes=[i2])
            S0 = ctx.sb(st, "S0", [P, RW_NHB, 64], F32)
            ctx.op("dve", lambda: nc.vector.memset(S0[:], 0.0), writes=[S0])
            Mp = ctx.sb(st, "Mp", [P, RW_NHB, 64], F32)
            Lp = ctx.sb(st, "Lp", [P, RW_NHB, 64], F32)
            MT = ctx.sb(st, "MT", [P, RW_NHB, 64], F32)
            pm = [ctx.ps(st, "pm", [P, 8, 64], F32) for _ in range(2)]
            for s in range(NSL):
                slot = sm[s % 2]
                ctx.dma("sp", slot[:], CO.ap()[s], writes=[slot])
                sv = slot[:].rearrange("p (h k) -> p h k", k=P)
                ctx.op("dve", lambda: nc.vector.tensor_scalar_mul(Lp[:], sv[:, :, 0:64], selm[:, s:s + 1]),
                       reads=[slot, selm], writes=[Lp])
                ctx.op("dve", lambda: nc.vector.tensor_scalar_mul(Mp[:], sv[:, :, 64:128], selm[:, s:s + 1]),
                       reads=[slot, selm], writes=[Mp])
                ctx.op("dve", lambda: nc.vector.scalar_tensor_tensor(Mp[:], i2[:].unsqueeze(1).broadcast_to([P, RW_NHB, 64]),
                                                                     nselm[:, s:s + 1], Mp[:], ALU.mult, ALU.add),
                       reads=[i2, nselm, Mp], writes=[Mp])
                for half in range(2):
                    for jj in range(8):
                        hb = half * 8 + jj
                        for hi, po in enumerate((0, 64)):
                            ctx.op("pe", lambda: nc.tensor.matmul(pm[hi][po:po + 64, jj, :], lhsT=Mp[po:po + 64, hb, :],
                                                                  rhs=self.identf[po:po + 64, po:po + 64], start=True, stop=True),
                                   reads=[Mp, self.identf], writes=[pm[hi]])
                    for hi, po in enumerate((0, 64)):
                        ctx.op("act", lambda: nc.scalar.copy(MT[po:po + 64, half * 8:(half + 1) * 8, :], pm[hi][po:po + 64]),
                               reads=[pm[hi]], writes=[MT])
                for half in range(2):
                    for jj in range(8):
                        hb = half * 8 + jj
                        for hi, po in enumerate((0, 64)):
                            ctx.op("pe", lambda: nc.tensor.matmul(pm[hi][po:po + 64, jj, :], lhsT=MT[po:po + 64, hb, :],
                                                                  rhs=S0[po:po + 64, hb, :], start=True, stop=True),
                                   reads=[MT, S0], writes=[pm[hi]])
                    for hi, po in enumerate((0, 64)):
                        ctx.op("dve", lambda: nc.vector.tensor_tensor(S0[po:po + 64, half * 8:(half + 1) * 8, :], pm[hi][po:po + 64],
                                                                      Lp[po:po + 64, half * 8:(half + 1) * 8, :], ALU.add),
                               reads=[pm[hi], Lp, S0], writes=[S0])
            ctx.op("act", lambda: nc.scalar.copy(S0b[:], S0[:]), reads=[S0], writes=[S0b])
            ctx.barrier()

    if cfg.stop == "rw3":
        return
    with contextlib.ExitStack() as st:
        gnb = self.bcast_rows(st, "gnb", gi("rwkv_gn_gain", [D]), D)
        gbb = self.bcast_rows(st, "gbb", gi("rwkv_gn_bias", [D]), D)
        y = ctx.sb(st, "y", [P, D], F32)
        vv = ctx.sb(st, "vv", [P, D], F32)
        gg = ctx.sb(st, "gg", [P, D], F32)
        sq = ctx.sb(st, "sq", [P, D], F32)
        bon = ctx.sb(st, "bon", [P, RW_H], F32)
        s1 = ctx.sb(st, "s1", [P, RW_H], F32)
        s2 = ctx.sb(st, "s2", [P, RW_H], F32)
        ytr = ctx.sb(st, "ytr", [P, RW_NHB, P], BF16)
        ogb = [ctx.sb(st, "ogb", [P, D], BF16) for _ in range(2)]
        G = 4 if NT % 4 == 0 else 2
        stg = [ctx.sb(st, "stg", [P, KC, G * P], BF16) for _ in range(2)]
        pst = [[ctx.ps(st, "pst", [P, 8 * P], BF16) for _ in range(2)] for _ in range(2)]
        pc = [ctx.ps(st, "pc", [P, 512], F32) for _ in range(4)]
        OGTv = OGT.ap().rearrange("(kc p) t -> p kc t", p=P)
        v3 = lambda t_: t_[:].rearrange("p (h d) -> p h d", d=RW_HD)
        bc3 = lambda small: small[:].unsqueeze(2).broadcast_to([P, RW_H, RW_HD])
        for n in range(NT):
            rows = slice(n * P, (n + 1) * P)
            ctx.dma("sp", y[:], Y0.ap()[rows, :], writes=[y])
            ctx.dma("sp", vv[:], Vs.ap()[rows, :], writes=[vv])
            ctx.dma("sp", gg[:], Gs.ap()[rows, :], writes=[gg])
            ctx.dma("sp", bon[:], BON.ap()[rows, :], writes=[bon])
            if NSEG > 1:
                ctx.dma("sp", ytr[:], YTR.ap()[n], writes=[ytr])
                for q4 in range(4):
                    for jj in range(4):
                        hb = q4 * 4 + jj
                        for hi, po in enumerate((0, 64)):
                            pcc = pc[2 * (q4 % 2) + hi]
                            ctx.op("pe", lambda: nc.tensor.matmul(pcc[:, jj * 64:(jj + 1) * 64],
                                                                  lhsT=ytr[po:po + 64, hb, :], rhs=S0b[po:po + 64, hb, :],
                                                                  start=True, stop=True), reads=[ytr, S0b], writes=[pcc])
                    for hi in range(2):
                        pcc = pc[2 * (q4 % 2) + hi]
                        yv = y[:, q4 * 512:(q4 + 1) * 512].rearrange("p (j k) -> p j k", k=P)[:, :, hi * 64:(hi + 1) * 64]
                        ctx.op("dve", lambda: nc.vector.tensor_tensor(yv, yv, pcc[:, 0:256].rearrange("p (j k) -> p j k", k=64), ALU.add),
                               reads=[pcc, y], writes=[y])
            ctx.op("dve", lambda: nc.vector.tensor_reduce(s1[:], v3(y), AX.X, ALU.add), reads=[y], writes=[s1])
            ctx.op("dve", lambda: nc.vector.tensor_scalar_mul(s1[:], s1[:], 1.0 / RW_HD), reads=[s1], writes=[s1])
            ctx.op("dve", lambda: nc.vector.tensor_tensor(v3(y), v3(y), bc3(s1), ALU.subtract), reads=[y, s1], writes=[y])
            ctx.op("pool", lambda: nc.gpsimd.tensor_tensor(sq[:], y[:], y[:], ALU.mult), reads=[y], writes=[sq])
            ctx.op("dve", lambda: nc.vector.tensor_reduce(s2[:], v3(sq), AX.X, ALU.add), reads=[sq], writes=[s2])
            ctx.op("dve", lambda: nc.vector.tensor_scalar(s2[:], s2[:], 1.0 / RW_HD, RW_EPS, ALU.mult, ALU.add), reads=[s2], writes=[s2])
            ctx.op("act", lambda: nc.scalar.activation(s2[:], s2[:], AF.Sqrt), reads=[s2], writes=[s2])
            ctx.op("dve", lambda: nc.vector.reciprocal(s2[:], s2[:]), reads=[s2], writes=[s2])
            ctx.op("dve", lambda: nc.vector.tensor_tensor(v3(y), v3(y), bc3(s2), ALU.mult), reads=[y, s2], writes=[y])
            ctx.op("pool", lambda: nc.gpsimd.tensor_tensor(y[:], y[:], gnb[:], ALU.mult), reads=[y, gnb], writes=[y])
            ctx.op("pool", lambda: nc.gpsimd.tensor_tensor(y[:], y[:], gbb[:], ALU.add), reads=[y, gbb], writes=[y])
            ctx.op("dve", lambda: nc.vector.tensor_tensor(v3(vv), v3(vv), bc3(bon), ALU.mult), reads=[vv, bon], writes=[vv])
            ctx.op("pool", lambda: nc.gpsimd.tensor_tensor(y[:], y[:], vv[:], ALU.add), reads=[y, vv], writes=[y])
            ob = ogb[n % 2]
            ctx.op("dve", lambda: nc.vector.tensor_tensor(ob[:], y[:], gg[:], ALU.mult), reads=[y, gg], writes=[ob])
            g_, gi_ = divmod(n, G)
            self.xt_emit_tile(ob, ob, stg[g_ % 2], stg[g_ % 2], gi_ * P, pst[n % 2])
            if gi_ == G - 1:
                ctx.dma("sp", OGTv[:, :, g_ * G * P:(g_ + 1) * G * P], stg[g_ % 2][:], reads=[stg[g_ % 2]], writes=[("OGT", g_)])
        ctx.barrier()

    if cfg.stop == "rw4":
        return
    with contextlib.ExitStack() as st:
        aT = self.load_AT(st, "oTa", OGT, KC, 0, T)
        res = GemmRes(self, st, KC, 512, 3)
        epi = self.epi_resid(st, X, Z1)
        self.gemm_tok(res, aT, aT, KC, w_out, 0, D, range(NT), epi)
        ctx.barrier()


Prog.rwkv_layer = _rwkv_layer
def _const_inputs(cfg, core):
    T, NSEG = cfg.T, cfg.NSEG
    seg = core % NSEG
    f32 = np.float32
    own = np.zeros((P, NSEG), f32)
    own[:, seg] = 1
    hs = np.zeros((P, NSEG), f32)
    if seg > 0:
        hs[:, seg - 1] = 1
    c = {"ident": np.eye(P, dtype=f32).astype(ml_dtypes.bfloat16), "identf": np.eye(P, dtype=f32),
         "own": own, "halo_sel": hs}
    inv = (1.0 / (10000.0 ** (np.arange(0, RET_DK, 2, dtype=f32) / f32(RET_DK)))).astype(f32)
    pos = (seg * T + np.arange(T)).astype(f32)
    ang = (pos[None, :] * inv[:, None]).astype(f32)
    c["rope_cos"] = np.cos(ang).astype(f32)
    c["rope_sin"] = np.sin(ang).astype(f32)
    gam = np.array(RET_GAMMA, np.float64)
    idx = np.arange(P, dtype=np.float64)
    c["ret_kdec"] = (gam[None, :] ** (P - 1 - idx[:, None])).astype(f32)
    diff = idx[None, :] - idx[:, None]
    m = np.where(diff[:, None, :] >= 0, gam[None, :, None] ** np.maximum(diff[:, None, :], 0), 0.0)
    c["ret_maskT"] = m.astype(f32)
    c["ret_qdec"] = np.broadcast_to((gam[:, None] ** (idx[None, :] + 1.0))[None], (P, RET_H, P)).astype(f32).copy()
    coef = np.zeros((P, NSEG, RET_H), f32)
    for s in range(seg):
        coef[:, s, :] = (gam ** (T * (seg - s - 1)))[None, :]
    c["ret_coef"] = coef
    _swa_consts(cfg, core, c)
    _rwkv_consts(cfg, core, c)
    return c


def make_in_maps(cfg, prog, inputs):
    T, NSEG = cfg.T, cfg.NSEG
    maps = []
    shared = {}
    per_layer = ["ret_w_in", "ret_w_out", "ret_gn_gain", "swa_w_qkv", "swa_sinks", "swa_w_out",
                 "rwkv_mix", "rwkv_w_rkv", "rwkv_w0", "rwkv_w1", "rwkv_w2", "rwkv_a0", "rwkv_a1", "rwkv_a2",
                 "rwkv_g1", "rwkv_g2", "rwkv_k_k", "rwkv_k_a", "rwkv_r_k", "rwkv_gn_gain", "rwkv_gn_bias",
                 "rwkv_w_out", "ffn_w_up", "ffn_conv_w", "ffn_conv_b", "ffn_w_down", "ple_w_proj", "ple_w_gate"]
    for name in prog.inputs:
        if name in prog.tiled:
            fn, K_, N_, tw_, kp_ = prog.tiled[name]
            w = np.asarray(fn(inputs))
            assert w.shape == (K_, N_), (name, w.shape)
            shared[name] = np.ascontiguousarray(
                w.reshape(K_ // kp_, kp_, N_ // tw_, tw_).transpose(2, 1, 0, 3).reshape(N_ // tw_, kp_, (K_ // kp_) * tw_))
            continue
        if name in inputs and name not in ("x",):
            shared[name] = np.ascontiguousarray(inputs[name])
            continue
        for base in per_layer:
            if name.startswith(base + "_") and name[len(base) + 1:].isdigit():
                shared[name] = np.ascontiguousarray(inputs[base][int(name[len(base) + 1:])])
    for core in range(cfg.ncores):
        b, seg = divmod(core, NSEG)
        consts = _const_inputs(cfg, core)
        m = {}
        for name in prog.inputs:
            if name in shared:
                m[name] = shared[name]
            elif name == "x":
                m[name] = np.ascontiguousarray(inputs["x"][b, seg * T:(seg + 1) * T, :])
            elif name.startswith("pT_"):
                l = int(name[3:])
                m[name] = np.ascontiguousarray(inputs["p"][l, b, seg * T:(seg + 1) * T, :].T)
            elif name in consts:
                m[name] = consts[name]
            else:
                raise KeyError(name)
        maps.append(m)
    return maps


def run_cfg(cfg, inputs):
    prog = Prog(cfg)
    prog.build()
    maps = make_in_maps(cfg, prog, inputs)
    res = run_bass_kernel_spmd(prog.nc, maps, core_ids=list(range(cfg.ncores)))
    return prog, res.results


def kernel(**inputs):
    cfg = Cfg()
    prog, results = run_cfg(cfg, inputs)
    out = np.empty((cfg.NB, cfg.NSEG * cfg.T, D), np.float32)
    for core in range(cfg.ncores):
        b, seg = divmod(core, cfg.NSEG)
        out[b, seg * cfg.T:(seg + 1) * cfg.T, :] = results[core]["out"]
    return out
```

```python
import contextlib
import math
import numpy as np
import ml_dtypes
import concourse.bass as bass
import concourse.mybir as mybir
from concourse.bass_utils import run_bass_kernel_spmd

F32 = mybir.dt.float32
BF16 = mybir.dt.bfloat16
AF = mybir.ActivationFunctionType
ALU = mybir.AluOpType
AX = mybir.AxisListType

P = 128
D = 2048
KC = D // P
DEPTH = 4
DFF = 5504
NFB = DFF // P
PLE = 256
ALPHA = (2.0 * DEPTH) ** 0.25
LN_EPS = 1e-5
RET_H, RET_DK, RET_DV = 8, 256, 512
RET_EPS = 1e-5
RET_GAMMA = [1.0 - 2.0 ** (-5.0 - h) for h in range(RET_H)]


class Ctx:
    NDMA = {"sp": 8, "act": 4, "pool": 8}

    def __init__(self, nc, stack):
        self.nc = nc
        self.stack = stack
        self.eng = {"pe": nc.tensor, "act": nc.scalar, "dve": nc.vector,
                    "pool": nc.gpsimd, "sp": nc.sync}
        self.sems = {}
        self.val = {}
        for e in ("pe", "act", "dve", "pool"):
            self.sems[e] = stack.enter_context(nc.semaphore("c_" + e))
            self.val[e] = 0
        self.dq = {}
        self.dq_next = {}
        for q, n in self.NDMA.items():
            keys = []
            for i in range(n):
                k = "d_%s%d" % (q, i)
                self.sems[k] = stack.enter_context(nc.semaphore(k))
                self.val[k] = 0
                keys.append(k)
            self.dq[q] = keys
            self.dq_next[q] = 0
        self.sems["cc"] = stack.enter_context(nc.semaphore("cc"))
        self.val["cc"] = 0
        self.known = {e: {} for e in self.eng}
        self.lastw = {}
        self.readers = {}
        self.uid = 0
        self.n_ins = 0

    def sb(self, stack, name, shape, dtype=F32):
        self.uid += 1
        return stack.enter_context(self.nc.sbuf_tensor("%s_%d" % (name, self.uid), list(shape), dtype))

    def ps(self, stack, name, shape, dtype=F32):
        self.uid += 1
        return stack.enter_context(self.nc.psum_tensor("%s_%d" % (name, self.uid), list(shape), dtype))

    def _key(self, b):
        return b if isinstance(b, (str, tuple)) else id(b)

    def _deps(self, reads, writes, merge=False):
        deps = {}
        for b in list(reads) + ([] if merge else list(writes)):
            for k, v in self.lastw.get(self._key(b), {}).items():
                if deps.get(k, 0) < v:
                    deps[k] = v
        for b in writes:
            for k, v in self.readers.get(self._key(b), {}).items():
                if deps.get(k, 0) < v:
                    deps[k] = v
        return deps

    def _wait(self, e, deps):
        kn = self.known[e]
        for k, v in deps.items():
            if e == "pe" and k == "pe":
                continue
            if kn.get(k, 0) >= v:
                continue
            self.eng[e].wait_ge(self.sems[k], v)
            kn[k] = v

    def _commit(self, ev, reads, writes, merge=False):
        k, v = ev
        for b in reads:
            self.readers.setdefault(self._key(b), {})[k] = v
        for b in writes:
            if merge:
                self.lastw.setdefault(self._key(b), {})[k] = v
            else:
                self.lastw[self._key(b)] = {k: v}
                self.readers[self._key(b)] = {}

    def op(self, e, fn, reads=(), writes=()):
        self._wait(e, self._deps(reads, writes))
        ins = fn()
        self.val[e] += 1
        ins.then_inc(self.sems[e], 1)
        self._commit((e, self.val[e]), reads, writes)
        self.n_ins += 1
        return ins

    def dma(self, q, out, in_, reads=(), writes=(), merge=False, **kw):
        deps = self._deps(reads, writes, merge)
        i = self.dq_next[q]
        self.dq_next[q] = (i + 1) % len(self.dq[q])
        k = self.dq[q][i]
        if self.val[k] > 0:
            deps[k] = max(deps.get(k, 0), self.val[k])
        self._wait(q, deps)
        ins = self.eng[q].dma_start(out=out, in_=in_, **kw)
        self.val[k] += 16
        ins.then_inc(self.sems[k], 16)
        self._commit((k, self.val[k]), reads, writes, merge)
        self.n_ins += 1
        return ins

    def allreduce(self, groups, in_ap, out_ap, reads=(), writes=()):
        deps = self._deps(reads, writes)
        self._wait("pool", deps)
        ins = self.nc.gpsimd.collective_compute("AllReduce", ALU.add, replica_groups=groups,
                                                ins=[in_ap.opt()], outs=[out_ap.opt()])
        self.val["cc"] += 1
        ins.then_inc(self.sems["cc"], 1)
        self._commit(("cc", self.val["cc"]), reads, writes)

    def barrier(self, engines=("pe", "act", "dve", "pool", "sp")):
        deps = {k: v for k, v in self.val.items() if v > 0}
        for e in engines:
            self._wait(e, dict(deps))
        if len(engines) == 5:
            self.lastw = {}
            self.readers = {}

    def finish(self):
        self._wait("sp", {k: v for k, v in self.val.items() if v > 0})


class Cfg:
    def __init__(self, NB=2, NSEG=4, T=2048, layers=(0, 1, 2, 3), debug=(), stop=None):
        self.stop = stop
        self.NB, self.NSEG, self.T = NB, NSEG, T
        self.layers = tuple(layers)
        self.NT = T // P
        self.ncores = NB * NSEG
        self.groups = [[b * NSEG + s for s in range(NSEG)] for b in range(NB)]
        self.debug = tuple(debug)


class Prog:
    def __init__(self, cfg):
        self.cfg = cfg
        self.nc = bass.Bass("TRN2", target_bir_lowering=False)
        self.inputs = {}
        self.scratch = {}
        self.tiled = {}

    def inp(self, name, shape, dtype=F32):
        if name not in self.inputs:
            self.inputs[name] = self.nc.dram_tensor(name, list(shape), dtype, kind="ExternalInput")
        return self.inputs[name]

    def scr(self, name, shape, dtype=F32):
        if name not in self.scratch:
            kind = "ExternalOutput" if name in self.cfg.debug else "Internal"
            self.scratch[name] = self.nc.dram_tensor(name, list(shape), dtype, kind=kind)
        return self.scratch[name]

    def build(self):
        cfg = self.cfg
        nc = self.nc
        T = cfg.T
        with contextlib.ExitStack() as top:
            ctx = self.ctx = Ctx(nc, top)
            self.top = top
            self.ident = ctx.sb(top, "ident", [P, P], BF16)
            ctx.dma("sp", self.ident[:], self.inp("ident", [P, P], BF16).ap(), writes=[self.ident])
            self.identf = ctx.sb(top, "identf", [P, P], F32)
            ctx.dma("sp", self.identf[:], self.inp("identf", [P, P], F32).ap(), writes=[self.identf])
            self.own = ctx.sb(top, "own", [P, cfg.NSEG], F32)
            ctx.dma("sp", self.own[:], self.inp("own", [P, cfg.NSEG]).ap(), writes=[self.own])
            self.halo_sel = ctx.sb(top, "halo_sel", [P, cfg.NSEG], F32)
            ctx.dma("sp", self.halo_sel[:], self.inp("halo_sel", [P, cfg.NSEG]).ap(), writes=[self.halo_sel])

            x_in = self.inp("x", [T, D])
            out = self.nc.dram_tensor("out", [T, D], F32, kind="ExternalOutput")
            XT = self.scr("XT", [D, T], BF16)
            cur = x_in
            self.xt_stage(cur, XT)
            for li, layer in enumerate(cfg.layers):
                kind = layer % 3
                Z1 = self.scr("Z1", [T, D])
                if kind == 0:
                    self.retention_layer(layer, cur, XT, Z1)
                elif kind == 1:
                    self.swa_layer(layer, cur, XT, Z1)
                else:
                    self.rwkv_layer(layer, cur, XT, Z1)
                if cfg.stop is not None:
                    break
                X1 = self.scr("X1", [T, D])
                XT1 = self.scr("XT1", [D, T], BF16)
                self.ln_stage(Z1, layer, 0, X1, XT1)
                Z2 = self.scr("Z2", [T, D])
                self.ffn_layer(layer, X1, XT1, Z2)
                X2 = self.scr("X2", [T, D])
                self.ln_stage(Z2, layer, 1, X2, XT)
                last = li == len(cfg.layers) - 1
                X3 = out if last else self.scr("X3_%d" % (li % 2), [T, D])
                self.ple_layer(layer, X2, XT, X3)
                if not last:
                    self.xt_stage(X3, XT)
                cur = X3
            ctx.barrier()
            ctx.finish()
        return nc

    def load_w(self, dst, src_ap, key):
        self.ctx.dma("pool", dst, src_ap, writes=[key])

    def bcast_rows(self, stack, name, src_ap_1d, n):
        t = self.ctx.sb(stack, name, [P, n], F32)
        self.ctx.dma("sp", t[:], src_ap_1d.partition_broadcast(P), writes=[t])
        return t

    def load_cols(self, stack, name, src_ap_2d, R, ps):
        ctx, nc = self.ctx, self.nc
        out = ctx.sb(stack, name, [P, R], F32)
        with contextlib.ExitStack() as st:
            r0 = 0
            while r0 < R:
                r = min(P, R - r0)
                tmp = ctx.sb(st, name + "_r", [P, P], F32)
                ctx.dma("sp", tmp[0:r, :], src_ap_2d[r0:r0 + r, :], writes=[tmp])
                ctx.op("pe", lambda: nc.tensor.matmul(ps[:, 0:r], lhsT=tmp[0:r, :], rhs=self.identf[0:r, 0:r],
                                                      start=True, stop=True), reads=[tmp, self.identf], writes=[ps])
                ctx.op("dve", lambda: nc.vector.tensor_copy(out[:, r0:r0 + r], ps[:, 0:r]), reads=[ps], writes=[out])
                r0 += r
            ctx.barrier(("pe", "dve", "sp"))
        return out

    def transpose_to(self, src_bf16_ap, pst_ap, reads, writes):
        nc = self.nc
        self.ctx.op("pe", lambda: nc.tensor.transpose(pst_ap, src_bf16_ap, self.ident[:]),
                    reads=list(reads) + [self.ident], writes=writes)

    def xt_emit_tile(self, xb, xb_key, stg, stg_key, col0, pst):
        ctx, nc = self.ctx, self.nc
        for half in range(2):
            pt = pst[half]
            for j in range(8):
                kc = half * 8 + j
                self.transpose_to(xb[:, kc * P:(kc + 1) * P], pt[:, j * P:(j + 1) * P], [xb_key], [pt])
            eng = "dve" if half == 0 else "act"
            src = pt[:].rearrange("p (j c) -> p j c", j=8)
            dst = stg[:, half * 8:(half + 1) * 8, col0:col0 + P]
            if eng == "dve":
                ctx.op("dve", lambda: nc.vector.tensor_copy(dst, src), reads=[pt], writes=[stg_key])
            else:
                ctx.op("act", lambda: nc.scalar.copy(dst, src), reads=[pt], writes=[stg_key])

    def xt_stage(self, X, XT):
        ctx, nc, cfg = self.ctx, self.nc, self.cfg
        T = cfg.T
        G = 4 if cfg.NT % 4 == 0 else 2
        with contextlib.ExitStack() as st:
            xf = [ctx.sb(st, "xf", [P, D], F32) for _ in range(2)]
            xb = [ctx.sb(st, "xb", [P, D], BF16) for _ in range(2)]
            stg = [ctx.sb(st, "stg", [P, KC, G * P], BF16) for _ in range(2)]
            pst = [[ctx.ps(st, "pst", [P, 8 * P], BF16) for _ in range(2)] for _ in range(2)]
            XTv = XT.ap().rearrange("(kc p) t -> p kc t", p=P)
            for tt in range(cfg.NT):
                b = tt % 2
                g, gi = divmod(tt, G)
                ctx.dma("sp", xf[b][:], X.ap()[tt * P:(tt + 1) * P, :], writes=[xf[b]])
                ctx.op("pool", lambda: nc.gpsimd.tensor_copy(xb[b][:], xf[b][:]), reads=[xf[b]], writes=[xb[b]])
                self.xt_emit_tile(xb[b], xb[b], stg[g % 2], stg[g % 2], gi * P, pst[b])
                if gi == G - 1:
                    ctx.dma("sp", XTv[:, :, g * G * P:(g + 1) * G * P], stg[g % 2][:], reads=[stg[g % 2]],
                            writes=[("XT", g)])
            ctx.barrier()

    def ln_stage(self, Z, layer, which, X, XT):
        ctx, nc, cfg = self.ctx, self.nc, self.cfg
        G = 4 if cfg.NT % 4 == 0 else 2
        with contextlib.ExitStack() as st:
            gain = self.bcast_rows(st, "lng", self.inp("ln_gain", [DEPTH, 2, D]).ap()[layer, which, :], D)
            bias = self.bcast_rows(st, "lnb", self.inp("ln_bias", [DEPTH, 2, D]).ap()[layer, which, :], D)
            zf = [ctx.sb(st, "zf", [P, D], F32) for _ in range(2)]
            xn = [ctx.sb(st, "xn", [P, D], F32) for _ in range(2)]
            xo = [ctx.sb(st, "xo", [P, D], F32) for _ in range(2)]
            xb = [ctx.sb(st, "xb", [P, D], BF16) for _ in range(2)]
            stats = [ctx.sb(st, "stats", [P, 4, 6], F32) for _ in range(2)]
            mv = [ctx.sb(st, "mv", [P, 4], F32) for _ in range(2)]
            stg = [ctx.sb(st, "stg", [P, KC, G * P], BF16) for _ in range(2)]
            pst = [[ctx.ps(st, "pst", [P, 8 * P], BF16) for _ in range(2)] for _ in range(2)]
            XTv = XT.ap().rearrange("(kc p) t -> p kc t", p=P)
            for tt in range(cfg.NT):
                b = tt % 2
                g, gi = divmod(tt, G)
                ctx.dma("sp", zf[b][:], Z.ap()[tt * P:(tt + 1) * P, :], writes=[zf[b]])
                self.layernorm_tile(zf[b], xn[b], stats[b], mv[b], D, LN_EPS)
                ctx.op("dve", lambda: nc.vector.tensor_tensor(xn[b][:], xn[b][:], gain[:], ALU.mult),
                       reads=[xn[b], gain], writes=[xn[b]])
                ctx.op("pool", lambda: nc.gpsimd.tensor_tensor(xo[b][:], xn[b][:], bias[:], ALU.add),
                       reads=[xn[b], bias], writes=[xo[b]])
                ctx.dma("sp", X.ap()[tt * P:(tt + 1) * P, :], xo[b][:], reads=[xo[b]], writes=[("X", tt)])
                ctx.op("act", lambda: nc.scalar.copy(xb[b][:], xo[b][:]), reads=[xo[b]], writes=[xb[b]])
                self.xt_emit_tile(xb[b], xb[b], stg[g % 2], stg[g % 2], gi * P, pst[b])
                if gi == G - 1:
                    ctx.dma("sp", XTv[:, :, g * G * P:(g + 1) * G * P], stg[g % 2][:], reads=[stg[g % 2]],
                            writes=[("XT", g)])
            ctx.barrier()

    def layernorm_tile(self, src, dst, stats, mv, n, eps, src_key=None, dst_key=None):
        ctx, nc = self.ctx, self.nc
        src_key = src if src_key is None else src_key
        dst_key = dst if dst_key is None else dst_key
        nch = max(1, n // 512)
        w = n // nch
        for c in range(nch):
            ctx.op("dve", lambda: nc.vector.bn_stats(stats[:, c, :], src[:, c * w:(c + 1) * w]),
                   reads=[src_key], writes=[stats])
        ctx.op("dve", lambda: nc.vector.bn_aggr(mv[:, 0:2], stats[:, 0:nch, :]), reads=[stats], writes=[mv])
        ctx.op("dve", lambda: nc.vector.tensor_scalar_add(mv[:, 2:3], mv[:, 1:2], eps), reads=[mv], writes=[mv])
        ctx.op("act", lambda: nc.scalar.activation(mv[:, 2:3], mv[:, 2:3], AF.Sqrt), reads=[mv], writes=[mv])
        ctx.op("dve", lambda: nc.vector.reciprocal(mv[:, 2:3], mv[:, 2:3]), reads=[mv], writes=[mv])
        ctx.op("dve", lambda: nc.vector.tensor_scalar(mv[:, 3:4], mv[:, 0:1], mv[:, 2:3], -1.0, ALU.mult, ALU.mult),
               reads=[mv], writes=[mv])
        ctx.op("act", lambda: nc.scalar.activation(dst[:, 0:n], src[:, 0:n], AF.Identity, bias=mv[:, 3:4],
                                                   scale=mv[:, 2:3]), reads=[src_key, mv], writes=[dst_key])


class TiledW:
    def __init__(self, h, K, N, tw, kp):
        self.h, self.K, self.N, self.tw, self.kp = h, K, N, tw, kp
        self.kcn = K // kp


def _tw(self, name, src_fn, K, N, tw, kp=P):
    if name not in self.inputs:
        self.inp(name, [N // tw, kp, (K // kp) * tw])
        self.tiled[name] = (src_fn, K, N, tw, kp)
    return TiledW(self.inputs[name], K, N, tw, kp)


def _load_wt(self, wb, W, c0, nb, key):
    assert c0 % W.tw == 0 and nb % W.tw == 0, (c0, nb, W.tw)
    i0, nt = c0 // W.tw, nb // W.tw
    for i in range(nt):
        dst = wb[0:W.kp, 0:W.kcn, i * W.tw:(i + 1) * W.tw]
        src = W.h.ap()[i0 + i].rearrange("p (kc j) -> p kc j", j=W.tw)
        self.ctx.dma("pool", dst, src, writes=[key], merge=(i > 0))


Prog.tw = _tw
Prog.load_wt = _load_wt


class GemmRes:
    def __init__(self, prog, st, kcmax, nblk, npsum, nw=2):
        ctx = prog.ctx
        self.w = [ctx.sb(st, "wbuf", [P, kcmax, nblk], BF16) for _ in range(nw)]
        self.ps = [ctx.ps(st, "gps", [P, 512], F32) for _ in range(npsum)]
        self.wi = 0
        self.pi = 0
        self.pending = {}

    def next_w(self):
        w = self.w[self.wi]
        self.wi = (self.wi + 1) % len(self.w)
        for k in [k for k, v in self.pending.items() if v is w]:
            del self.pending[k]
        return w

    def prefetch(self, prog, W, c0, nb):
        key = (id(W.h), c0, nb)
        if key in self.pending:
            return
        wb = self.next_w()
        prog.load_wt(wb, W, c0, nb, wb)
        self.pending[key] = wb

    def get_w(self, prog, W, c0, nb):
        wb = self.pending.pop((id(W.h), c0, nb), None)
        if wb is None:
            wb = self.next_w()
            prog.load_wt(wb, W, c0, nb, wb)
        return wb

    def next_ps(self):
        p = self.ps[self.pi]
        self.pi = (self.pi + 1) % len(self.ps)
        return p


def _gemm_tok(self, res, AT, at_key, kcn, W, n0, ncols, tts, epi, nblk=512, kp=P, nxt=None):
    ctx, nc = self.ctx, self.nc
    blocks = [(c0, min(nblk, n0 + ncols - c0)) for c0 in range(n0, n0 + ncols, nblk)]
    for bi, (c0, nb) in enumerate(blocks):
        wb = res.get_w(self, W, c0, nb)
        if bi + 1 < len(blocks):
            res.prefetch(self, W, *blocks[bi + 1])
        elif nxt is not None:
            res.prefetch(self, *nxt)
        for tt in tts:
            ps = res.next_ps()
            for kc in range(kcn):
                ctx.op("pe", lambda: nc.tensor.matmul(ps[:, 0:nb], lhsT=AT[0:kp, kc, tt * P:(tt + 1) * P],
                                                      rhs=wb[0:kp, kc, 0:nb], start=(kc == 0), stop=(kc == kcn - 1)),
                       reads=[at_key, wb], writes=[ps])
            epi(ps, tt, c0, nb)


def _gemm_feat(self, res, AT, at_key, kcn, W, n0, ncols, tgs, epi, nblk=512, nxt=None):
    ctx, nc = self.ctx, self.nc
    blocks = [(c0, min(nblk, n0 + ncols - c0)) for c0 in range(n0, n0 + ncols, nblk)]
    for bi, (c0, nb) in enumerate(blocks):
        wb = res.get_w(self, W, c0, nb)
        if bi + 1 < len(blocks):
            res.prefetch(self, W, *blocks[bi + 1])
        elif nxt is not None:
            res.prefetch(self, *nxt)
        for (t0, tn) in tgs:
            for fb in range((nb + P - 1) // P):
                fw = min(P, nb - fb * P)
                ps = res.next_ps()
                for kc in range(kcn):
                    ctx.op("pe", lambda: nc.tensor.matmul(ps[0:fw, 0:tn], lhsT=wb[:, kc, fb * P:fb * P + fw],
                                                          rhs=AT[:, kc, t0:t0 + tn], start=(kc == 0),
                                                          stop=(kc == kcn - 1)),
                           reads=[at_key, wb], writes=[ps])
                epi(ps, c0 + fb * P, t0, tn)


Prog.gemm_tok = _gemm_tok
Prog.gemm_feat = _gemm_feat


def _load_AT(self, st, name, XT, kcn, t0, tn, pad=0):
    ctx = self.ctx
    t = ctx.sb(st, name, [P, kcn, pad + tn], BF16)
    v = XT.ap().rearrange("(kc p) t -> p kc t", p=P)
    step = max(1, kcn // 4)
    for k0 in range(0, kcn, step):
        k1 = min(kcn, k0 + step)
        ctx.dma("sp", t[:, k0:k1, pad:pad + tn], v[:, k0:k1, t0:t0 + tn], writes=[t], merge=(k0 > 0))
    return t


Prog.load_AT = _load_AT


def _epi_resid(self, st, Xold, Zout, tok_base=0):
    ctx, nc = self.ctx, self.nc
    xo = [ctx.sb(st, "rx", [P, 512], F32) for _ in range(3)]
    zt = [ctx.sb(st, "rz", [P, 512], F32) for _ in range(3)]
    cnt = [0]

    def epi(ps, tt, c0, nb):
        i = cnt[0] % 3
        cnt[0] += 1
        r0 = tok_base + tt * P
        ctx.dma("sp", xo[i][:, 0:nb], Xold.ap()[r0:r0 + P, c0:c0 + nb], writes=[xo[i]])
        ctx.op("dve", lambda: nc.vector.scalar_tensor_tensor(zt[i][:, 0:nb], xo[i][:, 0:nb], ALPHA, ps[:, 0:nb],
                                                             ALU.mult, ALU.add),
               reads=[xo[i], ps], writes=[zt[i]])
        ctx.dma("sp", Zout.ap()[r0:r0 + P, c0:c0 + nb], zt[i][:, 0:nb], reads=[zt[i]], writes=[("Z", r0, c0)])
    return epi


Prog.epi_resid = _epi_resid


def _retention_layer(self, layer, X, XT, Z1):
    ctx, nc, cfg = self.ctx, self.nc, self.cfg
    T, NT, NSEG = cfg.T, cfg.NT, cfg.NSEG
    j = layer // 3
    w_qk = self.tw("ret_w_qk_t%d" % j, lambda inp, j=j: inp["ret_w_in"][j][:, 0:4096], D, 4096, 256)
    w_vg = self.tw("ret_w_vg_t%d" % j, lambda inp, j=j: inp["ret_w_in"][j][:, 4096:12288], D, 8192, 512)
    w_out = self.tw("ret_w_out_t%d" % j, lambda inp, j=j: inp["ret_w_out"][j], 4096, D, 256)
    gn_ap = self.inp("ret_gn_gain_%d" % j, [4096]).ap()
    KTs = self.scr("ret_KT", [D, T], BF16)
    Vs = self.scr("ret_V", [T, 4096], BF16)
    OGT = self.scr("ret_OGT", [4096, T], BF16)
    CCI = self.scr("ret_cci", [RET_H, NSEG, 2, P, 512])
    CCO = self.scr("ret_cco", [RET_H, NSEG, 2, P, 512])
    TH = min(1024, T)
    NTH = TH // P
    TGW = min(512, TH)
    tgs = [(t0, TGW) for t0 in range(0, TH, TGW)]
    QOFF, KOFF, VOFF, GOFF = 0, 2048, 0, 4096
    rope_cos = self.inp("rope_cos", [P, T]).ap()
    rope_sin = self.inp("rope_sin", [P, T]).ap()

    def rotary_epi(cosT, sinT, dstT, tmp):
        state = {}

        def epi(ps, c, t0, tn):
            half = (c // P) % 2
            if half == 0:
                state["A"] = ps
                return
            psA, psB = state["A"], ps
            t1, t2, t3, t4 = tmp
            cs, sn = cosT[:, t0:t0 + tn], sinT[:, t0:t0 + tn]
            ctx.op("dve", lambda: nc.vector.tensor_tensor(t1[:, 0:tn], psA[:, 0:tn], cs, ALU.mult),
                   reads=[psA, cosT], writes=[t1])
            ctx.op("dve", lambda: nc.vector.tensor_tensor(t2[:, 0:tn], psB[:, 0:tn], sn, ALU.mult),
                   reads=[psB, sinT], writes=[t2])
            ctx.op("dve", lambda: nc.vector.tensor_tensor(t3[:, 0:tn], psA[:, 0:tn], sn, ALU.mult),
                   reads=[psA, sinT], writes=[t3])
            ctx.op("dve", lambda: nc.vector.tensor_tensor(t4[:, 0:tn], psB[:, 0:tn], cs, ALU.mult),
                   reads=[psB, cosT], writes=[t4])
            ctx.op("pool", lambda: nc.gpsimd.tensor_tensor(dstT[:, 0, t0:t0 + tn], t1[:, 0:tn], t2[:, 0:tn],
                                                           ALU.subtract), reads=[t1, t2], writes=[dstT])
            ctx.op("pool", lambda: nc.gpsimd.tensor_tensor(dstT[:, 1, t0:t0 + tn], t3[:, 0:tn], t4[:, 0:tn],
                                                           ALU.add), reads=[t3, t4], writes=[dstT])
        return epi

    KTv = KTs.ap().rearrange("(h two p) t -> h p two t", two=2, p=P)
    Vv = Vs.ap().rearrange("(tt p) e -> p tt e", p=P)
    OGTv = OGT.ap().rearrange("(h fc p) t -> h p fc t", fc=4, p=P)

    with contextlib.ExitStack() as st:
        kdec = ctx.sb(st, "kdec", [P, RET_H], F32)
        ctx.dma("sp", kdec[:], self.inp("ret_kdec", [P, RET_H]).ap(), writes=[kdec])
        res = GemmRes(self, st, KC, 512, 3)
        tmp = [ctx.sb(st, "rt", [P, 512], F32) for _ in range(4)]
        kT = [ctx.sb(st, "kT", [P, 2, TH], BF16) for _ in range(2)]
        vh = [ctx.sb(st, "vh", [P, NTH, 512], BF16) for _ in range(2)]
        kdA = [ctx.sb(st, "kdA", [P, 2 * P], BF16) for _ in range(2)]
        pst = [ctx.ps(st, "pst", [P, 2 * P], BF16) for _ in range(2)]
        Lps = [ctx.ps(st, "Lps", [P, 512], F32) for _ in range(2)]
        Lacc = ctx.sb(st, "Lacc", [P, RET_H, 2, 512], F32)
        Lm = [ctx.sb(st, "Lm", [P, 2, 512], F32) for _ in range(2)]
        cosk = ctx.sb(st, "cosk", [P, TH], F32)
        sink = ctx.sb(st, "sink", [P, TH], F32)
        xT = ctx.sb(st, "xT", [P, KC, TH], BF16)
        XTv = XT.ap().rearrange("(kc p) t -> p kc t", p=P)
        for th in range(T // TH):
            t0h = th * TH
            for k0 in range(0, KC, 4):
                ctx.dma("sp", xT[:, k0:k0 + 4, :], XTv[:, k0:k0 + 4, t0h:t0h + TH], writes=[xT], merge=(k0 > 0))
            ctx.dma("sp", cosk[:], rope_cos[:, t0h:t0h + TH], writes=[cosk])
            ctx.dma("sp", sink[:], rope_sin[:, t0h:t0h + TH], writes=[sink])
            ctx.op("pool", lambda: nc.gpsimd.tensor_scalar_mul(cosk[:], cosk[:], RET_DK ** -0.5), reads=[cosk], writes=[cosk])
            ctx.op("pool", lambda: nc.gpsimd.tensor_scalar_mul(sink[:], sink[:], RET_DK ** -0.5), reads=[sink], writes=[sink])
            for h in range(RET_H):
                kTh, vhh = kT[h % 2], vh[h % 2]
                self.gemm_feat(res, xT, xT, KC, w_qk, KOFF + h * 256, 256, tgs, rotary_epi(cosk, sink, kTh, tmp), nblk=256,
                               nxt=(w_vg, VOFF + h * 512, 512))
                ctx.dma("sp", KTv[h][:, :, t0h:t0h + TH], kTh[:], reads=[kTh], writes=[("KT", h, th)])

                def v_epi(ps, tt, c0, nb):
                    ctx.op("act", lambda: nc.scalar.copy(vhh[:, tt, :], ps[:, 0:nb]), reads=[ps], writes=[vhh])
                self.gemm_tok(res, xT, xT, KC, w_vg, VOFF + h * 512, 512, range(NTH), v_epi,
                              nxt=(w_qk, KOFF + ((h + 1) % RET_H) * 256, 256))
                ctx.dma("sp", Vv[:, th * NTH:(th + 1) * NTH, h * 512:(h + 1) * 512], vhh[:],
                        reads=[vhh], writes=[("V", h, th)])
                if NSEG > 1:
                    g = RET_GAMMA[h]
                    for cl in range(NTH):
                        c = th * NTH + cl
                        pt = pst[cl % 2]
                        kd = kdA[cl % 2]
                        for half in range(2):
                            self.transpose_to(kTh[:, half, cl * P:(cl + 1) * P], pt[:, half * P:(half + 1) * P], [kTh], [pt])
                        ctx.op("dve", lambda: nc.vector.tensor_scalar(kd[:], pt[:], kdec[:, h:h + 1],
                                                                      float(g ** (P * (NT - 1 - c))), ALU.mult, ALU.mult),
                               reads=[pt, kdec], writes=[kd])
                        for half in range(2):
                            ctx.op("pe", lambda: nc.tensor.matmul(Lps[half][:], lhsT=kd[:, half * P:(half + 1) * P],
                                                                  rhs=vhh[:, cl, :], start=(cl == 0), stop=(cl == NTH - 1)),
                                   reads=[kd, vhh], writes=[Lps[half]])
                    for half in range(2):
                        if th == 0:
                            ctx.op("act", lambda: nc.scalar.copy(Lacc[:, h, half, :], Lps[half][:]),
                                   reads=[Lps[half]], writes=[(id(Lacc), h)])
                        else:
                            ctx.op("dve", lambda: nc.vector.tensor_tensor(Lacc[:, h, half, :], Lacc[:, h, half, :],
                                                                          Lps[half][:], ALU.add),
                                   reads=[Lps[half], (id(Lacc), h)], writes=[(id(Lacc), h)])
        if NSEG > 1:
            for h in range(RET_H):
                for s in range(NSEG):
                    lm = Lm[(h * NSEG + s) % 2]
                    ctx.op("dve", lambda: nc.vector.tensor_scalar_mul(lm[:], Lacc[:, h], self.own[:, s:s + 1]),
                           reads=[(id(Lacc), h), self.own], writes=[lm])
                    ctx.dma("sp", CCI.ap()[h, s].rearrange("two p e -> p two e"), lm[:], reads=[lm],
                            writes=[("CCI", s, h)])
        ctx.barrier()
    with contextlib.ExitStack() as st:
        res = GemmRes(self, st, KC, 512, 2)
        if NSEG > 1:
            for h in range(RET_H):
                ctx.allreduce(cfg.groups, CCI.ap()[h].rearrange("s two p e -> (s two p) e"),
                              CCO.ap()[h].rearrange("s two p e -> (s two p) e"), writes=[("CCO", h)])
        kdec = ctx.sb(st, "kdec", [P, RET_H], F32)
        ctx.dma("sp", kdec[:], self.inp("ret_kdec", [P, RET_H]).ap(), writes=[kdec])
        maskT = ctx.sb(st, "maskT", [P, RET_H, P], F32)
        ctx.dma("sp", maskT[:], self.inp("ret_maskT", [P, RET_H, P]).ap(), writes=[maskT])
        qdec = ctx.sb(st, "qdec", [P, RET_H, P], F32)
        ctx.dma("sp", qdec[:], self.inp("ret_qdec", [P, RET_H, P]).ap(), writes=[qdec])
        coef = ctx.sb(st, "coef", [P, NSEG, RET_H], F32)
        ctx.dma("sp", coef[:], self.inp("ret_coef", [P, NSEG, RET_H]).ap(), writes=[coef])
        gain = self.bcast_rows(st, "gng", gn_ap, 4096)
        tmp = [ctx.sb(st, "rt", [P, 512], F32) for _ in range(4)]
        cosT = ctx.sb(st, "cos", [P, TH], F32)
        sinT = ctx.sb(st, "sin", [P, TH], F32)
        xT = ctx.sb(st, "xT", [P, KC, TH], BF16)
        XTv = XT.ap().rearrange("(kc p) t -> p kc t", p=P)
        kTh = ctx.sb(st, "kT", [P, 2, TH], BF16)
        qTh = ctx.sb(st, "qT", [P, 2, TH], BF16)
        vhh = ctx.sb(st, "vh", [P, NTH, 512], BF16)
        gsh = ctx.sb(st, "gs", [P, NTH, 512], BF16)
        ogTh = ctx.sb(st, "ogT", [P, 4, TH], BF16)
        Rall = ctx.sb(st, "Rall", [P, RET_H, 2, 512], F32)
        Rb = ctx.sb(st, "Rb", [P, 2, 512], BF16)
        cin = [ctx.sb(st, "cin", [P, 2, 512], F32) for _ in range(2)]
        sT = [ctx.sb(st, "sT", [P, P], BF16) for _ in range(2)]
        qd = [ctx.sb(st, "qd", [P, 2, P], BF16) for _ in range(2)]
        kd = [ctx.sb(st, "kd", [P, 2 * P], BF16) for _ in range(2)]
        on = [ctx.sb(st, "on", [P, 512], F32) for _ in range(2)]
        og = [ctx.sb(st, "og", [P, 512], F32) for _ in range(2)]
        og2 = [ctx.sb(st, "og2", [P, 512], BF16) for _ in range(2)]
        stats = [ctx.sb(st, "stats", [P, 4, 6], F32) for _ in range(2)]
        mv = [ctx.sb(st, "mv", [P, 4], F32) for _ in range(2)]
        ps_s = ctx.ps(st, "ps_s", [P, 512], F32)
        ps_o2 = [ctx.ps(st, "ps_o", [P, 512], F32) for _ in range(2)]
        ps_tg = ctx.ps(st, "ps_tg", [P, 6 * P], BF16)
        ps_R = [ctx.ps(st, "ps_R", [P, 512], F32) for _ in range(2)]
        for th in range(T // TH):
            t0h = th * TH
            for k0 in range(0, KC, 4):
                ctx.dma("sp", xT[:, k0:k0 + 4, :], XTv[:, k0:k0 + 4, t0h:t0h + TH], writes=[xT], merge=(k0 > 0))
            ctx.dma("sp", cosT[:], rope_cos[:, t0h:t0h + TH], writes=[cosT])
            ctx.dma("sp", sinT[:], rope_sin[:, t0h:t0h + TH], writes=[sinT])
            for h in range(RET_H):
                Rk = (id(Rall), h)
                ctx.dma("sp", kTh[:], KTv[h][:, :, t0h:t0h + TH], writes=[kTh])
                ctx.dma("sp", vhh[:], Vv[:, th * NTH:(th + 1) * NTH, h * 512:(h + 1) * 512], writes=[vhh])
                self.gemm_feat(res, xT, xT, KC, w_qk, QOFF + h * 256, 256, tgs, rotary_epi(cosT, sinT, qTh, tmp), nblk=256,
                               nxt=(w_vg, GOFF + h * 512, 512))

                def g_epi(ps, tt, c0, nb):
                    ctx.op("act", lambda: nc.scalar.activation(gsh[:, tt, :], ps[:, 0:nb], AF.Silu), reads=[ps], writes=[gsh])
                self.gemm_tok(res, xT, xT, KC, w_vg, GOFF + h * 512, 512, range(NTH), g_epi,
                              nxt=(w_qk, QOFF + ((h + 1) % RET_H) * 256, 256))
                if th == 0:
                    if NSEG > 1:
                        for s_ in range(NSEG):
                            ci = cin[s_ % 2]
                            ctx.dma("sp", ci[:], CCO.ap()[h, s_].rearrange("two p e -> p two e"), reads=[("CCO", h)], writes=[ci])
                            if s_ == 0:
                                ctx.op("dve", lambda: nc.vector.tensor_scalar_mul(Rall[:, h], ci[:], coef[:, s_, h:h + 1]),
                                       reads=[ci, coef], writes=[Rk])
                            else:
                                ctx.op("dve", lambda: nc.vector.scalar_tensor_tensor(Rall[:, h], ci[:], coef[:, s_, h:h + 1],
                                                                                     Rall[:, h], ALU.mult, ALU.add),
                                       reads=[ci, coef, Rk], writes=[Rk])
                    else:
                        ctx.op("dve", lambda: nc.vector.memset(Rall[:, h], 0.0), writes=[Rk])
                ctx.op("act", lambda: nc.scalar.copy(Rb[:], Rall[:, h]), reads=[Rk], writes=[Rb])
                gam = RET_GAMMA[h]
                for cl in range(NTH):
                    cb = cl % 2
                    ps_o = ps_o2[cb]
                    cs = slice(cl * P, (cl + 1) * P)
                    for half in range(2):
                        ctx.op("pe", lambda: nc.tensor.matmul(ps_s[:, 0:P], lhsT=kTh[:, half, cs], rhs=qTh[:, half, cs],
                                                              start=(half == 0), stop=(half == 1)),
                               reads=[kTh, qTh], writes=[ps_s])
                    ctx.op("dve", lambda: nc.vector.tensor_tensor(sT[cb][:], ps_s[:, 0:P], maskT[:, h, :], ALU.mult),
                           reads=[ps_s, maskT], writes=[sT[cb]])
                    for half in range(2):
                        ctx.op("pool", lambda: nc.gpsimd.tensor_tensor(qd[cb][:, half, :], qTh[:, half, cs],
                                                                       qdec[:, h, :], ALU.mult),
                               reads=[qTh, qdec], writes=[qd[cb]])
                    ctx.op("pe", lambda: nc.tensor.matmul(ps_o[:], lhsT=sT[cb][:], rhs=vhh[:, cl, :], start=True, stop=False),
                           reads=[sT[cb], vhh], writes=[ps_o])
                    for half in range(2):
                        ctx.op("pe", lambda: nc.tensor.matmul(ps_o[:], lhsT=qd[cb][:, half, :], rhs=Rb[:, half, :],
                                                              start=False, stop=(half == 1)),
                               reads=[qd[cb], Rb], writes=[ps_o])
                    for half in range(2):
                        self.transpose_to(kTh[:, half, cs], ps_tg[:, half * P:(half + 1) * P], [kTh], [ps_tg])
                    ctx.op("dve", lambda: nc.vector.tensor_scalar_mul(kd[cb][:], ps_tg[:, 0:2 * P], kdec[:, h:h + 1]),
                           reads=[ps_tg, kdec], writes=[kd[cb]])
                    for half in range(2):
                        ctx.op("pe", lambda: nc.tensor.matmul(ps_R[half][:], lhsT=kd[cb][:, half * P:(half + 1) * P],
                                                              rhs=vhh[:, cl, :], start=True, stop=True),
                               reads=[kd[cb], vhh], writes=[ps_R[half]])
                        ctx.op("dve", lambda: nc.vector.scalar_tensor_tensor(Rall[:, h, half, :], Rall[:, h, half, :],
                                                                             float(gam ** P), ps_R[half][:],
                                                                             ALU.mult, ALU.add),
                               reads=[Rk, ps_R[half]], writes=[Rk])
                    ctx.op("act", lambda: nc.scalar.copy(Rb[:], Rall[:, h]), reads=[Rk], writes=[Rb])
                    self.layernorm_tile(ps_o, on[cb], stats[cb], mv[cb], 512, RET_EPS)
                    ctx.op("dve", lambda: nc.vector.tensor_tensor(og[cb][:], on[cb][:], gain[:, h * 512:(h + 1) * 512], ALU.mult),
                           reads=[on[cb], gain], writes=[og[cb]])
                    ctx.op("pool", lambda: nc.gpsimd.tensor_tensor(og2[cb][:], og[cb][:], gsh[:, cl, :], ALU.mult),
                           reads=[og[cb], gsh], writes=[og2[cb]])
                    for fc in range(4):
                        self.transpose_to(og2[cb][:, fc * P:(fc + 1) * P], ps_tg[:, (2 + fc) * P:(3 + fc) * P], [og2[cb]], [ps_tg])
                    ctx.op("act", lambda: nc.scalar.copy(ogTh[:, :, cs], ps_tg[:, 2 * P:6 * P].rearrange("p (f c) -> p f c", f=4)),
                           reads=[ps_tg], writes=[ogTh])
                ctx.dma("sp", OGTv[h][:, :, t0h:t0h + TH], ogTh[:], reads=[ogTh], writes=[("OGT", h, th)])
        ctx.barrier()

    for t0 in range(0, T, TH):
        with contextlib.ExitStack() as st:
            aT = self.load_AT(st, "ogTa", OGT, 32, t0, TH)
            res = GemmRes(self, st, 32, 256, 3)
            epi = self.epi_resid(st, X, Z1, tok_base=t0)
            self.gemm_tok(res, aT, aT, 32, w_out, 0, D, range(TH // P), epi, nblk=256)
            ctx.barrier()


Prog.retention_layer = _retention_layer
def _halo_rows(self, st, Xsrc, nrows, name):
    ctx, nc, cfg = self.ctx, self.nc, self.cfg
    NSEG, T = cfg.NSEG, cfg.T
    halo = ctx.sb(st, name, [nrows, D], F32)
    if NSEG == 1:
        ctx.op("dve", lambda: nc.vector.memset(halo[:], 0.0), writes=[halo])
        return halo
    HCI = self.scr("halo_ci_%d" % nrows, [NSEG, nrows, D])
    HCO = self.scr("halo_co_%d" % nrows, [NSEG, nrows, D])
    hx = ctx.sb(st, name + "_x", [nrows, D], F32)
    hm = [ctx.sb(st, name + "_m", [nrows, D], F32) for _ in range(2)]
    ctx.dma("sp", hx[:], Xsrc.ap()[T - nrows:T, :], writes=[hx])
    for s in range(NSEG):
        ctx.op("dve", lambda: nc.vector.tensor_scalar_mul(hm[s % 2][:], hx[:], self.own[0:nrows, s:s + 1]),
               reads=[hx, self.own], writes=[hm[s % 2]])
        ctx.dma("sp", HCI.ap()[s], hm[s % 2][:], reads=[hm[s % 2]], writes=[("HCI", s)])
    ctx.barrier()
    ctx.allreduce(cfg.groups, HCI.ap().rearrange("s r d -> (s r) d"), HCO.ap().rearrange("s r d -> (s r) d"),
                  writes=[("HCO",)])
    ctx.barrier()
    for s in range(NSEG):
        ctx.dma("sp", hm[s % 2][:], HCO.ap()[s], writes=[hm[s % 2]])
        if s == 0:
            ctx.op("dve", lambda: nc.vector.tensor_scalar_mul(halo[:], hm[s % 2][:], self.halo_sel[0:nrows, s:s + 1]),
                   reads=[hm[s % 2], self.halo_sel], writes=[halo])
        else:
            ctx.op("dve", lambda: nc.vector.scalar_tensor_tensor(halo[:], hm[s % 2][:], self.halo_sel[0:nrows, s:s + 1],
                                                                 halo[:], ALU.mult, ALU.add),
                   reads=[hm[s % 2], self.halo_sel, halo], writes=[halo])
    return halo


Prog.halo_rows = _halo_rows


def _ffn_layer(self, layer, X1, XT1, Z2):
    ctx, nc, cfg = self.ctx, self.nc, self.cfg
    T = cfg.T
    w_up = self.tw("ffn_w_up_t%d" % layer, lambda inp, l=layer: inp["ffn_w_up"][l], D, 2 * DFF, 128)
    w_dn = self.tw("ffn_w_down_t%d" % layer, lambda inp, l=layer: inp["ffn_w_down"][l], DFF, D, 256)
    cw_ap = self.inp("ffn_conv_w_%d" % layer, [3, 2 * DFF]).ap().rearrange("t (b p) -> (t b) p", p=P)
    cb_ap = self.inp("ffn_conv_b_%d" % layer, [2 * DFF]).ap().rearrange("(b p) -> b p", p=P)
    NB2 = 2 * NFB
    TG = min(1024, T)
    W = TG + 2
    nsub = (W + 511) // 512
    bounds = [(W * i) // nsub for i in range(nsub + 1)]
    with contextlib.ExitStack() as st0:
        psc = ctx.ps(st0, "psc", [P, 512], F32)
        cw = self.load_cols(st0, "cw", cw_ap, 3 * NB2, psc)
        cb = self.load_cols(st0, "cb", cb_ap, NB2, psc)
        haloT = ctx.sb(st0, "haloT", [P, KC, 2], BF16)
        with contextlib.ExitStack() as sth:
            halo = self.halo_rows(sth, X1, 2, "halo2")
            for kc in range(KC):
                ctx.op("pe", lambda: nc.tensor.matmul(psc[:, kc * 2:kc * 2 + 2], lhsT=halo[0:2, kc * P:(kc + 1) * P],
                                                      rhs=self.identf[0:2, 0:2], start=True, stop=True),
                       reads=[halo, self.identf], writes=[psc])
            ctx.op("dve", lambda: nc.vector.tensor_copy(haloT[:], psc[:, 0:2 * KC].rearrange("p (k c) -> p k c", c=2)),
                   reads=[psc], writes=[haloT])
            ctx.barrier()
        gT = ctx.sb(st0, "gT", [P, NFB, TG], BF16)
        XTv = XT1.ap().rearrange("(kc p) t -> p kc t", p=P)
        for g in range(T // TG):
            t0 = g * TG
            with contextlib.ExitStack() as st:
                xT = ctx.sb(st, "x1T", [P, KC, W], BF16)
                for k0 in range(0, KC, 4):
                    ctx.dma("sp", xT[:, k0:k0 + 4, 2:W], XTv[:, k0:k0 + 4, t0:t0 + TG], writes=[xT], merge=(k0 > 0))
                if g == 0:
                    ctx.op("pool", lambda: nc.gpsimd.tensor_copy(xT[:, :, 0:2], haloT[:]), reads=[haloT], writes=[xT])
                else:
                    ctx.dma("sp", xT[:, :, 0:2], XTv[:, :, t0 - 2:t0], writes=[xT], merge=True)
                wu = [ctx.sb(st, "wu", [P, 2, KC, P], BF16) for _ in range(2)]
                wg = [ctx.sb(st, "wg", [P, 2, KC, P], BF16) for _ in range(2)]
                pss = [ctx.ps(st, "fps", [P, 512], F32) for _ in range(6)]
                hs = [ctx.sb(st, "hs", [P, W], F32) for _ in range(2)]
                acc = [ctx.sb(st, "acc", [P, TG], F32) for _ in range(2)]
                sg = ctx.sb(st, "sg", [P, TG], F32)
                pi = 0

                def load_pair(pr):
                    fb0 = 2 * pr
                    wi_ = pr % 2
                    for ti in range(min(2, NFB - fb0)):
                        ctx.dma("pool", wu[wi_][:, ti], w_up.h.ap()[fb0 + ti].rearrange("p (kc j) -> p kc j", j=P),
                                writes=[wu[wi_]], merge=(ti > 0))
                        ctx.dma("pool", wg[wi_][:, ti], w_up.h.ap()[NFB + fb0 + ti].rearrange("p (kc j) -> p kc j", j=P),
                                writes=[wg[wi_]], merge=(ti > 0))
                load_pair(0)
                for fb in range(NFB):
                    if fb % 2 == 0 and fb + 2 < NFB:
                        load_pair(fb // 2 + 1)
                    wi = (fb // 2) % 2
                    fo = (fb % 2) * P
                    for ui, wt in enumerate((wu[wi], wg[wi])):
                        for si in range(nsub):
                            a, b_ = bounds[si], bounds[si + 1]
                            ps = pss[pi % 6]
                            pi += 1
                            for kc in range(KC):
                                ctx.op("pe", lambda: nc.tensor.matmul(ps[:, 0:b_ - a], lhsT=wt[:, fb % 2, kc, :],
                                                                      rhs=xT[:, kc, a:b_], start=(kc == 0), stop=(kc == KC - 1)),
                                       reads=[wt, xT], writes=[ps])
                            ctx.op("act", lambda: nc.scalar.copy(hs[ui][:, a:b_], ps[:, 0:b_ - a]), reads=[ps], writes=[hs[ui]])
                        blk = fb if ui == 0 else NFB + fb
                        w0 = cw[:, 0 * NB2 + blk:0 * NB2 + blk + 1]
                        w1 = cw[:, 1 * NB2 + blk:1 * NB2 + blk + 1]
                        w2 = cw[:, 2 * NB2 + blk:2 * NB2 + blk + 1]
                        ctx.op("act", lambda: nc.scalar.activation(acc[ui][:], hs[ui][:, 2:W], AF.Identity,
                                                                   bias=cb[:, blk:blk + 1], scale=w2),
                               reads=[hs[ui], cw, cb], writes=[acc[ui]])
                        ctx.op("dve", lambda: nc.vector.scalar_tensor_tensor(acc[ui][:], hs[ui][:, 1:W - 1], w1, acc[ui][:],
                                                                             ALU.mult, ALU.add),
                               reads=[hs[ui], cw, acc[ui]], writes=[acc[ui]])
                        ctx.op("dve", lambda: nc.vector.scalar_tensor_tensor(acc[ui][:], hs[ui][:, 0:W - 2], w0, acc[ui][:],
                                                                             ALU.mult, ALU.add),
                               reads=[hs[ui], cw, acc[ui]], writes=[acc[ui]])
                    ctx.op("act", lambda: nc.scalar.activation(sg[:], acc[1][:], AF.Silu), reads=[acc[1]], writes=[sg])
                    ctx.op("pool", lambda: nc.gpsimd.tensor_tensor(gT[:, fb, :], sg[:], acc[0][:], ALU.mult),
                           reads=[sg, acc[0]], writes=[gT])
                ctx.barrier()
            with contextlib.ExitStack() as st:
                res = GemmRes(self, st, NFB, 256, 4)
                epi = self.epi_resid(st, X1, Z2, tok_base=t0)
                self.gemm_tok(res, gT, gT, NFB, w_dn, 0, D, range(TG // P), epi, nblk=256)
                ctx.barrier()


Prog.ffn_layer = _ffn_layer


def _ple_layer(self, layer, X2, XT2, X3):
    ctx, nc, cfg = self.ctx, self.nc, self.cfg
    T, NT = cfg.T, cfg.NT
    w_gate = self.tw("ple_w_gate_t%d" % layer, lambda inp, l=layer: inp["ple_w_gate"][l], D, D, 512)
    w_proj = self.tw("ple_w_proj_t%d" % layer, lambda inp, l=layer: inp["ple_w_proj"][l], PLE, D, 512)
    pT_in = self.inp("pT_%d" % layer, [PLE, T]).ap()
    with contextlib.ExitStack() as st:
        xT = self.load_AT(st, "x2T", XT2, KC, 0, T)
        pT = ctx.sb(st, "pT", [P, 2, T], BF16)
        self.load_w(pT[:], pT_in.rearrange("(kc p) t -> p kc t", p=P), pT)
        wg = [ctx.sb(st, "wg", [P, KC, 512], BF16) for _ in range(2)]
        wp = [ctx.sb(st, "wp", [P, 2, 512], BF16) for _ in range(2)]
        psg = [ctx.ps(st, "psg", [P, 512], F32) for _ in range(3)]
        psp = [ctx.ps(st, "psp", [P, 512], F32) for _ in range(3)]
        sg = [ctx.sb(st, "sg", [P, 512], F32) for _ in range(3)]
        x2 = [ctx.sb(st, "x2", [P, 512], F32) for _ in range(3)]
        x3 = [ctx.sb(st, "x3", [P, 512], F32) for _ in range(3)]
        it = 0

        def load_blk(ci_):
            self.load_wt(wg[ci_ % 2], w_gate, ci_ * 512, 512, wg[ci_ % 2])
            self.load_wt(wp[ci_ % 2], w_proj, ci_ * 512, 512, wp[ci_ % 2])
        load_blk(0)
        for ci, c0 in enumerate(range(0, D, 512)):
            wgi, wpi = wg[ci % 2], wp[ci % 2]
            if ci + 1 < D // 512:
                load_blk(ci + 1)
            for tt in range(NT):
                i = it % 3
                it += 1
                ts = slice(tt * P, (tt + 1) * P)
                for kc in range(KC):
                    ctx.op("pe", lambda: nc.tensor.matmul(psg[i][:], lhsT=xT[:, kc, ts], rhs=wgi[:, kc, :],
                                                          start=(kc == 0), stop=(kc == KC - 1)),
                           reads=[xT, wgi], writes=[psg[i]])
                for kc in range(2):
                    ctx.op("pe", lambda: nc.tensor.matmul(psp[i][:], lhsT=pT[:, kc, ts], rhs=wpi[:, kc, :],
                                                          start=(kc == 0), stop=(kc == 1)),
                           reads=[pT, wpi], writes=[psp[i]])
                ctx.dma("sp", x2[i][:], X2.ap()[tt * P:(tt + 1) * P, c0:c0 + 512], writes=[x2[i]])
                ctx.op("act", lambda: nc.scalar.activation(sg[i][:], psg[i][:], AF.Sigmoid), reads=[psg[i]], writes=[sg[i]])
                ctx.op("dve", lambda: nc.vector.tensor_tensor(sg[i][:], sg[i][:], psp[i][:], ALU.mult),
                       reads=[sg[i], psp[i]], writes=[sg[i]])
                ctx.op("pool", lambda: nc.gpsimd.tensor_tensor(x3[i][:], sg[i][:], x2[i][:], ALU.add),
                       reads=[sg[i], x2[i]], writes=[x3[i]])
                ctx.dma("sp", X3.ap()[tt * P:(tt + 1) * P, c0:c0 + 512], x3[i][:], reads=[x3[i]], writes=[("X3", tt, c0)])
        ctx.barrier()


Prog.ple_layer = _ple_layer
SWA_HQ, SWA_HKV, SWA_HD, SWA_W = 32, 4, 64, 128
NEG = -1e30


def _swa_head_order():
    order = []
    for pair in range(2):
        for g in range(8):
            order.append((2 * pair) * 8 + g)
            order.append((2 * pair + 1) * 8 + g)
    return order


def _t5_bucket(n):
    max_exact = 16
    if n < max_exact:
        return n
    large = max_exact + int(np.log(max(n, 1) / max_exact) / np.log(SWA_W / max_exact) * (32 - max_exact))
    return min(large, 31)


def _swa_consts(cfg, core, c):
    seg = core % cfg.NSEG
    E = np.zeros((32, 383), np.float32)
    for u in range(383):
        d = u - 127
        if 0 <= d < SWA_W:
            nn = np.maximum(np.array([d]), 0)
            large = 16 + (np.log(np.maximum(nn, 1) / 16) / np.log(SWA_W / 16) * 16).astype(np.int32)
            large = np.minimum(large, 31)
            b = int(np.where(nn < 16, nn, large)[0])
            E[b, u] = 1.0
    c["swa_E"] = E
    i = np.arange(P)[:, None]
    j = np.arange(2 * P)[None, :]
    d = i + P - j
    c["swa_maskc"] = np.where((d >= 0) & (d < SWA_W), 0.0, NEG).astype(np.float32)
    mf = np.zeros((P, 2 * P), np.float32)
    if seg == 0:
        mf[:, :P] = NEG
    c["swa_mask_first"] = mf


def _swa_layer(self, layer, X, XT, Z1):
    ctx, nc, cfg = self.ctx, self.nc, self.cfg
    T, NT, NSEG = cfg.T, cfg.NT, cfg.NSEG
    j = layer // 3
    def _qkv_src(inp, j=j):
        w = inp["swa_w_qkv"][j]
        qcols = np.concatenate([np.arange(h * 64, (h + 1) * 64) for h in _swa_head_order()])
        return np.concatenate([w[:, qcols], w[:, 2048:]], axis=1)

    def _out_src(inp, j=j):
        rows = np.concatenate([np.arange(h * 64, (h + 1) * 64) for h in _swa_head_order()])
        return inp["swa_w_out"][j][rows, :]
    w_q = self.tw("swa_w_q_t%d" % j, lambda inp: _qkv_src(inp)[:, 0:2048], D, 2048, 512)
    w_kv = self.tw("swa_w_kv_t%d" % j, lambda inp: _qkv_src(inp)[:, 2048:2560], D, 512, 256)
    w_out = self.tw("swa_w_out_t%d" % j, _out_src, D, D, 512)
    sinks_ap = self.inp("swa_sinks_%d" % j, [SWA_HQ]).ap()
    relb_ap = self.inp("rel_bias", [32, SWA_HQ]).ap()
    OT = self.scr("swa_OT", [D, T], BF16)
    QTs = self.scr("swa_QT", [D, T], BF16)
    order = _swa_head_order()
    TGW = min(512, T)
    tgs = [(t0, TGW) for t0 in range(0, T, TGW)]
    with contextlib.ExitStack() as st0:
        kT = ctx.sb(st0, "kT", [P, 2, P + T], BF16)
        vS = ctx.sb(st0, "vS", [P, 1 + NT, 256], BF16)
        biasS = ctx.sb(st0, "biasS", [P, SWA_HQ, 2 * P], F32)
        sinkb = self.bcast_rows(st0, "sinkb", sinks_ap, SWA_HQ)
        mfirst = ctx.sb(st0, "mfirst", [P, 2 * P], F32)
        ctx.dma("sp", mfirst[:], self.inp("swa_mask_first", [P, 2 * P]).ap(), writes=[mfirst])
        with contextlib.ExitStack() as st:
            E = ctx.sb(st, "E", [32, 383], F32)
            RB = ctx.sb(st, "RB", [32, SWA_HQ], F32)
            maskc = ctx.sb(st, "maskc", [P, 2 * P], F32)
            ctx.dma("sp", E[:], self.inp("swa_E", [32, 383]).ap(), writes=[E])
            ctx.dma("sp", RB[:], relb_ap, writes=[RB])
            ctx.dma("sp", maskc[:], self.inp("swa_maskc", [P, 2 * P]).ap(), writes=[maskc])
            psb = [ctx.ps(st, "psb", [P, 512], F32) for _ in range(2)]
            for r in range(16):
                ps = psb[r % 2]
                for jj in range(16):
                    jk = r * 16 + jj
                    ctx.op("pe", lambda: nc.tensor.matmul(ps[:, jj * 32:(jj + 1) * 32], lhsT=E[:, 255 - jk:383 - jk], rhs=RB[:],
                                                          start=True, stop=True), reads=[E, RB], writes=[ps])
                ctx.op("dve", lambda: nc.vector.tensor_tensor(
                    biasS[:, :, r * 16:(r + 1) * 16].rearrange("p h j -> p j h"),
                    ps[:].rearrange("p (j h) -> p j h", h=32),
                    maskc[:, r * 16:(r + 1) * 16].unsqueeze(2).broadcast_to([P, 16, 32]), ALU.add),
                    reads=[ps, maskc], writes=[biasS])
            ctx.barrier()
        with contextlib.ExitStack() as st:
            xT = self.load_AT(st, "xT", XT, KC, 0, T)
            res = GemmRes(self, st, KC, 512, 3)
            qst = [ctx.sb(st, "qst", [P, 4, TGW], BF16) for _ in range(2)]
            QTv = QTs.ap().rearrange("(kc p) t -> p kc t", p=P)
            qcnt = [0]

            def q_epi(ps, c, t0, tn):
                kc = c // P
                qs = qst[(qcnt[0] // 4) % 2]
                ctx.op("act", lambda: nc.scalar.activation(qs[:, kc % 4, 0:tn], ps[:, 0:tn], AF.Copy, scale=SWA_HD ** -0.5),
                       reads=[ps], writes=[qs])
                qcnt[0] += 1
                if kc % 4 == 3:
                    ctx.dma("sp", QTv[:, kc - 3:kc + 1, t0:t0 + tn], qs[:, :, 0:tn], reads=[qs], writes=[("QT", kc, t0)])
            self.gemm_feat(res, xT, xT, KC, w_q, 0, 2048, tgs, q_epi)

            def k_epi(ps, c, t0, tn):
                fb = c // P
                ctx.op("act", lambda: nc.scalar.copy(kT[:, fb, P + t0:P + t0 + tn], ps[:, 0:tn]), reads=[ps], writes=[kT])
            self.gemm_feat(res, xT, xT, KC, w_kv, 0, 256, tgs, k_epi, nblk=256)

            def v_epi(ps, tt, c0, nb):
                ctx.op("act", lambda: nc.scalar.copy(vS[:, 1 + tt, :], ps[:, 0:nb]), reads=[ps], writes=[vS])
            self.gemm_tok(res, xT, xT, KC, w_kv, 256, 256, range(NT), v_epi, nblk=256)
            ctx.barrier()
        with contextlib.ExitStack() as st:
            if NSEG == 1:
                ctx.op("dve", lambda: nc.vector.memset(kT[:, :, 0:P], 0.0), writes=[kT])
                ctx.op("dve", lambda: nc.vector.memset(vS[:, 0, :], 0.0), writes=[vS])
            else:
                HCI = self.scr("swa_ci", [NSEG, P, 512])
                HCO = self.scr("swa_co", [NSEG, P, 512])
                hb = ctx.sb(st, "hb", [P, 512], F32)
                hm = [ctx.sb(st, "hm", [P, 512], F32) for _ in range(2)]
                ctx.op("dve", lambda: nc.vector.tensor_copy(hb[:, 0:256].rearrange("p (a b) -> p a b", a=2), kT[:, :, T:T + P]),
                       reads=[kT], writes=[hb])
                ctx.op("dve", lambda: nc.vector.tensor_copy(hb[:, 256:512], vS[:, NT, :]), reads=[vS], writes=[hb])
                for s in range(NSEG):
                    ctx.op("dve", lambda: nc.vector.tensor_scalar_mul(hm[s % 2][:], hb[:], self.own[:, s:s + 1]),
                           reads=[hb, self.own], writes=[hm[s % 2]])
                    ctx.dma("sp", HCI.ap()[s], hm[s % 2][:], reads=[hm[s % 2]], writes=[("HCI", s)])
                ctx.barrier()
                ctx.allreduce(cfg.groups, HCI.ap().rearrange("s p e -> (s p) e"), HCO.ap().rearrange("s p e -> (s p) e"),
                              writes=[("HCO",)])
                ctx.barrier()
                for s in range(NSEG):
                    ctx.dma("sp", hm[s % 2][:], HCO.ap()[s], writes=[hm[s % 2]])
                    if s == 0:
                        ctx.op("dve", lambda: nc.vector.tensor_scalar_mul(hb[:], hm[s % 2][:], self.halo_sel[:, s:s + 1]),
                               reads=[hm[s % 2], self.halo_sel], writes=[hb])
                    else:
                        ctx.op("dve", lambda: nc.vector.scalar_tensor_tensor(hb[:], hm[s % 2][:], self.halo_sel[:, s:s + 1],
                                                                             hb[:], ALU.mult, ALU.add),
                               reads=[hm[s % 2], self.halo_sel, hb], writes=[hb])
                ctx.op("dve", lambda: nc.vector.tensor_copy(kT[:, :, 0:P], hb[:, 0:256].rearrange("p (a b) -> p a b", a=2)),
                       reads=[hb], writes=[kT])
                ctx.op("dve", lambda: nc.vector.tensor_copy(vS[:, 0, :], hb[:, 256:512]), reads=[hb], writes=[vS])
            ctx.barrier()
        with contextlib.ExitStack() as st:
            qT = self.load_AT(st, "qT", QTs, KC, 0, T)
            ps_s = ctx.ps(st, "ps_s", [P, 8, 2 * P], F32)
            ps_t = ctx.ps(st, "ps_t", [P, 16, P], BF16)
            ps_o = ctx.ps(st, "ps_o", [P, 8, P], F32)
            s_sb = ctx.sb(st, "s_sb", [P, 8, 2 * P], F32)
            e_sb = ctx.sb(st, "e_sb", [P, 8, 2 * P], F32)
            p_sb = ctx.sb(st, "p_sb", [P, 8, 2 * P], BF16)
            pT = ctx.sb(st, "pT", [P, 16, P], BF16)
            mx = ctx.sb(st, "mx", [P, 8], F32)
            nmx = ctx.sb(st, "nmx", [P, 8], F32)
            rs = ctx.sb(st, "rs", [P, 8], F32)
            es = ctx.sb(st, "es", [P, 8], F32)
            G = 4 if NT % 4 == 0 else 2
            ost = [ctx.sb(st, "ost", [P, KC, G * P], BF16) for _ in range(2)]
            OTv = OT.ap().rearrange("(kc p) t -> p kc t", p=P)
            for n in range(NT):
                og = ost[(n // G) % 2]
                for pair in range(2):
                    for par in range(2):
                        hk = 2 * pair + par
                        po = par * 64
                        kc_k = hk // 2
                        for g in range(8):
                            ch = pair * 8 + g
                            ctx.op("pe", lambda: nc.tensor.matmul(ps_s[:, g, :], lhsT=qT[po:po + 64, ch, n * P:(n + 1) * P],
                                                                  rhs=kT[po:po + 64, kc_k, n * P:n * P + 2 * P],
                                                                  start=True, stop=True),
                                   reads=[qT, kT], writes=[ps_s])
                        ctx.op("dve", lambda: nc.vector.tensor_tensor(s_sb[:], ps_s[:], biasS[:, hk * 8:(hk + 1) * 8, :], ALU.add),
                               reads=[ps_s, biasS], writes=[s_sb])
                        if n == 0:
                            ctx.op("pool", lambda: nc.gpsimd.tensor_tensor(s_sb[:], s_sb[:],
                                                                           mfirst[:].unsqueeze(1).broadcast_to([P, 8, 2 * P]), ALU.add),
                                   reads=[s_sb, mfirst], writes=[s_sb])
                        ctx.op("dve", lambda: nc.vector.tensor_reduce(mx[:], s_sb[:], AX.X, ALU.max), reads=[s_sb], writes=[mx])
                        ctx.op("dve", lambda: nc.vector.tensor_tensor(mx[:], mx[:], sinkb[:, hk * 8:(hk + 1) * 8], ALU.max),
                               reads=[mx, sinkb], writes=[mx])
                        ctx.op("dve", lambda: nc.vector.tensor_scalar_mul(nmx[:], mx[:], -1.0), reads=[mx], writes=[nmx])
                        ctx.op("dve", lambda: nc.vector.memset(rs[:], 0.0), writes=[rs])
                        for g in range(8):
                            ctx.op("act", lambda: nc.scalar.activation(e_sb[:, g, :], s_sb[:, g, :], AF.Exp, bias=nmx[:, g:g + 1],
                                                                       scale=1.0, accum_out=rs[:, g:g + 1]),
                                   reads=[s_sb, nmx], writes=[e_sb, rs])
                        ctx.op("dve", lambda: nc.vector.tensor_tensor(es[:], sinkb[:, hk * 8:(hk + 1) * 8], mx[:], ALU.subtract),
                               reads=[sinkb, mx], writes=[es])
                        ctx.op("act", lambda: nc.scalar.activation(es[:], es[:], AF.Exp), reads=[es], writes=[es])
                        ctx.op("dve", lambda: nc.vector.tensor_tensor(rs[:], rs[:], es[:], ALU.add), reads=[rs, es], writes=[rs])
                        ctx.op("dve", lambda: nc.vector.reciprocal(rs[:], rs[:]), reads=[rs], writes=[rs])
                        ctx.op("pool", lambda: nc.gpsimd.tensor_tensor(p_sb[:], e_sb[:], rs[:].unsqueeze(2).broadcast_to([P, 8, 2 * P]),
                                                                       ALU.mult), reads=[e_sb, rs], writes=[p_sb])
                        for g in range(8):
                            for hf in range(2):
                                self.transpose_to(p_sb[:, g, hf * P:(hf + 1) * P], ps_t[:, g * 2 + hf, :], [p_sb], [ps_t])
                        ctx.op("act", lambda: nc.scalar.copy(pT[:], ps_t[:]), reads=[ps_t], writes=[pT])
                        for g in range(8):
                            for hf in range(2):
                                ctx.op("pe", lambda: nc.tensor.matmul(ps_o[po:po + 64, g, :], lhsT=vS[:, n + hf, hk * 64:(hk + 1) * 64],
                                                                      rhs=pT[:, g * 2 + hf, :], start=(hf == 0), stop=(hf == 1)),
                                       reads=[vS, pT], writes=[ps_o])
                    ctx.op("dve", lambda: nc.vector.tensor_copy(og[:, pair * 8:(pair + 1) * 8, (n % G) * P:(n % G + 1) * P], ps_o[:]),
                           reads=[ps_o], writes=[og])
                if n % G == G - 1:
                    g0 = (n // G) * G * P
                    ctx.dma("sp", OTv[:, :, g0:g0 + G * P], og[:], reads=[og], writes=[("OT", n)])
            ctx.barrier()
    with contextlib.ExitStack() as st:
        aT = self.load_AT(st, "oTa", OT, KC, 0, T)
        res = GemmRes(self, st, KC, 512, 3)
        epi = self.epi_resid(st, X, Z1)
        self.gemm_tok(res, aT, aT, KC, w_out, 0, D, range(NT), epi)
        ctx.barrier()


Prog.swa_layer = _swa_layer
RW_H, RW_HD = 32, 64
RW_EPS = 64e-5
RW_NHB = RW_H // 2


def _rwkv_consts(cfg, core, c):
    seg = core % cfg.NSEG
    f32 = np.float32
    s = np.arange(P)[:, None]
    t = np.arange(P)[None, :]
    c["rw_mus"] = (s < t).astype(f32)
    c["rw_mui"] = (s <= t).astype(f32)
    c["rw_mls"] = (s > t).astype(f32)
    c["rw_tri"] = (s <= t).astype(f32)
    c["rw_suf"] = (s > t).astype(f32)
    i2 = np.zeros((P, 64), f32)
    i2[np.arange(P), np.arange(P) % 64] = 1.0
    c["rw_i2"] = i2
    selm = np.zeros((P, cfg.NSEG), f32)
    selm[:, :seg] = 1.0
    c["rw_selm"] = selm
    c["rw_nselm"] = 1.0 - selm


def _rwkv_layer(self, layer, X, XT, Z1):
    ctx, nc, cfg = self.ctx, self.nc, self.cfg
    T, NT, NSEG = cfg.T, cfg.NT, cfg.NSEG
    j = layer // 3
    gi = lambda name, shape: self.inp("%s_%d" % (name, j), shape).ap()
    tw_ = lambda nm, fn, K_, N_, t_, kp_=P: self.tw("%s_t%d" % (nm, j), fn, K_, N_, t_, kp_)
    w_rkv = [tw_("rwkv_w_rkv%d" % i, (lambda inp, i=i: inp["rwkv_w_rkv"][j][i]), D, D, 256) for i in range(3)]
    w1 = tw_("rwkv_w1", lambda inp: inp["rwkv_w1"][j], D, 96, 96)
    w2 = tw_("rwkv_w2", lambda inp: inp["rwkv_w2"][j], 96, D, 256, 96)
    a1 = tw_("rwkv_a1", lambda inp: inp["rwkv_a1"][j], D, 96, 96)
    a2 = tw_("rwkv_a2", lambda inp: inp["rwkv_a2"][j], 96, D, 256, 96)
    g1 = tw_("rwkv_g1", lambda inp: inp["rwkv_g1"][j], D, 256, 256)
    g2 = tw_("rwkv_g2", lambda inp: inp["rwkv_g2"][j], 256, D, 256)
    w_out = tw_("rwkv_w_out", lambda inp: inp["rwkv_w_out"][j], D, D, 512)
    mix_ap = gi("rwkv_mix", [6, D]).rearrange("i (kc p) -> (i kc) p", p=P)
    Rs, Ks, Vs = self.scr("rw_R", [T, D]), self.scr("rw_K", [T, D]), self.scr("rw_V", [T, D])
    WLs, ALs, Gs = self.scr("rw_WL", [T, D]), self.scr("rw_AL", [T, D]), self.scr("rw_G", [T, D])
    Y0 = self.scr("rw_Y0", [T, D])
    BON = self.scr("rw_BON", [T, RW_H])
    YTR = self.scr("rw_YTR", [NT, P, RW_NHB, P], BF16)
    OGT = self.scr("rw_OGT", [D, T], BF16)
    TGW = min(512, T)
    tgs = [(t0, TGW) for t0 in range(0, T, TGW)]

    with contextlib.ExitStack() as st:
        psc = ctx.ps(st, "psc", [P, 512], F32)
        mixc = self.load_cols(st, "mixc", mix_ap, 6 * KC, psc)
        xT = self.load_AT(st, "xT", XT, KC, 0, T, pad=1)
        with contextlib.ExitStack() as sth:
            halo = self.halo_rows(sth, X, 1, "halo1")
            for kc in range(KC):
                ctx.op("pe", lambda: nc.tensor.matmul(psc[:, kc:kc + 1], lhsT=halo[0:1, kc * P:(kc + 1) * P],
                                                      rhs=self.identf[0:1, 0:1], start=True, stop=True),
                       reads=[halo, self.identf], writes=[psc])
            ctx.op("dve", lambda: nc.vector.tensor_copy(xT[:, :, 0:1], psc[:, 0:KC].unsqueeze(2)), reads=[psc], writes=[xT])
            ctx.barrier()
        xm = ctx.sb(st, "xm", [P, KC, T], BF16)
        dtmp = [ctx.sb(st, "dtmp", [P, T], F32) for _ in range(2)]
        hT = ctx.sb(st, "hT", [P, 2, T], BF16)
        res = GemmRes(self, st, KC, 256, 4)
        obuf = [ctx.sb(st, "obuf", [P, 512], F32) for _ in range(3)]
        ocnt = [0]

        def store_epi(dst):
            def epi(ps, tt, c0, nb):
                o = obuf[ocnt[0] % 3]
                ocnt[0] += 1
                ctx.op("act", lambda: nc.scalar.copy(o[:, 0:nb], ps[:, 0:nb]), reads=[ps], writes=[o])
                ctx.dma("sp", dst.ap()[tt * P:(tt + 1) * P, c0:c0 + nb], o[:, 0:nb], reads=[o], writes=[("o", id(dst), tt, c0)])
            return epi

        def build_mix(i):
            for kc in range(KC):
                d = dtmp[kc % 2]
                ctx.op("pool", lambda: nc.gpsimd.tensor_tensor(d[:], xT[:, kc, 0:T], xT[:, kc, 1:T + 1], ALU.subtract),
                       reads=[xT], writes=[d])
                ctx.op("dve", lambda: nc.vector.scalar_tensor_tensor(xm[:, kc, :], d[:], mixc[:, i * KC + kc:i * KC + kc + 1],
                                                                     xT[:, kc, 1:T + 1], ALU.mult, ALU.add),
                       reads=[d, mixc, xT], writes=[xm])

        def lora(i, wa, na, func, wb_, dst):
            build_mix(i)
            kcn2 = (na + P - 1) // P
            kp = min(P, na)

            def h_epi(ps, c, t0, tn):
                fw = min(P, na - c)
                ctx.op("act", lambda: nc.scalar.activation(hT[0:fw, c // P, t0:t0 + tn], ps[0:fw, 0:tn], func),
                       reads=[ps], writes=[hT])
            self.gemm_feat(res, xm, xm, KC, wa, 0, na, tgs, h_epi, nblk=256)
            self.gemm_tok(res, hT, hT, kcn2, wb_, 0, D, range(NT), store_epi(dst), kp=kp, nblk=256)

        build_mix(0)
        self.gemm_tok(res, xm, xm, KC, w_rkv[0], 0, D, range(NT), store_epi(Rs), nblk=256)
        build_mix(2)
        self.gemm_tok(res, xm, xm, KC, w_rkv[1], 0, D, range(NT), store_epi(Ks), nblk=256)
        build_mix(3)
        self.gemm_tok(res, xm, xm, KC, w_rkv[2], 0, D, range(NT), store_epi(Vs), nblk=256)
        lora(1, w1, 96, AF.Tanh, w2, WLs)
        lora(4, a1, 96, AF.Identity, a2, ALs)
        lora(5, g1, 256, AF.Sigmoid, g2, Gs)
        ctx.barrier()
    if cfg.stop == "rw1":
        return

    SXs = self.scr("rw_SX", [P, RW_NHB, P])
    with contextlib.ExitStack() as st:
        def cload(name, shape, dtype=F32):
            t_ = ctx.sb(st, name, shape, dtype)
            ctx.dma("sp", t_[:], self.inp(name, shape, dtype).ap(), writes=[t_])
            return t_
        mus, mui, mls = cload("rw_mus", [P, P]), cload("rw_mui", [P, P]), cload("rw_mls", [P, P])
        tri, suft = cload("rw_tri", [P, P]), cload("rw_suf", [P, P])
        i2 = cload("rw_i2", [P, 64])
        ones = ctx.sb(st, "ones", [P, 1], F32)
        ctx.op("dve", lambda: nc.vector.memset(ones[:], 1.0), writes=[ones])
        w0b = self.bcast_rows(st, "w0b", gi("rwkv_w0", [D]), D)
        a0b = self.bcast_rows(st, "a0b", gi("rwkv_a0", [D]), D)
        kkb = self.bcast_rows(st, "kkb", gi("rwkv_k_k", [D]), D)
        kab = self.bcast_rows(st, "kab", gi("rwkv_k_a", [D]), D)
        rkb = self.bcast_rows(st, "rkb", gi("rwkv_r_k", [RW_H, RW_HD]).rearrange("h d -> (h d)"), D)
        A = ctx.sb(st, "A", [P, D], F32)
        B = ctx.sb(st, "B", [P, D], F32)
        Dw = ctx.sb(st, "Dw", [P, D], F32)
        Ea = ctx.sb(st, "Ea", [P, D], F32)
        Fk = ctx.sb(st, "Fk", [P, D], F32)
        T1 = ctx.sb(st, "T1", [P, D], F32)
        ET = [ctx.sb(st, "ET", [P, 512], F32) for _ in range(4)]
        tok = [ctx.sb(st, "tokb", [P, D], BF16) for _ in range(4)]
        bbk = ctx.sb(st, "bbk", [P, 2, D], BF16)
        vbx = ctx.sb(st, "vbx", [P, RW_H, P], BF16)
        ctx.op("pool", lambda: nc.gpsimd.memset(vbx[:], 0.0), writes=[vbx])
        CM = ctx.sb(st, "CM", [P, RW_NHB, 4, P], BF16)
        ytile = ctx.sb(st, "ytile", [P, D], F32)
        T2 = ytile
        ytr = ctx.sb(st, "ytr", [P, RW_NHB, P], BF16)
        ss = ctx.sb(st, "ss", [P, RW_H], F32)
        bon = ctx.sb(st, "bon", [P, RW_H], F32)
        dectot = ctx.sb(st, "dectot", [P, RW_NHB], F32)
        SX = ctx.sb(st, "SX", [P, RW_NHB, P], F32)
        SXb = ctx.sb(st, "SXb", [P, RW_NHB, P], BF16)
        NPAIR = 3

        class PairRes:
            pass
        PR = []
        for _ in range(NPAIR):
            r_ = PairRes()
            r_.GA = [ctx.sb(st, "GA", [P, 2, 2, P], BF16) for _ in range(2)]
            r_.Brb = ctx.sb(st, "Brb", [P, 2, P], BF16)
            r_.Aak = ctx.sb(st, "Aak", [P, 2, P], BF16)
            r_.Brk = ctx.sb(st, "Brk", [P, 2, P], BF16)
            r_.Tt = [ctx.sb(st, "Tt", [P, 2, P], BF16) for _ in range(2)]
            r_.Wb = ctx.sb(st, "Wb", [P, 2, P], BF16)
            r_.Ub = ctx.sb(st, "Ub", [P, 2, P], BF16)
            r_.H = [ctx.ps(st, "pbH", [P, 512], F32) for _ in range(2)]
            PR.append(r_)
        pbx = ctx.ps(st, "pbx", [P, 512], F32)
        for i_, r_ in enumerate(PR):
            r_.S = pbx
            r_.so = i_ * P
        pb = [PR[0].H[0], PR[0].H[1], PR[1].H[0], PR[1].H[1], PR[2].H[0]]
        ptr = ctx.ps(st, "ptr", [P, 8, P], BF16)
        sxk = [(id(SX), hb) for hb in range(RW_NHB)]
        sxbk = [(id(SXb), hb) for hb in range(RW_NHB)]
        ctx.op("dve", lambda: nc.vector.memset(SX[:], 0.0), writes=sxk)
        ctx.op("dve", lambda: nc.vector.tensor_copy(SX[:, :, 64:128], i2[:].unsqueeze(1).broadcast_to([P, RW_NHB, 64])),
               reads=[i2] + sxk, writes=sxk)
        ctx.op("pool", lambda: nc.gpsimd.tensor_copy(SXb[:], SX[:]), reads=sxk, writes=sxbk)
        v3 = lambda t_: t_[:].rearrange("p (h d) -> p h d", d=RW_HD)
        bc3 = lambda small: small[:].unsqueeze(2).broadcast_to([P, RW_H, RW_HD])
        for n in range(NT):
            rows = slice(n * P, (n + 1) * P)
            ctx.dma("sp", A[:], Rs.ap()[rows, :], writes=[A])
            ctx.dma("sp", B[:], Ks.ap()[rows, :], writes=[B])
            ctx.dma("sp", T1[:], Vs.ap()[rows, :], writes=[T1])
            ctx.op("act", lambda: nc.scalar.copy(vbx[:, :, 0:64], v3(T1)), reads=[T1], writes=[vbx])
            ctx.dma("sp", Dw[:], WLs.ap()[rows, :], writes=[Dw])
            ctx.dma("sp", Ea[:], ALs.ap()[rows, :], writes=[Ea])
            ctx.op("dve", lambda: nc.vector.tensor_tensor(Dw[:], Dw[:], w0b[:], ALU.add), reads=[Dw, w0b], writes=[Dw])
            ctx.op("act", lambda: nc.scalar.activation(Dw[:], Dw[:], AF.Sigmoid), reads=[Dw], writes=[Dw])
            ctx.op("dve", lambda: nc.vector.tensor_scalar_mul(Dw[:], Dw[:], -math.exp(-0.5)), reads=[Dw], writes=[Dw])
            ctx.op("dve", lambda: nc.vector.tensor_tensor(Ea[:], Ea[:], a0b[:], ALU.add), reads=[Ea, a0b], writes=[Ea])
            ctx.op("act", lambda: nc.scalar.activation(Ea[:], Ea[:], AF.Sigmoid), reads=[Ea], writes=[Ea])
            ctx.op("pool", lambda: nc.gpsimd.tensor_tensor(Fk[:], B[:], kkb[:], ALU.mult), reads=[B, kkb], writes=[Fk])
            ctx.op("pool", lambda: nc.gpsimd.tensor_tensor(T2[:], Fk[:], Fk[:], ALU.mult), reads=[Fk], writes=[T2])
            ctx.op("dve", lambda: nc.vector.tensor_reduce(ss[:], v3(T2), AX.X, ALU.add), reads=[T2], writes=[ss])
            ctx.op("act", lambda: nc.scalar.activation(ss[:], ss[:], AF.Sqrt), reads=[ss], writes=[ss])
            ctx.op("dve", lambda: nc.vector.tensor_scalar_max(ss[:], ss[:], 1e-12), reads=[ss], writes=[ss])
            ctx.op("dve", lambda: nc.vector.reciprocal(ss[:], ss[:]), reads=[ss], writes=[ss])
            ctx.op("dve", lambda: nc.vector.tensor_tensor(v3(Fk), v3(Fk), bc3(ss), ALU.mult), reads=[Fk, ss], writes=[Fk])
            ctx.op("dve", lambda: nc.vector.scalar_tensor_tensor(T1[:], Ea[:], -1.0, kab[:], ALU.add, ALU.mult),
                   reads=[Ea, kab], writes=[T1])
            ctx.op("pool", lambda: nc.gpsimd.tensor_tensor(T1[:], T1[:], B[:], ALU.mult), reads=[T1, B], writes=[T1])
            ctx.op("pool", lambda: nc.gpsimd.tensor_tensor(B[:], B[:], T1[:], ALU.add), reads=[T1, B], writes=[B])
            ctx.op("pool", lambda: nc.gpsimd.tensor_tensor(T2[:], A[:], B[:], ALU.mult), reads=[A, B, T2], writes=[T2])
            ctx.op("dve", lambda: nc.vector.tensor_tensor(T2[:], T2[:], rkb[:], ALU.mult), reads=[T2, rkb], writes=[T2])
            ctx.op("dve", lambda: nc.vector.tensor_reduce(bon[:], v3(T2), AX.X, ALU.add), reads=[T2], writes=[bon])
            ctx.dma("sp", BON.ap()[rows, :], bon[:], reads=[bon], writes=[("BON", n)])
            ctx.op("dve", lambda: nc.vector.tensor_tensor(T1[:], Fk[:], Ea[:], ALU.mult), reads=[Fk, Ea, T1], writes=[T1])
            for hb in range(RW_NHB):
                ctx.op("pe", lambda: nc.tensor.matmul(pbx[:, 448 + hb:448 + hb + 1], lhsT=Dw[:, hb * P:(hb + 1) * P], rhs=ones[:, 0:1],
                                                      start=True, stop=True), reads=[Dw, ones], writes=[pbx])
            ctx.op("act", lambda: nc.scalar.activation(dectot[:], pbx[:, 448:448 + RW_NHB], AF.Exp), reads=[pbx], writes=[dectot])
            for cb in range(4):
                cs = slice(cb * 512, (cb + 1) * 512)
                pcum, psuf = pb[1 + (cb % 2) * 2], pb[2 + (cb % 2) * 2]
                ctx.op("pe", lambda: nc.tensor.matmul(pcum[:], lhsT=tri[:], rhs=Dw[:, cs], start=True, stop=True),
                       reads=[tri, Dw], writes=[pcum])
                ctx.op("pe", lambda: nc.tensor.matmul(psuf[:], lhsT=suft[:], rhs=Dw[:, cs], start=True, stop=True),
                       reads=[suft, Dw], writes=[psuf])
                ctx.op("act", lambda: nc.scalar.activation(ET[0][:], pcum[:], AF.Exp), reads=[pcum], writes=[ET[0]])
                ctx.op("pool", lambda: nc.gpsimd.tensor_tensor(tok[3][:, cs], A[:, cs], ET[0][:], ALU.mult),
                       reads=[A, ET[0]], writes=[tok[3]])
                ctx.op("act", lambda: nc.scalar.activation(ET[1][:], pcum[:], AF.Exp, scale=-1.0), reads=[pcum], writes=[ET[1]])
                ctx.op("dve", lambda: nc.vector.tensor_tensor(tok[0][:, cs], T1[:, cs], ET[1][:], ALU.mult),
                       reads=[T1, ET[1]], writes=[tok[0]])
                ctx.op("pool", lambda: nc.gpsimd.tensor_tensor(tok[1][:, cs], B[:, cs], ET[1][:], ALU.mult),
                       reads=[B, ET[1]], writes=[tok[1]])
                ctx.op("dve", lambda: nc.vector.tensor_tensor(ET[2][:], pcum[:], Dw[:, cs], ALU.subtract),
                       reads=[pcum, Dw], writes=[ET[2]])
                ctx.op("act", lambda: nc.scalar.activation(ET[2][:], ET[2][:], AF.Exp), reads=[ET[2]], writes=[ET[2]])
                ctx.op("dve", lambda: nc.vector.scalar_tensor_tensor(tok[2][:, cs], Fk[:, cs], -1.0, ET[2][:], ALU.mult, ALU.mult),
                       reads=[Fk, ET[2]], writes=[tok[2]])
                ctx.op("act", lambda: nc.scalar.activation(ET[3][:], psuf[:], AF.Exp), reads=[psuf], writes=[ET[3]])
                ctx.op("dve", lambda: nc.vector.tensor_tensor(bbk[:, 0, cs], T1[:, cs], ET[3][:], ALU.mult),
                       reads=[T1, ET[3]], writes=[bbk])
                ctx.op("pool", lambda: nc.gpsimd.tensor_tensor(bbk[:, 1, cs], B[:, cs], ET[3][:], ALU.mult),
                       reads=[B, ET[3]], writes=[bbk])
            for kind in range(4):
                for half in range(2):
                    for jj in range(8):
                        hb = half * 8 + jj
                        self.transpose_to(tok[kind][:, hb * P:(hb + 1) * P], ptr[:, jj, :], [tok[kind]], [ptr])
                    if (kind + half) % 2 == 0:
                        ctx.op("act", lambda: nc.scalar.copy(CM[:, half * 8:(half + 1) * 8, kind, :], ptr[:]), reads=[ptr], writes=[CM])
                    else:
                        ctx.op("dve", lambda: nc.vector.tensor_copy(CM[:, half * 8:(half + 1) * 8, kind, :], ptr[:]), reads=[ptr], writes=[CM])
            def pair_gen(hb, R):
                H = R.H
                g0 = R.GA[0]
                hp = ((0, 0), (1, 64))
                for hi, po in hp:
                    rhs_ar = CM[po:po + 64, hb, 2:4, :].rearrange("p k t -> p (k t)")
                    ctx.op("pe", lambda: nc.tensor.matmul(H[hi][:, 0:256], lhsT=CM[po:po + 64, hb, 0, :], rhs=rhs_ar,
                                                          start=True, stop=True), reads=[CM], writes=[H[hi]])
                    ctx.op("pe", lambda: nc.tensor.matmul(H[hi][:, 256:384], lhsT=CM[po:po + 64, hb, 2, :],
                                                          rhs=CM[po:po + 64, hb, 0, :], start=True, stop=True),
                           reads=[CM], writes=[H[hi]])
                yield
                for hi, po in hp:
                    ctx.op("dve", lambda: nc.vector.tensor_tensor(g0[:, hi, 0, :], H[hi][:, 0:P], mus[:], ALU.mult),
                           reads=[H[hi], mus], writes=[g0])
                    ctx.op("dve", lambda: nc.vector.tensor_tensor(R.Brb[:, hi, :], H[hi][:, P:2 * P], mui[:], ALU.mult),
                           reads=[H[hi], mui], writes=[R.Brb])
                    ctx.op("dve", lambda: nc.vector.tensor_tensor(g0[:, hi, 1, :], H[hi][:, 2 * P:3 * P], mls[:], ALU.mult),
                           reads=[H[hi], mls], writes=[g0])
                    ctx.op("pool", lambda: nc.gpsimd.tensor_tensor(R.Tt[0][:, hi, :], g0[:, hi, 0, :], self.ident[:], ALU.add),
                           reads=[g0, self.ident], writes=[R.Tt[0]])
                yield
                if cfg.stop == "g1":
                    return
                tcur = 0
                for lvl in range(1, 7):
                    gc, gn = R.GA[(lvl - 1) % 2], R.GA[lvl % 2]
                    for hi, po in hp:
                        if lvl < 6:
                            ctx.op("pe", lambda: nc.tensor.matmul(H[hi][:, 0:P], lhsT=gc[:, hi, 1, :], rhs=gc[:, hi, 0, :],
                                                                  start=True, stop=True), reads=[gc], writes=[H[hi]])
                        ctx.op("pe", lambda: nc.tensor.matmul(H[hi][:, P:2 * P], lhsT=gc[:, hi, 0, :], rhs=gc[:, hi, 1, :],
                                                              start=True, stop=True), reads=[gc], writes=[H[hi]])
                    yield
                    for hi, po in hp:
                        if lvl < 6:
                            ctx.op("act", lambda: nc.scalar.copy(gn[:, hi].rearrange("p k t -> p (k t)"), H[hi][:, 0:2 * P]),
                                   reads=[H[hi]], writes=[gn])
                        else:
                            ctx.op("act", lambda: nc.scalar.copy(gn[:, hi, 1, :], H[hi][:, P:2 * P]), reads=[H[hi]], writes=[gn])
                    yield
                    for hi, po in hp:
                        ctx.op("pe", lambda: nc.tensor.matmul(H[hi][:, 2 * P:3 * P], lhsT=gn[:, hi, 1, :], rhs=R.Tt[tcur][:, hi, :],
                                                              start=True, stop=True), reads=[gn, R.Tt[tcur]], writes=[H[hi]])
                    yield
                    for hi, po in hp:
                        ctx.op("dve", lambda: nc.vector.tensor_tensor(R.Tt[1 - tcur][:, hi, :], H[hi][:, 2 * P:3 * P], R.Tt[tcur][:, hi, :],
                                                                      ALU.add), reads=[H[hi], R.Tt[tcur]], writes=[R.Tt[1 - tcur]])
                    tcur = 1 - tcur
                    yield
                TT = R.Tt[tcur]
                if cfg.stop == "g2":
                    return
                for hi, po in hp:
                    rhs_ar = CM[po:po + 64, hb, 2:4, :].rearrange("p k t -> p (k t)")
                    ctx.op("pe", lambda: nc.tensor.matmul(H[hi][:, 0:256], lhsT=CM[po:po + 64, hb, 1, :], rhs=rhs_ar,
                                                          start=True, stop=True), reads=[CM], writes=[H[hi]])
                    ctx.op("pe", lambda: nc.tensor.matmul(H[hi][:, 2 * P:3 * P], lhsT=CM[po:po + 64, hb, 2, :],
                                                          rhs=SXb[po:po + 64, hb, :], start=True, stop=False),
                           reads=[CM, (id(SXb), hb)], writes=[H[hi]])
                yield
                for hi, po in hp:
                    ctx.op("dve", lambda: nc.vector.tensor_tensor(R.Aak[:, hi, :], H[hi][:, 0:P], mus[:], ALU.mult),
                           reads=[H[hi], mus], writes=[R.Aak])
                    ctx.op("dve", lambda: nc.vector.tensor_tensor(R.Brk[:, hi, :], H[hi][:, P:2 * P], mui[:], ALU.mult),
                           reads=[H[hi], mui], writes=[R.Brk])
                yield
                for hi, po in hp:
                    h = 2 * hb + hi
                    ctx.op("pe", lambda: nc.tensor.matmul(H[hi][:, 2 * P:3 * P], lhsT=R.Aak[:, hi, :], rhs=vbx[:, h, :],
                                                          start=False, stop=True), reads=[R.Aak, vbx], writes=[H[hi]])
                yield
                for hi, po in hp:
                    ctx.op("act", lambda: nc.scalar.copy(R.Wb[:, hi, :], H[hi][:, 2 * P:3 * P]), reads=[H[hi]], writes=[R.Wb])
                yield
                for hi, po in hp:
                    ctx.op("pe", lambda: nc.tensor.matmul(H[hi][:, 3 * P:4 * P], lhsT=TT[:, hi, :], rhs=R.Wb[:, hi, :],
                                                          start=True, stop=True), reads=[TT, R.Wb], writes=[H[hi]])
                yield
                for hi, po in hp:
                    ctx.op("dve", lambda: nc.vector.tensor_copy(R.Ub[:, hi, :], H[hi][:, 3 * P:4 * P]), reads=[H[hi]], writes=[R.Ub])
                yield
                if cfg.stop == "g3":
                    return
                for hi, po in hp:
                    h = 2 * hb + hi
                    yo = H[hi][:, 0:64]
                    ctx.op("pe", lambda: nc.tensor.matmul(yo, lhsT=CM[po:po + 64, hb, 3, :], rhs=SXb[po:po + 64, hb, 0:64],
                                                          start=True, stop=False), reads=[CM, (id(SXb), hb)], writes=[H[hi]])
                    ctx.op("pe", lambda: nc.tensor.matmul(yo, lhsT=R.Brb[:, hi, :], rhs=R.Ub[:, hi, 0:64], start=False, stop=False),
                           reads=[R.Brb, R.Ub], writes=[H[hi]])
                    ctx.op("pe", lambda: nc.tensor.matmul(yo, lhsT=R.Brk[:, hi, :], rhs=vbx[:, h, 0:64], start=False, stop=True),
                           reads=[R.Brk, vbx], writes=[H[hi]])
                    if NSEG > 1:
                        to = H[hi][po:po + 64, P:2 * P]
                        ctx.op("pe", lambda: nc.tensor.matmul(to, lhsT=SXb[po:po + 64, hb, 64:128], rhs=CM[po:po + 64, hb, 3, :],
                                                              start=True, stop=False), reads=[CM, (id(SXb), hb)], writes=[H[hi]])
                        ctx.op("pe", lambda: nc.tensor.matmul(to, lhsT=R.Ub[:, hi, 64:128], rhs=R.Brb[:, hi, :], start=False, stop=True),
                               reads=[R.Brb, R.Ub], writes=[H[hi]])
                    so = R.S[po:po + 64, R.so:R.so + P]
                    ctx.op("pe", lambda: nc.tensor.matmul(so, lhsT=bbk[:, 0, h * 64:(h + 1) * 64], rhs=R.Ub[:, hi, :], start=True, stop=False),
                           reads=[bbk, R.Ub], writes=[R.S])
                    ctx.op("pe", lambda: nc.tensor.matmul(so, lhsT=bbk[:, 1, h * 64:(h + 1) * 64], rhs=vbx[:, h, :], start=False, stop=True),
                           reads=[bbk, vbx], writes=[R.S])
                yield
                for hi, po in hp:
                    ctx.op("act", lambda: nc.scalar.copy(ytile[:, hb * P + hi * 64:hb * P + (hi + 1) * 64], H[hi][:, 0:64]),
                           reads=[H[hi]], writes=[(id(ytile), hb)])
                    if NSEG > 1:
                        ctx.op("act", lambda: nc.scalar.copy(ytr[po:po + 64, hb, :], H[hi][po:po + 64, P:2 * P]),
                               reads=[H[hi]], writes=[(id(ytr), hb)])
                ctx.op("dve", lambda: nc.vector.scalar_tensor_tensor(SX[:, hb, :], SX[:, hb, :], dectot[:, hb:hb + 1],
                                                                     R.S[:, R.so:R.so + P], ALU.mult, ALU.add),
                       reads=[(id(SX), hb), dectot, R.S], writes=[(id(SX), hb)])
                ctx.op("pool", lambda: nc.gpsimd.tensor_copy(SXb[:, hb, :], SX[:, hb, :]), reads=[(id(SX), hb)], writes=[(id(SXb), hb)])
                yield

            if cfg.stop != "rw2p":
                for g0_ in range(0, RW_NHB, NPAIR):
                    gens = [pair_gen(hb, PR[i]) for i, hb in enumerate(range(g0_, min(RW_NHB, g0_ + NPAIR)))]
                    while gens:
                        for g_ in list(gens):
                            try:
                                next(g_)
                            except StopIteration:
                                gens.remove(g_)
            all_hb = list(range(RW_NHB))
            ctx.dma("sp", Y0.ap()[rows, :], ytile[:], reads=[ytile] + [(id(ytile), hb) for hb in all_hb], writes=[("Y0", n)])
            if NSEG > 1:
                ctx.dma("sp", YTR.ap()[n], ytr[:], reads=[ytr] + [(id(ytr), hb) for hb in all_hb], writes=[("YTR", n)])
        if NSEG > 1:
            ctx.dma("sp", SXs.ap(), SX[:], reads=[SX] + [(id(SX), hb) for hb in range(RW_NHB)], writes=[("SXs",)])
        ctx.barrier()

    if cfg.stop in ("rw2", "rw2p", "g1", "g2", "g3"):
        return
    S0b = ctx.sb(self.top, "rw_S0b_%d" % layer, [P, RW_NHB, 64], BF16)
    if NSEG > 1:
        NSL = NSEG - 1
        CI = self.scr("rw_ci", [NSL, P, RW_NHB * P])
        CO = self.scr("rw_co", [NSL, P, RW_NHB * P])
        with contextlib.ExitStack() as st:
            sx = ctx.sb(st, "sx", [P, RW_NHB * P], F32)
            sm = [ctx.sb(st, "sm", [P, RW_NHB * P], F32) for _ in range(2)]
            ctx.dma("sp", sx[:], SXs.ap().rearrange("p h k -> p (h k)"), writes=[sx])
            for s in range(NSL):
                ctx.op("dve", lambda: nc.vector.tensor_scalar_mul(sm[s % 2][:], sx[:], self.own[:, s:s + 1]),
                       reads=[sx, self.own], writes=[sm[s % 2]])
                ctx.dma("sp", CI.ap()[s], sm[s % 2][:], reads=[sm[s % 2]], writes=[("CI", s)])
            ctx.barrier()
            ctx.allreduce(cfg.groups, CI.ap().rearrange("s p e -> (s p) e"), CO.ap().rearrange("s p e -> (s p) e"),
                          writes=[("CO",)])
            ctx.barrier()
            selm = ctx.sb(st, "selm", [P, NSEG], F32)
            nselm = ctx.sb(st, "nselm", [P, NSEG], F32)
            ctx.dma("sp", selm[:], self.inp("rw_selm", [P, NSEG]).ap(), writes=[selm])
            ctx.dma("sp", nselm[:], self.inp("rw_nselm", [P, NSEG]).ap(), writes=[nselm])
            i2 = ctx.sb(st, "i2", [P, 64], F32)
            ctx.dma("sp", i2[:], self.inp("rw_i2", [P, 64]).ap(), writes=[i2])
            S0 = ctx.sb(st, "S0", [P, RW_NHB, 64], F32)
            ctx.op("dve", lambda: nc.vector.memset(S0[:], 0.0), writes=[S0])
            Mp = ctx.sb(st, "Mp", [P, RW_NHB, 64], F32)
            Lp = ctx.sb(st, "Lp", [P, RW_NHB, 64], F32)
            MT = ctx.sb(st, "MT", [P, RW_NHB, 64], F32)
            pm = [ctx.ps(st, "pm", [P, 8, 64], F32) for _ in range(2)]
            for s in range(NSL):
                slot = sm[s % 2]
                ctx.dma("sp", slot[:], CO.ap()[s], writes=[slot])
                sv = slot[:].rearrange("p (h k) -> p h k", k=P)
                ctx.op("dve", lambda: nc.vector.tensor_scalar_mul(Lp[:], sv[:, :, 0:64], selm[:, s:s + 1]),
                       reads=[slot, selm], writes=[Lp])
                ctx.op("dve", lambda: nc.vector.tensor_scalar_mul(Mp[:], sv[:, :, 64:128], selm[:, s:s + 1]),
                       reads=[slot, selm], writes=[Mp])
                ctx.op("dve", lambda: nc.vector.scalar_tensor_tensor(Mp[:], i2[:].unsqueeze(1).broadcast_to([P, RW_NHB, 64]),
                                                                     nselm[:, s:s + 1], Mp[:], ALU.mult, ALU.add),
                       reads=[i2, nselm, Mp], writes=[Mp])
                for half in range(2):
                    for jj in range(8):
                        hb = half * 8 + jj
                        for hi, po in enumerate((0, 64)):
                            ctx.op("pe", lambda: nc.tensor.matmul(pm[hi][po:po + 64, jj, :], lhsT=Mp[po:po + 64, hb, :],
                                                                  rhs=self.identf[po:po + 64, po:po + 64], start=True, stop=True),
                                   reads=[Mp, self.identf], writes=[pm[hi]])
                    for hi, po in enumerate((0, 64)):
                        ctx.op("act", lambda: nc.scalar.copy(MT[po:po + 64, half * 8:(half + 1) * 8, :], pm[hi][po:po + 64]),
                               reads=[pm[hi]], writes=[MT])
                for half in range(2):
                    for jj in range(8):
                        hb = half * 8 + jj
                        for hi, po in enumerate((0, 64)):
                            ctx.op("pe", lambda: nc.tensor.matmul(pm[hi][po:po + 64, jj, :], lhsT=MT[po:po + 64, hb, :],
                                                                  rhs=S0[po:po + 64, hb, :], start=True, stop=True),
                                   reads=[MT, S0], writes=[pm[hi]])
                    for hi, po in enumerate((0, 64)):
                        ctx.op("dve", lambda: nc.vector.tensor_tensor(S0[po:po + 64, half * 8:(half + 1) * 8, :], pm[hi][po:po + 64],
                                                                      Lp[po:po + 64, half * 8:(half + 1) * 8, :], ALU.add),
                               reads=[pm[hi], Lp, S0], writes=[S0])
            ctx.op("act", lambda: nc.scalar.copy(S0b[:], S0[:]), reads=[S0], writes=[S0b])
            ctx.barrier()

    if cfg.stop == "rw3":
        return
    with contextlib.ExitStack() as st:
        gnb = self.bcast_rows(st, "gnb", gi("rwkv_gn_gain", [D]), D)
        gbb = self.bcast_rows(st, "gbb", gi("rwkv_gn_bias", [D]), D)
        y = ctx.sb(st, "y", [P, D], F32)
        vv = ctx.sb(st, "vv", [P, D], F32)
        gg = ctx.sb(st, "gg", [P, D], F32)
        sq = ctx.sb(st, "sq", [P, D], F32)
        bon = ctx.sb(st, "bon", [P, RW_H], F32)
        s1 = ctx.sb(st, "s1", [P, RW_H], F32)
        s2 = ctx.sb(st, "s2", [P, RW_H], F32)
        ytr = ctx.sb(st, "ytr", [P, RW_NHB, P], BF16)
        ogb = [ctx.sb(st, "ogb", [P, D], BF16) for _ in range(2)]
        G = 4 if NT % 4 == 0 else 2
        stg = [ctx.sb(st, "stg", [P, KC, G * P], BF16) for _ in range(2)]
        pst = [[ctx.ps(st, "pst", [P, 8 * P], BF16) for _ in range(2)] for _ in range(2)]
        pc = [ctx.ps(st, "pc", [P, 512], F32) for _ in range(4)]
        OGTv = OGT.ap().rearrange("(kc p) t -> p kc t", p=P)
        v3 = lambda t_: t_[:].rearrange("p (h d) -> p h d", d=RW_HD)
        bc3 = lambda small: small[:].unsqueeze(2).broadcast_to([P, RW_H, RW_HD])
        for n in range(NT):
            rows = slice(n * P, (n + 1) * P)
            ctx.dma("sp", y[:], Y0.ap()[rows, :], writes=[y])
            ctx.dma("sp", vv[:], Vs.ap()[rows, :], writes=[vv])
            ctx.dma("sp", gg[:], Gs.ap()[rows, :], writes=[gg])
            ctx.dma("sp", bon[:], BON.ap()[rows, :], writes=[bon])
            if NSEG > 1:
                ctx.dma("sp", ytr[:], YTR.ap()[n], writes=[ytr])
                for q4 in range(4):
                    for jj in range(4):
                        hb = q4 * 4 + jj
                        for hi, po in enumerate((0, 64)):
                            pcc = pc[2 * (q4 % 2) + hi]
                            ctx.op("pe", lambda: nc.tensor.matmul(pcc[:, jj * 64:(jj + 1) * 64],
                                                                  lhsT=ytr[po:po + 64, hb, :], rhs=S0b[po:po + 64, hb, :],
                                                                  start=True, stop=True), reads=[ytr, S0b], writes=[pcc])
                    for hi in range(2):
                        pcc = pc[2 * (q4 % 2) + hi]
                        yv = y[:, q4 * 512:(q4 + 1) * 512].rearrange("p (j k) -> p j k", k=P)[:, :, hi * 64:(hi + 1) * 64]
                        ctx.op("dve", lambda: nc.vector.tensor_tensor(yv, yv, pcc[:, 0:256].rearrange("p (j k) -> p j k", k=64), ALU.add),
                               reads=[pcc, y], writes=[y])
            ctx.op("dve", lambda: nc.vector.tensor_reduce(s1[:], v3(y), AX.X, ALU.add), reads=[y], writes=[s1])
            ctx.op("dve", lambda: nc.vector.tensor_scalar_mul(s1[:], s1[:], 1.0 / RW_HD), reads=[s1], writes=[s1])
            ctx.op("dve", lambda: nc.vector.tensor_tensor(v3(y), v3(y), bc3(s1), ALU.subtract), reads=[y, s1], writes=[y])
            ctx.op("pool", lambda: nc.gpsimd.tensor_tensor(sq[:], y[:], y[:], ALU.mult), reads=[y], writes=[sq])
            ctx.op("dve", lambda: nc.vector.tensor_reduce(s2[:], v3(sq), AX.X, ALU.add), reads=[sq], writes=[s2])
            ctx.op("dve", lambda: nc.vector.tensor_scalar(s2[:], s2[:], 1.0 / RW_HD, RW_EPS, ALU.mult, ALU.add), reads=[s2], writes=[s2])
            ctx.op("act", lambda: nc.scalar.activation(s2[:], s2[:], AF.Sqrt), reads=[s2], writes=[s2])
            ctx.op("dve", lambda: nc.vector.reciprocal(s2[:], s2[:]), reads=[s2], writes=[s2])
            ctx.op("dve", lambda: nc.vector.tensor_tensor(v3(y), v3(y), bc3(s2), ALU.mult), reads=[y, s2], writes=[y])
            ctx.op("pool", lambda: nc.gpsimd.tensor_tensor(y[:], y[:], gnb[:], ALU.mult), reads=[y, gnb], writes=[y])
            ctx.op("pool", lambda: nc.gpsimd.tensor_tensor(y[:], y[:], gbb[:], ALU.add), reads=[y, gbb], writes=[y])
            ctx.op("dve", lambda: nc.vector.tensor_tensor(v3(vv), v3(vv), bc3(bon), ALU.mult), reads=[vv, bon], writes=[vv])
            ctx.op("pool", lambda: nc.gpsimd.tensor_tensor(y[:], y[:], vv[:], ALU.add), reads=[y, vv], writes=[y])
            ob = ogb[n % 2]
            ctx.op("dve", lambda: nc.vector.tensor_tensor(ob[:], y[:], gg[:], ALU.mult), reads=[y, gg], writes=[ob])
            g_, gi_ = divmod(n, G)
            self.xt_emit_tile(ob, ob, stg[g_ % 2], stg[g_ % 2], gi_ * P, pst[n % 2])
            if gi_ == G - 1:
                ctx.dma("sp", OGTv[:, :, g_ * G * P:(g_ + 1) * G * P], stg[g_ % 2][:], reads=[stg[g_ % 2]], writes=[("OGT", g_)])
        ctx.barrier()

    if cfg.stop == "rw4":
        return
    with contextlib.ExitStack() as st:
        aT = self.load_AT(st, "oTa", OGT, KC, 0, T)
        res = GemmRes(self, st, KC, 512, 3)
        epi = self.epi_resid(st, X, Z1)
        self.gemm_tok(res, aT, aT, KC, w_out, 0, D, range(NT), epi)
        ctx.barrier()


Prog.rwkv_layer = _rwkv_layer
def _const_inputs(cfg, core):
    T, NSEG = cfg.T, cfg.NSEG
    seg = core % NSEG
    f32 = np.float32
    own = np.zeros((P, NSEG), f32)
    own[:, seg] = 1
    hs = np.zeros((P, NSEG), f32)
    if seg > 0:
        hs[:, seg - 1] = 1
    c = {"ident": np.eye(P, dtype=f32).astype(ml_dtypes.bfloat16), "identf": np.eye(P, dtype=f32),
         "own": own, "halo_sel": hs}
    inv = (1.0 / (10000.0 ** (np.arange(0, RET_DK, 2, dtype=f32) / f32(RET_DK)))).astype(f32)
    pos = (seg * T + np.arange(T)).astype(f32)
    ang = (pos[None, :] * inv[:, None]).astype(f32)
    c["rope_cos"] = np.cos(ang).astype(f32)
    c["rope_sin"] = np.sin(ang).astype(f32)
    gam = np.array(RET_GAMMA, np.float64)
    idx = np.arange(P, dtype=np.float64)
    c["ret_kdec"] = (gam[None, :] ** (P - 1 - idx[:, None])).astype(f32)
    diff = idx[None, :] - idx[:, None]
    m = np.where(diff[:, None, :] >= 0, gam[None, :, None] ** np.maximum(diff[:, None, :], 0), 0.0)
    c["ret_maskT"] = m.astype(f32)
    c["ret_qdec"] = np.broadcast_to((gam[:, None] ** (idx[None, :] + 1.0))[None], (P, RET_H, P)).astype(f32).copy()
    coef = np.zeros((P, NSEG, RET_H), f32)
    for s in range(seg):
        coef[:, s, :] = (gam ** (T * (seg - s - 1)))[None, :]
    c["ret_coef"] = coef
    _swa_consts(cfg, core, c)
    _rwkv_consts(cfg, core, c)
    return c


def make_in_maps(cfg, prog, inputs):
    T, NSEG = cfg.T, cfg.NSEG
    maps = []
    shared = {}
    per_layer = ["ret_w_in", "ret_w_out", "ret_gn_gain", "swa_w_qkv", "swa_sinks", "swa_w_out",
                 "rwkv_mix", "rwkv_w_rkv", "rwkv_w0", "rwkv_w1", "rwkv_w2", "rwkv_a0", "rwkv_a1", "rwkv_a2",
                 "rwkv_g1", "rwkv_g2", "rwkv_k_k", "rwkv_k_a", "rwkv_r_k", "rwkv_gn_gain", "rwkv_gn_bias",
                 "rwkv_w_out", "ffn_w_up", "ffn_conv_w", "ffn_conv_b", "ffn_w_down", "ple_w_proj", "ple_w_gate"]
    for name in prog.inputs:
        if name in prog.tiled:
            fn, K_, N_, tw_, kp_ = prog.tiled[name]
            w = np.asarray(fn(inputs))
            assert w.shape == (K_, N_), (name, w.shape)
            shared[name] = np.ascontiguousarray(
                w.reshape(K_ // kp_, kp_, N_ // tw_, tw_).transpose(2, 1, 0, 3).reshape(N_ // tw_, kp_, (K_ // kp_) * tw_))
            continue
        if name in inputs and name not in ("x",):
            shared[name] = np.ascontiguousarray(inputs[name])
            continue
        for base in per_layer:
            if name.startswith(base + "_") and name[len(base) + 1:].isdigit():
                shared[name] = np.ascontiguousarray(inputs[base][int(name[len(base) + 1:])])
    for core in range(cfg.ncores):
        b, seg = divmod(core, NSEG)
        consts = _const_inputs(cfg, core)
        m = {}
        for name in prog.inputs:
            if name in shared:
                m[name] = shared[name]
            elif name == "x":
                m[name] = np.ascontiguousarray(inputs["x"][b, seg * T:(seg + 1) * T, :])
            elif name.startswith("pT_"):
                l = int(name[3:])
                m[name] = np.ascontiguousarray(inputs["p"][l, b, seg * T:(seg + 1) * T, :].T)
            elif name in consts:
                m[name] = consts[name]
            else:
                raise KeyError(name)
        maps.append(m)
    return maps


def run_cfg(cfg, inputs):
    prog = Prog(cfg)
    prog.build()
    maps = make_in_maps(cfg, prog, inputs)
    res = run_bass_kernel_spmd(prog.nc, maps, core_ids=list(range(cfg.ncores)))
    return prog, res.results


def kernel(**inputs):
    cfg = Cfg()
    prog, results = run_cfg(cfg, inputs)
    out = np.empty((cfg.NB, cfg.NSEG * cfg.T, D), np.float32)
    for core in range(cfg.ncores):
        b, seg = divmod(core, cfg.NSEG)
        out[b, seg * cfg.T:(seg + 1) * cfg.T, :] = results[core]["out"]
    return out
```

```python
import contextlib
import math
import numpy as np
import ml_dtypes
import concourse.bass as bass
import concourse.mybir as mybir
from concourse.bass_utils import run_bass_kernel_spmd

F32 = mybir.dt.float32
BF16 = mybir.dt.bfloat16
AF = mybir.ActivationFunctionType
ALU = mybir.AluOpType
AX = mybir.AxisListType

P = 128
D = 2048
KC = D // P
DEPTH = 4
DFF = 5504
NFB = DFF // P
PLE = 256
ALPHA = (2.0 * DEPTH) ** 0.25
LN_EPS = 1e-5
RET_H, RET_DK, RET_DV = 8, 256, 512
RET_EPS = 1e-5
RET_GAMMA = [1.0 - 2.0 ** (-5.0 - h) for h in range(RET_H)]


class Ctx:
    NDMA = {"sp": 8, "act": 4, "pool": 8}

    def __init__(self, nc, stack):
        self.nc = nc
        self.stack = stack
        self.eng = {"pe": nc.tensor, "act": nc.scalar, "dve": nc.vector,
                    "pool": nc.gpsimd, "sp": nc.sync}
        self.sems = {}
        self.val = {}
        for e in ("pe", "act", "dve", "pool"):
            self.sems[e] = stack.enter_context(nc.semaphore("c_" + e))
            self.val[e] = 0
        self.dq = {}
        self.dq_next = {}
        for q, n in self.NDMA.items():
            keys = []
            for i in range(n):
                k = "d_%s%d" % (q, i)
                self.sems[k] = stack.enter_context(nc.semaphore(k))
                self.val[k] = 0
                keys.append(k)
            self.dq[q] = keys
            self.dq_next[q] = 0
        self.sems["cc"] = stack.enter_context(nc.semaphore("cc"))
        self.val["cc"] = 0
        self.known = {e: {} for e in self.eng}
        self.lastw = {}
        self.readers = {}
        self.uid = 0
        self.n_ins = 0

    def sb(self, stack, name, shape, dtype=F32):
        self.uid += 1
        return stack.enter_context(self.nc.sbuf_tensor("%s_%d" % (name, self.uid), list(shape), dtype))

    def ps(self, stack, name, shape, dtype=F32):
        self.uid += 1
        return stack.enter_context(self.nc.psum_tensor("%s_%d" % (name, self.uid), list(shape), dtype))

    def _key(self, b):
        return b if isinstance(b, (str, tuple)) else id(b)

    def _deps(self, reads, writes, merge=False):
        deps = {}
        for b in list(reads) + ([] if merge else list(writes)):
            for k, v in self.lastw.get(self._key(b), {}).items():
                if deps.get(k, 0) < v:
                    deps[k] = v
        for b in writes:
            for k, v in self.readers.get(self._key(b), {}).items():
                if deps.get(k, 0) < v:
                    deps[k] = v
        return deps

    def _wait(self, e, deps):
        kn = self.known[e]
        for k, v in deps.items():
            if e == "pe" and k == "pe":
                continue
            if kn.get(k, 0) >= v:
                continue
            self.eng[e].wait_ge(self.sems[k], v)
            kn[k] = v

    def _commit(self, ev, reads, writes, merge=False):
        k, v = ev
        for b in reads:
            self.readers.setdefault(self._key(b), {})[k] = v
        for b in writes:
            if merge:
                self.lastw.setdefault(self._key(b), {})[k] = v
            else:
                self.lastw[self._key(b)] = {k: v}
                self.readers[self._key(b)] = {}

    def op(self, e, fn, reads=(), writes=()):
        self._wait(e, self._deps(reads, writes))
        ins = fn()
        self.val[e] += 1
        ins.then_inc(self.sems[e], 1)
        self._commit((e, self.val[e]), reads, writes)
        self.n_ins += 1
        return ins

    def dma(self, q, out, in_, reads=(), writes=(), merge=False, **kw):
        deps = self._deps(reads, writes, merge)
        i = self.dq_next[q]
        self.dq_next[q] = (i + 1) % len(self.dq[q])
        k = self.dq[q][i]
        if self.val[k] > 0:
            deps[k] = max(deps.get(k, 0), self.val[k])
        self._wait(q, deps)
        ins = self.eng[q].dma_start(out=out, in_=in_, **kw)
        self.val[k] += 16
        ins.then_inc(self.sems[k], 16)
        self._commit((k, self.val[k]), reads, writes, merge)
        self.n_ins += 1
        return ins

    def allreduce(self, groups, in_ap, out_ap, reads=(), writes=()):
        deps = self._deps(reads, writes)
        self._wait("pool", deps)
        ins = self.nc.gpsimd.collective_compute("AllReduce", ALU.add, replica_groups=groups,
                                                ins=[in_ap.opt()], outs=[out_ap.opt()])
        self.val["cc"] += 1
        ins.then_inc(self.sems["cc"], 1)
        self._commit(("cc", self.val["cc"]), reads, writes)

    def barrier(self, engines=("pe", "act", "dve", "pool", "sp")):
        deps = {k: v for k, v in self.val.items() if v > 0}
        for e in engines:
            self._wait(e, dict(deps))
        if len(engines) == 5:
            self.lastw = {}
            self.readers = {}

    def finish(self):
        self._wait("sp", {k: v for k, v in self.val.items() if v > 0})


class Cfg:
    def __init__(self, NB=2, NSEG=4, T=2048, layers=(0, 1, 2, 3), debug=(), stop=None):
        self.stop = stop
        self.NB, self.NSEG, self.T = NB, NSEG, T
        self.layers = tuple(layers)
        self.NT = T // P
        self.ncores = NB * NSEG
        self.groups = [[b * NSEG + s for s in range(NSEG)] for b in range(NB)]
        self.debug = tuple(debug)


class Prog:
    def __init__(self, cfg):
        self.cfg = cfg
        self.nc = bass.Bass("TRN2", target_bir_lowering=False)
        self.inputs = {}
        self.scratch = {}
        self.tiled = {}

    def inp(self, name, shape, dtype=F32):
        if name not in self.inputs:
            self.inputs[name] = self.nc.dram_tensor(name, list(shape), dtype, kind="ExternalInput")
        return self.inputs[name]

    def scr(self, name, shape, dtype=F32):
        if name not in self.scratch:
            kind = "ExternalOutput" if name in self.cfg.debug else "Internal"
            self.scratch[name] = self.nc.dram_tensor(name, list(shape), dtype, kind=kind)
        return self.scratch[name]

    def build(self):
        cfg = self.cfg
        nc = self.nc
        T = cfg.T
        with contextlib.ExitStack() as top:
            ctx = self.ctx = Ctx(nc, top)
            self.top = top
            self.ident = ctx.sb(top, "ident", [P, P], BF16)
            ctx.dma("sp", self.ident[:], self.inp("ident", [P, P], BF16).ap(), writes=[self.ident])
            self.identf = ctx.sb(top, "identf", [P, P], F32)
            ctx.dma("sp", self.identf[:], self.inp("identf", [P, P], F32).ap(), writes=[self.identf])
            self.own = ctx.sb(top, "own", [P, cfg.NSEG], F32)
            ctx.dma("sp", self.own[:], self.inp("own", [P, cfg.NSEG]).ap(), writes=[self.own])
            self.halo_sel = ctx.sb(top, "halo_sel", [P, cfg.NSEG], F32)
            ctx.dma("sp", self.halo_sel[:], self.inp("halo_sel", [P, cfg.NSEG]).ap(), writes=[self.halo_sel])

            x_in = self.inp("x", [T, D])
            out = self.nc.dram_tensor("out", [T, D], F32, kind="ExternalOutput")
            XT = self.scr("XT", [D, T], BF16)
            cur = x_in
            self.xt_stage(cur, XT)
            for li, layer in enumerate(cfg.layers):
                kind = layer % 3
                Z1 = self.scr("Z1", [T, D])
                if kind == 0:
                    self.retention_layer(layer, cur, XT, Z1)
                elif kind == 1:
                    self.swa_layer(layer, cur, XT, Z1)
                else:
                    self.rwkv_layer(layer, cur, XT, Z1)
                if cfg.stop is not None:
                    break
                X1 = self.scr("X1", [T, D])
                XT1 = self.scr("XT1", [D, T], BF16)
                self.ln_stage(Z1, layer, 0, X1, XT1)
                Z2 = self.scr("Z2", [T, D])
                self.ffn_layer(layer, X1, XT1, Z2)
                X2 = self.scr("X2", [T, D])
                self.ln_stage(Z2, layer, 1, X2, XT)
                last = li == len(cfg.layers) - 1
                X3 = out if last else self.scr("X3_%d" % (li % 2), [T, D])
                self.ple_layer(layer, X2, XT, X3)
                if not last:
                    self.xt_stage(X3, XT)
                cur = X3
            ctx.barrier()
            ctx.finish()
        return nc

    def load_w(self, dst, src_ap, key):
        self.ctx.dma("pool", dst, src_ap, writes=[key])

    def bcast_rows(self, stack, name, src_ap_1d, n):
        t = self.ctx.sb(stack, name, [P, n], F32)
        self.ctx.dma("sp", t[:], src_ap_1d.partition_broadcast(P), writes=[t])
        return t

    def load_cols(self, stack, name, src_ap_2d, R, ps):
        ctx, nc = self.ctx, self.nc
        out = ctx.sb(stack, name, [P, R], F32)
        with contextlib.ExitStack() as st:
            r0 = 0
            while r0 < R:
                r = min(P, R - r0)
                tmp = ctx.sb(st, name + "_r", [P, P], F32)
                ctx.dma("sp", tmp[0:r, :], src_ap_2d[r0:r0 + r, :], writes=[tmp])
                ctx.op("pe", lambda: nc.tensor.matmul(ps[:, 0:r], lhsT=tmp[0:r, :], rhs=self.identf[0:r, 0:r],
                                                      start=True, stop=True), reads=[tmp, self.identf], writes=[ps])
                ctx.op("dve", lambda: nc.vector.tensor_copy(out[:, r0:r0 + r], ps[:, 0:r]), reads=[ps], writes=[out])
                r0 += r
            ctx.barrier(("pe", "dve", "sp"))
        return out

    def transpose_to(self, src_bf16_ap, pst_ap, reads, writes):
        nc = self.nc
        self.ctx.op("pe", lambda: nc.tensor.transpose(pst_ap, src_bf16_ap, self.ident[:]),
                    reads=list(reads) + [self.ident], writes=writes)

    def xt_emit_tile(self, xb, xb_key, stg, stg_key, col0, pst):
        ctx, nc = self.ctx, self.nc
        for half in range(2):
            pt = pst[half]
            for j in range(8):
                kc = half * 8 + j
                self.transpose_to(xb[:, kc * P:(kc + 1) * P], pt[:, j * P:(j + 1) * P], [xb_key], [pt])
            eng = "dve" if half == 0 else "act"
            src = pt[:].rearrange("p (j c) -> p j c", j=8)
            dst = stg[:, half * 8:(half + 1) * 8, col0:col0 + P]
            if eng == "dve":
                ctx.op("dve", lambda: nc.vector.tensor_copy(dst, src), reads=[pt], writes=[stg_key])
            else:
                ctx.op("act", lambda: nc.scalar.copy(dst, src), reads=[pt], writes=[stg_key])

    def xt_stage(self, X, XT):
        ctx, nc, cfg = self.ctx, self.nc, self.cfg
        T = cfg.T
        G = 4 if cfg.NT % 4 == 0 else 2
        with contextlib.ExitStack() as st:
            xf = [ctx.sb(st, "xf", [P, D], F32) for _ in range(2)]
            xb = [ctx.sb(st, "xb", [P, D], BF16) for _ in range(2)]
            stg = [ctx.sb(st, "stg", [P, KC, G * P], BF16) for _ in range(2)]
            pst = [[ctx.ps(st, "pst", [P, 8 * P], BF16) for _ in range(2)] for _ in range(2)]
            XTv = XT.ap().rearrange("(kc p) t -> p kc t", p=P)
            for tt in range(cfg.NT):
                b = tt % 2
                g, gi = divmod(tt, G)
                ctx.dma("sp", xf[b][:], X.ap()[tt * P:(tt + 1) * P, :], writes=[xf[b]])
                ctx.op("pool", lambda: nc.gpsimd.tensor_copy(xb[b][:], xf[b][:]), reads=[xf[b]], writes=[xb[b]])
                self.xt_emit_tile(xb[b], xb[b], stg[g % 2], stg[g % 2], gi * P, pst[b])
                if gi == G - 1:
                    ctx.dma("sp", XTv[:, :, g * G * P:(g + 1) * G * P], stg[g % 2][:], reads=[stg[g % 2]],
                            writes=[("XT", g)])
            ctx.barrier()

    def ln_stage(self, Z, layer, which, X, XT):
        ctx, nc, cfg = self.ctx, self.nc, self.cfg
        G = 4 if cfg.NT % 4 == 0 else 2
        with contextlib.ExitStack() as st:
            gain = self.bcast_rows(st, "lng", self.inp("ln_gain", [DEPTH, 2, D]).ap()[layer, which, :], D)
            bias = self.bcast_rows(st, "lnb", self.inp("ln_bias", [DEPTH, 2, D]).ap()[layer, which, :], D)
            zf = [ctx.sb(st, "zf", [P, D], F32) for _ in range(2)]
            xn = [ctx.sb(st, "xn", [P, D], F32) for _ in range(2)]
            xo = [ctx.sb(st, "xo", [P, D], F32) for _ in range(2)]
            xb = [ctx.sb(st, "xb", [P, D], BF16) for _ in range(2)]
            stats = [ctx.sb(st, "stats", [P, 4, 6], F32) for _ in range(2)]
            mv = [ctx.sb(st, "mv", [P, 4], F32) for _ in range(2)]
            stg = [ctx.sb(st, "stg", [P, KC, G * P], BF16) for _ in range(2)]
            pst = [[ctx.ps(st, "pst", [P, 8 * P], BF16) for _ in range(2)] for _ in range(2)]
            XTv = XT.ap().rearrange("(kc p) t -> p kc t", p=P)
            for tt in range(cfg.NT):
                b = tt % 2
                g, gi = divmod(tt, G)
                ctx.dma("sp", zf[b][:], Z.ap()[tt * P:(tt + 1) * P, :], writes=[zf[b]])
                self.layernorm_tile(zf[b], xn[b], stats[b], mv[b], D, LN_EPS)
                ctx.op("dve", lambda: nc.vector.tensor_tensor(xn[b][:], xn[b][:], gain[:], ALU.mult),
                       reads=[xn[b], gain], writes=[xn[b]])
                ctx.op("pool", lambda: nc.gpsimd.tensor_tensor(xo[b][:], xn[b][:], bias[:], ALU.add),
                       reads=[xn[b], bias], writes=[xo[b]])
                ctx.dma("sp", X.ap()[tt * P:(tt + 1) * P, :], xo[b][:], reads=[xo[b]], writes=[("X", tt)])
                ctx.op("act", lambda: nc.scalar.copy(xb[b][:], xo[b][:]), reads=[xo[b]], writes=[xb[b]])
                self.xt_emit_tile(xb[b], xb[b], stg[g % 2], stg[g % 2], gi * P, pst[b])
                if gi == G - 1:
                    ctx.dma("sp", XTv[:, :, g * G * P:(g + 1) * G * P], stg[g % 2][:], reads=[stg[g % 2]],
                            writes=[("XT", g)])
            ctx.barrier()

    def layernorm_tile(self, src, dst, stats, mv, n, eps, src_key=None, dst_key=None):
        ctx, nc = self.ctx, self.nc
        src_key = src if src_key is None else src_key
        dst_key = dst if dst_key is None else dst_key
        nch = max(1, n // 512)
        w = n // nch
        for c in range(nch):
            ctx.op("dve", lambda: nc.vector.bn_stats(stats[:, c, :], src[:, c * w:(c + 1) * w]),
                   reads=[src_key], writes=[stats])
        ctx.op("dve", lambda: nc.vector.bn_aggr(mv[:, 0:2], stats[:, 0:nch, :]), reads=[stats], writes=[mv])
        ctx.op("dve", lambda: nc.vector.tensor_scalar_add(mv[:, 2:3], mv[:, 1:2], eps), reads=[mv], writes=[mv])
        ctx.op("act", lambda: nc.scalar.activation(mv[:, 2:3], mv[:, 2:3], AF.Sqrt), reads=[mv], writes=[mv])
        ctx.op("dve", lambda: nc.vector.reciprocal(mv[:, 2:3], mv[:, 2:3]), reads=[mv], writes=[mv])
        ctx.op("dve", lambda: nc.vector.tensor_scalar(mv[:, 3:4], mv[:, 0:1], mv[:, 2:3], -1.0, ALU.mult, ALU.mult),
               reads=[mv], writes=[mv])
        ctx.op("act", lambda: nc.scalar.activation(dst[:, 0:n], src[:, 0:n], AF.Identity, bias=mv[:, 3:4],
                                                   scale=mv[:, 2:3]), reads=[src_key, mv], writes=[dst_key])


class TiledW:
    def __init__(self, h, K, N, tw, kp):
        self.h, self.K, self.N, self.tw, self.kp = h, K, N, tw, kp
        self.kcn = K // kp


def _tw(self, name, src_fn, K, N, tw, kp=P):
    if name not in self.inputs:
        self.inp(name, [N // tw, kp, (K // kp) * tw])
        self.tiled[name] = (src_fn, K, N, tw, kp)
    return TiledW(self.inputs[name], K, N, tw, kp)


def _load_wt(self, wb, W, c0, nb, key):
    assert c0 % W.tw == 0 and nb % W.tw == 0, (c0, nb, W.tw)
    i0, nt = c0 // W.tw, nb // W.tw
    for i in range(nt):
        dst = wb[0:W.kp, 0:W.kcn, i * W.tw:(i + 1) * W.tw]
        src = W.h.ap()[i0 + i].rearrange("p (kc j) -> p kc j", j=W.tw)
        self.ctx.dma("pool", dst, src, writes=[key], merge=(i > 0))


Prog.tw = _tw
Prog.load_wt = _load_wt


class GemmRes:
    def __init__(self, prog, st, kcmax, nblk, npsum, nw=2):
        ctx = prog.ctx
        self.w = [ctx.sb(st, "wbuf", [P, kcmax, nblk], BF16) for _ in range(nw)]
        self.ps = [ctx.ps(st, "gps", [P, 512], F32) for _ in range(npsum)]
        self.wi = 0
        self.pi = 0
        self.pending = {}

    def next_w(self):
        w = self.w[self.wi]
        self.wi = (self.wi + 1) % len(self.w)
        for k in [k for k, v in self.pending.items() if v is w]:
            del self.pending[k]
        return w

    def prefetch(self, prog, W, c0, nb):
        key = (id(W.h), c0, nb)
        if key in self.pending:
            return
        wb = self.next_w()
        prog.load_wt(wb, W, c0, nb, wb)
        self.pending[key] = wb

    def get_w(self, prog, W, c0, nb):
        wb = self.pending.pop((id(W.h), c0, nb), None)
        if wb is None:
            wb = self.next_w()
            prog.load_wt(wb, W, c0, nb, wb)
        return wb

    def next_ps(self):
        p = self.ps[self.pi]
        self.pi = (self.pi + 1) % len(self.ps)
        return p


def _gemm_tok(self, res, AT, at_key, kcn, W, n0, ncols, tts, epi, nblk=512, kp=P, nxt=None):
    ctx, nc = self.ctx, self.nc
    blocks = [(c0, min(nblk, n0 + ncols - c0)) for c0 in range(n0, n0 + ncols, nblk)]
    for bi, (c0, nb) in enumerate(blocks):
        wb = res.get_w(self, W, c0, nb)
        if bi + 1 < len(blocks):
            res.prefetch(self, W, *blocks[bi + 1])
        elif nxt is not None:
            res.prefetch(self, *nxt)
        for tt in tts:
            ps = res.next_ps()
            for kc in range(kcn):
                ctx.op("pe", lambda: nc.tensor.matmul(ps[:, 0:nb], lhsT=AT[0:kp, kc, tt * P:(tt + 1) * P],
                                                      rhs=wb[0:kp, kc, 0:nb], start=(kc == 0), stop=(kc == kcn - 1)),
                       reads=[at_key, wb], writes=[ps])
            epi(ps, tt, c0, nb)


def _gemm_feat(self, res, AT, at_key, kcn, W, n0, ncols, tgs, epi, nblk=512, nxt=None):
    ctx, nc = self.ctx, self.nc
    blocks = [(c0, min(nblk, n0 + ncols - c0)) for c0 in range(n0, n0 + ncols, nblk)]
    for bi, (c0, nb) in enumerate(blocks):
        wb = res.get_w(self, W, c0, nb)
        if bi + 1 < len(blocks):
            res.prefetch(self, W, *blocks[bi + 1])
        elif nxt is not None:
            res.prefetch(self, *nxt)
        for (t0, tn) in tgs:
            for fb in range((nb + P - 1) // P):
                fw = min(P, nb - fb * P)
                ps = res.next_ps()
                for kc in range(kcn):
                    ctx.op("pe", lambda: nc.tensor.matmul(ps[0:fw, 0:tn], lhsT=wb[:, kc, fb * P:fb * P + fw],
                                                          rhs=AT[:, kc, t0:t0 + tn], start=(kc == 0),
                                                          stop=(kc == kcn - 1)),
                           reads=[at_key, wb], writes=[ps])
                epi(ps, c0 + fb * P, t0, tn)


Prog.gemm_tok = _gemm_tok
Prog.gemm_feat = _gemm_feat


def _load_AT(self, st, name, XT, kcn, t0, tn, pad=0):
    ctx = self.ctx
    t = ctx.sb(st, name, [P, kcn, pad + tn], BF16)
    v = XT.ap().rearrange("(kc p) t -> p kc t", p=P)
    step = max(1, kcn // 4)
    for k0 in range(0, kcn, step):
        k1 = min(kcn, k0 + step)
        ctx.dma("sp", t[:, k0:k1, pad:pad + tn], v[:, k0:k1, t0:t0 + tn], writes=[t], merge=(k0 > 0))
    return t


Prog.load_AT = _load_AT


def _epi_resid(self, st, Xold, Zout, tok_base=0):
    ctx, nc = self.ctx, self.nc
    xo = [ctx.sb(st, "rx", [P, 512], F32) for _ in range(3)]
    zt = [ctx.sb(st, "rz", [P, 512], F32) for _ in range(3)]
    cnt = [0]

    def epi(ps, tt, c0, nb):
        i = cnt[0] % 3
        cnt[0] += 1
        r0 = tok_base + tt * P
        ctx.dma("sp", xo[i][:, 0:nb], Xold.ap()[r0:r0 + P, c0:c0 + nb], writes=[xo[i]])
        ctx.op("dve", lambda: nc.vector.scalar_tensor_tensor(zt[i][:, 0:nb], xo[i][:, 0:nb], ALPHA, ps[:, 0:nb],
                                                             ALU.mult, ALU.add),
               reads=[xo[i], ps], writes=[zt[i]])
        ctx.dma("sp", Zout.ap()[r0:r0 + P, c0:c0 + nb], zt[i][:, 0:nb], reads=[zt[i]], writes=[("Z", r0, c0)])
    return epi


Prog.epi_resid = _epi_resid


def _retention_layer(self, layer, X, XT, Z1):
    ctx, nc, cfg = self.ctx, self.nc, self.cfg
    T, NT, NSEG = cfg.T, cfg.NT, cfg.NSEG
    j = layer // 3
    w_qk = self.tw("ret_w_qk_t%d" % j, lambda inp, j=j: inp["ret_w_in"][j][:, 0:4096], D, 4096, 256)
    w_vg = self.tw("ret_w_vg_t%d" % j, lambda inp, j=j: inp["ret_w_in"][j][:, 4096:12288], D, 8192, 512)
    w_out = self.tw("ret_w_out_t%d" % j, lambda inp, j=j: inp["ret_w_out"][j], 4096, D, 256)
    gn_ap = self.inp("ret_gn_gain_%d" % j, [4096]).ap()
    KTs = self.scr("ret_KT", [D, T], BF16)
    Vs = self.scr("ret_V", [T, 4096], BF16)
    OGT = self.scr("ret_OGT", [4096, T], BF16)
    NSL = max(1, NSEG - 1)
    CCI = self.scr("ret_cci", [RET_H, NSL, 2, P, 512])
    CCO = self.scr("ret_cco", [RET_H, NSL, 2, P, 512])
    TH = min(1024, T)
    NTH = TH // P
    TGW = min(512, TH)
    tgs = [(t0, TGW) for t0 in range(0, TH, TGW)]
    QOFF, KOFF, VOFF, GOFF = 0, 2048, 0, 4096
    rope_cos = self.inp("rope_cos", [P, T]).ap()
    rope_sin = self.inp("rope_sin", [P, T]).ap()

    def rotary_epi(cosT, sinT, dstT, tmp):
        state = {}

        def epi(ps, c, t0, tn):
            half = (c // P) % 2
            if half == 0:
                state["A"] = ps
                return
            psA, psB = state["A"], ps
            t1, t2, t3, t4 = tmp
            cs, sn = cosT[:, t0:t0 + tn], sinT[:, t0:t0 + tn]
            ctx.op("dve", lambda: nc.vector.tensor_tensor(t1[:, 0:tn], psA[:, 0:tn], cs, ALU.mult),
                   reads=[psA, cosT], writes=[t1])
            ctx.op("dve", lambda: nc.vector.tensor_tensor(t2[:, 0:tn], psB[:, 0:tn], sn, ALU.mult),
                   reads=[psB, sinT], writes=[t2])
            ctx.op("dve", lambda: nc.vector.tensor_tensor(t3[:, 0:tn], psA[:, 0:tn], sn, ALU.mult),
                   reads=[psA, sinT], writes=[t3])
            ctx.op("dve", lambda: nc.vector.tensor_tensor(t4[:, 0:tn], psB[:, 0:tn], cs, ALU.mult),
                   reads=[psB, cosT], writes=[t4])
            ctx.op("pool", lambda: nc.gpsimd.tensor_tensor(dstT[:, 0, t0:t0 + tn], t1[:, 0:tn], t2[:, 0:tn],
                                                           ALU.subtract), reads=[t1, t2], writes=[dstT])
            ctx.op("pool", lambda: nc.gpsimd.tensor_tensor(dstT[:, 1, t0:t0 + tn], t3[:, 0:tn], t4[:, 0:tn],
                                                           ALU.add), reads=[t3, t4], writes=[dstT])
        return epi

    KTv = KTs.ap().rearrange("(h two p) t -> h p two t", two=2, p=P)
    Vv = Vs.ap().rearrange("(tt p) e -> p tt e", p=P)
    OGTv = OGT.ap().rearrange("(h fc p) t -> h p fc t", fc=4, p=P)

    with contextlib.ExitStack() as st:
        kdec = ctx.sb(st, "kdec", [P, RET_H], F32)
        ctx.dma("sp", kdec[:], self.inp("ret_kdec", [P, RET_H]).ap(), writes=[kdec])
        res = GemmRes(self, st, KC, 512, 3)
        tmp = [ctx.sb(st, "rt", [P, 512], F32) for _ in range(4)]
        kT = [ctx.sb(st, "kT", [P, 2, TH], BF16) for _ in range(2)]
        vh = [ctx.sb(st, "vh", [P, NTH, 512], BF16) for _ in range(2)]
        kdA = [ctx.sb(st, "kdA", [P, 2 * P], BF16) for _ in range(2)]
        pst = [ctx.ps(st, "pst", [P, 2 * P], BF16) for _ in range(2)]
        Lps = [ctx.ps(st, "Lps", [P, 512], F32) for _ in range(2)]
        Lacc = ctx.sb(st, "Lacc", [P, RET_H, 2, 512], F32)
        Lm = [ctx.sb(st, "Lm", [P, 2, 512], F32) for _ in range(2)]
        cosk = ctx.sb(st, "cosk", [P, TH], F32)
        sink = ctx.sb(st, "sink", [P, TH], F32)
        xT = ctx.sb(st, "xT", [P, KC, TH], BF16)
        XTv = XT.ap().rearrange("(kc p) t -> p kc t", p=P)
        for th in range(T // TH):
            t0h = th * TH
            for k0 in range(0, KC, 4):
                ctx.dma("sp", xT[:, k0:k0 + 4, :], XTv[:, k0:k0 + 4, t0h:t0h + TH], writes=[xT], merge=(k0 > 0))
            ctx.dma("sp", cosk[:], rope_cos[:, t0h:t0h + TH], writes=[cosk])
            ctx.dma("sp", sink[:], rope_sin[:, t0h:t0h + TH], writes=[sink])
            ctx.op("pool", lambda: nc.gpsimd.tensor_scalar_mul(cosk[:], cosk[:], RET_DK ** -0.5), reads=[cosk], writes=[cosk])
            ctx.op("pool", lambda: nc.gpsimd.tensor_scalar_mul(sink[:], sink[:], RET_DK ** -0.5), reads=[sink], writes=[sink])
            for h in range(RET_H):
                kTh, vhh = kT[h % 2], vh[h % 2]
                self.gemm_feat(res, xT, xT, KC, w_qk, KOFF + h * 256, 256, tgs, rotary_epi(cosk, sink, kTh, tmp), nblk=256,
                               nxt=(w_vg, VOFF + h * 512, 512))
                ctx.dma("sp", KTv[h][:, :, t0h:t0h + TH], kTh[:], reads=[kTh], writes=[("KT", h, th)])

                def v_epi(ps, tt, c0, nb):
                    ctx.op("act", lambda: nc.scalar.copy(vhh[:, tt, :], ps[:, 0:nb]), reads=[ps], writes=[vhh])
                self.gemm_tok(res, xT, xT, KC, w_vg, VOFF + h * 512, 512, range(NTH), v_epi,
                              nxt=(w_qk, KOFF + ((h + 1) % RET_H) * 256, 256))
                ctx.dma("sp", Vv[:, th * NTH:(th + 1) * NTH, h * 512:(h + 1) * 512], vhh[:],
                        reads=[vhh], writes=[("V", h, th)])
                if NSEG > 1:
                    g = RET_GAMMA[h]
                    for cl in range(NTH):
                        c = th * NTH + cl
                        pt = pst[cl % 2]
                        kd = kdA[cl % 2]
                        for half in range(2):
                            self.transpose_to(kTh[:, half, cl * P:(cl + 1) * P], pt[:, half * P:(half + 1) * P], [kTh], [pt])
                        ctx.op("dve", lambda: nc.vector.tensor_scalar(kd[:], pt[:], kdec[:, h:h + 1],
                                                                      float(g ** (P * (NT - 1 - c))), ALU.mult, ALU.mult),
                               reads=[pt, kdec], writes=[kd])
                        for half in range(2):
                            ctx.op("pe", lambda: nc.tensor.matmul(Lps[half][:], lhsT=kd[:, half * P:(half + 1) * P],
                                                                  rhs=vhh[:, cl, :], start=(cl == 0), stop=(cl == NTH - 1)),
                                   reads=[kd, vhh], writes=[Lps[half]])
                    for half in range(2):
                        if th == 0:
                            ctx.op("act", lambda: nc.scalar.copy(Lacc[:, h, half, :], Lps[half][:]),
                                   reads=[Lps[half]], writes=[(id(Lacc), h)])
                        else:
                            ctx.op("dve", lambda: nc.vector.tensor_tensor(Lacc[:, h, half, :], Lacc[:, h, half, :],
                                                                          Lps[half][:], ALU.add),
                                   reads=[Lps[half], (id(Lacc), h)], writes=[(id(Lacc), h)])
        if NSEG > 1:
            for h in range(RET_H):
                for s in range(NSL):
                    lm = Lm[(h * NSL + s) % 2]
                    ctx.op("dve", lambda: nc.vector.tensor_scalar_mul(lm[:], Lacc[:, h], self.own[:, s:s + 1]),
                           reads=[(id(Lacc), h), self.own], writes=[lm])
                    ctx.dma("sp", CCI.ap()[h, s].rearrange("two p e -> p two e"), lm[:], reads=[lm],
                            writes=[("CCI", s, h)])
        ctx.barrier()
    with contextlib.ExitStack() as st:
        res = GemmRes(self, st, KC, 512, 2)
        if NSEG > 1:
            for h in range(0, RET_H, 2):
                ctx.allreduce(cfg.groups, CCI.ap()[h:h + 2].rearrange("h s two p e -> (h s two p) e"),
                              CCO.ap()[h:h + 2].rearrange("h s two p e -> (h s two p) e"), writes=[("CCO", h), ("CCO", h + 1)])
        kdec = ctx.sb(st, "kdec", [P, RET_H], F32)
        ctx.dma("sp", kdec[:], self.inp("ret_kdec", [P, RET_H]).ap(), writes=[kdec])
        maskT = ctx.sb(st, "maskT", [P, RET_H, P], F32)
        ctx.dma("sp", maskT[:], self.inp("ret_maskT", [P, RET_H, P]).ap(), writes=[maskT])
        qdec = ctx.sb(st, "qdec", [P, RET_H, P], F32)
        ctx.dma("sp", qdec[:], self.inp("ret_qdec", [P, RET_H, P]).ap(), writes=[qdec])
        coef = ctx.sb(st, "coef", [P, NSEG, RET_H], F32)
        ctx.dma("sp", coef[:], self.inp("ret_coef", [P, NSEG, RET_H]).ap(), writes=[coef])
        gain = self.bcast_rows(st, "gng", gn_ap, 4096)
        tmp = [ctx.sb(st, "rt", [P, 512], F32) for _ in range(4)]
        cosT = ctx.sb(st, "cos", [P, TH], F32)
        sinT = ctx.sb(st, "sin", [P, TH], F32)
        xT = ctx.sb(st, "xT", [P, KC, TH], BF16)
        XTv = XT.ap().rearrange("(kc p) t -> p kc t", p=P)
        kTh = ctx.sb(st, "kT", [P, 2, TH], BF16)
        qTh = ctx.sb(st, "qT", [P, 2, TH], BF16)
        vhh = ctx.sb(st, "vh", [P, NTH, 512], BF16)
        gsh = ctx.sb(st, "gs", [P, NTH, 512], BF16)
        ogTh = ctx.sb(st, "ogT", [P, 4, TH], BF16)
        Rall = ctx.sb(st, "Rall", [P, RET_H, 2, 512], F32)
        Rb = ctx.sb(st, "Rb", [P, 2, 512], BF16)
        cin = [ctx.sb(st, "cin", [P, 2, 512], F32) for _ in range(2)]
        sT = [ctx.sb(st, "sT", [P, P], BF16) for _ in range(2)]
        qd = [ctx.sb(st, "qd", [P, 2, P], BF16) for _ in range(2)]
        kd = [ctx.sb(st, "kd", [P, 2 * P], BF16) for _ in range(2)]
        on = [ctx.sb(st, "on", [P, 512], F32) for _ in range(2)]
        og = [ctx.sb(st, "og", [P, 512], F32) for _ in range(2)]
        og2 = [ctx.sb(st, "og2", [P, 512], BF16) for _ in range(2)]
        stats = [ctx.sb(st, "stats", [P, 4, 6], F32) for _ in range(2)]
        mv = [ctx.sb(st, "mv", [P, 4], F32) for _ in range(2)]
        ps_s = ctx.ps(st, "ps_s", [P, 512], F32)
        ps_o2 = [ctx.ps(st, "ps_o", [P, 512], F32) for _ in range(2)]
        ps_tg = ctx.ps(st, "ps_tg", [P, 6 * P], BF16)
        ps_R = [ctx.ps(st, "ps_R", [P, 512], F32) for _ in range(2)]
        for th in range(T // TH):
            t0h = th * TH
            for k0 in range(0, KC, 4):
                ctx.dma("sp", xT[:, k0:k0 + 4, :], XTv[:, k0:k0 + 4, t0h:t0h + TH], writes=[xT], merge=(k0 > 0))
            ctx.dma("sp", cosT[:], rope_cos[:, t0h:t0h + TH], writes=[cosT])
            ctx.dma("sp", sinT[:], rope_sin[:, t0h:t0h + TH], writes=[sinT])
            for h in range(RET_H):
                Rk = (id(Rall), h)
                ctx.dma("sp", kTh[:], KTv[h][:, :, t0h:t0h + TH], writes=[kTh])
                ctx.dma("sp", vhh[:], Vv[:, th * NTH:(th + 1) * NTH, h * 512:(h + 1) * 512], writes=[vhh])
                self.gemm_feat(res, xT, xT, KC, w_qk, QOFF + h * 256, 256, tgs, rotary_epi(cosT, sinT, qTh, tmp), nblk=256,
                               nxt=(w_vg, GOFF + h * 512, 512))

                def g_epi(ps, tt, c0, nb):
                    ctx.op("act", lambda: nc.scalar.activation(gsh[:, tt, :], ps[:, 0:nb], AF.Silu), reads=[ps], writes=[gsh])
                self.gemm_tok(res, xT, xT, KC, w_vg, GOFF + h * 512, 512, range(NTH), g_epi,
                              nxt=(w_qk, QOFF + ((h + 1) % RET_H) * 256, 256))
                if th == 0:
                    if NSEG > 1:
                        for s_ in range(NSL):
                            ci = cin[s_ % 2]
                            ctx.dma("sp", ci[:], CCO.ap()[h, s_].rearrange("two p e -> p two e"), reads=[("CCO", h)], writes=[ci])
                            if s_ == 0:
                                ctx.op("dve", lambda: nc.vector.tensor_scalar_mul(Rall[:, h], ci[:], coef[:, s_, h:h + 1]),
                                       reads=[ci, coef], writes=[Rk])
                            else:
                                ctx.op("dve", lambda: nc.vector.scalar_tensor_tensor(Rall[:, h], ci[:], coef[:, s_, h:h + 1],
                                                                                     Rall[:, h], ALU.mult, ALU.add),
                                       reads=[ci, coef, Rk], writes=[Rk])
                    else:
                        ctx.op("dve", lambda: nc.vector.memset(Rall[:, h], 0.0), writes=[Rk])
                ctx.op("act", lambda: nc.scalar.copy(Rb[:], Rall[:, h]), reads=[Rk], writes=[Rb])
                gam = RET_GAMMA[h]
                for cl in range(NTH):
                    cb = cl % 2
                    ps_o = ps_o2[cb]
                    cs = slice(cl * P, (cl + 1) * P)
                    for half in range(2):
                        ctx.op("pe", lambda: nc.tensor.matmul(ps_s[:, 0:P], lhsT=kTh[:, half, cs], rhs=qTh[:, half, cs],
                                                              start=(half == 0), stop=(half == 1)),
                               reads=[kTh, qTh], writes=[ps_s])
                    ctx.op("dve", lambda: nc.vector.tensor_tensor(sT[cb][:], ps_s[:, 0:P], maskT[:, h, :], ALU.mult),
                           reads=[ps_s, maskT], writes=[sT[cb]])
                    for half in range(2):
                        ctx.op("pool", lambda: nc.gpsimd.tensor_tensor(qd[cb][:, half, :], qTh[:, half, cs],
                                                                       qdec[:, h, :], ALU.mult),
                               reads=[qTh, qdec], writes=[qd[cb]])
                    ctx.op("pe", lambda: nc.tensor.matmul(ps_o[:], lhsT=sT[cb][:], rhs=vhh[:, cl, :], start=True, stop=False),
                           reads=[sT[cb], vhh], writes=[ps_o])
                    for half in range(2):
                        ctx.op("pe", lambda: nc.tensor.matmul(ps_o[:], lhsT=qd[cb][:, half, :], rhs=Rb[:, half, :],
                                                              start=False, stop=(half == 1)),
                               reads=[qd[cb], Rb], writes=[ps_o])
                    for half in range(2):
                        self.transpose_to(kTh[:, half, cs], ps_tg[:, half * P:(half + 1) * P], [kTh], [ps_tg])
                    ctx.op("dve", lambda: nc.vector.tensor_scalar_mul(kd[cb][:], ps_tg[:, 0:2 * P], kdec[:, h:h + 1]),
                           reads=[ps_tg, kdec], writes=[kd[cb]])
                    for half in range(2):
                        ctx.op("pe", lambda: nc.tensor.matmul(ps_R[half][:], lhsT=kd[cb][:, half * P:(half + 1) * P],
                                                              rhs=vhh[:, cl, :], start=True, stop=True),
                               reads=[kd[cb], vhh], writes=[ps_R[half]])
                        ctx.op("dve", lambda: nc.vector.scalar_tensor_tensor(Rall[:, h, half, :], Rall[:, h, half, :],
                                                                             float(gam ** P), ps_R[half][:],
                                                                             ALU.mult, ALU.add),
                               reads=[Rk, ps_R[half]], writes=[Rk])
                    ctx.op("act", lambda: nc.scalar.copy(Rb[:], Rall[:, h]), reads=[Rk], writes=[Rb])
                    self.layernorm_tile(ps_o, on[cb], stats[cb], mv[cb], 512, RET_EPS)
                    ctx.op("dve", lambda: nc.vector.tensor_tensor(og[cb][:], on[cb][:], gain[:, h * 512:(h + 1) * 512], ALU.mult),
                           reads=[on[cb], gain], writes=[og[cb]])
                    ctx.op("pool", lambda: nc.gpsimd.tensor_tensor(og2[cb][:], og[cb][:], gsh[:, cl, :], ALU.mult),
                           reads=[og[cb], gsh], writes=[og2[cb]])
                    for fc in range(4):
                        self.transpose_to(og2[cb][:, fc * P:(fc + 1) * P], ps_tg[:, (2 + fc) * P:(3 + fc) * P], [og2[cb]], [ps_tg])
                    ctx.op("act", lambda: nc.scalar.copy(ogTh[:, :, cs], ps_tg[:, 2 * P:6 * P].rearrange("p (f c) -> p f c", f=4)),
                           reads=[ps_tg], writes=[ogTh])
                ctx.dma("sp", OGTv[h][:, :, t0h:t0h + TH], ogTh[:], reads=[ogTh], writes=[("OGT", h, th)])
        ctx.barrier()

    for t0 in range(0, T, TH):
        with contextlib.ExitStack() as st:
            aT = self.load_AT(st, "ogTa", OGT, 32, t0, TH)
            res = GemmRes(self, st, 32, 256, 3)
            epi = self.epi_resid(st, X, Z1, tok_base=t0)
            self.gemm_tok(res, aT, aT, 32, w_out, 0, D, range(TH // P), epi, nblk=256)
            ctx.barrier()


Prog.retention_layer = _retention_layer
def _halo_rows(self, st, Xsrc, nrows, name):
    ctx, nc, cfg = self.ctx, self.nc, self.cfg
    NSEG, T = cfg.NSEG, cfg.T
    halo = ctx.sb(st, name, [nrows, D], F32)
    if NSEG == 1:
        ctx.op("dve", lambda: nc.vector.memset(halo[:], 0.0), writes=[halo])
        return halo
    HCI = self.scr("halo_ci_%d" % nrows, [NSEG, nrows, D])
    HCO = self.scr("halo_co_%d" % nrows, [NSEG, nrows, D])
    hx = ctx.sb(st, name + "_x", [nrows, D], F32)
    hm = [ctx.sb(st, name + "_m", [nrows, D], F32) for _ in range(2)]
    ctx.dma("sp", hx[:], Xsrc.ap()[T - nrows:T, :], writes=[hx])
    for s in range(NSEG):
        ctx.op("dve", lambda: nc.vector.tensor_scalar_mul(hm[s % 2][:], hx[:], self.own[0:nrows, s:s + 1]),
               reads=[hx, self.own], writes=[hm[s % 2]])
        ctx.dma("sp", HCI.ap()[s], hm[s % 2][:], reads=[hm[s % 2]], writes=[("HCI", s)])
    ctx.barrier()
    ctx.allreduce(cfg.groups, HCI.ap().rearrange("s r d -> (s r) d"), HCO.ap().rearrange("s r d -> (s r) d"),
                  writes=[("HCO",)])
    ctx.barrier()
    for s in range(NSEG):
        ctx.dma("sp", hm[s % 2][:], HCO.ap()[s], writes=[hm[s % 2]])
        if s == 0:
            ctx.op("dve", lambda: nc.vector.tensor_scalar_mul(halo[:], hm[s % 2][:], self.halo_sel[0:nrows, s:s + 1]),
                   reads=[hm[s % 2], self.halo_sel], writes=[halo])
        else:
            ctx.op("dve", lambda: nc.vector.scalar_tensor_tensor(halo[:], hm[s % 2][:], self.halo_sel[0:nrows, s:s + 1],
                                                                 halo[:], ALU.mult, ALU.add),
                   reads=[hm[s % 2], self.halo_sel, halo], writes=[halo])
    return halo


Prog.halo_rows = _halo_rows


def _ffn_layer(self, layer, X1, XT1, Z2):
    ctx, nc, cfg = self.ctx, self.nc, self.cfg
    T = cfg.T
    w_up = self.tw("ffn_w_up_t%d" % layer, lambda inp, l=layer: inp["ffn_w_up"][l], D, 2 * DFF, 128)
    w_dn = self.tw("ffn_w_down_t%d" % layer, lambda inp, l=layer: inp["ffn_w_down"][l], DFF, D, 256)
    cw_ap = self.inp("ffn_conv_w_%d" % layer, [3, 2 * DFF]).ap().rearrange("t (b p) -> (t b) p", p=P)
    cb_ap = self.inp("ffn_conv_b_%d" % layer, [2 * DFF]).ap().rearrange("(b p) -> b p", p=P)
    NB2 = 2 * NFB
    TG = min(1024, T)
    W = TG + 2
    nsub = (W + 511) // 512
    bounds = [(W * i) // nsub for i in range(nsub + 1)]
    with contextlib.ExitStack() as st0:
        psc = ctx.ps(st0, "psc", [P, 512], F32)
        cw = self.load_cols(st0, "cw", cw_ap, 3 * NB2, psc)
        cb = self.load_cols(st0, "cb", cb_ap, NB2, psc)
        haloT = ctx.sb(st0, "haloT", [P, KC, 2], BF16)
        with contextlib.ExitStack() as sth:
            halo = self.halo_rows(sth, X1, 2, "halo2")
            for kc in range(KC):
                ctx.op("pe", lambda: nc.tensor.matmul(psc[:, kc * 2:kc * 2 + 2], lhsT=halo[0:2, kc * P:(kc + 1) * P],
                                                      rhs=self.identf[0:2, 0:2], start=True, stop=True),
                       reads=[halo, self.identf], writes=[psc])
            ctx.op("dve", lambda: nc.vector.tensor_copy(haloT[:], psc[:, 0:2 * KC].rearrange("p (k c) -> p k c", c=2)),
                   reads=[psc], writes=[haloT])
            ctx.barrier()
        gT = ctx.sb(st0, "gT", [P, NFB, TG], BF16)
        XTv = XT1.ap().rearrange("(kc p) t -> p kc t", p=P)
        for g in range(T // TG):
            t0 = g * TG
            with contextlib.ExitStack() as st:
                xT = ctx.sb(st, "x1T", [P, KC, W], BF16)
                for k0 in range(0, KC, 4):
                    ctx.dma("sp", xT[:, k0:k0 + 4, 2:W], XTv[:, k0:k0 + 4, t0:t0 + TG], writes=[xT], merge=(k0 > 0))
                if g == 0:
                    ctx.op("pool", lambda: nc.gpsimd.tensor_copy(xT[:, :, 0:2], haloT[:]), reads=[haloT], writes=[xT])
                else:
                    ctx.dma("sp", xT[:, :, 0:2], XTv[:, :, t0 - 2:t0], writes=[xT], merge=True)
                wu = [ctx.sb(st, "wu", [P, 2, KC, P], BF16) for _ in range(2)]
                wg = [ctx.sb(st, "wg", [P, 2, KC, P], BF16) for _ in range(2)]
                pss = [ctx.ps(st, "fps", [P, 512], F32) for _ in range(6)]
                hs = [ctx.sb(st, "hs", [P, W], F32) for _ in range(2)]
                acc = [ctx.sb(st, "acc", [P, TG], F32) for _ in range(2)]
                sg = ctx.sb(st, "sg", [P, TG], F32)
                pi = 0

                def load_pair(pr):
                    fb0 = 2 * pr
                    wi_ = pr % 2
                    for ti in range(min(2, NFB - fb0)):
                        ctx.dma("pool", wu[wi_][:, ti], w_up.h.ap()[fb0 + ti].rearrange("p (kc j) -> p kc j", j=P),
                                writes=[wu[wi_]], merge=(ti > 0))
                        ctx.dma("pool", wg[wi_][:, ti], w_up.h.ap()[NFB + fb0 + ti].rearrange("p (kc j) -> p kc j", j=P),
                                writes=[wg[wi_]], merge=(ti > 0))
                load_pair(0)
                for fb in range(NFB):
                    if fb % 2 == 0 and fb + 2 < NFB:
                        load_pair(fb // 2 + 1)
                    wi = (fb // 2) % 2
                    fo = (fb % 2) * P
                    for ui, wt in enumerate((wu[wi], wg[wi])):
                        for si in range(nsub):
                            a, b_ = bounds[si], bounds[si + 1]
                            ps = pss[pi % 6]
                            pi += 1
                            for kc in range(KC):
                                ctx.op("pe", lambda: nc.tensor.matmul(ps[:, 0:b_ - a], lhsT=wt[:, fb % 2, kc, :],
                                                                      rhs=xT[:, kc, a:b_], start=(kc == 0), stop=(kc == KC - 1)),
                                       reads=[wt, xT], writes=[ps])
                            ctx.op("act", lambda: nc.scalar.copy(hs[ui][:, a:b_], ps[:, 0:b_ - a]), reads=[ps], writes=[hs[ui]])
                        blk = fb if ui == 0 else NFB + fb
                        w0 = cw[:, 0 * NB2 + blk:0 * NB2 + blk + 1]
                        w1 = cw[:, 1 * NB2 + blk:1 * NB2 + blk + 1]
                        w2 = cw[:, 2 * NB2 + blk:2 * NB2 + blk + 1]
                        ctx.op("act", lambda: nc.scalar.activation(acc[ui][:], hs[ui][:, 2:W], AF.Identity,
                                                                   bias=cb[:, blk:blk + 1], scale=w2),
                               reads=[hs[ui], cw, cb], writes=[acc[ui]])
                        ctx.op("dve", lambda: nc.vector.scalar_tensor_tensor(acc[ui][:], hs[ui][:, 1:W - 1], w1, acc[ui][:],
                                                                             ALU.mult, ALU.add),
                               reads=[hs[ui], cw, acc[ui]], writes=[acc[ui]])
                        ctx.op("dve", lambda: nc.vector.scalar_tensor_tensor(acc[ui][:], hs[ui][:, 0:W - 2], w0, acc[ui][:],
                                                                             ALU.mult, ALU.add),
                               reads=[hs[ui], cw, acc[ui]], writes=[acc[ui]])
                    ctx.op("act", lambda: nc.scalar.activation(sg[:], acc[1][:], AF.Silu), reads=[acc[1]], writes=[sg])
                    ctx.op("pool", lambda: nc.gpsimd.tensor_tensor(gT[:, fb, :], sg[:], acc[0][:], ALU.mult),
                           reads=[sg, acc[0]], writes=[gT])
                ctx.barrier()
            with contextlib.ExitStack() as st:
                res = GemmRes(self, st, NFB, 256, 4)
                epi = self.epi_resid(st, X1, Z2, tok_base=t0)
                self.gemm_tok(res, gT, gT, NFB, w_dn, 0, D, range(TG // P), epi, nblk=256)
                ctx.barrier()


Prog.ffn_layer = _ffn_layer


def _ple_layer(self, layer, X2, XT2, X3):
    ctx, nc, cfg = self.ctx, self.nc, self.cfg
    T, NT = cfg.T, cfg.NT
    w_gate = self.tw("ple_w_gate_t%d" % layer, lambda inp, l=layer: inp["ple_w_gate"][l], D, D, 512)
    w_proj = self.tw("ple_w_proj_t%d" % layer, lambda inp, l=layer: inp["ple_w_proj"][l], PLE, D, 512)
    pT_in = self.inp("pT_%d" % layer, [PLE, T]).ap()
    with contextlib.ExitStack() as st:
        xT = self.load_AT(st, "x2T", XT2, KC, 0, T)
        pT = ctx.sb(st, "pT", [P, 2, T], BF16)
        self.load_w(pT[:], pT_in.rearrange("(kc p) t -> p kc t", p=P), pT)
        wg = [ctx.sb(st, "wg", [P, KC, 512], BF16) for _ in range(2)]
        wp = [ctx.sb(st, "wp", [P, 2, 512], BF16) for _ in range(2)]
        psg = [ctx.ps(st, "psg", [P, 512], F32) for _ in range(3)]
        psp = [ctx.ps(st, "psp", [P, 512], F32) for _ in range(3)]
        sg = [ctx.sb(st, "sg", [P, 512], F32) for _ in range(3)]
        x2 = [ctx.sb(st, "x2", [P, 512], F32) for _ in range(3)]
        x3 = [ctx.sb(st, "x3", [P, 512], F32) for _ in range(3)]
        it = 0

        def load_blk(ci_):
            self.load_wt(wg[ci_ % 2], w_gate, ci_ * 512, 512, wg[ci_ % 2])
            self.load_wt(wp[ci_ % 2], w_proj, ci_ * 512, 512, wp[ci_ % 2])
        load_blk(0)
        for ci, c0 in enumerate(range(0, D, 512)):
            wgi, wpi = wg[ci % 2], wp[ci % 2]
            if ci + 1 < D // 512:
                load_blk(ci + 1)
            for tt in range(NT):
                i = it % 3
                it += 1
                ts = slice(tt * P, (tt + 1) * P)
                for kc in range(KC):
                    ctx.op("pe", lambda: nc.tensor.matmul(psg[i][:], lhsT=xT[:, kc, ts], rhs=wgi[:, kc, :],
                                                          start=(kc == 0), stop=(kc == KC - 1)),
                           reads=[xT, wgi], writes=[psg[i]])
                for kc in range(2):
                    ctx.op("pe", lambda: nc.tensor.matmul(psp[i][:], lhsT=pT[:, kc, ts], rhs=wpi[:, kc, :],
                                                          start=(kc == 0), stop=(kc == 1)),
                           reads=[pT, wpi], writes=[psp[i]])
                ctx.dma("sp", x2[i][:], X2.ap()[tt * P:(tt + 1) * P, c0:c0 + 512], writes=[x2[i]])
                ctx.op("act", lambda: nc.scalar.activation(sg[i][:], psg[i][:], AF.Sigmoid), reads=[psg[i]], writes=[sg[i]])
                ctx.op("dve", lambda: nc.vector.tensor_tensor(sg[i][:], sg[i][:], psp[i][:], ALU.mult),
                       reads=[sg[i], psp[i]], writes=[sg[i]])
                ctx.op("pool", lambda: nc.gpsimd.tensor_tensor(x3[i][:], sg[i][:], x2[i][:], ALU.add),
                       reads=[sg[i], x2[i]], writes=[x3[i]])
                ctx.dma("sp", X3.ap()[tt * P:(tt + 1) * P, c0:c0 + 512], x3[i][:], reads=[x3[i]], writes=[("X3", tt, c0)])
        ctx.barrier()


Prog.ple_layer = _ple_layer
SWA_HQ, SWA_HKV, SWA_HD, SWA_W = 32, 4, 64, 128
NEG = -1e30


def _swa_head_order():
    order = []
    for pair in range(2):
        for g in range(8):
            order.append((2 * pair) * 8 + g)
            order.append((2 * pair + 1) * 8 + g)
    return order


def _t5_bucket(n):
    max_exact = 16
    if n < max_exact:
        return n
    large = max_exact + int(np.log(max(n, 1) / max_exact) / np.log(SWA_W / max_exact) * (32 - max_exact))
    return min(large, 31)


def _swa_consts(cfg, core, c):
    seg = core % cfg.NSEG
    E = np.zeros((32, 383), np.float32)
    for u in range(383):
        d = u - 127
        if 0 <= d < SWA_W:
            nn = np.maximum(np.array([d]), 0)
            large = 16 + (np.log(np.maximum(nn, 1) / 16) / np.log(SWA_W / 16) * 16).astype(np.int32)
            large = np.minimum(large, 31)
            b = int(np.where(nn < 16, nn, large)[0])
            E[b, u] = 1.0
    c["swa_E"] = E
    i = np.arange(P)[:, None]
    j = np.arange(2 * P)[None, :]
    d = i + P - j
    c["swa_maskc"] = np.where((d >= 0) & (d < SWA_W), 0.0, NEG).astype(np.float32)
    mf = np.zeros((P, 2 * P), np.float32)
    if seg == 0:
        mf[:, :P] = NEG
    c["swa_mask_first"] = mf


def _swa_layer(self, layer, X, XT, Z1):
    ctx, nc, cfg = self.ctx, self.nc, self.cfg
    T, NT, NSEG = cfg.T, cfg.NT, cfg.NSEG
    j = layer // 3
    def _qkv_src(inp, j=j):
        w = inp["swa_w_qkv"][j]
        qcols = np.concatenate([np.arange(h * 64, (h + 1) * 64) for h in _swa_head_order()])
        return np.concatenate([w[:, qcols], w[:, 2048:]], axis=1)

    def _out_src(inp, j=j):
        rows = np.concatenate([np.arange(h * 64, (h + 1) * 64) for h in _swa_head_order()])
        return inp["swa_w_out"][j][rows, :]
    w_q = self.tw("swa_w_q_t%d" % j, lambda inp: _qkv_src(inp)[:, 0:2048], D, 2048, 512)
    w_kv = self.tw("swa_w_kv_t%d" % j, lambda inp: _qkv_src(inp)[:, 2048:2560], D, 512, 256)
    w_out = self.tw("swa_w_out_t%d" % j, _out_src, D, D, 512)
    sinks_ap = self.inp("swa_sinks_%d" % j, [SWA_HQ]).ap()
    relb_ap = self.inp("rel_bias", [32, SWA_HQ]).ap()
    OT = self.scr("swa_OT", [D, T], BF16)
    QTs = self.scr("swa_QT", [D, T], BF16)
    order = _swa_head_order()
    TGW = min(512, T)
    tgs = [(t0, TGW) for t0 in range(0, T, TGW)]
    with contextlib.ExitStack() as st0:
        kT = ctx.sb(st0, "kT", [P, 2, P + T], BF16)
        vS = ctx.sb(st0, "vS", [P, 1 + NT, 256], BF16)
        biasS = ctx.sb(st0, "biasS", [P, SWA_HQ, 2 * P], F32)
        sinkb = self.bcast_rows(st0, "sinkb", sinks_ap, SWA_HQ)
        mfirst = ctx.sb(st0, "mfirst", [P, 2 * P], F32)
        ctx.dma("sp", mfirst[:], self.inp("swa_mask_first", [P, 2 * P]).ap(), writes=[mfirst])
        with contextlib.ExitStack() as st:
            E = ctx.sb(st, "E", [32, 383], F32)
            RB = ctx.sb(st, "RB", [32, SWA_HQ], F32)
            maskc = ctx.sb(st, "maskc", [P, 2 * P], F32)
            ctx.dma("sp", E[:], self.inp("swa_E", [32, 383]).ap(), writes=[E])
            ctx.dma("sp", RB[:], relb_ap, writes=[RB])
            ctx.dma("sp", maskc[:], self.inp("swa_maskc", [P, 2 * P]).ap(), writes=[maskc])
            psb = [ctx.ps(st, "psb", [P, 512], F32) for _ in range(2)]
            for r in range(16):
                ps = psb[r % 2]
                for jj in range(16):
                    jk = r * 16 + jj
                    ctx.op("pe", lambda: nc.tensor.matmul(ps[:, jj * 32:(jj + 1) * 32], lhsT=E[:, 255 - jk:383 - jk], rhs=RB[:],
                                                          start=True, stop=True), reads=[E, RB], writes=[ps])
                ctx.op("dve", lambda: nc.vector.tensor_tensor(
                    biasS[:, :, r * 16:(r + 1) * 16].rearrange("p h j -> p j h"),
                    ps[:].rearrange("p (j h) -> p j h", h=32),
                    maskc[:, r * 16:(r + 1) * 16].unsqueeze(2).broadcast_to([P, 16, 32]), ALU.add),
                    reads=[ps, maskc], writes=[biasS])
            ctx.barrier()
        with contextlib.ExitStack() as st:
            xT = self.load_AT(st, "xT", XT, KC, 0, T)
            res = GemmRes(self, st, KC, 512, 3)
            qst = [ctx.sb(st, "qst", [P, 4, TGW], BF16) for _ in range(2)]
            QTv = QTs.ap().rearrange("(kc p) t -> p kc t", p=P)
            qcnt = [0]

            def q_epi(ps, c, t0, tn):
                kc = c // P
                qs = qst[(qcnt[0] // 4) % 2]
                ctx.op("act", lambda: nc.scalar.activation(qs[:, kc % 4, 0:tn], ps[:, 0:tn], AF.Copy, scale=SWA_HD ** -0.5),
                       reads=[ps], writes=[qs])
                qcnt[0] += 1
                if kc % 4 == 3:
                    ctx.dma("sp", QTv[:, kc - 3:kc + 1, t0:t0 + tn], qs[:, :, 0:tn], reads=[qs], writes=[("QT", kc, t0)])
            self.gemm_feat(res, xT, xT, KC, w_q, 0, 2048, tgs, q_epi)

            def k_epi(ps, c, t0, tn):
                fb = c // P
                ctx.op("act", lambda: nc.scalar.copy(kT[:, fb, P + t0:P + t0 + tn], ps[:, 0:tn]), reads=[ps], writes=[kT])
            self.gemm_feat(res, xT, xT, KC, w_kv, 0, 256, tgs, k_epi, nblk=256)

            def v_epi(ps, tt, c0, nb):
                ctx.op("act", lambda: nc.scalar.copy(vS[:, 1 + tt, :], ps[:, 0:nb]), reads=[ps], writes=[vS])
            self.gemm_tok(res, xT, xT, KC, w_kv, 256, 256, range(NT), v_epi, nblk=256)
            ctx.barrier()
        with contextlib.ExitStack() as st:
            if NSEG == 1:
                ctx.op("dve", lambda: nc.vector.memset(kT[:, :, 0:P], 0.0), writes=[kT])
                ctx.op("dve", lambda: nc.vector.memset(vS[:, 0, :], 0.0), writes=[vS])
            else:
                HCI = self.scr("swa_ci", [NSEG, P, 512])
                HCO = self.scr("swa_co", [NSEG, P, 512])
                hb = ctx.sb(st, "hb", [P, 512], F32)
                hm = [ctx.sb(st, "hm", [P, 512], F32) for _ in range(2)]
                ctx.op("dve", lambda: nc.vector.tensor_copy(hb[:, 0:256].rearrange("p (a b) -> p a b", a=2), kT[:, :, T:T + P]),
                       reads=[kT], writes=[hb])
                ctx.op("dve", lambda: nc.vector.tensor_copy(hb[:, 256:512], vS[:, NT, :]), reads=[vS], writes=[hb])
                for s in range(NSEG):
                    ctx.op("dve", lambda: nc.vector.tensor_scalar_mul(hm[s % 2][:], hb[:], self.own[:, s:s + 1]),
                           reads=[hb, self.own], writes=[hm[s % 2]])
                    ctx.dma("sp", HCI.ap()[s], hm[s % 2][:], reads=[hm[s % 2]], writes=[("HCI", s)])
                ctx.barrier()
                ctx.allreduce(cfg.groups, HCI.ap().rearrange("s p e -> (s p) e"), HCO.ap().rearrange("s p e -> (s p) e"),
                              writes=[("HCO",)])
                ctx.barrier()
                for s in range(NSEG):
                    ctx.dma("sp", hm[s % 2][:], HCO.ap()[s], writes=[hm[s % 2]])
                    if s == 0:
                        ctx.op("dve", lambda: nc.vector.tensor_scalar_mul(hb[:], hm[s % 2][:], self.halo_sel[:, s:s + 1]),
                               reads=[hm[s % 2], self.halo_sel], writes=[hb])
                    else:
                        ctx.op("dve", lambda: nc.vector.scalar_tensor_tensor(hb[:], hm[s % 2][:], self.halo_sel[:, s:s + 1],
                                                                             hb[:], ALU.mult, ALU.add),
                               reads=[hm[s % 2], self.halo_sel, hb], writes=[hb])
                ctx.op("dve", lambda: nc.vector.tensor_copy(kT[:, :, 0:P], hb[:, 0:256].rearrange("p (a b) -> p a b", a=2)),
                       reads=[hb], writes=[kT])
                ctx.op("dve", lambda: nc.vector.tensor_copy(vS[:, 0, :], hb[:, 256:512]), reads=[hb], writes=[vS])
            ctx.barrier()
        with contextlib.ExitStack() as st:
            qT = self.load_AT(st, "qT", QTs, KC, 0, T)
            ps_s = ctx.ps(st, "ps_s", [P, 8, 2 * P], F32)
            ps_t = ctx.ps(st, "ps_t", [P, 16, P], BF16)
            ps_o = ctx.ps(st, "ps_o", [P, 8, P], F32)
            s_sb = ctx.sb(st, "s_sb", [P, 8, 2 * P], F32)
            e_sb = ctx.sb(st, "e_sb", [P, 8, 2 * P], F32)
            p_sb = ctx.sb(st, "p_sb", [P, 8, 2 * P], BF16)
            pT = ctx.sb(st, "pT", [P, 16, P], BF16)
            mx = ctx.sb(st, "mx", [P, 8], F32)
            nmx = ctx.sb(st, "nmx", [P, 8], F32)
            rs = ctx.sb(st, "rs", [P, 8], F32)
            es = ctx.sb(st, "es", [P, 8], F32)
            G = 4 if NT % 4 == 0 else 2
            ost = [ctx.sb(st, "ost", [P, KC, G * P], BF16) for _ in range(2)]
            OTv = OT.ap().rearrange("(kc p) t -> p kc t", p=P)
            for n in range(NT):
                og = ost[(n // G) % 2]
                for pair in range(2):
                    for par in range(2):
                        hk = 2 * pair + par
                        po = par * 64
                        kc_k = hk // 2
                        for g in range(8):
                            ch = pair * 8 + g
                            ctx.op("pe", lambda: nc.tensor.matmul(ps_s[:, g, :], lhsT=qT[po:po + 64, ch, n * P:(n + 1) * P],
                                                                  rhs=kT[po:po + 64, kc_k, n * P:n * P + 2 * P],
                                                                  start=True, stop=True),
                                   reads=[qT, kT], writes=[ps_s])
                        ctx.op("dve", lambda: nc.vector.tensor_tensor(s_sb[:], ps_s[:], biasS[:, hk * 8:(hk + 1) * 8, :], ALU.add),
                               reads=[ps_s, biasS], writes=[s_sb])
                        if n == 0:
                            ctx.op("pool", lambda: nc.gpsimd.tensor_tensor(s_sb[:], s_sb[:],
                                                                           mfirst[:].unsqueeze(1).broadcast_to([P, 8, 2 * P]), ALU.add),
                                   reads=[s_sb, mfirst], writes=[s_sb])
                        ctx.op("dve", lambda: nc.vector.tensor_reduce(mx[:], s_sb[:], AX.X, ALU.max), reads=[s_sb], writes=[mx])
                        ctx.op("dve", lambda: nc.vector.tensor_tensor(mx[:], mx[:], sinkb[:, hk * 8:(hk + 1) * 8], ALU.max),
                               reads=[mx, sinkb], writes=[mx])
                        ctx.op("dve", lambda: nc.vector.tensor_scalar_mul(nmx[:], mx[:], -1.0), reads=[mx], writes=[nmx])
                        ctx.op("dve", lambda: nc.vector.memset(rs[:], 0.0), writes=[rs])
                        for g in range(8):
                            ctx.op("act", lambda: nc.scalar.activation(e_sb[:, g, :], s_sb[:, g, :], AF.Exp, bias=nmx[:, g:g + 1],
                                                                       scale=1.0, accum_out=rs[:, g:g + 1]),
                                   reads=[s_sb, nmx], writes=[e_sb, rs])
                        ctx.op("dve", lambda: nc.vector.tensor_tensor(es[:], sinkb[:, hk * 8:(hk + 1) * 8], mx[:], ALU.subtract),
                               reads=[sinkb, mx], writes=[es])
                        ctx.op("act", lambda: nc.scalar.activation(es[:], es[:], AF.Exp), reads=[es], writes=[es])
                        ctx.op("dve", lambda: nc.vector.tensor_tensor(rs[:], rs[:], es[:], ALU.add), reads=[rs, es], writes=[rs])
                        ctx.op("dve", lambda: nc.vector.reciprocal(rs[:], rs[:]), reads=[rs], writes=[rs])
                        ctx.op("pool", lambda: nc.gpsimd.tensor_tensor(p_sb[:], e_sb[:], rs[:].unsqueeze(2).broadcast_to([P, 8, 2 * P]),
                                                                       ALU.mult), reads=[e_sb, rs], writes=[p_sb])
                        for g in range(8):
                            for hf in range(2):
                                self.transpose_to(p_sb[:, g, hf * P:(hf + 1) * P], ps_t[:, g * 2 + hf, :], [p_sb], [ps_t])
                        ctx.op("act", lambda: nc.scalar.copy(pT[:], ps_t[:]), reads=[ps_t], writes=[pT])
                        for g in range(8):
                            for hf in range(2):
                                ctx.op("pe", lambda: nc.tensor.matmul(ps_o[po:po + 64, g, :], lhsT=vS[:, n + hf, hk * 64:(hk + 1) * 64],
                                                                      rhs=pT[:, g * 2 + hf, :], start=(hf == 0), stop=(hf == 1)),
                                       reads=[vS, pT], writes=[ps_o])
                    ctx.op("dve", lambda: nc.vector.tensor_copy(og[:, pair * 8:(pair + 1) * 8, (n % G) * P:(n % G + 1) * P], ps_o[:]),
                           reads=[ps_o], writes=[og])
                if n % G == G - 1:
                    g0 = (n // G) * G * P
                    ctx.dma("sp", OTv[:, :, g0:g0 + G * P], og[:], reads=[og], writes=[("OT", n)])
            ctx.barrier()
    with contextlib.ExitStack() as st:
        aT = self.load_AT(st, "oTa", OT, KC, 0, T)
        res = GemmRes(self, st, KC, 512, 3)
        epi = self.epi_resid(st, X, Z1)
        self.gemm_tok(res, aT, aT, KC, w_out, 0, D, range(NT), epi)
        ctx.barrier()


Prog.swa_layer = _swa_layer
RW_H, RW_HD = 32, 64
RW_EPS = 64e-5
RW_NHB = RW_H // 2


def _rwkv_consts(cfg, core, c):
    seg = core % cfg.NSEG
    f32 = np.float32
    s = np.arange(P)[:, None]
    t = np.arange(P)[None, :]
    c["rw_mus"] = (s < t).astype(f32)
    c["rw_mui"] = (s <= t).astype(f32)
    c["rw_mls"] = (s > t).astype(f32)
    c["rw_tri"] = (s <= t).astype(f32)
    c["rw_suf"] = (s > t).astype(f32)
    i2 = np.zeros((P, 64), f32)
    i2[np.arange(P), np.arange(P) % 64] = 1.0
    c["rw_i2"] = i2
    selm = np.zeros((P, cfg.NSEG), f32)
    selm[:, :seg] = 1.0
    c["rw_selm"] = selm
    c["rw_nselm"] = 1.0 - selm


def _rwkv_layer(self, layer, X, XT, Z1):
    ctx, nc, cfg = self.ctx, self.nc, self.cfg
    T, NT, NSEG = cfg.T, cfg.NT, cfg.NSEG
    j = layer // 3
    gi = lambda name, shape: self.inp("%s_%d" % (name, j), shape).ap()
    tw_ = lambda nm, fn, K_, N_, t_, kp_=P: self.tw("%s_t%d" % (nm, j), fn, K_, N_, t_, kp_)
    w_rkv = [tw_("rwkv_w_rkv%d" % i, (lambda inp, i=i: inp["rwkv_w_rkv"][j][i]), D, D, 256) for i in range(3)]
    w1 = tw_("rwkv_w1", lambda inp: inp["rwkv_w1"][j], D, 96, 96)
    w2 = tw_("rwkv_w2", lambda inp: inp["rwkv_w2"][j], 96, D, 256, 96)
    a1 = tw_("rwkv_a1", lambda inp: inp["rwkv_a1"][j], D, 96, 96)
    a2 = tw_("rwkv_a2", lambda inp: inp["rwkv_a2"][j], 96, D, 256, 96)
    g1 = tw_("rwkv_g1", lambda inp: inp["rwkv_g1"][j], D, 256, 256)
    g2 = tw_("rwkv_g2", lambda inp: inp["rwkv_g2"][j], 256, D, 256)
    w_out = tw_("rwkv_w_out", lambda inp: inp["rwkv_w_out"][j], D, D, 512)
    mix_ap = gi("rwkv_mix", [6, D]).rearrange("i (kc p) -> (i kc) p", p=P)
    Rs, Ks, Vs = self.scr("rw_R", [T, D]), self.scr("rw_K", [T, D]), self.scr("rw_V", [T, D])
    WLs, ALs, Gs = self.scr("rw_WL", [T, D]), self.scr("rw_AL", [T, D]), self.scr("rw_G", [T, D])
    Y0 = self.scr("rw_Y0", [T, D])
    BON = self.scr("rw_BON", [T, RW_H])
    YTR = self.scr("rw_YTR", [NT, P, RW_NHB, P], BF16)
    OGT = self.scr("rw_OGT", [D, T], BF16)
    TGW = min(512, T)
    tgs = [(t0, TGW) for t0 in range(0, T, TGW)]

    with contextlib.ExitStack() as st:
        psc = ctx.ps(st, "psc", [P, 512], F32)
        mixc = self.load_cols(st, "mixc", mix_ap, 6 * KC, psc)
        xT = self.load_AT(st, "xT", XT, KC, 0, T, pad=1)
        with contextlib.ExitStack() as sth:
            halo = self.halo_rows(sth, X, 1, "halo1")
            for kc in range(KC):
                ctx.op("pe", lambda: nc.tensor.matmul(psc[:, kc:kc + 1], lhsT=halo[0:1, kc * P:(kc + 1) * P],
                                                      rhs=self.identf[0:1, 0:1], start=True, stop=True),
                       reads=[halo, self.identf], writes=[psc])
            ctx.op("dve", lambda: nc.vector.tensor_copy(xT[:, :, 0:1], psc[:, 0:KC].unsqueeze(2)), reads=[psc], writes=[xT])
            ctx.barrier()
        xm = ctx.sb(st, "xm", [P, KC, T], BF16)
        dtmp = [ctx.sb(st, "dtmp", [P, T], F32) for _ in range(2)]
        hT = ctx.sb(st, "hT", [P, 2, T], BF16)
        res = GemmRes(self, st, KC, 256, 4)
        obuf = [ctx.sb(st, "obuf", [P, 512], F32) for _ in range(3)]
        ocnt = [0]

        def store_epi(dst):
            def epi(ps, tt, c0, nb):
                o = obuf[ocnt[0] % 3]
                ocnt[0] += 1
                ctx.op("act", lambda: nc.scalar.copy(o[:, 0:nb], ps[:, 0:nb]), reads=[ps], writes=[o])
                ctx.dma("sp", dst.ap()[tt * P:(tt + 1) * P, c0:c0 + nb], o[:, 0:nb], reads=[o], writes=[("o", id(dst), tt, c0)])
            return epi

        def build_mix(i):
            for kc in range(KC):
                d = dtmp[kc % 2]
                ctx.op("pool", lambda: nc.gpsimd.tensor_tensor(d[:], xT[:, kc, 0:T], xT[:, kc, 1:T + 1], ALU.subtract),
                       reads=[xT], writes=[d])
                ctx.op("dve", lambda: nc.vector.scalar_tensor_tensor(xm[:, kc, :], d[:], mixc[:, i * KC + kc:i * KC + kc + 1],
                                                                     xT[:, kc, 1:T + 1], ALU.mult, ALU.add),
                       reads=[d, mixc, xT], writes=[xm])

        def lora(i, wa, na, func, wb_, dst):
            build_mix(i)
            kcn2 = (na + P - 1) // P
            kp = min(P, na)

            def h_epi(ps, c, t0, tn):
                fw = min(P, na - c)
                ctx.op("act", lambda: nc.scalar.activation(hT[0:fw, c // P, t0:t0 + tn], ps[0:fw, 0:tn], func),
                       reads=[ps], writes=[hT])
            self.gemm_feat(res, xm, xm, KC, wa, 0, na, tgs, h_epi, nblk=256)
            self.gemm_tok(res, hT, hT, kcn2, wb_, 0, D, range(NT), store_epi(dst), kp=kp, nblk=256)

        build_mix(0)
        self.gemm_tok(res, xm, xm, KC, w_rkv[0], 0, D, range(NT), store_epi(Rs), nblk=256)
        build_mix(2)
        self.gemm_tok(res, xm, xm, KC, w_rkv[1], 0, D, range(NT), store_epi(Ks), nblk=256)
        build_mix(3)
        self.gemm_tok(res, xm, xm, KC, w_rkv[2], 0, D, range(NT), store_epi(Vs), nblk=256)
        lora(1, w1, 96, AF.Tanh, w2, WLs)
        lora(4, a1, 96, AF.Identity, a2, ALs)
        lora(5, g1, 256, AF.Sigmoid, g2, Gs)
        ctx.barrier()
    if cfg.stop == "rw1":
        return

    SXs = self.scr("rw_SX", [P, RW_NHB, P])
    with contextlib.ExitStack() as st:
        def cload(name, shape, dtype=F32):
            t_ = ctx.sb(st, name, shape, dtype)
            ctx.dma("sp", t_[:], self.inp(name, shape, dtype).ap(), writes=[t_])
            return t_
        mus, mui, mls = cload("rw_mus", [P, P]), cload("rw_mui", [P, P]), cload("rw_mls", [P, P])
        tri, suft = cload("rw_tri", [P, P]), cload("rw_suf", [P, P])
        i2 = cload("rw_i2", [P, 64])
        ones = ctx.sb(st, "ones", [P, 1], F32)
        ctx.op("dve", lambda: nc.vector.memset(ones[:], 1.0), writes=[ones])
        w0b = self.bcast_rows(st, "w0b", gi("rwkv_w0", [D]), D)
        a0b = self.bcast_rows(st, "a0b", gi("rwkv_a0", [D]), D)
        kkb = self.bcast_rows(st, "kkb", gi("rwkv_k_k", [D]), D)
        kab = self.bcast_rows(st, "kab", gi("rwkv_k_a", [D]), D)
        rkb = self.bcast_rows(st, "rkb", gi("rwkv_r_k", [RW_H, RW_HD]).rearrange("h d -> (h d)"), D)
        A = ctx.sb(st, "A", [P, D], F32)
        B = ctx.sb(st, "B", [P, D], F32)
        Dw = ctx.sb(st, "Dw", [P, D], F32)
        Ea = ctx.sb(st, "Ea", [P, D], F32)
        Fk = ctx.sb(st, "Fk", [P, D], F32)
        T1 = ctx.sb(st, "T1", [P, D], F32)
        ET = [ctx.sb(st, "ET", [P, 512], F32) for _ in range(4)]
        tok = [ctx.sb(st, "tokb", [P, D], BF16) for _ in range(4)]
        bbk = ctx.sb(st, "bbk", [P, 2, D], BF16)
        vbx = ctx.sb(st, "vbx", [P, RW_H, P], BF16)
        ctx.op("pool", lambda: nc.gpsimd.memset(vbx[:], 0.0), writes=[vbx])
        CM = ctx.sb(st, "CM", [P, RW_NHB, 4, P], BF16)
        ytile = ctx.sb(st, "ytile", [P, D], F32)
        T2 = ytile
        ytr = ctx.sb(st, "ytr", [P, RW_NHB, P], BF16)
        ss = ctx.sb(st, "ss", [P, RW_H], F32)
        bon = ctx.sb(st, "bon", [P, RW_H], F32)
        dectot = ctx.sb(st, "dectot", [P, RW_NHB], F32)
        SX = ctx.sb(st, "SX", [P, RW_NHB, P], F32)
        SXb = ctx.sb(st, "SXb", [P, RW_NHB, P], BF16)
        NPAIR = 3

        class PairRes:
            pass
        PR = []
        for _ in range(NPAIR):
            r_ = PairRes()
            r_.GA = [ctx.sb(st, "GA", [P, 2, 2, P], BF16) for _ in range(2)]
            r_.Brb = ctx.sb(st, "Brb", [P, 2, P], BF16)
            r_.Aak = ctx.sb(st, "Aak", [P, 2, P], BF16)
            r_.Brk = ctx.sb(st, "Brk", [P, 2, P], BF16)
            r_.Tt = [ctx.sb(st, "Tt", [P, 2, P], BF16) for _ in range(2)]
            r_.Wb = ctx.sb(st, "Wb", [P, 2, P], BF16)
            r_.Ub = ctx.sb(st, "Ub", [P, 2, P], BF16)
            r_.H = [ctx.ps(st, "pbH", [P, 512], F32) for _ in range(2)]
            PR.append(r_)
        pbx = ctx.ps(st, "pbx", [P, 512], F32)
        for i_, r_ in enumerate(PR):
            r_.S = pbx
            r_.so = i_ * P
        pb = [PR[0].H[0], PR[0].H[1], PR[1].H[0], PR[1].H[1], PR[2].H[0]]
        ptr = ctx.ps(st, "ptr", [P, 8, P], BF16)
        sxk = [(id(SX), hb) for hb in range(RW_NHB)]
        sxbk = [(id(SXb), hb) for hb in range(RW_NHB)]
        ctx.op("dve", lambda: nc.vector.memset(SX[:], 0.0), writes=sxk)
        ctx.op("dve", lambda: nc.vector.tensor_copy(SX[:, :, 64:128], i2[:].unsqueeze(1).broadcast_to([P, RW_NHB, 64])),
               reads=[i2] + sxk, writes=sxk)
        ctx.op("pool", lambda: nc.gpsimd.tensor_copy(SXb[:], SX[:]), reads=sxk, writes=sxbk)
        v3 = lambda t_: t_[:].rearrange("p (h d) -> p h d", d=RW_HD)
        bc3 = lambda small: small[:].unsqueeze(2).broadcast_to([P, RW_H, RW_HD])
        for n in range(NT):
            rows = slice(n * P, (n + 1) * P)
            ctx.dma("sp", A[:], Rs.ap()[rows, :], writes=[A])
            ctx.dma("sp", B[:], Ks.ap()[rows, :], writes=[B])
            ctx.dma("sp", T1[:], Vs.ap()[rows, :], writes=[T1])
            ctx.op("act", lambda: nc.scalar.copy(vbx[:, :, 0:64], v3(T1)), reads=[T1], writes=[vbx])
            ctx.dma("sp", Dw[:], WLs.ap()[rows, :], writes=[Dw])
            ctx.dma("sp", Ea[:], ALs.ap()[rows, :], writes=[Ea])
            ctx.op("dve", lambda: nc.vector.tensor_tensor(Dw[:], Dw[:], w0b[:], ALU.add), reads=[Dw, w0b], writes=[Dw])
            ctx.op("act", lambda: nc.scalar.activation(Dw[:], Dw[:], AF.Sigmoid), reads=[Dw], writes=[Dw])
            ctx.op("dve", lambda: nc.vector.tensor_scalar_mul(Dw[:], Dw[:], -math.exp(-0.5)), reads=[Dw], writes=[Dw])
            ctx.op("dve", lambda: nc.vector.tensor_tensor(Ea[:], Ea[:], a0b[:], ALU.add), reads=[Ea, a0b], writes=[Ea])
            ctx.op("act", lambda: nc.scalar.activation(Ea[:], Ea[:], AF.Sigmoid), reads=[Ea], writes=[Ea])
            ctx.op("pool", lambda: nc.gpsimd.tensor_tensor(Fk[:], B[:], kkb[:], ALU.mult), reads=[B, kkb], writes=[Fk])
            ctx.op("pool", lambda: nc.gpsimd.tensor_tensor(T2[:], Fk[:], Fk[:], ALU.mult), reads=[Fk], writes=[T2])
            ctx.op("dve", lambda: nc.vector.tensor_reduce(ss[:], v3(T2), AX.X, ALU.add), reads=[T2], writes=[ss])
            ctx.op("act", lambda: nc.scalar.activation(ss[:], ss[:], AF.Sqrt), reads=[ss], writes=[ss])
            ctx.op("dve", lambda: nc.vector.tensor_scalar_max(ss[:], ss[:], 1e-12), reads=[ss], writes=[ss])
            ctx.op("dve", lambda: nc.vector.reciprocal(ss[:], ss[:]), reads=[ss], writes=[ss])
            ctx.op("dve", lambda: nc.vector.tensor_tensor(v3(Fk), v3(Fk), bc3(ss), ALU.mult), reads=[Fk, ss], writes=[Fk])
            ctx.op("dve", lambda: nc.vector.scalar_tensor_tensor(T1[:], Ea[:], -1.0, kab[:], ALU.add, ALU.mult),
                   reads=[Ea, kab], writes=[T1])
            ctx.op("pool", lambda: nc.gpsimd.tensor_tensor(T1[:], T1[:], B[:], ALU.mult), reads=[T1, B], writes=[T1])
            ctx.op("pool", lambda: nc.gpsimd.tensor_tensor(B[:], B[:], T1[:], ALU.add), reads=[T1, B], writes=[B])
            ctx.op("pool", lambda: nc.gpsimd.tensor_tensor(T2[:], A[:], B[:], ALU.mult), reads=[A, B, T2], writes=[T2])
            ctx.op("dve", lambda: nc.vector.tensor_tensor(T2[:], T2[:], rkb[:], ALU.mult), reads=[T2, rkb], writes=[T2])
            ctx.op("dve", lambda: nc.vector.tensor_reduce(bon[:], v3(T2), AX.X, ALU.add), reads=[T2], writes=[bon])
            ctx.dma("sp", BON.ap()[rows, :], bon[:], reads=[bon], writes=[("BON", n)])
            ctx.op("dve", lambda: nc.vector.tensor_tensor(T1[:], Fk[:], Ea[:], ALU.mult), reads=[Fk, Ea, T1], writes=[T1])
            for hb in range(RW_NHB):
                ctx.op("pe", lambda: nc.tensor.matmul(pbx[:, 448 + hb:448 + hb + 1], lhsT=Dw[:, hb * P:(hb + 1) * P], rhs=ones[:, 0:1],
                                                      start=True, stop=True), reads=[Dw, ones], writes=[pbx])
            ctx.op("act", lambda: nc.scalar.activation(dectot[:], pbx[:, 448:448 + RW_NHB], AF.Exp), reads=[pbx], writes=[dectot])
            for cb in range(4):
                cs = slice(cb * 512, (cb + 1) * 512)
                pcum, psuf = pb[1 + (cb % 2) * 2], pb[2 + (cb % 2) * 2]
                ctx.op("pe", lambda: nc.tensor.matmul(pcum[:], lhsT=tri[:], rhs=Dw[:, cs], start=True, stop=True),
                       reads=[tri, Dw], writes=[pcum])
                ctx.op("pe", lambda: nc.tensor.matmul(psuf[:], lhsT=suft[:], rhs=Dw[:, cs], start=True, stop=True),
                       reads=[suft, Dw], writes=[psuf])
                ctx.op("act", lambda: nc.scalar.activation(ET[0][:], pcum[:], AF.Exp), reads=[pcum], writes=[ET[0]])
                ctx.op("pool", lambda: nc.gpsimd.tensor_tensor(tok[3][:, cs], A[:, cs], ET[0][:], ALU.mult),
                       reads=[A, ET[0]], writes=[tok[3]])
                ctx.op("act", lambda: nc.scalar.activation(ET[1][:], pcum[:], AF.Exp, scale=-1.0), reads=[pcum], writes=[ET[1]])
                ctx.op("dve", lambda: nc.vector.tensor_tensor(tok[0][:, cs], T1[:, cs], ET[1][:], ALU.mult),
                       reads=[T1, ET[1]], writes=[tok[0]])
                ctx.op("pool", lambda: nc.gpsimd.tensor_tensor(tok[1][:, cs], B[:, cs], ET[1][:], ALU.mult),
                       reads=[B, ET[1]], writes=[tok[1]])
                ctx.op("dve", lambda: nc.vector.tensor_tensor(ET[2][:], pcum[:], Dw[:, cs], ALU.subtract),
                       reads=[pcum, Dw], writes=[ET[2]])
                ctx.op("act", lambda: nc.scalar.activation(ET[2][:], ET[2][:], AF.Exp), reads=[ET[2]], writes=[ET[2]])
                ctx.op("dve", lambda: nc.vector.scalar_tensor_tensor(tok[2][:, cs], Fk[:, cs], -1.0, ET[2][:], ALU.mult, ALU.mult),
                       reads=[Fk, ET[2]], writes=[tok[2]])
                ctx.op("act", lambda: nc.scalar.activation(ET[3][:], psuf[:], AF.Exp), reads=[psuf], writes=[ET[3]])
                ctx.op("dve", lambda: nc.vector.tensor_tensor(bbk[:, 0, cs], T1[:, cs], ET[3][:], ALU.mult),
                       reads=[T1, ET[3]], writes=[bbk])
                ctx.op("pool", lambda: nc.gpsimd.tensor_tensor(bbk[:, 1, cs], B[:, cs], ET[3][:], ALU.mult),
                       reads=[B, ET[3]], writes=[bbk])
            for kind in range(4):
                for half in range(2):
                    for jj in range(8):
                        hb = half * 8 + jj
                        self.transpose_to(tok[kind][:, hb * P:(hb + 1) * P], ptr[:, jj, :], [tok[kind]], [ptr])
                    if (kind + half) % 2 == 0:
                        ctx.op("act", lambda: nc.scalar.copy(CM[:, half * 8:(half + 1) * 8, kind, :], ptr[:]), reads=[ptr], writes=[CM])
                    else:
                        ctx.op("dve", lambda: nc.vector.tensor_copy(CM[:, half * 8:(half + 1) * 8, kind, :], ptr[:]), reads=[ptr], writes=[CM])
            def pair_gen(hb, R):
                H = R.H
                g0 = R.GA[0]
                hp = ((0, 0), (1, 64))
                for hi, po in hp:
                    rhs_ar = CM[po:po + 64, hb, 2:4, :].rearrange("p k t -> p (k t)")
                    ctx.op("pe", lambda: nc.tensor.matmul(H[hi][:, 0:256], lhsT=CM[po:po + 64, hb, 0, :], rhs=rhs_ar,
                                                          start=True, stop=True), reads=[CM], writes=[H[hi]])
                    ctx.op("pe", lambda: nc.tensor.matmul(H[hi][:, 256:384], lhsT=CM[po:po + 64, hb, 2, :],
                                                          rhs=CM[po:po + 64, hb, 0, :], start=True, stop=True),
                           reads=[CM], writes=[H[hi]])
                yield
                for hi, po in hp:
                    ctx.op("dve", lambda: nc.vector.tensor_tensor(g0[:, hi, 0, :], H[hi][:, 0:P], mus[:], ALU.mult),
                           reads=[H[hi], mus], writes=[g0])
                    ctx.op("dve", lambda: nc.vector.tensor_tensor(R.Brb[:, hi, :], H[hi][:, P:2 * P], mui[:], ALU.mult),
                           reads=[H[hi], mui], writes=[R.Brb])
                    ctx.op("dve", lambda: nc.vector.tensor_tensor(g0[:, hi, 1, :], H[hi][:, 2 * P:3 * P], mls[:], ALU.mult),
                           reads=[H[hi], mls], writes=[g0])
                    ctx.op("pool", lambda: nc.gpsimd.tensor_tensor(R.Tt[0][:, hi, :], g0[:, hi, 0, :], self.ident[:], ALU.add),
                           reads=[g0, self.ident], writes=[R.Tt[0]])
                yield
                if cfg.stop == "g1":
                    return
                tcur = 0
                for lvl in range(1, 7):
                    gc, gn = R.GA[(lvl - 1) % 2], R.GA[lvl % 2]
                    for hi, po in hp:
                        if lvl < 6:
                            ctx.op("pe", lambda: nc.tensor.matmul(H[hi][:, 0:P], lhsT=gc[:, hi, 1, :], rhs=gc[:, hi, 0, :],
                                                                  start=True, stop=True), reads=[gc], writes=[H[hi]])
                        ctx.op("pe", lambda: nc.tensor.matmul(H[hi][:, P:2 * P], lhsT=gc[:, hi, 0, :], rhs=gc[:, hi, 1, :],
                                                              start=True, stop=True), reads=[gc], writes=[H[hi]])
                    yield
                    for hi, po in hp:
                        if lvl < 6:
                            ctx.op("act", lambda: nc.scalar.copy(gn[:, hi].rearrange("p k t -> p (k t)"), H[hi][:, 0:2 * P]),
                                   reads=[H[hi]], writes=[gn])
                        else:
                            ctx.op("act", lambda: nc.scalar.copy(gn[:, hi, 1, :], H[hi][:, P:2 * P]), reads=[H[hi]], writes=[gn])
                    yield
                    for hi, po in hp:
                        ctx.op("pe", lambda: nc.tensor.matmul(H[hi][:, 2 * P:3 * P], lhsT=gn[:, hi, 1, :], rhs=R.Tt[tcur][:, hi, :],
                                                              start=True, stop=True), reads=[gn, R.Tt[tcur]], writes=[H[hi]])
                    yield
                    for hi, po in hp:
                        ctx.op("dve", lambda: nc.vector.tensor_tensor(R.Tt[1 - tcur][:, hi, :], H[hi][:, 2 * P:3 * P], R.Tt[tcur][:, hi, :],
                                                                      ALU.add), reads=[H[hi], R.Tt[tcur]], writes=[R.Tt[1 - tcur]])
                    tcur = 1 - tcur
                    yield
                TT = R.Tt[tcur]
                if cfg.stop == "g2":
                    return
                for hi, po in hp:
                    rhs_ar = CM[po:po + 64, hb, 2:4, :].rearrange("p k t -> p (k t)")
                    ctx.op("pe", lambda: nc.tensor.matmul(H[hi][:, 0:256], lhsT=CM[po:po + 64, hb, 1, :], rhs=rhs_ar,
                                                          start=True, stop=True), reads=[CM], writes=[H[hi]])
                    ctx.op("pe", lambda: nc.tensor.matmul(H[hi][:, 2 * P:3 * P], lhsT=CM[po:po + 64, hb, 2, :],
                                                          rhs=SXb[po:po + 64, hb, :], start=True, stop=False),
                           reads=[CM, (id(SXb), hb)], writes=[H[hi]])
                yield
                for hi, po in hp:
                    ctx.op("dve", lambda: nc.vector.tensor_tensor(R.Aak[:, hi, :], H[hi][:, 0:P], mus[:], ALU.mult),
                           reads=[H[hi], mus], writes=[R.Aak])
                    ctx.op("dve", lambda: nc.vector.tensor_tensor(R.Brk[:, hi, :], H[hi][:, P:2 * P], mui[:], ALU.mult),
                           reads=[H[hi], mui], writes=[R.Brk])
                yield
                for hi, po in hp:
                    h = 2 * hb + hi
                    ctx.op("pe", lambda: nc.tensor.matmul(H[hi][:, 2 * P:3 * P], lhsT=R.Aak[:, hi, :], rhs=vbx[:, h, :],
                                                          start=False, stop=True), reads=[R.Aak, vbx], writes=[H[hi]])
                yield
                for hi, po in hp:
                    ctx.op("act", lambda: nc.scalar.copy(R.Wb[:, hi, :], H[hi][:, 2 * P:3 * P]), reads=[H[hi]], writes=[R.Wb])
                yield
                for hi, po in hp:
                    ctx.op("pe", lambda: nc.tensor.matmul(H[hi][:, 3 * P:4 * P], lhsT=TT[:, hi, :], rhs=R.Wb[:, hi, :],
                                                          start=True, stop=True), reads=[TT, R.Wb], writes=[H[hi]])
                yield
                for hi, po in hp:
                    ctx.op("dve", lambda: nc.vector.tensor_copy(R.Ub[:, hi, :], H[hi][:, 3 * P:4 * P]), reads=[H[hi]], writes=[R.Ub])
                yield
                if cfg.stop == "g3":
                    return
                for hi, po in hp:
                    h = 2 * hb + hi
                    yo = H[hi][:, 0:64]
                    ctx.op("pe", lambda: nc.tensor.matmul(yo, lhsT=CM[po:po + 64, hb, 3, :], rhs=SXb[po:po + 64, hb, 0:64],
                                                          start=True, stop=False), reads=[CM, (id(SXb), hb)], writes=[H[hi]])
                    ctx.op("pe", lambda: nc.tensor.matmul(yo, lhsT=R.Brb[:, hi, :], rhs=R.Ub[:, hi, 0:64], start=False, stop=False),
                           reads=[R.Brb, R.Ub], writes=[H[hi]])
                    ctx.op("pe", lambda: nc.tensor.matmul(yo, lhsT=R.Brk[:, hi, :], rhs=vbx[:, h, 0:64], start=False, stop=True),
                           reads=[R.Brk, vbx], writes=[H[hi]])
                    if NSEG > 1:
                        to = H[hi][po:po + 64, P:2 * P]
                        ctx.op("pe", lambda: nc.tensor.matmul(to, lhsT=SXb[po:po + 64, hb, 64:128], rhs=CM[po:po + 64, hb, 3, :],
                                                              start=True, stop=False), reads=[CM, (id(SXb), hb)], writes=[H[hi]])
                        ctx.op("pe", lambda: nc.tensor.matmul(to, lhsT=R.Ub[:, hi, 64:128], rhs=R.Brb[:, hi, :], start=False, stop=True),
                               reads=[R.Brb, R.Ub], writes=[H[hi]])
                    so = R.S[po:po + 64, R.so:R.so + P]
                    ctx.op("pe", lambda: nc.tensor.matmul(so, lhsT=bbk[:, 0, h * 64:(h + 1) * 64], rhs=R.Ub[:, hi, :], start=True, stop=False),
                           reads=[bbk, R.Ub], writes=[R.S])
                    ctx.op("pe", lambda: nc.tensor.matmul(so, lhsT=bbk[:, 1, h * 64:(h + 1) * 64], rhs=vbx[:, h, :], start=False, stop=True),
                           reads=[bbk, vbx], writes=[R.S])
                yield
                for hi, po in hp:
                    ctx.op("act", lambda: nc.scalar.copy(ytile[:, hb * P + hi * 64:hb * P + (hi + 1) * 64], H[hi][:, 0:64]),
                           reads=[H[hi]], writes=[(id(ytile), hb)])
                    if NSEG > 1:
                        ctx.op("act", lambda: nc.scalar.copy(ytr[po:po + 64, hb, :], H[hi][po:po + 64, P:2 * P]),
                               reads=[H[hi]], writes=[(id(ytr), hb)])
                ctx.op("dve", lambda: nc.vector.scalar_tensor_tensor(SX[:, hb, :], SX[:, hb, :], dectot[:, hb:hb + 1],
                                                                     R.S[:, R.so:R.so + P], ALU.mult, ALU.add),
                       reads=[(id(SX), hb), dectot, R.S], writes=[(id(SX), hb)])
                ctx.op("pool", lambda: nc.gpsimd.tensor_copy(SXb[:, hb, :], SX[:, hb, :]), reads=[(id(SX), hb)], writes=[(id(SXb), hb)])
                yield

            if cfg.stop != "rw2p":
                for g0_ in range(0, RW_NHB, NPAIR):
                    gens = [pair_gen(hb, PR[i]) for i, hb in enumerate(range(g0_, min(RW_NHB, g0_ + NPAIR)))]
                    while gens:
                        for g_ in list(gens):
                            try:
                                next(g_)
                            except StopIteration:
                                gens.remove(g_)
            all_hb = list(range(RW_NHB))
            ctx.dma("sp", Y0.ap()[rows, :], ytile[:], reads=[ytile] + [(id(ytile), hb) for hb in all_hb], writes=[("Y0", n)])
            if NSEG > 1:
                ctx.dma("sp", YTR.ap()[n], ytr[:], reads=[ytr] + [(id(ytr), hb) for hb in all_hb], writes=[("YTR", n)])
        if NSEG > 1:
            ctx.dma("sp", SXs.ap(), SX[:], reads=[SX] + [(id(SX), hb) for hb in range(RW_NHB)], writes=[("SXs",)])
        ctx.barrier()

    if cfg.stop in ("rw2", "rw2p", "g1", "g2", "g3"):
        return
    S0b = ctx.sb(self.top, "rw_S0b_%d" % layer, [P, RW_NHB, 64], BF16)
    if NSEG > 1:
        NSL = NSEG - 1
        CI = self.scr("rw_ci", [NSL, P, RW_NHB * P])
        CO = self.scr("rw_co", [NSL, P, RW_NHB * P])
        with contextlib.ExitStack() as st:
            sx = ctx.sb(st, "sx", [P, RW_NHB * P], F32)
            sm = [ctx.sb(st, "sm", [P, RW_NHB * P], F32) for _ in range(2)]
            ctx.dma("sp", sx[:], SXs.ap().rearrange("p h k -> p (h k)"), writes=[sx])
            for s in range(NSL):
                ctx.op("dve", lambda: nc.vector.tensor_scalar_mul(sm[s % 2][:], sx[:], self.own[:, s:s + 1]),
                       reads=[sx, self.own], writes=[sm[s % 2]])
                ctx.dma("sp", CI.ap()[s], sm[s % 2][:], reads=[sm[s % 2]], writes=[("CI", s)])
            ctx.barrier()
            ctx.allreduce(cfg.groups, CI.ap().rearrange("s p e -> (s p) e"), CO.ap().rearrange("s p e -> (s p) e"),
                          writes=[("CO",)])
            ctx.barrier()
            selm = ctx.sb(st, "selm", [P, NSEG], F32)
            nselm = ctx.sb(st, "nselm", [P, NSEG], F32)
            ctx.dma("sp", selm[:], self.inp("rw_selm", [P, NSEG]).ap(), writes=[selm])
            ctx.dma("sp", nselm[:], self.inp("rw_nselm", [P, NSEG]).ap(), writes=[nselm])
            i2 = ctx.sb(st, "i2", [P, 64], F32)
            ctx.dma("sp", i2[:], self.inp("rw_i2", [P, 64]).ap(), writes=[i2])
            S0 = ctx.sb(st, "S0", [P, RW_NHB, 64], F32)
            ctx.op("dve", lambda: nc.vector.memset(S0[:], 0.0), writes=[S0])
            Mp = ctx.sb(st, "Mp", [P, RW_NHB, 64], F32)
            Lp = ctx.sb(st, "Lp", [P, RW_NHB, 64], F32)
            MT = ctx.sb(st, "MT", [P, RW_NHB, 64], F32)
            pm = [ctx.ps(st, "pm", [P, 8, 64], F32) for _ in range(2)]
            for s in range(NSL):
                slot = sm[s % 2]
                ctx.dma("sp", slot[:], CO.ap()[s], writes=[slot])
                sv = slot[:].rearrange("p (h k) -> p h k", k=P)
                ctx.op("dve", lambda: nc.vector.tensor_scalar_mul(Lp[:], sv[:, :, 0:64], selm[:, s:s + 1]),
                       reads=[slot, selm], writes=[Lp])
                ctx.op("dve", lambda: nc.vector.tensor_scalar_mul(Mp[:], sv[:, :, 64:128], selm[:, s:s + 1]),
                       reads=[slot, selm], writes=[Mp])
                ctx.op("dve", lambda: nc.vector.scalar_tensor_tensor(Mp[:], i2[:].unsqueeze(1).broadcast_to([P, RW_NHB, 64]),
                                                                     nselm[:, s:s + 1], Mp[:], ALU.mult, ALU.add),
                       reads=[i2, nselm, Mp], writes=[Mp])
                for half in range(2):
                    for jj in range(8):
                        hb = half * 8 + jj
                        for hi, po in enumerate((0, 64)):
                            ctx.op("pe", lambda: nc.tensor.matmul(pm[hi][po:po + 64, jj, :], lhsT=Mp[po:po + 64, hb, :],
                                                                  rhs=self.identf[po:po + 64, po:po + 64], start=True, stop=True),
                                   reads=[Mp, self.identf], writes=[pm[hi]])
                    for hi, po in enumerate((0, 64)):
                        ctx.op("act", lambda: nc.scalar.copy(MT[po:po + 64, half * 8:(half + 1) * 8, :], pm[hi][po:po + 64]),
                               reads=[pm[hi]], writes=[MT])
                for half in range(2):
                    for jj in range(8):
                        hb = half * 8 + jj
                        for hi, po in enumerate((0, 64)):
                            ctx.op("pe", lambda: nc.tensor.matmul(pm[hi][po:po + 64, jj, :], lhsT=MT[po:po + 64, hb, :],
                                                                  rhs=S0[po:po + 64, hb, :], start=True, stop=True),
                                   reads=[MT, S0], writes=[pm[hi]])
                    for hi, po in enumerate((0, 64)):
                        ctx.op("dve", lambda: nc.vector.tensor_tensor(S0[po:po + 64, half * 8:(half + 1) * 8, :], pm[hi][po:po + 64],
                                                                      Lp[po:po + 64, half * 8:(half + 1) * 8, :], ALU.add),
                               reads=[pm[hi], Lp, S0], writes=[S0])
            ctx.op("act", lambda: nc.scalar.copy(S0b[:], S0[:]), reads=[S0], writes=[S0b])
            ctx.barrier()

    if cfg.stop == "rw3":
        return
    with contextlib.ExitStack() as st:
        gnb = self.bcast_rows(st, "gnb", gi("rwkv_gn_gain", [D]), D)
        gbb = self.bcast_rows(st, "gbb", gi("rwkv_gn_bias", [D]), D)
        y = ctx.sb(st, "y", [P, D], F32)
        vv = ctx.sb(st, "vv", [P, D], F32)
        gg = ctx.sb(st, "gg", [P, D], F32)
        sq = ctx.sb(st, "sq", [P, D], F32)
        bon = ctx.sb(st, "bon", [P, RW_H], F32)
        s1 = ctx.sb(st, "s1", [P, RW_H], F32)
        s2 = ctx.sb(st, "s2", [P, RW_H], F32)
        ytr = ctx.sb(st, "ytr", [P, RW_NHB, P], BF16)
        ogb = [ctx.sb(st, "ogb", [P, D], BF16) for _ in range(2)]
        G = 4 if NT % 4 == 0 else 2
        stg = [ctx.sb(st, "stg", [P, KC, G * P], BF16) for _ in range(2)]
        pst = [[ctx.ps(st, "pst", [P, 8 * P], BF16) for _ in range(2)] for _ in range(2)]
        pc = [ctx.ps(st, "pc", [P, 512], F32) for _ in range(4)]
        OGTv = OGT.ap().rearrange("(kc p) t -> p kc t", p=P)
        v3 = lambda t_: t_[:].rearrange("p (h d) -> p h d", d=RW_HD)
        bc3 = lambda small: small[:].unsqueeze(2).broadcast_to([P, RW_H, RW_HD])
        for n in range(NT):
            rows = slice(n * P, (n + 1) * P)
            ctx.dma("sp", y[:], Y0.ap()[rows, :], writes=[y])
            ctx.dma("sp", vv[:], Vs.ap()[rows, :], writes=[vv])
            ctx.dma("sp", gg[:], Gs.ap()[rows, :], writes=[gg])
            ctx.dma("sp", bon[:], BON.ap()[rows, :], writes=[bon])
            if NSEG > 1:
                ctx.dma("sp", ytr[:], YTR.ap()[n], writes=[ytr])
                for q4 in range(4):
                    for jj in range(4):
                        hb = q4 * 4 + jj
                        for hi, po in enumerate((0, 64)):
                            pcc = pc[2 * (q4 % 2) + hi]
                            ctx.op("pe", lambda: nc.tensor.matmul(pcc[:, jj * 64:(jj + 1) * 64],
                                                                  lhsT=ytr[po:po + 64, hb, :], rhs=S0b[po:po + 64, hb, :],
                                                                  start=True, stop=True), reads=[ytr, S0b], writes=[pcc])
                    for hi in range(2):
                        pcc = pc[2 * (q4 % 2) + hi]
                        yv = y[:, q4 * 512:(q4 + 1) * 512].rearrange("p (j k) -> p j k", k=P)[:, :, hi * 64:(hi + 1) * 64]
                        ctx.op("dve", lambda: nc.vector.tensor_tensor(yv, yv, pcc[:, 0:256].rearrange("p (j k) -> p j k", k=64), ALU.add),
                               reads=[pcc, y], writes=[y])
            ctx.op("dve", lambda: nc.vector.tensor_reduce(s1[:], v3(y), AX.X, ALU.add), reads=[y], writes=[s1])
            ctx.op("dve", lambda: nc.vector.tensor_scalar_mul(s1[:], s1[:], 1.0 / RW_HD), reads=[s1], writes=[s1])
            ctx.op("dve", lambda: nc.vector.tensor_tensor(v3(y), v3(y), bc3(s1), ALU.subtract), reads=[y, s1], writes=[y])
            ctx.op("pool", lambda: nc.gpsimd.tensor_tensor(sq[:], y[:], y[:], ALU.mult), reads=[y], writes=[sq])
            ctx.op("dve", lambda: nc.vector.tensor_reduce(s2[:], v3(sq), AX.X, ALU.add), reads=[sq], writes=[s2])
            ctx.op("dve", lambda: nc.vector.tensor_scalar(s2[:], s2[:], 1.0 / RW_HD, RW_EPS, ALU.mult, ALU.add), reads=[s2], writes=[s2])
            ctx.op("act", lambda: nc.scalar.activation(s2[:], s2[:], AF.Sqrt), reads=[s2], writes=[s2])
            ctx.op("dve", lambda: nc.vector.reciprocal(s2[:], s2[:]), reads=[s2], writes=[s2])
            ctx.op("dve", lambda: nc.vector.tensor_tensor(v3(y), v3(y), bc3(s2), ALU.mult), reads=[y, s2], writes=[y])
            ctx.op("pool", lambda: nc.gpsimd.tensor_tensor(y[:], y[:], gnb[:], ALU.mult), reads=[y, gnb], writes=[y])
            ctx.op("pool", lambda: nc.gpsimd.tensor_tensor(y[:], y[:], gbb[:], ALU.add), reads=[y, gbb], writes=[y])
            ctx.op("dve", lambda: nc.vector.tensor_tensor(v3(vv), v3(vv), bc3(bon), ALU.mult), reads=[vv, bon], writes=[vv])
            ctx.op("pool", lambda: nc.gpsimd.tensor_tensor(y[:], y[:], vv[:], ALU.add), reads=[y, vv], writes=[y])
            ob = ogb[n % 2]
            ctx.op("dve", lambda: nc.vector.tensor_tensor(ob[:], y[:], gg[:], ALU.mult), reads=[y, gg], writes=[ob])
            g_, gi_ = divmod(n, G)
            self.xt_emit_tile(ob, ob, stg[g_ % 2], stg[g_ % 2], gi_ * P, pst[n % 2])
            if gi_ == G - 1:
                ctx.dma("sp", OGTv[:, :, g_ * G * P:(g_ + 1) * G * P], stg[g_ % 2][:], reads=[stg[g_ % 2]], writes=[("OGT", g_)])
        ctx.barrier()

    if cfg.stop == "rw4":
        return
    with contextlib.ExitStack() as st:
        aT = self.load_AT(st, "oTa", OGT, KC, 0, T)
        res = GemmRes(self, st, KC, 512, 3)
        epi = self.epi_resid(st, X, Z1)
        self.gemm_tok(res, aT, aT, KC, w_out, 0, D, range(NT), epi)
        ctx.barrier()


Prog.rwkv_layer = _rwkv_layer
def _const_inputs(cfg, core):
    T, NSEG = cfg.T, cfg.NSEG
    seg = core % NSEG
    f32 = np.float32
    own = np.zeros((P, NSEG), f32)
    own[:, seg] = 1
    hs = np.zeros((P, NSEG), f32)
    if seg > 0:
        hs[:, seg - 1] = 1
    c = {"ident": np.eye(P, dtype=f32).astype(ml_dtypes.bfloat16), "identf": np.eye(P, dtype=f32),
         "own": own, "halo_sel": hs}
    inv = (1.0 / (10000.0 ** (np.arange(0, RET_DK, 2, dtype=f32) / f32(RET_DK)))).astype(f32)
    pos = (seg * T + np.arange(T)).astype(f32)
    ang = (pos[None, :] * inv[:, None]).astype(f32)
    c["rope_cos"] = np.cos(ang).astype(f32)
    c["rope_sin"] = np.sin(ang).astype(f32)
    gam = np.array(RET_GAMMA, np.float64)
    idx = np.arange(P, dtype=np.float64)
    c["ret_kdec"] = (gam[None, :] ** (P - 1 - idx[:, None])).astype(f32)
    diff = idx[None, :] - idx[:, None]
    m = np.where(diff[:, None, :] >= 0, gam[None, :, None] ** np.maximum(diff[:, None, :], 0), 0.0)
    c["ret_maskT"] = m.astype(f32)
    c["ret_qdec"] = np.broadcast_to((gam[:, None] ** (idx[None, :] + 1.0))[None], (P, RET_H, P)).astype(f32).copy()
    coef = np.zeros((P, NSEG, RET_H), f32)
    for s in range(seg):
        coef[:, s, :] = (gam ** (T * (seg - s - 1)))[None, :]
    c["ret_coef"] = coef
    _swa_consts(cfg, core, c)
    _rwkv_consts(cfg, core, c)
    return c


def make_in_maps(cfg, prog, inputs):
    T, NSEG = cfg.T, cfg.NSEG
    maps = []
    shared = {}
    per_layer = ["ret_w_in", "ret_w_out", "ret_gn_gain", "swa_w_qkv", "swa_sinks", "swa_w_out",
                 "rwkv_mix", "rwkv_w_rkv", "rwkv_w0", "rwkv_w1", "rwkv_w2", "rwkv_a0", "rwkv_a1", "rwkv_a2",
                 "rwkv_g1", "rwkv_g2", "rwkv_k_k", "rwkv_k_a", "rwkv_r_k", "rwkv_gn_gain", "rwkv_gn_bias",
                 "rwkv_w_out", "ffn_w_up", "ffn_conv_w", "ffn_conv_b", "ffn_w_down", "ple_w_proj", "ple_w_gate"]
    for name in prog.inputs:
        if name in prog.tiled:
            fn, K_, N_, tw_, kp_ = prog.tiled[name]
            w = np.asarray(fn(inputs))
            assert w.shape == (K_, N_), (name, w.shape)
            shared[name] = np.ascontiguousarray(
                w.reshape(K_ // kp_, kp_, N_ // tw_, tw_).transpose(2, 1, 0, 3).reshape(N_ // tw_, kp_, (K_ // kp_) * tw_))
            continue
        if name in inputs and name not in ("x",):
            shared[name] = np.ascontiguousarray(inputs[name])
            continue
        for base in per_layer:
            if name.startswith(base + "_") and name[len(base) + 1:].isdigit():
                shared[name] = np.ascontiguousarray(inputs[base][int(name[len(base) + 1:])])
    for core in range(cfg.ncores):
        b, seg = divmod(core, NSEG)
        consts = _const_inputs(cfg, core)
        m = {}
        for name in prog.inputs:
            if name in shared:
                m[name] = shared[name]
            elif name == "x":
                m[name] = np.ascontiguousarray(inputs["x"][b, seg * T:(seg + 1) * T, :])
            elif name.startswith("pT_"):
                l = int(name[3:])
                m[name] = np.ascontiguousarray(inputs["p"][l, b, seg * T:(seg + 1) * T, :].T)
            elif name in consts:
                m[name] = consts[name]
            else:
                raise KeyError(name)
        maps.append(m)
    return maps


def run_cfg(cfg, inputs):
    prog = Prog(cfg)
    prog.build()
    maps = make_in_maps(cfg, prog, inputs)
    res = run_bass_kernel_spmd(prog.nc, maps, core_ids=list(range(cfg.ncores)))
    return prog, res.results


def kernel(**inputs):
    cfg = Cfg()
    prog, results = run_cfg(cfg, inputs)
    out = np.empty((cfg.NB, cfg.NSEG * cfg.T, D), np.float32)
    for core in range(cfg.ncores):
        b, seg = divmod(core, cfg.NSEG)
        out[b, seg * cfg.T:(seg + 1) * cfg.T, :] = results[core]["out"]
    return out
```

```python
import contextlib
import math
import numpy as np
import ml_dtypes
import concourse.bass as bass
import concourse.mybir as mybir
from concourse.bass_utils import run_bass_kernel_spmd

F32 = mybir.dt.float32
BF16 = mybir.dt.bfloat16
AF = mybir.ActivationFunctionType
ALU = mybir.AluOpType
AX = mybir.AxisListType

P = 128
D = 2048
KC = D // P
DEPTH = 4
DFF = 5504
NFB = DFF // P
PLE = 256
ALPHA = (2.0 * DEPTH) ** 0.25
LN_EPS = 1e-5
RET_H, RET_DK, RET_DV = 8, 256, 512
RET_EPS = 1e-5
RET_GAMMA = [1.0 - 2.0 ** (-5.0 - h) for h in range(RET_H)]


class Ctx:
    NDMA = {"sp": 16, "act": 2, "pool": 8}

    def __init__(self, nc, stack):
        self.nc = nc
        self.stack = stack
        self.eng = {"pe": nc.tensor, "act": nc.scalar, "dve": nc.vector,
                    "pool": nc.gpsimd, "sp": nc.sync}
        self.sems = {}
        self.val = {}
        for e in ("pe", "act", "dve", "pool"):
            self.sems[e] = stack.enter_context(nc.semaphore("c_" + e))
            self.val[e] = 0
        self.dq = {}
        self.dq_next = {}
        for q, n in self.NDMA.items():
            keys = []
            for i in range(n):
                k = "d_%s%d" % (q, i)
                self.sems[k] = stack.enter_context(nc.semaphore(k))
                self.val[k] = 0
                keys.append(k)
            self.dq[q] = keys
            self.dq_next[q] = 0
        self.sems["cc"] = stack.enter_context(nc.semaphore("cc"))
        self.val["cc"] = 0
        self.known = {e: {} for e in self.eng}
        self.lastw = {}
        self.readers = {}
        self.uid = 0
        self.n_ins = 0

    def sb(self, stack, name, shape, dtype=F32):
        self.uid += 1
        return stack.enter_context(self.nc.sbuf_tensor("%s_%d" % (name, self.uid), list(shape), dtype))

    def ps(self, stack, name, shape, dtype=F32):
        self.uid += 1
        return stack.enter_context(self.nc.psum_tensor("%s_%d" % (name, self.uid), list(shape), dtype))

    def _key(self, b):
        return b if isinstance(b, (str, tuple)) else id(b)

    def _deps(self, reads, writes, merge=False):
        deps = {}
        for b in list(reads) + ([] if merge else list(writes)):
            for k, v in self.lastw.get(self._key(b), {}).items():
                if deps.get(k, 0) < v:
                    deps[k] = v
        for b in writes:
            for k, v in self.readers.get(self._key(b), {}).items():
                if deps.get(k, 0) < v:
                    deps[k] = v
        return deps

    def _wait(self, e, deps):
        kn = self.known[e]
        for k, v in deps.items():
            if e == "pe" and k == "pe":
                continue
            if kn.get(k, 0) >= v:
                continue
            self.eng[e].wait_ge(self.sems[k], v)
            kn[k] = v

    def _commit(self, ev, reads, writes, merge=False):
        k, v = ev
        for b in reads:
            self.readers.setdefault(self._key(b), {})[k] = v
        for b in writes:
            if merge:
                self.lastw.setdefault(self._key(b), {})[k] = v
            else:
                self.lastw[self._key(b)] = {k: v}
                self.readers[self._key(b)] = {}

    def op(self, e, fn, reads=(), writes=()):
        self._wait(e, self._deps(reads, writes))
        ins = fn()
        self.val[e] += 1
        ins.then_inc(self.sems[e], 1)
        self._commit((e, self.val[e]), reads, writes)
        self.n_ins += 1
        return ins

    def dma(self, q, out, in_, reads=(), writes=(), merge=False, **kw):
        deps = self._deps(reads, writes, merge)
        i = self.dq_next[q]
        self.dq_next[q] = (i + 1) % len(self.dq[q])
        k = self.dq[q][i]
        if self.val[k] > 0:
            deps[k] = max(deps.get(k, 0), self.val[k])
        self._wait(q, deps)
        ins = self.eng[q].dma_start(out=out, in_=in_, **kw)
        self.val[k] += 16
        ins.then_inc(self.sems[k], 16)
        self._commit((k, self.val[k]), reads, writes, merge)
        self.n_ins += 1
        return ins

    def allreduce(self, groups, in_ap, out_ap, reads=(), writes=()):
        deps = self._deps(reads, writes)
        self._wait("pool", deps)
        ins = self.nc.gpsimd.collective_compute("AllReduce", ALU.add, replica_groups=groups,
                                                ins=[in_ap.opt()], outs=[out_ap.opt()])
        self.val["cc"] += 1
        ins.then_inc(self.sems["cc"], 1)
        self._commit(("cc", self.val["cc"]), reads, writes)

    def barrier(self, engines=("pe", "act", "dve", "pool", "sp")):
        deps = {k: v for k, v in self.val.items() if v > 0}
        for e in engines:
            self._wait(e, dict(deps))
        if len(engines) == 5:
            self.lastw = {}
            self.readers = {}

    def finish(self):
        self._wait("sp", {k: v for k, v in self.val.items() if v > 0})


class Cfg:
    def __init__(self, NB=2, NSEG=4, T=2048, layers=(0, 1, 2, 3), debug=(), stop=None):
        self.stop = stop
        self.NB, self.NSEG, self.T = NB, NSEG, T
        self.layers = tuple(layers)
        self.NT = T // P
        self.ncores = NB * NSEG
        self.groups = [[b * NSEG + s for s in range(NSEG)] for b in range(NB)]
        self.debug = tuple(debug)


class Prog:
    def __init__(self, cfg):
        self.cfg = cfg
        self.nc = bass.Bass("TRN2", target_bir_lowering=False)
        self.inputs = {}
        self.scratch = {}
        self.tiled = {}

    def inp(self, name, shape, dtype=F32):
        if name not in self.inputs:
            self.inputs[name] = self.nc.dram_tensor(name, list(shape), dtype, kind="ExternalInput")
        return self.inputs[name]

    def scr(self, name, shape, dtype=F32):
        if name not in self.scratch:
            kind = "ExternalOutput" if name in self.cfg.debug else "Internal"
            self.scratch[name] = self.nc.dram_tensor(name, list(shape), dtype, kind=kind)
        return self.scratch[name]

    def build(self):
        cfg = self.cfg
        nc = self.nc
        T = cfg.T
        with contextlib.ExitStack() as top:
            ctx = self.ctx = Ctx(nc, top)
            self.top = top
            self.ident = ctx.sb(top, "ident", [P, P], BF16)
            ctx.dma("sp", self.ident[:], self.inp("ident", [P, P], BF16).ap(), writes=[self.ident])
            self.identf = ctx.sb(top, "identf", [P, P], F32)
            ctx.dma("sp", self.identf[:], self.inp("identf", [P, P], F32).ap(), writes=[self.identf])
            self.own = ctx.sb(top, "own", [P, cfg.NSEG], F32)
            ctx.dma("sp", self.own[:], self.inp("own", [P, cfg.NSEG]).ap(), writes=[self.own])
            self.halo_sel = ctx.sb(top, "halo_sel", [P, cfg.NSEG], F32)
            ctx.dma("sp", self.halo_sel[:], self.inp("halo_sel", [P, cfg.NSEG]).ap(), writes=[self.halo_sel])

            x_in = self.inp("x", [T, D])
            out = self.nc.dram_tensor("out", [T, D], F32, kind="ExternalOutput")
            XT = self.scr("XT", [D, T], BF16)
            cur = x_in
            self.xt_stage(cur, XT)
            for li, layer in enumerate(cfg.layers):
                kind = layer % 3
                Z1 = self.scr("Z1", [T, D])
                if kind == 0:
                    self.retention_layer(layer, cur, XT, Z1)
                elif kind == 1:
                    self.swa_layer(layer, cur, XT, Z1)
                else:
                    self.rwkv_layer(layer, cur, XT, Z1)
                if cfg.stop is not None:
                    break
                X1 = self.scr("X1", [T, D])
                XT1 = self.scr("XT1", [D, T], BF16)
                self.ln_stage(Z1, layer, 0, X1, XT1)
                Z2 = self.scr("Z2", [T, D])
                self.ffn_layer(layer, X1, XT1, Z2)
                X2 = self.scr("X2", [T, D])
                self.ln_stage(Z2, layer, 1, X2, XT)
                last = li == len(cfg.layers) - 1
                X3 = out if last else self.scr("X3_%d" % (li % 2), [T, D])
                self.ple_layer(layer, X2, XT, X3)
                if not last:
                    self.xt_stage(X3, XT)
                cur = X3
            ctx.barrier()
            ctx.finish()
        return nc

    def load_w(self, dst, src_ap, key):
        self.ctx.dma("pool", dst, src_ap, writes=[key])

    def bcast_rows(self, stack, name, src_ap_1d, n):
        t = self.ctx.sb(stack, name, [P, n], F32)
        self.ctx.dma("sp", t[:], src_ap_1d.partition_broadcast(P), writes=[t])
        return t

    def load_cols(self, stack, name, src_ap_2d, R, ps):
        ctx, nc = self.ctx, self.nc
        out = ctx.sb(stack, name, [P, R], F32)
        with contextlib.ExitStack() as st:
            r0 = 0
            while r0 < R:
                r = min(P, R - r0)
                tmp = ctx.sb(st, name + "_r", [P, P], F32)
                ctx.dma("sp", tmp[0:r, :], src_ap_2d[r0:r0 + r, :], writes=[tmp])
                ctx.op("pe", lambda: nc.tensor.matmul(ps[:, 0:r], lhsT=tmp[0:r, :], rhs=self.identf[0:r, 0:r],
                                                      start=True, stop=True), reads=[tmp, self.identf], writes=[ps])
                ctx.op("dve", lambda: nc.vector.tensor_copy(out[:, r0:r0 + r], ps[:, 0:r]), reads=[ps], writes=[out])
                r0 += r
            ctx.barrier(("pe", "dve", "sp"))
        return out

    def transpose_to(self, src_bf16_ap, pst_ap, reads, writes):
        nc = self.nc
        self.ctx.op("pe", lambda: nc.tensor.transpose(pst_ap, src_bf16_ap, self.ident[:]),
                    reads=list(reads) + [self.ident], writes=writes)

    def xt_emit_tile(self, xb, xb_key, stg, stg_key, col0, pst):
        ctx, nc = self.ctx, self.nc
        for half in range(2):
            pt = pst[half]
            for j in range(8):
                kc = half * 8 + j
                self.transpose_to(xb[:, kc * P:(kc + 1) * P], pt[:, j * P:(j + 1) * P], [xb_key], [pt])
            eng = "dve" if half == 0 else "act"
            src = pt[:].rearrange("p (j c) -> p j c", j=8)
            dst = stg[:, half * 8:(half + 1) * 8, col0:col0 + P]
            if eng == "dve":
                ctx.op("dve", lambda: nc.vector.tensor_copy(dst, src), reads=[pt], writes=[stg_key])
            else:
                ctx.op("act", lambda: nc.scalar.copy(dst, src), reads=[pt], writes=[stg_key])

    def xt_stage(self, X, XT):
        ctx, nc, cfg = self.ctx, self.nc, self.cfg
        T = cfg.T
        G = 4 if cfg.NT % 4 == 0 else 2
        with contextlib.ExitStack() as st:
            xf = [ctx.sb(st, "xf", [P, D], F32) for _ in range(2)]
            xb = [ctx.sb(st, "xb", [P, D], BF16) for _ in range(2)]
            stg = [ctx.sb(st, "stg", [P, KC, G * P], BF16) for _ in range(2)]
            pst = [[ctx.ps(st, "pst", [P, 8 * P], BF16) for _ in range(2)] for _ in range(2)]
            XTv = XT.ap().rearrange("(kc p) t -> p kc t", p=P)
            for tt in range(cfg.NT):
                b = tt % 2
                g, gi = divmod(tt, G)
                ctx.dma("sp", xf[b][:], X.ap()[tt * P:(tt + 1) * P, :], writes=[xf[b]])
                ctx.op("pool", lambda: nc.gpsimd.tensor_copy(xb[b][:], xf[b][:]), reads=[xf[b]], writes=[xb[b]])
                self.xt_emit_tile(xb[b], xb[b], stg[g % 2], stg[g % 2], gi * P, pst[b])
                if gi == G - 1:
                    ctx.dma("sp", XTv[:, :, g * G * P:(g + 1) * G * P], stg[g % 2][:], reads=[stg[g % 2]],
                            writes=[("XT", g)])
            ctx.barrier()

    def ln_stage(self, Z, layer, which, X, XT):
        ctx, nc, cfg = self.ctx, self.nc, self.cfg
        G = 4 if cfg.NT % 4 == 0 else 2
        with contextlib.ExitStack() as st:
            gain = self.bcast_rows(st, "lng", self.inp("ln_gain", [DEPTH, 2, D]).ap()[layer, which, :], D)
            bias = self.bcast_rows(st, "lnb", self.inp("ln_bias", [DEPTH, 2, D]).ap()[layer, which, :], D)
            zf = [ctx.sb(st, "zf", [P, D], F32) for _ in range(2)]
            xn = [ctx.sb(st, "xn", [P, D], F32) for _ in range(2)]
            xo = [ctx.sb(st, "xo", [P, D], F32) for _ in range(2)]
            xb = [ctx.sb(st, "xb", [P, D], BF16) for _ in range(2)]
            stats = [ctx.sb(st, "stats", [P, 4, 6], F32) for _ in range(2)]
            mv = [ctx.sb(st, "mv", [P, 4], F32) for _ in range(2)]
            stg = [ctx.sb(st, "stg", [P, KC, G * P], BF16) for _ in range(2)]
            pst = [[ctx.ps(st, "pst", [P, 8 * P], BF16) for _ in range(2)] for _ in range(2)]
            XTv = XT.ap().rearrange("(kc p) t -> p kc t", p=P)
            for tt in range(cfg.NT):
                b = tt % 2
                g, gi = divmod(tt, G)
                ctx.dma("sp", zf[b][:], Z.ap()[tt * P:(tt + 1) * P, :], writes=[zf[b]])
                self.layernorm_tile(zf[b], xn[b], stats[b], mv[b], D, LN_EPS)
                ctx.op("dve", lambda: nc.vector.tensor_tensor(xn[b][:], xn[b][:], gain[:], ALU.mult),
                       reads=[xn[b], gain], writes=[xn[b]])
                ctx.op("pool", lambda: nc.gpsimd.tensor_tensor(xo[b][:], xn[b][:], bias[:], ALU.add),
                       reads=[xn[b], bias], writes=[xo[b]])
                ctx.dma("sp", X.ap()[tt * P:(tt + 1) * P, :], xo[b][:], reads=[xo[b]], writes=[("X", tt)])
                ctx.op("act", lambda: nc.scalar.copy(xb[b][:], xo[b][:]), reads=[xo[b]], writes=[xb[b]])
                self.xt_emit_tile(xb[b], xb[b], stg[g % 2], stg[g % 2], gi * P, pst[b])
                if gi == G - 1:
                    ctx.dma("sp", XTv[:, :, g * G * P:(g + 1) * G * P], stg[g % 2][:], reads=[stg[g % 2]],
                            writes=[("XT", g)])
            ctx.barrier()

    def layernorm_tile(self, src, dst, stats, mv, n, eps, src_key=None, dst_key=None):
        ctx, nc = self.ctx, self.nc
        src_key = src if src_key is None else src_key
        dst_key = dst if dst_key is None else dst_key
        nch = max(1, n // 512)
        w = n // nch
        for c in range(nch):
            ctx.op("dve", lambda: nc.vector.bn_stats(stats[:, c, :], src[:, c * w:(c + 1) * w]),
                   reads=[src_key], writes=[stats])
        ctx.op("dve", lambda: nc.vector.bn_aggr(mv[:, 0:2], stats[:, 0:nch, :]), reads=[stats], writes=[mv])
        ctx.op("dve", lambda: nc.vector.tensor_scalar_add(mv[:, 2:3], mv[:, 1:2], eps), reads=[mv], writes=[mv])
        ctx.op("act", lambda: nc.scalar.activation(mv[:, 2:3], mv[:, 2:3], AF.Sqrt), reads=[mv], writes=[mv])
        ctx.op("dve", lambda: nc.vector.reciprocal(mv[:, 2:3], mv[:, 2:3]), reads=[mv], writes=[mv])
        ctx.op("dve", lambda: nc.vector.tensor_scalar(mv[:, 3:4], mv[:, 0:1], mv[:, 2:3], -1.0, ALU.mult, ALU.mult),
               reads=[mv], writes=[mv])
        ctx.op("act", lambda: nc.scalar.activation(dst[:, 0:n], src[:, 0:n], AF.Identity, bias=mv[:, 3:4],
                                                   scale=mv[:, 2:3]), reads=[src_key, mv], writes=[dst_key])


class TiledW:
    def __init__(self, h, K, N, tw, kp):
        self.h, self.K, self.N, self.tw, self.kp = h, K, N, tw, kp
        self.kcn = K // kp


def _tw(self, name, src_fn, K, N, tw, kp=P):
    if name not in self.inputs:
        self.inp(name, [N // tw, kp, (K // kp) * tw])
        self.tiled[name] = (src_fn, K, N, tw, kp)
    return TiledW(self.inputs[name], K, N, tw, kp)


def _load_wt(self, wb, W, c0, nb, key):
    assert c0 % W.tw == 0 and nb % W.tw == 0, (c0, nb, W.tw)
    i0, nt = c0 // W.tw, nb // W.tw
    for i in range(nt):
        dst = wb[0:W.kp, 0:W.kcn, i * W.tw:(i + 1) * W.tw]
        src = W.h.ap()[i0 + i].rearrange("p (kc j) -> p kc j", j=W.tw)
        self.ctx.dma("pool", dst, src, writes=[key], merge=(i > 0))


Prog.tw = _tw
Prog.load_wt = _load_wt


class GemmRes:
    def __init__(self, prog, st, kcmax, nblk, npsum, nw=2):
        ctx = prog.ctx
        self.w = [ctx.sb(st, "wbuf", [P, kcmax, nblk], BF16) for _ in range(nw)]
        self.ps = [ctx.ps(st, "gps", [P, 512], F32) for _ in range(npsum)]
        self.wi = 0
        self.pi = 0
        self.pending = {}

    def next_w(self):
        w = self.w[self.wi]
        self.wi = (self.wi + 1) % len(self.w)
        for k in [k for k, v in self.pending.items() if v is w]:
            del self.pending[k]
        return w

    def prefetch(self, prog, W, c0, nb):
        key = (id(W.h), c0, nb)
        if key in self.pending:
            return
        wb = self.next_w()
        prog.load_wt(wb, W, c0, nb, wb)
        self.pending[key] = wb

    def get_w(self, prog, W, c0, nb):
        wb = self.pending.pop((id(W.h), c0, nb), None)
        if wb is None:
            wb = self.next_w()
            prog.load_wt(wb, W, c0, nb, wb)
        return wb

    def next_ps(self):
        p = self.ps[self.pi]
        self.pi = (self.pi + 1) % len(self.ps)
        return p


def _gemm_tok(self, res, AT, at_key, kcn, W, n0, ncols, tts, epi, nblk=512, kp=P, nxt=None):
    ctx, nc = self.ctx, self.nc
    blocks = [(c0, min(nblk, n0 + ncols - c0)) for c0 in range(n0, n0 + ncols, nblk)]
    for bi, (c0, nb) in enumerate(blocks):
        wb = res.get_w(self, W, c0, nb)
        if bi + 1 < len(blocks):
            res.prefetch(self, W, *blocks[bi + 1])
        elif nxt is not None:
            res.prefetch(self, *nxt)
        for tt in tts:
            ps = res.next_ps()
            for kc in range(kcn):
                ctx.op("pe", lambda: nc.tensor.matmul(ps[:, 0:nb], lhsT=AT[0:kp, kc, tt * P:(tt + 1) * P],
                                                      rhs=wb[0:kp, kc, 0:nb], start=(kc == 0), stop=(kc == kcn - 1)),
                       reads=[at_key, wb], writes=[ps])
            epi(ps, tt, c0, nb)


def _gemm_feat(self, res, AT, at_key, kcn, W, n0, ncols, tgs, epi, nblk=512, nxt=None):
    ctx, nc = self.ctx, self.nc
    blocks = [(c0, min(nblk, n0 + ncols - c0)) for c0 in range(n0, n0 + ncols, nblk)]
    for bi, (c0, nb) in enumerate(blocks):
        wb = res.get_w(self, W, c0, nb)
        if bi + 1 < len(blocks):
            res.prefetch(self, W, *blocks[bi + 1])
        elif nxt is not None:
            res.prefetch(self, *nxt)
        for (t0, tn) in tgs:
            for fb in range((nb + P - 1) // P):
                fw = min(P, nb - fb * P)
                ps = res.next_ps()
                for kc in range(kcn):
                    ctx.op("pe", lambda: nc.tensor.matmul(ps[0:fw, 0:tn], lhsT=wb[:, kc, fb * P:fb * P + fw],
                                                          rhs=AT[:, kc, t0:t0 + tn], start=(kc == 0),
                                                          stop=(kc == kcn - 1)),
                           reads=[at_key, wb], writes=[ps])
                epi(ps, c0 + fb * P, t0, tn)


Prog.gemm_tok = _gemm_tok
Prog.gemm_feat = _gemm_feat


def _load_AT(self, st, name, XT, kcn, t0, tn, pad=0):
    ctx = self.ctx
    t = ctx.sb(st, name, [P, kcn, pad + tn], BF16)
    v = XT.ap().rearrange("(kc p) t -> p kc t", p=P)
    step = max(1, kcn // 4)
    for k0 in range(0, kcn, step):
        k1 = min(kcn, k0 + step)
        ctx.dma("sp", t[:, k0:k1, pad:pad + tn], v[:, k0:k1, t0:t0 + tn], writes=[t], merge=(k0 > 0))
    return t


Prog.load_AT = _load_AT


def _epi_resid(self, st, Xold, Zout, tok_base=0):
    ctx, nc = self.ctx, self.nc
    xo = [ctx.sb(st, "rx", [P, 512], F32) for _ in range(3)]
    zt = [ctx.sb(st, "rz", [P, 512], F32) for _ in range(3)]
    cnt = [0]

    def epi(ps, tt, c0, nb):
        i = cnt[0] % 3
        cnt[0] += 1
        r0 = tok_base + tt * P
        ctx.dma("sp", xo[i][:, 0:nb], Xold.ap()[r0:r0 + P, c0:c0 + nb], writes=[xo[i]])
        ctx.op("dve", lambda: nc.vector.scalar_tensor_tensor(zt[i][:, 0:nb], xo[i][:, 0:nb], ALPHA, ps[:, 0:nb],
                                                             ALU.mult, ALU.add),
               reads=[xo[i], ps], writes=[zt[i]])
        ctx.dma("sp", Zout.ap()[r0:r0 + P, c0:c0 + nb], zt[i][:, 0:nb], reads=[zt[i]], writes=[("Z", r0, c0)])
    return epi


Prog.epi_resid = _epi_resid


def _retention_layer(self, layer, X, XT, Z1):
    ctx, nc, cfg = self.ctx, self.nc, self.cfg
    T, NT, NSEG = cfg.T, cfg.NT, cfg.NSEG
    j = layer // 3
    w_qk = self.tw("ret_w_qk_t%d" % j, lambda inp, j=j: inp["ret_w_in"][j][:, 0:4096], D, 4096, 256)
    w_vg = self.tw("ret_w_vg_t%d" % j, lambda inp, j=j: inp["ret_w_in"][j][:, 4096:12288], D, 8192, 512)
    w_out = self.tw("ret_w_out_t%d" % j, lambda inp, j=j: inp["ret_w_out"][j], 4096, D, 256)
    gn_ap = self.inp("ret_gn_gain_%d" % j, [4096]).ap()
    KTs = self.scr("ret_KT", [D, T], BF16)
    Vs = self.scr("ret_V", [T, 4096], BF16)
    OGT = self.scr("ret_OGT", [4096, T], BF16)
    CCI = self.scr("ret_cci", [RET_H, NSEG, 2, P, 512])
    CCO = self.scr("ret_cco", [RET_H, NSEG, 2, P, 512])
    TH = min(1024, T)
    NTH = TH // P
    TGW = min(512, TH)
    tgs = [(t0, TGW) for t0 in range(0, TH, TGW)]
    QOFF, KOFF, VOFF, GOFF = 0, 2048, 0, 4096
    rope_cos = self.inp("rope_cos", [P, T]).ap()
    rope_sin = self.inp("rope_sin", [P, T]).ap()

    def rotary_epi(cosT, sinT, dstT, tmp):
        state = {}

        def epi(ps, c, t0, tn):
            half = (c // P) % 2
            if half == 0:
                state["A"] = ps
                return
            psA, psB = state["A"], ps
            t1, t2, t3, t4 = tmp
            cs, sn = cosT[:, t0:t0 + tn], sinT[:, t0:t0 + tn]
            ctx.op("dve", lambda: nc.vector.tensor_tensor(t1[:, 0:tn], psA[:, 0:tn], cs, ALU.mult),
                   reads=[psA, cosT], writes=[t1])
            ctx.op("dve", lambda: nc.vector.tensor_tensor(t2[:, 0:tn], psB[:, 0:tn], sn, ALU.mult),
                   reads=[psB, sinT], writes=[t2])
            ctx.op("dve", lambda: nc.vector.tensor_tensor(t3[:, 0:tn], psA[:, 0:tn], sn, ALU.mult),
                   reads=[psA, sinT], writes=[t3])
            ctx.op("dve", lambda: nc.vector.tensor_tensor(t4[:, 0:tn], psB[:, 0:tn], cs, ALU.mult),
                   reads=[psB, cosT], writes=[t4])
            ctx.op("pool", lambda: nc.gpsimd.tensor_tensor(dstT[:, 0, t0:t0 + tn], t1[:, 0:tn], t2[:, 0:tn],
                                                           ALU.subtract), reads=[t1, t2], writes=[dstT])
            ctx.op("pool", lambda: nc.gpsimd.tensor_tensor(dstT[:, 1, t0:t0 + tn], t3[:, 0:tn], t4[:, 0:tn],
                                                           ALU.add), reads=[t3, t4], writes=[dstT])
        return epi

    KTv = KTs.ap().rearrange("(h two p) t -> h p two t", two=2, p=P)
    Vv = Vs.ap().rearrange("(tt p) e -> p tt e", p=P)
    OGTv = OGT.ap().rearrange("(h fc p) t -> h p fc t", fc=4, p=P)

    with contextlib.ExitStack() as st:
        kdec = ctx.sb(st, "kdec", [P, RET_H], F32)
        ctx.dma("sp", kdec[:], self.inp("ret_kdec", [P, RET_H]).ap(), writes=[kdec])
        res = GemmRes(self, st, KC, 512, 3)
        tmp = [ctx.sb(st, "rt", [P, 512], F32) for _ in range(4)]
        kT = [ctx.sb(st, "kT", [P, 2, TH], BF16) for _ in range(2)]
        vh = [ctx.sb(st, "vh", [P, NTH, 512], BF16) for _ in range(2)]
        kdA = [ctx.sb(st, "kdA", [P, 2 * P], BF16) for _ in range(2)]
        pst = [ctx.ps(st, "pst", [P, 2 * P], BF16) for _ in range(2)]
        Lps = [ctx.ps(st, "Lps", [P, 512], F32) for _ in range(2)]
        Lacc = ctx.sb(st, "Lacc", [P, RET_H, 2, 512], F32)
        Lm = [ctx.sb(st, "Lm", [P, 2, 512], F32) for _ in range(2)]
        cosk = ctx.sb(st, "cosk", [P, TH], F32)
        sink = ctx.sb(st, "sink", [P, TH], F32)
        xT = ctx.sb(st, "xT", [P, KC, TH], BF16)
        XTv = XT.ap().rearrange("(kc p) t -> p kc t", p=P)
        for th in range(T // TH):
            t0h = th * TH
            for k0 in range(0, KC, 4):
                ctx.dma("sp", xT[:, k0:k0 + 4, :], XTv[:, k0:k0 + 4, t0h:t0h + TH], writes=[xT], merge=(k0 > 0))
            ctx.dma("sp", cosk[:], rope_cos[:, t0h:t0h + TH], writes=[cosk])
            ctx.dma("sp", sink[:], rope_sin[:, t0h:t0h + TH], writes=[sink])
            ctx.op("pool", lambda: nc.gpsimd.tensor_scalar_mul(cosk[:], cosk[:], RET_DK ** -0.5), reads=[cosk], writes=[cosk])
            ctx.op("pool", lambda: nc.gpsimd.tensor_scalar_mul(sink[:], sink[:], RET_DK ** -0.5), reads=[sink], writes=[sink])
            for h in range(RET_H):
                kTh, vhh = kT[h % 2], vh[h % 2]
                self.gemm_feat(res, xT, xT, KC, w_qk, KOFF + h * 256, 256, tgs, rotary_epi(cosk, sink, kTh, tmp), nblk=256,
                               nxt=(w_vg, VOFF + h * 512, 512))
                ctx.dma("sp", KTv[h][:, :, t0h:t0h + TH], kTh[:], reads=[kTh], writes=[("KT", h, th)])

                def v_epi(ps, tt, c0, nb):
                    ctx.op("act", lambda: nc.scalar.copy(vhh[:, tt, :], ps[:, 0:nb]), reads=[ps], writes=[vhh])
                self.gemm_tok(res, xT, xT, KC, w_vg, VOFF + h * 512, 512, range(NTH), v_epi,
                              nxt=(w_qk, KOFF + ((h + 1) % RET_H) * 256, 256))
                ctx.dma("sp", Vv[:, th * NTH:(th + 1) * NTH, h * 512:(h + 1) * 512], vhh[:],
                        reads=[vhh], writes=[("V", h, th)])
                if NSEG > 1:
                    g = RET_GAMMA[h]
                    for cl in range(NTH):
                        c = th * NTH + cl
                        pt = pst[cl % 2]
                        kd = kdA[cl % 2]
                        for half in range(2):
                            self.transpose_to(kTh[:, half, cl * P:(cl + 1) * P], pt[:, half * P:(half + 1) * P], [kTh], [pt])
                        ctx.op("dve", lambda: nc.vector.tensor_scalar(kd[:], pt[:], kdec[:, h:h + 1],
                                                                      float(g ** (P * (NT - 1 - c))), ALU.mult, ALU.mult),
                               reads=[pt, kdec], writes=[kd])
                        for half in range(2):
                            ctx.op("pe", lambda: nc.tensor.matmul(Lps[half][:], lhsT=kd[:, half * P:(half + 1) * P],
                                                                  rhs=vhh[:, cl, :], start=(cl == 0), stop=(cl == NTH - 1)),
                                   reads=[kd, vhh], writes=[Lps[half]])
                    for half in range(2):
                        if th == 0:
                            ctx.op("act", lambda: nc.scalar.copy(Lacc[:, h, half, :], Lps[half][:]),
                                   reads=[Lps[half]], writes=[(id(Lacc), h)])
                        else:
                            ctx.op("dve", lambda: nc.vector.tensor_tensor(Lacc[:, h, half, :], Lacc[:, h, half, :],
                                                                          Lps[half][:], ALU.add),
                                   reads=[Lps[half], (id(Lacc), h)], writes=[(id(Lacc), h)])
        if NSEG > 1:
            for h in range(RET_H):
                for s in range(NSEG):
                    lm = Lm[(h * NSEG + s) % 2]
                    ctx.op("dve", lambda: nc.vector.tensor_scalar_mul(lm[:], Lacc[:, h], self.own[:, s:s + 1]),
                           reads=[(id(Lacc), h), self.own], writes=[lm])
                    ctx.dma("sp", CCI.ap()[h, s].rearrange("two p e -> p two e"), lm[:], reads=[lm],
                            writes=[("CCI", s, h)])
        ctx.barrier()
    with contextlib.ExitStack() as st:
        res = GemmRes(self, st, KC, 512, 2)
        if NSEG > 1:
            for h in range(RET_H):
                ctx.allreduce(cfg.groups, CCI.ap()[h].rearrange("s two p e -> (s two p) e"),
                              CCO.ap()[h].rearrange("s two p e -> (s two p) e"), writes=[("CCO", h)])
        kdec = ctx.sb(st, "kdec", [P, RET_H], F32)
        ctx.dma("sp", kdec[:], self.inp("ret_kdec", [P, RET_H]).ap(), writes=[kdec])
        maskT = ctx.sb(st, "maskT", [P, RET_H, P], F32)
        ctx.dma("sp", maskT[:], self.inp("ret_maskT", [P, RET_H, P]).ap(), writes=[maskT])
        qdec = ctx.sb(st, "qdec", [P, RET_H, P], F32)
        ctx.dma("sp", qdec[:], self.inp("ret_qdec", [P, RET_H, P]).ap(), writes=[qdec])
        coef = ctx.sb(st, "coef", [P, NSEG, RET_H], F32)
        ctx.dma("sp", coef[:], self.inp("ret_coef", [P, NSEG, RET_H]).ap(), writes=[coef])
        gain = self.bcast_rows(st, "gng", gn_ap, 4096)
        tmp = [ctx.sb(st, "rt", [P, 512], F32) for _ in range(4)]
        cosT = ctx.sb(st, "cos", [P, TH], F32)
        sinT = ctx.sb(st, "sin", [P, TH], F32)
        xT = ctx.sb(st, "xT", [P, KC, TH], BF16)
        XTv = XT.ap().rearrange("(kc p) t -> p kc t", p=P)
        kTh = ctx.sb(st, "kT", [P, 2, TH], BF16)
        qTh = ctx.sb(st, "qT", [P, 2, TH], BF16)
        vhh = ctx.sb(st, "vh", [P, NTH, 512], BF16)
        gsh = ctx.sb(st, "gs", [P, NTH, 512], BF16)
        ogTh = ctx.sb(st, "ogT", [P, 4, TH], BF16)
        Rall = ctx.sb(st, "Rall", [P, RET_H, 2, 512], F32)
        Rb = ctx.sb(st, "Rb", [P, 2, 512], BF16)
        cin = [ctx.sb(st, "cin", [P, 2, 512], F32) for _ in range(2)]
        sT = [ctx.sb(st, "sT", [P, P], BF16) for _ in range(2)]
        qd = [ctx.sb(st, "qd", [P, 2, P], BF16) for _ in range(2)]
        kd = [ctx.sb(st, "kd", [P, 2 * P], BF16) for _ in range(2)]
        on = [ctx.sb(st, "on", [P, 512], F32) for _ in range(2)]
        og = [ctx.sb(st, "og", [P, 512], F32) for _ in range(2)]
        og2 = [ctx.sb(st, "og2", [P, 512], BF16) for _ in range(2)]
        stats = [ctx.sb(st, "stats", [P, 4, 6], F32) for _ in range(2)]
        mv = [ctx.sb(st, "mv", [P, 4], F32) for _ in range(2)]
        ps_s = ctx.ps(st, "ps_s", [P, 512], F32)
        ps_o2 = [ctx.ps(st, "ps_o", [P, 512], F32) for _ in range(2)]
        ps_tg = ctx.ps(st, "ps_tg", [P, 6 * P], BF16)
        ps_R = [ctx.ps(st, "ps_R", [P, 512], F32) for _ in range(2)]
        for th in range(T // TH):
            t0h = th * TH
            for k0 in range(0, KC, 4):
                ctx.dma("sp", xT[:, k0:k0 + 4, :], XTv[:, k0:k0 + 4, t0h:t0h + TH], writes=[xT], merge=(k0 > 0))
            ctx.dma("sp", cosT[:], rope_cos[:, t0h:t0h + TH], writes=[cosT])
            ctx.dma("sp", sinT[:], rope_sin[:, t0h:t0h + TH], writes=[sinT])
            for h in range(RET_H):
                Rk = (id(Rall), h)
                ctx.dma("sp", kTh[:], KTv[h][:, :, t0h:t0h + TH], writes=[kTh])
                ctx.dma("sp", vhh[:], Vv[:, th * NTH:(th + 1) * NTH, h * 512:(h + 1) * 512], writes=[vhh])
                self.gemm_feat(res, xT, xT, KC, w_qk, QOFF + h * 256, 256, tgs, rotary_epi(cosT, sinT, qTh, tmp), nblk=256,
                               nxt=(w_vg, GOFF + h * 512, 512))

                def g_epi(ps, tt, c0, nb):
                    ctx.op("act", lambda: nc.scalar.activation(gsh[:, tt, :], ps[:, 0:nb], AF.Silu), reads=[ps], writes=[gsh])
                self.gemm_tok(res, xT, xT, KC, w_vg, GOFF + h * 512, 512, range(NTH), g_epi,
                              nxt=(w_qk, QOFF + ((h + 1) % RET_H) * 256, 256))
                if th == 0:
                    if NSEG > 1:
                        for s_ in range(NSEG):
                            ci = cin[s_ % 2]
                            ctx.dma("sp", ci[:], CCO.ap()[h, s_].rearrange("two p e -> p two e"), reads=[("CCO", h)], writes=[ci])
                            if s_ == 0:
                                ctx.op("dve", lambda: nc.vector.tensor_scalar_mul(Rall[:, h], ci[:], coef[:, s_, h:h + 1]),
                                       reads=[ci, coef], writes=[Rk])
                            else:
                                ctx.op("dve", lambda: nc.vector.scalar_tensor_tensor(Rall[:, h], ci[:], coef[:, s_, h:h + 1],
                                                                                     Rall[:, h], ALU.mult, ALU.add),
                                       reads=[ci, coef, Rk], writes=[Rk])
                    else:
                        ctx.op("dve", lambda: nc.vector.memset(Rall[:, h], 0.0), writes=[Rk])
                ctx.op("act", lambda: nc.scalar.copy(Rb[:], Rall[:, h]), reads=[Rk], writes=[Rb])
                gam = RET_GAMMA[h]
                for cl in range(NTH):
                    cb = cl % 2
                    ps_o = ps_o2[cb]
                    cs = slice(cl * P, (cl + 1) * P)
                    for half in range(2):
                        ctx.op("pe", lambda: nc.tensor.matmul(ps_s[:, 0:P], lhsT=kTh[:, half, cs], rhs=qTh[:, half, cs],
                                                              start=(half == 0), stop=(half == 1)),
                               reads=[kTh, qTh], writes=[ps_s])
                    ctx.op("dve", lambda: nc.vector.tensor_tensor(sT[cb][:], ps_s[:, 0:P], maskT[:, h, :], ALU.mult),
                           reads=[ps_s, maskT], writes=[sT[cb]])
                    for half in range(2):
                        ctx.op("pool", lambda: nc.gpsimd.tensor_tensor(qd[cb][:, half, :], qTh[:, half, cs],
                                                                       qdec[:, h, :], ALU.mult),
                               reads=[qTh, qdec], writes=[qd[cb]])
                    ctx.op("pe", lambda: nc.tensor.matmul(ps_o[:], lhsT=sT[cb][:], rhs=vhh[:, cl, :], start=True, stop=False),
                           reads=[sT[cb], vhh], writes=[ps_o])
                    for half in range(2):
                        ctx.op("pe", lambda: nc.tensor.matmul(ps_o[:], lhsT=qd[cb][:, half, :], rhs=Rb[:, half, :],
                                                              start=False, stop=(half == 1)),
                               reads=[qd[cb], Rb], writes=[ps_o])
                    for half in range(2):
                        self.transpose_to(kTh[:, half, cs], ps_tg[:, half * P:(half + 1) * P], [kTh], [ps_tg])
                    ctx.op("dve", lambda: nc.vector.tensor_scalar_mul(kd[cb][:], ps_tg[:, 0:2 * P], kdec[:, h:h + 1]),
                           reads=[ps_tg, kdec], writes=[kd[cb]])
                    for half in range(2):
                        ctx.op("pe", lambda: nc.tensor.matmul(ps_R[half][:], lhsT=kd[cb][:, half * P:(half + 1) * P],
                                                              rhs=vhh[:, cl, :], start=True, stop=True),
                               reads=[kd[cb], vhh], writes=[ps_R[half]])
                        ctx.op("dve", lambda: nc.vector.scalar_tensor_tensor(Rall[:, h, half, :], Rall[:, h, half, :],
                                                                             float(gam ** P), ps_R[half][:],
                                                                             ALU.mult, ALU.add),
                               reads=[Rk, ps_R[half]], writes=[Rk])
                    ctx.op("act", lambda: nc.scalar.copy(Rb[:], Rall[:, h]), reads=[Rk], writes=[Rb])
                    self.layernorm_tile(ps_o, on[cb], stats[cb], mv[cb], 512, RET_EPS)
                    ctx.op("dve", lambda: nc.vector.tensor_tensor(og[cb][:], on[cb][:], gain[:, h * 512:(h + 1) * 512], ALU.mult),
                           reads=[on[cb], gain], writes=[og[cb]])
                    ctx.op("pool", lambda: nc.gpsimd.tensor_tensor(og2[cb][:], og[cb][:], gsh[:, cl, :], ALU.mult),
                           reads=[og[cb], gsh], writes=[og2[cb]])
                    for fc in range(4):
                        self.transpose_to(og2[cb][:, fc * P:(fc + 1) * P], ps_tg[:, (2 + fc) * P:(3 + fc) * P], [og2[cb]], [ps_tg])
                    ctx.op("act", lambda: nc.scalar.copy(ogTh[:, :, cs], ps_tg[:, 2 * P:6 * P].rearrange("p (f c) -> p f c", f=4)),
                           reads=[ps_tg], writes=[ogTh])
                ctx.dma("sp", OGTv[h][:, :, t0h:t0h + TH], ogTh[:], reads=[ogTh], writes=[("OGT", h, th)])
        ctx.barrier()

    for t0 in range(0, T, TH):
        with contextlib.ExitStack() as st:
            aT = self.load_AT(st, "ogTa", OGT, 32, t0, TH)
            res = GemmRes(self, st, 32, 256, 3)
            epi = self.epi_resid(st, X, Z1, tok_base=t0)
            self.gemm_tok(res, aT, aT, 32, w_out, 0, D, range(TH // P), epi, nblk=256)
            ctx.barrier()


Prog.retention_layer = _retention_layer
def _halo_rows(self, st, Xsrc, nrows, name):
    ctx, nc, cfg = self.ctx, self.nc, self.cfg
    NSEG, T = cfg.NSEG, cfg.T
    halo = ctx.sb(st, name, [nrows, D], F32)
    if NSEG == 1:
        ctx.op("dve", lambda: nc.vector.memset(halo[:], 0.0), writes=[halo])
        return halo
    HCI = self.scr("halo_ci_%d" % nrows, [NSEG, nrows, D])
    HCO = self.scr("halo_co_%d" % nrows, [NSEG, nrows, D])
    hx = ctx.sb(st, name + "_x", [nrows, D], F32)
    hm = [ctx.sb(st, name + "_m", [nrows, D], F32) for _ in range(2)]
    ctx.dma("sp", hx[:], Xsrc.ap()[T - nrows:T, :], writes=[hx])
    for s in range(NSEG):
        ctx.op("dve", lambda: nc.vector.tensor_scalar_mul(hm[s % 2][:], hx[:], self.own[0:nrows, s:s + 1]),
               reads=[hx, self.own], writes=[hm[s % 2]])
        ctx.dma("sp", HCI.ap()[s], hm[s % 2][:], reads=[hm[s % 2]], writes=[("HCI", s)])
    ctx.barrier()
    ctx.allreduce(cfg.groups, HCI.ap().rearrange("s r d -> (s r) d"), HCO.ap().rearrange("s r d -> (s r) d"),
                  writes=[("HCO",)])
    ctx.barrier()
    for s in range(NSEG):
        ctx.dma("sp", hm[s % 2][:], HCO.ap()[s], writes=[hm[s % 2]])
        if s == 0:
            ctx.op("dve", lambda: nc.vector.tensor_scalar_mul(halo[:], hm[s % 2][:], self.halo_sel[0:nrows, s:s + 1]),
                   reads=[hm[s % 2], self.halo_sel], writes=[halo])
        else:
            ctx.op("dve", lambda: nc.vector.scalar_tensor_tensor(halo[:], hm[s % 2][:], self.halo_sel[0:nrows, s:s + 1],
                                                                 halo[:], ALU.mult, ALU.add),
                   reads=[hm[s % 2], self.halo_sel, halo], writes=[halo])
    return halo


Prog.halo_rows = _halo_rows


def _ffn_layer(self, layer, X1, XT1, Z2):
    ctx, nc, cfg = self.ctx, self.nc, self.cfg
    T = cfg.T
    w_up = self.tw("ffn_w_up_t%d" % layer, lambda inp, l=layer: inp["ffn_w_up"][l], D, 2 * DFF, 128)
    w_dn = self.tw("ffn_w_down_t%d" % layer, lambda inp, l=layer: inp["ffn_w_down"][l], DFF, D, 256)
    cw_ap = self.inp("ffn_conv_w_%d" % layer, [3, 2 * DFF]).ap().rearrange("t (b p) -> (t b) p", p=P)
    cb_ap = self.inp("ffn_conv_b_%d" % layer, [2 * DFF]).ap().rearrange("(b p) -> b p", p=P)
    NB2 = 2 * NFB
    TG = min(1024, T)
    W = TG + 2
    nsub = (W + 511) // 512
    bounds = [(W * i) // nsub for i in range(nsub + 1)]
    with contextlib.ExitStack() as st0:
        psc = ctx.ps(st0, "psc", [P, 512], F32)
        cw = self.load_cols(st0, "cw", cw_ap, 3 * NB2, psc)
        cb = self.load_cols(st0, "cb", cb_ap, NB2, psc)
        haloT = ctx.sb(st0, "haloT", [P, KC, 2], BF16)
        with contextlib.ExitStack() as sth:
            halo = self.halo_rows(sth, X1, 2, "halo2")
            for kc in range(KC):
                ctx.op("pe", lambda: nc.tensor.matmul(psc[:, kc * 2:kc * 2 + 2], lhsT=halo[0:2, kc * P:(kc + 1) * P],
                                                      rhs=self.identf[0:2, 0:2], start=True, stop=True),
                       reads=[halo, self.identf], writes=[psc])
            ctx.op("dve", lambda: nc.vector.tensor_copy(haloT[:], psc[:, 0:2 * KC].rearrange("p (k c) -> p k c", c=2)),
                   reads=[psc], writes=[haloT])
            ctx.barrier()
        gT = ctx.sb(st0, "gT", [P, NFB, TG], BF16)
        XTv = XT1.ap().rearrange("(kc p) t -> p kc t", p=P)
        for g in range(T // TG):
            t0 = g * TG
            with contextlib.ExitStack() as st:
                xT = ctx.sb(st, "x1T", [P, KC, W], BF16)
                for k0 in range(0, KC, 4):
                    ctx.dma("sp", xT[:, k0:k0 + 4, 2:W], XTv[:, k0:k0 + 4, t0:t0 + TG], writes=[xT], merge=(k0 > 0))
                if g == 0:
                    ctx.op("pool", lambda: nc.gpsimd.tensor_copy(xT[:, :, 0:2], haloT[:]), reads=[haloT], writes=[xT])
                else:
                    ctx.dma("sp", xT[:, :, 0:2], XTv[:, :, t0 - 2:t0], writes=[xT], merge=True)
                wu = [ctx.sb(st, "wu", [P, 2, KC, P], BF16) for _ in range(2)]
                wg = [ctx.sb(st, "wg", [P, 2, KC, P], BF16) for _ in range(2)]
                pss = [ctx.ps(st, "fps", [P, 512], F32) for _ in range(6)]
                hs = [ctx.sb(st, "hs", [P, W], F32) for _ in range(2)]
                acc = [ctx.sb(st, "acc", [P, TG], F32) for _ in range(2)]
                sg = ctx.sb(st, "sg", [P, TG], F32)
                pi = 0

                def load_pair(pr):
                    fb0 = 2 * pr
                    wi_ = pr % 2
                    for ti in range(min(2, NFB - fb0)):
                        ctx.dma("pool", wu[wi_][:, ti], w_up.h.ap()[fb0 + ti].rearrange("p (kc j) -> p kc j", j=P),
                                writes=[wu[wi_]], merge=(ti > 0))
                        ctx.dma("pool", wg[wi_][:, ti], w_up.h.ap()[NFB + fb0 + ti].rearrange("p (kc j) -> p kc j", j=P),
                                writes=[wg[wi_]], merge=(ti > 0))
                load_pair(0)
                for fb in range(NFB):
                    if fb % 2 == 0 and fb + 2 < NFB:
                        load_pair(fb // 2 + 1)
                    wi = (fb // 2) % 2
                    fo = (fb % 2) * P
                    for ui, wt in enumerate((wu[wi], wg[wi])):
                        for si in range(nsub):
                            a, b_ = bounds[si], bounds[si + 1]
                            ps = pss[pi % 6]
                            pi += 1
                            for kc in range(KC):
                                ctx.op("pe", lambda: nc.tensor.matmul(ps[:, 0:b_ - a], lhsT=wt[:, fb % 2, kc, :],
                                                                      rhs=xT[:, kc, a:b_], start=(kc == 0), stop=(kc == KC - 1)),
                                       reads=[wt, xT], writes=[ps])
                            ctx.op("act", lambda: nc.scalar.copy(hs[ui][:, a:b_], ps[:, 0:b_ - a]), reads=[ps], writes=[hs[ui]])
                        blk = fb if ui == 0 else NFB + fb
                        w0 = cw[:, 0 * NB2 + blk:0 * NB2 + blk + 1]
                        w1 = cw[:, 1 * NB2 + blk:1 * NB2 + blk + 1]
                        w2 = cw[:, 2 * NB2 + blk:2 * NB2 + blk + 1]
                        ctx.op("act", lambda: nc.scalar.activation(acc[ui][:], hs[ui][:, 2:W], AF.Identity,
                                                                   bias=cb[:, blk:blk + 1], scale=w2),
                               reads=[hs[ui], cw, cb], writes=[acc[ui]])
                        ctx.op("dve", lambda: nc.vector.scalar_tensor_tensor(acc[ui][:], hs[ui][:, 1:W - 1], w1, acc[ui][:],
                                                                             ALU.mult, ALU.add),
                               reads=[hs[ui], cw, acc[ui]], writes=[acc[ui]])
                        ctx.op("dve", lambda: nc.vector.scalar_tensor_tensor(acc[ui][:], hs[ui][:, 0:W - 2], w0, acc[ui][:],
                                                                             ALU.mult, ALU.add),
                               reads=[hs[ui], cw, acc[ui]], writes=[acc[ui]])
                    ctx.op("act", lambda: nc.scalar.activation(sg[:], acc[1][:], AF.Silu), reads=[acc[1]], writes=[sg])
                    ctx.op("pool", lambda: nc.gpsimd.tensor_tensor(gT[:, fb, :], sg[:], acc[0][:], ALU.mult),
                           reads=[sg, acc[0]], writes=[gT])
                ctx.barrier()
            with contextlib.ExitStack() as st:
                res = GemmRes(self, st, NFB, 256, 4)
                epi = self.epi_resid(st, X1, Z2, tok_base=t0)
                self.gemm_tok(res, gT, gT, NFB, w_dn, 0, D, range(TG // P), epi, nblk=256)
                ctx.barrier()


Prog.ffn_layer = _ffn_layer


def _ple_layer(self, layer, X2, XT2, X3):
    ctx, nc, cfg = self.ctx, self.nc, self.cfg
    T, NT = cfg.T, cfg.NT
    w_gate = self.tw("ple_w_gate_t%d" % layer, lambda inp, l=layer: inp["ple_w_gate"][l], D, D, 512)
    w_proj = self.tw("ple_w_proj_t%d" % layer, lambda inp, l=layer: inp["ple_w_proj"][l], PLE, D, 512)
    pT_in = self.inp("pT_%d" % layer, [PLE, T]).ap()
    with contextlib.ExitStack() as st:
        xT = self.load_AT(st, "x2T", XT2, KC, 0, T)
        pT = ctx.sb(st, "pT", [P, 2, T], BF16)
        self.load_w(pT[:], pT_in.rearrange("(kc p) t -> p kc t", p=P), pT)
        wg = [ctx.sb(st, "wg", [P, KC, 512], BF16) for _ in range(2)]
        wp = [ctx.sb(st, "wp", [P, 2, 512], BF16) for _ in range(2)]
        psg = [ctx.ps(st, "psg", [P, 512], F32) for _ in range(3)]
        psp = [ctx.ps(st, "psp", [P, 512], F32) for _ in range(3)]
        sg = [ctx.sb(st, "sg", [P, 512], F32) for _ in range(3)]
        x2 = [ctx.sb(st, "x2", [P, 512], F32) for _ in range(3)]
        x3 = [ctx.sb(st, "x3", [P, 512], F32) for _ in range(3)]
        it = 0

        def load_blk(ci_):
            self.load_wt(wg[ci_ % 2], w_gate, ci_ * 512, 512, wg[ci_ % 2])
            self.load_wt(wp[ci_ % 2], w_proj, ci_ * 512, 512, wp[ci_ % 2])
        load_blk(0)
        for ci, c0 in enumerate(range(0, D, 512)):
            wgi, wpi = wg[ci % 2], wp[ci % 2]
            if ci + 1 < D // 512:
                load_blk(ci + 1)
            for tt in range(NT):
                i = it % 3
                it += 1
                ts = slice(tt * P, (tt + 1) * P)
                for kc in range(KC):
                    ctx.op("pe", lambda: nc.tensor.matmul(psg[i][:], lhsT=xT[:, kc, ts], rhs=wgi[:, kc, :],
                                                          start=(kc == 0), stop=(kc == KC - 1)),
                           reads=[xT, wgi], writes=[psg[i]])
                for kc in range(2):
                    ctx.op("pe", lambda: nc.tensor.matmul(psp[i][:], lhsT=pT[:, kc, ts], rhs=wpi[:, kc, :],
                                                          start=(kc == 0), stop=(kc == 1)),
                           reads=[pT, wpi], writes=[psp[i]])
                ctx.dma("sp", x2[i][:], X2.ap()[tt * P:(tt + 1) * P, c0:c0 + 512], writes=[x2[i]])
                ctx.op("act", lambda: nc.scalar.activation(sg[i][:], psg[i][:], AF.Sigmoid), reads=[psg[i]], writes=[sg[i]])
                ctx.op("dve", lambda: nc.vector.tensor_tensor(sg[i][:], sg[i][:], psp[i][:], ALU.mult),
                       reads=[sg[i], psp[i]], writes=[sg[i]])
                ctx.op("pool", lambda: nc.gpsimd.tensor_tensor(x3[i][:], sg[i][:], x2[i][:], ALU.add),
                       reads=[sg[i], x2[i]], writes=[x3[i]])
                ctx.dma("sp", X3.ap()[tt * P:(tt + 1) * P, c0:c0 + 512], x3[i][:], reads=[x3[i]], writes=[("X3", tt, c0)])
        ctx.barrier()


Prog.ple_layer = _ple_layer
SWA_HQ, SWA_HKV, SWA_HD, SWA_W = 32, 4, 64, 128
NEG = -1e30


def _swa_head_order():
    order = []
    for pair in range(2):
        for g in range(8):
            order.append((2 * pair) * 8 + g)
            order.append((2 * pair + 1) * 8 + g)
    return order


def _t5_bucket(n):
    max_exact = 16
    if n < max_exact:
        return n
    large = max_exact + int(np.log(max(n, 1) / max_exact) / np.log(SWA_W / max_exact) * (32 - max_exact))
    return min(large, 31)


def _swa_consts(cfg, core, c):
    seg = core % cfg.NSEG
    E = np.zeros((32, 383), np.float32)
    for u in range(383):
        d = u - 127
        if 0 <= d < SWA_W:
            nn = np.maximum(np.array([d]), 0)
            large = 16 + (np.log(np.maximum(nn, 1) / 16) / np.log(SWA_W / 16) * 16).astype(np.int32)
            large = np.minimum(large, 31)
            b = int(np.where(nn < 16, nn, large)[0])
            E[b, u] = 1.0
    c["swa_E"] = E
    i = np.arange(P)[:, None]
    j = np.arange(2 * P)[None, :]
    d = i + P - j
    c["swa_maskc"] = np.where((d >= 0) & (d < SWA_W), 0.0, NEG).astype(np.float32)
    mf = np.zeros((P, 2 * P), np.float32)
    if seg == 0:
        mf[:, :P] = NEG
    c["swa_mask_first"] = mf


def _swa_layer(self, layer, X, XT, Z1):
    ctx, nc, cfg = self.ctx, self.nc, self.cfg
    T, NT, NSEG = cfg.T, cfg.NT, cfg.NSEG
    j = layer // 3
    def _qkv_src(inp, j=j):
        w = inp["swa_w_qkv"][j]
        qcols = np.concatenate([np.arange(h * 64, (h + 1) * 64) for h in _swa_head_order()])
        return np.concatenate([w[:, qcols], w[:, 2048:]], axis=1)

    def _out_src(inp, j=j):
        rows = np.concatenate([np.arange(h * 64, (h + 1) * 64) for h in _swa_head_order()])
        return inp["swa_w_out"][j][rows, :]
    w_q = self.tw("swa_w_q_t%d" % j, lambda inp: _qkv_src(inp)[:, 0:2048], D, 2048, 512)
    w_kv = self.tw("swa_w_kv_t%d" % j, lambda inp: _qkv_src(inp)[:, 2048:2560], D, 512, 256)
    w_out = self.tw("swa_w_out_t%d" % j, _out_src, D, D, 512)
    sinks_ap = self.inp("swa_sinks_%d" % j, [SWA_HQ]).ap()
    relb_ap = self.inp("rel_bias", [32, SWA_HQ]).ap()
    OT = self.scr("swa_OT", [D, T], BF16)
    QTs = self.scr("swa_QT", [D, T], BF16)
    order = _swa_head_order()
    TGW = min(512, T)
    tgs = [(t0, TGW) for t0 in range(0, T, TGW)]
    with contextlib.ExitStack() as st0:
        kT = ctx.sb(st0, "kT", [P, 2, P + T], BF16)
        vS = ctx.sb(st0, "vS", [P, 1 + NT, 256], BF16)
        biasS = ctx.sb(st0, "biasS", [P, SWA_HQ, 2 * P], F32)
        sinkb = self.bcast_rows(st0, "sinkb", sinks_ap, SWA_HQ)
        mfirst = ctx.sb(st0, "mfirst", [P, 2 * P], F32)
        ctx.dma("sp", mfirst[:], self.inp("swa_mask_first", [P, 2 * P]).ap(), writes=[mfirst])
        with contextlib.ExitStack() as st:
            E = ctx.sb(st, "E", [32, 383], F32)
            RB = ctx.sb(st, "RB", [32, SWA_HQ], F32)
            maskc = ctx.sb(st, "maskc", [P, 2 * P], F32)
            ctx.dma("sp", E[:], self.inp("swa_E", [32, 383]).ap(), writes=[E])
            ctx.dma("sp", RB[:], relb_ap, writes=[RB])
            ctx.dma("sp", maskc[:], self.inp("swa_maskc", [P, 2 * P]).ap(), writes=[maskc])
            psb = [ctx.ps(st, "psb", [P, 512], F32) for _ in range(2)]
            for r in range(16):
                ps = psb[r % 2]
                for jj in range(16):
                    jk = r * 16 + jj
                    ctx.op("pe", lambda: nc.tensor.matmul(ps[:, jj * 32:(jj + 1) * 32], lhsT=E[:, 255 - jk:383 - jk], rhs=RB[:],
                                                          start=True, stop=True), reads=[E, RB], writes=[ps])
                ctx.op("dve", lambda: nc.vector.tensor_tensor(
                    biasS[:, :, r * 16:(r + 1) * 16].rearrange("p h j -> p j h"),
                    ps[:].rearrange("p (j h) -> p j h", h=32),
                    maskc[:, r * 16:(r + 1) * 16].unsqueeze(2).broadcast_to([P, 16, 32]), ALU.add),
                    reads=[ps, maskc], writes=[biasS])
            ctx.barrier()
        with contextlib.ExitStack() as st:
            xT = self.load_AT(st, "xT", XT, KC, 0, T)
            res = GemmRes(self, st, KC, 512, 3)
            qst = [ctx.sb(st, "qst", [P, 4, TGW], BF16) for _ in range(2)]
            QTv = QTs.ap().rearrange("(kc p) t -> p kc t", p=P)
            qcnt = [0]

            def q_epi(ps, c, t0, tn):
                kc = c // P
                qs = qst[(qcnt[0] // 4) % 2]
                ctx.op("act", lambda: nc.scalar.activation(qs[:, kc % 4, 0:tn], ps[:, 0:tn], AF.Copy, scale=SWA_HD ** -0.5),
                       reads=[ps], writes=[qs])
                qcnt[0] += 1
                if kc % 4 == 3:
                    ctx.dma("sp", QTv[:, kc - 3:kc + 1, t0:t0 + tn], qs[:, :, 0:tn], reads=[qs], writes=[("QT", kc, t0)])
            self.gemm_feat(res, xT, xT, KC, w_q, 0, 2048, tgs, q_epi)

            def k_epi(ps, c, t0, tn):
                fb = c // P
                ctx.op("act", lambda: nc.scalar.copy(kT[:, fb, P + t0:P + t0 + tn], ps[:, 0:tn]), reads=[ps], writes=[kT])
            self.gemm_feat(res, xT, xT, KC, w_kv, 0, 256, tgs, k_epi, nblk=256)

            def v_epi(ps, tt, c0, nb):
                ctx.op("act", lambda: nc.scalar.copy(vS[:, 1 + tt, :], ps[:, 0:nb]), reads=[ps], writes=[vS])
            self.gemm_tok(res, xT, xT, KC, w_kv, 256, 256, range(NT), v_epi, nblk=256)
            ctx.barrier()
        with contextlib.ExitStack() as st:
            if NSEG == 1:
                ctx.op("dve", lambda: nc.vector.memset(kT[:, :, 0:P], 0.0), writes=[kT])
                ctx.op("dve", lambda: nc.vector.memset(vS[:, 0, :], 0.0), writes=[vS])
            else:
                HCI = self.scr("swa_ci", [NSEG, P, 512])
                HCO = self.scr("swa_co", [NSEG, P, 512])
                hb = ctx.sb(st, "hb", [P, 512], F32)
                hm = [ctx.sb(st, "hm", [P, 512], F32) for _ in range(2)]
                ctx.op("dve", lambda: nc.vector.tensor_copy(hb[:, 0:256].rearrange("p (a b) -> p a b", a=2), kT[:, :, T:T + P]),
                       reads=[kT], writes=[hb])
                ctx.op("dve", lambda: nc.vector.tensor_copy(hb[:, 256:512], vS[:, NT, :]), reads=[vS], writes=[hb])
                for s in range(NSEG):
                    ctx.op("dve", lambda: nc.vector.tensor_scalar_mul(hm[s % 2][:], hb[:], self.own[:, s:s + 1]),
                           reads=[hb, self.own], writes=[hm[s % 2]])
                    ctx.dma("sp", HCI.ap()[s], hm[s % 2][:], reads=[hm[s % 2]], writes=[("HCI", s)])
                ctx.barrier()
                ctx.allreduce(cfg.groups, HCI.ap().rearrange("s p e -> (s p) e"), HCO.ap().rearrange("s p e -> (s p) e"),
                              writes=[("HCO",)])
                ctx.barrier()
                for s in range(NSEG):
                    ctx.dma("sp", hm[s % 2][:], HCO.ap()[s], writes=[hm[s % 2]])
                    if s == 0:
                        ctx.op("dve", lambda: nc.vector.tensor_scalar_mul(hb[:], hm[s % 2][:], self.halo_sel[:, s:s + 1]),
                               reads=[hm[s % 2], self.halo_sel], writes=[hb])
                    else:
                        ctx.op("dve", lambda: nc.vector.scalar_tensor_tensor(hb[:], hm[s % 2][:], self.halo_sel[:, s:s + 1],
                                                                             hb[:], ALU.mult, ALU.add),
                               reads=[hm[s % 2], self.halo_sel, hb], writes=[hb])
                ctx.op("dve", lambda: nc.vector.tensor_copy(kT[:, :, 0:P], hb[:, 0:256].rearrange("p (a b) -> p a b", a=2)),
                       reads=[hb], writes=[kT])
                ctx.op("dve", lambda: nc.vector.tensor_copy(vS[:, 0, :], hb[:, 256:512]), reads=[hb], writes=[vS])
            ctx.barrier()
        with contextlib.ExitStack() as st:
            qT = self.load_AT(st, "qT", QTs, KC, 0, T)
            ps_s = ctx.ps(st, "ps_s", [P, 8, 2 * P], F32)
            ps_t = ctx.ps(st, "ps_t", [P, 16, P], BF16)
            ps_o = ctx.ps(st, "ps_o", [P, 8, P], F32)
            s_sb = ctx.sb(st, "s_sb", [P, 8, 2 * P], F32)
            e_sb = ctx.sb(st, "e_sb", [P, 8, 2 * P], F32)
            p_sb = ctx.sb(st, "p_sb", [P, 8, 2 * P], BF16)
            pT = ctx.sb(st, "pT", [P, 16, P], BF16)
            mx = ctx.sb(st, "mx", [P, 8], F32)
            nmx = ctx.sb(st, "nmx", [P, 8], F32)
            rs = ctx.sb(st, "rs", [P, 8], F32)
            es = ctx.sb(st, "es", [P, 8], F32)
            G = 4 if NT % 4 == 0 else 2
            ost = [ctx.sb(st, "ost", [P, KC, G * P], BF16) for _ in range(2)]
            OTv = OT.ap().rearrange("(kc p) t -> p kc t", p=P)
            for n in range(NT):
                og = ost[(n // G) % 2]
                for pair in range(2):
                    for par in range(2):
                        hk = 2 * pair + par
                        po = par * 64
                        kc_k = hk // 2
                        for g in range(8):
                            ch = pair * 8 + g
                            ctx.op("pe", lambda: nc.tensor.matmul(ps_s[:, g, :], lhsT=qT[po:po + 64, ch, n * P:(n + 1) * P],
                                                                  rhs=kT[po:po + 64, kc_k, n * P:n * P + 2 * P],
                                                                  start=True, stop=True),
                                   reads=[qT, kT], writes=[ps_s])
                        ctx.op("dve", lambda: nc.vector.tensor_tensor(s_sb[:], ps_s[:], biasS[:, hk * 8:(hk + 1) * 8, :], ALU.add),
                               reads=[ps_s, biasS], writes=[s_sb])
                        if n == 0:
                            ctx.op("pool", lambda: nc.gpsimd.tensor_tensor(s_sb[:], s_sb[:],
                                                                           mfirst[:].unsqueeze(1).broadcast_to([P, 8, 2 * P]), ALU.add),
                                   reads=[s_sb, mfirst], writes=[s_sb])
                        ctx.op("dve", lambda: nc.vector.tensor_reduce(mx[:], s_sb[:], AX.X, ALU.max), reads=[s_sb], writes=[mx])
                        ctx.op("dve", lambda: nc.vector.tensor_tensor(mx[:], mx[:], sinkb[:, hk * 8:(hk + 1) * 8], ALU.max),
                               reads=[mx, sinkb], writes=[mx])
                        ctx.op("dve", lambda: nc.vector.tensor_scalar_mul(nmx[:], mx[:], -1.0), reads=[mx], writes=[nmx])
                        ctx.op("dve", lambda: nc.vector.memset(rs[:], 0.0), writes=[rs])
                        for g in range(8):
                            ctx.op("act", lambda: nc.scalar.activation(e_sb[:, g, :], s_sb[:, g, :], AF.Exp, bias=nmx[:, g:g + 1],
                                                                       scale=1.0, accum_out=rs[:, g:g + 1]),
                                   reads=[s_sb, nmx], writes=[e_sb, rs])
                        ctx.op("dve", lambda: nc.vector.tensor_tensor(es[:], sinkb[:, hk * 8:(hk + 1) * 8], mx[:], ALU.subtract),
                               reads=[sinkb, mx], writes=[es])
                        ctx.op("act", lambda: nc.scalar.activation(es[:], es[:], AF.Exp), reads=[es], writes=[es])
                        ctx.op("dve", lambda: nc.vector.tensor_tensor(rs[:], rs[:], es[:], ALU.add), reads=[rs, es], writes=[rs])
                        ctx.op("dve", lambda: nc.vector.reciprocal(rs[:], rs[:]), reads=[rs], writes=[rs])
                        ctx.op("pool", lambda: nc.gpsimd.tensor_tensor(p_sb[:], e_sb[:], rs[:].unsqueeze(2).broadcast_to([P, 8, 2 * P]),
                                                                       ALU.mult), reads=[e_sb, rs], writes=[p_sb])
                        for g in range(8):
                            for hf in range(2):
                                self.transpose_to(p_sb[:, g, hf * P:(hf + 1) * P], ps_t[:, g * 2 + hf, :], [p_sb], [ps_t])
                        ctx.op("act", lambda: nc.scalar.copy(pT[:], ps_t[:]), reads=[ps_t], writes=[pT])
                        for g in range(8):
                            for hf in range(2):
                                ctx.op("pe", lambda: nc.tensor.matmul(ps_o[po:po + 64, g, :], lhsT=vS[:, n + hf, hk * 64:(hk + 1) * 64],
                                                                      rhs=pT[:, g * 2 + hf, :], start=(hf == 0), stop=(hf == 1)),
                                       reads=[vS, pT], writes=[ps_o])
                    ctx.op("dve", lambda: nc.vector.tensor_copy(og[:, pair * 8:(pair + 1) * 8, (n % G) * P:(n % G + 1) * P], ps_o[:]),
                           reads=[ps_o], writes=[og])
                if n % G == G - 1:
                    g0 = (n // G) * G * P
                    ctx.dma("sp", OTv[:, :, g0:g0 + G * P], og[:], reads=[og], writes=[("OT", n)])
            ctx.barrier()
    with contextlib.ExitStack() as st:
        aT = self.load_AT(st, "oTa", OT, KC, 0, T)
        res = GemmRes(self, st, KC, 512, 3)
        epi = self.epi_resid(st, X, Z1)
        self.gemm_tok(res, aT, aT, KC, w_out, 0, D, range(NT), epi)
        ctx.barrier()


Prog.swa_layer = _swa_layer
RW_H, RW_HD = 32, 64
RW_EPS = 64e-5
RW_NHB = RW_H // 2


def _rwkv_consts(cfg, core, c):
    seg = core % cfg.NSEG
    f32 = np.float32
    s = np.arange(P)[:, None]
    t = np.arange(P)[None, :]
    c["rw_mus"] = (s < t).astype(f32)
    c["rw_mui"] = (s <= t).astype(f32)
    c["rw_mls"] = (s > t).astype(f32)
    c["rw_tri"] = (s <= t).astype(f32)
    c["rw_suf"] = (s > t).astype(f32)
    i2 = np.zeros((P, 64), f32)
    i2[np.arange(P), np.arange(P) % 64] = 1.0
    c["rw_i2"] = i2
    selm = np.zeros((P, cfg.NSEG), f32)
    selm[:, :seg] = 1.0
    c["rw_selm"] = selm
    c["rw_nselm"] = 1.0 - selm


def _rwkv_layer(self, layer, X, XT, Z1):
    ctx, nc, cfg = self.ctx, self.nc, self.cfg
    T, NT, NSEG = cfg.T, cfg.NT, cfg.NSEG
    j = layer // 3
    gi = lambda name, shape: self.inp("%s_%d" % (name, j), shape).ap()
    tw_ = lambda nm, fn, K_, N_, t_, kp_=P: self.tw("%s_t%d" % (nm, j), fn, K_, N_, t_, kp_)
    w_rkv = [tw_("rwkv_w_rkv%d" % i, (lambda inp, i=i: inp["rwkv_w_rkv"][j][i]), D, D, 256) for i in range(3)]
    w1 = tw_("rwkv_w1", lambda inp: inp["rwkv_w1"][j], D, 96, 96)
    w2 = tw_("rwkv_w2", lambda inp: inp["rwkv_w2"][j], 96, D, 256, 96)
    a1 = tw_("rwkv_a1", lambda inp: inp["rwkv_a1"][j], D, 96, 96)
    a2 = tw_("rwkv_a2", lambda inp: inp["rwkv_a2"][j], 96, D, 256, 96)
    g1 = tw_("rwkv_g1", lambda inp: inp["rwkv_g1"][j], D, 256, 256)
    g2 = tw_("rwkv_g2", lambda inp: inp["rwkv_g2"][j], 256, D, 256)
    w_out = tw_("rwkv_w_out", lambda inp: inp["rwkv_w_out"][j], D, D, 512)
    mix_ap = gi("rwkv_mix", [6, D]).rearrange("i (kc p) -> (i kc) p", p=P)
    Rs, Ks, Vs = self.scr("rw_R", [T, D]), self.scr("rw_K", [T, D]), self.scr("rw_V", [T, D])
    WLs, ALs, Gs = self.scr("rw_WL", [T, D]), self.scr("rw_AL", [T, D]), self.scr("rw_G", [T, D])
    Y0 = self.scr("rw_Y0", [T, D])
    BON = self.scr("rw_BON", [T, RW_H])
    YTR = self.scr("rw_YTR", [NT, P, RW_NHB, P], BF16)
    OGT = self.scr("rw_OGT", [D, T], BF16)
    TGW = min(512, T)
    tgs = [(t0, TGW) for t0 in range(0, T, TGW)]

    with contextlib.ExitStack() as st:
        psc = ctx.ps(st, "psc", [P, 512], F32)
        mixc = self.load_cols(st, "mixc", mix_ap, 6 * KC, psc)
        xT = self.load_AT(st, "xT", XT, KC, 0, T, pad=1)
        with contextlib.ExitStack() as sth:
            halo = self.halo_rows(sth, X, 1, "halo1")
            for kc in range(KC):
                ctx.op("pe", lambda: nc.tensor.matmul(psc[:, kc:kc + 1], lhsT=halo[0:1, kc * P:(kc + 1) * P],
                                                      rhs=self.identf[0:1, 0:1], start=True, stop=True),
                       reads=[halo, self.identf], writes=[psc])
            ctx.op("dve", lambda: nc.vector.tensor_copy(xT[:, :, 0:1], psc[:, 0:KC].unsqueeze(2)), reads=[psc], writes=[xT])
            ctx.barrier()
        xm = ctx.sb(st, "xm", [P, KC, T], BF16)
        dtmp = [ctx.sb(st, "dtmp", [P, T], F32) for _ in range(2)]
        hT = ctx.sb(st, "hT", [P, 2, T], BF16)
        res = GemmRes(self, st, KC, 256, 4)
        obuf = [ctx.sb(st, "obuf", [P, 512], F32) for _ in range(3)]
        ocnt = [0]

        def store_epi(dst):
            def epi(ps, tt, c0, nb):
                o = obuf[ocnt[0] % 3]
                ocnt[0] += 1
                ctx.op("act", lambda: nc.scalar.copy(o[:, 0:nb], ps[:, 0:nb]), reads=[ps], writes=[o])
                ctx.dma("sp", dst.ap()[tt * P:(tt + 1) * P, c0:c0 + nb], o[:, 0:nb], reads=[o], writes=[("o", id(dst), tt, c0)])
            return epi

        def build_mix(i):
            for kc in range(KC):
                d = dtmp[kc % 2]
                ctx.op("pool", lambda: nc.gpsimd.tensor_tensor(d[:], xT[:, kc, 0:T], xT[:, kc, 1:T + 1], ALU.subtract),
                       reads=[xT], writes=[d])
                ctx.op("dve", lambda: nc.vector.scalar_tensor_tensor(xm[:, kc, :], d[:], mixc[:, i * KC + kc:i * KC + kc + 1],
                                                                     xT[:, kc, 1:T + 1], ALU.mult, ALU.add),
                       reads=[d, mixc, xT], writes=[xm])

        def lora(i, wa, na, func, wb_, dst):
            build_mix(i)
            kcn2 = (na + P - 1) // P
            kp = min(P, na)

            def h_epi(ps, c, t0, tn):
                fw = min(P, na - c)
                ctx.op("act", lambda: nc.scalar.activation(hT[0:fw, c // P, t0:t0 + tn], ps[0:fw, 0:tn], func),
                       reads=[ps], writes=[hT])
            self.gemm_feat(res, xm, xm, KC, wa, 0, na, tgs, h_epi, nblk=256)
            self.gemm_tok(res, hT, hT, kcn2, wb_, 0, D, range(NT), store_epi(dst), kp=kp, nblk=256)

        build_mix(0)
        self.gemm_tok(res, xm, xm, KC, w_rkv[0], 0, D, range(NT), store_epi(Rs), nblk=256)
        build_mix(2)
        self.gemm_tok(res, xm, xm, KC, w_rkv[1], 0, D, range(NT), store_epi(Ks), nblk=256)
        build_mix(3)
        self.gemm_tok(res, xm, xm, KC, w_rkv[2], 0, D, range(NT), store_epi(Vs), nblk=256)
        lora(1, w1, 96, AF.Tanh, w2, WLs)
        lora(4, a1, 96, AF.Identity, a2, ALs)
        lora(5, g1, 256, AF.Sigmoid, g2, Gs)
        ctx.barrier()
    if cfg.stop == "rw1":
        return

    SXs = self.scr("rw_SX", [P, RW_NHB, P])
    with contextlib.ExitStack() as st:
        def cload(name, shape, dtype=F32):
            t_ = ctx.sb(st, name, shape, dtype)
            ctx.dma("sp", t_[:], self.inp(name, shape, dtype).ap(), writes=[t_])
            return t_
        mus, mui, mls = cload("rw_mus", [P, P]), cload("rw_mui", [P, P]), cload("rw_mls", [P, P])
        tri, suft = cload("rw_tri", [P, P]), cload("rw_suf", [P, P])
        i2 = cload("rw_i2", [P, 64])
        ones = ctx.sb(st, "ones", [P, 1], F32)
        ctx.op("dve", lambda: nc.vector.memset(ones[:], 1.0), writes=[ones])
        w0b = self.bcast_rows(st, "w0b", gi("rwkv_w0", [D]), D)
        a0b = self.bcast_rows(st, "a0b", gi("rwkv_a0", [D]), D)
        kkb = self.bcast_rows(st, "kkb", gi("rwkv_k_k", [D]), D)
        kab = self.bcast_rows(st, "kab", gi("rwkv_k_a", [D]), D)
        rkb = self.bcast_rows(st, "rkb", gi("rwkv_r_k", [RW_H, RW_HD]).rearrange("h d -> (h d)"), D)
        A = ctx.sb(st, "A", [P, D], F32)
        B = ctx.sb(st, "B", [P, D], F32)
        Dw = ctx.sb(st, "Dw", [P, D], F32)
        Ea = ctx.sb(st, "Ea", [P, D], F32)
        Fk = ctx.sb(st, "Fk", [P, D], F32)
        T1 = ctx.sb(st, "T1", [P, D], F32)
        ET = [ctx.sb(st, "ET", [P, 512], F32) for _ in range(4)]
        tok = [ctx.sb(st, "tokb", [P, D], BF16) for _ in range(4)]
        bbk = ctx.sb(st, "bbk", [P, 2, D], BF16)
        vbx = ctx.sb(st, "vbx", [P, RW_H, P], BF16)
        ctx.op("pool", lambda: nc.gpsimd.memset(vbx[:], 0.0), writes=[vbx])
        CM = ctx.sb(st, "CM", [P, RW_NHB, 4, P], BF16)
        ytile = ctx.sb(st, "ytile", [P, D], F32)
        T2 = ytile
        ytr = ctx.sb(st, "ytr", [P, RW_NHB, P], BF16)
        ss = ctx.sb(st, "ss", [P, RW_H], F32)
        bon = ctx.sb(st, "bon", [P, RW_H], F32)
        dectot = ctx.sb(st, "dectot", [P, RW_NHB], F32)
        SX = ctx.sb(st, "SX", [P, RW_NHB, P], F32)
        SXb = ctx.sb(st, "SXb", [P, RW_NHB, P], BF16)
        NPAIR = 3

        class PairRes:
            pass
        PR = []
        for _ in range(NPAIR):
            r_ = PairRes()
            r_.GA = [ctx.sb(st, "GA", [P, 2, 2, P], BF16) for _ in range(2)]
            r_.Brb = ctx.sb(st, "Brb", [P, 2, P], BF16)
            r_.Aak = ctx.sb(st, "Aak", [P, 2, P], BF16)
            r_.Brk = ctx.sb(st, "Brk", [P, 2, P], BF16)
            r_.Tt = [ctx.sb(st, "Tt", [P, 2, P], BF16) for _ in range(2)]
            r_.Wb = ctx.sb(st, "Wb", [P, 2, P], BF16)
            r_.Ub = ctx.sb(st, "Ub", [P, 2, P], BF16)
            r_.H = [ctx.ps(st, "pbH", [P, 512], F32) for _ in range(2)]
            PR.append(r_)
        pbx = ctx.ps(st, "pbx", [P, 512], F32)
        for i_, r_ in enumerate(PR):
            r_.S = pbx
            r_.so = i_ * P
        pb = [PR[0].H[0], PR[0].H[1], PR[1].H[0], PR[1].H[1], PR[2].H[0]]
        ptr = ctx.ps(st, "ptr", [P, 8, P], BF16)
        sxk = [(id(SX), hb) for hb in range(RW_NHB)]
        sxbk = [(id(SXb), hb) for hb in range(RW_NHB)]
        ctx.op("dve", lambda: nc.vector.memset(SX[:], 0.0), writes=sxk)
        ctx.op("dve", lambda: nc.vector.tensor_copy(SX[:, :, 64:128], i2[:].unsqueeze(1).broadcast_to([P, RW_NHB, 64])),
               reads=[i2] + sxk, writes=sxk)
        ctx.op("pool", lambda: nc.gpsimd.tensor_copy(SXb[:], SX[:]), reads=sxk, writes=sxbk)
        v3 = lambda t_: t_[:].rearrange("p (h d) -> p h d", d=RW_HD)
        bc3 = lambda small: small[:].unsqueeze(2).broadcast_to([P, RW_H, RW_HD])
        for n in range(NT):
            rows = slice(n * P, (n + 1) * P)
            ctx.dma("sp", A[:], Rs.ap()[rows, :], writes=[A])
            ctx.dma("sp", B[:], Ks.ap()[rows, :], writes=[B])
            ctx.dma("sp", T1[:], Vs.ap()[rows, :], writes=[T1])
            ctx.op("act", lambda: nc.scalar.copy(vbx[:, :, 0:64], v3(T1)), reads=[T1], writes=[vbx])
            ctx.dma("sp", Dw[:], WLs.ap()[rows, :], writes=[Dw])
            ctx.dma("sp", Ea[:], ALs.ap()[rows, :], writes=[Ea])
            ctx.op("dve", lambda: nc.vector.tensor_tensor(Dw[:], Dw[:], w0b[:], ALU.add), reads=[Dw, w0b], writes=[Dw])
            ctx.op("act", lambda: nc.scalar.activation(Dw[:], Dw[:], AF.Sigmoid), reads=[Dw], writes=[Dw])
            ctx.op("dve", lambda: nc.vector.tensor_scalar_mul(Dw[:], Dw[:], -math.exp(-0.5)), reads=[Dw], writes=[Dw])
            ctx.op("dve", lambda: nc.vector.tensor_tensor(Ea[:], Ea[:], a0b[:], ALU.add), reads=[Ea, a0b], writes=[Ea])
            ctx.op("act", lambda: nc.scalar.activation(Ea[:], Ea[:], AF.Sigmoid), reads=[Ea], writes=[Ea])
            ctx.op("pool", lambda: nc.gpsimd.tensor_tensor(Fk[:], B[:], kkb[:], ALU.mult), reads=[B, kkb], writes=[Fk])
            ctx.op("pool", lambda: nc.gpsimd.tensor_tensor(T2[:], Fk[:], Fk[:], ALU.mult), reads=[Fk], writes=[T2])
            ctx.op("dve", lambda: nc.vector.tensor_reduce(ss[:], v3(T2), AX.X, ALU.add), reads=[T2], writes=[ss])
            ctx.op("act", lambda: nc.scalar.activation(ss[:], ss[:], AF.Sqrt), reads=[ss], writes=[ss])
            ctx.op("dve", lambda: nc.vector.tensor_scalar_max(ss[:], ss[:], 1e-12), reads=[ss], writes=[ss])
            ctx.op("dve", lambda: nc.vector.reciprocal(ss[:], ss[:]), reads=[ss], writes=[ss])
            ctx.op("dve", lambda: nc.vector.tensor_tensor(v3(Fk), v3(Fk), bc3(ss), ALU.mult), reads=[Fk, ss], writes=[Fk])
            ctx.op("dve", lambda: nc.vector.scalar_tensor_tensor(T1[:], Ea[:], -1.0, kab[:], ALU.add, ALU.mult),
                   reads=[Ea, kab], writes=[T1])
            ctx.op("pool", lambda: nc.gpsimd.tensor_tensor(T1[:], T1[:], B[:], ALU.mult), reads=[T1, B], writes=[T1])
            ctx.op("pool", lambda: nc.gpsimd.tensor_tensor(B[:], B[:], T1[:], ALU.add), reads=[T1, B], writes=[B])
            ctx.op("pool", lambda: nc.gpsimd.tensor_tensor(T2[:], A[:], B[:], ALU.mult), reads=[A, B, T2], writes=[T2])
            ctx.op("dve", lambda: nc.vector.tensor_tensor(T2[:], T2[:], rkb[:], ALU.mult), reads=[T2, rkb], writes=[T2])
            ctx.op("dve", lambda: nc.vector.tensor_reduce(bon[:], v3(T2), AX.X, ALU.add), reads=[T2], writes=[bon])
            ctx.dma("sp", BON.ap()[rows, :], bon[:], reads=[bon], writes=[("BON", n)])
            ctx.op("dve", lambda: nc.vector.tensor_tensor(T1[:], Fk[:], Ea[:], ALU.mult), reads=[Fk, Ea, T1], writes=[T1])
            for hb in range(RW_NHB):
                ctx.op("pe", lambda: nc.tensor.matmul(pbx[:, 448 + hb:448 + hb + 1], lhsT=Dw[:, hb * P:(hb + 1) * P], rhs=ones[:, 0:1],
                                                      start=True, stop=True), reads=[Dw, ones], writes=[pbx])
            ctx.op("act", lambda: nc.scalar.activation(dectot[:], pbx[:, 448:448 + RW_NHB], AF.Exp), reads=[pbx], writes=[dectot])
            for cb in range(4):
                cs = slice(cb * 512, (cb + 1) * 512)
                pcum, psuf = pb[1 + (cb % 2) * 2], pb[2 + (cb % 2) * 2]
                ctx.op("pe", lambda: nc.tensor.matmul(pcum[:], lhsT=tri[:], rhs=Dw[:, cs], start=True, stop=True),
                       reads=[tri, Dw], writes=[pcum])
                ctx.op("pe", lambda: nc.tensor.matmul(psuf[:], lhsT=suft[:], rhs=Dw[:, cs], start=True, stop=True),
                       reads=[suft, Dw], writes=[psuf])
                ctx.op("act", lambda: nc.scalar.activation(ET[0][:], pcum[:], AF.Exp), reads=[pcum], writes=[ET[0]])
                ctx.op("pool", lambda: nc.gpsimd.tensor_tensor(tok[3][:, cs], A[:, cs], ET[0][:], ALU.mult),
                       reads=[A, ET[0]], writes=[tok[3]])
                ctx.op("act", lambda: nc.scalar.activation(ET[1][:], pcum[:], AF.Exp, scale=-1.0), reads=[pcum], writes=[ET[1]])
                ctx.op("dve", lambda: nc.vector.tensor_tensor(tok[0][:, cs], T1[:, cs], ET[1][:], ALU.mult),
                       reads=[T1, ET[1]], writes=[tok[0]])
                ctx.op("pool", lambda: nc.gpsimd.tensor_tensor(tok[1][:, cs], B[:, cs], ET[1][:], ALU.mult),
                       reads=[B, ET[1]], writes=[tok[1]])
                ctx.op("dve", lambda: nc.vector.tensor_tensor(ET[2][:], pcum[:], Dw[:, cs], ALU.subtract),
                       reads=[pcum, Dw], writes=[ET[2]])
                ctx.op("act", lambda: nc.scalar.activation(ET[2][:], ET[2][:], AF.Exp), reads=[ET[2]], writes=[ET[2]])
                ctx.op("dve", lambda: nc.vector.scalar_tensor_tensor(tok[2][:, cs], Fk[:, cs], -1.0, ET[2][:], ALU.mult, ALU.mult),
                       reads=[Fk, ET[2]], writes=[tok[2]])
                ctx.op("act", lambda: nc.scalar.activation(ET[3][:], psuf[:], AF.Exp), reads=[psuf], writes=[ET[3]])
                ctx.op("dve", lambda: nc.vector.tensor_tensor(bbk[:, 0, cs], T1[:, cs], ET[3][:], ALU.mult),
                       reads=[T1, ET[3]], writes=[bbk])
                ctx.op("pool", lambda: nc.gpsimd.tensor_tensor(bbk[:, 1, cs], B[:, cs], ET[3][:], ALU.mult),
                       reads=[B, ET[3]], writes=[bbk])
            for kind in range(4):
                for half in range(2):
                    for jj in range(8):
                        hb = half * 8 + jj
                        self.transpose_to(tok[kind][:, hb * P:(hb + 1) * P], ptr[:, jj, :], [tok[kind]], [ptr])
                    if (kind + half) % 2 == 0:
                        ctx.op("act", lambda: nc.scalar.copy(CM[:, half * 8:(half + 1) * 8, kind, :], ptr[:]), reads=[ptr], writes=[CM])
                    else:
                        ctx.op("dve", lambda: nc.vector.tensor_copy(CM[:, half * 8:(half + 1) * 8, kind, :], ptr[:]), reads=[ptr], writes=[CM])
            def pair_gen(hb, R):
                H = R.H
                g0 = R.GA[0]
                hp = ((0, 0), (1, 64))
                for hi, po in hp:
                    rhs_ar = CM[po:po + 64, hb, 2:4, :].rearrange("p k t -> p (k t)")
                    ctx.op("pe", lambda: nc.tensor.matmul(H[hi][:, 0:256], lhsT=CM[po:po + 64, hb, 0, :], rhs=rhs_ar,
                                                          start=True, stop=True), reads=[CM], writes=[H[hi]])
                    ctx.op("pe", lambda: nc.tensor.matmul(H[hi][:, 256:384], lhsT=CM[po:po + 64, hb, 2, :],
                                                          rhs=CM[po:po + 64, hb, 0, :], start=True, stop=True),
                           reads=[CM], writes=[H[hi]])
                yield
                for hi, po in hp:
                    ctx.op("dve", lambda: nc.vector.tensor_tensor(g0[:, hi, 0, :], H[hi][:, 0:P], mus[:], ALU.mult),
                           reads=[H[hi], mus], writes=[g0])
                    ctx.op("dve", lambda: nc.vector.tensor_tensor(R.Brb[:, hi, :], H[hi][:, P:2 * P], mui[:], ALU.mult),
                           reads=[H[hi], mui], writes=[R.Brb])
                    ctx.op("dve", lambda: nc.vector.tensor_tensor(g0[:, hi, 1, :], H[hi][:, 2 * P:3 * P], mls[:], ALU.mult),
                           reads=[H[hi], mls], writes=[g0])
                    ctx.op("pool", lambda: nc.gpsimd.tensor_tensor(R.Tt[0][:, hi, :], g0[:, hi, 0, :], self.ident[:], ALU.add),
                           reads=[g0, self.ident], writes=[R.Tt[0]])
                yield
                if cfg.stop == "g1":
                    return
                tcur = 0
                for lvl in range(1, 7):
                    gc, gn = R.GA[(lvl - 1) % 2], R.GA[lvl % 2]
                    for hi, po in hp:
                        if lvl < 6:
                            ctx.op("pe", lambda: nc.tensor.matmul(H[hi][:, 0:P], lhsT=gc[:, hi, 1, :], rhs=gc[:, hi, 0, :],
                                                                  start=True, stop=True), reads=[gc], writes=[H[hi]])
                        ctx.op("pe", lambda: nc.tensor.matmul(H[hi][:, P:2 * P], lhsT=gc[:, hi, 0, :], rhs=gc[:, hi, 1, :],
                                                              start=True, stop=True), reads=[gc], writes=[H[hi]])
                    yield
                    for hi, po in hp:
                        if lvl < 6:
                            ctx.op("act", lambda: nc.scalar.copy(gn[:, hi].rearrange("p k t -> p (k t)"), H[hi][:, 0:2 * P]),
                                   reads=[H[hi]], writes=[gn])
                        else:
                            ctx.op("act", lambda: nc.scalar.copy(gn[:, hi, 1, :], H[hi][:, P:2 * P]), reads=[H[hi]], writes=[gn])
                    yield
                    for hi, po in hp:
                        ctx.op("pe", lambda: nc.tensor.matmul(H[hi][:, 2 * P:3 * P], lhsT=gn[:, hi, 1, :], rhs=R.Tt[tcur][:, hi, :],
                                                              start=True, stop=True), reads=[gn, R.Tt[tcur]], writes=[H[hi]])
                    yield
                    for hi, po in hp:
                        ctx.op("dve", lambda: nc.vector.tensor_tensor(R.Tt[1 - tcur][:, hi, :], H[hi][:, 2 * P:3 * P], R.Tt[tcur][:, hi, :],
                                                                      ALU.add), reads=[H[hi], R.Tt[tcur]], writes=[R.Tt[1 - tcur]])
                    tcur = 1 - tcur
                    yield
                TT = R.Tt[tcur]
                if cfg.stop == "g2":
                    return
                for hi, po in hp:
                    rhs_ar = CM[po:po + 64, hb, 2:4, :].rearrange("p k t -> p (k t)")
                    ctx.op("pe", lambda: nc.tensor.matmul(H[hi][:, 0:256], lhsT=CM[po:po + 64, hb, 1, :], rhs=rhs_ar,
                                                          start=True, stop=True), reads=[CM], writes=[H[hi]])
                    ctx.op("pe", lambda: nc.tensor.matmul(H[hi][:, 2 * P:3 * P], lhsT=CM[po:po + 64, hb, 2, :],
                                                          rhs=SXb[po:po + 64, hb, :], start=True, stop=False),
                           reads=[CM, (id(SXb), hb)], writes=[H[hi]])
                yield
                for hi, po in hp:
                    ctx.op("dve", lambda: nc.vector.tensor_tensor(R.Aak[:, hi, :], H[hi][:, 0:P], mus[:], ALU.mult),
                           reads=[H[hi], mus], writes=[R.Aak])
                    ctx.op("dve", lambda: nc.vector.tensor_tensor(R.Brk[:, hi, :], H[hi][:, P:2 * P], mui[:], ALU.mult),
                           reads=[H[hi], mui], writes=[R.Brk])
                yield
                for hi, po in hp:
                    h = 2 * hb + hi
                    ctx.op("pe", lambda: nc.tensor.matmul(H[hi][:, 2 * P:3 * P], lhsT=R.Aak[:, hi, :], rhs=vbx[:, h, :],
                                                          start=False, stop=True), reads=[R.Aak, vbx], writes=[H[hi]])
                yield
                for hi, po in hp:
                    ctx.op("act", lambda: nc.scalar.copy(R.Wb[:, hi, :], H[hi][:, 2 * P:3 * P]), reads=[H[hi]], writes=[R.Wb])
                yield
                for hi, po in hp:
                    ctx.op("pe", lambda: nc.tensor.matmul(H[hi][:, 3 * P:4 * P], lhsT=TT[:, hi, :], rhs=R.Wb[:, hi, :],
                                                          start=True, stop=True), reads=[TT, R.Wb], writes=[H[hi]])
                yield
                for hi, po in hp:
                    ctx.op("dve", lambda: nc.vector.tensor_copy(R.Ub[:, hi, :], H[hi][:, 3 * P:4 * P]), reads=[H[hi]], writes=[R.Ub])
                yield
                if cfg.stop == "g3":
                    return
                for hi, po in hp:
                    h = 2 * hb + hi
                    yo = H[hi][:, 0:64]
                    ctx.op("pe", lambda: nc.tensor.matmul(yo, lhsT=CM[po:po + 64, hb, 3, :], rhs=SXb[po:po + 64, hb, 0:64],
                                                          start=True, stop=False), reads=[CM, (id(SXb), hb)], writes=[H[hi]])
                    ctx.op("pe", lambda: nc.tensor.matmul(yo, lhsT=R.Brb[:, hi, :], rhs=R.Ub[:, hi, 0:64], start=False, stop=False),
                           reads=[R.Brb, R.Ub], writes=[H[hi]])
                    ctx.op("pe", lambda: nc.tensor.matmul(yo, lhsT=R.Brk[:, hi, :], rhs=vbx[:, h, 0:64], start=False, stop=True),
                           reads=[R.Brk, vbx], writes=[H[hi]])
                    if NSEG > 1:
                        to = H[hi][po:po + 64, P:2 * P]
                        ctx.op("pe", lambda: nc.tensor.matmul(to, lhsT=SXb[po:po + 64, hb, 64:128], rhs=CM[po:po + 64, hb, 3, :],
                                                              start=True, stop=False), reads=[CM, (id(SXb), hb)], writes=[H[hi]])
                        ctx.op("pe", lambda: nc.tensor.matmul(to, lhsT=R.Ub[:, hi, 64:128], rhs=R.Brb[:, hi, :], start=False, stop=True),
                               reads=[R.Brb, R.Ub], writes=[H[hi]])
                    so = R.S[po:po + 64, R.so:R.so + P]
                    ctx.op("pe", lambda: nc.tensor.matmul(so, lhsT=bbk[:, 0, h * 64:(h + 1) * 64], rhs=R.Ub[:, hi, :], start=True, stop=False),
                           reads=[bbk, R.Ub], writes=[R.S])
                    ctx.op("pe", lambda: nc.tensor.matmul(so, lhsT=bbk[:, 1, h * 64:(h + 1) * 64], rhs=vbx[:, h, :], start=False, stop=True),
                           reads=[bbk, vbx], writes=[R.S])
                yield
                for hi, po in hp:
                    ctx.op("act", lambda: nc.scalar.copy(ytile[:, hb * P + hi * 64:hb * P + (hi + 1) * 64], H[hi][:, 0:64]),
                           reads=[H[hi]], writes=[(id(ytile), hb)])
                    if NSEG > 1:
                        ctx.op("act", lambda: nc.scalar.copy(ytr[po:po + 64, hb, :], H[hi][po:po + 64, P:2 * P]),
                               reads=[H[hi]], writes=[(id(ytr), hb)])
                ctx.op("dve", lambda: nc.vector.scalar_tensor_tensor(SX[:, hb, :], SX[:, hb, :], dectot[:, hb:hb + 1],
                                                                     R.S[:, R.so:R.so + P], ALU.mult, ALU.add),
                       reads=[(id(SX), hb), dectot, R.S], writes=[(id(SX), hb)])
                ctx.op("pool", lambda: nc.gpsimd.tensor_copy(SXb[:, hb, :], SX[:, hb, :]), reads=[(id(SX), hb)], writes=[(id(SXb), hb)])
                yield

            if cfg.stop != "rw2p":
                for g0_ in range(0, RW_NHB, NPAIR):
                    gens = [pair_gen(hb, PR[i]) for i, hb in enumerate(range(g0_, min(RW_NHB, g0_ + NPAIR)))]
                    while gens:
                        for g_ in list(gens):
                            try:
                                next(g_)
                            except StopIteration:
                                gens.remove(g_)
            all_hb = list(range(RW_NHB))
            ctx.dma("sp", Y0.ap()[rows, :], ytile[:], reads=[ytile] + [(id(ytile), hb) for hb in all_hb], writes=[("Y0", n)])
            if NSEG > 1:
                ctx.dma("sp", YTR.ap()[n], ytr[:], reads=[ytr] + [(id(ytr), hb) for hb in all_hb], writes=[("YTR", n)])
        if NSEG > 1:
            ctx.dma("sp", SXs.ap(), SX[:], reads=[SX] + [(id(SX), hb) for hb in range(RW_NHB)], writes=[("SXs",)])
        ctx.barrier()

    if cfg.stop in ("rw2", "rw2p", "g1", "g2", "g3"):
        return
    S0b = ctx.sb(self.top, "rw_S0b_%d" % layer, [P, RW_NHB, 64], BF16)
    if NSEG > 1:
        NSL = NSEG - 1
        CI = self.scr("rw_ci", [NSL, P, RW_NHB * P])
        CO = self.scr("rw_co", [NSL, P, RW_NHB * P])
        with contextlib.ExitStack() as st:
            sx = ctx.sb(st, "sx", [P, RW_NHB * P], F32)
            sm = [ctx.sb(st, "sm", [P, RW_NHB * P], F32) for _ in range(2)]
            ctx.dma("sp", sx[:], SXs.ap().rearrange("p h k -> p (h k)"), writes=[sx])
            for s in range(NSL):
                ctx.op("dve", lambda: nc.vector.tensor_scalar_mul(sm[s % 2][:], sx[:], self.own[:, s:s + 1]),
                       reads=[sx, self.own], writes=[sm[s % 2]])
                ctx.dma("sp", CI.ap()[s], sm[s % 2][:], reads=[sm[s % 2]], writes=[("CI", s)])
            ctx.barrier()
            ctx.allreduce(cfg.groups, CI.ap().rearrange("s p e -> (s p) e"), CO.ap().rearrange("s p e -> (s p) e"),
                          writes=[("CO",)])
            ctx.barrier()
            selm = ctx.sb(st, "selm", [P, NSEG], F32)
            nselm = ctx.sb(st, "nselm", [P, NSEG], F32)
            ctx.dma("sp", selm[:], self.inp("rw_selm", [P, NSEG]).ap(), writes=[selm])
            ctx.dma("sp", nselm[:], self.inp("rw_nselm", [P, NSEG]).ap(), writes=[nselm])
            i2 = ctx.sb(st, "i2", [P, 64], F32)
            ctx.dma("sp", i2[:], self.inp("rw_i2", [P, 64]).ap(), writes=[i2])
            S0 = ctx.sb(st, "S0", [P, RW_NHB, 64], F32)
            ctx.op("dve", lambda: nc.vector.memset(S0[:], 0.0), writes=[S0])
            Mp = ctx.sb(st, "Mp", [P, RW_NHB, 64], F32)
            Lp = ctx.sb(st, "Lp", [P, RW_NHB, 64], F32)
            MT = ctx.sb(st, "MT", [P, RW_NHB, 64], F32)
            pm = [ctx.ps(st, "pm", [P, 8, 64], F32) for _ in range(2)]
            for s in range(NSL):
                slot = sm[s % 2]
                ctx.dma("sp", slot[:], CO.ap()[s], writes=[slot])
                sv = slot[:].rearrange("p (h k) -> p h k", k=P)
                ctx.op("dve", lambda: nc.vector.tensor_scalar_mul(Lp[:], sv[:, :, 0:64], selm[:, s:s + 1]),
                       reads=[slot, selm], writes=[Lp])
                ctx.op("dve", lambda: nc.vector.tensor_scalar_mul(Mp[:], sv[:, :, 64:128], selm[:, s:s + 1]),
                       reads=[slot, selm], writes=[Mp])
                ctx.op("dve", lambda: nc.vector.scalar_tensor_tensor(Mp[:], i2[:].unsqueeze(1).broadcast_to([P, RW_NHB, 64]),
                                                                     nselm[:, s:s + 1], Mp[:], ALU.mult, ALU.add),
                       reads=[i2, nselm, Mp], writes=[Mp])
                for half in range(2):
                    for jj in range(8):
                        hb = half * 8 + jj
                        for hi, po in enumerate((0, 64)):
                            ctx.op("pe", lambda: nc.tensor.matmul(pm[hi][po:po + 64, jj, :], lhsT=Mp[po:po + 64, hb, :],
                                                                  rhs=self.identf[po:po + 64, po:po + 64], start=True, stop=True),
                                   reads=[Mp, self.identf], writes=[pm[hi]])
                    for hi, po in enumerate((0, 64)):
                        ctx.op("act", lambda: nc.scalar.copy(MT[po:po + 64, half * 8:(half + 1) * 8, :], pm[hi][po:po + 64]),
                               reads=[pm[hi]], writes=[MT])
                for half in range(2):
                    for jj in range(8):
                        hb = half * 8 + jj
                        for hi, po in enumerate((0, 64)):
                            ctx.op("pe", lambda: nc.tensor.matmul(pm[hi][po:po + 64, jj, :], lhsT=MT[po:po + 64, hb, :],
                                                                  rhs=S0[po:po + 64, hb, :], start=True, stop=True),
                                   reads=[MT, S0], writes=[pm[hi]])
                    for hi, po in enumerate((0, 64)):
                        ctx.op("dve", lambda: nc.vector.tensor_tensor(S0[po:po + 64, half * 8:(half + 1) * 8, :], pm[hi][po:po + 64],
                                                                      Lp[po:po + 64, half * 8:(half + 1) * 8, :], ALU.add),
                               reads=[pm[hi], Lp, S0], writes=[S0])
            ctx.op("act", lambda: nc.scalar.copy(S0b[:], S0[:]), reads=[S0], writes=[S0b])
            ctx.barrier()

    if cfg.stop == "rw3":
        return
    with contextlib.ExitStack() as st:
        gnb = self.bcast_rows(st, "gnb", gi("rwkv_gn_gain", [D]), D)
        gbb = self.bcast_rows(st, "gbb", gi("rwkv_gn_bias", [D]), D)
        y = ctx.sb(st, "y", [P, D], F32)
        vv = ctx.sb(st, "vv", [P, D], F32)
        gg = ctx.sb(st, "gg", [P, D], F32)
        sq = ctx.sb(st, "sq", [P, D], F32)
        bon = ctx.sb(st, "bon", [P, RW_H], F32)
        s1 = ctx.sb(st, "s1", [P, RW_H], F32)
        s2 = ctx.sb(st, "s2", [P, RW_H], F32)
        ytr = ctx.sb(st, "ytr", [P, RW_NHB, P], BF16)
        ogb = [ctx.sb(st, "ogb", [P, D], BF16) for _ in range(2)]
        G = 4 if NT % 4 == 0 else 2
        stg = [ctx.sb(st, "stg", [P, KC, G * P], BF16) for _ in range(2)]
        pst = [[ctx.ps(st, "pst", [P, 8 * P], BF16) for _ in range(2)] for _ in range(2)]
        pc = [ctx.ps(st, "pc", [P, 512], F32) for _ in range(4)]
        OGTv = OGT.ap().rearrange("(kc p) t -> p kc t", p=P)
        v3 = lambda t_: t_[:].rearrange("p (h d) -> p h d", d=RW_HD)
        bc3 = lambda small: small[:].unsqueeze(2).broadcast_to([P, RW_H, RW_HD])
        for n in range(NT):
            rows = slice(n * P, (n + 1) * P)
            ctx.dma("sp", y[:], Y0.ap()[rows, :], writes=[y])
            ctx.dma("sp", vv[:], Vs.ap()[rows, :], writes=[vv])
            ctx.dma("sp", gg[:], Gs.ap()[rows, :], writes=[gg])
            ctx.dma("sp", bon[:], BON.ap()[rows, :], writes=[bon])
            if NSEG > 1:
                ctx.dma("sp", ytr[:], YTR.ap()[n], writes=[ytr])
                for q4 in range(4):
                    for jj in range(4):
                        hb = q4 * 4 + jj
                        for hi, po in enumerate((0, 64)):
                            pcc = pc[2 * (q4 % 2) + hi]
                            ctx.op("pe", lambda: nc.tensor.matmul(pcc[:, jj * 64:(jj + 1) * 64],
                                                                  lhsT=ytr[po:po + 64, hb, :], rhs=S0b[po:po + 64, hb, :],
                                                                  start=True, stop=True), reads=[ytr, S0b], writes=[pcc])
                    for hi in range(2):
                        pcc = pc[2 * (q4 % 2) + hi]
                        yv = y[:, q4 * 512:(q4 + 1) * 512].rearrange("p (j k) -> p j k", k=P)[:, :, hi * 64:(hi + 1) * 64]
                        ctx.op("dve", lambda: nc.vector.tensor_tensor(yv, yv, pcc[:, 0:256].rearrange("p (j k) -> p j k", k=64), ALU.add),
                               reads=[pcc, y], writes=[y])
            ctx.op("dve", lambda: nc.vector.tensor_reduce(s1[:], v3(y), AX.X, ALU.add), reads=[y], writes=[s1])
            ctx.op("dve", lambda: nc.vector.tensor_scalar_mul(s1[:], s1[:], 1.0 / RW_HD), reads=[s1], writes=[s1])
            ctx.op("dve", lambda: nc.vector.tensor_tensor(v3(y), v3(y), bc3(s1), ALU.subtract), reads=[y, s1], writes=[y])
            ctx.op("pool", lambda: nc.gpsimd.tensor_tensor(sq[:], y[:], y[:], ALU.mult), reads=[y], writes=[sq])
            ctx.op("dve", lambda: nc.vector.tensor_reduce(s2[:], v3(sq), AX.X, ALU.add), reads=[sq], writes=[s2])
            ctx.op("dve", lambda: nc.vector.tensor_scalar(s2[:], s2[:], 1.0 / RW_HD, RW_EPS, ALU.mult, ALU.add), reads=[s2], writes=[s2])
            ctx.op("act", lambda: nc.scalar.activation(s2[:], s2[:], AF.Sqrt), reads=[s2], writes=[s2])
            ctx.op("dve", lambda: nc.vector.reciprocal(s2[:], s2[:]), reads=[s2], writes=[s2])
            ctx.op("dve", lambda: nc.vector.tensor_tensor(v3(y), v3(y), bc3(s2), ALU.mult), reads=[y, s2], writes=[y])
            ctx.op("pool", lambda: nc.gpsimd.tensor_tensor(y[:], y[:], gnb[:], ALU.mult), reads=[y, gnb], writes=[y])
            ctx.op("pool", lambda: nc.gpsimd.tensor_tensor(y[:], y[:], gbb[:], ALU.add), reads=[y, gbb], writes=[y])
            ctx.op("dve", lambda: nc.vector.tensor_tensor(v3(vv), v3(vv), bc3(bon), ALU.mult), reads=[vv, bon], writes=[vv])
            ctx.op("pool", lambda: nc.gpsimd.tensor_tensor(y[:], y[:], vv[:], ALU.add), reads=[y, vv], writes=[y])
            ob = ogb[n % 2]
            ctx.op("dve", lambda: nc.vector.tensor_tensor(ob[:], y[:], gg[:], ALU.mult), reads=[y, gg], writes=[ob])
            g_, gi_ = divmod(n, G)
            self.xt_emit_tile(ob, ob, stg[g_ % 2], stg[g_ % 2], gi_ * P, pst[n % 2])
            if gi_ == G - 1:
                ctx.dma("sp", OGTv[:, :, g_ * G * P:(g_ + 1) * G * P], stg[g_ % 2][:], reads=[stg[g_ % 2]], writes=[("OGT", g_)])
        ctx.barrier()

    if cfg.stop == "rw4":
        return
    with contextlib.ExitStack() as st:
        aT = self.load_AT(st, "oTa", OGT, KC, 0, T)
        res = GemmRes(self, st, KC, 512, 3)
        epi = self.epi_resid(st, X, Z1)
        self.gemm_tok(res, aT, aT, KC, w_out, 0, D, range(NT), epi)
        ctx.barrier()


Prog.rwkv_layer = _rwkv_layer
def _const_inputs(cfg, core):
    T, NSEG = cfg.T, cfg.NSEG
    seg = core % NSEG
    f32 = np.float32
    own = np.zeros((P, NSEG), f32)
    own[:, seg] = 1
    hs = np.zeros((P, NSEG), f32)
    if seg > 0:
        hs[:, seg - 1] = 1
    c = {"ident": np.eye(P, dtype=f32).astype(ml_dtypes.bfloat16), "identf": np.eye(P, dtype=f32),
         "own": own, "halo_sel": hs}
    inv = (1.0 / (10000.0 ** (np.arange(0, RET_DK, 2, dtype=f32) / f32(RET_DK)))).astype(f32)
    pos = (seg * T + np.arange(T)).astype(f32)
    ang = (pos[None, :] * inv[:, None]).astype(f32)
    c["rope_cos"] = np.cos(ang).astype(f32)
    c["rope_sin"] = np.sin(ang).astype(f32)
    gam = np.array(RET_GAMMA, np.float64)
    idx = np.arange(P, dtype=np.float64)
    c["ret_kdec"] = (gam[None, :] ** (P - 1 - idx[:, None])).astype(f32)
    diff = idx[None, :] - idx[:, None]
    m = np.where(diff[:, None, :] >= 0, gam[None, :, None] ** np.maximum(diff[:, None, :], 0), 0.0)
    c["ret_maskT"] = m.astype(f32)
    c["ret_qdec"] = np.broadcast_to((gam[:, None] ** (idx[None, :] + 1.0))[None], (P, RET_H, P)).astype(f32).copy()
    coef = np.zeros((P, NSEG, RET_H), f32)
    for s in range(seg):
        coef[:, s, :] = (gam ** (T * (seg - s - 1)))[None, :]
    c["ret_coef"] = coef
    _swa_consts(cfg, core, c)
    _rwkv_consts(cfg, core, c)
    return c


def make_in_maps(cfg, prog, inputs):
    T, NSEG = cfg.T, cfg.NSEG
    maps = []
    shared = {}
    per_layer = ["ret_w_in", "ret_w_out", "ret_gn_gain", "swa_w_qkv", "swa_sinks", "swa_w_out",
                 "rwkv_mix", "rwkv_w_rkv", "rwkv_w0", "rwkv_w1", "rwkv_w2", "rwkv_a0", "rwkv_a1", "rwkv_a2",
                 "rwkv_g1", "rwkv_g2", "rwkv_k_k", "rwkv_k_a", "rwkv_r_k", "rwkv_gn_gain", "rwkv_gn_bias",
                 "rwkv_w_out", "ffn_w_up", "ffn_conv_w", "ffn_conv_b", "ffn_w_down", "ple_w_proj", "ple_w_gate"]
    for name in prog.inputs:
        if name in prog.tiled:
            fn, K_, N_, tw_, kp_ = prog.tiled[name]
            w = np.asarray(fn(inputs))
            assert w.shape == (K_, N_), (name, w.shape)
            shared[name] = np.ascontiguousarray(
                w.reshape(K_ // kp_, kp_, N_ // tw_, tw_).transpose(2, 1, 0, 3).reshape(N_ // tw_, kp_, (K_ // kp_) * tw_))
            continue
        if name in inputs and name not in ("x",):
            shared[name] = np.ascontiguousarray(inputs[name])
            continue
        for base in per_layer:
            if name.startswith(base + "_") and name[len(base) + 1:].isdigit():
                shared[name] = np.ascontiguousarray(inputs[base][int(name[len(base) + 1:])])
    for core in range(cfg.ncores):
        b, seg = divmod(core, NSEG)
        consts = _const_inputs(cfg, core)
        m = {}
        for name in prog.inputs:
            if name in shared:
                m[name] = shared[name]
            elif name == "x":
                m[name] = np.ascontiguousarray(inputs["x"][b, seg * T:(seg + 1) * T, :])
            elif name.startswith("pT_"):
                l = int(name[3:])
                m[name] = np.ascontiguousarray(inputs["p"][l, b, seg * T:(seg + 1) * T, :].T)
            elif name in consts:
                m[name] = consts[name]
            else:
                raise KeyError(name)
        maps.append(m)
    return maps


def run_cfg(cfg, inputs):
    prog = Prog(cfg)
    prog.build()
    maps = make_in_maps(cfg, prog, inputs)
    res = run_bass_kernel_spmd(prog.nc, maps, core_ids=list(range(cfg.ncores)))
    return prog, res.results


def kernel(**inputs):
    cfg = Cfg()
    prog, results = run_cfg(cfg, inputs)
    out = np.empty((cfg.NB, cfg.NSEG * cfg.T, D), np.float32)
    for core in range(cfg.ncores):
        b, seg = divmod(core, cfg.NSEG)
        out[b, seg * cfg.T:(seg + 1) * cfg.T, :] = results[core]["out"]
    return out
```
